# Optimizing a Trainium2 kernel written in Bass

```python
import math
import jax, jax.numpy as jnp
from jax import lax
import numpy as np

D_MODEL = 1024
BATCH = 4
SEQ = 8192
DEPTH = 2

D_MIX = D_MODEL
HEAD_DIM = 64
M_WIDTH = D_MIX // 2
M_HEAD_DIM = 128
M_HEADS = M_WIDTH // M_HEAD_DIM
M_CHUNK = 64
CONV_W = 4
B_WIDTH = D_MIX // 4
B_HEADS = B_WIDTH // HEAD_DIM
MOBA_BLOCK = 256
MOBA_TOPK = 3
N_WIDTH = D_MIX // 4
N_HEADS = N_WIDTH // HEAD_DIM
CMP_LEN = 32
CMP_STRIDE = 16
CMP_HIDDEN = 128
SEL_BLOCK = 64
SEL_TOPK = 16
WINDOW = 512
Q_BLOCK = 64
D_FF = 4 * D_MODEL
ROPE_THETA = 500000.0
ROT_DIM = HEAD_DIM // 4
NORM_EPS = 1e-6
NEG = -1e30
BIG = 1e9

SPLITS = ((M_WIDTH,) * 4 + (M_HEADS, M_HEADS)
          + (B_WIDTH,) * 3
          + (N_WIDTH,) + (HEAD_DIM,) * 6
          + (3 * N_HEADS,))
D_IN = sum(SPLITS)
SPLIT_POINTS = tuple(int(v) for v in np.cumsum(SPLITS)[:-1])

kernel_name = "hybrid_mlstm_moba_nsa_block"


def rmsnorm(x, g):
    xf = x.astype(jnp.float32)
    y = xf * lax.rsqrt(jnp.mean(xf * xf, axis=-1, keepdims=True) + NORM_EPS)
    return (y * g.astype(jnp.float32)).astype(x.dtype)


def partial_rope(x):
    S = x.shape[-2]
    half = ROT_DIM // 2
    inv_freq = jnp.exp(-math.log(ROPE_THETA) * jnp.arange(half, dtype=jnp.float32) * (2.0 / ROT_DIM))
    ang = jnp.arange(S, dtype=jnp.float32)[:, None] * inv_freq[None, :]
    cos, sin = jnp.cos(ang), jnp.sin(ang)
    x1, x2, rest = x[..., :half], x[..., half:ROT_DIM], x[..., ROT_DIM:]
    return jnp.concatenate([x1 * cos - x2 * sin, x2 * cos + x1 * sin, rest], axis=-1)


def causal_conv(x, w):
    S = x.shape[1]
    xp = jnp.pad(x, ((0, 0), (CONV_W - 1, 0), (0, 0)))
    return sum(xp[:, j:j + S] * w[j] for j in range(CONV_W))


def masked_softmax(s, mask):
    p = jax.nn.softmax(jnp.where(mask, s, NEG), axis=-1)
    return jnp.where(mask, p, 0.0)


def mlstm(q, k, v, o_pre, i_pre, f_pre, norm_g):
    B, S, _ = q.shape
    NC, L, H, D = S // M_CHUNK, M_CHUNK, M_HEADS, M_HEAD_DIM

    def to_chunks(t):
        return t.reshape(B, NC, L, H, D).transpose(1, 0, 3, 2, 4)

    qc, kc, vc = to_chunks(q), to_chunks(k) * (D ** -0.5), to_chunks(v)
    lf = jax.nn.log_sigmoid(f_pre).reshape(B, NC, L, H).transpose(1, 0, 3, 2)
    li = i_pre.reshape(B, NC, L, H).transpose(1, 0, 3, 2)
    causal = jnp.tril(jnp.ones((L, L), dtype=bool))

    def step(carry, xs):
        C, n, m = carry
        qt, kt, vt, lft, lit = xs
        b = jnp.cumsum(lft, axis=-1)
        dmat = jnp.where(causal, b[..., :, None] - b[..., None, :] + lit[..., None, :], -jnp.inf)
        inter = b + m[..., None]
        m_t = jnp.maximum(inter, jnp.max(dmat, axis=-1))
        w_intra = jnp.exp(dmat - m_t[..., None])
        w_prev = jnp.exp(inter - m_t)
        s = jnp.einsum('bhtd,bhsd->bhts', qt, kt) * w_intra
        num = jnp.einsum('bhts,bhse->bhte', s, vt) + w_prev[..., None] * jnp.einsum('bhtd,bhde->bhte', qt, C)
        den = jnp.sum(s, axis=-1) + w_prev * jnp.einsum('bhtd,bhd->bht', qt, n)
        h = num / jnp.maximum(jnp.abs(den), jnp.exp(-m_t))[..., None]
        b_last = b[..., -1]
        g = b_last[..., None] - b + lit
        m_new = jnp.maximum(b_last + m, jnp.max(g, axis=-1))
        a = jnp.exp(b_last + m - m_new)
        w_s = jnp.exp(g - m_new[..., None])
        C = a[..., None, None] * C + jnp.einsum('bhs,bhsd,bhse->bhde', w_s, kt, vt)
        n = a[..., None] * n + jnp.einsum('bhs,bhsd->bhd', w_s, kt)
        return (C, n, m_new), h

    init = (jnp.zeros((B, H, D, D), jnp.float32), jnp.zeros((B, H, D), jnp.float32),
            jnp.zeros((B, H), jnp.float32))
    _, h = lax.scan(step, init, (qc, kc, vc, lf, li))
    h = h.transpose(1, 0, 3, 2, 4).reshape(B, S, H, D)
    h = rmsnorm(h, norm_g.reshape(H, D)).reshape(B, S, M_WIDTH)
    return jax.nn.sigmoid(o_pre) * h


def moba(q, k, v, q_g, k_g):
    B, S, _ = q.shape
    H, D, BS = B_HEADS, HEAD_DIM, MOBA_BLOCK
    heads = lambda t: t.reshape(B, S, H, D).transpose(0, 2, 1, 3)
    q = partial_rope(rmsnorm(heads(q), q_g))
    k = partial_rope(rmsnorm(heads(k), k_g))
    v = heads(v)
    NB = -(-S // BS)
    pad = NB * BS - S
    kp = jnp.pad(k, ((0, 0), (0, 0), (0, pad), (0, 0)))
    vp = jnp.pad(v, ((0, 0), (0, 0), (0, pad), (0, 0)))
    kb = kp.reshape(B, H, NB, BS, D)
    vb = vp.reshape(B, H, NB, BS, D)
    kmean = jnp.mean(kb, axis=3)
    topk = min(MOBA_TOPK, NB)
    scale = D ** -0.5
    gather = jax.vmap(jax.vmap(lambda blocks, idx: blocks[idx]))

    def block(j):
        t0 = j * Q_BLOCK
        tq = t0 + jnp.arange(Q_BLOCK)
        qj = lax.dynamic_slice_in_dim(q, t0, Q_BLOCK, axis=2)
        own = t0 // BS
        gs = jnp.einsum('bhqd,bhnd->bhqn', qj, kmean)
        gs = jnp.where(jnp.arange(NB) < own, gs, NEG)
        _, idx = lax.top_k(gs, topk)
        gk, gv = gather(kb, idx), gather(vb, idx)
        s_sel = jnp.einsum('bhqd,bhqrkd->bhqrk', qj, gk) * scale
        s_sel = jnp.where((jnp.arange(topk) < own)[:, None], s_sel, NEG).reshape(B, H, Q_BLOCK, topk * BS)
        ko = lax.dynamic_slice_in_dim(kp, own * BS, BS, axis=2)
        vo = lax.dynamic_slice_in_dim(vp, own * BS, BS, axis=2)
        s_own = jnp.einsum('bhqd,bhkd->bhqk', qj, ko) * scale
        s_own = jnp.where(own * BS + jnp.arange(BS)[None, :] <= tq[:, None], s_own, NEG)
        p = jax.nn.softmax(jnp.concatenate([s_sel, s_own], axis=-1), axis=-1)
        p_sel = p[..., :topk * BS].reshape(B, H, Q_BLOCK, topk, BS)
        p_own = p[..., topk * BS:]
        return (jnp.einsum('bhqrk,bhqrkd->bhqd', p_sel, gv)
                + jnp.einsum('bhqk,bhkd->bhqd', p_own, vo))

    out = lax.map(block, jnp.arange(S // Q_BLOCK))
    return out.transpose(1, 0, 3, 2, 4).reshape(B, S, B_WIDTH)


def nsa(q, kc, vc, ks, vs, kw, vw, g_pre, q_g, k_g, pe, w1, w2):
    B, S, _ = q.shape
    H, D = N_HEADS, HEAD_DIM
    scale = D ** -0.5
    q = rmsnorm(q.reshape(B, S, H, D), q_g).transpose(0, 2, 1, 3)
    qr = partial_rope(q)
    ks = partial_rope(rmsnorm(ks, k_g[1]))
    kw = partial_rope(rmsnorm(kw, k_g[2]))

    n_sub = CMP_LEN // CMP_STRIDE
    Nc = S // CMP_STRIDE - n_sub + 1

    def compress(t, pe_, w1_, w2_):
        c = t.reshape(B, S // CMP_STRIDE, CMP_STRIDE, D)
        blocks = jnp.concatenate([c[:, r:Nc + r] for r in range(n_sub)], axis=2) + pe_
        hid = jax.nn.silu(blocks.reshape(B, Nc, CMP_LEN * D) @ w1_)
        return hid @ w2_

    Kc = rmsnorm(compress(kc, pe[0], w1[0], w2[0]), k_g[0])
    Vc = compress(vc, pe[1], w1[1], w2[1])
    c_end = jnp.arange(Nc) * CMP_STRIDE + CMP_LEN - 1

    Nsel = S // SEL_BLOCK
    c_start = np.arange(Nc) * CMP_STRIDE
    s_start = np.arange(Nsel) * SEL_BLOCK
    overlap = jnp.asarray(((c_start[:, None] < s_start[None, :] + SEL_BLOCK)
                           & (c_start[:, None] + CMP_LEN > s_start[None, :])).astype(np.float32))
    ksb = ks.reshape(B, Nsel, SEL_BLOCK, D)
    vsb = vs.reshape(B, Nsel, SEL_BLOCK, D)
    ksel = min(SEL_TOPK, Nsel)
    gather_b = jax.vmap(lambda blocks, idx: blocks[idx])
    kw_pad = jnp.pad(kw, ((0, 0), (WINDOW, 0), (0, 0)))
    vw_pad = jnp.pad(vw, ((0, 0), (WINDOW, 0), (0, 0)))
    gates = jax.nn.sigmoid(g_pre).reshape(B, S, 3, H)

    def block(j):
        t0 = j * Q_BLOCK
        tq = t0 + jnp.arange(Q_BLOCK)
        qj = lax.dynamic_slice_in_dim(q, t0, Q_BLOCK, axis=2)
        qrj = lax.dynamic_slice_in_dim(qr, t0, Q_BLOCK, axis=2)
        s_c = jnp.einsum('bhqd,bcd->bhqc', qj, Kc) * scale
        p_c = masked_softmax(s_c, c_end[None, :] <= tq[:, None])
        o_c = jnp.einsum('bhqc,bcd->bhqd', p_c, Vc)
        imp = jnp.einsum('bhqc,cn->bqn', p_c, overlap)
        blk_q = tq // SEL_BLOCK
        jn = jnp.arange(Nsel)
        causal_blk = jn[None, :] <= blk_q[:, None]
        forced = causal_blk & ((jn[None, :] == 0) | (jn[None, :] >= blk_q[:, None] - 1))
        imp = jnp.where(forced, BIG, jnp.where(causal_blk, imp, NEG))
        _, sidx = lax.top_k(imp, ksel)
        gk, gv = gather_b(ksb, sidx), gather_b(vsb, sidx)
        pos = sidx[..., None] * SEL_BLOCK + jnp.arange(SEL_BLOCK)
        mask_s = (pos <= tq[None, :, None, None])[:, None]
        s_s = jnp.where(mask_s, jnp.einsum('bhqd,bqrkd->bhqrk', qrj, gk) * scale, NEG)
        p_s = jax.nn.softmax(s_s.reshape(B, H, Q_BLOCK, ksel * SEL_BLOCK), axis=-1)
        o_s = jnp.einsum('bhqrk,bqrkd->bhqd', p_s.reshape(B, H, Q_BLOCK, ksel, SEL_BLOCK), gv)
        kwj = lax.dynamic_slice_in_dim(kw_pad, t0, WINDOW + Q_BLOCK, axis=1)
        vwj = lax.dynamic_slice_in_dim(vw_pad, t0, WINDOW + Q_BLOCK, axis=1)
        posw = t0 - WINDOW + jnp.arange(WINDOW + Q_BLOCK)
        mask_w = ((posw[None, :] <= tq[:, None]) & (posw[None, :] > tq[:, None] - WINDOW)
                  & (posw[None, :] >= 0))
        s_w = jnp.where(mask_w, jnp.einsum('bhqd,bkd->bhqk', qrj, kwj) * scale, NEG)
        o_w = jnp.einsum('bhqk,bkd->bhqd', jax.nn.softmax(s_w, axis=-1), vwj)
        gj = lax.dynamic_slice_in_dim(gates, t0, Q_BLOCK, axis=1).transpose(0, 2, 3, 1)[..., None]
        return gj[:, 0] * o_c + gj[:, 1] * o_s + gj[:, 2] * o_w

    out = lax.map(block, jnp.arange(S // Q_BLOCK))
    return out.transpose(1, 0, 3, 2, 4).reshape(B, S, N_WIDTH)


def setup_inputs(seed: int = 0) -> dict:
    key = jax.random.key(seed)
    ks = jax.random.split(key, 20)
    nrm = lambda k, shape, s: jax.random.normal(k, shape, jnp.float32) * s
    L = DEPTH
    b_i = nrm(ks[2], (L, M_HEADS), 0.1)
    b_f = jnp.linspace(3.0, 6.0, M_HEADS, dtype=jnp.float32)[None, :] + nrm(ks[3], (L, M_HEADS), 0.1)
    return {
        "x": nrm(ks[0], (BATCH, SEQ, D_MODEL), 1.0),
        "w_in": nrm(ks[1], (L, D_MODEL, D_IN), D_MODEL ** -0.5),
        "b_if": jnp.concatenate([b_i, b_f], axis=-1),
        "conv_qk": nrm(ks[4], (L, CONV_W, 2 * M_WIDTH), CONV_W ** -0.5),
        "m_norm": 1.0 + nrm(ks[5], (L, M_WIDTH), 0.02),
        "moba_qk_norm": 1.0 + nrm(ks[6], (L, 2, HEAD_DIM), 0.02),
        "nsa_q_norm": 1.0 + nrm(ks[7], (L, HEAD_DIM), 0.02),
        "nsa_k_norm": 1.0 + nrm(ks[8], (L, 3, HEAD_DIM), 0.02),
        "cmp_pe": nrm(ks[9], (L, 2, CMP_LEN, HEAD_DIM), 0.02),
        "cmp_w1": nrm(ks[10], (L, 2, CMP_LEN * HEAD_DIM, CMP_HIDDEN), (CMP_LEN * HEAD_DIM) ** -0.5),
        "cmp_w2": nrm(ks[11], (L, 2, CMP_HIDDEN, HEAD_DIM), CMP_HIDDEN ** -0.5),
        "w_out": nrm(ks[12], (L, D_MIX, D_MODEL), D_MIX ** -0.5),
        "norm_mix": 1.0 + nrm(ks[13], (L, D_MODEL), 0.02),
        "norm_ffn": 1.0 + nrm(ks[14], (L, D_MODEL), 0.02),
        "w_ff1": nrm(ks[15], (L, D_MODEL, D_FF), D_MODEL ** -0.5),
        "w_ff2": nrm(ks[16], (L, D_FF, D_MODEL), D_FF ** -0.5),
    }


def reference(x, w_in, b_if, conv_qk, m_norm, moba_qk_norm, nsa_q_norm, nsa_k_norm,
              cmp_pe, cmp_w1, cmp_w2, w_out, norm_mix, norm_ffn, w_ff1, w_ff2):
    for l in range(DEPTH):
        h = rmsnorm(x, norm_mix[l])
        proj = jnp.einsum('bsd,de->bse', h, w_in[l]).astype(jnp.float32)
        (m_q, m_k, m_v, m_o, m_i, m_f, b_q, b_k, b_v,
         n_q, n_kc, n_vc, n_ks, n_vs, n_kw, n_vw, n_g) = jnp.split(proj, SPLIT_POINTS, axis=-1)
        qk = jax.nn.silu(causal_conv(jnp.concatenate([m_q, m_k], axis=-1), conv_qk[l]))
        gif = jnp.concatenate([m_i, m_f], axis=-1) + b_if[l]
        y_m = mlstm(qk[..., :M_WIDTH], qk[..., M_WIDTH:], m_v, m_o,
                    gif[..., :M_HEADS], gif[..., M_HEADS:], m_norm[l])
        y_b = moba(b_q, b_k, b_v, moba_qk_norm[l, 0], moba_qk_norm[l, 1])
        y_n = nsa(n_q, n_kc, n_vc, n_ks, n_vs, n_kw, n_vw, n_g, nsa_q_norm[l], nsa_k_norm[l],
                  cmp_pe[l], cmp_w1[l], cmp_w2[l])
        mix = jnp.concatenate([y_m, y_b, y_n], axis=-1).astype(x.dtype)
        x = x + jnp.einsum('bse,ed->bsd', mix, w_out[l])
        h = rmsnorm(x, norm_ffn[l])
        u = jnp.square(jax.nn.relu(jnp.einsum('bsd,df->bsf', h, w_ff1[l])))
        x = x + jnp.einsum('bsf,fd->bsd', u, w_ff2[l])
    return x
```

```python
import math
import os
from contextlib import ExitStack

import numpy as np
import ml_dtypes

import concourse.bass as bass
import concourse.mybir as mybir
from concourse.bass_utils import run_bass_kernel_spmd

F32 = mybir.dt.float32
BF16 = mybir.dt.bfloat16
AF = mybir.ActivationFunctionType
ALU = mybir.AluOpType
AX = mybir.AxisListType

D_MODEL = 1024
D_IN = 3476
D_FF = 4096
EPS = 1e-6
NEGM = -30000.0
SAME_ENGINE_SYNC = True


class Prog:
    def __init__(self, nc, es, n_dma=24):
        self.nc = nc
        self.es = es
        self.eng = dict(pe=nc.tensor, act=nc.scalar, dve=nc.vector, pool=nc.gpsimd, sp=nc.sync)
        self.esem = {e: es.enter_context(nc.semaphore("s_" + e)) for e in self.eng}
        self.ecnt = {e: 0 for e in self.eng}
        self.dsem = [es.enter_context(nc.semaphore("d_%d" % i)) for i in range(n_dma)]
        self.dval = [0] * n_dma
        self.dnext = 0
        self.seen = {e: {} for e in self.eng}
        self.lastw = {}
        self.readers = {}
        self.nops = 0

    def _wait(self, eng, ev):
        kind, name, val = ev
        if kind == 'e' and name == eng:
            if eng == 'pe' or eng == 'sp' or not SAME_ENGINE_SYNC:
                return
        key = (kind, name)
        if self.seen[eng].get(key, 0) >= val:
            return
        self.seen[eng][key] = val
        sem = self.esem[name] if kind == 'e' else self.dsem[name]
        self.eng[eng].wait_ge(sem, val)

    def _deps(self, reads, writes):
        evs = []
        for k in reads:
            w = self.lastw.get(k)
            if w is not None:
                evs.append(w)
        for k in writes:
            w = self.lastw.get(k)
            if w is not None:
                evs.append(w)
            evs.extend(self.readers.get(k, ()))
        return evs

    def _commit(self, me, reads, writes):
        for k in reads:
            lst = self.readers.setdefault(k, [])
            lst[:] = [r for r in lst if (r[0], r[1]) != (me[0], me[1])]
            lst.append(me)
        for k in writes:
            self.lastw[k] = me
            self.readers[k] = []

    def op(self, eng, fn, reads=(), writes=()):
        for ev in self._deps(reads, writes):
            self._wait(eng, ev)
        ins = fn(self.eng[eng])
        self.ecnt[eng] += 1
        ins.then_inc(self.esem[eng], 1)
        self._commit(('e', eng, self.ecnt[eng]), reads, writes)
        self.nops += 1
        return ins

    def dma(self, q, out, in_, reads=(), writes=(), **kw):
        if q == 'auto':
            qs = os.environ.get("DMAQ", "sp").split(",")
            self.rr = getattr(self, 'rr', 0) + 1
            q = qs[self.rr % len(qs)]
        slot = self.dnext
        self.dnext = (self.dnext + 1) % len(self.dsem)
        evs = self._deps(reads, writes)
        if self.dval[slot] > 0:
            evs.append(('d', slot, self.dval[slot]))
        for ev in evs:
            self._wait(q, ev)
        self.dval[slot] += 16
        self.eng[q].dma_start(out=out, in_=in_, **kw).then_inc(self.dsem[slot], 16)
        self._commit(('d', slot, self.dval[slot]), reads, writes)
        self.nops += 1

    def finish(self):
        for slot in range(len(self.dsem)):
            if self.dval[slot] > 0:
                self._wait('sp', ('d', slot, self.dval[slot]))
        for e in self.eng:
            if e != 'sp' and self.ecnt[e] > 0:
                self._wait('sp', ('e', e, self.ecnt[e]))


class Builder:
    def __init__(self, S, depth, dbg=None):
        self.S = S
        self.depth = depth
        self.dbg = dbg or []
        self.nc = bass.Bass("TRN2", target_bir_lowering=False)
        self.es = ExitStack()
        self.P = Prog(self.nc, self.es)
        self.uid = 0

    def dram_in(self, name, shape, dt=F32):
        return self.nc.dram_tensor(name, list(shape), dt, kind="ExternalInput").ap()

    def dram_out(self, name, shape, dt=F32):
        return self.nc.dram_tensor(name, list(shape), dt, kind="ExternalOutput").ap()

    def dram_tmp(self, name, shape, dt):
        return self.nc.dram_tensor(name, list(shape), dt, kind="Internal").ap()

    def sb(self, st, name, shape, dt):
        self.uid += 1
        return st.enter_context(self.nc.sbuf_tensor("%s_%d" % (name, self.uid), list(shape), dt))

    def ps(self, st, name, shape, dt=F32):
        self.uid += 1
        return st.enter_context(self.nc.psum_tensor("%s_%d" % (name, self.uid), list(shape), dt))

    def dma_mid(self, out, in_, n_mid, step, reads=(), writes=()):
        for a in range(0, n_mid, step):
            b_ = min(n_mid, a + step)
            self.P.dma('sp', out[:, a:b_, :], in_[:, a:b_, :], reads=reads, writes=writes)

    def barrier(self):
        P = self.P
        for e in P.eng:
            for o in P.eng:
                if P.ecnt[o] > 0 and not (o == e and e in ('pe', 'sp')):
                    key = ('e', o)
                    if P.seen[e].get(key, 0) < P.ecnt[o]:
                        P.seen[e][key] = P.ecnt[o]
                        P.eng[e].wait_ge(P.esem[o], P.ecnt[o])
            for slot in range(len(P.dsem)):
                if P.dval[slot] > 0:
                    P._wait(e, ('d', slot, P.dval[slot]))

    def declare(self):
        S, L = self.S, self.depth
        self.x_in = self.dram_in("x", [S, D_MODEL])
        self.w_in = self.dram_in("w_in", [L, D_MODEL, D_IN])
        self.b_if = self.dram_in("b_if", [L, 8])
        self.conv_qk = self.dram_in("conv_qk", [L, 4, 1024])
        self.m_norm = self.dram_in("m_norm", [L, 512])
        self.moba_qk_norm = self.dram_in("moba_qk_norm", [L, 2, 64])
        self.nsa_q_norm = self.dram_in("nsa_q_norm", [L, 64])
        self.nsa_k_norm = self.dram_in("nsa_k_norm", [L, 3, 64])
        self.cmp_pe = self.dram_in("cmp_pe", [L, 2, 32, 64])
        self.cmp_w1 = self.dram_in("cmp_w1", [L, 2, 2048, 128])
        self.cmp_w2 = self.dram_in("cmp_w2", [L, 2, 128, 64])
        self.w_out = self.dram_in("w_out", [L, 1024, 1024])
        self.norm_mix = self.dram_in("norm_mix", [L, 1024])
        self.norm_ffn = self.dram_in("norm_ffn", [L, 1024])
        self.w_ff1 = self.dram_in("w_ff1", [L, 1024, D_FF])
        self.w_ff2 = self.dram_in("w_ff2", [L, D_FF, 1024])
        self.c_identb = self.dram_in("c_identb", [128, 128], BF16)
        self.c_identf = self.dram_in("c_identf", [128, 128], F32)
        self.c_cos = self.dram_in("c_cos", [S, 8], F32)
        self.c_sin = self.dram_in("c_sin", [S, 8], F32)
        self.c_ut = self.dram_in("c_ut", [128, 128], F32)
        self.c_tri = self.dram_in("c_tri", [128, 128], F32)
        self.c_ind32 = self.dram_in("c_ind32", [32, S], BF16)
        self.c_ind64 = self.dram_in("c_ind64", [64, S], BF16)
        self.c_tm4 = self.dram_in("c_tm4", [128, 4, 512], BF16)
        self.c_trib = self.dram_in("c_trib", [128, 128], BF16)
        self.c_triw = self.dram_in("c_triw", [128, 128], BF16)
        self.c_cmask = self.dram_in("c_cmask", [128, 17, 128], BF16)
        self.c_ovl = self.dram_in("c_ovl", [512, 128], BF16)
        self.c_cols = self.dram_in("c_cols", [128, 4], F32)
        self.y_out = self.dram_out("y", [S, D_MODEL])
        dbg = self.dbg

        def tmp(name, shape, dt):
            if name in dbg:
                return self.dram_out(name, shape, dt)
            return self.dram_tmp(name, shape, dt)
        self.xbuf = tmp("xbuf", [S, D_MODEL], F32)
        self.qkT = tmp("qkT", [1024, S], BF16)
        self.kcvcT = tmp("kcvcT", [128, S], BF16)
        self.gi_d = tmp("gi_d", [4, S], F32)
        self.gf_d = tmp("gf_d", [4, S], F32)
        self.mv_d = tmp("mv_d", [S, 512], BF16)
        self.og_d = tmp("og_d", [S, 512], BF16)
        self.bqT = tmp("bqT", [256, S], BF16)
        self.bkT = tmp("bkT", [256, S], BF16)
        self.bv_d = tmp("bv_d", [S, 256], BF16)
        self.nqrT = tmp("nqrT", [256, S], BF16)
        self.nquT = tmp("nquT", [256, S], BF16)
        self.ksT = tmp("ksT", [64, S], BF16)
        self.kwT = tmp("kwT", [64, S], BF16)
        self.vs_d = tmp("vs_d", [S, 64], BF16)
        self.vw_d = tmp("vw_d", [S, 64], BF16)
        self.ng_d = tmp("ng_d", [S, 12], F32)
        self.mix_d = tmp("mix_d", [S, 1024], BF16)

    def phase1(self, l, xsrc):
        S, P, nc = self.S, self.P, self.nc
        NB = S // 512
        with ExitStack() as st:
            sb, ps = self.sb, self.ps
            w_bf = sb(st, "w_in", [128, 8, D_IN], BF16)
            with ExitStack() as st2:
                stage = [sb(st2, "wst", [128, D_IN], F32) for _ in range(2)]
                for kt in range(8):
                    s_ = stage[kt % 2]
                    P.dma('sp', s_[:], self.w_in[l, kt * 128:(kt + 1) * 128, :], writes=[('wst', kt % 2)])
                    P.op(['pool', 'dve'][kt % 2], lambda e: e.tensor_copy(out=w_bf[:, kt, :], in_=s_[:]),
                         reads=[('wst', kt % 2)], writes=[('w_in', kt)])
                self.barrier()
            identb = sb(st, "identb", [128, 128], BF16)
            gmix = sb(st, "gmix", [128, 1024], F32)
            mnorm = sb(st, "mnorm", [128, 512], F32)
            gain20 = sb(st, "gain20", [128, 20, 64], F32)
            convw = sb(st, "convw", [128, 4, 8], F32)
            bi = sb(st, "bi", [4, 1], F32)
            nbf = sb(st, "nbf", [4, 1], F32)
            cosT = sb(st, "cosT", [128, S // 128, 8], F32)
            sinT = sb(st, "sinT", [128, S // 128, 8], F32)
            P.dma('sp', identb[:], self.c_identb, writes=['identb'])
            P.dma('sp', gmix[:], self.norm_mix[l].partition_broadcast(128), writes=['gmix'])
            P.dma('sp', mnorm[:], self.m_norm[l].partition_broadcast(128), writes=['mnorm'])
            P.op('pool', lambda e: e.memset(gain20[:], 1.0), writes=['gain20'])
            for i in range(20):
                src = None
                if i < 4:
                    src = self.moba_qk_norm[l, 0]
                elif i < 8:
                    src = self.moba_qk_norm[l, 1]
                elif 12 <= i < 16:
                    src = self.nsa_q_norm[l]
                elif i == 16:
                    src = self.nsa_k_norm[l, 1]
                elif i == 18:
                    src = self.nsa_k_norm[l, 2]
                if src is not None:
                    P.dma('sp', gain20[:, i, :], src.partition_broadcast(128), writes=['gain20'])
            with nc.allow_non_contiguous_dma(reason="tiny conv weight transpose"):
                for jj in range(4):
                    P.dma('sp', convw[:, jj, :], self.conv_qk[l, jj].rearrange("(ft p) -> p ft", p=128), writes=['convw'])
                P.dma('sp', bi[:], self.b_if[l, 0:4].rearrange("(p o) -> p o", o=1), writes=['bi'])
                P.dma('sp', nbf[:], self.b_if[l, 4:8].rearrange("(p o) -> p o", o=1), writes=['nbf'])
            P.op('dve', lambda e: e.tensor_scalar(out=nbf[:], in0=nbf[:], scalar1=-1.0, scalar2=None, op0=ALU.mult),
                 reads=['nbf'], writes=['nbf'])
            self.dma_mid(cosT[:], self.c_cos.rearrange("(j p) r -> p j r", p=128), S // 128, 8, writes=['cos'])
            self.dma_mid(sinT[:], self.c_sin.rearrange("(j p) r -> p j r", p=128), S // 128, 8, writes=['sin'])

            mh20 = sb(st, "mh20", [128, 20], F32)
            P.op('pool', lambda e: e.memset(mh20[:], -0.5), writes=['mh20'])
            xt = [sb(st, "xt", [128, 4, 1024], F32) for _ in range(2)]
            junk2_2 = [sb(st, "junk2", [128, 1280], BF16) for _ in range(2)]
            ss_2 = [sb(st, "ss", [128, 4], F32) for _ in range(2)]
            rstd_2 = [sb(st, "rstd", [128, 4], F32) for _ in range(2)]
            h_2 = [sb(st, "h", [128, 4, 1024], BF16)] * 2
            hT_2 = [sb(st, "hT", [128, 8, 512], BF16) for _ in range(2)]
            cbuf = [sb(st, "cbuf", [128, 515], F32) for _ in range(8)]
            for ft in range(8):
                P.op('pool', lambda e: e.memset(cbuf[ft][:, 0:3], 0.0), writes=[('cbuf', ft)])
            acc_2 = [sb(st, "acc", [128, 512], F32) for _ in range(2)]
            fo = [sb(st, "fo", [128, 512], BF16) for _ in range(2)]
            gsb = sb(st, "gsb", [4, 512], F32)
            gsb2 = sb(st, "gsb2", [4, 512], F32)
            mvb_2 = [sb(st, "mvb", [128, 512], BF16) for _ in range(2)]
            sg_2 = [sb(st, "sg", [128, 512], F32) for _ in range(2)]
            ogb_2 = [sb(st, "ogb", [128, 512], BF16) for _ in range(2)]
            tm_2 = [sb(st, "tm", [128, 1292], F32) for _ in range(2)]
            ssh_2 = [sb(st, "ssh", [128, 20], F32) for _ in range(2)]
            rs20_2 = [sb(st, "rs20", [128, 20], F32) for _ in range(2)]
            nrm_2 = [sb(st, "nrm", [128, 20, 64], F32) for _ in range(2)]
            nqu_2 = [sb(st, "nqu", [128, 256], BF16) for _ in range(2)]
            rt_2 = [[sb(st, "rt", [128, 20, 8], F32) for _ in range(4)] for _ in range(2)]
            nb_2 = [sb(st, "nb", [128, 20, 64], BF16) for _ in range(2)]
            tmb_2 = [sb(st, "tmb", [128, 1280], BF16) for _ in range(2)]
            tTa_2 = [sb(st, "tTa", [128, 8, 128], BF16) for _ in range(2)]
            tTb_2 = [sb(st, "tTb", [128, 2, 128], BF16) for _ in range(2)]
            ngs_2 = [sb(st, "ngs", [128, 12], F32) for _ in range(2)]
            pT_2 = [ps(st, "pT", [128, 8, 128], BF16) for _ in range(2)]
            pT = pT_2[0]
            pTb = ps(st, "pTb", [128, 2, 128], BF16)
            pf = [ps(st, "pf", [128, 512], F32) for _ in range(2)]
            pg = [ps(st, "pg", [4, 512], F32) for _ in range(1)]
            pt = [ps(st, "pt", [128, 512], F32) for _ in range(2)]
            pfi = 0
            pti = 0

            def load_x(b):
                P.dma('sp', xt[b % 2][:], xsrc[b * 512:(b + 1) * 512, :].rearrange("(j p) d -> p j d", p=128),
                      reads=[('xres', 2 * b), ('xres', 2 * b + 1)], writes=[('xt', b % 2)])
            load_x(0)
            pend_chain = []
            for b in range(NB):
                t0 = b * 512
                if b + 1 < NB:
                    load_x(b + 1)
                def pro_norm(bb):
                    z = bb % 2
                    x_ = xt[z]
                    kx = ('xt', z)
                    ss, rstd, h = ss_2[z], rstd_2[z], h_2[z]
                    for j in range(4):
                        P.op('act', lambda e: e.activation(out=junk2_2[0][:, 0:1024], in_=x_[:, j, :], func=AF.Square,
                                                           scale=1.0 / 32, accum_out=ss[:, j:j + 1]),
                             reads=[kx], writes=[('junk2', 0), ('ss', z)])
                    P.op('dve', lambda e: e.tensor_scalar(out=ss[:], in0=ss[:], scalar1=EPS, scalar2=None, op0=ALU.add),
                         reads=[('ss', z)], writes=[('ss', z)])
                    P.op('pool', lambda e: e.tensor_tensor(out=rstd[:], in0=ss[:], in1=mh20[:, 0:4], op=ALU.pow),
                         reads=[('ss', z), 'mh20'], writes=[('rstd', z)])
                    for j in range(4):
                        P.op('dve', lambda e: e.scalar_tensor_tensor(out=h[:, j, :], in0=x_[:, j, :], scalar=rstd[:, j:j + 1],
                                                                     in1=gmix[:], op0=ALU.mult, op1=ALU.mult),
                             reads=[kx, ('rstd', z), 'gmix'], writes=[('h', j)])

                def pro_tr(bb):
                    z = bb % 2
                    h, hT_ = h_2[z], hT_2[z]
                    for j in range(4):
                        pT = pT_2[j % 2]
                        for kt in range(8):
                            P.op('pe', lambda e: e.transpose(out=pT[:, kt, :], in_=h[:, j, kt * 128:(kt + 1) * 128],
                                                             identity=identb[:]),
                                 reads=[('h', j), 'identb'], writes=[('pT', j % 2)])
                        P.op(['act', 'dve'][j % 2], lambda e: (e.copy if j % 2 == 0 else e.tensor_copy)(out=hT_[:, :, j * 128:(j + 1) * 128], in_=pT[:]),
                             reads=[('pT', j % 2)], writes=[('hT', z, j)])
                if b == 0:
                    pro_norm(0)
                    pro_tr(0)
                hT = hT_2[b % 2]
                hk = [('hT', b % 2, j) for j in range(4)]
                wk = [('w_in', kt) for kt in range(8)]
                def ft_s1(ft):
                    nonlocal pfi
                    c0 = ft * 128 if ft < 8 else 3080
                    p_ = pf[pfi % 2]
                    pk = ('pf', pfi % 2)
                    pfi += 1
                    for kt in range(8):
                        P.op('pe', lambda e: e.matmul(p_[:], lhsT=w_bf[:, kt, c0:c0 + 128], rhs=hT[:, kt, :],
                                                      start=(kt == 0), stop=(kt == 7)),
                             reads=hk + [wk[kt]], writes=[pk])
                    f_ = fo[ft % 2]
                    fk = ('fo', ft % 2)
                    if ft < 8:
                        cb = cbuf[ft]
                        ck = ('cbuf', ft)
                        P.op('act', lambda e: e.copy(out=cb[:, 3:515], in_=p_[:]), reads=[pk], writes=[ck])
                    else:
                        P.op('act', lambda e: e.copy(out=f_[:], in_=p_[:]), reads=[pk], writes=[fk])
                        P.dma('auto', self.kcvcT[:, t0:t0 + 512], f_[:], reads=[fk], writes=[('kcvcT', b)])

                def ft_s2(ft):
                    if ft >= 8:
                        return
                    f_ = fo[ft % 2]
                    fk = ('fo', ft % 2)
                    cb = cbuf[ft]
                    ck = ('cbuf', ft)
                    acc = acc_2[ft % 2]
                    ak = ('acc', ft % 2)
                    P.op('dve', lambda e: e.tensor_scalar(out=acc[:], in0=cb[:, 0:512], scalar1=convw[:, 0, ft:ft + 1],
                                                          scalar2=None, op0=ALU.mult),
                         reads=[ck, 'convw'], writes=[ak])
                    for jj in range(1, 4):
                        P.op('dve', lambda e: e.scalar_tensor_tensor(out=acc[:], in0=cb[:, jj:jj + 512],
                                                                     scalar=convw[:, jj, ft:ft + 1], in1=acc[:],
                                                                     op0=ALU.mult, op1=ALU.add),
                             reads=[ck, 'convw', ak], writes=[ak])
                    P.op('act', lambda e: e.activation(out=f_[:], in_=acc[:], func=AF.Silu), reads=[ak], writes=[fk])
                    P.dma('auto', self.qkT[ft * 128:(ft + 1) * 128, t0:t0 + 512], f_[:], reads=[fk],
                          writes=[('qkT', ft, b)])
                    P.op('pool', lambda e: e.tensor_copy(out=cb[:, 0:3], in_=cb[:, 512:515]), reads=[ck], writes=[ck])
                ft_s1(0)
                for ft in range(9):
                    if ft + 1 < 9:
                        ft_s1(ft + 1)
                    ft_s2(ft)
                if b + 1 < NB:
                    pro_norm(b + 1)
                def gate_mm(gi_):
                    c0 = 2048 + 4 * gi_
                    for kt in range(8):
                        P.op('pe', lambda e: e.matmul(pg[0][:], lhsT=w_bf[:, kt, c0:c0 + 4], rhs=hT[:, kt, :],
                                                      start=(kt == 0), stop=(kt == 7)),
                             reads=hk + [wk[kt]], writes=[('pg', 0)])
                gate_mm(0)
                P.op('act', lambda e: e.activation(out=gsb[:], in_=pg[0][:], func=AF.Identity, bias=bi[:, 0:1]),
                     reads=[('pg', 0), 'bi'], writes=['gsb'])
                P.dma('auto', self.gi_d[:, t0:t0 + 512], gsb[:], reads=['gsb'], writes=[('gi_d', b)])
                gate_mm(1)
                P.op('act', lambda e: e.activation(out=gsb2[:], in_=pg[0][:], func=AF.Exp, bias=nbf[:, 0:1], scale=-1.0),
                     reads=[('pg', 0), 'nbf'], writes=['gsb2'])
                P.op('act', lambda e: e.activation(out=gsb2[:], in_=gsb2[:], func=AF.Ln, bias=1.0),
                     reads=['gsb2'], writes=['gsb2'])
                P.op('dve', lambda e: e.tensor_scalar(out=gsb2[:], in0=gsb2[:], scalar1=-1.0, scalar2=None, op0=ALU.mult),
                     reads=['gsb2'], writes=['gsb2'])
                P.dma('auto', self.gf_d[:, t0:t0 + 512], gsb2[:], reads=['gsb2'], writes=[('gf_d', b)])
                for j in range(4):
                    tok = slice(t0 + j * 128, t0 + (j + 1) * 128)
                    jg = b * 4 + j
                    z_ = jg % 2
                    mvb, sg, ogb, tm, ssh, rs20, nrm, nqu, nb, tmb, tTa, tTb, ngs = (
                        mvb_2[z_], sg_2[z_], ogb_2[z_], tm_2[z_], ssh_2[z_], rs20_2[z_], nrm_2[z_], nqu_2[z_], nb_2[z_],
                        tmb_2[z_], tTa_2[z_], tTb_2[z_], ngs_2[z_])
                    rt = rt_2[z_]
                    pT = pT_2[z_]
                    junk2 = junk2_2[z_]

                    def tm_mm(c0, n):
                        nonlocal pti
                        p_ = pt[pti % 2]
                        pk = ('pt', pti % 2)
                        pti += 1
                        for kt in range(8):
                            P.op('pe', lambda e: e.matmul(p_[:, 0:n], lhsT=hT[:, kt, j * 128:(j + 1) * 128],
                                                          rhs=w_bf[:, kt, c0:c0 + n], start=(kt == 0), stop=(kt == 7)),
                                 reads=[('hT', b % 2, j), wk[kt]], writes=[pk])
                        if pend_chain:
                            for _ in range(5):
                                try:
                                    next(pend_chain[0])
                                except StopIteration:
                                    pend_chain.pop(0)
                                    break
                        return p_, pk
                    p_, pk = tm_mm(1024, 512)
                    P.op('act', lambda e: e.copy(out=mvb[:], in_=p_[:]), reads=[pk], writes=[('mvb', z_)])
                    P.dma('auto', self.mv_d[tok, :], mvb[:], reads=[('mvb', z_)], writes=[('mv_d', jg)])
                    p_, pk = tm_mm(1536, 512)
                    P.op('act', lambda e: e.activation(out=sg[:], in_=p_[:], func=AF.Sigmoid), reads=[pk], writes=[('sg', z_)])
                    P.op('pool', lambda e: e.tensor_tensor(out=ogb[:], in0=sg[:], in1=mnorm[:], op=ALU.mult),
                         reads=[('sg', z_), 'mnorm'], writes=[('ogb', z_)])
                    P.dma('auto', self.og_d[tok, :], ogb[:], reads=[('ogb', z_)], writes=[('og_d', jg)])
                    p_, pk = tm_mm(2056, 512)
                    P.op('dve', lambda e: e.tensor_copy(out=tm[:, 0:512], in_=p_[:]), reads=[pk], writes=[('tm', z_)])
                    p_, pk = tm_mm(2568, 512)
                    P.op('act', lambda e: e.copy(out=tm[:, 512:1024], in_=p_[:]), reads=[pk], writes=[('tm', z_)])
                    p_, pk = tm_mm(3208, 268)
                    P.op('dve', lambda e: e.tensor_copy(out=tm[:, 1024:1292], in_=p_[:, 0:268]), reads=[pk], writes=[('tm', z_)])
                    def chain(tok=tok, jg=jg, z_=z_, tm=tm, ssh=ssh, rs20=rs20, nrm=nrm, nqu=nqu, nb=nb, tmb=tmb, tTa=tTa,
                              tTb=tTb, ngs=ngs, rt=rt, junk2=junk2, pT=pT):
                        yield
                        P.op('act', lambda e: e.activation(out=junk2[:, 0:1280], in_=tm[:, 0:1280], func=AF.Square),
                             reads=[('tm', z_)], writes=[('junk2', z_)])
                        yield
                        P.op('dve', lambda e: e.tensor_reduce(out=ssh[:], in_=junk2[:, 0:1280].rearrange("p (h d) -> p h d", d=64),
                                                              axis=AX.X, op=ALU.add),
                             reads=[('junk2', z_)], writes=[('ssh', z_)])
                        yield
                        P.op('dve', lambda e: e.tensor_scalar(out=ssh[:], in0=ssh[:], scalar1=1.0 / 64, scalar2=EPS,
                                                              op0=ALU.mult, op1=ALU.add), reads=[('ssh', z_)], writes=[('ssh', z_)])
                        yield
                        P.op('pool', lambda e: e.tensor_tensor(out=rs20[:], in0=ssh[:], in1=mh20[:], op=ALU.pow),
                             reads=[('ssh', z_), 'mh20'], writes=[('rs20', z_)])
                        yield
                        P.op('dve', lambda e: e.tensor_tensor(out=nrm[:], in0=tm[:, 0:1280].rearrange("p (h d) -> p h d", d=64),
                                                              in1=rs20[:].unsqueeze(2).broadcast_to([128, 20, 64]), op=ALU.mult),
                             reads=[('tm', z_), ('rs20', z_)], writes=[('nrm', z_)])
                        yield
                        P.op('dve', lambda e: e.tensor_tensor(out=nrm[:], in0=nrm[:], in1=gain20[:], op=ALU.mult),
                             reads=[('nrm', z_), 'gain20'], writes=[('nrm', z_)])
                        yield
                        P.op('act', lambda e: e.copy(out=nqu[:].rearrange("p (h d) -> p h d", d=64), in_=nrm[:, 12:16, :]),
                             reads=[('nrm', z_)], writes=[('nqu', z_)])
                        cb_ = cosT[:, jg, :].unsqueeze(1).broadcast_to([128, 20, 8])
                        sb_ = sinT[:, jg, :].unsqueeze(1).broadcast_to([128, 20, 8])
                        x1 = nrm[:, :, 0:8]
                        x2 = nrm[:, :, 8:16]
                        yield
                        P.op('dve', lambda e: e.tensor_tensor(out=rt[0][:], in0=x1, in1=cb_, op=ALU.mult),
                             reads=[('nrm', z_), 'cos'], writes=[('rt', 0, z_)])
                        yield
                        P.op('pool', lambda e: e.tensor_tensor(out=rt[1][:], in0=x2, in1=sb_, op=ALU.mult),
                             reads=[('nrm', z_), 'sin'], writes=[('rt', 1, z_)])
                        yield
                        P.op('dve', lambda e: e.tensor_tensor(out=rt[2][:], in0=x2, in1=cb_, op=ALU.mult),
                             reads=[('nrm', z_), 'cos'], writes=[('rt', 2, z_)])
                        yield
                        P.op('pool', lambda e: e.tensor_tensor(out=rt[3][:], in0=x1, in1=sb_, op=ALU.mult),
                             reads=[('nrm', z_), 'sin'], writes=[('rt', 3, z_)])
                        yield
                        P.op('dve', lambda e: e.tensor_tensor(out=x1, in0=rt[0][:], in1=rt[1][:], op=ALU.subtract),
                             reads=[('rt', 0, z_), ('rt', 1, z_)], writes=[('nrm', z_)])
                        yield
                        P.op('dve', lambda e: e.tensor_tensor(out=x2, in0=rt[2][:], in1=rt[3][:], op=ALU.add),
                             reads=[('rt', 2, z_), ('rt', 3, z_)], writes=[('nrm', z_)])
                        yield
                        P.op('act', lambda e: e.copy(out=nb[:], in_=nrm[:]), reads=[('nrm', z_)], writes=[('nb', z_)])
                        yield
                        P.op('act', lambda e: e.copy(out=tmb[:], in_=tm[:, 0:1280]), reads=[('tm', z_)], writes=[('tmb', z_)])
                        yield
                        P.op('act', lambda e: e.activation(out=ngs[:], in_=tm[:, 1280:1292], func=AF.Sigmoid),
                             reads=[('tm', z_)], writes=[('ngs', z_)])
                        srcs = [nb[:, 0:2, :], nb[:, 2:4, :], nb[:, 4:6, :], nb[:, 6:8, :], nb[:, 12:14, :], nb[:, 14:16, :]]
                        yield
                        for i, s_ in enumerate(srcs):
                            P.op('pe', lambda e: e.transpose(out=pT[:, i, :], in_=s_.rearrange("p h d -> p (h d)"),
                                                             identity=identb[:]), reads=[('nb', z_), 'identb'], writes=[('pT', z_)])
                        yield
                        for i in range(2):
                            P.op('pe', lambda e: e.transpose(out=pT[:, 6 + i, :], in_=nqu[:, i * 128:(i + 1) * 128],
                                                             identity=identb[:]), reads=[('nqu', z_), 'identb'], writes=[('pT', z_)])
                        yield
                        for i, hh in enumerate((16, 18)):
                            P.op('pe', lambda e: e.transpose(out=pTb[:, i, :], in_=nb[:, hh:hh + 2, :].rearrange("p h d -> p (h d)"),
                                                             identity=identb[:]), reads=[('nb', z_), 'identb'], writes=['pTb'])
                        yield
                        P.op('dve', lambda e: e.tensor_copy(out=tTa[:], in_=pT[:]), reads=[('pT', z_)], writes=[('tTa', z_)])
                        yield
                        P.op('act', lambda e: e.copy(out=tTb[:], in_=pTb[:]), reads=['pTb'], writes=[('tTb', z_)])
                        yield
                        for i, dst in enumerate((self.bqT, self.bkT, self.nqrT, self.nquT)):
                            P.dma('auto', dst[:, tok].rearrange("(a p) t -> p a t", p=128), tTa[:, 2 * i:2 * i + 2, :],
                                  reads=[('tTa', z_)], writes=[(("bqT","bkT","nqrT","nquT")[i], jg)])
                        yield
                        P.dma('auto', self.ksT[:, tok], tTb[0:64, 0, :], reads=[('tTb', z_)], writes=[('ksT', jg)])
                        yield
                        P.dma('auto', self.kwT[:, tok], tTb[0:64, 1, :], reads=[('tTb', z_)], writes=[('kwT', jg)])
                        yield
                        P.dma('auto', self.bv_d[tok, :], tmb[:, 512:768], reads=[('tmb', z_)], writes=[('bv_d', jg)])
                        yield
                        P.dma('auto', self.vs_d[tok, :], tmb[:, 1088:1152], reads=[('tmb', z_)], writes=[('vs_d', jg)])
                        yield
                        P.dma('auto', self.vw_d[tok, :], tmb[:, 1216:1280], reads=[('tmb', z_)], writes=[('vw_d', jg)])
                        yield
                        P.dma('auto', self.ng_d[tok, :], ngs[:], reads=[('ngs', z_)], writes=[('ng_d', jg)])
                    for _ in (pend_chain.pop(0) if pend_chain else ()):
                        pass
                    pend_chain.append(chain())
                    if j == 1 and b + 1 < NB:
                        pro_tr(b + 1)
            while pend_chain:
                for _ in pend_chain.pop(0):
                    pass
            self.barrier()

    def phase3(self, l, xsrc, xdst):
        S, P, nc = self.S, self.P, self.nc
        NB = S // 256
        with ExitStack() as st:
            sb, ps = self.sb, self.ps
            wo = sb(st, "wo", [128, 8, 1024], BF16)
            w1 = sb(st, "w1", [128, 8, 4096], BF16)
            w2 = sb(st, "w2", [128, 32, 1024], BF16)
            with ExitStack() as st2:
                stage = [sb(st2, "wst3", [128, 4096], F32) for _ in range(2)]
                slabs = []
                for i in range(2):
                    slabs.append((self.w_out[l, i * 512:(i + 1) * 512, :].rearrange("(a p) d -> p a d", p=128),
                                  wo[:, i * 4:(i + 1) * 4, :], ('wo', i), True))
                for kt in range(8):
                    slabs.append((self.w_ff1[l, kt * 128:(kt + 1) * 128, :], w1[:, kt, :], ('w1', kt), False))
                for i in range(8):
                    slabs.append((self.w_ff2[l, i * 512:(i + 1) * 512, :].rearrange("(a p) d -> p a d", p=128),
                                  w2[:, i * 4:(i + 1) * 4, :], ('w2', i), True))
                for n, (src, dst, key, three) in enumerate(slabs):
                    s_ = stage[n % 2]
                    sv = s_[:].rearrange("p (a d) -> p a d", a=4) if three else s_[:]
                    P.dma('sp', sv, src, writes=[('wst3', n % 2)])
                    eng = ['pool', 'dve', 'act'][n % 3]
                    if eng == 'act':
                        P.op(eng, lambda e: e.copy(out=dst, in_=sv), reads=[('wst3', n % 2)], writes=[key])
                    else:
                        P.op(eng, lambda e: e.tensor_copy(out=dst, in_=sv), reads=[('wst3', n % 2)], writes=[key])
                self.barrier()
            identb = sb(st, "identb3", [128, 128], BF16)
            gffn = sb(st, "gffn", [128, 1024], F32)
            P.dma('sp', identb[:], self.c_identb, writes=['identb'])
            P.dma('sp', gffn[:], self.norm_ffn[l].partition_broadcast(128), writes=['gffn'])
            xt_2 = [sb(st, "xt3", [128, 2, 1024], F32) for _ in range(2)]
            mixb_2 = [sb(st, "mixb", [128, 2, 1024], BF16) for _ in range(2)]
            mh3 = sb(st, "mh3", [128, 2], F32)
            P.op('pool', lambda e: e.memset(mh3[:], -0.5), writes=['mh3'])
            mT = sb(st, "mT", [128, 8, 256], BF16)
            hT = sb(st, "hT3", [128, 8, 256], BF16)
            uT = sb(st, "uT", [128, 32, 256], BF16)
            rr = [sb(st, "rr", [128, 256], F32) for _ in range(2)]
            junk = sb(st, "junk3", [128, 1024], BF16)
            ss = sb(st, "ss3", [128, 2], F32)
            rstd = sb(st, "rstd3", [128, 2], F32)
            pT = ps(st, "pT3", [128, 8, 128], BF16)
            py = [ps(st, "py", [128, 512], F32) for _ in range(2)]
            pz = [ps(st, "pz", [128, 256], F32) for _ in range(2)]
            pyi = 0
            pzi = 0
            wok = [('wo', 0), ('wo', 1)]
            def load3(b):
                t0_ = b * 256
                P.dma('sp', xt_2[b % 2][:], xsrc[t0_:t0_ + 256, :].rearrange("(j p) d -> p j d", p=128), reads=[('xres', b)],
                      writes=[('xt', b % 2)])
                P.dma('sp', mixb_2[b % 2][:], self.mix_d[t0_:t0_ + 256, :].rearrange("(j p) d -> p j d", p=128),
                      writes=[('mixb', b % 2)])
            load3(0)
            def frontA(b):
                nonlocal pyi, pzi
                t0 = b * 256
                xt = xt_2[b % 2]
                mixb = mixb_2[b % 2]
                xk = ('xt', b % 2)
                mk = ('mixb', b % 2)
                for j in range(2):
                    for kt in range(8):
                        P.op('pe', lambda e: e.transpose(out=pT[:, kt, :], in_=mixb[:, j, kt * 128:(kt + 1) * 128],
                                                         identity=identb[:]), reads=[mk, 'identb'], writes=['pT'])
                    P.op('act', lambda e: e.copy(out=mT[:, :, j * 128:(j + 1) * 128], in_=pT[:]),
                         reads=['pT'], writes=[('mT', j)])
                for j in range(2):
                    for hf in range(2):
                        p_ = py[pyi % 2]
                        pk = ('py', pyi % 2)
                        pyi += 1
                        for kt in range(8):
                            P.op('pe', lambda e: e.matmul(p_[:], lhsT=mT[:, kt, j * 128:(j + 1) * 128],
                                                          rhs=wo[:, kt, hf * 512:(hf + 1) * 512],
                                                          start=(kt == 0), stop=(kt == 7)),
                                 reads=[('mT', j), wok[kt // 4]], writes=[pk])
                        P.op('dve', lambda e: e.tensor_tensor(out=xt[:, j, hf * 512:(hf + 1) * 512],
                                                              in0=xt[:, j, hf * 512:(hf + 1) * 512], in1=p_[:], op=ALU.add),
                             reads=[xk, pk], writes=[xk])
                for j in range(2):
                    P.op('act', lambda e: e.activation(out=junk[:], in_=xt[:, j, :], func=AF.Square, scale=1.0 / 32,
                                                       accum_out=ss[:, j:j + 1]), reads=[xk], writes=['junk', 'ss'])
                P.op('dve', lambda e: e.tensor_scalar(out=ss[:], in0=ss[:], scalar1=EPS, scalar2=None, op0=ALU.add),
                     reads=['ss'], writes=['ss'])
                P.op('pool', lambda e: e.tensor_tensor(out=rstd[:], in0=ss[:], in1=mh3[:], op=ALU.pow), reads=['ss', 'mh3'], writes=['rstd'])
                for j in range(2):
                    P.op('dve', lambda e: e.scalar_tensor_tensor(out=mixb[:, j, :], in0=xt[:, j, :], scalar=rstd[:, j:j + 1],
                                                                 in1=gffn[:], op0=ALU.mult, op1=ALU.mult),
                         reads=[xk, 'rstd', 'gffn'], writes=[mk])

            def frontB(b):
                nonlocal pyi, pzi
                t0 = b * 256
                xt = xt_2[b % 2]
                mixb = mixb_2[b % 2]
                xk = ('xt', b % 2)
                mk = ('mixb', b % 2)
                for j in range(2):
                    for kt in range(8):
                        P.op('pe', lambda e: e.transpose(out=pT[:, kt, :], in_=mixb[:, j, kt * 128:(kt + 1) * 128],
                                                         identity=identb[:]), reads=[mk, 'identb'], writes=['pT'])
                    P.op('act', lambda e: e.copy(out=hT[:, :, j * 128:(j + 1) * 128], in_=pT[:]),
                         reads=['pT'], writes=[('hT', j)])

            def ffn1(b):
                nonlocal pyi, pzi
                t0 = b * 256
                xt = xt_2[b % 2]
                mixb = mixb_2[b % 2]
                xk = ('xt', b % 2)
                mk = ('mixb', b % 2)
                hk = [('hT', 0), ('hT', 1)]
                for ft in range(32):
                    p_ = pz[pzi % 2]
                    pk = ('pz', pzi % 2)
                    r_ = rr[pzi % 2]
                    rk = ('rr', pzi % 2)
                    pzi += 1
                    for kt in range(8):
                        P.op('pe', lambda e: e.matmul(p_[:], lhsT=w1[:, kt, ft * 128:(ft + 1) * 128], rhs=hT[:, kt, :],
                                                      start=(kt == 0), stop=(kt == 7)),
                             reads=hk + [('w1', kt)], writes=[pk])
                    P.op('act', lambda e: e.activation(out=r_[:], in_=p_[:], func=AF.Relu), reads=[pk], writes=[rk])
                    P.op(['dve', 'pool'][ft % 2], lambda e: e.tensor_tensor(out=uT[:, ft, :], in0=r_[:], in1=r_[:], op=ALU.mult),
                         reads=[rk], writes=[('uT', ft)])

            def ffn2(b):
                nonlocal pyi, pzi
                t0 = b * 256
                xt = xt_2[b % 2]
                mixb = mixb_2[b % 2]
                xk = ('xt', b % 2)
                mk = ('mixb', b % 2)
                uk = [('uT', ft) for ft in range(32)]
                for j in range(2):
                    for hf in range(2):
                        p_ = py[pyi % 2]
                        pk = ('py', pyi % 2)
                        pyi += 1
                        for ft in range(32):
                            P.op('pe', lambda e: e.matmul(p_[:], lhsT=uT[:, ft, j * 128:(j + 1) * 128],
                                                          rhs=w2[:, ft, hf * 512:(hf + 1) * 512],
                                                          start=(ft == 0), stop=(ft == 31)),
                                 reads=[uk[ft], ('w2', ft // 4)], writes=[pk])
                        P.op('dve', lambda e: e.tensor_tensor(out=xt[:, j, hf * 512:(hf + 1) * 512],
                                                              in0=xt[:, j, hf * 512:(hf + 1) * 512], in1=p_[:], op=ALU.add),
                             reads=[xk, pk], writes=[xk])
                P.dma('sp', xdst[t0:t0 + 256, :].rearrange("(j p) d -> p j d", p=128), xt[:], reads=[xk],
                      writes=[('xres', b)])
            frontA(0)
            frontB(0)
            for b in range(NB):
                if b + 1 < NB:
                    load3(b + 1)
                ffn1(b)
                if b + 1 < NB:
                    frontA(b + 1)
                ffn2(b)
                if b + 1 < NB:
                    frontB(b + 1)
            self.barrier()

    def _ml_norm(self, P, s_, sk, pO_t, pok, junk, flT, mhalf, ym_t, ymk, og_t, ogk, hh, c, cl):
        P.op('act', lambda e: e.activation(out=junk[:], in_=pO_t[0:64, 0:128], func=AF.Square,
                                           scale=128.0 ** -0.5, accum_out=s_[:, 0:1]),
             reads=[pok], writes=['junk', sk])
        P.op('dve', lambda e: e.tensor_scalar(out=s_[:, 6:7], in0=pO_t[0:64, 128:129], scalar1=-1.0,
                                              scalar2=flT[:, hh, c:c + 1], op0=ALU.mult, op1=ALU.max),
             reads=[pok, 'flT'], writes=[sk])
        P.op('dve', lambda e: e.tensor_tensor(out=s_[:, 1:2], in0=s_[:, 6:7], in1=pO_t[0:64, 128:129], op=ALU.max),
             reads=[pok, sk], writes=[sk])
        P.op('dve', lambda e: e.tensor_tensor(out=s_[:, 2:3], in0=s_[:, 1:2], in1=s_[:, 1:2], op=ALU.mult),
             reads=[sk], writes=[sk])
        P.op('dve', lambda e: e.scalar_tensor_tensor(out=s_[:, 3:4], in0=s_[:, 2:3], scalar=EPS, in1=s_[:, 0:1],
                                                     op0=ALU.mult, op1=ALU.add), reads=[sk], writes=[sk])
        P.op('pool', lambda e: e.tensor_tensor(out=s_[:, 5:6], in0=s_[:, 3:4], in1=mhalf[:, 0:1], op=ALU.pow),
             reads=[sk, 'mhalf'], writes=[sk])
        P.op('dve', lambda e: e.scalar_tensor_tensor(out=ym_t[:, cl, hh * 128:(hh + 1) * 128],
                                                     in0=pO_t[0:64, 0:128], scalar=s_[:, 5:6],
                                                     in1=og_t[:, cl, hh * 128:(hh + 1) * 128],
                                                     op0=ALU.mult, op1=ALU.mult),
             reads=[pok, sk, ogk], writes=[ymk])

    def phase2_mlstm(self, l):
        S, P, nc = self.S, self.P, self.nc
        NCH = S // 64
        LNSC = math.log(128.0 ** -0.5)
        with ExitStack() as st:
            sb, ps = self.sb, self.ps
            identb = sb(st, "identbm", [128, 128], BF16)
            identf = sb(st, "identfm", [128, 128], F32)
            ut = sb(st, "ut", [128, 128], F32)
            tri = sb(st, "tri", [128, 128], F32)
            P.dma('sp', identb[:], self.c_identb, writes=['identb'])
            P.dma('sp', identf[:], self.c_identf, writes=['identf'])
            P.dma('sp', ut[:], self.c_ut, writes=['ut'])
            P.dma('sp', tri[:], self.c_tri, writes=['tri'])
            uT = sb(st, "uT_m", [64, 4, NCH], F32)
            u2T = sb(st, "u2T_m", [64, 4, NCH], F32)
            flT = sb(st, "flT_m", [64, 4, NCH], F32)
            decB = sb(st, "decB", [128, 4, NCH], F32)
            with ExitStack() as s2:
                li = sb(s2, "li", [NCH, 4, 64], F32)
                lf = sb(s2, "lf", [NCH, 4, 64], F32)
                ones = sb(s2, "ones", [NCH, 64], F32)
                Fin = sb(s2, "Fin", [NCH, 4, 64], F32)
                Ft = sb(s2, "Ft", [NCH, 4, 64], F32)
                a_ = sb(s2, "a_", [NCH, 4, 64], F32)
                Ain = sb(s2, "Ain", [NCH, 4, 64], F32)
                tot = sb(s2, "tot", [NCH, 4], F32)
                cmax = sb(s2, "cmax", [NCH, 4], F32)
                cmT = sb(s2, "cmT", [4, NCH], F32)
                ET = sb(s2, "ET", [4, NCH], F32)
                ETn = sb(s2, "ETn", [4, NCH], F32)
                Ec = sb(s2, "Ec", [NCH, 4], F32)
                Enc = sb(s2, "Enc", [NCH, 4], F32)
                tmp = sb(s2, "tmpm", [NCH, 4, 64], F32)
                uu = sb(s2, "uu", [NCH, 4, 64], F32)
                uu2 = sb(s2, "uu2", [NCH, 4, 64], F32)
                fl = sb(s2, "fl", [NCH, 4, 64], F32)
                dec = sb(s2, "dec", [NCH, 4], F32)
                decrep = sb(s2, "decrep", [NCH, 4, 128], F32)
                pa = ps(s2, "pa", [128, 512], F32)
                pb = ps(s2, "pb", [128, 512], F32)
                P.dma('sp', li[:], self.gi_d.rearrange("h (c j) -> c h j", j=64),
                      reads=[('gi_d', b) for b in range(S // 512)], writes=['li'])
                P.dma('sp', lf[:], self.gf_d.rearrange("h (c j) -> c h j", j=64),
                      reads=[('gf_d', b) for b in range(S // 512)], writes=['lf'])
                P.op('pool', lambda e: e.memset(ones[:], 1.0), writes=['ones'])
                for hh in range(4):
                    P.op('dve', lambda e: e.tensor_tensor_scan(out=Fin[:, hh, :], data0=ones[:], data1=lf[:, hh, :],
                                                               initial=0.0, op0=ALU.mult, op1=ALU.add),
                         reads=['ones', 'lf'], writes=['Fin'])
                P.op('dve', lambda e: e.tensor_copy(out=tot[:], in_=Fin[:, :, 63]), reads=['Fin'], writes=['tot'])
                P.op('pe', lambda e: e.matmul(pa[0:NCH, 0:4], lhsT=ut[0:NCH, 0:NCH], rhs=tot[:], start=True, stop=True),
                     reads=['ut', 'tot'], writes=['pa'])
                P.op('dve', lambda e: e.tensor_tensor(out=Ft[:], in0=Fin[:],
                                                      in1=pa[0:NCH, 0:4].unsqueeze(2).broadcast_to([NCH, 4, 64]), op=ALU.add),
                     reads=['Fin', 'pa'], writes=['Ft'])
                P.op('dve', lambda e: e.tensor_tensor(out=a_[:], in0=li[:], in1=Ft[:], op=ALU.subtract),
                     reads=['li', 'Ft'], writes=['a_'])
                for hh in range(4):
                    P.op('dve', lambda e: e.tensor_tensor_scan(out=Ain[:, hh, :], data0=a_[:, hh, :], data1=a_[:, hh, :],
                                                               initial=-1e30, op0=ALU.max, op1=ALU.max),
                         reads=['a_'], writes=['Ain'])
                P.op('dve', lambda e: e.tensor_copy(out=cmax[:], in_=Ain[:, :, 63]), reads=['Ain'], writes=['cmax'])
                P.op('pe', lambda e: e.transpose(out=pb[0:4, 0:NCH], in_=cmax[:], identity=identf[0:NCH, 0:NCH]),
                     reads=['cmax', 'identf'], writes=['pb'])
                P.op('dve', lambda e: e.tensor_copy(out=cmT[:], in_=pb[0:4, 0:NCH]), reads=['pb'], writes=['cmT'])
                P.op('dve', lambda e: e.tensor_tensor_scan(out=ET[:], data0=cmT[:], data1=cmT[:], initial=0.0,
                                                           op0=ALU.max, op1=ALU.max), reads=['cmT'], writes=['ET'])
                if NCH > 1:
                    P.op('dve', lambda e: e.tensor_copy(out=ETn[:, 0:NCH - 1], in_=ET[:, 1:NCH]), reads=['ET'], writes=['ETn'])
                P.op('dve', lambda e: e.tensor_copy(out=ETn[:, NCH - 1:NCH], in_=ET[:, NCH - 1:NCH]), reads=['ET'], writes=['ETn'])
                P.op('pe', lambda e: e.transpose(out=pa[0:NCH, 0:4], in_=ET[:], identity=identf[0:4, 0:4]),
                     reads=['ET', 'identf'], writes=['pa'])
                P.op('dve', lambda e: e.tensor_copy(out=Ec[:], in_=pa[0:NCH, 0:4]), reads=['pa'], writes=['Ec'])
                P.op('pe', lambda e: e.transpose(out=pb[0:NCH, 0:4], in_=ETn[:], identity=identf[0:4, 0:4]),
                     reads=['ETn', 'identf'], writes=['pb'])
                P.op('dve', lambda e: e.tensor_copy(out=Enc[:], in_=pb[0:NCH, 0:4]), reads=['pb'], writes=['Enc'])
                Eb = Ec[:].unsqueeze(2).broadcast_to([NCH, 4, 64])
                Enb = Enc[:].unsqueeze(2).broadcast_to([NCH, 4, 64])
                P.op('dve', lambda e: e.tensor_tensor(out=tmp[:], in0=a_[:], in1=Eb, op=ALU.subtract),
                     reads=['a_', 'Ec'], writes=['tmp'])
                P.op('act', lambda e: e.activation(out=uu[:], in_=tmp[:], func=AF.Exp, bias=LNSC), reads=['tmp'], writes=['uu'])
                P.op('dve', lambda e: e.tensor_tensor(out=tmp[:], in0=a_[:], in1=Enb, op=ALU.subtract),
                     reads=['a_', 'Enc'], writes=['tmp'])
                P.op('act', lambda e: e.activation(out=uu2[:], in_=tmp[:], func=AF.Exp, bias=LNSC), reads=['tmp'], writes=['uu2'])
                P.op('dve', lambda e: e.tensor_tensor(out=tmp[:], in0=Ft[:], in1=Eb, op=ALU.add),
                     reads=['Ft', 'Ec'], writes=['tmp'])
                P.op('act', lambda e: e.activation(out=fl[:], in_=tmp[:], func=AF.Exp, scale=-1.0), reads=['tmp'], writes=['fl'])
                P.op('dve', lambda e: e.tensor_tensor(out=dec[:], in0=Ec[:], in1=Enc[:], op=ALU.subtract),
                     reads=['Ec', 'Enc'], writes=['dec'])
                P.op('act', lambda e: e.activation(out=dec[:], in_=dec[:], func=AF.Exp), reads=['dec'], writes=['dec'])
                P.op('dve', lambda e: e.tensor_copy(out=decrep[:], in_=dec[:].unsqueeze(2).broadcast_to([NCH, 4, 128])),
                     reads=['dec'], writes=['decrep'])
                for src, dst, nm in ((uu, uT, 'uT'), (uu2, u2T, 'u2T'), (fl, flT, 'flT')):
                    for hh in range(4):
                        P.op('pe', lambda e: e.transpose(out=pa[0:64, hh * 128:hh * 128 + NCH], in_=src[:, hh, :],
                                                         identity=identf[0:NCH, 0:NCH]),
                             reads=['uu', 'uu2', 'fl', 'identf'], writes=['pa'])
                    P.op('dve', lambda e: e.tensor_copy(out=dst[:], in_=pa[0:64, :].rearrange("p (h c) -> p h c", h=4)[:, :, 0:NCH]),
                         reads=['pa'], writes=[nm])
                for hh in range(4):
                    P.op('pe', lambda e: e.matmul(pb[:, hh * 128:hh * 128 + NCH], lhsT=decrep[:, hh, :],
                                                  rhs=identf[0:NCH, 0:NCH], start=True, stop=True),
                         reads=['decrep', 'identf'], writes=['pb'])
                P.op('dve', lambda e: e.tensor_copy(out=decB[:], in_=pb[:].rearrange("p (h c) -> p h c", h=4)[:, :, 0:NCH]),
                     reads=['pb'], writes=['decB'])
                self.barrier()
            NG = S // 512
            qg = [sb(st, "qg", [128, 4, 512], BF16) for _ in range(2)]
            kg = [sb(st, "kg", [128, 4, 512], BF16) for _ in range(2)]
            vg = [sb(st, "vg", [64, 8, 4, 129], BF16) for _ in range(2)]
            ogg = [sb(st, "ogg", [64, 8, 512], BF16) for _ in range(2)]
            ym = [sb(st, "ym", [64, 8, 512], BF16) for _ in range(2)]
            G = [sb(st, "G", [128, 129], F32) for _ in range(4)]
            Gb = [sb(st, "Gb", [128, 129], BF16) for _ in range(4)]
            ku2 = [sb(st, "ku2", [64, 128], BF16) for _ in range(2)]
            Sm = [sb(st, "Sm", [64, 64], BF16) for _ in range(2)]
            Smu = [sb(st, "Smu", [64, 64], F32) for _ in range(2)]
            junk = sb(st, "junkm", [64, 128], BF16)
            mhalf = sb(st, "mhalf", [64, 4], F32)
            P.op('pool', lambda e: e.memset(mhalf[:], -0.5), writes=['mhalf'])
            osb = [sb(st, "osb", [64, 4, 129], F32) for _ in range(2)]
            ssq = [sb(st, "ssq", [64, 4], F32) for _ in range(2)]
            nt = [sb(st, "nt", [64, 4, 4], F32) for _ in range(2)]
            ytmp = sb(st, "ytmp", [64, 4, 128], F32)
            sc = [sb(st, "scm", [64, 8], F32) for _ in range(2)]
            pkT = [ps(st, "pkT", [128, 1024], BF16) for _ in range(2)]
            pS = [ps(st, "pS", [128, 512], F32) for _ in range(2)]
            pO = [ps(st, "pO", [128, 512], F32) for _ in range(2)]
            pG = [ps(st, "pG", [128, 512], F32) for _ in range(2)]
            for i in range(2):
                P.op('pool', lambda e: e.memset(vg[i][:], 1.0), writes=[('vg', i)])
            for hh in range(4):
                P.op('pool', lambda e: e.memset(G[hh][:], 0.0), writes=[('G', hh)])
                P.op('pool', lambda e: e.memset(Gb[hh][:], 0.0), writes=[('Gb', hh)])

            def load_group(g):
                i = g % 2
                tk = slice(g * 512, (g + 1) * 512)
                P.dma('sp', qg[i][:], self.qkT[0:512, tk].rearrange("(h p) t -> p h t", p=128),
                      reads=[('qkT', ft, g) for ft in range(4)], writes=[('qg', i)])
                P.dma('sp', kg[i][:], self.qkT[512:1024, tk].rearrange("(h p) t -> p h t", p=128),
                      reads=[('qkT', ft, g) for ft in range(4, 8)], writes=[('kg', i)])
                for hh in range(4):
                    P.dma('sp', vg[i][:, :, hh, 0:128],
                          self.mv_d[tk, hh * 128:(hh + 1) * 128].rearrange("(c s) e -> s c e", s=64),
                          reads=[('mv_d', 4 * g + j) for j in range(4)], writes=[('vg', i)])
                P.dma('sp', ogg[i][:], self.og_d[tk, :].rearrange("(c s) e -> s c e", s=64),
                      reads=[('og_d', 4 * g + j) for j in range(4)], writes=[('ogg', i)])
            load_group(0)
            steps = [(g, cl, hh) for g in range(NG) for cl in range(8) for hh in range(4)]

            def stageA(n):
                g, cl, hh = steps[n]
                gi_ = g % 2
                c = g * 8 + cl
                i2 = n % 2
                k_ = kg[gi_][:, hh, cl * 64:(cl + 1) * 64]
                q_ = qg[gi_][:, hh, cl * 64:(cl + 1) * 64]
                P.op('pe', lambda e: e.transpose(out=pkT[i2][0:64, 0:128], in_=k_, identity=identb[:]),
                     reads=[('kg', gi_), 'identb'], writes=[('pkT', i2)])
                P.op('pe', lambda e: e.matmul(pS[i2][0:64, 0:64], lhsT=k_, rhs=q_, start=True, stop=True),
                     reads=[('kg', gi_), ('qg', gi_)], writes=[('pS', i2)])
                P.op('act', lambda e: e.activation(out=ku2[i2][:], in_=pkT[i2][0:64, 0:128], func=AF.Copy,
                                                   scale=u2T[:, hh, c:c + 1]),
                     reads=[('pkT', i2), 'u2T'], writes=[('ku2', i2)])
                P.op('act', lambda e: e.activation(out=Smu[i2][:], in_=pS[i2][0:64, 0:64], func=AF.Copy,
                                                   scale=uT[:, hh, c:c + 1]),
                     reads=[('pS', i2), 'uT'], writes=[('Smu', i2)])
                P.op('pool', lambda e: e.tensor_tensor(out=Sm[i2][:], in0=Smu[i2][:], in1=tri[0:64, 0:64], op=ALU.mult),
                     reads=[('Smu', i2), 'tri'], writes=[('Sm', i2)])

            def stageB(n):
                g, cl, hh = steps[n]
                gi_ = g % 2
                c = g * 8 + cl
                i2 = n % 2
                q_ = qg[gi_][:, hh, cl * 64:(cl + 1) * 64]
                v_ = vg[gi_][:, cl, hh, :]
                P.op('pe', lambda e: e.matmul(pO[i2][0:64, 0:129], lhsT=Sm[i2][:], rhs=v_, start=True, stop=False),
                     reads=[('Sm', i2), ('vg', gi_)], writes=[('pO', i2)])
                P.op('pe', lambda e: e.matmul(pO[i2][0:64, 0:129], lhsT=q_, rhs=Gb[hh][:], start=False, stop=True),
                     reads=[('qg', gi_), ('Gb', hh)], writes=[('pO', i2)])
                P.op('pe', lambda e: e.matmul(pG[i2][:, 0:129], lhsT=ku2[i2][:], rhs=v_, start=True, stop=True),
                     reads=[('ku2', i2), ('vg', gi_)], writes=[('pG', i2)])
                P.op('dve', lambda e: e.scalar_tensor_tensor(out=G[hh][:], in0=G[hh][:], scalar=decB[:, hh, c:c + 1],
                                                             in1=pG[i2][:, 0:129], op0=ALU.mult, op1=ALU.add),
                     reads=[('G', hh), 'decB', ('pG', i2)], writes=[('G', hh)])
                P.op('pool', lambda e: e.tensor_copy(out=Gb[hh][:], in_=G[hh][:]), reads=[('G', hh)], writes=[('Gb', hh)])
                cb = c % 2
                P.op('act', lambda e: e.activation(out=junk[:], in_=pO[i2][0:64, 0:128], func=AF.Square,
                                                   scale=128.0 ** -0.5, accum_out=ssq[cb][:, hh:hh + 1]),
                     reads=[('pO', i2)], writes=['junk', ('ssq', cb, hh)])
                P.op('act', lambda e: e.copy(out=osb[cb][:, hh, :], in_=pO[i2][0:64, 0:129]),
                     reads=[('pO', i2)], writes=[('osb', cb, hh)])

            def stageN(n):
                g, cl, hh = steps[n]
                if hh != 3:
                    return
                gi_ = g % 2
                c = g * 8 + cl
                cb = c % 2
                t_ = nt[cb]
                tk = ('nt', cb)
                den = osb[cb][:, :, 128]
                ok = [('osb', cb, h_) for h_ in range(4)]
                P.op('dve', lambda e: e.tensor_scalar(out=t_[:, 0, :], in0=den, scalar1=-1.0, scalar2=None, op0=ALU.mult),
                     reads=ok, writes=[tk])
                P.op('dve', lambda e: e.tensor_tensor(out=t_[:, 0, :], in0=t_[:, 0, :], in1=flT[:, :, c], op=ALU.max),
                     reads=[tk, 'flT'], writes=[tk])
                P.op('dve', lambda e: e.tensor_tensor(out=t_[:, 0, :], in0=t_[:, 0, :], in1=den, op=ALU.max),
                     reads=[tk] + ok, writes=[tk])
                P.op('dve', lambda e: e.tensor_tensor(out=t_[:, 1, :], in0=t_[:, 0, :], in1=t_[:, 0, :], op=ALU.mult),
                     reads=[tk], writes=[tk])
                P.op('dve', lambda e: e.scalar_tensor_tensor(out=t_[:, 2, :], in0=t_[:, 1, :], scalar=EPS, in1=ssq[cb][:],
                                                             op0=ALU.mult, op1=ALU.add),
                     reads=[tk] + [('ssq', cb, h_) for h_ in range(4)], writes=[tk])
                P.op('pool', lambda e: e.tensor_tensor(out=t_[:, 3, :], in0=t_[:, 2, :], in1=mhalf[:], op=ALU.pow),
                     reads=[tk, 'mhalf'], writes=[tk])
                P.op('dve', lambda e: e.tensor_tensor(out=ytmp[:], in0=osb[cb][:, :, 0:128],
                                                      in1=t_[:, 3, :].unsqueeze(2).broadcast_to([64, 4, 128]), op=ALU.mult),
                     reads=ok + [tk], writes=['ytmp'])
                P.op('dve', lambda e: e.tensor_tensor(out=ym[gi_][:, cl, :].rearrange("p (h d) -> p h d", h=4), in0=ytmp[:],
                                                      in1=ogg[gi_][:, cl, :].rearrange("p (h d) -> p h d", h=4), op=ALU.mult),
                     reads=['ytmp', ('ogg', gi_)], writes=[('ym', gi_)])
                if cl == 7:
                    P.dma('sp', self.mix_d[g * 512:(g + 1) * 512, 0:512].rearrange("(c s) e -> s c e", s=64), ym[gi_][:],
                          reads=[('ym', gi_)], writes=[('mixm', g)])
            NS = len(steps)
            stageA(0)
            for n in range(NS):
                g, cl, hh = steps[n]
                if n + 1 < NS:
                    stageA(n + 1)
                stageB(n)
                if n >= 1:
                    stageN(n - 1)
                if cl == 0 and hh == 1 and g + 1 < NG:
                    load_group(g + 1)
            stageN(NS - 1)
            self.barrier()

    def _negB(self, st, name, g1, g2):
        P = self.P
        ga = self.sb(st, name + "_ga", [128, 64], F32)
        gb = self.sb(st, name + "_gb", [128, 64], F32)
        m1 = self.sb(st, name + "_m1", [128, 1], F32)
        m2 = self.sb(st, name + "_m2", [128, 1], F32)
        nb = self.sb(st, name, [128, 1], F32)
        P.dma('sp', ga[:], g1.partition_broadcast(128), writes=[name + 'ga'])
        P.dma('sp', gb[:], g2.partition_broadcast(128), writes=[name + 'gb'])
        P.op('dve', lambda e: e.tensor_reduce(out=m1[:], in_=ga[:], axis=AX.X, op=ALU.max, apply_absolute_value=True),
             reads=[name + 'ga'], writes=[name + 'm1'])
        P.op('dve', lambda e: e.tensor_reduce(out=m2[:], in_=gb[:], axis=AX.X, op=ALU.max, apply_absolute_value=True),
             reads=[name + 'gb'], writes=[name + 'm2'])
        P.op('dve', lambda e: e.scalar_tensor_tensor(out=nb[:], in0=m1[:], scalar=-8.0, in1=m2[:], op0=ALU.mult, op1=ALU.mult),
             reads=[name + 'm1', name + 'm2'], writes=[name])
        return nb

    def phase2_moba(self, l):
        S, P, nc = self.S, self.P, self.nc
        NQC = S // 512
        NKT = S // 128
        NB = S // 256
        with ExitStack() as st:
            sb, ps = self.sb, self.ps
            identb = sb(st, "identb_b", [128, 128], BF16)
            tm4 = sb(st, "tm4", [128, 4, 512], BF16)
            zl = sb(st, "zl", [1, 128], BF16)
            zr = sb(st, "zr", [1, 512], BF16)
            P.dma('sp', identb[:], self.c_identb, writes=['identb'])
            P.dma('sp', tm4[:], self.c_tm4, writes=['tm4'])
            P.op('pool', lambda e: e.memset(zl[:], 0.0), writes=['zl'])
            P.op('pool', lambda e: e.memset(zr[:], 0.0), writes=['zr'])
            negB = self._negB(st, "negBm", self.moba_qk_norm[l, 0], self.moba_qk_norm[l, 1])
            KX = sb(st, "KX", [96, S], BF16)
            VX = sb(st, "VX", [128, NKT, 65], BF16)
            kmean = sb(st, "kmean", [64, 32], F32)
            kmb = sb(st, "kmb", [64, 32], BF16)
            QX = [sb(st, "QX", [96, 512], BF16) for _ in range(2)]
            gsb = sb(st, "gsb_b", [128, 32], F32)
            m8 = sb(st, "m8", [128, 8], F32)
            sel = sb(st, "sel_b", [128, 32], F32)
            MBw = sb(st, "MBw", [128, 128], BF16)
            MLA = int(os.environ.get("LA", "3"))
            PT = [sb(st, "PT", [128, 512], BF16) for _ in range(MLA + 2)]
            rz = sb(st, "rz_b", [128, 4], F32)
            yb = [sb(st, "yb", [128, 4, 64], BF16) for _ in range(2)]
            pS = [ps(st, "pS_b", [128, 512], F32) for _ in range(MLA + 1)]
            pO = [ps(st, "pO_b", [128, 512], F32) for _ in range(2)]
            pM = ps(st, "pM_b", [128, 512], F32)
            pMb = ps(st, "pMb_b", [128, 1024], BF16)
            P.dma('sp', KX[64:96, :], self.c_ind32, writes=['KXi'])
            P.op('pool', lambda e: e.memset(VX[:], 1.0), writes=['VX'])
            P.op('pool', lambda e: e.memset(MBw[:], 0.0), writes=['MBw'])
            P.op('pool', lambda e: e.memset(kmean[:], 0.0), writes=['kmean'])
            all_tiles = list(range(S // 128))
            si = 0
            qi = 0
            for h in range(4):
                P.dma('sp', KX[0:64, :], self.bkT[h * 64:(h + 1) * 64, :], reads=[('bkT', j) for j in all_tiles], writes=['KX'])
                self.dma_mid(VX[:, :, 0:64], self.bv_d[:, h * 64:(h + 1) * 64].rearrange("(kt p) d -> p kt d", p=128), NKT, 8,
                             reads=[('bv_d', j) for j in all_tiles], writes=['VX'])
                P.op('dve', lambda e: e.tensor_reduce(out=kmean[:, 0:NB], in_=KX[0:64, :].rearrange("p (n k) -> p n k", k=256),
                                                      axis=AX.X, op=ALU.add), reads=['KX'], writes=['kmean'])
                P.op('dve', lambda e: e.tensor_scalar(out=kmb[:], in0=kmean[:], scalar1=1.0 / 256, scalar2=None, op0=ALU.mult),
                     reads=['kmean'], writes=['kmb'])
                def prep(qc, Q_, qk_):
                    q0 = qc * 512
                    P.dma('sp', Q_[0:64, :], self.bqT[h * 64:(h + 1) * 64, q0:q0 + 512],
                          reads=[('bqT', 4 * qc + j) for j in range(4)], writes=[qk_])
                    yield
                    for j in range(4):
                        own = 2 * qc + j // 2
                        P.op('pe', lambda e: e.matmul(pM[:, 0:32], lhsT=Q_[0:64, j * 128:(j + 1) * 128], rhs=kmb[:],
                                                      start=True, stop=True), reads=[qk_, 'kmb'], writes=['pM'])
                        yield
                        P.op('pool', lambda e: e.memset(gsb[:], -1e30), writes=['gsb'])
                        yield
                        if own > 0:
                            P.op('dve', lambda e: e.tensor_copy(out=gsb[:, 0:own], in_=pM[:, 0:own]), reads=['pM'], writes=['gsb'])
                            yield
                        yield
                        P.op('dve', lambda e: e.max(out=m8[:], in_=gsb[:]), reads=['gsb'], writes=['m8'])
                        yield
                        P.op('dve', lambda e: e.tensor_scalar(out=sel[:], in0=gsb[:], scalar1=m8[:, 2:3], scalar2=None,
                                                              op0=ALU.is_ge), reads=['gsb', 'm8'], writes=['sel'])
                        yield
                        yield
                        P.op('dve', lambda e: e.tensor_scalar(out=MBw[:, 64:96], in0=sel[:], scalar1=-NEGM, scalar2=NEGM,
                                                              op0=ALU.mult, op1=ALU.add), reads=['sel'], writes=['MBw'])
                        yield
                        P.op('dve', lambda e: e.memset(MBw[:, 64 + own:65 + own], 0.0), writes=['MBw'])
                        yield
                        if own + 1 < 32:
                            P.op('dve', lambda e: e.memset(MBw[:, 65 + own:96], NEGM), writes=['MBw'])
                            yield
                        yield
                        P.op('pe', lambda e: e.transpose(out=pMb[:, 0:128], in_=MBw[:], identity=identb[:]),
                             reads=['MBw', 'identb'], writes=['pMb'])
                        yield
                        P.op('act', lambda e: e.copy(out=Q_[64:96, j * 128:(j + 1) * 128], in_=pMb[64:96, 0:128]),
                             reads=['pMb'], writes=[qk_])
                        yield
                        yield
                for _ in prep(0, QX[qi % 2], ('QX', qi % 2)):
                    pass
                for qc in range(NQC):
                    Q_ = QX[qi % 2]
                    qk_ = ('QX', qi % 2)
                    qi += 1
                    q0 = qc * 512
                    gp = prep(qc + 1, QX[qi % 2], ('QX', qi % 2)) if qc + 1 < NQC else None
                    pacc = 0.0
                    prate = 52.0 / (4 * qc + 4)
                    po = pO[qc % 2]
                    pok = ('pO', qc % 2)
                    P.op('pe', lambda e: e.matmul(po[:, 0:260], lhsT=zl[:], rhs=zr[:, 0:260], start=True, stop=True,
                                                  skip_group_check=True), reads=['zl', 'zr'], writes=[pok])
                    nkt = 4 * qc + 4
                    LA = MLA
                    base = si

                    def qkmm(kt):
                        pp = pS[(base + kt) % (MLA + 1)]
                        P.op('pe', lambda e: e.matmul(pp[:], lhsT=KX[:, kt * 128:(kt + 1) * 128], rhs=Q_[:], start=True, stop=True),
                             reads=['KX', 'KXi', qk_], writes=[('pS', (base + kt) % (MLA + 1))])
                    for kt in range(min(LA, nkt)):
                        qkmm(kt)
                    for kt in range(nkt):
                        if kt + LA < nkt:
                            qkmm(kt + LA)
                        p_ = pS[si % (MLA + 1)]
                        pk = ('pS', si % (MLA + 1))
                        t_ = PT[si % (MLA + 2)]
                        tk = ('PT', si % (MLA + 2))
                        si += 1
                        if LA == 0:
                            qkmm(kt)
                        P.op('act', lambda e: e.activation(out=t_[:], in_=p_[:], func=AF.Exp, bias=negB[:, 0:1], scale=0.125),
                             reads=[pk, 'negBm'], writes=[tk])
                        off = kt - 4 * qc
                        if off >= 0:
                            P.op('pool', lambda e: e.tensor_tensor(out=t_[:], in0=t_[:], in1=tm4[:, off, :], op=ALU.mult),
                                 reads=[tk, 'tm4'], writes=[tk])
                        for j in range(4):
                            if kt > 4 * qc + j:
                                continue
                            P.op('pe', lambda e: e.matmul(po[:, j * 65:(j + 1) * 65], lhsT=t_[:, j * 128:(j + 1) * 128],
                                                          rhs=VX[:, kt, :], start=False, stop=(kt == 4 * qc + j),
                                                          skip_group_check=True), reads=[tk, 'VX'], writes=[pok])
                        if gp is not None:
                            pacc += prate
                            while pacc >= 1.0 and gp is not None:
                                pacc -= 1.0
                                try:
                                    next(gp)
                                except StopIteration:
                                    gp = None
                    if gp is not None:
                        for _ in gp:
                            pass
                    y_ = yb[qc % 2]
                    yk = ('yb', qc % 2)
                    pov = po[:, 0:260].rearrange("p (j d) -> p j d", d=65)
                    P.op('dve', lambda e: e.reciprocal(out=rz[:], in_=pov[:, :, 64]), reads=[pok], writes=['rz'])
                    P.op('dve', lambda e: e.tensor_tensor(out=y_[:], in0=pov[:, :, 0:64],
                                                          in1=rz[:].unsqueeze(2).broadcast_to([128, 4, 64]), op=ALU.mult),
                         reads=[pok, 'rz'], writes=[yk])
                    P.dma('sp', self.mix_d[q0:q0 + 512, 512 + h * 64:512 + (h + 1) * 64].rearrange("(j p) d -> p j d", p=128),
                          y_[:], reads=[yk], writes=[('mixb', h, qc)])
            self.barrier()

    def phase2_nsa(self, l):
        S, P, nc = self.S, self.P, self.nc
        NT = S // 128
        Nc = S // 16 - 1
        NCT = max(1, S // 2048)
        with ExitStack() as st:
            sb, ps = self.sb, self.ps
            identb = sb(st, "identb_n", [128, 128], BF16)
            trib = sb(st, "trib", [128, 128], BF16)
            triw = sb(st, "triw", [128, 128], BF16)
            cmask = sb(st, "cmask", [128, 17, 128], BF16)
            OVL = sb(st, "OVL", [128, NCT, 128], BF16)
            cols = sb(st, "cols", [128, 4], F32)
            zl = sb(st, "zl_n", [1, 128], BF16)
            zr = sb(st, "zr_n", [1, 512], BF16)
            NG = sb(st, "NG", [128, NT, 12], F32)
            P.dma('sp', identb[:], self.c_identb, writes=['identb'])
            P.dma('sp', trib[:], self.c_trib, writes=['trib'])
            P.dma('sp', triw[:], self.c_triw, writes=['triw'])
            P.dma('sp', cmask[:], self.c_cmask, writes=['cmask'])
            P.dma('sp', OVL[:], self.c_ovl.rearrange("(ct p) n -> p ct n", p=128)[:, 0:NCT, :], writes=['OVL'])
            P.dma('sp', cols[:], self.c_cols, writes=['cols'])
            P.op('pool', lambda e: e.memset(zl[:], 0.0), writes=['zl'])
            P.op('pool', lambda e: e.memset(zr[:], 0.0), writes=['zr'])
            allt = list(range(NT))
            self.dma_mid(NG[:], self.ng_d.rearrange("(j p) g -> p j g", p=128), NT, 8, reads=[('ng_d', j) for j in allt], writes=['NG'])
            negBc = self._negB(st, "negBc", self.nsa_q_norm[l], self.nsa_k_norm[l, 0])
            negBs = self._negB(st, "negBs", self.nsa_q_norm[l], self.nsa_k_norm[l, 1])
            negBw = self._negB(st, "negBw", self.nsa_q_norm[l], self.nsa_k_norm[l, 2])
            KSX = sb(st, "KSX", [128, S], BF16)
            KWX = sb(st, "KWX", [64, S], BF16)
            VSX = sb(st, "VSX", [128, NT, 65], BF16)
            VWX = sb(st, "VWX", [128, NT, 65], BF16)
            KcT = sb(st, "KcT", [64, NCT * 128], BF16)
            VCX = sb(st, "VCX", [128, NCT, 65], BF16)
            P.dma('sp', KSX[0:64, :], self.ksT, reads=[('ksT', j) for j in allt], writes=['KSX'])
            P.dma('sp', KSX[64:128, :], self.c_ind64, writes=['KSXi'])
            P.dma('sp', KWX[:], self.kwT, reads=[('kwT', j) for j in allt], writes=['KWX'])
            for V_, src, nm in ((VSX, self.vs_d, 'vs_d'), (VWX, self.vw_d, 'vw_d')):
                P.op('pool', lambda e: e.memset(V_[:], 1.0), writes=[nm + 'X'])
                self.dma_mid(V_[:, :, 0:64], src.rearrange("(kt p) d -> p kt d", p=128), NT, 8,
                             reads=[(nm, j) for j in allt], writes=[nm + 'X'])
            P.op('pool', lambda e: e.memset(VCX[:], 1.0), writes=['VCX'])
            pS = [ps(st, "pS_n", [128, 512], F32) for _ in range(2)]
            pOc = ps(st, "pOc", [128, 512], F32)
            pU = ps(st, "pU", [128, 512], F32)
            pOs = ps(st, "pOs", [128, 512], F32)
            pOw = ps(st, "pOw", [128, 512], F32)
            pMb = ps(st, "pMb_n", [128, 1024], BF16)
            pM = ps(st, "pM_n", [128, 512], F32)
            with ExitStack() as s2:
                KCV = sb(s2, "KCV", [128, S], BF16)
                W1s = sb(s2, "W1s", [128, 32, 128], F32)
                W1 = sb(s2, "W1", [128, 32, 128], BF16)
                pes = sb(s2, "pes", [32, 128], F32)
                peb = sb(s2, "peb", [32, 128], BF16)
                peT = sb(s2, "peT", [128, 32], BF16)
                w2s = sb(s2, "w2s", [128, 2, 64], F32)
                w2 = sb(s2, "w2", [128, 2, 64], BF16)
                gk0 = sb(s2, "gk0", [128, 64], F32)
                bias = sb(s2, "bias_c", [128, 2], F32)
                hidb = sb(s2, "hidb", [128, NCT * 128], BF16)
                kc32 = sb(s2, "kc32", [128, 64], F32)
                kcn = sb(s2, "kcn", [128, 64], BF16)
                junk = sb(s2, "junk_c", [128, 64], F32)
                ssc = sb(s2, "ssc", [128, 2], F32)
                P.dma('sp', KCV[:], self.kcvcT, reads=[('kcvcT', b) for b in range(S // 512)], writes=['KCV'])
                for br in range(2):
                    P.dma('sp', W1s[64 * br:64 * br + 64], self.cmp_w1[l, br].rearrange("(r d) j -> d r j", d=64), writes=['W1s'])
                    P.dma('sp', w2s[:, br, :], self.cmp_w2[l, br], writes=['w2s'])
                    P.dma('sp', pes[:, 64 * br:64 * br + 64], self.cmp_pe[l, br], writes=['pes'])
                P.dma('sp', gk0[:], self.nsa_k_norm[l, 0].partition_broadcast(128), writes=['gk0'])
                P.op('dve', lambda e: e.tensor_copy(out=W1[:], in_=W1s[:]), reads=['W1s'], writes=['W1'])
                P.op('dve', lambda e: e.tensor_copy(out=peb[:], in_=pes[:]), reads=['pes'], writes=['peb'])
                P.op('pe', lambda e: e.transpose(out=pMb[:, 0:32], in_=peb[:], identity=identb[0:32, 0:32]),
                     reads=['peb', 'identb'], writes=['pMb'])
                P.op('dve', lambda e: e.tensor_copy(out=peT[:], in_=pMb[:, 0:32]), reads=['pMb'], writes=['peT'])
                P.op('dve', lambda e: e.tensor_copy(out=w2[:], in_=w2s[:]), reads=['w2s'], writes=['w2'])
                P.op('pool', lambda e: e.memset(hidb[:], 0.0), writes=['hidb'])
                for br in range(2):
                    rows = slice(64 * br, 64 * br + 64)
                    kview = KCV[rows, :].rearrange("p (c s) -> p c s", s=16)
                    for r in range(32):
                        P.op('pe', lambda e: e.matmul(pM[:, 0:1], lhsT=W1[rows, r, :], rhs=peT[rows, r:r + 1],
                                                      start=(r == 0), stop=(r == 31)), reads=['W1', 'peT'], writes=['pM'])
                    P.op('dve', lambda e: e.tensor_copy(out=bias[:, br:br + 1], in_=pM[:, 0:1]), reads=['pM'], writes=['bias'])
                    for r in range(32):
                        rhs = kview[:, 0:Nc, r] if r < 16 else kview[:, 1:Nc + 1, r - 16]
                        P.op('pe', lambda e: e.matmul(pS[0][:, 0:Nc], lhsT=W1[rows, r, :], rhs=rhs,
                                                      start=(r == 0), stop=(r == 31)), reads=['W1', 'KCV'], writes=[('pS', 0)])
                    P.op('act', lambda e: e.activation(out=hidb[:, 0:Nc], in_=pS[0][:, 0:Nc], func=AF.Silu, bias=bias[:, br:br + 1]),
                         reads=[('pS', 0), 'bias'], writes=['hidb'])
                    for ct in range(NCT):
                        P.op('pe', lambda e: e.matmul(pM[:, 0:64], lhsT=hidb[:, ct * 128:(ct + 1) * 128], rhs=w2[:, br, :],
                                                      start=True, stop=True), reads=['hidb', 'w2'], writes=['pM'])
                        if br == 0:
                            P.op('act', lambda e: e.activation(out=junk[:], in_=pM[:, 0:64], func=AF.Square, scale=0.125,
                                                               accum_out=ssc[:, 0:1]), reads=['pM'], writes=['junk_c', 'ssc'])
                            P.op('dve', lambda e: e.tensor_scalar(out=ssc[:, 0:1], in0=ssc[:, 0:1], scalar1=EPS, scalar2=None,
                                                                  op0=ALU.add), reads=['ssc'], writes=['ssc'])
                            P.op('act', lambda e: e.activation(out=ssc[:, 0:1], in_=ssc[:, 0:1], func=AF.Sqrt), reads=['ssc'], writes=['ssc'])
                            P.op('dve', lambda e: e.reciprocal(out=ssc[:, 1:2], in_=ssc[:, 0:1]), reads=['ssc'], writes=['ssc'])
                            P.op('dve', lambda e: e.scalar_tensor_tensor(out=kcn[:], in0=pM[:, 0:64], scalar=ssc[:, 1:2], in1=gk0[:],
                                                                         op0=ALU.mult, op1=ALU.mult),
                                 reads=['pM', 'ssc', 'gk0'], writes=['kcn'])
                            P.op('pe', lambda e: e.transpose(out=pMb[0:64, 0:128], in_=kcn[:], identity=identb[:]),
                                 reads=['kcn', 'identb'], writes=['pMb'])
                            P.op('act', lambda e: e.copy(out=KcT[:, ct * 128:(ct + 1) * 128], in_=pMb[0:64, 0:128]),
                                 reads=['pMb'], writes=['KcT'])
                        else:
                            P.op('act', lambda e: e.copy(out=VCX[:, ct, 0:64], in_=pM[:, 0:64]), reads=['pM'], writes=['VCX'])
                self.barrier()
            QU = [sb(st, "QU", [64, 512], BF16) for _ in range(2)]
            QR0 = [sb(st, "QR0", [128, 512], BF16) for _ in range(2)]
            QR1 = [sb(st, "QR1", [128, 512], BF16) for _ in range(2)]
            PTc = [sb(st, "PTc", [128, 512], BF16) for _ in range(NCT)]
            PT = [sb(st, "PTn", [128, 512], BF16) for _ in range(3)]
            zz = sb(st, "zz", [128, 3, 4], F32)
            rzz = sb(st, "rzz", [128, 3, 4], F32)
            coef = sb(st, "coef", [128, 3, 4], F32)
            imp = sb(st, "imp", [128, 128], F32)
            work = sb(st, "work", [128, 128], F32)
            m8a = sb(st, "m8a", [128, 8], F32)
            m8b = sb(st, "m8b", [128, 8], F32)
            selm = sb(st, "selm", [128, 128], F32)
            MB = sb(st, "MB", [128, 128], BF16)
            MBs = sb(st, "MBs", [128, 128], BF16)
            yacc = sb(st, "yacc", [128, 4, 64], F32)
            yn = [sb(st, "yn", [128, 256], BF16) for _ in range(2)]
            si = 0

            def seed(t, n, key):
                P.op('pe', lambda e: e.matmul(t[:, 0:n], lhsT=zl[:], rhs=zr[:, 0:n], start=True, stop=True, skip_group_check=True),
                     reads=['zl', 'zr'], writes=[key])

            def hb(ap):
                return ap.unsqueeze(1).broadcast_to([ap.shape[0], 4, 128])

            for m in range(NT):
                t0 = m * 128
                b2 = m % 2
                qu, qr0, qr1 = QU[b2], QR0[b2], QR1[b2]
                P.dma('sp', qu[:].rearrange("p (h t) -> p h t", h=4), self.nquT[:, t0:t0 + 128].rearrange("(h d) t -> d h t", d=64),
                      reads=[('nquT', m)], writes=[('QU', b2)])
                P.dma('sp', qr0[0:64, :].rearrange("p (h t) -> p h t", h=4), self.nqrT[:, t0:t0 + 128].rearrange("(h d) t -> d h t", d=64),
                      reads=[('nqrT', m)], writes=[('QR0', b2)])
                use_g1 = (2 * m + 1) >= 64
                if use_g1:
                    P.dma('sp', qr1[0:64, :].rearrange("p (h t) -> p h t", h=4),
                          self.nqrT[:, t0:t0 + 128].rearrange("(h d) t -> d h t", d=64), reads=[('nqrT', m)], writes=[('QR1', b2)])
                ctn = min(NCT, (8 * m + 6) // 128 + 1)
                for ct in range(ctn):
                    p_ = pS[si % 2]
                    pk = ('pS', si % 2)
                    si += 1
                    P.op('pe', lambda e: e.matmul(p_[:], lhsT=KcT[:, ct * 128:(ct + 1) * 128], rhs=qu[:], start=True, stop=True),
                         reads=['KcT', ('QU', b2)], writes=[pk])
                    P.op('act', lambda e: e.activation(out=PTc[ct][:], in_=p_[:], func=AF.Exp, bias=negBc[:, 0:1], scale=0.125),
                         reads=[pk, 'negBc'], writes=[('PTc', ct)])
                    r = m - 16 * ct
                    if r <= 16:
                        P.op('pool', lambda e: e.tensor_tensor(out=PTc[ct][:].rearrange("p (h t) -> p h t", h=4),
                                                               in0=PTc[ct][:].rearrange("p (h t) -> p h t", h=4),
                                                               in1=hb(cmask[:, r, :]), op=ALU.mult),
                             reads=[('PTc', ct), 'cmask'], writes=[('PTc', ct)])
                seed(pOc, 260, 'pOc')
                seed(pU, 512, 'pU')
                for h in range(4):
                    for ct in range(ctn):
                        P.op('pe', lambda e: e.matmul(pOc[:, h * 65:(h + 1) * 65], lhsT=PTc[ct][:, h * 128:(h + 1) * 128],
                                                      rhs=VCX[:, ct, :], start=False, stop=(ct == ctn - 1), skip_group_check=True),
                             reads=[('PTc', ct), 'VCX'], writes=['pOc'])
                        P.op('pe', lambda e: e.matmul(pU[:, h * 128:(h + 1) * 128], lhsT=PTc[ct][:, h * 128:(h + 1) * 128],
                                                      rhs=OVL[:, ct, :], start=False, stop=(ct == ctn - 1), skip_group_check=True),
                             reads=[('PTc', ct), 'OVL'], writes=['pU'])
                pocv = pOc[:, 0:260].rearrange("p (h d) -> p h d", d=65)
                P.op('dve', lambda e: e.tensor_scalar(out=zz[:, 0, :], in0=pocv[:, :, 64], scalar1=1e-30, scalar2=None, op0=ALU.max),
                     reads=['pOc'], writes=['zz0'])
                P.op('dve', lambda e: e.reciprocal(out=rzz[:, 0, :], in_=zz[:, 0, :]), reads=['zz0'], writes=['rzz0'])
                P.op('dve', lambda e: e.tensor_scalar(out=imp[:], in0=pU[:, 0:128], scalar1=rzz[:, 0, 0:1], scalar2=None, op0=ALU.mult),
                     reads=['pU', 'rzz0'], writes=['imp'])
                for h in range(1, 4):
                    P.op('dve', lambda e: e.scalar_tensor_tensor(out=imp[:], in0=pU[:, h * 128:(h + 1) * 128], scalar=rzz[:, 0, h:h + 1],
                                                                 in1=imp[:], op0=ALU.mult, op1=ALU.add),
                         reads=['pU', 'rzz0', 'imp'], writes=['imp'])
                n1 = 2 * m + 1
                if n1 + 1 < 128:
                    P.op('pool', lambda e: e.memset(imp[:, n1 + 1:128], -1e30), reads=['imp'], writes=['imp'])
                P.op('pool', lambda e: e.tensor_copy(out=imp[:, n1:n1 + 1], in_=cols[:, 0:1]), reads=['cols', 'imp'], writes=['imp'])
                P.op('pool', lambda e: e.memset(imp[:, n1 - 1:n1], 1e9), reads=['imp'], writes=['imp'])
                if n1 - 2 >= 0:
                    P.op('dve', lambda e: e.tensor_tensor(out=imp[:, n1 - 2:n1 - 1], in0=imp[:, n1 - 2:n1 - 1], in1=cols[:, 1:2], op=ALU.max),
                         reads=['cols', 'imp'], writes=['imp'])
                P.op('pool', lambda e: e.memset(imp[:, 0:1], 1e9), reads=['imp'], writes=['imp'])
                P.op('dve', lambda e: e.max(out=m8a[:], in_=imp[:]), reads=['imp'], writes=['m8a'])
                P.op('dve', lambda e: e.match_replace(out=work[:], in_to_replace=m8a[:], in_values=imp[:], imm_value=-1e30),
                     reads=['imp', 'm8a'], writes=['work'])
                P.op('dve', lambda e: e.max(out=m8b[:], in_=work[:]), reads=['work'], writes=['m8b'])
                P.op('dve', lambda e: e.tensor_scalar(out=selm[:], in0=imp[:], scalar1=m8b[:, 7:8], scalar2=None, op0=ALU.is_ge),
                     reads=['imp', 'm8b'], writes=['selm'])
                P.op('dve', lambda e: e.tensor_scalar(out=MB[:], in0=selm[:], scalar1=-NEGM, scalar2=NEGM, op0=ALU.mult, op1=ALU.add),
                     reads=['selm'], writes=['MB'])
                if n1 + 1 < 128:
                    P.op('pool', lambda e: e.memset(MB[:, n1 + 1:128], NEGM), reads=['MB'], writes=['MB'])
                P.op('pool', lambda e: e.tensor_copy(out=MB[:, n1:n1 + 1], in_=cols[:, 2:3]), reads=['cols', 'MB'], writes=['MB'])
                P.op('pool', lambda e: e.tensor_copy(out=MBs[:, 0:64], in_=MB[:, 64:128]), reads=['MB'], writes=['MBs'])
                P.op('pool', lambda e: e.tensor_copy(out=MBs[:, 64:128], in_=MB[:, 0:64]), reads=['MB'], writes=['MBs'])
                P.op('pe', lambda e: e.transpose(out=pMb[:, 0:128], in_=MBs[:], identity=identb[:]), reads=['MBs', 'identb'], writes=['pMb'])
                P.op('act', lambda e: e.copy(out=qr0[64:128, :].rearrange("p (h t) -> p h t", h=4), in_=hb(pMb[64:128, 0:128])),
                     reads=['pMb'], writes=[('QR0', b2)])
                if use_g1:
                    P.op('pe', lambda e: e.transpose(out=pMb[:, 128:256], in_=MB[:], identity=identb[:]), reads=['MB', 'identb'], writes=['pMb'])
                    P.op('act', lambda e: e.copy(out=qr1[64:128, :].rearrange("p (h t) -> p h t", h=4), in_=hb(pMb[64:128, 128:256])),
                         reads=['pMb'], writes=[('QR1', b2)])
                seed(pOs, 260, 'pOs')
                for kt in range(m + 1):
                    g = kt // 32
                    qr_, qrk = (qr0, ('QR0', b2)) if g == 0 else (qr1, ('QR1', b2))
                    p_ = pS[si % 2]
                    pk = ('pS', si % 2)
                    t_ = PT[si % 3]
                    tk = ('PTn', si % 3)
                    si += 1
                    P.op('pe', lambda e: e.matmul(p_[:], lhsT=KSX[:, kt * 128:(kt + 1) * 128], rhs=qr_[:], start=True, stop=True),
                         reads=['KSX', 'KSXi', qrk], writes=[pk])
                    P.op('act', lambda e: e.activation(out=t_[:], in_=p_[:], func=AF.Exp, bias=negBs[:, 0:1], scale=0.125),
                         reads=[pk, 'negBs'], writes=[tk])
                    if kt == m:
                        P.op('pool', lambda e: e.tensor_tensor(out=t_[:].rearrange("p (h t) -> p h t", h=4),
                                                               in0=t_[:].rearrange("p (h t) -> p h t", h=4), in1=hb(trib[:]), op=ALU.mult),
                             reads=[tk, 'trib'], writes=[tk])
                    for h in range(4):
                        P.op('pe', lambda e: e.matmul(pOs[:, h * 65:(h + 1) * 65], lhsT=t_[:, h * 128:(h + 1) * 128], rhs=VSX[:, kt, :],
                                                      start=False, stop=(kt == m), skip_group_check=True),
                             reads=[tk, 'vs_dX'], writes=['pOs'])
                seed(pOw, 260, 'pOw')
                for kt in range(max(0, m - 4), m + 1):
                    p_ = pS[si % 2]
                    pk = ('pS', si % 2)
                    t_ = PT[si % 3]
                    tk = ('PTn', si % 3)
                    si += 1
                    P.op('pe', lambda e: e.matmul(p_[:], lhsT=KWX[:, kt * 128:(kt + 1) * 128], rhs=qr0[0:64, :], start=True, stop=True),
                         reads=['KWX', ('QR0', b2)], writes=[pk])
                    P.op('act', lambda e: e.activation(out=t_[:], in_=p_[:], func=AF.Exp, bias=negBw[:, 0:1], scale=0.125),
                         reads=[pk, 'negBw'], writes=[tk])
                    if kt == m or kt == m - 4:
                        mk = trib if kt == m else triw
                        P.op('pool', lambda e: e.tensor_tensor(out=t_[:].rearrange("p (h t) -> p h t", h=4),
                                                               in0=t_[:].rearrange("p (h t) -> p h t", h=4), in1=hb(mk[:]), op=ALU.mult),
                             reads=[tk, 'trib', 'triw'], writes=[tk])
                    for h in range(4):
                        P.op('pe', lambda e: e.matmul(pOw[:, h * 65:(h + 1) * 65], lhsT=t_[:, h * 128:(h + 1) * 128], rhs=VWX[:, kt, :],
                                                      start=False, stop=(kt == m), skip_group_check=True),
                             reads=[tk, 'vw_dX'], writes=['pOw'])
                posv = pOs[:, 0:260].rearrange("p (h d) -> p h d", d=65)
                powv = pOw[:, 0:260].rearrange("p (h d) -> p h d", d=65)
                P.op('dve', lambda e: e.tensor_copy(out=zz[:, 1, :], in_=posv[:, :, 64]), reads=['pOs'], writes=['zz1'])
                P.op('dve', lambda e: e.tensor_copy(out=zz[:, 2, :], in_=powv[:, :, 64]), reads=['pOw'], writes=['zz1'])
                P.op('dve', lambda e: e.reciprocal(out=rzz[:, 1:3, :], in_=zz[:, 1:3, :]), reads=['zz1'], writes=['rzz1'])
                P.op('dve', lambda e: e.tensor_tensor(out=coef[:], in0=rzz[:], in1=NG[:, m, :].rearrange("p (b h) -> p b h", b=3), op=ALU.mult),
                     reads=['rzz0', 'rzz1', 'NG'], writes=['coef'])
                for h in range(4):
                    P.op('dve', lambda e: e.tensor_scalar(out=yacc[:, h, :], in0=pocv[:, h, 0:64], scalar1=coef[:, 0, h:h + 1], scalar2=None,
                                                          op0=ALU.mult), reads=['pOc', 'coef'], writes=[('yacc', h)])
                    P.op('dve', lambda e: e.scalar_tensor_tensor(out=yacc[:, h, :], in0=posv[:, h, 0:64], scalar=coef[:, 1, h:h + 1],
                                                                 in1=yacc[:, h, :], op0=ALU.mult, op1=ALU.add),
                         reads=['pOs', 'coef', ('yacc', h)], writes=[('yacc', h)])
                    P.op('dve', lambda e: e.scalar_tensor_tensor(out=yn[b2][:, h * 64:(h + 1) * 64], in0=powv[:, h, 0:64],
                                                                 scalar=coef[:, 2, h:h + 1], in1=yacc[:, h, :], op0=ALU.mult, op1=ALU.add),
                         reads=['pOw', 'coef', ('yacc', h)], writes=[('yn', b2)])
                P.dma('sp', self.mix_d[t0:t0 + 128, 768:1024], yn[b2][:], reads=[('yn', b2)], writes=[('mixn', m)])
            self.barrier()

    def build_all(self):
        self.declare()
        for l in range(self.depth):
            xsrc = self.x_in if l == 0 else self.xbuf
            xdst = self.y_out if l == self.depth - 1 else self.xbuf
            self.phase1(l, xsrc)
            if os.environ.get("PH2", "old") == "new":
                self.phase2(l)
            elif os.environ.get("PH2", "old") == "b":
                self.phase2_moba(l)
                self.phase2b(l)
            else:
                self.phase2_mlstm(l)
                self.phase2_moba(l)
                self.phase2_nsa2(l)
            self.phase3(l, xsrc, xdst)
        self.P.finish()
        return self.nc


def make_consts(S):
    bf = ml_dtypes.bfloat16
    half = 8
    inv = np.exp(-math.log(500000.0) * np.arange(half, dtype=np.float32) * (2.0 / 16)).astype(np.float32)
    ang = np.arange(S, dtype=np.float32)[:, None] * inv[None, :]
    d = dict(c_identb=np.eye(128, dtype=np.float32).astype(bf), c_identf=np.eye(128, dtype=np.float32),
             c_cos=np.cos(ang).astype(np.float32), c_sin=np.sin(ang).astype(np.float32))
    i = np.arange(128)
    d['c_ut'] = (i[:, None] < i[None, :]).astype(np.float32)
    d['c_tri'] = (i[:, None] <= i[None, :]).astype(np.float32)
    key = np.arange(S)
    d['c_ind32'] = (key[None, :] // 256 == np.arange(32)[:, None]).astype(np.float32).astype(bf)
    d['c_ind64'] = (((key[None, :] // 64) % 64) == np.arange(64)[:, None]).astype(np.float32).astype(bf)
    k = np.arange(128)[:, None, None]
    o = np.arange(4)[None, :, None]
    q = np.arange(512)[None, None, :]
    d['c_tm4'] = (q >= k + 128 * o).astype(np.float32).astype(bf)
    d['c_trib'] = (i[:, None] <= i[None, :]).astype(np.float32).astype(bf)
    d['c_triw'] = (i[:, None] > i[None, :]).astype(np.float32).astype(bf)
    ii = np.arange(128)[:, None, None]
    r = np.arange(17)[None, :, None]
    j = np.arange(128)[None, None, :]
    d['c_cmask'] = (16 * ii + 31 <= 128 * r + j).astype(np.float32).astype(bf)
    c = np.arange(512)[:, None]
    n = np.arange(128)[None, :]
    Nc = S // 16 - 1
    d['c_ovl'] = ((c >= 4 * n - 1) & (c <= 4 * n + 3) & (c < Nc)).astype(np.float32).astype(bf)
    cols = np.zeros((128, 4), np.float32)
    cols[:64, 0] = -1e30
    cols[64:, 0] = 1e9
    cols[:64, 1] = 1e9
    cols[64:, 1] = -1e30
    cols[:64, 2] = NEGM
    cols[64:, 2] = 0.0
    d['c_cols'] = cols
    return d


_CACHE = {}


def kernel(x, w_in, b_if, conv_qk, m_norm, moba_qk_norm, nsa_q_norm, nsa_k_norm,
           cmp_pe, cmp_w1, cmp_w2, w_out, norm_mix, norm_ffn, w_ff1, w_ff2):
    x = np.asarray(x, dtype=np.float32)
    Bsz, S, _ = x.shape
    depth = int(np.asarray(w_in).shape[0])
    n_cores = 8
    shared = dict(w_in=w_in, b_if=b_if, conv_qk=conv_qk, m_norm=m_norm, moba_qk_norm=moba_qk_norm,
                  nsa_q_norm=nsa_q_norm, nsa_k_norm=nsa_k_norm, cmp_pe=cmp_pe, cmp_w1=cmp_w1, cmp_w2=cmp_w2,
                  w_out=w_out, norm_mix=norm_mix, norm_ffn=norm_ffn, w_ff1=w_ff1, w_ff2=w_ff2)
    shared = {k: np.ascontiguousarray(np.asarray(v, dtype=np.float32)) for k, v in shared.items()}
    shared.update(make_consts(S))
    B = Builder(S, depth)
    nc = B.build_all()
    in_maps = []
    for c in range(n_cores):
        m = dict(shared)
        m['x'] = np.ascontiguousarray(x[c % Bsz])
        in_maps.append(m)
    res = run_bass_kernel_spmd(nc, in_maps, core_ids=list(range(n_cores)))
    out = np.stack([np.asarray(res.results[b]["y"], dtype=np.float32) for b in range(Bsz)], axis=0)
    return out


class BG:
    def __init__(self):
        self.gens = []
        self.acc = 0.0

    def add(self, g):
        self.gens.append(g)

    def step(self, rate=1.0):
        self.acc += rate
        while self.acc >= 1.0:
            self.acc -= 1.0
            for g in list(self.gens):
                try:
                    next(g)
                except StopIteration:
                    self.gens.remove(g)

    def drain(self, g=None):
        if g is None:
            while self.gens:
                self.step(1.0)
        else:
            while g in self.gens:
                try:
                    next(g)
                except StopIteration:
                    self.gens.remove(g)


def _phase2(self, l):
    S, P, nc = self.S, self.P, self.nc
    NCH = S // 64
    NT = S // 128
    NQC = S // 512
    NB = S // 256
    Nc = S // 16 - 1
    NCT = max(1, S // 2048)
    LNSC = math.log(128.0 ** -0.5)
    sb, ps = self.sb, self.ps
    allt = list(range(NT))
    with ExitStack() as st:
        identb = sb(st, "identb2", [128, 128], BF16)
        identf = sb(st, "identf2", [128, 128], F32)
        ut = sb(st, "ut2", [128, 128], F32)
        tri = sb(st, "tri2", [128, 128], F32)
        zl = sb(st, "zl2", [1, 128], BF16)
        zr = sb(st, "zr2", [1, 512], BF16)
        P.dma('sp', identb[:], self.c_identb, writes=['identb'])
        P.dma('sp', identf[:], self.c_identf, writes=['identf'])
        P.dma('sp', ut[:], self.c_ut, writes=['ut'])
        P.dma('sp', tri[:], self.c_tri, writes=['tri'])
        P.op('pool', lambda e: e.memset(zl[:], 0.0), writes=['zl'])
        P.op('pool', lambda e: e.memset(zr[:], 0.0), writes=['zr'])
        uT = sb(st, "uT_m", [64, 4, NCH], F32)
        u2T = sb(st, "u2T_m", [64, 4, NCH], F32)
        flT = sb(st, "flT_m", [64, 4, NCH], F32)
        decB = sb(st, "decB", [128, 4, NCH], F32)
        with ExitStack() as s2:
            li = sb(s2, "li", [NCH, 4, 64], F32)
            lf = sb(s2, "lf", [NCH, 4, 64], F32)
            ones = sb(s2, "ones", [NCH, 64], F32)
            Fin = sb(s2, "Fin", [NCH, 4, 64], F32)
            Ft = sb(s2, "Ft", [NCH, 4, 64], F32)
            a_ = sb(s2, "a_", [NCH, 4, 64], F32)
            Ain = sb(s2, "Ain", [NCH, 4, 64], F32)
            tot = sb(s2, "tot", [NCH, 4], F32)
            cmax = sb(s2, "cmax", [NCH, 4], F32)
            cmT = sb(s2, "cmT", [4, NCH], F32)
            ET = sb(s2, "ET", [4, NCH], F32)
            ETn = sb(s2, "ETn", [4, NCH], F32)
            Ec = sb(s2, "Ec", [NCH, 4], F32)
            Enc = sb(s2, "Enc", [NCH, 4], F32)
            tmp = sb(s2, "tmpm", [NCH, 4, 64], F32)
            uu = sb(s2, "uu", [NCH, 4, 64], F32)
            uu2 = sb(s2, "uu2", [NCH, 4, 64], F32)
            fl = sb(s2, "fl", [NCH, 4, 64], F32)
            dec = sb(s2, "dec", [NCH, 4], F32)
            decrep = sb(s2, "decrep", [NCH, 4, 128], F32)
            pa = ps(s2, "pa", [128, 512], F32)
            pb = ps(s2, "pb", [128, 512], F32)
            P.dma('sp', li[:], self.gi_d.rearrange("h (c j) -> c h j", j=64),
                  reads=[('gi_d', b) for b in range(S // 512)], writes=['li'])
            P.dma('sp', lf[:], self.gf_d.rearrange("h (c j) -> c h j", j=64),
                  reads=[('gf_d', b) for b in range(S // 512)], writes=['lf'])
            P.op('pool', lambda e: e.memset(ones[:], 1.0), writes=['ones'])
            for hh in range(4):
                P.op('dve', lambda e: e.tensor_tensor_scan(out=Fin[:, hh, :], data0=ones[:], data1=lf[:, hh, :],
                                                           initial=0.0, op0=ALU.mult, op1=ALU.add),
                     reads=['ones', 'lf'], writes=['Fin'])
            P.op('dve', lambda e: e.tensor_copy(out=tot[:], in_=Fin[:, :, 63]), reads=['Fin'], writes=['tot'])
            P.op('pe', lambda e: e.matmul(pa[0:NCH, 0:4], lhsT=ut[0:NCH, 0:NCH], rhs=tot[:], start=True, stop=True),
                 reads=['ut', 'tot'], writes=['pa'])
            P.op('dve', lambda e: e.tensor_tensor(out=Ft[:], in0=Fin[:],
                                                  in1=pa[0:NCH, 0:4].unsqueeze(2).broadcast_to([NCH, 4, 64]), op=ALU.add),
                 reads=['Fin', 'pa'], writes=['Ft'])
            P.op('dve', lambda e: e.tensor_tensor(out=a_[:], in0=li[:], in1=Ft[:], op=ALU.subtract),
                 reads=['li', 'Ft'], writes=['a_'])
            for hh in range(4):
                P.op('dve', lambda e: e.tensor_tensor_scan(out=Ain[:, hh, :], data0=a_[:, hh, :], data1=a_[:, hh, :],
                                                           initial=-1e30, op0=ALU.max, op1=ALU.max),
                     reads=['a_'], writes=['Ain'])
            P.op('dve', lambda e: e.tensor_copy(out=cmax[:], in_=Ain[:, :, 63]), reads=['Ain'], writes=['cmax'])
            P.op('pe', lambda e: e.transpose(out=pb[0:4, 0:NCH], in_=cmax[:], identity=identf[0:NCH, 0:NCH]),
                 reads=['cmax', 'identf'], writes=['pb'])
            P.op('dve', lambda e: e.tensor_copy(out=cmT[:], in_=pb[0:4, 0:NCH]), reads=['pb'], writes=['cmT'])
            P.op('dve', lambda e: e.tensor_tensor_scan(out=ET[:], data0=cmT[:], data1=cmT[:], initial=0.0,
                                                       op0=ALU.max, op1=ALU.max), reads=['cmT'], writes=['ET'])
            if NCH > 1:
                P.op('dve', lambda e: e.tensor_copy(out=ETn[:, 0:NCH - 1], in_=ET[:, 1:NCH]), reads=['ET'], writes=['ETn'])
            P.op('dve', lambda e: e.tensor_copy(out=ETn[:, NCH - 1:NCH], in_=ET[:, NCH - 1:NCH]), reads=['ET'], writes=['ETn'])
            P.op('pe', lambda e: e.transpose(out=pa[0:NCH, 0:4], in_=ET[:], identity=identf[0:4, 0:4]),
                 reads=['ET', 'identf'], writes=['pa'])
            P.op('dve', lambda e: e.tensor_copy(out=Ec[:], in_=pa[0:NCH, 0:4]), reads=['pa'], writes=['Ec'])
            P.op('pe', lambda e: e.transpose(out=pb[0:NCH, 0:4], in_=ETn[:], identity=identf[0:4, 0:4]),
                 reads=['ETn', 'identf'], writes=['pb'])
            P.op('dve', lambda e: e.tensor_copy(out=Enc[:], in_=pb[0:NCH, 0:4]), reads=['pb'], writes=['Enc'])
            Eb = Ec[:].unsqueeze(2).broadcast_to([NCH, 4, 64])
            Enb = Enc[:].unsqueeze(2).broadcast_to([NCH, 4, 64])
            P.op('dve', lambda e: e.tensor_tensor(out=tmp[:], in0=a_[:], in1=Eb, op=ALU.subtract),
                 reads=['a_', 'Ec'], writes=['tmp'])
            P.op('act', lambda e: e.activation(out=uu[:], in_=tmp[:], func=AF.Exp, bias=LNSC), reads=['tmp'], writes=['uu'])
            P.op('dve', lambda e: e.tensor_tensor(out=tmp[:], in0=a_[:], in1=Enb, op=ALU.subtract),
                 reads=['a_', 'Enc'], writes=['tmp'])
            P.op('act', lambda e: e.activation(out=uu2[:], in_=tmp[:], func=AF.Exp, bias=LNSC), reads=['tmp'], writes=['uu2'])
            P.op('dve', lambda e: e.tensor_tensor(out=tmp[:], in0=Ft[:], in1=Eb, op=ALU.add),
                 reads=['Ft', 'Ec'], writes=['tmp'])
            P.op('act', lambda e: e.activation(out=fl[:], in_=tmp[:], func=AF.Exp, scale=-1.0), reads=['tmp'], writes=['fl'])
            P.op('dve', lambda e: e.tensor_tensor(out=dec[:], in0=Ec[:], in1=Enc[:], op=ALU.subtract),
                 reads=['Ec', 'Enc'], writes=['dec'])
            P.op('act', lambda e: e.activation(out=dec[:], in_=dec[:], func=AF.Exp), reads=['dec'], writes=['dec'])
            P.op('dve', lambda e: e.tensor_copy(out=decrep[:], in_=dec[:].unsqueeze(2).broadcast_to([NCH, 4, 128])),
                 reads=['dec'], writes=['decrep'])
            for src, dst, nm in ((uu, uT, 'uT'), (uu2, u2T, 'u2T'), (fl, flT, 'flT')):
                for hh in range(4):
                    P.op('pe', lambda e: e.transpose(out=pa[0:64, hh * 128:hh * 128 + NCH], in_=src[:, hh, :],
                                                     identity=identf[0:NCH, 0:NCH]),
                         reads=['uu', 'uu2', 'fl', 'identf'], writes=['pa'])
                P.op('dve', lambda e: e.tensor_copy(out=dst[:], in_=pa[0:64, :].rearrange("p (h c) -> p h c", h=4)[:, :, 0:NCH]),
                     reads=['pa'], writes=[nm])
            for hh in range(4):
                P.op('pe', lambda e: e.matmul(pb[:, hh * 128:hh * 128 + NCH], lhsT=decrep[:, hh, :],
                                              rhs=identf[0:NCH, 0:NCH], start=True, stop=True),
                     reads=['decrep', 'identf'], writes=['pb'])
            P.op('dve', lambda e: e.tensor_copy(out=decB[:], in_=pb[:].rearrange("p (h c) -> p h c", h=4)[:, :, 0:NCH]),
                 reads=['pb'], writes=['decB'])
            self.barrier()

        pSr = [ps(st, "pSr", [128, 512], F32) for _ in range(3)]
        pMi = ps(st, "pMi", [128, 512], F32)
        pMb = pMi[:, 128:192].bitcast(BF16)
        pMb2 = pMi[:, 192:256].bitcast(BF16)
        pM = pMi[:, 256:288]
        bg = BG()

        sm = ExitStack()
        st = sm
        GC = 4
        NG = S // (64 * GC)
        TG = 64 * GC
        qg = [sb(st, "qg", [128, 4, TG], BF16) for _ in range(2)]
        kg = [sb(st, "kg", [128, 4, TG], BF16) for _ in range(2)]
        vg = [sb(st, "vg", [64, GC, 4, 129], BF16) for _ in range(2)]
        ogg = [sb(st, "ogg", [64, GC, 512], BF16) for _ in range(2)]
        ym = [sb(st, "ym", [64, GC, 512], BF16) for _ in range(2)]
        G = [sb(st, "G", [128, 129], F32) for _ in range(4)]
        Gb = [sb(st, "Gb", [128, 129], BF16) for _ in range(4)]
        ku2 = [sb(st, "ku2", [64, 128], BF16) for _ in range(2)]
        Sm = [sb(st, "Sm", [64, 64], BF16) for _ in range(2)]
        junkm = sb(st, "junkm", [64, 128], BF16)
        scm = [sb(st, "scm", [64, 8], F32) for _ in range(2)]
        mlA = [ps(st, "mlA", [128, 512], F32) for _ in range(2)]

        def gen_ml():
            for i in range(2):
                P.op('pool', lambda e: e.memset(vg[i][:], 1.0), writes=[('vg', i)])
            for hh in range(4):
                P.op('pool', lambda e: e.memset(G[hh][:], 0.0), writes=[('G', hh)])
                P.op('pool', lambda e: e.memset(Gb[hh][:], 0.0), writes=[('Gb', hh)])
            yield

            def load_group(g):
                i = g % 2
                tk = slice(g * TG, (g + 1) * TG)
                bl = sorted(set([(g * TG) // 512, ((g + 1) * TG - 1) // 512]))
                jl = list(range((g * TG) // 128, ((g + 1) * TG + 127) // 128))
                P.dma('sp', qg[i][:], self.qkT[0:512, tk].rearrange("(h p) t -> p h t", p=128),
                      reads=[('qkT', ft, b) for ft in range(4) for b in bl], writes=[('qg', i)])
                P.dma('sp', kg[i][:], self.qkT[512:1024, tk].rearrange("(h p) t -> p h t", p=128),
                      reads=[('qkT', ft, b) for ft in range(4, 8) for b in bl], writes=[('kg', i)])
                for hh in range(4):
                    P.dma('sp', vg[i][:, :, hh, 0:128],
                          self.mv_d[tk, hh * 128:(hh + 1) * 128].rearrange("(c s) e -> s c e", s=64),
                          reads=[('mv_d', j) for j in jl], writes=[('vg', i)])
                P.dma('sp', ogg[i][:], self.og_d[tk, :].rearrange("(c s) e -> s c e", s=64),
                      reads=[('og_d', j) for j in jl], writes=[('ogg', i)])
            load_group(0)
            it = 0
            for g in range(NG):
                if g + 1 < NG:
                    load_group(g + 1)
                gi_ = g % 2
                for cl in range(GC):
                    c = g * GC + cl
                    for hh in range(4):
                        k_ = kg[gi_][:, hh, cl * 64:(cl + 1) * 64]
                        q_ = qg[gi_][:, hh, cl * 64:(cl + 1) * 64]
                        v_ = vg[gi_][:, cl, hh, :]
                        i2 = it % 2
                        it += 1
                        A = mlA[i2]
                        pS_ = A[0:64, 0:64]
                        pO_ = A[0:64, 64:193]
                        pG_ = A[:, 256:385]
                        pkT_ = A[0:64, 448:512].bitcast(BF16)
                        P.op('pe', lambda e: e.transpose(out=pkT_, in_=k_, identity=identb[:]),
                             reads=[('kg', gi_), 'identb'], writes=[('mlA', i2)])
                        P.op('pe', lambda e: e.matmul(pS_, lhsT=k_, rhs=q_, start=True, stop=True),
                             reads=[('kg', gi_), ('qg', gi_)], writes=[('mlA', i2)])
                        yield
                        P.op('act', lambda e: e.activation(out=ku2[i2][:], in_=pkT_, func=AF.Copy,
                                                           scale=u2T[:, hh, c:c + 1]),
                             reads=[('mlA', i2), 'u2T'], writes=[('ku2', i2)])
                        P.op('dve', lambda e: e.scalar_tensor_tensor(out=Sm[i2][:], in0=pS_,
                                                                     scalar=uT[:, hh, c:c + 1], in1=tri[0:64, 0:64],
                                                                     op0=ALU.mult, op1=ALU.mult),
                             reads=[('mlA', i2), 'uT', 'tri'], writes=[('Sm', i2)])
                        yield
                        P.op('pe', lambda e: e.matmul(pO_, lhsT=Sm[i2][:], rhs=v_, start=True, stop=False),
                             reads=[('Sm', i2), ('vg', gi_)], writes=[('mlA', i2)])
                        P.op('pe', lambda e: e.matmul(pO_, lhsT=q_, rhs=Gb[hh][:], start=False, stop=True),
                             reads=[('qg', gi_), ('Gb', hh)], writes=[('mlA', i2)])
                        P.op('pe', lambda e: e.matmul(pG_, lhsT=ku2[i2][:], rhs=v_, start=True, stop=True),
                             reads=[('ku2', i2), ('vg', gi_)], writes=[('mlA', i2)])
                        yield
                        P.op('dve', lambda e: e.scalar_tensor_tensor(out=G[hh][:], in0=G[hh][:], scalar=decB[:, hh, c:c + 1],
                                                                     in1=pG_, op0=ALU.mult, op1=ALU.add),
                             reads=[('G', hh), 'decB', ('mlA', i2)], writes=[('G', hh)])
                        P.op('pool', lambda e: e.tensor_copy(out=Gb[hh][:], in_=G[hh][:]), reads=[('G', hh)], writes=[('Gb', hh)])
                        s_ = scm[i2]
                        sk = ('scm', i2)
                        P.op('act', lambda e: e.activation(out=junkm[:], in_=pO_[:, 0:128], func=AF.Square,
                                                           scale=128.0 ** -0.5, accum_out=s_[:, 0:1]),
                             reads=[('mlA', i2)], writes=['junkm', sk])
                        yield
                        P.op('dve', lambda e: e.tensor_scalar(out=s_[:, 6:7], in0=pO_[:, 128:129], scalar1=-1.0,
                                                              scalar2=flT[:, hh, c:c + 1], op0=ALU.mult, op1=ALU.max),
                             reads=[('mlA', i2), 'flT'], writes=[sk])
                        P.op('dve', lambda e: e.tensor_tensor(out=s_[:, 1:2], in0=s_[:, 6:7], in1=pO_[:, 128:129], op=ALU.max),
                             reads=[('mlA', i2), sk], writes=[sk])
                        P.op('dve', lambda e: e.tensor_tensor(out=s_[:, 2:3], in0=s_[:, 1:2], in1=s_[:, 1:2], op=ALU.mult),
                             reads=[sk], writes=[sk])
                        P.op('dve', lambda e: e.scalar_tensor_tensor(out=s_[:, 3:4], in0=s_[:, 2:3], scalar=EPS, in1=s_[:, 0:1],
                                                                     op0=ALU.mult, op1=ALU.add), reads=[sk], writes=[sk])
                        yield
                        P.op('act', lambda e: e.activation(out=s_[:, 4:5], in_=s_[:, 3:4], func=AF.Ln), reads=[sk], writes=[sk])
                        P.op('act', lambda e: e.activation(out=s_[:, 5:6], in_=s_[:, 4:5], func=AF.Exp, scale=-0.5), reads=[sk], writes=[sk])
                        P.op('dve', lambda e: e.scalar_tensor_tensor(out=ym[gi_][:, cl, hh * 128:(hh + 1) * 128],
                                                                     in0=pO_[:, 0:128], scalar=s_[:, 5:6],
                                                                     in1=ogg[gi_][:, cl, hh * 128:(hh + 1) * 128],
                                                                     op0=ALU.mult, op1=ALU.mult),
                             reads=[('mlA', i2), sk, ('ogg', gi_)], writes=[('ym', gi_)])
                        yield
                P.dma('sp', self.mix_d[g * TG:(g + 1) * TG, 0:512].rearrange("(c s) e -> s c e", s=64), ym[gi_][:],
                      reads=[('ym', gi_)], writes=[('mixm', g)])
        ML_YIELDS = NCH * 4 * 6 + 1
        g_ml = gen_ml()
        if not os.environ.get("SKIP_ML"):
            bg.add(g_ml)

        with sm:
            tm4 = sb(sm, "tm4", [128, 4, 512], BF16)
            P.dma('sp', tm4[:], self.c_tm4, writes=['tm4'])
            negB = self._negB(sm, "negBm", self.moba_qk_norm[l, 0], self.moba_qk_norm[l, 1])
            KX = [sb(sm, "KX", [96, S], BF16) for _ in range(2)]
            QXA = [sb(sm, "QXA", [96, S], BF16) for _ in range(2)]
            VX = [sb(sm, "VX", [128, NT, 65], BF16) for _ in range(2)]
            kmean = sb(sm, "kmean", [64, 32], F32)
            kmb = [sb(sm, "kmb", [64, 32], BF16) for _ in range(2)]
            gsb = [sb(sm, "gsb_b", [128, 32], F32) for _ in range(2)]
            m8 = [sb(sm, "m8", [128, 8], F32) for _ in range(2)]
            sel = [sb(sm, "sel_b", [128, 32], F32) for _ in range(2)]
            MBw = [sb(sm, "MBw", [128, 128], BF16) for _ in range(2)]
            PT = [sb(sm, "PT", [128, 512], BF16) for _ in range(4)]
            rz = sb(sm, "rz_b", [128, 4], F32)
            yb = [sb(sm, "yb", [128, 4, 64], BF16) for _ in range(2)]
            pO = [ps(sm, "pO_b", [128, 512], F32) for _ in range(1)] * 2
            if os.environ.get("NO_BITCAST"):
                pMb = ps(sm, "pMbx", [128, 128], BF16)[:]
            for i in range(2):
                P.dma('sp', KX[i][64:96, :], self.c_ind32, writes=[('KXi', i)])
                P.op('pool', lambda e: e.memset(VX[i][:], 1.0), writes=[('VX', i)])
                P.op('pool', lambda e: e.memset(MBw[i][:], 0.0), writes=[('MBw', i)])
            P.op('pool', lambda e: e.memset(kmean[:], 0.0), writes=['kmean'])

            def gen_prep(h):
                i = h % 2
                P.dma('sp', KX[i][0:64, :], self.bkT[h * 64:(h + 1) * 64, :], reads=[('bkT', j) for j in allt], writes=[('KX', i)])
                self.dma_mid(VX[i][:, :, 0:64], self.bv_d[:, h * 64:(h + 1) * 64].rearrange("(kt p) d -> p kt d", p=128), NT, 8,
                             reads=[('bv_d', j) for j in allt], writes=[('VX', i)])
                P.dma('sp', QXA[i][0:64, :], self.bqT[h * 64:(h + 1) * 64, :], reads=[('bqT', j) for j in allt], writes=[('QXAq', i)])
                yield
                P.op('dve', lambda e: e.tensor_reduce(out=kmean[:, 0:NB], in_=KX[i][0:64, :].rearrange("p (n k) -> p n k", k=256),
                                                      axis=AX.X, op=ALU.add), reads=[('KX', i)], writes=['kmean'])
                P.op('dve', lambda e: e.tensor_scalar(out=kmb[i][:], in0=kmean[:], scalar1=1.0 / 256, scalar2=None, op0=ALU.mult),
                     reads=['kmean'], writes=[('kmb', i)])
                yield
                for jt in range(NT if os.environ.get("BIS", "0") != "2" else 0):
                    own = jt // 2
                    b_ = jt % 2
                    P.op('pe', lambda e: e.matmul(pM, lhsT=QXA[i][0:64, jt * 128:(jt + 1) * 128], rhs=kmb[i][:],
                                                  start=True, stop=True), reads=[('QXAq', i), ('kmb', i)], writes=['pMi'])
                    P.op('pool', lambda e: e.memset(gsb[b_][:], -1e30), writes=[('gsb', b_)])
                    if own > 0:
                        P.op('dve', lambda e: e.tensor_copy(out=gsb[b_][:, 0:own], in_=pM[:, 0:own]), reads=['pMi'], writes=[('gsb', b_)])
                    yield
                    P.op('dve', lambda e: e.max(out=m8[b_][:], in_=gsb[b_][:]), reads=[('gsb', b_)], writes=[('m8', b_)])
                    P.op('dve', lambda e: e.tensor_scalar(out=sel[b_][:], in0=gsb[b_][:], scalar1=m8[b_][:, 2:3], scalar2=None,
                                                          op0=ALU.is_ge), reads=[('gsb', b_), ('m8', b_)], writes=[('sel', b_)])
                    P.op('dve', lambda e: e.tensor_scalar(out=MBw[b_][:, 64:96], in0=sel[b_][:], scalar1=-NEGM, scalar2=NEGM,
                                                          op0=ALU.mult, op1=ALU.add), reads=[('sel', b_)], writes=[('MBw', b_)])
                    P.op('dve', lambda e: e.memset(MBw[b_][:, 64 + own:65 + own], 0.0), writes=[('MBw', b_)])
                    if own + 1 < 32:
                        P.op('dve', lambda e: e.memset(MBw[b_][:, 65 + own:96], NEGM), writes=[('MBw', b_)])
                    yield
                    P.op('pe', lambda e: e.transpose(out=pMb, in_=MBw[b_][:], identity=identb[:]),
                         reads=[('MBw', b_), 'identb'], writes=['pMi'])
                    P.op('act', lambda e: e.copy(out=QXA[i][64:96, jt * 128:(jt + 1) * 128], in_=pMb[64:96, :]),
                         reads=['pMi'], writes=[('QXAm', i, jt)])
                    yield
            PREP_YIELDS = 3 * NT + 2
            main_iters_h = sum(4 * qc + 4 for qc in range(NQC))
            ml_rate = ML_YIELDS / float(4 * main_iters_h) * 1.15
            prep_rate = PREP_YIELDS / float(main_iters_h) * 1.3
            g0 = gen_prep(0)
            for _ in g0:
                bg.step(1.0)
            si = 0
            pi = 0
            for h in range(4):
                i = h % 2
                gp = None
                pacc = 0.0
                if h + 1 < 4:
                    gp = gen_prep(h + 1)
                    if os.environ.get("NOINT"):
                        for _ in gp:
                            pass
                        gp = None
                for qc in range(min(NQC, int(os.environ.get("QCMAX", "99"))) if (os.environ.get("BIS", "0") == "0" and h < int(os.environ.get("HMAX", "9"))) else 0):
                    q0 = qc * 512
                    Q_ = QXA[i][:, q0:q0 + 512]
                    qkeys = [('QXAq', i)] + [('QXAm', i, 4 * qc + j) for j in range(4)]
                    po = pO[pi % 2]
                    pok = ('pO', pi % 2)
                    pi += 1
                    P.op('pe', lambda e: e.matmul(po[:, 0:260], lhsT=zl[:], rhs=zr[:, 0:260], start=True, stop=True,
                                                  skip_group_check=True), reads=['zl', 'zr'], writes=[pok])
                    nkt = 4 * qc + 4
                    base = si

                    def qk(kt):
                        p_ = pSr[(base + kt) % 3]
                        P.op('pe', lambda e: e.matmul(p_[:], lhsT=KX[i][:, kt * 128:(kt + 1) * 128], rhs=Q_, start=True, stop=True),
                             reads=[('KX', i), ('KXi', i)] + qkeys, writes=[('pSr', (base + kt) % 3)])
                    LA = int(os.environ.get("LA", "2"))
                    for kt in range(min(LA, nkt)):
                        qk(kt)
                    for kt in range(nkt):
                        if kt + LA < nkt:
                            qk(kt + LA)
                        p_ = pSr[(base + kt) % 3]
                        pk = ('pSr', (base + kt) % 3)
                        t_ = PT[(base + kt) % 4]
                        tk = ('PT', (base + kt) % 4)
                        P.op('act', lambda e: e.activation(out=t_[:], in_=p_[:], func=AF.Exp, bias=negB[:, 0:1], scale=0.125),
                             reads=[pk, 'negBm'], writes=[tk])
                        off = kt - 4 * qc
                        if off >= 0:
                            P.op('pool', lambda e: e.tensor_tensor(out=t_[:], in0=t_[:], in1=tm4[:, off, :], op=ALU.mult),
                                 reads=[tk, 'tm4'], writes=[tk])
                        for j in range(4):
                            if kt > 4 * qc + j:
                                continue
                            P.op('pe', lambda e: e.matmul(po[:, j * 65:(j + 1) * 65], lhsT=t_[:, j * 128:(j + 1) * 128],
                                                          rhs=VX[i][:, kt, :], start=False, stop=(kt == 4 * qc + j),
                                                          skip_group_check=True), reads=[tk, ('VX', i)], writes=[pok])
                        bg.step(ml_rate)
                        if gp is not None:
                            pacc += prep_rate
                            while pacc >= 1.0:
                                pacc -= 1.0
                                try:
                                    next(gp)
                                except StopIteration:
                                    gp = None
                                    break
                    si += nkt
                    y_ = yb[qc % 2]
                    yk = ('yb', qc % 2)
                    pov = po[:, 0:260].rearrange("p (j d) -> p j d", d=65)
                    P.op('dve', lambda e: e.reciprocal(out=rz[:], in_=pov[:, :, 64]), reads=[pok], writes=['rz'])
                    P.op('dve', lambda e: e.tensor_tensor(out=y_[:], in0=pov[:, :, 0:64],
                                                          in1=rz[:].unsqueeze(2).broadcast_to([128, 4, 64]), op=ALU.mult),
                         reads=[pok, 'rz'], writes=[yk])
                    P.dma('sp', self.mix_d[q0:q0 + 512, 512 + h * 64:512 + (h + 1) * 64].rearrange("(j p) d -> p j d", p=128),
                          y_[:], reads=[yk], writes=[('mixb', h, qc)])
                if gp is not None:
                    for _ in gp:
                        bg.step(1.0)
            bg.drain()
            self.barrier()
        if not os.environ.get("SKIP_NSA"):
            self._phase2_nsa_pipe(l, pSr, pMi, identb, zl, zr)
        self.barrier()


Builder.phase2 = _phase2


def _phase2_nsa_pipe(self, l, pSr, pMi, identb, zl, zr, bgen=None, bg_yields=0):
    S, P, nc = self.S, self.P, self.nc
    NT = S // 128
    Nc = S // 16 - 1
    NCT = max(1, S // 2048)
    sb, ps = self.sb, self.ps
    allt = list(range(NT))
    pMb = pMi[:, 128:192].bitcast(BF16)
    pMb2 = pMi[:, 192:256].bitcast(BF16)
    pM = pMi[:, 256:320]
    with ExitStack() as st:
        trib = sb(st, "trib", [128, 128], BF16)
        triw = sb(st, "triw", [128, 128], BF16)
        cmask = sb(st, "cmask", [128, 17, 128], BF16)
        OVL = sb(st, "OVL", [128, NCT, 128], BF16)
        cols = sb(st, "cols", [128, 4], F32)
        NG = sb(st, "NG", [128, NT, 12], F32)
        P.dma('sp', trib[:], self.c_trib, writes=['trib'])
        P.dma('sp', triw[:], self.c_triw, writes=['triw'])
        P.dma('sp', cmask[:], self.c_cmask, writes=['cmask'])
        P.dma('sp', OVL[:], self.c_ovl.rearrange("(ct p) n -> p ct n", p=128)[:, 0:NCT, :], writes=['OVL'])
        P.dma('sp', cols[:], self.c_cols, writes=['cols'])
        self.dma_mid(NG[:], self.ng_d.rearrange("(j p) g -> p j g", p=128), NT, 8, reads=[('ng_d', j) for j in allt], writes=['NG'])
        negBc = self._negB(st, "negBc", self.nsa_q_norm[l], self.nsa_k_norm[l, 0])
        negBs = self._negB(st, "negBs", self.nsa_q_norm[l], self.nsa_k_norm[l, 1])
        negBw = self._negB(st, "negBw", self.nsa_q_norm[l], self.nsa_k_norm[l, 2])
        KSX = sb(st, "KSX", [128, S], BF16)
        KWX = sb(st, "KWX", [64, S], BF16)
        VSX = sb(st, "VSX", [128, NT, 65], BF16)
        VWX = sb(st, "VWX", [128, NT, 65], BF16)
        KcT = sb(st, "KcT", [64, NCT * 128], BF16)
        VCX = sb(st, "VCX", [128, NCT, 65], BF16)
        P.dma('sp', KSX[0:64, :], self.ksT, reads=[('ksT', j) for j in allt], writes=['KSX'])
        P.dma('sp', KSX[64:128, :], self.c_ind64, writes=['KSXi'])
        P.dma('sp', KWX[:], self.kwT, reads=[('kwT', j) for j in allt], writes=['KWX'])
        for V_, src, nm in ((VSX, self.vs_d, 'vs_d'), (VWX, self.vw_d, 'vw_d')):
            P.op('pool', lambda e: e.memset(V_[:], 1.0), writes=[nm + 'X'])
            self.dma_mid(V_[:, :, 0:64], src.rearrange("(kt p) d -> p kt d", p=128), NT, 8,
                         reads=[(nm, j) for j in allt], writes=[nm + 'X'])
        P.op('pool', lambda e: e.memset(VCX[:], 1.0), writes=['VCX'])
        pSc = ps(st, "pSc", [128, 512], F32) if bgen is None else pMi
        pSc_key = 'pSc' if bgen is None else 'pMi'
        pOcU = ps(st, "pOcU", [128, 512], F32)
        pOs = ps(st, "pOs", [128, 512], F32)
        pOw = ps(st, "pOw", [128, 512], F32)
        with ExitStack() as s2:
            KCV = sb(s2, "KCV", [128, S], BF16)
            W1s = sb(s2, "W1s", [128, 32, 128], F32)
            W1 = sb(s2, "W1", [128, 32, 128], BF16)
            pes = sb(s2, "pes", [32, 128], F32)
            peb = sb(s2, "peb", [32, 128], BF16)
            peT = sb(s2, "peT", [128, 32], BF16)
            w2s = sb(s2, "w2s", [128, 2, 64], F32)
            w2 = sb(s2, "w2", [128, 2, 64], BF16)
            gk0 = sb(s2, "gk0", [128, 64], F32)
            bias = sb(s2, "bias_c", [128, 2], F32)
            hidb = sb(s2, "hidb", [128, NCT * 128], BF16)
            kcn = sb(s2, "kcn", [128, 64], BF16)
            junk = sb(s2, "junk_c", [128, 64], F32)
            ssc = sb(s2, "ssc", [128, 2], F32)
            P.dma('sp', KCV[:], self.kcvcT, reads=[('kcvcT', b) for b in range(S // 512)], writes=['KCV'])
            for br in range(2):
                P.dma('sp', W1s[64 * br:64 * br + 64], self.cmp_w1[l, br].rearrange("(r d) j -> d r j", d=64), writes=['W1s'])
                P.dma('sp', w2s[:, br, :], self.cmp_w2[l, br], writes=['w2s'])
                P.dma('sp', pes[:, 64 * br:64 * br + 64], self.cmp_pe[l, br], writes=['pes'])
            P.dma('sp', gk0[:], self.nsa_k_norm[l, 0].partition_broadcast(128), writes=['gk0'])
            P.op('dve', lambda e: e.tensor_copy(out=W1[:], in_=W1s[:]), reads=['W1s'], writes=['W1'])
            P.op('dve', lambda e: e.tensor_copy(out=peb[:], in_=pes[:]), reads=['pes'], writes=['peb'])
            P.op('pe', lambda e: e.transpose(out=pMb[:, 0:32], in_=peb[:], identity=identb[0:32, 0:32]),
                 reads=['peb', 'identb'], writes=['pMi'])
            P.op('dve', lambda e: e.tensor_copy(out=peT[:], in_=pMb[:, 0:32]), reads=['pMi'], writes=['peT'])
            P.op('dve', lambda e: e.tensor_copy(out=w2[:], in_=w2s[:]), reads=['w2s'], writes=['w2'])
            P.op('pool', lambda e: e.memset(hidb[:], 0.0), writes=['hidb'])
            for br in range(2):
                rows = slice(64 * br, 64 * br + 64)
                kview = KCV[rows, :].rearrange("p (c s) -> p c s", s=16)
                for r in range(32):
                    P.op('pe', lambda e: e.matmul(pM[:, 0:1], lhsT=W1[rows, r, :], rhs=peT[rows, r:r + 1],
                                                  start=(r == 0), stop=(r == 31)), reads=['W1', 'peT'], writes=['pMi'])
                P.op('dve', lambda e: e.tensor_copy(out=bias[:, br:br + 1], in_=pM[:, 0:1]), reads=['pMi'], writes=['bias'])
                for r in range(32):
                    rhs = kview[:, 0:Nc, r] if r < 16 else kview[:, 1:Nc + 1, r - 16]
                    P.op('pe', lambda e: e.matmul(pSc[:, 0:Nc], lhsT=W1[rows, r, :], rhs=rhs,
                                                  start=(r == 0), stop=(r == 31)), reads=['W1', 'KCV'], writes=[pSc_key])
                P.op('act', lambda e: e.activation(out=hidb[:, 0:Nc], in_=pSc[:, 0:Nc], func=AF.Silu, bias=bias[:, br:br + 1]),
                     reads=[pSc_key, 'bias'], writes=['hidb'])
                for ct in range(NCT):
                    P.op('pe', lambda e: e.matmul(pM[:, 0:64], lhsT=hidb[:, ct * 128:(ct + 1) * 128], rhs=w2[:, br, :],
                                                  start=True, stop=True), reads=['hidb', 'w2'], writes=['pMi'])
                    if br == 0:
                        P.op('act', lambda e: e.activation(out=junk[:], in_=pM[:, 0:64], func=AF.Square, scale=0.125,
                                                           accum_out=ssc[:, 0:1]), reads=['pMi'], writes=['junk_c', 'ssc'])
                        P.op('dve', lambda e: e.tensor_scalar(out=ssc[:, 0:1], in0=ssc[:, 0:1], scalar1=EPS, scalar2=None,
                                                              op0=ALU.add), reads=['ssc'], writes=['ssc'])
                        P.op('act', lambda e: e.activation(out=ssc[:, 0:1], in_=ssc[:, 0:1], func=AF.Sqrt), reads=['ssc'], writes=['ssc'])
                        P.op('dve', lambda e: e.reciprocal(out=ssc[:, 1:2], in_=ssc[:, 0:1]), reads=['ssc'], writes=['ssc'])
                        P.op('dve', lambda e: e.scalar_tensor_tensor(out=kcn[:], in0=pM[:, 0:64], scalar=ssc[:, 1:2], in1=gk0[:],
                                                                     op0=ALU.mult, op1=ALU.mult),
                             reads=['pMi', 'ssc', 'gk0'], writes=['kcn'])
                        P.op('pe', lambda e: e.transpose(out=pMb[0:64, :], in_=kcn[:], identity=identb[:]),
                             reads=['kcn', 'identb'], writes=['pMi'])
                        P.op('act', lambda e: e.copy(out=KcT[:, ct * 128:(ct + 1) * 128], in_=pMb[0:64, :]),
                             reads=['pMi'], writes=['KcT'])
                    else:
                        P.op('act', lambda e: e.copy(out=VCX[:, ct, 0:64], in_=pM[:, 0:64]), reads=['pMi'], writes=['VCX'])
            self.barrier()
        QU = [sb(st, "QU", [64, 512], BF16) for _ in range(2)]
        QR0 = [sb(st, "QR0", [128, 512], BF16) for _ in range(2)]
        QR1 = [sb(st, "QR1", [128, 512], BF16) for _ in range(2)]
        PTc = [sb(st, "PTc", [128, 512], BF16) for _ in range(NCT)]
        PT = [sb(st, "PTn", [128, 512], BF16) for _ in range(4)]
        zc = sb(st, "zc", [128, 4], F32)
        rzc = sb(st, "rzc", [128, 4], F32)
        zz = sb(st, "zz", [128, 2, 4], F32)
        rzz = sb(st, "rzz", [128, 2, 4], F32)
        coef = sb(st, "coef", [128, 2, 4], F32)
        imp = sb(st, "imp", [128, 128], F32)
        work = sb(st, "work", [128, 128], F32)
        m8a = sb(st, "m8a", [128, 8], F32)
        m8b = sb(st, "m8b", [128, 8], F32)
        selm = sb(st, "selm", [128, 128], F32)
        MB = sb(st, "MB", [128, 128], BF16)
        MBs = sb(st, "MBs", [128, 128], BF16)
        yc = [sb(st, "yc", [128, 4, 64], F32) for _ in range(2)]
        yacc = sb(st, "yacc", [128, 4, 64], F32)
        yn = [sb(st, "yn", [128, 256], BF16) for _ in range(2)]

        def hb(ap):
            return ap.unsqueeze(1).broadcast_to([ap.shape[0], 4, 128])

        def h4(ap):
            return ap.rearrange("p (h t) -> p h t", h=4)

        def gen_prep(m):
            t0 = m * 128
            b2 = m % 2
            qu, qr0, qr1 = QU[b2], QR0[b2], QR1[b2]
            use_g1 = (2 * m + 1) >= 64
            P.dma('sp', h4(qu[:]), self.nquT[:, t0:t0 + 128].rearrange("(h d) t -> d h t", d=64),
                  reads=[('nquT', m)], writes=[('QU', b2)])
            P.dma('sp', h4(qr0[0:64, :]), self.nqrT[:, t0:t0 + 128].rearrange("(h d) t -> d h t", d=64),
                  reads=[('nqrT', m)], writes=[('QR0q', b2)])
            if use_g1:
                P.dma('sp', h4(qr1[0:64, :]), self.nqrT[:, t0:t0 + 128].rearrange("(h d) t -> d h t", d=64),
                      reads=[('nqrT', m)], writes=[('QR1q', b2)])
            yield
            ctn = min(NCT, (8 * m + 6) // 128 + 1)
            for ct in range(ctn):
                P.op('pe', lambda e: e.matmul(pSc[:], lhsT=KcT[:, ct * 128:(ct + 1) * 128], rhs=qu[:], start=True, stop=True),
                     reads=['KcT', ('QU', b2)], writes=[pSc_key])
                yield
                P.op('act', lambda e: e.activation(out=PTc[ct][:], in_=pSc[:], func=AF.Exp, bias=negBc[:, 0:1], scale=0.125),
                     reads=[pSc_key, 'negBc'], writes=[('PTc', ct)])
                yield
                r = m - 16 * ct
                if r <= 16:
                    P.op('pool', lambda e: e.tensor_tensor(out=h4(PTc[ct][:]), in0=h4(PTc[ct][:]), in1=hb(cmask[:, r, :]), op=ALU.mult),
                         reads=[('PTc', ct), 'cmask'], writes=[('PTc', ct)])
                    yield
                yield
            for h in range(4):
                for ct in range(ctn):
                    P.op('pe', lambda e: e.matmul(pOcU[:, h * 65:(h + 1) * 65], lhsT=PTc[ct][:, h * 128:(h + 1) * 128],
                                                  rhs=VCX[:, ct, :], start=(ct == 0), stop=(ct == ctn - 1)),
                         reads=[('PTc', ct), 'VCX'], writes=['pOcU'])
                    yield
                for ct in range(ctn):
                    P.op('pe', lambda e: e.matmul(pOcU[:, 260:388], lhsT=PTc[ct][:, h * 128:(h + 1) * 128],
                                                  rhs=OVL[:, ct, :], start=(ct == 0), stop=(ct == ctn - 1)),
                         reads=[('PTc', ct), 'OVL'], writes=['pOcU'])
                    yield
                P.op('dve', lambda e: e.tensor_scalar(out=zc[:, h:h + 1], in0=pOcU[:, h * 65 + 64:h * 65 + 65], scalar1=1e-30,
                                                      scalar2=None, op0=ALU.max), reads=['pOcU'], writes=['zc'])
                yield
                P.op('dve', lambda e: e.reciprocal(out=rzc[:, h:h + 1], in_=zc[:, h:h + 1]), reads=['zc'], writes=['rzc'])
                yield
                if h == 0:
                    P.op('dve', lambda e: e.tensor_scalar(out=imp[:], in0=pOcU[:, 260:388], scalar1=rzc[:, 0:1], scalar2=None,
                                                          op0=ALU.mult), reads=['pOcU', 'rzc'], writes=['imp'])
                    yield
                else:
                    P.op('dve', lambda e: e.scalar_tensor_tensor(out=imp[:], in0=pOcU[:, 260:388], scalar=rzc[:, h:h + 1],
                                                                 in1=imp[:], op0=ALU.mult, op1=ALU.add),
                         reads=['pOcU', 'rzc', 'imp'], writes=['imp'])
                    yield
                P.op('dve', lambda e: e.tensor_scalar(out=yc[b2][:, h, :], in0=pOcU[:, h * 65:h * 65 + 64], scalar1=rzc[:, h:h + 1],
                                                      scalar2=None, op0=ALU.mult), reads=['pOcU', 'rzc'], writes=[('yc', b2)])
                yield
                yield
            n1 = 2 * m + 1
            if n1 + 1 < 128:
                P.op('pool', lambda e: e.memset(imp[:, n1 + 1:128], -1e30), reads=['imp'], writes=['imp'])
                yield
            P.op('pool', lambda e: e.tensor_copy(out=imp[:, n1:n1 + 1], in_=cols[:, 0:1]), reads=['cols', 'imp'], writes=['imp'])
            yield
            P.op('pool', lambda e: e.memset(imp[:, n1 - 1:n1], 1e9), reads=['imp'], writes=['imp'])
            yield
            if n1 - 2 >= 0:
                P.op('dve', lambda e: e.tensor_tensor(out=imp[:, n1 - 2:n1 - 1], in0=imp[:, n1 - 2:n1 - 1], in1=cols[:, 1:2], op=ALU.max),
                     reads=['cols', 'imp'], writes=['imp'])
                yield
            P.op('pool', lambda e: e.memset(imp[:, 0:1], 1e9), reads=['imp'], writes=['imp'])
            yield
            yield
            P.op('dve', lambda e: e.max(out=m8a[:], in_=imp[:]), reads=['imp'], writes=['m8a'])
            yield
            P.op('dve', lambda e: e.match_replace(out=work[:], in_to_replace=m8a[:], in_values=imp[:], imm_value=-1e30),
                 reads=['imp', 'm8a'], writes=['work'])
            yield
            P.op('dve', lambda e: e.max(out=m8b[:], in_=work[:]), reads=['work'], writes=['m8b'])
            yield
            yield
            P.op('dve', lambda e: e.tensor_scalar(out=selm[:], in0=imp[:], scalar1=m8b[:, 7:8], scalar2=None, op0=ALU.is_ge),
                 reads=['imp', 'm8b'], writes=['selm'])
            yield
            P.op('dve', lambda e: e.tensor_scalar(out=MB[:], in0=selm[:], scalar1=-NEGM, scalar2=NEGM, op0=ALU.mult, op1=ALU.add),
                 reads=['selm'], writes=['MB'])
            yield
            if n1 + 1 < 128:
                P.op('pool', lambda e: e.memset(MB[:, n1 + 1:128], NEGM), reads=['MB'], writes=['MB'])
                yield
            P.op('pool', lambda e: e.tensor_copy(out=MB[:, n1:n1 + 1], in_=cols[:, 2:3]), reads=['cols', 'MB'], writes=['MB'])
            yield
            yield
            P.op('pool', lambda e: e.tensor_copy(out=MBs[:, 0:64], in_=MB[:, 64:128]), reads=['MB'], writes=['MBs'])
            yield
            P.op('pool', lambda e: e.tensor_copy(out=MBs[:, 64:128], in_=MB[:, 0:64]), reads=['MB'], writes=['MBs'])
            yield
            P.op('pe', lambda e: e.transpose(out=pMb, in_=MBs[:], identity=identb[:]), reads=['MBs', 'identb'], writes=['pMi'])
            yield
            P.op('act', lambda e: e.copy(out=h4(qr0[64:128, :]), in_=hb(pMb[64:128, :])), reads=['pMi'], writes=[('QR0m', b2)])
            yield
            if use_g1:
                P.op('pe', lambda e: e.transpose(out=pMb2, in_=MB[:], identity=identb[:]), reads=['MB', 'identb'], writes=['pMi'])
                yield
                P.op('act', lambda e: e.copy(out=h4(qr1[64:128, :]), in_=hb(pMb2[64:128, :])), reads=['pMi'], writes=[('QR1m', b2)])
                yield
            yield

        si = 0
        total_main = sum((m_ + 1) + (m_ + 1 - max(0, m_ - 4)) for m_ in range(NT))
        bg_rate = (bg_yields / float(total_main)) * 1.1 if bgen is not None else 0.0
        bg_state = {'g': bgen, 'acc': 0.0}

        def bg_step():
            if bg_state['g'] is None:
                return
            bg_state['acc'] += bg_rate
            while bg_state['acc'] >= 1.0 and bg_state['g'] is not None:
                bg_state['acc'] -= 1.0
                try:
                    next(bg_state['g'])
                except StopIteration:
                    bg_state['g'] = None
        g = gen_prep(0)
        for _ in g:
            pass
        for m in range(NT):
            t0 = m * 128
            b2 = m % 2
            qr0, qr1 = QR0[b2], QR1[b2]
            gp = gen_prep(m + 1) if m + 1 < NT else None
            P.op('pe', lambda e: e.matmul(pOs[:, 0:260], lhsT=zl[:], rhs=zr[:, 0:260], start=True, stop=True, skip_group_check=True),
                 reads=['zl', 'zr'], writes=['pOs'])
            P.op('pe', lambda e: e.matmul(pOw[:, 0:260], lhsT=zl[:], rhs=zr[:, 0:260], start=True, stop=True, skip_group_check=True),
                 reads=['zl', 'zr'], writes=['pOw'])
            items = [('s', kt) for kt in range(m + 1)] + [('w', kt) for kt in range(max(0, m - 4), m + 1)]
            n_it = len(items)
            rate = 80.0 / n_it
            base = si

            def qk(ix):
                typ, kt = items[ix]
                p_ = pSr[(base + ix) % 3]
                pk = ('pSr', (base + ix) % 3)
                if typ == 's':
                    if kt // 32 == 0:
                        P.op('pe', lambda e: e.matmul(p_[:], lhsT=KSX[:, kt * 128:(kt + 1) * 128], rhs=qr0[:], start=True, stop=True),
                             reads=['KSX', 'KSXi', ('QR0q', b2), ('QR0m', b2)], writes=[pk])
                    else:
                        P.op('pe', lambda e: e.matmul(p_[:], lhsT=KSX[:, kt * 128:(kt + 1) * 128], rhs=qr1[:], start=True, stop=True),
                             reads=['KSX', 'KSXi', ('QR1q', b2), ('QR1m', b2)], writes=[pk])
                else:
                    P.op('pe', lambda e: e.matmul(p_[:], lhsT=KWX[:, kt * 128:(kt + 1) * 128], rhs=qr0[0:64, :], start=True, stop=True),
                         reads=['KWX', ('QR0q', b2)], writes=[pk])
            for ix in range(min(2, n_it)):
                qk(ix)
            pacc = 0.0
            for ix in range(n_it):
                if ix + 2 < n_it:
                    qk(ix + 2)
                typ, kt = items[ix]
                p_ = pSr[(base + ix) % 3]
                pk = ('pSr', (base + ix) % 3)
                t_ = PT[(base + ix) % 4]
                tk = ('PTn', (base + ix) % 4)
                nb_ = negBs if typ == 's' else negBw
                P.op('act', lambda e: e.activation(out=t_[:], in_=p_[:], func=AF.Exp, bias=nb_[:, 0:1], scale=0.125),
                     reads=[pk, 'negBs', 'negBw'], writes=[tk])
                mk = None
                if kt == m:
                    mk = trib
                elif typ == 'w' and kt == m - 4:
                    mk = triw
                if mk is not None:
                    P.op('pool', lambda e: e.tensor_tensor(out=h4(t_[:]), in0=h4(t_[:]), in1=hb(mk[:]), op=ALU.mult),
                         reads=[tk, 'trib', 'triw'], writes=[tk])
                po_, pok, V_, vk = (pOs, 'pOs', VSX, 'vs_dX') if typ == 's' else (pOw, 'pOw', VWX, 'vw_dX')
                for h in range(4):
                    P.op('pe', lambda e: e.matmul(po_[:, h * 65:(h + 1) * 65], lhsT=t_[:, h * 128:(h + 1) * 128], rhs=V_[:, kt, :],
                                                  start=False, stop=(kt == m), skip_group_check=True),
                         reads=[tk, vk], writes=[pok])
                bg_step()
                if gp is not None:
                    pacc += rate
                    while pacc >= 1.0 and gp is not None:
                        pacc -= 1.0
                        try:
                            next(gp)
                        except StopIteration:
                            gp = None
            si += n_it
            if gp is not None:
                for _ in gp:
                    pass
            posv = pOs[:, 0:260].rearrange("p (h d) -> p h d", d=65)
            powv = pOw[:, 0:260].rearrange("p (h d) -> p h d", d=65)
            P.op('dve', lambda e: e.tensor_copy(out=zz[:, 0, :], in_=posv[:, :, 64]), reads=['pOs'], writes=['zz'])
            P.op('dve', lambda e: e.tensor_copy(out=zz[:, 1, :], in_=powv[:, :, 64]), reads=['pOw'], writes=['zz'])
            P.op('dve', lambda e: e.reciprocal(out=rzz[:], in_=zz[:]), reads=['zz'], writes=['rzz'])
            P.op('dve', lambda e: e.tensor_tensor(out=coef[:], in0=rzz[:], in1=NG[:, m, 4:12].rearrange("p (b h) -> p b h", b=2), op=ALU.mult),
                 reads=['rzz', 'NG'], writes=['coef'])
            for h in range(4):
                P.op('dve', lambda e: e.tensor_scalar(out=yacc[:, h, :], in0=yc[b2][:, h, :], scalar1=NG[:, m, h:h + 1], scalar2=None,
                                                      op0=ALU.mult), reads=[('yc', b2), 'NG'], writes=[('yacc', h)])
                P.op('dve', lambda e: e.scalar_tensor_tensor(out=yacc[:, h, :], in0=posv[:, h, 0:64], scalar=coef[:, 0, h:h + 1],
                                                             in1=yacc[:, h, :], op0=ALU.mult, op1=ALU.add),
                     reads=['pOs', 'coef', ('yacc', h)], writes=[('yacc', h)])
                P.op('dve', lambda e: e.scalar_tensor_tensor(out=yn[b2][:, h * 64:(h + 1) * 64], in0=powv[:, h, 0:64],
                                                             scalar=coef[:, 1, h:h + 1], in1=yacc[:, h, :], op0=ALU.mult, op1=ALU.add),
                     reads=['pOw', 'coef', ('yacc', h)], writes=[('yn', b2)])
            P.dma('sp', self.mix_d[t0:t0 + 128, 768:1024], yn[b2][:], reads=[('yn', b2)], writes=[('mixn', m)])
        while bg_state['g'] is not None:
            try:
                next(bg_state['g'])
            except StopIteration:
                bg_state['g'] = None


Builder._phase2_nsa_pipe = _phase2_nsa_pipe


def _phase2_nsa2(self, l):
    P = self.P
    with ExitStack() as st:
        identb = self.sb(st, "identb_n2", [128, 128], BF16)
        zl = self.sb(st, "zl_n2", [1, 128], BF16)
        zr = self.sb(st, "zr_n2", [1, 512], BF16)
        P.dma('sp', identb[:], self.c_identb, writes=['identb'])
        P.op('pool', lambda e: e.memset(zl[:], 0.0), writes=['zl'])
        P.op('pool', lambda e: e.memset(zr[:], 0.0), writes=['zr'])
        pSr = [self.ps(st, "pSr2", [128, 512], F32) for _ in range(3)]
        pMi = self.ps(st, "pMi2", [128, 512], F32)
        self._phase2_nsa_pipe(l, pSr, pMi, identb, zl, zr)
        self.barrier()


Builder.phase2_nsa2 = _phase2_nsa2


def _phase2b(self, l):
    S, P, nc = self.S, self.P, self.nc
    NCH = S // 64
    NT = S // 128
    NQC = S // 512
    NB = S // 256
    Nc = S // 16 - 1
    NCT = max(1, S // 2048)
    LNSC = math.log(128.0 ** -0.5)
    sb, ps = self.sb, self.ps
    allt = list(range(NT))
    with ExitStack() as st:
        identb = sb(st, "identb2", [128, 128], BF16)
        identf = sb(st, "identf2", [128, 128], F32)
        ut = sb(st, "ut2", [128, 128], F32)
        tri = sb(st, "tri2", [128, 128], F32)
        zl = sb(st, "zl2", [1, 128], BF16)
        zr = sb(st, "zr2", [1, 512], BF16)
        P.dma('sp', identb[:], self.c_identb, writes=['identb'])
        P.dma('sp', identf[:], self.c_identf, writes=['identf'])
        P.dma('sp', ut[:], self.c_ut, writes=['ut'])
        P.dma('sp', tri[:], self.c_tri, writes=['tri'])
        P.op('pool', lambda e: e.memset(zl[:], 0.0), writes=['zl'])
        P.op('pool', lambda e: e.memset(zr[:], 0.0), writes=['zr'])
        uT = sb(st, "uT_m", [64, 4, NCH], F32)
        u2T = sb(st, "u2T_m", [64, 4, NCH], F32)
        flT = sb(st, "flT_m", [64, 4, NCH], F32)
        decB = sb(st, "decB", [128, 4, NCH], F32)
        with ExitStack() as s2:
            li = sb(s2, "li", [NCH, 4, 64], F32)
            lf = sb(s2, "lf", [NCH, 4, 64], F32)
            ones = sb(s2, "ones", [NCH, 64], F32)
            Fin = sb(s2, "Fin", [NCH, 4, 64], F32)
            Ft = sb(s2, "Ft", [NCH, 4, 64], F32)
            a_ = sb(s2, "a_", [NCH, 4, 64], F32)
            Ain = sb(s2, "Ain", [NCH, 4, 64], F32)
            tot = sb(s2, "tot", [NCH, 4], F32)
            cmax = sb(s2, "cmax", [NCH, 4], F32)
            cmT = sb(s2, "cmT", [4, NCH], F32)
            ET = sb(s2, "ET", [4, NCH], F32)
            ETn = sb(s2, "ETn", [4, NCH], F32)
            Ec = sb(s2, "Ec", [NCH, 4], F32)
            Enc = sb(s2, "Enc", [NCH, 4], F32)
            tmp = sb(s2, "tmpm", [NCH, 4, 64], F32)
            uu = sb(s2, "uu", [NCH, 4, 64], F32)
            uu2 = sb(s2, "uu2", [NCH, 4, 64], F32)
            fl = sb(s2, "fl", [NCH, 4, 64], F32)
            dec = sb(s2, "dec", [NCH, 4], F32)
            decrep = sb(s2, "decrep", [NCH, 4, 128], F32)
            pa = ps(s2, "pa", [128, 512], F32)
            pb = ps(s2, "pb", [128, 512], F32)
            P.dma('sp', li[:], self.gi_d.rearrange("h (c j) -> c h j", j=64),
                  reads=[('gi_d', b) for b in range(S // 512)], writes=['li'])
            P.dma('sp', lf[:], self.gf_d.rearrange("h (c j) -> c h j", j=64),
                  reads=[('gf_d', b) for b in range(S // 512)], writes=['lf'])
            P.op('pool', lambda e: e.memset(ones[:], 1.0), writes=['ones'])
            for hh in range(4):
                P.op('dve', lambda e: e.tensor_tensor_scan(out=Fin[:, hh, :], data0=ones[:], data1=lf[:, hh, :],
                                                           initial=0.0, op0=ALU.mult, op1=ALU.add),
                     reads=['ones', 'lf'], writes=['Fin'])
            P.op('dve', lambda e: e.tensor_copy(out=tot[:], in_=Fin[:, :, 63]), reads=['Fin'], writes=['tot'])
            P.op('pe', lambda e: e.matmul(pa[0:NCH, 0:4], lhsT=ut[0:NCH, 0:NCH], rhs=tot[:], start=True, stop=True),
                 reads=['ut', 'tot'], writes=['pa'])
            P.op('dve', lambda e: e.tensor_tensor(out=Ft[:], in0=Fin[:],
                                                  in1=pa[0:NCH, 0:4].unsqueeze(2).broadcast_to([NCH, 4, 64]), op=ALU.add),
                 reads=['Fin', 'pa'], writes=['Ft'])
            P.op('dve', lambda e: e.tensor_tensor(out=a_[:], in0=li[:], in1=Ft[:], op=ALU.subtract),
                 reads=['li', 'Ft'], writes=['a_'])
            for hh in range(4):
                P.op('dve', lambda e: e.tensor_tensor_scan(out=Ain[:, hh, :], data0=a_[:, hh, :], data1=a_[:, hh, :],
                                                           initial=-1e30, op0=ALU.max, op1=ALU.max),
                     reads=['a_'], writes=['Ain'])
            P.op('dve', lambda e: e.tensor_copy(out=cmax[:], in_=Ain[:, :, 63]), reads=['Ain'], writes=['cmax'])
            P.op('pe', lambda e: e.transpose(out=pb[0:4, 0:NCH], in_=cmax[:], identity=identf[0:NCH, 0:NCH]),
                 reads=['cmax', 'identf'], writes=['pb'])
            P.op('dve', lambda e: e.tensor_copy(out=cmT[:], in_=pb[0:4, 0:NCH]), reads=['pb'], writes=['cmT'])
            P.op('dve', lambda e: e.tensor_tensor_scan(out=ET[:], data0=cmT[:], data1=cmT[:], initial=0.0,
                                                       op0=ALU.max, op1=ALU.max), reads=['cmT'], writes=['ET'])
            if NCH > 1:
                P.op('dve', lambda e: e.tensor_copy(out=ETn[:, 0:NCH - 1], in_=ET[:, 1:NCH]), reads=['ET'], writes=['ETn'])
            P.op('dve', lambda e: e.tensor_copy(out=ETn[:, NCH - 1:NCH], in_=ET[:, NCH - 1:NCH]), reads=['ET'], writes=['ETn'])
            P.op('pe', lambda e: e.transpose(out=pa[0:NCH, 0:4], in_=ET[:], identity=identf[0:4, 0:4]),
                 reads=['ET', 'identf'], writes=['pa'])
            P.op('dve', lambda e: e.tensor_copy(out=Ec[:], in_=pa[0:NCH, 0:4]), reads=['pa'], writes=['Ec'])
            P.op('pe', lambda e: e.transpose(out=pb[0:NCH, 0:4], in_=ETn[:], identity=identf[0:4, 0:4]),
                 reads=['ETn', 'identf'], writes=['pb'])
            P.op('dve', lambda e: e.tensor_copy(out=Enc[:], in_=pb[0:NCH, 0:4]), reads=['pb'], writes=['Enc'])
            Eb = Ec[:].unsqueeze(2).broadcast_to([NCH, 4, 64])
            Enb = Enc[:].unsqueeze(2).broadcast_to([NCH, 4, 64])
            P.op('dve', lambda e: e.tensor_tensor(out=tmp[:], in0=a_[:], in1=Eb, op=ALU.subtract),
                 reads=['a_', 'Ec'], writes=['tmp'])
            P.op('act', lambda e: e.activation(out=uu[:], in_=tmp[:], func=AF.Exp, bias=LNSC), reads=['tmp'], writes=['uu'])
            P.op('dve', lambda e: e.tensor_tensor(out=tmp[:], in0=a_[:], in1=Enb, op=ALU.subtract),
                 reads=['a_', 'Enc'], writes=['tmp'])
            P.op('act', lambda e: e.activation(out=uu2[:], in_=tmp[:], func=AF.Exp, bias=LNSC), reads=['tmp'], writes=['uu2'])
            P.op('dve', lambda e: e.tensor_tensor(out=tmp[:], in0=Ft[:], in1=Eb, op=ALU.add),
                 reads=['Ft', 'Ec'], writes=['tmp'])
            P.op('act', lambda e: e.activation(out=fl[:], in_=tmp[:], func=AF.Exp, scale=-1.0), reads=['tmp'], writes=['fl'])
            P.op('dve', lambda e: e.tensor_tensor(out=dec[:], in0=Ec[:], in1=Enc[:], op=ALU.subtract),
                 reads=['Ec', 'Enc'], writes=['dec'])
            P.op('act', lambda e: e.activation(out=dec[:], in_=dec[:], func=AF.Exp), reads=['dec'], writes=['dec'])
            P.op('dve', lambda e: e.tensor_copy(out=decrep[:], in_=dec[:].unsqueeze(2).broadcast_to([NCH, 4, 128])),
                 reads=['dec'], writes=['decrep'])
            for src, dst, nm in ((uu, uT, 'uT'), (uu2, u2T, 'u2T'), (fl, flT, 'flT')):
                for hh in range(4):
                    P.op('pe', lambda e: e.transpose(out=pa[0:64, hh * 128:hh * 128 + NCH], in_=src[:, hh, :],
                                                     identity=identf[0:NCH, 0:NCH]),
                         reads=['uu', 'uu2', 'fl', 'identf'], writes=['pa'])
                P.op('dve', lambda e: e.tensor_copy(out=dst[:], in_=pa[0:64, :].rearrange("p (h c) -> p h c", h=4)[:, :, 0:NCH]),
                     reads=['pa'], writes=[nm])
            for hh in range(4):
                P.op('pe', lambda e: e.matmul(pb[:, hh * 128:hh * 128 + NCH], lhsT=decrep[:, hh, :],
                                              rhs=identf[0:NCH, 0:NCH], start=True, stop=True),
                     reads=['decrep', 'identf'], writes=['pb'])
            P.op('dve', lambda e: e.tensor_copy(out=decB[:], in_=pb[:].rearrange("p (h c) -> p h c", h=4)[:, :, 0:NCH]),
                 reads=['pb'], writes=['decB'])
            self.barrier()

        pSr = [ps(st, "pSr", [128, 512], F32) for _ in range(3)]
        pMi = ps(st, "pMi", [128, 512], F32)
        pMb = pMi[:, 128:192].bitcast(BF16)
        pMb2 = pMi[:, 192:256].bitcast(BF16)
        pM = pMi[:, 256:288]
        bg = BG()

        GC = 4
        NG = S // (64 * GC)
        TG = 64 * GC
        qg = [sb(st, "qg", [128, 4, TG], BF16) for _ in range(2)]
        kg = [sb(st, "kg", [128, 4, TG], BF16) for _ in range(2)]
        vg = [sb(st, "vg", [64, GC, 4, 129], BF16) for _ in range(2)]
        ogg = [sb(st, "ogg", [64, GC, 512], BF16) for _ in range(2)]
        ym = [sb(st, "ym", [64, GC, 512], BF16) for _ in range(2)]
        G = [sb(st, "G", [128, 129], F32) for _ in range(4)]
        Gb = [sb(st, "Gb", [128, 129], BF16) for _ in range(4)]
        ku2 = [sb(st, "ku2", [64, 128], BF16) for _ in range(2)]
        Sm = [sb(st, "Sm", [64, 64], BF16) for _ in range(2)]
        junkm = sb(st, "junkm", [64, 128], BF16)
        scm = [sb(st, "scm", [64, 8], F32) for _ in range(2)]
        mlA = [ps(st, "mlA", [128, 512], F32) for _ in range(1)]

        def gen_ml():
            for i in range(2):
                P.op('pool', lambda e: e.memset(vg[i][:], 1.0), writes=[('vg', i)])
            for hh in range(4):
                P.op('pool', lambda e: e.memset(G[hh][:], 0.0), writes=[('G', hh)])
                P.op('pool', lambda e: e.memset(Gb[hh][:], 0.0), writes=[('Gb', hh)])
            yield

            def load_group(g):
                i = g % 2
                tk = slice(g * TG, (g + 1) * TG)
                bl = sorted(set([(g * TG) // 512, ((g + 1) * TG - 1) // 512]))
                jl = list(range((g * TG) // 128, ((g + 1) * TG + 127) // 128))
                P.dma('sp', qg[i][:], self.qkT[0:512, tk].rearrange("(h p) t -> p h t", p=128),
                      reads=[('qkT', ft, b) for ft in range(4) for b in bl], writes=[('qg', i)])
                P.dma('sp', kg[i][:], self.qkT[512:1024, tk].rearrange("(h p) t -> p h t", p=128),
                      reads=[('qkT', ft, b) for ft in range(4, 8) for b in bl], writes=[('kg', i)])
                for hh in range(4):
                    P.dma('sp', vg[i][:, :, hh, 0:128],
                          self.mv_d[tk, hh * 128:(hh + 1) * 128].rearrange("(c s) e -> s c e", s=64),
                          reads=[('mv_d', j) for j in jl], writes=[('vg', i)])
                P.dma('sp', ogg[i][:], self.og_d[tk, :].rearrange("(c s) e -> s c e", s=64),
                      reads=[('og_d', j) for j in jl], writes=[('ogg', i)])
            load_group(0)
            it = 0
            for g in range(NG):
                if g + 1 < NG:
                    load_group(g + 1)
                gi_ = g % 2
                for cl in range(GC):
                    c = g * GC + cl
                    for hh in range(4):
                        k_ = kg[gi_][:, hh, cl * 64:(cl + 1) * 64]
                        q_ = qg[gi_][:, hh, cl * 64:(cl + 1) * 64]
                        v_ = vg[gi_][:, cl, hh, :]
                        i2 = it % 2
                        it += 1
                        A = mlA[i2 % len(mlA)]
                        if os.environ.get("UNPACK"):
                            pS_ = pSr[0][0:64, 0:64]
                            pO_ = pSr[1][0:64, 0:129]
                            pG_ = pSr[2][:, 0:129]
                            pkT_ = A[0:64, 0:64].bitcast(BF16)
                        else:
                            pS_ = A[0:64, 0:64]
                            pO_ = A[0:64, 64:193]
                            pG_ = A[:, 256:385]
                            pkT_ = A[0:64, 448:512].bitcast(BF16)
                        P.op('pe', lambda e: e.transpose(out=pkT_, in_=k_, identity=identb[:]),
                             reads=[('kg', gi_), 'identb'], writes=[('mlA', i2 % len(mlA))])
                        P.op('pe', lambda e: e.matmul(pS_, lhsT=k_, rhs=q_, start=True, stop=True),
                             reads=[('kg', gi_), ('qg', gi_)], writes=[('mlA', i2 % len(mlA))])
                        yield
                        P.op('act', lambda e: e.activation(out=ku2[i2][:], in_=pkT_, func=AF.Copy,
                                                           scale=u2T[:, hh, c:c + 1]),
                             reads=[('mlA', i2 % len(mlA)), 'u2T'], writes=[('ku2', i2)])
                        P.op('dve', lambda e: e.scalar_tensor_tensor(out=Sm[i2][:], in0=pS_,
                                                                     scalar=uT[:, hh, c:c + 1], in1=tri[0:64, 0:64],
                                                                     op0=ALU.mult, op1=ALU.mult),
                             reads=[('mlA', i2 % len(mlA)), 'uT', 'tri'], writes=[('Sm', i2)])
                        yield
                        P.op('pe', lambda e: e.matmul(pO_, lhsT=Sm[i2][:], rhs=v_, start=True, stop=False),
                             reads=[('Sm', i2), ('vg', gi_)], writes=[('mlA', i2 % len(mlA))])
                        P.op('pe', lambda e: e.matmul(pO_, lhsT=q_, rhs=Gb[hh][:], start=False, stop=True),
                             reads=[('qg', gi_), ('Gb', hh)], writes=[('mlA', i2 % len(mlA))])
                        P.op('pe', lambda e: e.matmul(pG_, lhsT=ku2[i2][:], rhs=v_, start=True, stop=True),
                             reads=[('ku2', i2), ('vg', gi_)], writes=[('mlA', i2 % len(mlA))])
                        yield
                        P.op('dve', lambda e: e.scalar_tensor_tensor(out=G[hh][:], in0=G[hh][:], scalar=decB[:, hh, c:c + 1],
                                                                     in1=pG_, op0=ALU.mult, op1=ALU.add),
                             reads=[('G', hh), 'decB', ('mlA', i2 % len(mlA))], writes=[('G', hh)])
                        P.op('pool', lambda e: e.tensor_copy(out=Gb[hh][:], in_=G[hh][:]), reads=[('G', hh)], writes=[('Gb', hh)])
                        s_ = scm[i2]
                        sk = ('scm', i2)
                        P.op('act', lambda e: e.activation(out=junkm[:], in_=pO_[:, 0:128], func=AF.Square,
                                                           scale=128.0 ** -0.5, accum_out=s_[:, 0:1]),
                             reads=[('mlA', i2 % len(mlA))], writes=['junkm', sk])
                        yield
                        P.op('dve', lambda e: e.tensor_scalar(out=s_[:, 6:7], in0=pO_[:, 128:129], scalar1=-1.0,
                                                              scalar2=flT[:, hh, c:c + 1], op0=ALU.mult, op1=ALU.max),
                             reads=[('mlA', i2 % len(mlA)), 'flT'], writes=[sk])
                        P.op('dve', lambda e: e.tensor_tensor(out=s_[:, 1:2], in0=s_[:, 6:7], in1=pO_[:, 128:129], op=ALU.max),
                             reads=[('mlA', i2 % len(mlA)), sk], writes=[sk])
                        P.op('dve', lambda e: e.tensor_tensor(out=s_[:, 2:3], in0=s_[:, 1:2], in1=s_[:, 1:2], op=ALU.mult),
                             reads=[sk], writes=[sk])
                        P.op('dve', lambda e: e.scalar_tensor_tensor(out=s_[:, 3:4], in0=s_[:, 2:3], scalar=EPS, in1=s_[:, 0:1],
                                                                     op0=ALU.mult, op1=ALU.add), reads=[sk], writes=[sk])
                        yield
                        if os.environ.get("USE_SQRT"):
                            P.op('act', lambda e: e.activation(out=s_[:, 4:5], in_=s_[:, 3:4], func=AF.Sqrt), reads=[sk], writes=[sk])
                            P.op('dve', lambda e: e.reciprocal(out=s_[:, 5:6], in_=s_[:, 4:5]), reads=[sk], writes=[sk])
                        else:
                            P.op('act', lambda e: e.activation(out=s_[:, 4:5], in_=s_[:, 3:4], func=AF.Ln), reads=[sk], writes=[sk])
                            P.op('act', lambda e: e.activation(out=s_[:, 5:6], in_=s_[:, 4:5], func=AF.Exp, scale=-0.5), reads=[sk], writes=[sk])
                        P.op('dve', lambda e: e.scalar_tensor_tensor(out=ym[gi_][:, cl, hh * 128:(hh + 1) * 128],
                                                                     in0=pO_[:, 0:128], scalar=s_[:, 5:6],
                                                                     in1=ogg[gi_][:, cl, hh * 128:(hh + 1) * 128],
                                                                     op0=ALU.mult, op1=ALU.mult),
                             reads=[('mlA', i2 % len(mlA)), sk, ('ogg', gi_)], writes=[('ym', gi_)])
                        yield
                P.dma('sp', self.mix_d[g * TG:(g + 1) * TG, 0:512].rearrange("(c s) e -> s c e", s=64), ym[gi_][:],
                      reads=[('ym', gi_)], writes=[('mixm', g)])
        ML_YIELDS = NCH * 4 * 6 + 1
        g_ml = gen_ml()
        if os.environ.get("NO_NSA"):
            for _ in g_ml:
                pass
        else:
            self._phase2_nsa_pipe(l, pSr, pMi, identb, zl, zr, bgen=g_ml, bg_yields=ML_YIELDS)
        self.barrier()


Builder.phase2b = _phase2b
```

```python
import math
import os
from contextlib import ExitStack

import numpy as np
import ml_dtypes

import concourse.bass as bass
import concourse.mybir as mybir
from concourse.bass_utils import run_bass_kernel_spmd

F32 = mybir.dt.float32
BF16 = mybir.dt.bfloat16
AF = mybir.ActivationFunctionType
ALU = mybir.AluOpType
AX = mybir.AxisListType

D_MODEL = 1024
D_IN = 3476
D_FF = 4096
EPS = 1e-6
NEGM = -30000.0
SAME_ENGINE_SYNC = True


class Prog:
    def __init__(self, nc, es, n_dma=24):
        self.nc = nc
        self.es = es
        self.eng = dict(pe=nc.tensor, act=nc.scalar, dve=nc.vector, pool=nc.gpsimd, sp=nc.sync)
        self.esem = {e: es.enter_context(nc.semaphore("s_" + e)) for e in self.eng}
        self.ecnt = {e: 0 for e in self.eng}
        self.dsem = [es.enter_context(nc.semaphore("d_%d" % i)) for i in range(n_dma)]
        self.dval = [0] * n_dma
        self.dnext = 0
        self.seen = {e: {} for e in self.eng}
        self.lastw = {}
        self.readers = {}
        self.nops = 0

    def _wait(self, eng, ev):
        kind, name, val = ev
        if kind == 'e' and name == eng:
            if eng == 'pe' or eng == 'sp' or not SAME_ENGINE_SYNC:
                return
        key = (kind, name)
        if self.seen[eng].get(key, 0) >= val:
            return
        self.seen[eng][key] = val
        sem = self.esem[name] if kind == 'e' else self.dsem[name]
        self.eng[eng].wait_ge(sem, val)

    def _deps(self, reads, writes):
        evs = []
        for k in reads:
            w = self.lastw.get(k)
            if w is not None:
                evs.append(w)
        for k in writes:
            w = self.lastw.get(k)
            if w is not None:
                evs.append(w)
            evs.extend(self.readers.get(k, ()))
        return evs

    def _commit(self, me, reads, writes):
        for k in reads:
            lst = self.readers.setdefault(k, [])
            lst[:] = [r for r in lst if (r[0], r[1]) != (me[0], me[1])]
            lst.append(me)
        for k in writes:
            self.lastw[k] = me
            self.readers[k] = []

    def op(self, eng, fn, reads=(), writes=()):
        for ev in self._deps(reads, writes):
            self._wait(eng, ev)
        ins = fn(self.eng[eng])
        self.ecnt[eng] += 1
        ins.then_inc(self.esem[eng], 1)
        self._commit(('e', eng, self.ecnt[eng]), reads, writes)
        self.nops += 1
        return ins

    def dma(self, q, out, in_, reads=(), writes=(), **kw):
        if q == 'auto':
            qs = os.environ.get("DMAQ", "sp").split(",")
            self.rr = getattr(self, 'rr', 0) + 1
            q = qs[self.rr % len(qs)]
        slot = self.dnext
        self.dnext = (self.dnext + 1) % len(self.dsem)
        evs = self._deps(reads, writes)
        if self.dval[slot] > 0:
            evs.append(('d', slot, self.dval[slot]))
        for ev in evs:
            self._wait(q, ev)
        self.dval[slot] += 16
        self.eng[q].dma_start(out=out, in_=in_, **kw).then_inc(self.dsem[slot], 16)
        self._commit(('d', slot, self.dval[slot]), reads, writes)
        self.nops += 1

    def finish(self):
        for slot in range(len(self.dsem)):
            if self.dval[slot] > 0:
                self._wait('sp', ('d', slot, self.dval[slot]))
        for e in self.eng:
            if e != 'sp' and self.ecnt[e] > 0:
                self._wait('sp', ('e', e, self.ecnt[e]))


class Builder:
    def __init__(self, S, depth, dbg=None):
        self.S = S
        self.depth = depth
        self.dbg = dbg or []
        self.nc = bass.Bass("TRN2", target_bir_lowering=False)
        self.es = ExitStack()
        self.P = Prog(self.nc, self.es)
        self.uid = 0

    def dram_in(self, name, shape, dt=F32):
        return self.nc.dram_tensor(name, list(shape), dt, kind="ExternalInput").ap()

    def dram_out(self, name, shape, dt=F32):
        return self.nc.dram_tensor(name, list(shape), dt, kind="ExternalOutput").ap()

    def dram_tmp(self, name, shape, dt):
        return self.nc.dram_tensor(name, list(shape), dt, kind="Internal").ap()

    def sb(self, st, name, shape, dt):
        self.uid += 1
        return st.enter_context(self.nc.sbuf_tensor("%s_%d" % (name, self.uid), list(shape), dt))

    def ps(self, st, name, shape, dt=F32):
        self.uid += 1
        return st.enter_context(self.nc.psum_tensor("%s_%d" % (name, self.uid), list(shape), dt))

    def dma_mid(self, out, in_, n_mid, step, reads=(), writes=()):
        for a in range(0, n_mid, step):
            b_ = min(n_mid, a + step)
            self.P.dma('sp', out[:, a:b_, :], in_[:, a:b_, :], reads=reads, writes=writes)

    def barrier(self):
        P = self.P
        for e in P.eng:
            for o in P.eng:
                if P.ecnt[o] > 0 and not (o == e and e in ('pe', 'sp')):
                    key = ('e', o)
                    if P.seen[e].get(key, 0) < P.ecnt[o]:
                        P.seen[e][key] = P.ecnt[o]
                        P.eng[e].wait_ge(P.esem[o], P.ecnt[o])
            for slot in range(len(P.dsem)):
                if P.dval[slot] > 0:
                    P._wait(e, ('d', slot, P.dval[slot]))

    def declare(self):
        S, L = self.S, self.depth
        self.x_in = self.dram_in("x", [S, D_MODEL])
        self.w_in = self.dram_in("w_in", [L, D_MODEL, D_IN])
        self.b_if = self.dram_in("b_if", [L, 8])
        self.conv_qk = self.dram_in("conv_qk", [L, 4, 1024])
        self.m_norm = self.dram_in("m_norm", [L, 512])
        self.moba_qk_norm = self.dram_in("moba_qk_norm", [L, 2, 64])
        self.nsa_q_norm = self.dram_in("nsa_q_norm", [L, 64])
        self.nsa_k_norm = self.dram_in("nsa_k_norm", [L, 3, 64])
        self.cmp_pe = self.dram_in("cmp_pe", [L, 2, 32, 64])
        self.cmp_w1 = self.dram_in("cmp_w1", [L, 2, 2048, 128])
        self.cmp_w2 = self.dram_in("cmp_w2", [L, 2, 128, 64])
        self.w_out = self.dram_in("w_out", [L, 1024, 1024])
        self.norm_mix = self.dram_in("norm_mix", [L, 1024])
        self.norm_ffn = self.dram_in("norm_ffn", [L, 1024])
        self.w_ff1 = self.dram_in("w_ff1", [L, 1024, D_FF])
        self.w_ff2 = self.dram_in("w_ff2", [L, D_FF, 1024])
        self.c_identb = self.dram_in("c_identb", [128, 128], BF16)
        self.c_identf = self.dram_in("c_identf", [128, 128], F32)
        self.c_cos = self.dram_in("c_cos", [S, 8], F32)
        self.c_sin = self.dram_in("c_sin", [S, 8], F32)
        self.c_ut = self.dram_in("c_ut", [128, 128], F32)
        self.c_tri = self.dram_in("c_tri", [128, 128], F32)
        self.c_ind32 = self.dram_in("c_ind32", [32, S], BF16)
        self.c_ind64 = self.dram_in("c_ind64", [64, S], BF16)
        self.c_tm4 = self.dram_in("c_tm4", [128, 4, 512], BF16)
        self.c_trib = self.dram_in("c_trib", [128, 128], BF16)
        self.c_triw = self.dram_in("c_triw", [128, 128], BF16)
        self.c_cmask = self.dram_in("c_cmask", [128, 17, 128], BF16)
        self.c_ovl = self.dram_in("c_ovl", [512, 128], BF16)
        self.c_cols = self.dram_in("c_cols", [128, 4], F32)
        self.y_out = self.dram_out("y", [S, D_MODEL])
        dbg = self.dbg

        def tmp(name, shape, dt):
            if name in dbg:
                return self.dram_out(name, shape, dt)
            return self.dram_tmp(name, shape, dt)
        self.xbuf = tmp("xbuf", [S, D_MODEL], F32)
        self.qkT = tmp("qkT", [1024, S], BF16)
        self.kcvcT = tmp("kcvcT", [128, S], BF16)
        self.gi_d = tmp("gi_d", [4, S], F32)
        self.gf_d = tmp("gf_d", [4, S], F32)
        self.mv_d = tmp("mv_d", [S, 512], BF16)
        self.og_d = tmp("og_d", [S, 512], BF16)
        self.bqT = tmp("bqT", [256, S], BF16)
        self.bkT = tmp("bkT", [256, S], BF16)
        self.bv_d = tmp("bv_d", [S, 256], BF16)
        self.nqrT = tmp("nqrT", [256, S], BF16)
        self.nquT = tmp("nquT", [256, S], BF16)
        self.ksT = tmp("ksT", [64, S], BF16)
        self.kwT = tmp("kwT", [64, S], BF16)
        self.vs_d = tmp("vs_d", [S, 64], BF16)
        self.vw_d = tmp("vw_d", [S, 64], BF16)
        self.ng_d = tmp("ng_d", [S, 12], F32)
        self.mix_d = tmp("mix_d", [S, 1024], BF16)

    def phase1(self, l, xsrc):
        S, P, nc = self.S, self.P, self.nc
        NB = S // 512
        with ExitStack() as st:
            sb, ps = self.sb, self.ps
            w_bf = sb(st, "w_in", [128, 8, D_IN], BF16)
            with ExitStack() as st2:
                stage = [sb(st2, "wst", [128, D_IN], F32) for _ in range(2)]
                for kt in range(8):
                    s_ = stage[kt % 2]
                    P.dma('sp', s_[:], self.w_in[l, kt * 128:(kt + 1) * 128, :], writes=[('wst', kt % 2)])
                    P.op(['pool', 'dve'][kt % 2], lambda e: e.tensor_copy(out=w_bf[:, kt, :], in_=s_[:]),
                         reads=[('wst', kt % 2)], writes=[('w_in', kt)])
                self.barrier()
            identb = sb(st, "identb", [128, 128], BF16)
            gmix = sb(st, "gmix", [128, 1024], F32)
            mnorm = sb(st, "mnorm", [128, 512], F32)
            gain20 = sb(st, "gain20", [128, 20, 64], F32)
            convw = sb(st, "convw", [128, 4, 8], F32)
            bi = sb(st, "bi", [4, 1], F32)
            nbf = sb(st, "nbf", [4, 1], F32)
            cosT = sb(st, "cosT", [128, S // 128, 8], F32)
            sinT = sb(st, "sinT", [128, S // 128, 8], F32)
            P.dma('sp', identb[:], self.c_identb, writes=['identb'])
            P.dma('sp', gmix[:], self.norm_mix[l].partition_broadcast(128), writes=['gmix'])
            P.dma('sp', mnorm[:], self.m_norm[l].partition_broadcast(128), writes=['mnorm'])
            P.op('pool', lambda e: e.memset(gain20[:], 1.0), writes=['gain20'])
            for i in range(20):
                src = None
                if i < 4:
                    src = self.moba_qk_norm[l, 0]
                elif i < 8:
                    src = self.moba_qk_norm[l, 1]
                elif 12 <= i < 16:
                    src = self.nsa_q_norm[l]
                elif i == 16:
                    src = self.nsa_k_norm[l, 1]
                elif i == 18:
                    src = self.nsa_k_norm[l, 2]
                if src is not None:
                    P.dma('sp', gain20[:, i, :], src.partition_broadcast(128), writes=['gain20'])
            with nc.allow_non_contiguous_dma(reason="tiny conv weight transpose"):
                for jj in range(4):
                    P.dma('sp', convw[:, jj, :], self.conv_qk[l, jj].rearrange("(ft p) -> p ft", p=128), writes=['convw'])
                P.dma('sp', bi[:], self.b_if[l, 0:4].rearrange("(p o) -> p o", o=1), writes=['bi'])
                P.dma('sp', nbf[:], self.b_if[l, 4:8].rearrange("(p o) -> p o", o=1), writes=['nbf'])
            P.op('dve', lambda e: e.tensor_scalar(out=nbf[:], in0=nbf[:], scalar1=-1.0, scalar2=None, op0=ALU.mult),
                 reads=['nbf'], writes=['nbf'])
            self.dma_mid(cosT[:], self.c_cos.rearrange("(j p) r -> p j r", p=128), S // 128, 8, writes=['cos'])
            self.dma_mid(sinT[:], self.c_sin.rearrange("(j p) r -> p j r", p=128), S // 128, 8, writes=['sin'])

            mh20 = sb(st, "mh20", [128, 20], F32)
            P.op('pool', lambda e: e.memset(mh20[:], -0.5), writes=['mh20'])
            xt = [sb(st, "xt", [128, 4, 1024], F32) for _ in range(2)]
            junk2_2 = [sb(st, "junk2", [128, 1280], BF16) for _ in range(2)]
            ss_2 = [sb(st, "ss", [128, 4], F32) for _ in range(2)]
            rstd_2 = [sb(st, "rstd", [128, 4], F32) for _ in range(2)]
            h_2 = [sb(st, "h", [128, 4, 1024], BF16)] * 2
            hT_2 = [sb(st, "hT", [128, 8, 512], BF16) for _ in range(2)]
            cbuf = [sb(st, "cbuf", [128, 515], F32) for _ in range(8)]
            for ft in range(8):
                P.op('pool', lambda e: e.memset(cbuf[ft][:, 0:3], 0.0), writes=[('cbuf', ft)])
            acc_2 = [sb(st, "acc", [128, 512], F32) for _ in range(2)]
            fo = [sb(st, "fo", [128, 512], BF16) for _ in range(2)]
            gsb = sb(st, "gsb", [4, 512], F32)
            gsb2 = sb(st, "gsb2", [4, 512], F32)
            mvb_2 = [sb(st, "mvb", [128, 512], BF16) for _ in range(2)]
            sg_2 = [sb(st, "sg", [128, 512], F32) for _ in range(2)]
            ogb_2 = [sb(st, "ogb", [128, 512], BF16) for _ in range(2)]
            tm_2 = [sb(st, "tm", [128, 1292], F32) for _ in range(2)]
            ssh_2 = [sb(st, "ssh", [128, 20], F32) for _ in range(2)]
            rs20_2 = [sb(st, "rs20", [128, 20], F32) for _ in range(2)]
            nrm_2 = [sb(st, "nrm", [128, 20, 64], F32) for _ in range(2)]
            nqu_2 = [sb(st, "nqu", [128, 256], BF16) for _ in range(2)]
            rt_2 = [[sb(st, "rt", [128, 20, 8], F32) for _ in range(4)] for _ in range(2)]
            nb_2 = [sb(st, "nb", [128, 20, 64], BF16) for _ in range(2)]
            tmb_2 = [sb(st, "tmb", [128, 1280], BF16) for _ in range(2)]
            tTa_2 = [sb(st, "tTa", [128, 8, 128], BF16) for _ in range(2)]
            tTb_2 = [sb(st, "tTb", [128, 2, 128], BF16) for _ in range(2)]
            ngs_2 = [sb(st, "ngs", [128, 12], F32) for _ in range(2)]
            pT_2 = [ps(st, "pT", [128, 8, 128], BF16) for _ in range(2)]
            pT = pT_2[0]
            pTb = ps(st, "pTb", [128, 2, 128], BF16)
            pf = [ps(st, "pf", [128, 512], F32) for _ in range(2)]
            pg = [ps(st, "pg", [4, 512], F32) for _ in range(1)]
            pt = [ps(st, "pt", [128, 512], F32) for _ in range(2)]
            pfi = 0
            pti = 0

            def load_x(b):
                P.dma('sp', xt[b % 2][:], xsrc[b * 512:(b + 1) * 512, :].rearrange("(j p) d -> p j d", p=128),
                      reads=[('xres', 2 * b), ('xres', 2 * b + 1)], writes=[('xt', b % 2)])
            load_x(0)
            pend_chain = []
            for b in range(NB):
                t0 = b * 512
                if b + 1 < NB:
                    load_x(b + 1)
                def pro_norm(bb):
                    z = bb % 2
                    x_ = xt[z]
                    kx = ('xt', z)
                    ss, rstd, h = ss_2[z], rstd_2[z], h_2[z]
                    for j in range(4):
                        P.op('act', lambda e: e.activation(out=junk2_2[0][:, 0:1024], in_=x_[:, j, :], func=AF.Square,
                                                           scale=1.0 / 32, accum_out=ss[:, j:j + 1]),
                             reads=[kx], writes=[('junk2', 0), ('ss', z)])
                    P.op('dve', lambda e: e.tensor_scalar(out=ss[:], in0=ss[:], scalar1=EPS, scalar2=None, op0=ALU.add),
                         reads=[('ss', z)], writes=[('ss', z)])
                    P.op('pool', lambda e: e.tensor_tensor(out=rstd[:], in0=ss[:], in1=mh20[:, 0:4], op=ALU.pow),
                         reads=[('ss', z), 'mh20'], writes=[('rstd', z)])
                    for j in range(4):
                        P.op('dve', lambda e: e.scalar_tensor_tensor(out=h[:, j, :], in0=x_[:, j, :], scalar=rstd[:, j:j + 1],
                                                                     in1=gmix[:], op0=ALU.mult, op1=ALU.mult),
                             reads=[kx, ('rstd', z), 'gmix'], writes=[('h', j)])

                def pro_tr(bb):
                    z = bb % 2
                    h, hT_ = h_2[z], hT_2[z]
                    for j in range(4):
                        pT = pT_2[j % 2]
                        for kt in range(8):
                            P.op('pe', lambda e: e.transpose(out=pT[:, kt, :], in_=h[:, j, kt * 128:(kt + 1) * 128],
                                                             identity=identb[:]),
                                 reads=[('h', j), 'identb'], writes=[('pT', j % 2)])
                        P.op(['act', 'dve'][j % 2], lambda e: (e.copy if j % 2 == 0 else e.tensor_copy)(out=hT_[:, :, j * 128:(j + 1) * 128], in_=pT[:]),
                             reads=[('pT', j % 2)], writes=[('hT', z, j)])
                if b == 0:
                    pro_norm(0)
                    pro_tr(0)
                hT = hT_2[b % 2]
                hk = [('hT', b % 2, j) for j in range(4)]
                wk = [('w_in', kt) for kt in range(8)]
                def ft_s1(ft):
                    nonlocal pfi
                    c0 = ft * 128 if ft < 8 else 3080
                    p_ = pf[pfi % 2]
                    pk = ('pf', pfi % 2)
                    pfi += 1
                    for kt in range(8):
                        P.op('pe', lambda e: e.matmul(p_[:], lhsT=w_bf[:, kt, c0:c0 + 128], rhs=hT[:, kt, :],
                                                      start=(kt == 0), stop=(kt == 7)),
                             reads=hk + [wk[kt]], writes=[pk])
                    f_ = fo[ft % 2]
                    fk = ('fo', ft % 2)
                    if ft < 8:
                        cb = cbuf[ft]
                        ck = ('cbuf', ft)
                        P.op('act', lambda e: e.copy(out=cb[:, 3:515], in_=p_[:]), reads=[pk], writes=[ck])
                    else:
                        P.op('act', lambda e: e.copy(out=f_[:], in_=p_[:]), reads=[pk], writes=[fk])
                        P.dma('auto', self.kcvcT[:, t0:t0 + 512], f_[:], reads=[fk], writes=[('kcvcT', b)])

                def ft_s2(ft):
                    if ft >= 8:
                        return
                    f_ = fo[ft % 2]
                    fk = ('fo', ft % 2)
                    cb = cbuf[ft]
                    ck = ('cbuf', ft)
                    acc = acc_2[ft % 2]
                    ak = ('acc', ft % 2)
                    P.op('dve', lambda e: e.tensor_scalar(out=acc[:], in0=cb[:, 0:512], scalar1=convw[:, 0, ft:ft + 1],
                                                          scalar2=None, op0=ALU.mult),
                         reads=[ck, 'convw'], writes=[ak])
                    for jj in range(1, 4):
                        P.op('dve', lambda e: e.scalar_tensor_tensor(out=acc[:], in0=cb[:, jj:jj + 512],
                                                                     scalar=convw[:, jj, ft:ft + 1], in1=acc[:],
                                                                     op0=ALU.mult, op1=ALU.add),
                             reads=[ck, 'convw', ak], writes=[ak])
                    P.op('act', lambda e: e.activation(out=f_[:], in_=acc[:], func=AF.Silu), reads=[ak], writes=[fk])
                    P.dma('auto', self.qkT[ft * 128:(ft + 1) * 128, t0:t0 + 512], f_[:], reads=[fk],
                          writes=[('qkT', ft, b)])
                    P.op('pool', lambda e: e.tensor_copy(out=cb[:, 0:3], in_=cb[:, 512:515]), reads=[ck], writes=[ck])
                ft_s1(0)
                for ft in range(9):
                    if ft + 1 < 9:
                        ft_s1(ft + 1)
                    ft_s2(ft)
                if b + 1 < NB:
                    pro_norm(b + 1)
                def gate_mm(gi_):
                    c0 = 2048 + 4 * gi_
                    for kt in range(8):
                        P.op('pe', lambda e: e.matmul(pg[0][:], lhsT=w_bf[:, kt, c0:c0 + 4], rhs=hT[:, kt, :],
                                                      start=(kt == 0), stop=(kt == 7)),
                             reads=hk + [wk[kt]], writes=[('pg', 0)])
                gate_mm(0)
                P.op('act', lambda e: e.activation(out=gsb[:], in_=pg[0][:], func=AF.Identity, bias=bi[:, 0:1]),
                     reads=[('pg', 0), 'bi'], writes=['gsb'])
                P.dma('auto', self.gi_d[:, t0:t0 + 512], gsb[:], reads=['gsb'], writes=[('gi_d', b)])
                gate_mm(1)
                P.op('act', lambda e: e.activation(out=gsb2[:], in_=pg[0][:], func=AF.Exp, bias=nbf[:, 0:1], scale=-1.0),
                     reads=[('pg', 0), 'nbf'], writes=['gsb2'])
                P.op('act', lambda e: e.activation(out=gsb2[:], in_=gsb2[:], func=AF.Ln, bias=1.0),
                     reads=['gsb2'], writes=['gsb2'])
                P.op('dve', lambda e: e.tensor_scalar(out=gsb2[:], in0=gsb2[:], scalar1=-1.0, scalar2=None, op0=ALU.mult),
                     reads=['gsb2'], writes=['gsb2'])
                P.dma('auto', self.gf_d[:, t0:t0 + 512], gsb2[:], reads=['gsb2'], writes=[('gf_d', b)])
                for j in range(4):
                    tok = slice(t0 + j * 128, t0 + (j + 1) * 128)
                    jg = b * 4 + j
                    z_ = jg % 2
                    mvb, sg, ogb, tm, ssh, rs20, nrm, nqu, nb, tmb, tTa, tTb, ngs = (
                        mvb_2[z_], sg_2[z_], ogb_2[z_], tm_2[z_], ssh_2[z_], rs20_2[z_], nrm_2[z_], nqu_2[z_], nb_2[z_],
                        tmb_2[z_], tTa_2[z_], tTb_2[z_], ngs_2[z_])
                    rt = rt_2[z_]
                    pT = pT_2[z_]
                    junk2 = junk2_2[z_]

                    def tm_mm(c0, n):
                        nonlocal pti
                        p_ = pt[pti % 2]
                        pk = ('pt', pti % 2)
                        pti += 1
                        for kt in range(8):
                            P.op('pe', lambda e: e.matmul(p_[:, 0:n], lhsT=hT[:, kt, j * 128:(j + 1) * 128],
                                                          rhs=w_bf[:, kt, c0:c0 + n], start=(kt == 0), stop=(kt == 7)),
                                 reads=[('hT', b % 2, j), wk[kt]], writes=[pk])
                        if pend_chain:
                            for _ in range(5):
                                try:
                                    next(pend_chain[0])
                                except StopIteration:
                                    pend_chain.pop(0)
                                    break
                        return p_, pk
                    p_, pk = tm_mm(1024, 512)
                    P.op('act', lambda e: e.copy(out=mvb[:], in_=p_[:]), reads=[pk], writes=[('mvb', z_)])
                    P.dma('auto', self.mv_d[tok, :], mvb[:], reads=[('mvb', z_)], writes=[('mv_d', jg)])
                    p_, pk = tm_mm(1536, 512)
                    P.op('act', lambda e: e.activation(out=sg[:], in_=p_[:], func=AF.Sigmoid), reads=[pk], writes=[('sg', z_)])
                    P.op('pool', lambda e: e.tensor_tensor(out=ogb[:], in0=sg[:], in1=mnorm[:], op=ALU.mult),
                         reads=[('sg', z_), 'mnorm'], writes=[('ogb', z_)])
                    P.dma('auto', self.og_d[tok, :], ogb[:], reads=[('ogb', z_)], writes=[('og_d', jg)])
                    p_, pk = tm_mm(2056, 512)
                    P.op('dve', lambda e: e.tensor_copy(out=tm[:, 0:512], in_=p_[:]), reads=[pk], writes=[('tm', z_)])
                    p_, pk = tm_mm(2568, 512)
                    P.op('act', lambda e: e.copy(out=tm[:, 512:1024], in_=p_[:]), reads=[pk], writes=[('tm', z_)])
                    p_, pk = tm_mm(3208, 268)
                    P.op('dve', lambda e: e.tensor_copy(out=tm[:, 1024:1292], in_=p_[:, 0:268]), reads=[pk], writes=[('tm', z_)])
                    def chain(tok=tok, jg=jg, z_=z_, tm=tm, ssh=ssh, rs20=rs20, nrm=nrm, nqu=nqu, nb=nb, tmb=tmb, tTa=tTa,
                              tTb=tTb, ngs=ngs, rt=rt, junk2=junk2, pT=pT):
                        yield
                        P.op('act', lambda e: e.activation(out=junk2[:, 0:1280], in_=tm[:, 0:1280], func=AF.Square),
                             reads=[('tm', z_)], writes=[('junk2', z_)])
                        yield
                        P.op('dve', lambda e: e.tensor_reduce(out=ssh[:], in_=junk2[:, 0:1280].rearrange("p (h d) -> p h d", d=64),
                                                              axis=AX.X, op=ALU.add),
                             reads=[('junk2', z_)], writes=[('ssh', z_)])
                        yield
                        P.op('dve', lambda e: e.tensor_scalar(out=ssh[:], in0=ssh[:], scalar1=1.0 / 64, scalar2=EPS,
                                                              op0=ALU.mult, op1=ALU.add), reads=[('ssh', z_)], writes=[('ssh', z_)])
                        yield
                        P.op('pool', lambda e: e.tensor_tensor(out=rs20[:], in0=ssh[:], in1=mh20[:], op=ALU.pow),
                             reads=[('ssh', z_), 'mh20'], writes=[('rs20', z_)])
                        yield
                        P.op('dve', lambda e: e.tensor_tensor(out=nrm[:], in0=tm[:, 0:1280].rearrange("p (h d) -> p h d", d=64),
                                                              in1=rs20[:].unsqueeze(2).broadcast_to([128, 20, 64]), op=ALU.mult),
                             reads=[('tm', z_), ('rs20', z_)], writes=[('nrm', z_)])
                        yield
                        P.op('dve', lambda e: e.tensor_tensor(out=nrm[:], in0=nrm[:], in1=gain20[:], op=ALU.mult),
                             reads=[('nrm', z_), 'gain20'], writes=[('nrm', z_)])
                        yield
                        P.op('act', lambda e: e.copy(out=nqu[:].rearrange("p (h d) -> p h d", d=64), in_=nrm[:, 12:16, :]),
                             reads=[('nrm', z_)], writes=[('nqu', z_)])
                        cb_ = cosT[:, jg, :].unsqueeze(1).broadcast_to([128, 20, 8])
                        sb_ = sinT[:, jg, :].unsqueeze(1).broadcast_to([128, 20, 8])
                        x1 = nrm[:, :, 0:8]
                        x2 = nrm[:, :, 8:16]
                        yield
                        P.op('dve', lambda e: e.tensor_tensor(out=rt[0][:], in0=x1, in1=cb_, op=ALU.mult),
                             reads=[('nrm', z_), 'cos'], writes=[('rt', 0, z_)])
                        yield
                        P.op('pool', lambda e: e.tensor_tensor(out=rt[1][:], in0=x2, in1=sb_, op=ALU.mult),
                             reads=[('nrm', z_), 'sin'], writes=[('rt', 1, z_)])
                        yield
                        P.op('dve', lambda e: e.tensor_tensor(out=rt[2][:], in0=x2, in1=cb_, op=ALU.mult),
                             reads=[('nrm', z_), 'cos'], writes=[('rt', 2, z_)])
                        yield
                        P.op('pool', lambda e: e.tensor_tensor(out=rt[3][:], in0=x1, in1=sb_, op=ALU.mult),
                             reads=[('nrm', z_), 'sin'], writes=[('rt', 3, z_)])
                        yield
                        P.op('dve', lambda e: e.tensor_tensor(out=x1, in0=rt[0][:], in1=rt[1][:], op=ALU.subtract),
                             reads=[('rt', 0, z_), ('rt', 1, z_)], writes=[('nrm', z_)])
                        yield
                        P.op('dve', lambda e: e.tensor_tensor(out=x2, in0=rt[2][:], in1=rt[3][:], op=ALU.add),
                             reads=[('rt', 2, z_), ('rt', 3, z_)], writes=[('nrm', z_)])
                        yield
                        P.op('act', lambda e: e.copy(out=nb[:], in_=nrm[:]), reads=[('nrm', z_)], writes=[('nb', z_)])
                        yield
                        P.op('act', lambda e: e.copy(out=tmb[:], in_=tm[:, 0:1280]), reads=[('tm', z_)], writes=[('tmb', z_)])
                        yield
                        P.op('act', lambda e: e.activation(out=ngs[:], in_=tm[:, 1280:1292], func=AF.Sigmoid),
                             reads=[('tm', z_)], writes=[('ngs', z_)])
                        srcs = [nb[:, 0:2, :], nb[:, 2:4, :], nb[:, 4:6, :], nb[:, 6:8, :], nb[:, 12:14, :], nb[:, 14:16, :]]
                        yield
                        for i, s_ in enumerate(srcs):
                            P.op('pe', lambda e: e.transpose(out=pT[:, i, :], in_=s_.rearrange("p h d -> p (h d)"),
                                                             identity=identb[:]), reads=[('nb', z_), 'identb'], writes=[('pT', z_)])
                        yield
                        for i in range(2):
                            P.op('pe', lambda e: e.transpose(out=pT[:, 6 + i, :], in_=nqu[:, i * 128:(i + 1) * 128],
                                                             identity=identb[:]), reads=[('nqu', z_), 'identb'], writes=[('pT', z_)])
                        yield
                        for i, hh in enumerate((16, 18)):
                            P.op('pe', lambda e: e.transpose(out=pTb[:, i, :], in_=nb[:, hh:hh + 2, :].rearrange("p h d -> p (h d)"),
                                                             identity=identb[:]), reads=[('nb', z_), 'identb'], writes=['pTb'])
                        yield
                        P.op('dve', lambda e: e.tensor_copy(out=tTa[:], in_=pT[:]), reads=[('pT', z_)], writes=[('tTa', z_)])
                        yield
                        P.op('act', lambda e: e.copy(out=tTb[:], in_=pTb[:]), reads=['pTb'], writes=[('tTb', z_)])
                        yield
                        for i, dst in enumerate((self.bqT, self.bkT, self.nqrT, self.nquT)):
                            P.dma('auto', dst[:, tok].rearrange("(a p) t -> p a t", p=128), tTa[:, 2 * i:2 * i + 2, :],
                                  reads=[('tTa', z_)], writes=[(("bqT","bkT","nqrT","nquT")[i], jg)])
                        yield
                        P.dma('auto', self.ksT[:, tok], tTb[0:64, 0, :], reads=[('tTb', z_)], writes=[('ksT', jg)])
                        yield
                        P.dma('auto', self.kwT[:, tok], tTb[0:64, 1, :], reads=[('tTb', z_)], writes=[('kwT', jg)])
                        yield
                        P.dma('auto', self.bv_d[tok, :], tmb[:, 512:768], reads=[('tmb', z_)], writes=[('bv_d', jg)])
                        yield
                        P.dma('auto', self.vs_d[tok, :], tmb[:, 1088:1152], reads=[('tmb', z_)], writes=[('vs_d', jg)])
                        yield
                        P.dma('auto', self.vw_d[tok, :], tmb[:, 1216:1280], reads=[('tmb', z_)], writes=[('vw_d', jg)])
                        yield
                        P.dma('auto', self.ng_d[tok, :], ngs[:], reads=[('ngs', z_)], writes=[('ng_d', jg)])
                    for _ in (pend_chain.pop(0) if pend_chain else ()):
                        pass
                    pend_chain.append(chain())
                    if j == 1 and b + 1 < NB:
                        pro_tr(b + 1)
            while pend_chain:
                for _ in pend_chain.pop(0):
                    pass
            self.barrier()

    def phase3(self, l, xsrc, xdst):
        S, P, nc = self.S, self.P, self.nc
        NB = S // 256
        with ExitStack() as st:
            sb, ps = self.sb, self.ps
            wo = sb(st, "wo", [128, 8, 1024], BF16)
            w1 = sb(st, "w1", [128, 8, 4096], BF16)
            w2 = sb(st, "w2", [128, 32, 1024], BF16)
            with ExitStack() as st2:
                stage = [sb(st2, "wst3", [128, 4096], F32) for _ in range(2)]
                slabs = []
                for i in range(2):
                    slabs.append((self.w_out[l, i * 512:(i + 1) * 512, :].rearrange("(a p) d -> p a d", p=128),
                                  wo[:, i * 4:(i + 1) * 4, :], ('wo', i), True))
                for kt in range(8):
                    slabs.append((self.w_ff1[l, kt * 128:(kt + 1) * 128, :], w1[:, kt, :], ('w1', kt), False))
                for i in range(8):
                    slabs.append((self.w_ff2[l, i * 512:(i + 1) * 512, :].rearrange("(a p) d -> p a d", p=128),
                                  w2[:, i * 4:(i + 1) * 4, :], ('w2', i), True))
                for n, (src, dst, key, three) in enumerate(slabs):
                    s_ = stage[n % 2]
                    sv = s_[:].rearrange("p (a d) -> p a d", a=4) if three else s_[:]
                    P.dma('sp', sv, src, writes=[('wst3', n % 2)])
                    eng = ['pool', 'dve', 'act'][n % 3]
                    if eng == 'act':
                        P.op(eng, lambda e: e.copy(out=dst, in_=sv), reads=[('wst3', n % 2)], writes=[key])
                    else:
                        P.op(eng, lambda e: e.tensor_copy(out=dst, in_=sv), reads=[('wst3', n % 2)], writes=[key])
                self.barrier()
            identb = sb(st, "identb3", [128, 128], BF16)
            gffn = sb(st, "gffn", [128, 1024], F32)
            P.dma('sp', identb[:], self.c_identb, writes=['identb'])
            P.dma('sp', gffn[:], self.norm_ffn[l].partition_broadcast(128), writes=['gffn'])
            xt_2 = [sb(st, "xt3", [128, 2, 1024], F32) for _ in range(2)]
            mixb_2 = [sb(st, "mixb", [128, 2, 1024], BF16) for _ in range(2)]
            mh3 = sb(st, "mh3", [128, 2], F32)
            P.op('pool', lambda e: e.memset(mh3[:], -0.5), writes=['mh3'])
            mT = sb(st, "mT", [128, 8, 256], BF16)
            hT = sb(st, "hT3", [128, 8, 256], BF16)
            uT = sb(st, "uT", [128, 32, 256], BF16)
            rr = [sb(st, "rr", [128, 256], F32) for _ in range(2)]
            junk = sb(st, "junk3", [128, 1024], BF16)
            ss = sb(st, "ss3", [128, 2], F32)
            rstd = sb(st, "rstd3", [128, 2], F32)
            pT = ps(st, "pT3", [128, 8, 128], BF16)
            py = [ps(st, "py", [128, 512], F32) for _ in range(2)]
            pz = [ps(st, "pz", [128, 256], F32) for _ in range(2)]
            pyi = 0
            pzi = 0
            wok = [('wo', 0), ('wo', 1)]
            def load3(b):
                t0_ = b * 256
                P.dma('sp', xt_2[b % 2][:], xsrc[t0_:t0_ + 256, :].rearrange("(j p) d -> p j d", p=128), reads=[('xres', b)],
                      writes=[('xt', b % 2)])
                P.dma('sp', mixb_2[b % 2][:], self.mix_d[t0_:t0_ + 256, :].rearrange("(j p) d -> p j d", p=128),
                      writes=[('mixb', b % 2)])
            load3(0)
            def frontA(b):
                nonlocal pyi, pzi
                t0 = b * 256
                xt = xt_2[b % 2]
                mixb = mixb_2[b % 2]
                xk = ('xt', b % 2)
                mk = ('mixb', b % 2)
                for j in range(2):
                    for kt in range(8):
                        P.op('pe', lambda e: e.transpose(out=pT[:, kt, :], in_=mixb[:, j, kt * 128:(kt + 1) * 128],
                                                         identity=identb[:]), reads=[mk, 'identb'], writes=['pT'])
                    P.op('act', lambda e: e.copy(out=mT[:, :, j * 128:(j + 1) * 128], in_=pT[:]),
                         reads=['pT'], writes=[('mT', j)])
                for j in range(2):
                    for hf in range(2):
                        p_ = py[pyi % 2]
                        pk = ('py', pyi % 2)
                        pyi += 1
                        for kt in range(8):
                            P.op('pe', lambda e: e.matmul(p_[:], lhsT=mT[:, kt, j * 128:(j + 1) * 128],
                                                          rhs=wo[:, kt, hf * 512:(hf + 1) * 512],
                                                          start=(kt == 0), stop=(kt == 7)),
                                 reads=[('mT', j), wok[kt // 4]], writes=[pk])
                        P.op('dve', lambda e: e.tensor_tensor(out=xt[:, j, hf * 512:(hf + 1) * 512],
                                                              in0=xt[:, j, hf * 512:(hf + 1) * 512], in1=p_[:], op=ALU.add),
                             reads=[xk, pk], writes=[xk])
                for j in range(2):
                    P.op('act', lambda e: e.activation(out=junk[:], in_=xt[:, j, :], func=AF.Square, scale=1.0 / 32,
                                                       accum_out=ss[:, j:j + 1]), reads=[xk], writes=['junk', 'ss'])
                P.op('dve', lambda e: e.tensor_scalar(out=ss[:], in0=ss[:], scalar1=EPS, scalar2=None, op0=ALU.add),
                     reads=['ss'], writes=['ss'])
                P.op('pool', lambda e: e.tensor_tensor(out=rstd[:], in0=ss[:], in1=mh3[:], op=ALU.pow), reads=['ss', 'mh3'], writes=['rstd'])
                for j in range(2):
                    P.op('dve', lambda e: e.scalar_tensor_tensor(out=mixb[:, j, :], in0=xt[:, j, :], scalar=rstd[:, j:j + 1],
                                                                 in1=gffn[:], op0=ALU.mult, op1=ALU.mult),
                         reads=[xk, 'rstd', 'gffn'], writes=[mk])

            def frontB(b):
                nonlocal pyi, pzi
                t0 = b * 256
                xt = xt_2[b % 2]
                mixb = mixb_2[b % 2]
                xk = ('xt', b % 2)
                mk = ('mixb', b % 2)
                for j in range(2):
                    for kt in range(8):
                        P.op('pe', lambda e: e.transpose(out=pT[:, kt, :], in_=mixb[:, j, kt * 128:(kt + 1) * 128],
                                                         identity=identb[:]), reads=[mk, 'identb'], writes=['pT'])
                    P.op('act', lambda e: e.copy(out=hT[:, :, j * 128:(j + 1) * 128], in_=pT[:]),
                         reads=['pT'], writes=[('hT', j)])

            def ffn1(b):
                nonlocal pyi, pzi
                t0 = b * 256
                xt = xt_2[b % 2]
                mixb = mixb_2[b % 2]
                xk = ('xt', b % 2)
                mk = ('mixb', b % 2)
                hk = [('hT', 0), ('hT', 1)]
                for ft in range(32):
                    p_ = pz[pzi % 2]
                    pk = ('pz', pzi % 2)
                    r_ = rr[pzi % 2]
                    rk = ('rr', pzi % 2)
                    pzi += 1
                    for kt in range(8):
                        P.op('pe', lambda e: e.matmul(p_[:], lhsT=w1[:, kt, ft * 128:(ft + 1) * 128], rhs=hT[:, kt, :],
                                                      start=(kt == 0), stop=(kt == 7)),
                             reads=hk + [('w1', kt)], writes=[pk])
                    P.op('act', lambda e: e.activation(out=r_[:], in_=p_[:], func=AF.Relu), reads=[pk], writes=[rk])
                    P.op(['dve', 'pool'][ft % 2], lambda e: e.tensor_tensor(out=uT[:, ft, :], in0=r_[:], in1=r_[:], op=ALU.mult),
                         reads=[rk], writes=[('uT', ft)])

            def ffn2(b):
                nonlocal pyi, pzi
                t0 = b * 256
                xt = xt_2[b % 2]
                mixb = mixb_2[b % 2]
                xk = ('xt', b % 2)
                mk = ('mixb', b % 2)
                uk = [('uT', ft) for ft in range(32)]
                for j in range(2):
                    for hf in range(2):
                        p_ = py[pyi % 2]
                        pk = ('py', pyi % 2)
                        pyi += 1
                        for ft in range(32):
                            P.op('pe', lambda e: e.matmul(p_[:], lhsT=uT[:, ft, j * 128:(j + 1) * 128],
                                                          rhs=w2[:, ft, hf * 512:(hf + 1) * 512],
                                                          start=(ft == 0), stop=(ft == 31)),
                                 reads=[uk[ft], ('w2', ft // 4)], writes=[pk])
                        P.op('dve', lambda e: e.tensor_tensor(out=xt[:, j, hf * 512:(hf + 1) * 512],
                                                              in0=xt[:, j, hf * 512:(hf + 1) * 512], in1=p_[:], op=ALU.add),
                             reads=[xk, pk], writes=[xk])
                P.dma('sp', xdst[t0:t0 + 256, :].rearrange("(j p) d -> p j d", p=128), xt[:], reads=[xk],
                      writes=[('xres', b)])
            frontA(0)
            frontB(0)
            for b in range(NB):
                if b + 1 < NB:
                    load3(b + 1)
                ffn1(b)
                if b + 1 < NB:
                    frontA(b + 1)
                ffn2(b)
                if b + 1 < NB:
                    frontB(b + 1)
            self.barrier()

    def _ml_norm(self, P, s_, sk, pO_t, pok, junk, flT, mhalf, ym_t, ymk, og_t, ogk, hh, c, cl):
        P.op('act', lambda e: e.activation(out=junk[:], in_=pO_t[0:64, 0:128], func=AF.Square,
                                           scale=128.0 ** -0.5, accum_out=s_[:, 0:1]),
             reads=[pok], writes=['junk', sk])
        P.op('dve', lambda e: e.tensor_scalar(out=s_[:, 6:7], in0=pO_t[0:64, 128:129], scalar1=-1.0,
                                              scalar2=flT[:, hh, c:c + 1], op0=ALU.mult, op1=ALU.max),
             reads=[pok, 'flT'], writes=[sk])
        P.op('dve', lambda e: e.tensor_tensor(out=s_[:, 1:2], in0=s_[:, 6:7], in1=pO_t[0:64, 128:129], op=ALU.max),
             reads=[pok, sk], writes=[sk])
        P.op('dve', lambda e: e.tensor_tensor(out=s_[:, 2:3], in0=s_[:, 1:2], in1=s_[:, 1:2], op=ALU.mult),
             reads=[sk], writes=[sk])
        P.op('dve', lambda e: e.scalar_tensor_tensor(out=s_[:, 3:4], in0=s_[:, 2:3], scalar=EPS, in1=s_[:, 0:1],
                                                     op0=ALU.mult, op1=ALU.add), reads=[sk], writes=[sk])
        P.op('pool', lambda e: e.tensor_tensor(out=s_[:, 5:6], in0=s_[:, 3:4], in1=mhalf[:, 0:1], op=ALU.pow),
             reads=[sk, 'mhalf'], writes=[sk])
        P.op('dve', lambda e: e.scalar_tensor_tensor(out=ym_t[:, cl, hh * 128:(hh + 1) * 128],
                                                     in0=pO_t[0:64, 0:128], scalar=s_[:, 5:6],
                                                     in1=og_t[:, cl, hh * 128:(hh + 1) * 128],
                                                     op0=ALU.mult, op1=ALU.mult),
             reads=[pok, sk, ogk], writes=[ymk])

    def phase2_mlstm(self, l):
        S, P, nc = self.S, self.P, self.nc
        NCH = S // 64
        LNSC = math.log(128.0 ** -0.5)
        with ExitStack() as st:
            sb, ps = self.sb, self.ps
            identb = sb(st, "identbm", [128, 128], BF16)
            identf = sb(st, "identfm", [128, 128], F32)
            ut = sb(st, "ut", [128, 128], F32)
            tri = sb(st, "tri", [128, 128], F32)
            P.dma('sp', identb[:], self.c_identb, writes=['identb'])
            P.dma('sp', identf[:], self.c_identf, writes=['identf'])
            P.dma('sp', ut[:], self.c_ut, writes=['ut'])
            P.dma('sp', tri[:], self.c_tri, writes=['tri'])
            uT = sb(st, "uT_m", [64, 4, NCH], F32)
            u2T = sb(st, "u2T_m", [64, 4, NCH], F32)
            flT = sb(st, "flT_m", [64, 4, NCH], F32)
            decB = sb(st, "decB", [128, 4, NCH], F32)
            with ExitStack() as s2:
                li = sb(s2, "li", [NCH, 4, 64], F32)
                lf = sb(s2, "lf", [NCH, 4, 64], F32)
                ones = sb(s2, "ones", [NCH, 64], F32)
                Fin = sb(s2, "Fin", [NCH, 4, 64], F32)
                Ft = sb(s2, "Ft", [NCH, 4, 64], F32)
                a_ = sb(s2, "a_", [NCH, 4, 64], F32)
                Ain = sb(s2, "Ain", [NCH, 4, 64], F32)
                tot = sb(s2, "tot", [NCH, 4], F32)
                cmax = sb(s2, "cmax", [NCH, 4], F32)
                cmT = sb(s2, "cmT", [4, NCH], F32)
                ET = sb(s2, "ET", [4, NCH], F32)
                ETn = sb(s2, "ETn", [4, NCH], F32)
                Ec = sb(s2, "Ec", [NCH, 4], F32)
                Enc = sb(s2, "Enc", [NCH, 4], F32)
                tmp = sb(s2, "tmpm", [NCH, 4, 64], F32)
                uu = sb(s2, "uu", [NCH, 4, 64], F32)
                uu2 = sb(s2, "uu2", [NCH, 4, 64], F32)
                fl = sb(s2, "fl", [NCH, 4, 64], F32)
                dec = sb(s2, "dec", [NCH, 4], F32)
                decrep = sb(s2, "decrep", [NCH, 4, 128], F32)
                pa = ps(s2, "pa", [128, 512], F32)
                pb = ps(s2, "pb", [128, 512], F32)
                P.dma('sp', li[:], self.gi_d.rearrange("h (c j) -> c h j", j=64),
                      reads=[('gi_d', b) for b in range(S // 512)], writes=['li'])
                P.dma('sp', lf[:], self.gf_d.rearrange("h (c j) -> c h j", j=64),
                      reads=[('gf_d', b) for b in range(S // 512)], writes=['lf'])
                P.op('pool', lambda e: e.memset(ones[:], 1.0), writes=['ones'])
                for hh in range(4):
                    P.op('dve', lambda e: e.tensor_tensor_scan(out=Fin[:, hh, :], data0=ones[:], data1=lf[:, hh, :],
                                                               initial=0.0, op0=ALU.mult, op1=ALU.add),
                         reads=['ones', 'lf'], writes=['Fin'])
                P.op('dve', lambda e: e.tensor_copy(out=tot[:], in_=Fin[:, :, 63]), reads=['Fin'], writes=['tot'])
                P.op('pe', lambda e: e.matmul(pa[0:NCH, 0:4], lhsT=ut[0:NCH, 0:NCH], rhs=tot[:], start=True, stop=True),
                     reads=['ut', 'tot'], writes=['pa'])
                P.op('dve', lambda e: e.tensor_tensor(out=Ft[:], in0=Fin[:],
                                                      in1=pa[0:NCH, 0:4].unsqueeze(2).broadcast_to([NCH, 4, 64]), op=ALU.add),
                     reads=['Fin', 'pa'], writes=['Ft'])
                P.op('dve', lambda e: e.tensor_tensor(out=a_[:], in0=li[:], in1=Ft[:], op=ALU.subtract),
                     reads=['li', 'Ft'], writes=['a_'])
                for hh in range(4):
                    P.op('dve', lambda e: e.tensor_tensor_scan(out=Ain[:, hh, :], data0=a_[:, hh, :], data1=a_[:, hh, :],
                                                               initial=-1e30, op0=ALU.max, op1=ALU.max),
                         reads=['a_'], writes=['Ain'])
                P.op('dve', lambda e: e.tensor_copy(out=cmax[:], in_=Ain[:, :, 63]), reads=['Ain'], writes=['cmax'])
                P.op('pe', lambda e: e.transpose(out=pb[0:4, 0:NCH], in_=cmax[:], identity=identf[0:NCH, 0:NCH]),
                     reads=['cmax', 'identf'], writes=['pb'])
                P.op('dve', lambda e: e.tensor_copy(out=cmT[:], in_=pb[0:4, 0:NCH]), reads=['pb'], writes=['cmT'])
                P.op('dve', lambda e: e.tensor_tensor_scan(out=ET[:], data0=cmT[:], data1=cmT[:], initial=0.0,
                                                           op0=ALU.max, op1=ALU.max), reads=['cmT'], writes=['ET'])
                if NCH > 1:
                    P.op('dve', lambda e: e.tensor_copy(out=ETn[:, 0:NCH - 1], in_=ET[:, 1:NCH]), reads=['ET'], writes=['ETn'])
                P.op('dve', lambda e: e.tensor_copy(out=ETn[:, NCH - 1:NCH], in_=ET[:, NCH - 1:NCH]), reads=['ET'], writes=['ETn'])
                P.op('pe', lambda e: e.transpose(out=pa[0:NCH, 0:4], in_=ET[:], identity=identf[0:4, 0:4]),
                     reads=['ET', 'identf'], writes=['pa'])
                P.op('dve', lambda e: e.tensor_copy(out=Ec[:], in_=pa[0:NCH, 0:4]), reads=['pa'], writes=['Ec'])
                P.op('pe', lambda e: e.transpose(out=pb[0:NCH, 0:4], in_=ETn[:], identity=identf[0:4, 0:4]),
                     reads=['ETn', 'identf'], writes=['pb'])
                P.op('dve', lambda e: e.tensor_copy(out=Enc[:], in_=pb[0:NCH, 0:4]), reads=['pb'], writes=['Enc'])
                Eb = Ec[:].unsqueeze(2).broadcast_to([NCH, 4, 64])
                Enb = Enc[:].unsqueeze(2).broadcast_to([NCH, 4, 64])
                P.op('dve', lambda e: e.tensor_tensor(out=tmp[:], in0=a_[:], in1=Eb, op=ALU.subtract),
                     reads=['a_', 'Ec'], writes=['tmp'])
                P.op('act', lambda e: e.activation(out=uu[:], in_=tmp[:], func=AF.Exp, bias=LNSC), reads=['tmp'], writes=['uu'])
                P.op('dve', lambda e: e.tensor_tensor(out=tmp[:], in0=a_[:], in1=Enb, op=ALU.subtract),
                     reads=['a_', 'Enc'], writes=['tmp'])
                P.op('act', lambda e: e.activation(out=uu2[:], in_=tmp[:], func=AF.Exp, bias=LNSC), reads=['tmp'], writes=['uu2'])
                P.op('dve', lambda e: e.tensor_tensor(out=tmp[:], in0=Ft[:], in1=Eb, op=ALU.add),
                     reads=['Ft', 'Ec'], writes=['tmp'])
                P.op('act', lambda e: e.activation(out=fl[:], in_=tmp[:], func=AF.Exp, scale=-1.0), reads=['tmp'], writes=['fl'])
                P.op('dve', lambda e: e.tensor_tensor(out=dec[:], in0=Ec[:], in1=Enc[:], op=ALU.subtract),
                     reads=['Ec', 'Enc'], writes=['dec'])
                P.op('act', lambda e: e.activation(out=dec[:], in_=dec[:], func=AF.Exp), reads=['dec'], writes=['dec'])
                P.op('dve', lambda e: e.tensor_copy(out=decrep[:], in_=dec[:].unsqueeze(2).broadcast_to([NCH, 4, 128])),
                     reads=['dec'], writes=['decrep'])
                for src, dst, nm in ((uu, uT, 'uT'), (uu2, u2T, 'u2T'), (fl, flT, 'flT')):
                    for hh in range(4):
                        P.op('pe', lambda e: e.transpose(out=pa[0:64, hh * 128:hh * 128 + NCH], in_=src[:, hh, :],
                                                         identity=identf[0:NCH, 0:NCH]),
                             reads=['uu', 'uu2', 'fl', 'identf'], writes=['pa'])
                    P.op('dve', lambda e: e.tensor_copy(out=dst[:], in_=pa[0:64, :].rearrange("p (h c) -> p h c", h=4)[:, :, 0:NCH]),
                         reads=['pa'], writes=[nm])
                for hh in range(4):
                    P.op('pe', lambda e: e.matmul(pb[:, hh * 128:hh * 128 + NCH], lhsT=decrep[:, hh, :],
                                                  rhs=identf[0:NCH, 0:NCH], start=True, stop=True),
                         reads=['decrep', 'identf'], writes=['pb'])
                P.op('dve', lambda e: e.tensor_copy(out=decB[:], in_=pb[:].rearrange("p (h c) -> p h c", h=4)[:, :, 0:NCH]),
                     reads=['pb'], writes=['decB'])
                self.barrier()
            NG = S // 512
            qg = [sb(st, "qg", [128, 4, 512], BF16) for _ in range(2)]
            kg = [sb(st, "kg", [128, 4, 512], BF16) for _ in range(2)]
            vg = [sb(st, "vg", [64, 8, 4, 129], BF16) for _ in range(2)]
            ogg = [sb(st, "ogg", [64, 8, 512], BF16) for _ in range(2)]
            ym = [sb(st, "ym", [64, 8, 512], BF16) for _ in range(2)]
            G = [sb(st, "G", [128, 129], F32) for _ in range(4)]
            Gb = [sb(st, "Gb", [128, 129], BF16) for _ in range(4)]
            ku2 = [sb(st, "ku2", [64, 128], BF16) for _ in range(2)]
            Sm = [sb(st, "Sm", [64, 64], BF16) for _ in range(2)]
            Smu = [sb(st, "Smu", [64, 64], F32) for _ in range(2)]
            junk = sb(st, "junkm", [64, 128], BF16)
            mhalf = sb(st, "mhalf", [64, 4], F32)
            P.op('pool', lambda e: e.memset(mhalf[:], -0.5), writes=['mhalf'])
            osb = [sb(st, "osb", [64, 4, 129], F32) for _ in range(2)]
            ssq = [sb(st, "ssq", [64, 4], F32) for _ in range(2)]
            nt = [sb(st, "nt", [64, 4, 4], F32) for _ in range(2)]
            ytmp = sb(st, "ytmp", [64, 4, 128], F32)
            sc = [sb(st, "scm", [64, 8], F32) for _ in range(2)]
            pkT = [ps(st, "pkT", [128, 1024], BF16) for _ in range(2)]
            pS = [ps(st, "pS", [128, 512], F32) for _ in range(2)]
            pO = [ps(st, "pO", [128, 512], F32) for _ in range(2)]
            pG = [ps(st, "pG", [128, 512], F32) for _ in range(2)]
            for i in range(2):
                P.op('pool', lambda e: e.memset(vg[i][:], 1.0), writes=[('vg', i)])
            for hh in range(4):
                P.op('pool', lambda e: e.memset(G[hh][:], 0.0), writes=[('G', hh)])
                P.op('pool', lambda e: e.memset(Gb[hh][:], 0.0), writes=[('Gb', hh)])

            def load_group(g):
                i = g % 2
                tk = slice(g * 512, (g + 1) * 512)
                P.dma('sp', qg[i][:], self.qkT[0:512, tk].rearrange("(h p) t -> p h t", p=128),
                      reads=[('qkT', ft, g) for ft in range(4)], writes=[('qg', i)])
                P.dma('sp', kg[i][:], self.qkT[512:1024, tk].rearrange("(h p) t -> p h t", p=128),
                      reads=[('qkT', ft, g) for ft in range(4, 8)], writes=[('kg', i)])
                for hh in range(4):
                    P.dma('sp', vg[i][:, :, hh, 0:128],
                          self.mv_d[tk, hh * 128:(hh + 1) * 128].rearrange("(c s) e -> s c e", s=64),
                          reads=[('mv_d', 4 * g + j) for j in range(4)], writes=[('vg', i)])
                P.dma('sp', ogg[i][:], self.og_d[tk, :].rearrange("(c s) e -> s c e", s=64),
                      reads=[('og_d', 4 * g + j) for j in range(4)], writes=[('ogg', i)])
            load_group(0)
            steps = [(g, cl, hh) for g in range(NG) for cl in range(8) for hh in range(4)]

            def stageA(n):
                g, cl, hh = steps[n]
                gi_ = g % 2
                c = g * 8 + cl
                i2 = n % 2
                k_ = kg[gi_][:, hh, cl * 64:(cl + 1) * 64]
                q_ = qg[gi_][:, hh, cl * 64:(cl + 1) * 64]
                P.op('pe', lambda e: e.transpose(out=pkT[i2][0:64, 0:128], in_=k_, identity=identb[:]),
                     reads=[('kg', gi_), 'identb'], writes=[('pkT', i2)])
                P.op('pe', lambda e: e.matmul(pS[i2][0:64, 0:64], lhsT=k_, rhs=q_, start=True, stop=True),
                     reads=[('kg', gi_), ('qg', gi_)], writes=[('pS', i2)])
                P.op('act', lambda e: e.activation(out=ku2[i2][:], in_=pkT[i2][0:64, 0:128], func=AF.Copy,
                                                   scale=u2T[:, hh, c:c + 1]),
                     reads=[('pkT', i2), 'u2T'], writes=[('ku2', i2)])
                P.op('act', lambda e: e.activation(out=Smu[i2][:], in_=pS[i2][0:64, 0:64], func=AF.Copy,
                                                   scale=uT[:, hh, c:c + 1]),
                     reads=[('pS', i2), 'uT'], writes=[('Smu', i2)])
                P.op('pool', lambda e: e.tensor_tensor(out=Sm[i2][:], in0=Smu[i2][:], in1=tri[0:64, 0:64], op=ALU.mult),
                     reads=[('Smu', i2), 'tri'], writes=[('Sm', i2)])

            def stageB(n):
                g, cl, hh = steps[n]
                gi_ = g % 2
                c = g * 8 + cl
                i2 = n % 2
                q_ = qg[gi_][:, hh, cl * 64:(cl + 1) * 64]
                v_ = vg[gi_][:, cl, hh, :]
                P.op('pe', lambda e: e.matmul(pO[i2][0:64, 0:129], lhsT=Sm[i2][:], rhs=v_, start=True, stop=False),
                     reads=[('Sm', i2), ('vg', gi_)], writes=[('pO', i2)])
                P.op('pe', lambda e: e.matmul(pO[i2][0:64, 0:129], lhsT=q_, rhs=Gb[hh][:], start=False, stop=True),
                     reads=[('qg', gi_), ('Gb', hh)], writes=[('pO', i2)])
                P.op('pe', lambda e: e.matmul(pG[i2][:, 0:129], lhsT=ku2[i2][:], rhs=v_, start=True, stop=True),
                     reads=[('ku2', i2), ('vg', gi_)], writes=[('pG', i2)])
                P.op('dve', lambda e: e.scalar_tensor_tensor(out=G[hh][:], in0=G[hh][:], scalar=decB[:, hh, c:c + 1],
                                                             in1=pG[i2][:, 0:129], op0=ALU.mult, op1=ALU.add),
                     reads=[('G', hh), 'decB', ('pG', i2)], writes=[('G', hh)])
                P.op('pool', lambda e: e.tensor_copy(out=Gb[hh][:], in_=G[hh][:]), reads=[('G', hh)], writes=[('Gb', hh)])
                cb = c % 2
                P.op('act', lambda e: e.activation(out=junk[:], in_=pO[i2][0:64, 0:128], func=AF.Square,
                                                   scale=128.0 ** -0.5, accum_out=ssq[cb][:, hh:hh + 1]),
                     reads=[('pO', i2)], writes=['junk', ('ssq', cb, hh)])
                P.op('act', lambda e: e.copy(out=osb[cb][:, hh, :], in_=pO[i2][0:64, 0:129]),
                     reads=[('pO', i2)], writes=[('osb', cb, hh)])

            def stageN(n):
                g, cl, hh = steps[n]
                if hh != 3:
                    return
                gi_ = g % 2
                c = g * 8 + cl
                cb = c % 2
                t_ = nt[cb]
                tk = ('nt', cb)
                den = osb[cb][:, :, 128]
                ok = [('osb', cb, h_) for h_ in range(4)]
                P.op('dve', lambda e: e.tensor_scalar(out=t_[:, 0, :], in0=den, scalar1=-1.0, scalar2=None, op0=ALU.mult),
                     reads=ok, writes=[tk])
                P.op('dve', lambda e: e.tensor_tensor(out=t_[:, 0, :], in0=t_[:, 0, :], in1=flT[:, :, c], op=ALU.max),
                     reads=[tk, 'flT'], writes=[tk])
                P.op('dve', lambda e: e.tensor_tensor(out=t_[:, 0, :], in0=t_[:, 0, :], in1=den, op=ALU.max),
                     reads=[tk] + ok, writes=[tk])
                P.op('dve', lambda e: e.tensor_tensor(out=t_[:, 1, :], in0=t_[:, 0, :], in1=t_[:, 0, :], op=ALU.mult),
                     reads=[tk], writes=[tk])
                P.op('dve', lambda e: e.scalar_tensor_tensor(out=t_[:, 2, :], in0=t_[:, 1, :], scalar=EPS, in1=ssq[cb][:],
                                                             op0=ALU.mult, op1=ALU.add),
                     reads=[tk] + [('ssq', cb, h_) for h_ in range(4)], writes=[tk])
                P.op('pool', lambda e: e.tensor_tensor(out=t_[:, 3, :], in0=t_[:, 2, :], in1=mhalf[:], op=ALU.pow),
                     reads=[tk, 'mhalf'], writes=[tk])
                P.op('dve', lambda e: e.tensor_tensor(out=ytmp[:], in0=osb[cb][:, :, 0:128],
                                                      in1=t_[:, 3, :].unsqueeze(2).broadcast_to([64, 4, 128]), op=ALU.mult),
                     reads=ok + [tk], writes=['ytmp'])
                P.op('dve', lambda e: e.tensor_tensor(out=ym[gi_][:, cl, :].rearrange("p (h d) -> p h d", h=4), in0=ytmp[:],
                                                      in1=ogg[gi_][:, cl, :].rearrange("p (h d) -> p h d", h=4), op=ALU.mult),
                     reads=['ytmp', ('ogg', gi_)], writes=[('ym', gi_)])
                if cl == 7:
                    P.dma('sp', self.mix_d[g * 512:(g + 1) * 512, 0:512].rearrange("(c s) e -> s c e", s=64), ym[gi_][:],
                          reads=[('ym', gi_)], writes=[('mixm', g)])
            NS = len(steps)
            stageA(0)
            for n in range(NS):
                g, cl, hh = steps[n]
                if n + 1 < NS:
                    stageA(n + 1)
                stageB(n)
                if n >= 1:
                    stageN(n - 1)
                if cl == 0 and hh == 1 and g + 1 < NG:
                    load_group(g + 1)
            stageN(NS - 1)
            self.barrier()

    def _negB(self, st, name, g1, g2):
        P = self.P
        ga = self.sb(st, name + "_ga", [128, 64], F32)
        gb = self.sb(st, name + "_gb", [128, 64], F32)
        m1 = self.sb(st, name + "_m1", [128, 1], F32)
        m2 = self.sb(st, name + "_m2", [128, 1], F32)
        nb = self.sb(st, name, [128, 1], F32)
        P.dma('sp', ga[:], g1.partition_broadcast(128), writes=[name + 'ga'])
        P.dma('sp', gb[:], g2.partition_broadcast(128), writes=[name + 'gb'])
        P.op('dve', lambda e: e.tensor_reduce(out=m1[:], in_=ga[:], axis=AX.X, op=ALU.max, apply_absolute_value=True),
             reads=[name + 'ga'], writes=[name + 'm1'])
        P.op('dve', lambda e: e.tensor_reduce(out=m2[:], in_=gb[:], axis=AX.X, op=ALU.max, apply_absolute_value=True),
             reads=[name + 'gb'], writes=[name + 'm2'])
        P.op('dve', lambda e: e.scalar_tensor_tensor(out=nb[:], in0=m1[:], scalar=-8.0, in1=m2[:], op0=ALU.mult, op1=ALU.mult),
             reads=[name + 'm1', name + 'm2'], writes=[name])
        return nb

    def phase2_moba(self, l):
        S, P, nc = self.S, self.P, self.nc
        NQC = S // 512
        NKT = S // 128
        NB = S // 256
        with ExitStack() as st:
            sb, ps = self.sb, self.ps
            identb = sb(st, "identb_b", [128, 128], BF16)
            tm4 = sb(st, "tm4", [128, 4, 512], BF16)
            zl = sb(st, "zl", [1, 128], BF16)
            zr = sb(st, "zr", [1, 512], BF16)
            P.dma('sp', identb[:], self.c_identb, writes=['identb'])
            P.dma('sp', tm4[:], self.c_tm4, writes=['tm4'])
            P.op('pool', lambda e: e.memset(zl[:], 0.0), writes=['zl'])
            P.op('pool', lambda e: e.memset(zr[:], 0.0), writes=['zr'])
            negB = self._negB(st, "negBm", self.moba_qk_norm[l, 0], self.moba_qk_norm[l, 1])
            KX = sb(st, "KX", [96, S], BF16)
            VX = sb(st, "VX", [128, NKT, 65], BF16)
            kmean = sb(st, "kmean", [64, 32], F32)
            kmb = sb(st, "kmb", [64, 32], BF16)
            QX = [sb(st, "QX", [96, 512], BF16) for _ in range(2)]
            gsb = sb(st, "gsb_b", [128, 32], F32)
            m8 = sb(st, "m8", [128, 8], F32)
            sel = sb(st, "sel_b", [128, 32], F32)
            MBw = sb(st, "MBw", [128, 128], BF16)
            MLA = int(os.environ.get("LA", "3"))
            PT = [sb(st, "PT", [128, 512], BF16) for _ in range(MLA + 2)]
            rz = sb(st, "rz_b", [128, 4], F32)
            yb = [sb(st, "yb", [128, 4, 64], BF16) for _ in range(2)]
            pS = [ps(st, "pS_b", [128, 512], F32) for _ in range(MLA + 1)]
            pO = [ps(st, "pO_b", [128, 512], F32) for _ in range(2)]
            pM = ps(st, "pM_b", [128, 512], F32)
            pMb = ps(st, "pMb_b", [128, 1024], BF16)
            P.dma('sp', KX[64:96, :], self.c_ind32, writes=['KXi'])
            P.op('pool', lambda e: e.memset(VX[:], 1.0), writes=['VX'])
            P.op('pool', lambda e: e.memset(MBw[:], 0.0), writes=['MBw'])
            P.op('pool', lambda e: e.memset(kmean[:], 0.0), writes=['kmean'])
            all_tiles = list(range(S // 128))
            si = 0
            qi = 0
            for h in range(4):
                P.dma('sp', KX[0:64, :], self.bkT[h * 64:(h + 1) * 64, :], reads=[('bkT', j) for j in all_tiles], writes=['KX'])
                self.dma_mid(VX[:, :, 0:64], self.bv_d[:, h * 64:(h + 1) * 64].rearrange("(kt p) d -> p kt d", p=128), NKT, 8,
                             reads=[('bv_d', j) for j in all_tiles], writes=['VX'])
                P.op('dve', lambda e: e.tensor_reduce(out=kmean[:, 0:NB], in_=KX[0:64, :].rearrange("p (n k) -> p n k", k=256),
                                                      axis=AX.X, op=ALU.add), reads=['KX'], writes=['kmean'])
                P.op('dve', lambda e: e.tensor_scalar(out=kmb[:], in0=kmean[:], scalar1=1.0 / 256, scalar2=None, op0=ALU.mult),
                     reads=['kmean'], writes=['kmb'])
                def prep(qc, Q_, qk_):
                    q0 = qc * 512
                    P.dma('sp', Q_[0:64, :], self.bqT[h * 64:(h + 1) * 64, q0:q0 + 512],
                          reads=[('bqT', 4 * qc + j) for j in range(4)], writes=[qk_])
                    yield
                    for j in range(4):
                        own = 2 * qc + j // 2
                        P.op('pe', lambda e: e.matmul(pM[:, 0:32], lhsT=Q_[0:64, j * 128:(j + 1) * 128], rhs=kmb[:],
                                                      start=True, stop=True), reads=[qk_, 'kmb'], writes=['pM'])
                        yield
                        P.op('pool', lambda e: e.memset(gsb[:], -1e30), writes=['gsb'])
                        yield
                        if own > 0:
                            P.op('dve', lambda e: e.tensor_copy(out=gsb[:, 0:own], in_=pM[:, 0:own]), reads=['pM'], writes=['gsb'])
                            yield
                        yield
                        P.op('dve', lambda e: e.max(out=m8[:], in_=gsb[:]), reads=['gsb'], writes=['m8'])
                        yield
                        P.op('dve', lambda e: e.tensor_scalar(out=sel[:], in0=gsb[:], scalar1=m8[:, 2:3], scalar2=None,
                                                              op0=ALU.is_ge), reads=['gsb', 'm8'], writes=['sel'])
                        yield
                        yield
                        P.op('dve', lambda e: e.tensor_scalar(out=MBw[:, 64:96], in0=sel[:], scalar1=-NEGM, scalar2=NEGM,
                                                              op0=ALU.mult, op1=ALU.add), reads=['sel'], writes=['MBw'])
                        yield
                        P.op('dve', lambda e: e.memset(MBw[:, 64 + own:65 + own], 0.0), writes=['MBw'])
                        yield
                        if own + 1 < 32:
                            P.op('dve', lambda e: e.memset(MBw[:, 65 + own:96], NEGM), writes=['MBw'])
                            yield
                        yield
                        P.op('pe', lambda e: e.transpose(out=pMb[:, 0:128], in_=MBw[:], identity=identb[:]),
                             reads=['MBw', 'identb'], writes=['pMb'])
                        yield
                        P.op('act', lambda e: e.copy(out=Q_[64:96, j * 128:(j + 1) * 128], in_=pMb[64:96, 0:128]),
                             reads=['pMb'], writes=[qk_])
                        yield
                        yield
                for _ in prep(0, QX[qi % 2], ('QX', qi % 2)):
                    pass
                for qc in range(NQC):
                    Q_ = QX[qi % 2]
                    qk_ = ('QX', qi % 2)
                    qi += 1
                    q0 = qc * 512
                    gp = prep(qc + 1, QX[qi % 2], ('QX', qi % 2)) if qc + 1 < NQC else None
                    pacc = 0.0
                    prate = 52.0 / (4 * qc + 4)
                    po = pO[qc % 2]
                    pok = ('pO', qc % 2)
                    P.op('pe', lambda e: e.matmul(po[:, 0:260], lhsT=zl[:], rhs=zr[:, 0:260], start=True, stop=True,
                                                  skip_group_check=True), reads=['zl', 'zr'], writes=[pok])
                    nkt = 4 * qc + 4
                    LA = MLA
                    base = si

                    def qkmm(kt):
                        pp = pS[(base + kt) % (MLA + 1)]
                        P.op('pe', lambda e: e.matmul(pp[:], lhsT=KX[:, kt * 128:(kt + 1) * 128], rhs=Q_[:], start=True, stop=True),
                             reads=['KX', 'KXi', qk_], writes=[('pS', (base + kt) % (MLA + 1))])
                    for kt in range(min(LA, nkt)):
                        qkmm(kt)
                    for kt in range(nkt):
                        if kt + LA < nkt:
                            qkmm(kt + LA)
                        p_ = pS[si % (MLA + 1)]
                        pk = ('pS', si % (MLA + 1))
                        t_ = PT[si % (MLA + 2)]
                        tk = ('PT', si % (MLA + 2))
                        si += 1
                        if LA == 0:
                            qkmm(kt)
                        P.op('act', lambda e: e.activation(out=t_[:], in_=p_[:], func=AF.Exp, bias=negB[:, 0:1], scale=0.125),
                             reads=[pk, 'negBm'], writes=[tk])
                        off = kt - 4 * qc
                        if off >= 0:
                            P.op('dve', lambda e: e.tensor_tensor(out=t_[:], in0=t_[:], in1=tm4[:, off, :], op=ALU.mult),
                                 reads=[tk, 'tm4'], writes=[tk])
                        for j in range(4):
                            if kt > 4 * qc + j:
                                continue
                            P.op('pe', lambda e: e.matmul(po[:, j * 65:(j + 1) * 65], lhsT=t_[:, j * 128:(j + 1) * 128],
                                                          rhs=VX[:, kt, :], start=False, stop=(kt == 4 * qc + j),
                                                          skip_group_check=True), reads=[tk, 'VX'], writes=[pok])
                        if gp is not None:
                            pacc += prate
                            while pacc >= 1.0 and gp is not None:
                                pacc -= 1.0
                                try:
                                    next(gp)
                                except StopIteration:
                                    gp = None
                    if gp is not None:
                        for _ in gp:
                            pass
                    y_ = yb[qc % 2]
                    yk = ('yb', qc % 2)
                    pov = po[:, 0:260].rearrange("p (j d) -> p j d", d=65)
                    P.op('dve', lambda e: e.reciprocal(out=rz[:], in_=pov[:, :, 64]), reads=[pok], writes=['rz'])
                    P.op('dve', lambda e: e.tensor_tensor(out=y_[:], in0=pov[:, :, 0:64],
                                                          in1=rz[:].unsqueeze(2).broadcast_to([128, 4, 64]), op=ALU.mult),
                         reads=[pok, 'rz'], writes=[yk])
                    P.dma('sp', self.mix_d[q0:q0 + 512, 512 + h * 64:512 + (h + 1) * 64].rearrange("(j p) d -> p j d", p=128),
                          y_[:], reads=[yk], writes=[('mixb', h, qc)])
            self.barrier()

    def phase2_nsa(self, l):
        S, P, nc = self.S, self.P, self.nc
        NT = S // 128
        Nc = S // 16 - 1
        NCT = max(1, S // 2048)
        with ExitStack() as st:
            sb, ps = self.sb, self.ps
            identb = sb(st, "identb_n", [128, 128], BF16)
            trib = sb(st, "trib", [128, 128], BF16)
            triw = sb(st, "triw", [128, 128], BF16)
            cmask = sb(st, "cmask", [128, 17, 128], BF16)
            OVL = sb(st, "OVL", [128, NCT, 128], BF16)
            cols = sb(st, "cols", [128, 4], F32)
            zl = sb(st, "zl_n", [1, 128], BF16)
            zr = sb(st, "zr_n", [1, 512], BF16)
            NG = sb(st, "NG", [128, NT, 12], F32)
            P.dma('sp', identb[:], self.c_identb, writes=['identb'])
            P.dma('sp', trib[:], self.c_trib, writes=['trib'])
            P.dma('sp', triw[:], self.c_triw, writes=['triw'])
            P.dma('sp', cmask[:], self.c_cmask, writes=['cmask'])
            P.dma('sp', OVL[:], self.c_ovl.rearrange("(ct p) n -> p ct n", p=128)[:, 0:NCT, :], writes=['OVL'])
            P.dma('sp', cols[:], self.c_cols, writes=['cols'])
            P.op('pool', lambda e: e.memset(zl[:], 0.0), writes=['zl'])
            P.op('pool', lambda e: e.memset(zr[:], 0.0), writes=['zr'])
            allt = list(range(NT))
            self.dma_mid(NG[:], self.ng_d.rearrange("(j p) g -> p j g", p=128), NT, 8, reads=[('ng_d', j) for j in allt], writes=['NG'])
            negBc = self._negB(st, "negBc", self.nsa_q_norm[l], self.nsa_k_norm[l, 0])
            negBs = self._negB(st, "negBs", self.nsa_q_norm[l], self.nsa_k_norm[l, 1])
            negBw = self._negB(st, "negBw", self.nsa_q_norm[l], self.nsa_k_norm[l, 2])
            KSX = sb(st, "KSX", [128, S], BF16)
            KWX = sb(st, "KWX", [64, S], BF16)
            VSX = sb(st, "VSX", [128, NT, 65], BF16)
            VWX = sb(st, "VWX", [128, NT, 65], BF16)
            KcT = sb(st, "KcT", [64, NCT * 128], BF16)
            VCX = sb(st, "VCX", [128, NCT, 65], BF16)
            P.dma('sp', KSX[0:64, :], self.ksT, reads=[('ksT', j) for j in allt], writes=['KSX'])
            P.dma('sp', KSX[64:128, :], self.c_ind64, writes=['KSXi'])
            P.dma('sp', KWX[:], self.kwT, reads=[('kwT', j) for j in allt], writes=['KWX'])
            for V_, src, nm in ((VSX, self.vs_d, 'vs_d'), (VWX, self.vw_d, 'vw_d')):
                P.op('pool', lambda e: e.memset(V_[:], 1.0), writes=[nm + 'X'])
                self.dma_mid(V_[:, :, 0:64], src.rearrange("(kt p) d -> p kt d", p=128), NT, 8,
                             reads=[(nm, j) for j in allt], writes=[nm + 'X'])
            P.op('pool', lambda e: e.memset(VCX[:], 1.0), writes=['VCX'])
            pS = [ps(st, "pS_n", [128, 512], F32) for _ in range(2)]
            pOc = ps(st, "pOc", [128, 512], F32)
            pU = ps(st, "pU", [128, 512], F32)
            pOs = ps(st, "pOs", [128, 512], F32)
            pOw = ps(st, "pOw", [128, 512], F32)
            pMb = ps(st, "pMb_n", [128, 1024], BF16)
            pM = ps(st, "pM_n", [128, 512], F32)
            with ExitStack() as s2:
                KCV = sb(s2, "KCV", [128, S], BF16)
                W1s = sb(s2, "W1s", [128, 32, 128], F32)
                W1 = sb(s2, "W1", [128, 32, 128], BF16)
                pes = sb(s2, "pes", [32, 128], F32)
                peb = sb(s2, "peb", [32, 128], BF16)
                peT = sb(s2, "peT", [128, 32], BF16)
                w2s = sb(s2, "w2s", [128, 2, 64], F32)
                w2 = sb(s2, "w2", [128, 2, 64], BF16)
                gk0 = sb(s2, "gk0", [128, 64], F32)
                bias = sb(s2, "bias_c", [128, 2], F32)
                hidb = sb(s2, "hidb", [128, NCT * 128], BF16)
                kc32 = sb(s2, "kc32", [128, 64], F32)
                kcn = sb(s2, "kcn", [128, 64], BF16)
                junk = sb(s2, "junk_c", [128, 64], F32)
                ssc = sb(s2, "ssc", [128, 2], F32)
                P.dma('sp', KCV[:], self.kcvcT, reads=[('kcvcT', b) for b in range(S // 512)], writes=['KCV'])
                for br in range(2):
                    P.dma('sp', W1s[64 * br:64 * br + 64], self.cmp_w1[l, br].rearrange("(r d) j -> d r j", d=64), writes=['W1s'])
                    P.dma('sp', w2s[:, br, :], self.cmp_w2[l, br], writes=['w2s'])
                    P.dma('sp', pes[:, 64 * br:64 * br + 64], self.cmp_pe[l, br], writes=['pes'])
                P.dma('sp', gk0[:], self.nsa_k_norm[l, 0].partition_broadcast(128), writes=['gk0'])
                P.op('dve', lambda e: e.tensor_copy(out=W1[:], in_=W1s[:]), reads=['W1s'], writes=['W1'])
                P.op('dve', lambda e: e.tensor_copy(out=peb[:], in_=pes[:]), reads=['pes'], writes=['peb'])
                P.op('pe', lambda e: e.transpose(out=pMb[:, 0:32], in_=peb[:], identity=identb[0:32, 0:32]),
                     reads=['peb', 'identb'], writes=['pMb'])
                P.op('dve', lambda e: e.tensor_copy(out=peT[:], in_=pMb[:, 0:32]), reads=['pMb'], writes=['peT'])
                P.op('dve', lambda e: e.tensor_copy(out=w2[:], in_=w2s[:]), reads=['w2s'], writes=['w2'])
                P.op('pool', lambda e: e.memset(hidb[:], 0.0), writes=['hidb'])
                for br in range(2):
                    rows = slice(64 * br, 64 * br + 64)
                    kview = KCV[rows, :].rearrange("p (c s) -> p c s", s=16)
                    for r in range(32):
                        P.op('pe', lambda e: e.matmul(pM[:, 0:1], lhsT=W1[rows, r, :], rhs=peT[rows, r:r + 1],
                                                      start=(r == 0), stop=(r == 31)), reads=['W1', 'peT'], writes=['pM'])
                    P.op('dve', lambda e: e.tensor_copy(out=bias[:, br:br + 1], in_=pM[:, 0:1]), reads=['pM'], writes=['bias'])
                    for r in range(32):
                        rhs = kview[:, 0:Nc, r] if r < 16 else kview[:, 1:Nc + 1, r - 16]
                        P.op('pe', lambda e: e.matmul(pS[0][:, 0:Nc], lhsT=W1[rows, r, :], rhs=rhs,
                                                      start=(r == 0), stop=(r == 31)), reads=['W1', 'KCV'], writes=[('pS', 0)])
                    P.op('act', lambda e: e.activation(out=hidb[:, 0:Nc], in_=pS[0][:, 0:Nc], func=AF.Silu, bias=bias[:, br:br + 1]),
                         reads=[('pS', 0), 'bias'], writes=['hidb'])
                    for ct in range(NCT):
                        P.op('pe', lambda e: e.matmul(pM[:, 0:64], lhsT=hidb[:, ct * 128:(ct + 1) * 128], rhs=w2[:, br, :],
                                                      start=True, stop=True), reads=['hidb', 'w2'], writes=['pM'])
                        if br == 0:
                            P.op('act', lambda e: e.activation(out=junk[:], in_=pM[:, 0:64], func=AF.Square, scale=0.125,
                                                               accum_out=ssc[:, 0:1]), reads=['pM'], writes=['junk_c', 'ssc'])
                            P.op('dve', lambda e: e.tensor_scalar(out=ssc[:, 0:1], in0=ssc[:, 0:1], scalar1=EPS, scalar2=None,
                                                                  op0=ALU.add), reads=['ssc'], writes=['ssc'])
                            P.op('act', lambda e: e.activation(out=ssc[:, 0:1], in_=ssc[:, 0:1], func=AF.Sqrt), reads=['ssc'], writes=['ssc'])
                            P.op('dve', lambda e: e.reciprocal(out=ssc[:, 1:2], in_=ssc[:, 0:1]), reads=['ssc'], writes=['ssc'])
                            P.op('dve', lambda e: e.scalar_tensor_tensor(out=kcn[:], in0=pM[:, 0:64], scalar=ssc[:, 1:2], in1=gk0[:],
                                                                         op0=ALU.mult, op1=ALU.mult),
                                 reads=['pM', 'ssc', 'gk0'], writes=['kcn'])
                            P.op('pe', lambda e: e.transpose(out=pMb[0:64, 0:128], in_=kcn[:], identity=identb[:]),
                                 reads=['kcn', 'identb'], writes=['pMb'])
                            P.op('act', lambda e: e.copy(out=KcT[:, ct * 128:(ct + 1) * 128], in_=pMb[0:64, 0:128]),
                                 reads=['pMb'], writes=['KcT'])
                        else:
                            P.op('act', lambda e: e.copy(out=VCX[:, ct, 0:64], in_=pM[:, 0:64]), reads=['pM'], writes=['VCX'])
                self.barrier()
            QU = [sb(st, "QU", [64, 512], BF16) for _ in range(2)]
            QR0 = [sb(st, "QR0", [128, 512], BF16) for _ in range(2)]
            QR1 = [sb(st, "QR1", [128, 512], BF16) for _ in range(2)]
            PTc = [sb(st, "PTc", [128, 512], BF16) for _ in range(NCT)]
            PT = [sb(st, "PTn", [128, 512], BF16) for _ in range(3)]
            zz = sb(st, "zz", [128, 3, 4], F32)
            rzz = sb(st, "rzz", [128, 3, 4], F32)
            coef = sb(st, "coef", [128, 3, 4], F32)
            imp = sb(st, "imp", [128, 128], F32)
            work = sb(st, "work", [128, 128], F32)
            m8a = sb(st, "m8a", [128, 8], F32)
            m8b = sb(st, "m8b", [128, 8], F32)
            selm = sb(st, "selm", [128, 128], F32)
            MB = sb(st, "MB", [128, 128], BF16)
            MBs = sb(st, "MBs", [128, 128], BF16)
            yacc = sb(st, "yacc", [128, 4, 64], F32)
            yn = [sb(st, "yn", [128, 256], BF16) for _ in range(2)]
            si = 0

            def seed(t, n, key):
                P.op('pe', lambda e: e.matmul(t[:, 0:n], lhsT=zl[:], rhs=zr[:, 0:n], start=True, stop=True, skip_group_check=True),
                     reads=['zl', 'zr'], writes=[key])

            def hb(ap):
                return ap.unsqueeze(1).broadcast_to([ap.shape[0], 4, 128])

            for m in range(NT):
                t0 = m * 128
                b2 = m % 2
                qu, qr0, qr1 = QU[b2], QR0[b2], QR1[b2]
                P.dma('sp', qu[:].rearrange("p (h t) -> p h t", h=4), self.nquT[:, t0:t0 + 128].rearrange("(h d) t -> d h t", d=64),
                      reads=[('nquT', m)], writes=[('QU', b2)])
                P.dma('sp', qr0[0:64, :].rearrange("p (h t) -> p h t", h=4), self.nqrT[:, t0:t0 + 128].rearrange("(h d) t -> d h t", d=64),
                      reads=[('nqrT', m)], writes=[('QR0', b2)])
                use_g1 = (2 * m + 1) >= 64
                if use_g1:
                    P.dma('sp', qr1[0:64, :].rearrange("p (h t) -> p h t", h=4),
                          self.nqrT[:, t0:t0 + 128].rearrange("(h d) t -> d h t", d=64), reads=[('nqrT', m)], writes=[('QR1', b2)])
                ctn = min(NCT, (8 * m + 6) // 128 + 1)
                for ct in range(ctn):
                    p_ = pS[si % 2]
                    pk = ('pS', si % 2)
                    si += 1
                    P.op('pe', lambda e: e.matmul(p_[:], lhsT=KcT[:, ct * 128:(ct + 1) * 128], rhs=qu[:], start=True, stop=True),
                         reads=['KcT', ('QU', b2)], writes=[pk])
                    P.op('act', lambda e: e.activation(out=PTc[ct][:], in_=p_[:], func=AF.Exp, bias=negBc[:, 0:1], scale=0.125),
                         reads=[pk, 'negBc'], writes=[('PTc', ct)])
                    r = m - 16 * ct
                    if r <= 16:
                        P.op('pool', lambda e: e.tensor_tensor(out=PTc[ct][:].rearrange("p (h t) -> p h t", h=4),
                                                               in0=PTc[ct][:].rearrange("p (h t) -> p h t", h=4),
                                                               in1=hb(cmask[:, r, :]), op=ALU.mult),
                             reads=[('PTc', ct), 'cmask'], writes=[('PTc', ct)])
                seed(pOc, 260, 'pOc')
                seed(pU, 512, 'pU')
                for h in range(4):
                    for ct in range(ctn):
                        P.op('pe', lambda e: e.matmul(pOc[:, h * 65:(h + 1) * 65], lhsT=PTc[ct][:, h * 128:(h + 1) * 128],
                                                      rhs=VCX[:, ct, :], start=False, stop=(ct == ctn - 1), skip_group_check=True),
                             reads=[('PTc', ct), 'VCX'], writes=['pOc'])
                        P.op('pe', lambda e: e.matmul(pU[:, h * 128:(h + 1) * 128], lhsT=PTc[ct][:, h * 128:(h + 1) * 128],
                                                      rhs=OVL[:, ct, :], start=False, stop=(ct == ctn - 1), skip_group_check=True),
                             reads=[('PTc', ct), 'OVL'], writes=['pU'])
                pocv = pOc[:, 0:260].rearrange("p (h d) -> p h d", d=65)
                P.op('dve', lambda e: e.tensor_scalar(out=zz[:, 0, :], in0=pocv[:, :, 64], scalar1=1e-30, scalar2=None, op0=ALU.max),
                     reads=['pOc'], writes=['zz0'])
                P.op('dve', lambda e: e.reciprocal(out=rzz[:, 0, :], in_=zz[:, 0, :]), reads=['zz0'], writes=['rzz0'])
                P.op('dve', lambda e: e.tensor_scalar(out=imp[:], in0=pU[:, 0:128], scalar1=rzz[:, 0, 0:1], scalar2=None, op0=ALU.mult),
                     reads=['pU', 'rzz0'], writes=['imp'])
                for h in range(1, 4):
                    P.op('dve', lambda e: e.scalar_tensor_tensor(out=imp[:], in0=pU[:, h * 128:(h + 1) * 128], scalar=rzz[:, 0, h:h + 1],
                                                                 in1=imp[:], op0=ALU.mult, op1=ALU.add),
                         reads=['pU', 'rzz0', 'imp'], writes=['imp'])
                n1 = 2 * m + 1
                if n1 + 1 < 128:
                    P.op('pool', lambda e: e.memset(imp[:, n1 + 1:128], -1e30), reads=['imp'], writes=['imp'])
                P.op('pool', lambda e: e.tensor_copy(out=imp[:, n1:n1 + 1], in_=cols[:, 0:1]), reads=['cols', 'imp'], writes=['imp'])
                P.op('pool', lambda e: e.memset(imp[:, n1 - 1:n1], 1e9), reads=['imp'], writes=['imp'])
                if n1 - 2 >= 0:
                    P.op('dve', lambda e: e.tensor_tensor(out=imp[:, n1 - 2:n1 - 1], in0=imp[:, n1 - 2:n1 - 1], in1=cols[:, 1:2], op=ALU.max),
                         reads=['cols', 'imp'], writes=['imp'])
                P.op('pool', lambda e: e.memset(imp[:, 0:1], 1e9), reads=['imp'], writes=['imp'])
                P.op('dve', lambda e: e.max(out=m8a[:], in_=imp[:]), reads=['imp'], writes=['m8a'])
                P.op('dve', lambda e: e.match_replace(out=work[:], in_to_replace=m8a[:], in_values=imp[:], imm_value=-1e30),
                     reads=['imp', 'm8a'], writes=['work'])
                P.op('dve', lambda e: e.max(out=m8b[:], in_=work[:]), reads=['work'], writes=['m8b'])
                P.op('dve', lambda e: e.tensor_scalar(out=selm[:], in0=imp[:], scalar1=m8b[:, 7:8], scalar2=None, op0=ALU.is_ge),
                     reads=['imp', 'm8b'], writes=['selm'])
                P.op('dve', lambda e: e.tensor_scalar(out=MB[:], in0=selm[:], scalar1=-NEGM, scalar2=NEGM, op0=ALU.mult, op1=ALU.add),
                     reads=['selm'], writes=['MB'])
                if n1 + 1 < 128:
                    P.op('pool', lambda e: e.memset(MB[:, n1 + 1:128], NEGM), reads=['MB'], writes=['MB'])
                P.op('pool', lambda e: e.tensor_copy(out=MB[:, n1:n1 + 1], in_=cols[:, 2:3]), reads=['cols', 'MB'], writes=['MB'])
                P.op('pool', lambda e: e.tensor_copy(out=MBs[:, 0:64], in_=MB[:, 64:128]), reads=['MB'], writes=['MBs'])
                P.op('pool', lambda e: e.tensor_copy(out=MBs[:, 64:128], in_=MB[:, 0:64]), reads=['MB'], writes=['MBs'])
                P.op('pe', lambda e: e.transpose(out=pMb[:, 0:128], in_=MBs[:], identity=identb[:]), reads=['MBs', 'identb'], writes=['pMb'])
                P.op('act', lambda e: e.copy(out=qr0[64:128, :].rearrange("p (h t) -> p h t", h=4), in_=hb(pMb[64:128, 0:128])),
                     reads=['pMb'], writes=[('QR0', b2)])
                if use_g1:
                    P.op('pe', lambda e: e.transpose(out=pMb[:, 128:256], in_=MB[:], identity=identb[:]), reads=['MB', 'identb'], writes=['pMb'])
                    P.op('act', lambda e: e.copy(out=qr1[64:128, :].rearrange("p (h t) -> p h t", h=4), in_=hb(pMb[64:128, 128:256])),
                         reads=['pMb'], writes=[('QR1', b2)])
                seed(pOs, 260, 'pOs')
                for kt in range(m + 1):
                    g = kt // 32
                    qr_, qrk = (qr0, ('QR0', b2)) if g == 0 else (qr1, ('QR1', b2))
                    p_ = pS[si % 2]
                    pk = ('pS', si % 2)
                    t_ = PT[si % 3]
                    tk = ('PTn', si % 3)
                    si += 1
                    P.op('pe', lambda e: e.matmul(p_[:], lhsT=KSX[:, kt * 128:(kt + 1) * 128], rhs=qr_[:], start=True, stop=True),
                         reads=['KSX', 'KSXi', qrk], writes=[pk])
                    P.op('act', lambda e: e.activation(out=t_[:], in_=p_[:], func=AF.Exp, bias=negBs[:, 0:1], scale=0.125),
                         reads=[pk, 'negBs'], writes=[tk])
                    if kt == m:
                        P.op('pool', lambda e: e.tensor_tensor(out=t_[:].rearrange("p (h t) -> p h t", h=4),
                                                               in0=t_[:].rearrange("p (h t) -> p h t", h=4), in1=hb(trib[:]), op=ALU.mult),
                             reads=[tk, 'trib'], writes=[tk])
                    for h in range(4):
                        P.op('pe', lambda e: e.matmul(pOs[:, h * 65:(h + 1) * 65], lhsT=t_[:, h * 128:(h + 1) * 128], rhs=VSX[:, kt, :],
                                                      start=False, stop=(kt == m), skip_group_check=True),
                             reads=[tk, 'vs_dX'], writes=['pOs'])
                seed(pOw, 260, 'pOw')
                for kt in range(max(0, m - 4), m + 1):
                    p_ = pS[si % 2]
                    pk = ('pS', si % 2)
                    t_ = PT[si % 3]
                    tk = ('PTn', si % 3)
                    si += 1
                    P.op('pe', lambda e: e.matmul(p_[:], lhsT=KWX[:, kt * 128:(kt + 1) * 128], rhs=qr0[0:64, :], start=True, stop=True),
                         reads=['KWX', ('QR0', b2)], writes=[pk])
                    P.op('act', lambda e: e.activation(out=t_[:], in_=p_[:], func=AF.Exp, bias=negBw[:, 0:1], scale=0.125),
                         reads=[pk, 'negBw'], writes=[tk])
                    if kt == m or kt == m - 4:
                        mk = trib if kt == m else triw
                        P.op('pool', lambda e: e.tensor_tensor(out=t_[:].rearrange("p (h t) -> p h t", h=4),
                                                               in0=t_[:].rearrange("p (h t) -> p h t", h=4), in1=hb(mk[:]), op=ALU.mult),
                             reads=[tk, 'trib', 'triw'], writes=[tk])
                    for h in range(4):
                        P.op('pe', lambda e: e.matmul(pOw[:, h * 65:(h + 1) * 65], lhsT=t_[:, h * 128:(h + 1) * 128], rhs=VWX[:, kt, :],
                                                      start=False, stop=(kt == m), skip_group_check=True),
                             reads=[tk, 'vw_dX'], writes=['pOw'])
                posv = pOs[:, 0:260].rearrange("p (h d) -> p h d", d=65)
                powv = pOw[:, 0:260].rearrange("p (h d) -> p h d", d=65)
                P.op('dve', lambda e: e.tensor_copy(out=zz[:, 1, :], in_=posv[:, :, 64]), reads=['pOs'], writes=['zz1'])
                P.op('dve', lambda e: e.tensor_copy(out=zz[:, 2, :], in_=powv[:, :, 64]), reads=['pOw'], writes=['zz1'])
                P.op('dve', lambda e: e.reciprocal(out=rzz[:, 1:3, :], in_=zz[:, 1:3, :]), reads=['zz1'], writes=['rzz1'])
                P.op('dve', lambda e: e.tensor_tensor(out=coef[:], in0=rzz[:], in1=NG[:, m, :].rearrange("p (b h) -> p b h", b=3), op=ALU.mult),
                     reads=['rzz0', 'rzz1', 'NG'], writes=['coef'])
                for h in range(4):
                    P.op('dve', lambda e: e.tensor_scalar(out=yacc[:, h, :], in0=pocv[:, h, 0:64], scalar1=coef[:, 0, h:h + 1], scalar2=None,
                                                          op0=ALU.mult), reads=['pOc', 'coef'], writes=[('yacc', h)])
                    P.op('dve', lambda e: e.scalar_tensor_tensor(out=yacc[:, h, :], in0=posv[:, h, 0:64], scalar=coef[:, 1, h:h + 1],
                                                                 in1=yacc[:, h, :], op0=ALU.mult, op1=ALU.add),
                         reads=['pOs', 'coef', ('yacc', h)], writes=[('yacc', h)])
                    P.op('dve', lambda e: e.scalar_tensor_tensor(out=yn[b2][:, h * 64:(h + 1) * 64], in0=powv[:, h, 0:64],
                                                                 scalar=coef[:, 2, h:h + 1], in1=yacc[:, h, :], op0=ALU.mult, op1=ALU.add),
                         reads=['pOw', 'coef', ('yacc', h)], writes=[('yn', b2)])
                P.dma('sp', self.mix_d[t0:t0 + 128, 768:1024], yn[b2][:], reads=[('yn', b2)], writes=[('mixn', m)])
            self.barrier()

    def build_all(self):
        self.declare()
        for l in range(self.depth):
            xsrc = self.x_in if l == 0 else self.xbuf
            xdst = self.y_out if l == self.depth - 1 else self.xbuf
            self.phase1(l, xsrc)
            if os.environ.get("PH2", "old") == "new":
                self.phase2(l)
            elif os.environ.get("PH2", "old") == "b":
                self.phase2_moba(l)
                self.phase2b(l)
            else:
                self.phase2_mlstm(l)
                self.phase2_moba(l)
                self.phase2_nsa2(l)
            self.phase3(l, xsrc, xdst)
        self.P.finish()
        return self.nc


def make_consts(S):
    bf = ml_dtypes.bfloat16
    half = 8
    inv = np.exp(-math.log(500000.0) * np.arange(half, dtype=np.float32) * (2.0 / 16)).astype(np.float32)
    ang = np.arange(S, dtype=np.float32)[:, None] * inv[None, :]
    d = dict(c_identb=np.eye(128, dtype=np.float32).astype(bf), c_identf=np.eye(128, dtype=np.float32),
             c_cos=np.cos(ang).astype(np.float32), c_sin=np.sin(ang).astype(np.float32))
    i = np.arange(128)
    d['c_ut'] = (i[:, None] < i[None, :]).astype(np.float32)
    d['c_tri'] = (i[:, None] <= i[None, :]).astype(np.float32)
    key = np.arange(S)
    d['c_ind32'] = (key[None, :] // 256 == np.arange(32)[:, None]).astype(np.float32).astype(bf)
    d['c_ind64'] = (((key[None, :] // 64) % 64) == np.arange(64)[:, None]).astype(np.float32).astype(bf)
    k = np.arange(128)[:, None, None]
    o = np.arange(4)[None, :, None]
    q = np.arange(512)[None, None, :]
    d['c_tm4'] = (q >= k + 128 * o).astype(np.float32).astype(bf)
    d['c_trib'] = (i[:, None] <= i[None, :]).astype(np.float32).astype(bf)
    d['c_triw'] = (i[:, None] > i[None, :]).astype(np.float32).astype(bf)
    ii = np.arange(128)[:, None, None]
    r = np.arange(17)[None, :, None]
    j = np.arange(128)[None, None, :]
    d['c_cmask'] = (16 * ii + 31 <= 128 * r + j).astype(np.float32).astype(bf)
    c = np.arange(512)[:, None]
    n = np.arange(128)[None, :]
    Nc = S // 16 - 1
    d['c_ovl'] = ((c >= 4 * n - 1) & (c <= 4 * n + 3) & (c < Nc)).astype(np.float32).astype(bf)
    cols = np.zeros((128, 4), np.float32)
    cols[:64, 0] = -1e30
    cols[64:, 0] = 1e9
    cols[:64, 1] = 1e9
    cols[64:, 1] = -1e30
    cols[:64, 2] = NEGM
    cols[64:, 2] = 0.0
    d['c_cols'] = cols
    return d


_CACHE = {}


def kernel(x, w_in, b_if, conv_qk, m_norm, moba_qk_norm, nsa_q_norm, nsa_k_norm,
           cmp_pe, cmp_w1, cmp_w2, w_out, norm_mix, norm_ffn, w_ff1, w_ff2):
    x = np.asarray(x, dtype=np.float32)
    Bsz, S, _ = x.shape
    depth = int(np.asarray(w_in).shape[0])
    n_cores = 8
    shared = dict(w_in=w_in, b_if=b_if, conv_qk=conv_qk, m_norm=m_norm, moba_qk_norm=moba_qk_norm,
                  nsa_q_norm=nsa_q_norm, nsa_k_norm=nsa_k_norm, cmp_pe=cmp_pe, cmp_w1=cmp_w1, cmp_w2=cmp_w2,
                  w_out=w_out, norm_mix=norm_mix, norm_ffn=norm_ffn, w_ff1=w_ff1, w_ff2=w_ff2)
    shared = {k: np.ascontiguousarray(np.asarray(v, dtype=np.float32)) for k, v in shared.items()}
    shared.update(make_consts(S))
    B = Builder(S, depth)
    nc = B.build_all()
    in_maps = []
    for c in range(n_cores):
        m = dict(shared)
        m['x'] = np.ascontiguousarray(x[c % Bsz])
        in_maps.append(m)
    res = run_bass_kernel_spmd(nc, in_maps, core_ids=list(range(n_cores)))
    out = np.stack([np.asarray(res.results[b]["y"], dtype=np.float32) for b in range(Bsz)], axis=0)
    return out


class BG:
    def __init__(self):
        self.gens = []
        self.acc = 0.0

    def add(self, g):
        self.gens.append(g)

    def step(self, rate=1.0):
        self.acc += rate
        while self.acc >= 1.0:
            self.acc -= 1.0
            for g in list(self.gens):
                try:
                    next(g)
                except StopIteration:
                    self.gens.remove(g)

    def drain(self, g=None):
        if g is None:
            while self.gens:
                self.step(1.0)
        else:
            while g in self.gens:
                try:
                    next(g)
                except StopIteration:
                    self.gens.remove(g)


def _phase2(self, l):
    S, P, nc = self.S, self.P, self.nc
    NCH = S // 64
    NT = S // 128
    NQC = S // 512
    NB = S // 256
    Nc = S // 16 - 1
    NCT = max(1, S // 2048)
    LNSC = math.log(128.0 ** -0.5)
    sb, ps = self.sb, self.ps
    allt = list(range(NT))
    with ExitStack() as st:
        identb = sb(st, "identb2", [128, 128], BF16)
        identf = sb(st, "identf2", [128, 128], F32)
        ut = sb(st, "ut2", [128, 128], F32)
        tri = sb(st, "tri2", [128, 128], F32)
        zl = sb(st, "zl2", [1, 128], BF16)
        zr = sb(st, "zr2", [1, 512], BF16)
        P.dma('sp', identb[:], self.c_identb, writes=['identb'])
        P.dma('sp', identf[:], self.c_identf, writes=['identf'])
        P.dma('sp', ut[:], self.c_ut, writes=['ut'])
        P.dma('sp', tri[:], self.c_tri, writes=['tri'])
        P.op('pool', lambda e: e.memset(zl[:], 0.0), writes=['zl'])
        P.op('pool', lambda e: e.memset(zr[:], 0.0), writes=['zr'])
        uT = sb(st, "uT_m", [64, 4, NCH], F32)
        u2T = sb(st, "u2T_m", [64, 4, NCH], F32)
        flT = sb(st, "flT_m", [64, 4, NCH], F32)
        decB = sb(st, "decB", [128, 4, NCH], F32)
        with ExitStack() as s2:
            li = sb(s2, "li", [NCH, 4, 64], F32)
            lf = sb(s2, "lf", [NCH, 4, 64], F32)
            ones = sb(s2, "ones", [NCH, 64], F32)
            Fin = sb(s2, "Fin", [NCH, 4, 64], F32)
            Ft = sb(s2, "Ft", [NCH, 4, 64], F32)
            a_ = sb(s2, "a_", [NCH, 4, 64], F32)
            Ain = sb(s2, "Ain", [NCH, 4, 64], F32)
            tot = sb(s2, "tot", [NCH, 4], F32)
            cmax = sb(s2, "cmax", [NCH, 4], F32)
            cmT = sb(s2, "cmT", [4, NCH], F32)
            ET = sb(s2, "ET", [4, NCH], F32)
            ETn = sb(s2, "ETn", [4, NCH], F32)
            Ec = sb(s2, "Ec", [NCH, 4], F32)
            Enc = sb(s2, "Enc", [NCH, 4], F32)
            tmp = sb(s2, "tmpm", [NCH, 4, 64], F32)
            uu = sb(s2, "uu", [NCH, 4, 64], F32)
            uu2 = sb(s2, "uu2", [NCH, 4, 64], F32)
            fl = sb(s2, "fl", [NCH, 4, 64], F32)
            dec = sb(s2, "dec", [NCH, 4], F32)
            decrep = sb(s2, "decrep", [NCH, 4, 128], F32)
            pa = ps(s2, "pa", [128, 512], F32)
            pb = ps(s2, "pb", [128, 512], F32)
            P.dma('sp', li[:], self.gi_d.rearrange("h (c j) -> c h j", j=64),
                  reads=[('gi_d', b) for b in range(S // 512)], writes=['li'])
            P.dma('sp', lf[:], self.gf_d.rearrange("h (c j) -> c h j", j=64),
                  reads=[('gf_d', b) for b in range(S // 512)], writes=['lf'])
            P.op('pool', lambda e: e.memset(ones[:], 1.0), writes=['ones'])
            for hh in range(4):
                P.op('dve', lambda e: e.tensor_tensor_scan(out=Fin[:, hh, :], data0=ones[:], data1=lf[:, hh, :],
                                                           initial=0.0, op0=ALU.mult, op1=ALU.add),
                     reads=['ones', 'lf'], writes=['Fin'])
            P.op('dve', lambda e: e.tensor_copy(out=tot[:], in_=Fin[:, :, 63]), reads=['Fin'], writes=['tot'])
            P.op('pe', lambda e: e.matmul(pa[0:NCH, 0:4], lhsT=ut[0:NCH, 0:NCH], rhs=tot[:], start=True, stop=True),
                 reads=['ut', 'tot'], writes=['pa'])
            P.op('dve', lambda e: e.tensor_tensor(out=Ft[:], in0=Fin[:],
                                                  in1=pa[0:NCH, 0:4].unsqueeze(2).broadcast_to([NCH, 4, 64]), op=ALU.add),
                 reads=['Fin', 'pa'], writes=['Ft'])
            P.op('dve', lambda e: e.tensor_tensor(out=a_[:], in0=li[:], in1=Ft[:], op=ALU.subtract),
                 reads=['li', 'Ft'], writes=['a_'])
            for hh in range(4):
                P.op('dve', lambda e: e.tensor_tensor_scan(out=Ain[:, hh, :], data0=a_[:, hh, :], data1=a_[:, hh, :],
                                                           initial=-1e30, op0=ALU.max, op1=ALU.max),
                     reads=['a_'], writes=['Ain'])
            P.op('dve', lambda e: e.tensor_copy(out=cmax[:], in_=Ain[:, :, 63]), reads=['Ain'], writes=['cmax'])
            P.op('pe', lambda e: e.transpose(out=pb[0:4, 0:NCH], in_=cmax[:], identity=identf[0:NCH, 0:NCH]),
                 reads=['cmax', 'identf'], writes=['pb'])
            P.op('dve', lambda e: e.tensor_copy(out=cmT[:], in_=pb[0:4, 0:NCH]), reads=['pb'], writes=['cmT'])
            P.op('dve', lambda e: e.tensor_tensor_scan(out=ET[:], data0=cmT[:], data1=cmT[:], initial=0.0,
                                                       op0=ALU.max, op1=ALU.max), reads=['cmT'], writes=['ET'])
            if NCH > 1:
                P.op('dve', lambda e: e.tensor_copy(out=ETn[:, 0:NCH - 1], in_=ET[:, 1:NCH]), reads=['ET'], writes=['ETn'])
            P.op('dve', lambda e: e.tensor_copy(out=ETn[:, NCH - 1:NCH], in_=ET[:, NCH - 1:NCH]), reads=['ET'], writes=['ETn'])
            P.op('pe', lambda e: e.transpose(out=pa[0:NCH, 0:4], in_=ET[:], identity=identf[0:4, 0:4]),
                 reads=['ET', 'identf'], writes=['pa'])
            P.op('dve', lambda e: e.tensor_copy(out=Ec[:], in_=pa[0:NCH, 0:4]), reads=['pa'], writes=['Ec'])
            P.op('pe', lambda e: e.transpose(out=pb[0:NCH, 0:4], in_=ETn[:], identity=identf[0:4, 0:4]),
                 reads=['ETn', 'identf'], writes=['pb'])
            P.op('dve', lambda e: e.tensor_copy(out=Enc[:], in_=pb[0:NCH, 0:4]), reads=['pb'], writes=['Enc'])
            Eb = Ec[:].unsqueeze(2).broadcast_to([NCH, 4, 64])
            Enb = Enc[:].unsqueeze(2).broadcast_to([NCH, 4, 64])
            P.op('dve', lambda e: e.tensor_tensor(out=tmp[:], in0=a_[:], in1=Eb, op=ALU.subtract),
                 reads=['a_', 'Ec'], writes=['tmp'])
            P.op('act', lambda e: e.activation(out=uu[:], in_=tmp[:], func=AF.Exp, bias=LNSC), reads=['tmp'], writes=['uu'])
            P.op('dve', lambda e: e.tensor_tensor(out=tmp[:], in0=a_[:], in1=Enb, op=ALU.subtract),
                 reads=['a_', 'Enc'], writes=['tmp'])
            P.op('act', lambda e: e.activation(out=uu2[:], in_=tmp[:], func=AF.Exp, bias=LNSC), reads=['tmp'], writes=['uu2'])
            P.op('dve', lambda e: e.tensor_tensor(out=tmp[:], in0=Ft[:], in1=Eb, op=ALU.add),
                 reads=['Ft', 'Ec'], writes=['tmp'])
            P.op('act', lambda e: e.activation(out=fl[:], in_=tmp[:], func=AF.Exp, scale=-1.0), reads=['tmp'], writes=['fl'])
            P.op('dve', lambda e: e.tensor_tensor(out=dec[:], in0=Ec[:], in1=Enc[:], op=ALU.subtract),
                 reads=['Ec', 'Enc'], writes=['dec'])
            P.op('act', lambda e: e.activation(out=dec[:], in_=dec[:], func=AF.Exp), reads=['dec'], writes=['dec'])
            P.op('dve', lambda e: e.tensor_copy(out=decrep[:], in_=dec[:].unsqueeze(2).broadcast_to([NCH, 4, 128])),
                 reads=['dec'], writes=['decrep'])
            for src, dst, nm in ((uu, uT, 'uT'), (uu2, u2T, 'u2T'), (fl, flT, 'flT')):
                for hh in range(4):
                    P.op('pe', lambda e: e.transpose(out=pa[0:64, hh * 128:hh * 128 + NCH], in_=src[:, hh, :],
                                                     identity=identf[0:NCH, 0:NCH]),
                         reads=['uu', 'uu2', 'fl', 'identf'], writes=['pa'])
                P.op('dve', lambda e: e.tensor_copy(out=dst[:], in_=pa[0:64, :].rearrange("p (h c) -> p h c", h=4)[:, :, 0:NCH]),
                     reads=['pa'], writes=[nm])
            for hh in range(4):
                P.op('pe', lambda e: e.matmul(pb[:, hh * 128:hh * 128 + NCH], lhsT=decrep[:, hh, :],
                                              rhs=identf[0:NCH, 0:NCH], start=True, stop=True),
                     reads=['decrep', 'identf'], writes=['pb'])
            P.op('dve', lambda e: e.tensor_copy(out=decB[:], in_=pb[:].rearrange("p (h c) -> p h c", h=4)[:, :, 0:NCH]),
                 reads=['pb'], writes=['decB'])
            self.barrier()

        pSr = [ps(st, "pSr", [128, 512], F32) for _ in range(3)]
        pMi = ps(st, "pMi", [128, 512], F32)
        pMb = pMi[:, 128:192].bitcast(BF16)
        pMb2 = pMi[:, 192:256].bitcast(BF16)
        pM = pMi[:, 256:288]
        bg = BG()

        sm = ExitStack()
        st = sm
        GC = 4
        NG = S // (64 * GC)
        TG = 64 * GC
        qg = [sb(st, "qg", [128, 4, TG], BF16) for _ in range(2)]
        kg = [sb(st, "kg", [128, 4, TG], BF16) for _ in range(2)]
        vg = [sb(st, "vg", [64, GC, 4, 129], BF16) for _ in range(2)]
        ogg = [sb(st, "ogg", [64, GC, 512], BF16) for _ in range(2)]
        ym = [sb(st, "ym", [64, GC, 512], BF16) for _ in range(2)]
        G = [sb(st, "G", [128, 129], F32) for _ in range(4)]
        Gb = [sb(st, "Gb", [128, 129], BF16) for _ in range(4)]
        ku2 = [sb(st, "ku2", [64, 128], BF16) for _ in range(2)]
        Sm = [sb(st, "Sm", [64, 64], BF16) for _ in range(2)]
        junkm = sb(st, "junkm", [64, 128], BF16)
        scm = [sb(st, "scm", [64, 8], F32) for _ in range(2)]
        mlA = [ps(st, "mlA", [128, 512], F32) for _ in range(2)]

        def gen_ml():
            for i in range(2):
                P.op('pool', lambda e: e.memset(vg[i][:], 1.0), writes=[('vg', i)])
            for hh in range(4):
                P.op('pool', lambda e: e.memset(G[hh][:], 0.0), writes=[('G', hh)])
                P.op('pool', lambda e: e.memset(Gb[hh][:], 0.0), writes=[('Gb', hh)])
            yield

            def load_group(g):
                i = g % 2
                tk = slice(g * TG, (g + 1) * TG)
                bl = sorted(set([(g * TG) // 512, ((g + 1) * TG - 1) // 512]))
                jl = list(range((g * TG) // 128, ((g + 1) * TG + 127) // 128))
                P.dma('sp', qg[i][:], self.qkT[0:512, tk].rearrange("(h p) t -> p h t", p=128),
                      reads=[('qkT', ft, b) for ft in range(4) for b in bl], writes=[('qg', i)])
                P.dma('sp', kg[i][:], self.qkT[512:1024, tk].rearrange("(h p) t -> p h t", p=128),
                      reads=[('qkT', ft, b) for ft in range(4, 8) for b in bl], writes=[('kg', i)])
                for hh in range(4):
                    P.dma('sp', vg[i][:, :, hh, 0:128],
                          self.mv_d[tk, hh * 128:(hh + 1) * 128].rearrange("(c s) e -> s c e", s=64),
                          reads=[('mv_d', j) for j in jl], writes=[('vg', i)])
                P.dma('sp', ogg[i][:], self.og_d[tk, :].rearrange("(c s) e -> s c e", s=64),
                      reads=[('og_d', j) for j in jl], writes=[('ogg', i)])
            load_group(0)
            it = 0
            for g in range(NG):
                if g + 1 < NG:
                    load_group(g + 1)
                gi_ = g % 2
                for cl in range(GC):
                    c = g * GC + cl
                    for hh in range(4):
                        k_ = kg[gi_][:, hh, cl * 64:(cl + 1) * 64]
                        q_ = qg[gi_][:, hh, cl * 64:(cl + 1) * 64]
                        v_ = vg[gi_][:, cl, hh, :]
                        i2 = it % 2
                        it += 1
                        A = mlA[i2]
                        pS_ = A[0:64, 0:64]
                        pO_ = A[0:64, 64:193]
                        pG_ = A[:, 256:385]
                        pkT_ = A[0:64, 448:512].bitcast(BF16)
                        P.op('pe', lambda e: e.transpose(out=pkT_, in_=k_, identity=identb[:]),
                             reads=[('kg', gi_), 'identb'], writes=[('mlA', i2)])
                        P.op('pe', lambda e: e.matmul(pS_, lhsT=k_, rhs=q_, start=True, stop=True),
                             reads=[('kg', gi_), ('qg', gi_)], writes=[('mlA', i2)])
                        yield
                        P.op('act', lambda e: e.activation(out=ku2[i2][:], in_=pkT_, func=AF.Copy,
                                                           scale=u2T[:, hh, c:c + 1]),
                             reads=[('mlA', i2), 'u2T'], writes=[('ku2', i2)])
                        P.op('dve', lambda e: e.scalar_tensor_tensor(out=Sm[i2][:], in0=pS_,
                                                                     scalar=uT[:, hh, c:c + 1], in1=tri[0:64, 0:64],
                                                                     op0=ALU.mult, op1=ALU.mult),
                             reads=[('mlA', i2), 'uT', 'tri'], writes=[('Sm', i2)])
                        yield
                        P.op('pe', lambda e: e.matmul(pO_, lhsT=Sm[i2][:], rhs=v_, start=True, stop=False),
                             reads=[('Sm', i2), ('vg', gi_)], writes=[('mlA', i2)])
                        P.op('pe', lambda e: e.matmul(pO_, lhsT=q_, rhs=Gb[hh][:], start=False, stop=True),
                             reads=[('qg', gi_), ('Gb', hh)], writes=[('mlA', i2)])
                        P.op('pe', lambda e: e.matmul(pG_, lhsT=ku2[i2][:], rhs=v_, start=True, stop=True),
                             reads=[('ku2', i2), ('vg', gi_)], writes=[('mlA', i2)])
                        yield
                        P.op('dve', lambda e: e.scalar_tensor_tensor(out=G[hh][:], in0=G[hh][:], scalar=decB[:, hh, c:c + 1],
                                                                     in1=pG_, op0=ALU.mult, op1=ALU.add),
                             reads=[('G', hh), 'decB', ('mlA', i2)], writes=[('G', hh)])
                        P.op('pool', lambda e: e.tensor_copy(out=Gb[hh][:], in_=G[hh][:]), reads=[('G', hh)], writes=[('Gb', hh)])
                        s_ = scm[i2]
                        sk = ('scm', i2)
                        P.op('act', lambda e: e.activation(out=junkm[:], in_=pO_[:, 0:128], func=AF.Square,
                                                           scale=128.0 ** -0.5, accum_out=s_[:, 0:1]),
                             reads=[('mlA', i2)], writes=['junkm', sk])
                        yield
                        P.op('dve', lambda e: e.tensor_scalar(out=s_[:, 6:7], in0=pO_[:, 128:129], scalar1=-1.0,
                                                              scalar2=flT[:, hh, c:c + 1], op0=ALU.mult, op1=ALU.max),
                             reads=[('mlA', i2), 'flT'], writes=[sk])
                        P.op('dve', lambda e: e.tensor_tensor(out=s_[:, 1:2], in0=s_[:, 6:7], in1=pO_[:, 128:129], op=ALU.max),
                             reads=[('mlA', i2), sk], writes=[sk])
                        P.op('dve', lambda e: e.tensor_tensor(out=s_[:, 2:3], in0=s_[:, 1:2], in1=s_[:, 1:2], op=ALU.mult),
                             reads=[sk], writes=[sk])
                        P.op('dve', lambda e: e.scalar_tensor_tensor(out=s_[:, 3:4], in0=s_[:, 2:3], scalar=EPS, in1=s_[:, 0:1],
                                                                     op0=ALU.mult, op1=ALU.add), reads=[sk], writes=[sk])
                        yield
                        P.op('act', lambda e: e.activation(out=s_[:, 4:5], in_=s_[:, 3:4], func=AF.Ln), reads=[sk], writes=[sk])
                        P.op('act', lambda e: e.activation(out=s_[:, 5:6], in_=s_[:, 4:5], func=AF.Exp, scale=-0.5), reads=[sk], writes=[sk])
                        P.op('dve', lambda e: e.scalar_tensor_tensor(out=ym[gi_][:, cl, hh * 128:(hh + 1) * 128],
                                                                     in0=pO_[:, 0:128], scalar=s_[:, 5:6],
                                                                     in1=ogg[gi_][:, cl, hh * 128:(hh + 1) * 128],
                                                                     op0=ALU.mult, op1=ALU.mult),
                             reads=[('mlA', i2), sk, ('ogg', gi_)], writes=[('ym', gi_)])
                        yield
                P.dma('sp', self.mix_d[g * TG:(g + 1) * TG, 0:512].rearrange("(c s) e -> s c e", s=64), ym[gi_][:],
                      reads=[('ym', gi_)], writes=[('mixm', g)])
        ML_YIELDS = NCH * 4 * 6 + 1
        g_ml = gen_ml()
        if not os.environ.get("SKIP_ML"):
            bg.add(g_ml)

        with sm:
            tm4 = sb(sm, "tm4", [128, 4, 512], BF16)
            P.dma('sp', tm4[:], self.c_tm4, writes=['tm4'])
            negB = self._negB(sm, "negBm", self.moba_qk_norm[l, 0], self.moba_qk_norm[l, 1])
            KX = [sb(sm, "KX", [96, S], BF16) for _ in range(2)]
            QXA = [sb(sm, "QXA", [96, S], BF16) for _ in range(2)]
            VX = [sb(sm, "VX", [128, NT, 65], BF16) for _ in range(2)]
            kmean = sb(sm, "kmean", [64, 32], F32)
            kmb = [sb(sm, "kmb", [64, 32], BF16) for _ in range(2)]
            gsb = [sb(sm, "gsb_b", [128, 32], F32) for _ in range(2)]
            m8 = [sb(sm, "m8", [128, 8], F32) for _ in range(2)]
            sel = [sb(sm, "sel_b", [128, 32], F32) for _ in range(2)]
            MBw = [sb(sm, "MBw", [128, 128], BF16) for _ in range(2)]
            PT = [sb(sm, "PT", [128, 512], BF16) for _ in range(4)]
            rz = sb(sm, "rz_b", [128, 4], F32)
            yb = [sb(sm, "yb", [128, 4, 64], BF16) for _ in range(2)]
            pO = [ps(sm, "pO_b", [128, 512], F32) for _ in range(1)] * 2
            if os.environ.get("NO_BITCAST"):
                pMb = ps(sm, "pMbx", [128, 128], BF16)[:]
            for i in range(2):
                P.dma('sp', KX[i][64:96, :], self.c_ind32, writes=[('KXi', i)])
                P.op('pool', lambda e: e.memset(VX[i][:], 1.0), writes=[('VX', i)])
                P.op('pool', lambda e: e.memset(MBw[i][:], 0.0), writes=[('MBw', i)])
            P.op('pool', lambda e: e.memset(kmean[:], 0.0), writes=['kmean'])

            def gen_prep(h):
                i = h % 2
                P.dma('sp', KX[i][0:64, :], self.bkT[h * 64:(h + 1) * 64, :], reads=[('bkT', j) for j in allt], writes=[('KX', i)])
                self.dma_mid(VX[i][:, :, 0:64], self.bv_d[:, h * 64:(h + 1) * 64].rearrange("(kt p) d -> p kt d", p=128), NT, 8,
                             reads=[('bv_d', j) for j in allt], writes=[('VX', i)])
                P.dma('sp', QXA[i][0:64, :], self.bqT[h * 64:(h + 1) * 64, :], reads=[('bqT', j) for j in allt], writes=[('QXAq', i)])
                yield
                P.op('dve', lambda e: e.tensor_reduce(out=kmean[:, 0:NB], in_=KX[i][0:64, :].rearrange("p (n k) -> p n k", k=256),
                                                      axis=AX.X, op=ALU.add), reads=[('KX', i)], writes=['kmean'])
                P.op('dve', lambda e: e.tensor_scalar(out=kmb[i][:], in0=kmean[:], scalar1=1.0 / 256, scalar2=None, op0=ALU.mult),
                     reads=['kmean'], writes=[('kmb', i)])
                yield
                for jt in range(NT if os.environ.get("BIS", "0") != "2" else 0):
                    own = jt // 2
                    b_ = jt % 2
                    P.op('pe', lambda e: e.matmul(pM, lhsT=QXA[i][0:64, jt * 128:(jt + 1) * 128], rhs=kmb[i][:],
                                                  start=True, stop=True), reads=[('QXAq', i), ('kmb', i)], writes=['pMi'])
                    P.op('pool', lambda e: e.memset(gsb[b_][:], -1e30), writes=[('gsb', b_)])
                    if own > 0:
                        P.op('dve', lambda e: e.tensor_copy(out=gsb[b_][:, 0:own], in_=pM[:, 0:own]), reads=['pMi'], writes=[('gsb', b_)])
                    yield
                    P.op('dve', lambda e: e.max(out=m8[b_][:], in_=gsb[b_][:]), reads=[('gsb', b_)], writes=[('m8', b_)])
                    P.op('dve', lambda e: e.tensor_scalar(out=sel[b_][:], in0=gsb[b_][:], scalar1=m8[b_][:, 2:3], scalar2=None,
                                                          op0=ALU.is_ge), reads=[('gsb', b_), ('m8', b_)], writes=[('sel', b_)])
                    P.op('dve', lambda e: e.tensor_scalar(out=MBw[b_][:, 64:96], in0=sel[b_][:], scalar1=-NEGM, scalar2=NEGM,
                                                          op0=ALU.mult, op1=ALU.add), reads=[('sel', b_)], writes=[('MBw', b_)])
                    P.op('dve', lambda e: e.memset(MBw[b_][:, 64 + own:65 + own], 0.0), writes=[('MBw', b_)])
                    if own + 1 < 32:
                        P.op('dve', lambda e: e.memset(MBw[b_][:, 65 + own:96], NEGM), writes=[('MBw', b_)])
                    yield
                    P.op('pe', lambda e: e.transpose(out=pMb, in_=MBw[b_][:], identity=identb[:]),
                         reads=[('MBw', b_), 'identb'], writes=['pMi'])
                    P.op('act', lambda e: e.copy(out=QXA[i][64:96, jt * 128:(jt + 1) * 128], in_=pMb[64:96, :]),
                         reads=['pMi'], writes=[('QXAm', i, jt)])
                    yield
            PREP_YIELDS = 3 * NT + 2
            main_iters_h = sum(4 * qc + 4 for qc in range(NQC))
            ml_rate = ML_YIELDS / float(4 * main_iters_h) * 1.15
            prep_rate = PREP_YIELDS / float(main_iters_h) * 1.3
            g0 = gen_prep(0)
            for _ in g0:
                bg.step(1.0)
            si = 0
            pi = 0
            for h in range(4):
                i = h % 2
                gp = None
                pacc = 0.0
                if h + 1 < 4:
                    gp = gen_prep(h + 1)
                    if os.environ.get("NOINT"):
                        for _ in gp:
                            pass
                        gp = None
                for qc in range(min(NQC, int(os.environ.get("QCMAX", "99"))) if (os.environ.get("BIS", "0") == "0" and h < int(os.environ.get("HMAX", "9"))) else 0):
                    q0 = qc * 512
                    Q_ = QXA[i][:, q0:q0 + 512]
                    qkeys = [('QXAq', i)] + [('QXAm', i, 4 * qc + j) for j in range(4)]
                    po = pO[pi % 2]
                    pok = ('pO', pi % 2)
                    pi += 1
                    P.op('pe', lambda e: e.matmul(po[:, 0:260], lhsT=zl[:], rhs=zr[:, 0:260], start=True, stop=True,
                                                  skip_group_check=True), reads=['zl', 'zr'], writes=[pok])
                    nkt = 4 * qc + 4
                    base = si

                    def qk(kt):
                        p_ = pSr[(base + kt) % 3]
                        P.op('pe', lambda e: e.matmul(p_[:], lhsT=KX[i][:, kt * 128:(kt + 1) * 128], rhs=Q_, start=True, stop=True),
                             reads=[('KX', i), ('KXi', i)] + qkeys, writes=[('pSr', (base + kt) % 3)])
                    LA = int(os.environ.get("LA", "2"))
                    for kt in range(min(LA, nkt)):
                        qk(kt)
                    for kt in range(nkt):
                        if kt + LA < nkt:
                            qk(kt + LA)
                        p_ = pSr[(base + kt) % 3]
                        pk = ('pSr', (base + kt) % 3)
                        t_ = PT[(base + kt) % 4]
                        tk = ('PT', (base + kt) % 4)
                        P.op('act', lambda e: e.activation(out=t_[:], in_=p_[:], func=AF.Exp, bias=negB[:, 0:1], scale=0.125),
                             reads=[pk, 'negBm'], writes=[tk])
                        off = kt - 4 * qc
                        if off >= 0:
                            P.op('pool', lambda e: e.tensor_tensor(out=t_[:], in0=t_[:], in1=tm4[:, off, :], op=ALU.mult),
                                 reads=[tk, 'tm4'], writes=[tk])
                        for j in range(4):
                            if kt > 4 * qc + j:
                                continue
                            P.op('pe', lambda e: e.matmul(po[:, j * 65:(j + 1) * 65], lhsT=t_[:, j * 128:(j + 1) * 128],
                                                          rhs=VX[i][:, kt, :], start=False, stop=(kt == 4 * qc + j),
                                                          skip_group_check=True), reads=[tk, ('VX', i)], writes=[pok])
                        bg.step(ml_rate)
                        if gp is not None:
                            pacc += prep_rate
                            while pacc >= 1.0:
                                pacc -= 1.0
                                try:
                                    next(gp)
                                except StopIteration:
                                    gp = None
                                    break
                    si += nkt
                    y_ = yb[qc % 2]
                    yk = ('yb', qc % 2)
                    pov = po[:, 0:260].rearrange("p (j d) -> p j d", d=65)
                    P.op('dve', lambda e: e.reciprocal(out=rz[:], in_=pov[:, :, 64]), reads=[pok], writes=['rz'])
                    P.op('dve', lambda e: e.tensor_tensor(out=y_[:], in0=pov[:, :, 0:64],
                                                          in1=rz[:].unsqueeze(2).broadcast_to([128, 4, 64]), op=ALU.mult),
                         reads=[pok, 'rz'], writes=[yk])
                    P.dma('sp', self.mix_d[q0:q0 + 512, 512 + h * 64:512 + (h + 1) * 64].rearrange("(j p) d -> p j d", p=128),
                          y_[:], reads=[yk], writes=[('mixb', h, qc)])
                if gp is not None:
                    for _ in gp:
                        bg.step(1.0)
            bg.drain()
            self.barrier()
        if not os.environ.get("SKIP_NSA"):
            self._phase2_nsa_pipe(l, pSr, pMi, identb, zl, zr)
        self.barrier()


Builder.phase2 = _phase2


def _phase2_nsa_pipe(self, l, pSr, pMi, identb, zl, zr, bgen=None, bg_yields=0):
    S, P, nc = self.S, self.P, self.nc
    NT = S // 128
    Nc = S // 16 - 1
    NCT = max(1, S // 2048)
    sb, ps = self.sb, self.ps
    allt = list(range(NT))
    pMb = pMi[:, 128:192].bitcast(BF16)
    pMb2 = pMi[:, 192:256].bitcast(BF16)
    pM = pMi[:, 256:320]
    with ExitStack() as st:
        trib = sb(st, "trib", [128, 128], BF16)
        triw = sb(st, "triw", [128, 128], BF16)
        cmask = sb(st, "cmask", [128, 17, 128], BF16)
        OVL = sb(st, "OVL", [128, NCT, 128], BF16)
        cols = sb(st, "cols", [128, 4], F32)
        NG = sb(st, "NG", [128, NT, 12], F32)
        P.dma('sp', trib[:], self.c_trib, writes=['trib'])
        P.dma('sp', triw[:], self.c_triw, writes=['triw'])
        P.dma('sp', cmask[:], self.c_cmask, writes=['cmask'])
        P.dma('sp', OVL[:], self.c_ovl.rearrange("(ct p) n -> p ct n", p=128)[:, 0:NCT, :], writes=['OVL'])
        P.dma('sp', cols[:], self.c_cols, writes=['cols'])
        self.dma_mid(NG[:], self.ng_d.rearrange("(j p) g -> p j g", p=128), NT, 8, reads=[('ng_d', j) for j in allt], writes=['NG'])
        negBc = self._negB(st, "negBc", self.nsa_q_norm[l], self.nsa_k_norm[l, 0])
        negBs = self._negB(st, "negBs", self.nsa_q_norm[l], self.nsa_k_norm[l, 1])
        negBw = self._negB(st, "negBw", self.nsa_q_norm[l], self.nsa_k_norm[l, 2])
        KSX = sb(st, "KSX", [128, S], BF16)
        KWX = sb(st, "KWX", [64, S], BF16)
        VSX = sb(st, "VSX", [128, NT, 65], BF16)
        VWX = sb(st, "VWX", [128, NT, 65], BF16)
        KcT = sb(st, "KcT", [64, NCT * 128], BF16)
        VCX = sb(st, "VCX", [128, NCT, 65], BF16)
        P.dma('sp', KSX[0:64, :], self.ksT, reads=[('ksT', j) for j in allt], writes=['KSX'])
        P.dma('sp', KSX[64:128, :], self.c_ind64, writes=['KSXi'])
        P.dma('sp', KWX[:], self.kwT, reads=[('kwT', j) for j in allt], writes=['KWX'])
        for V_, src, nm in ((VSX, self.vs_d, 'vs_d'), (VWX, self.vw_d, 'vw_d')):
            P.op('pool', lambda e: e.memset(V_[:], 1.0), writes=[nm + 'X'])
            self.dma_mid(V_[:, :, 0:64], src.rearrange("(kt p) d -> p kt d", p=128), NT, 8,
                         reads=[(nm, j) for j in allt], writes=[nm + 'X'])
        P.op('pool', lambda e: e.memset(VCX[:], 1.0), writes=['VCX'])
        pSc = ps(st, "pSc", [128, 512], F32) if bgen is None else pMi
        pSc_key = 'pSc' if bgen is None else 'pMi'
        pOcU = ps(st, "pOcU", [128, 512], F32)
        pOs = ps(st, "pOs", [128, 512], F32)
        pOw = ps(st, "pOw", [128, 512], F32)
        with ExitStack() as s2:
            KCV = sb(s2, "KCV", [128, S], BF16)
            W1s = sb(s2, "W1s", [128, 32, 128], F32)
            W1 = sb(s2, "W1", [128, 32, 128], BF16)
            pes = sb(s2, "pes", [32, 128], F32)
            peb = sb(s2, "peb", [32, 128], BF16)
            peT = sb(s2, "peT", [128, 32], BF16)
            w2s = sb(s2, "w2s", [128, 2, 64], F32)
            w2 = sb(s2, "w2", [128, 2, 64], BF16)
            gk0 = sb(s2, "gk0", [128, 64], F32)
            bias = sb(s2, "bias_c", [128, 2], F32)
            hidb = sb(s2, "hidb", [128, NCT * 128], BF16)
            kcn = sb(s2, "kcn", [128, 64], BF16)
            junk = sb(s2, "junk_c", [128, 64], F32)
            ssc = sb(s2, "ssc", [128, 2], F32)
            P.dma('sp', KCV[:], self.kcvcT, reads=[('kcvcT', b) for b in range(S // 512)], writes=['KCV'])
            for br in range(2):
                P.dma('sp', W1s[64 * br:64 * br + 64], self.cmp_w1[l, br].rearrange("(r d) j -> d r j", d=64), writes=['W1s'])
                P.dma('sp', w2s[:, br, :], self.cmp_w2[l, br], writes=['w2s'])
                P.dma('sp', pes[:, 64 * br:64 * br + 64], self.cmp_pe[l, br], writes=['pes'])
            P.dma('sp', gk0[:], self.nsa_k_norm[l, 0].partition_broadcast(128), writes=['gk0'])
            P.op('dve', lambda e: e.tensor_copy(out=W1[:], in_=W1s[:]), reads=['W1s'], writes=['W1'])
            P.op('dve', lambda e: e.tensor_copy(out=peb[:], in_=pes[:]), reads=['pes'], writes=['peb'])
            P.op('pe', lambda e: e.transpose(out=pMb[:, 0:32], in_=peb[:], identity=identb[0:32, 0:32]),
                 reads=['peb', 'identb'], writes=['pMi'])
            P.op('dve', lambda e: e.tensor_copy(out=peT[:], in_=pMb[:, 0:32]), reads=['pMi'], writes=['peT'])
            P.op('dve', lambda e: e.tensor_copy(out=w2[:], in_=w2s[:]), reads=['w2s'], writes=['w2'])
            P.op('pool', lambda e: e.memset(hidb[:], 0.0), writes=['hidb'])
            for br in range(2):
                rows = slice(64 * br, 64 * br + 64)
                kview = KCV[rows, :].rearrange("p (c s) -> p c s", s=16)
                for r in range(32):
                    P.op('pe', lambda e: e.matmul(pM[:, 0:1], lhsT=W1[rows, r, :], rhs=peT[rows, r:r + 1],
                                                  start=(r == 0), stop=(r == 31)), reads=['W1', 'peT'], writes=['pMi'])
                P.op('dve', lambda e: e.tensor_copy(out=bias[:, br:br + 1], in_=pM[:, 0:1]), reads=['pMi'], writes=['bias'])
                for r in range(32):
                    rhs = kview[:, 0:Nc, r] if r < 16 else kview[:, 1:Nc + 1, r - 16]
                    P.op('pe', lambda e: e.matmul(pSc[:, 0:Nc], lhsT=W1[rows, r, :], rhs=rhs,
                                                  start=(r == 0), stop=(r == 31)), reads=['W1', 'KCV'], writes=[pSc_key])
                P.op('act', lambda e: e.activation(out=hidb[:, 0:Nc], in_=pSc[:, 0:Nc], func=AF.Silu, bias=bias[:, br:br + 1]),
                     reads=[pSc_key, 'bias'], writes=['hidb'])
                for ct in range(NCT):
                    P.op('pe', lambda e: e.matmul(pM[:, 0:64], lhsT=hidb[:, ct * 128:(ct + 1) * 128], rhs=w2[:, br, :],
                                                  start=True, stop=True), reads=['hidb', 'w2'], writes=['pMi'])
                    if br == 0:
                        P.op('act', lambda e: e.activation(out=junk[:], in_=pM[:, 0:64], func=AF.Square, scale=0.125,
                                                           accum_out=ssc[:, 0:1]), reads=['pMi'], writes=['junk_c', 'ssc'])
                        P.op('dve', lambda e: e.tensor_scalar(out=ssc[:, 0:1], in0=ssc[:, 0:1], scalar1=EPS, scalar2=None,
                                                              op0=ALU.add), reads=['ssc'], writes=['ssc'])
                        P.op('act', lambda e: e.activation(out=ssc[:, 0:1], in_=ssc[:, 0:1], func=AF.Sqrt), reads=['ssc'], writes=['ssc'])
                        P.op('dve', lambda e: e.reciprocal(out=ssc[:, 1:2], in_=ssc[:, 0:1]), reads=['ssc'], writes=['ssc'])
                        P.op('dve', lambda e: e.scalar_tensor_tensor(out=kcn[:], in0=pM[:, 0:64], scalar=ssc[:, 1:2], in1=gk0[:],
                                                                     op0=ALU.mult, op1=ALU.mult),
                             reads=['pMi', 'ssc', 'gk0'], writes=['kcn'])
                        P.op('pe', lambda e: e.transpose(out=pMb[0:64, :], in_=kcn[:], identity=identb[:]),
                             reads=['kcn', 'identb'], writes=['pMi'])
                        P.op('act', lambda e: e.copy(out=KcT[:, ct * 128:(ct + 1) * 128], in_=pMb[0:64, :]),
                             reads=['pMi'], writes=['KcT'])
                    else:
                        P.op('act', lambda e: e.copy(out=VCX[:, ct, 0:64], in_=pM[:, 0:64]), reads=['pMi'], writes=['VCX'])
            self.barrier()
        QU = [sb(st, "QU", [64, 512], BF16) for _ in range(2)]
        QR0 = [sb(st, "QR0", [128, 512], BF16) for _ in range(2)]
        QR1 = [sb(st, "QR1", [128, 512], BF16) for _ in range(2)]
        PTc = [sb(st, "PTc", [128, 512], BF16) for _ in range(NCT)]
        PT = [sb(st, "PTn", [128, 512], BF16) for _ in range(4)]
        zc = sb(st, "zc", [128, 4], F32)
        rzc = sb(st, "rzc", [128, 4], F32)
        zz = sb(st, "zz", [128, 2, 4], F32)
        rzz = sb(st, "rzz", [128, 2, 4], F32)
        coef = sb(st, "coef", [128, 2, 4], F32)
        imp = sb(st, "imp", [128, 128], F32)
        work = sb(st, "work", [128, 128], F32)
        m8a = sb(st, "m8a", [128, 8], F32)
        m8b = sb(st, "m8b", [128, 8], F32)
        selm = sb(st, "selm", [128, 128], F32)
        MB = sb(st, "MB", [128, 128], BF16)
        MBs = sb(st, "MBs", [128, 128], BF16)
        yc = [sb(st, "yc", [128, 4, 64], F32) for _ in range(2)]
        yacc = sb(st, "yacc", [128, 4, 64], F32)
        yn = [sb(st, "yn", [128, 256], BF16) for _ in range(2)]

        def hb(ap):
            return ap.unsqueeze(1).broadcast_to([ap.shape[0], 4, 128])

        def h4(ap):
            return ap.rearrange("p (h t) -> p h t", h=4)

        def gen_prep(m):
            t0 = m * 128
            b2 = m % 2
            qu, qr0, qr1 = QU[b2], QR0[b2], QR1[b2]
            use_g1 = (2 * m + 1) >= 64
            P.dma('sp', h4(qu[:]), self.nquT[:, t0:t0 + 128].rearrange("(h d) t -> d h t", d=64),
                  reads=[('nquT', m)], writes=[('QU', b2)])
            P.dma('sp', h4(qr0[0:64, :]), self.nqrT[:, t0:t0 + 128].rearrange("(h d) t -> d h t", d=64),
                  reads=[('nqrT', m)], writes=[('QR0q', b2)])
            if use_g1:
                P.dma('sp', h4(qr1[0:64, :]), self.nqrT[:, t0:t0 + 128].rearrange("(h d) t -> d h t", d=64),
                      reads=[('nqrT', m)], writes=[('QR1q', b2)])
            yield
            ctn = min(NCT, (8 * m + 6) // 128 + 1)
            for ct in range(ctn):
                P.op('pe', lambda e: e.matmul(pSc[:], lhsT=KcT[:, ct * 128:(ct + 1) * 128], rhs=qu[:], start=True, stop=True),
                     reads=['KcT', ('QU', b2)], writes=[pSc_key])
                yield
                P.op('act', lambda e: e.activation(out=PTc[ct][:], in_=pSc[:], func=AF.Exp, bias=negBc[:, 0:1], scale=0.125),
                     reads=[pSc_key, 'negBc'], writes=[('PTc', ct)])
                yield
                r = m - 16 * ct
                if r <= 16:
                    P.op('pool', lambda e: e.tensor_tensor(out=h4(PTc[ct][:]), in0=h4(PTc[ct][:]), in1=hb(cmask[:, r, :]), op=ALU.mult),
                         reads=[('PTc', ct), 'cmask'], writes=[('PTc', ct)])
                    yield
                yield
            for h in range(4):
                for ct in range(ctn):
                    P.op('pe', lambda e: e.matmul(pOcU[:, h * 65:(h + 1) * 65], lhsT=PTc[ct][:, h * 128:(h + 1) * 128],
                                                  rhs=VCX[:, ct, :], start=(ct == 0), stop=(ct == ctn - 1)),
                         reads=[('PTc', ct), 'VCX'], writes=['pOcU'])
                    yield
                for ct in range(ctn):
                    P.op('pe', lambda e: e.matmul(pOcU[:, 260:388], lhsT=PTc[ct][:, h * 128:(h + 1) * 128],
                                                  rhs=OVL[:, ct, :], start=(ct == 0), stop=(ct == ctn - 1)),
                         reads=[('PTc', ct), 'OVL'], writes=['pOcU'])
                    yield
                P.op('dve', lambda e: e.tensor_scalar(out=zc[:, h:h + 1], in0=pOcU[:, h * 65 + 64:h * 65 + 65], scalar1=1e-30,
                                                      scalar2=None, op0=ALU.max), reads=['pOcU'], writes=['zc'])
                yield
                P.op('dve', lambda e: e.reciprocal(out=rzc[:, h:h + 1], in_=zc[:, h:h + 1]), reads=['zc'], writes=['rzc'])
                yield
                if h == 0:
                    P.op('dve', lambda e: e.tensor_scalar(out=imp[:], in0=pOcU[:, 260:388], scalar1=rzc[:, 0:1], scalar2=None,
                                                          op0=ALU.mult), reads=['pOcU', 'rzc'], writes=['imp'])
                    yield
                else:
                    P.op('dve', lambda e: e.scalar_tensor_tensor(out=imp[:], in0=pOcU[:, 260:388], scalar=rzc[:, h:h + 1],
                                                                 in1=imp[:], op0=ALU.mult, op1=ALU.add),
                         reads=['pOcU', 'rzc', 'imp'], writes=['imp'])
                    yield
                P.op('dve', lambda e: e.tensor_scalar(out=yc[b2][:, h, :], in0=pOcU[:, h * 65:h * 65 + 64], scalar1=rzc[:, h:h + 1],
                                                      scalar2=None, op0=ALU.mult), reads=['pOcU', 'rzc'], writes=[('yc', b2)])
                yield
                yield
            n1 = 2 * m + 1
            if n1 + 1 < 128:
                P.op('pool', lambda e: e.memset(imp[:, n1 + 1:128], -1e30), reads=['imp'], writes=['imp'])
                yield
            P.op('pool', lambda e: e.tensor_copy(out=imp[:, n1:n1 + 1], in_=cols[:, 0:1]), reads=['cols', 'imp'], writes=['imp'])
            yield
            P.op('pool', lambda e: e.memset(imp[:, n1 - 1:n1], 1e9), reads=['imp'], writes=['imp'])
            yield
            if n1 - 2 >= 0:
                P.op('dve', lambda e: e.tensor_tensor(out=imp[:, n1 - 2:n1 - 1], in0=imp[:, n1 - 2:n1 - 1], in1=cols[:, 1:2], op=ALU.max),
                     reads=['cols', 'imp'], writes=['imp'])
                yield
            P.op('pool', lambda e: e.memset(imp[:, 0:1], 1e9), reads=['imp'], writes=['imp'])
            yield
            yield
            P.op('dve', lambda e: e.max(out=m8a[:], in_=imp[:]), reads=['imp'], writes=['m8a'])
            yield
            P.op('dve', lambda e: e.match_replace(out=work[:], in_to_replace=m8a[:], in_values=imp[:], imm_value=-1e30),
                 reads=['imp', 'm8a'], writes=['work'])
            yield
            P.op('dve', lambda e: e.max(out=m8b[:], in_=work[:]), reads=['work'], writes=['m8b'])
            yield
            yield
            P.op('dve', lambda e: e.tensor_scalar(out=selm[:], in0=imp[:], scalar1=m8b[:, 7:8], scalar2=None, op0=ALU.is_ge),
                 reads=['imp', 'm8b'], writes=['selm'])
            yield
            P.op('dve', lambda e: e.tensor_scalar(out=MB[:], in0=selm[:], scalar1=-NEGM, scalar2=NEGM, op0=ALU.mult, op1=ALU.add),
                 reads=['selm'], writes=['MB'])
            yield
            if n1 + 1 < 128:
                P.op('pool', lambda e: e.memset(MB[:, n1 + 1:128], NEGM), reads=['MB'], writes=['MB'])
                yield
            P.op('pool', lambda e: e.tensor_copy(out=MB[:, n1:n1 + 1], in_=cols[:, 2:3]), reads=['cols', 'MB'], writes=['MB'])
            yield
            yield
            P.op('pool', lambda e: e.tensor_copy(out=MBs[:, 0:64], in_=MB[:, 64:128]), reads=['MB'], writes=['MBs'])
            yield
            P.op('pool', lambda e: e.tensor_copy(out=MBs[:, 64:128], in_=MB[:, 0:64]), reads=['MB'], writes=['MBs'])
            yield
            P.op('pe', lambda e: e.transpose(out=pMb, in_=MBs[:], identity=identb[:]), reads=['MBs', 'identb'], writes=['pMi'])
            yield
            P.op('act', lambda e: e.copy(out=h4(qr0[64:128, :]), in_=hb(pMb[64:128, :])), reads=['pMi'], writes=[('QR0m', b2)])
            yield
            if use_g1:
                P.op('pe', lambda e: e.transpose(out=pMb2, in_=MB[:], identity=identb[:]), reads=['MB', 'identb'], writes=['pMi'])
                yield
                P.op('act', lambda e: e.copy(out=h4(qr1[64:128, :]), in_=hb(pMb2[64:128, :])), reads=['pMi'], writes=[('QR1m', b2)])
                yield
            yield

        si = 0
        total_main = sum((m_ + 1) + (m_ + 1 - max(0, m_ - 4)) for m_ in range(NT))
        bg_rate = (bg_yields / float(total_main)) * 1.1 if bgen is not None else 0.0
        bg_state = {'g': bgen, 'acc': 0.0}

        def bg_step():
            if bg_state['g'] is None:
                return
            bg_state['acc'] += bg_rate
            while bg_state['acc'] >= 1.0 and bg_state['g'] is not None:
                bg_state['acc'] -= 1.0
                try:
                    next(bg_state['g'])
                except StopIteration:
                    bg_state['g'] = None
        g = gen_prep(0)
        for _ in g:
            pass
        for m in range(NT):
            t0 = m * 128
            b2 = m % 2
            qr0, qr1 = QR0[b2], QR1[b2]
            gp = gen_prep(m + 1) if m + 1 < NT else None
            P.op('pe', lambda e: e.matmul(pOs[:, 0:260], lhsT=zl[:], rhs=zr[:, 0:260], start=True, stop=True, skip_group_check=True),
                 reads=['zl', 'zr'], writes=['pOs'])
            P.op('pe', lambda e: e.matmul(pOw[:, 0:260], lhsT=zl[:], rhs=zr[:, 0:260], start=True, stop=True, skip_group_check=True),
                 reads=['zl', 'zr'], writes=['pOw'])
            items = [('s', kt) for kt in range(m + 1)] + [('w', kt) for kt in range(max(0, m - 4), m + 1)]
            n_it = len(items)
            rate = 80.0 / n_it
            base = si

            def qk(ix):
                typ, kt = items[ix]
                p_ = pSr[(base + ix) % 3]
                pk = ('pSr', (base + ix) % 3)
                if typ == 's':
                    if kt // 32 == 0:
                        P.op('pe', lambda e: e.matmul(p_[:], lhsT=KSX[:, kt * 128:(kt + 1) * 128], rhs=qr0[:], start=True, stop=True),
                             reads=['KSX', 'KSXi', ('QR0q', b2), ('QR0m', b2)], writes=[pk])
                    else:
                        P.op('pe', lambda e: e.matmul(p_[:], lhsT=KSX[:, kt * 128:(kt + 1) * 128], rhs=qr1[:], start=True, stop=True),
                             reads=['KSX', 'KSXi', ('QR1q', b2), ('QR1m', b2)], writes=[pk])
                else:
                    P.op('pe', lambda e: e.matmul(p_[:], lhsT=KWX[:, kt * 128:(kt + 1) * 128], rhs=qr0[0:64, :], start=True, stop=True),
                         reads=['KWX', ('QR0q', b2)], writes=[pk])
            for ix in range(min(2, n_it)):
                qk(ix)
            pacc = 0.0
            for ix in range(n_it):
                if ix + 2 < n_it:
                    qk(ix + 2)
                typ, kt = items[ix]
                p_ = pSr[(base + ix) % 3]
                pk = ('pSr', (base + ix) % 3)
                t_ = PT[(base + ix) % 4]
                tk = ('PTn', (base + ix) % 4)
                nb_ = negBs if typ == 's' else negBw
                P.op('act', lambda e: e.activation(out=t_[:], in_=p_[:], func=AF.Exp, bias=nb_[:, 0:1], scale=0.125),
                     reads=[pk, 'negBs', 'negBw'], writes=[tk])
                mk = None
                if kt == m:
                    mk = trib
                elif typ == 'w' and kt == m - 4:
                    mk = triw
                if mk is not None:
                    P.op('dve', lambda e: e.tensor_tensor(out=h4(t_[:]), in0=h4(t_[:]), in1=hb(mk[:]), op=ALU.mult),
                         reads=[tk, 'trib', 'triw'], writes=[tk])
                po_, pok, V_, vk = (pOs, 'pOs', VSX, 'vs_dX') if typ == 's' else (pOw, 'pOw', VWX, 'vw_dX')
                for h in range(4):
                    P.op('pe', lambda e: e.matmul(po_[:, h * 65:(h + 1) * 65], lhsT=t_[:, h * 128:(h + 1) * 128], rhs=V_[:, kt, :],
                                                  start=False, stop=(kt == m), skip_group_check=True),
                         reads=[tk, vk], writes=[pok])
                bg_step()
                if gp is not None:
                    pacc += rate
                    while pacc >= 1.0 and gp is not None:
                        pacc -= 1.0
                        try:
                            next(gp)
                        except StopIteration:
                            gp = None
            si += n_it
            if gp is not None:
                for _ in gp:
                    pass
            posv = pOs[:, 0:260].rearrange("p (h d) -> p h d", d=65)
            powv = pOw[:, 0:260].rearrange("p (h d) -> p h d", d=65)
            P.op('dve', lambda e: e.tensor_copy(out=zz[:, 0, :], in_=posv[:, :, 64]), reads=['pOs'], writes=['zz'])
            P.op('dve', lambda e: e.tensor_copy(out=zz[:, 1, :], in_=powv[:, :, 64]), reads=['pOw'], writes=['zz'])
            P.op('dve', lambda e: e.reciprocal(out=rzz[:], in_=zz[:]), reads=['zz'], writes=['rzz'])
            P.op('dve', lambda e: e.tensor_tensor(out=coef[:], in0=rzz[:], in1=NG[:, m, 4:12].rearrange("p (b h) -> p b h", b=2), op=ALU.mult),
                 reads=['rzz', 'NG'], writes=['coef'])
            for h in range(4):
                P.op('dve', lambda e: e.tensor_scalar(out=yacc[:, h, :], in0=yc[b2][:, h, :], scalar1=NG[:, m, h:h + 1], scalar2=None,
                                                      op0=ALU.mult), reads=[('yc', b2), 'NG'], writes=[('yacc', h)])
                P.op('dve', lambda e: e.scalar_tensor_tensor(out=yacc[:, h, :], in0=posv[:, h, 0:64], scalar=coef[:, 0, h:h + 1],
                                                             in1=yacc[:, h, :], op0=ALU.mult, op1=ALU.add),
                     reads=['pOs', 'coef', ('yacc', h)], writes=[('yacc', h)])
                P.op('dve', lambda e: e.scalar_tensor_tensor(out=yn[b2][:, h * 64:(h + 1) * 64], in0=powv[:, h, 0:64],
                                                             scalar=coef[:, 1, h:h + 1], in1=yacc[:, h, :], op0=ALU.mult, op1=ALU.add),
                     reads=['pOw', 'coef', ('yacc', h)], writes=[('yn', b2)])
            P.dma('sp', self.mix_d[t0:t0 + 128, 768:1024], yn[b2][:], reads=[('yn', b2)], writes=[('mixn', m)])
        while bg_state['g'] is not None:
            try:
                next(bg_state['g'])
            except StopIteration:
                bg_state['g'] = None


Builder._phase2_nsa_pipe = _phase2_nsa_pipe


def _phase2_nsa2(self, l):
    P = self.P
    with ExitStack() as st:
        identb = self.sb(st, "identb_n2", [128, 128], BF16)
        zl = self.sb(st, "zl_n2", [1, 128], BF16)
        zr = self.sb(st, "zr_n2", [1, 512], BF16)
        P.dma('sp', identb[:], self.c_identb, writes=['identb'])
        P.op('pool', lambda e: e.memset(zl[:], 0.0), writes=['zl'])
        P.op('pool', lambda e: e.memset(zr[:], 0.0), writes=['zr'])
        pSr = [self.ps(st, "pSr2", [128, 512], F32) for _ in range(3)]
        pMi = self.ps(st, "pMi2", [128, 512], F32)
        self._phase2_nsa_pipe(l, pSr, pMi, identb, zl, zr)
        self.barrier()


Builder.phase2_nsa2 = _phase2_nsa2


def _phase2b(self, l):
    S, P, nc = self.S, self.P, self.nc
    NCH = S // 64
    NT = S // 128
    NQC = S // 512
    NB = S // 256
    Nc = S // 16 - 1
    NCT = max(1, S // 2048)
    LNSC = math.log(128.0 ** -0.5)
    sb, ps = self.sb, self.ps
    allt = list(range(NT))
    with ExitStack() as st:
        identb = sb(st, "identb2", [128, 128], BF16)
        identf = sb(st, "identf2", [128, 128], F32)
        ut = sb(st, "ut2", [128, 128], F32)
        tri = sb(st, "tri2", [128, 128], F32)
        zl = sb(st, "zl2", [1, 128], BF16)
        zr = sb(st, "zr2", [1, 512], BF16)
        P.dma('sp', identb[:], self.c_identb, writes=['identb'])
        P.dma('sp', identf[:], self.c_identf, writes=['identf'])
        P.dma('sp', ut[:], self.c_ut, writes=['ut'])
        P.dma('sp', tri[:], self.c_tri, writes=['tri'])
        P.op('pool', lambda e: e.memset(zl[:], 0.0), writes=['zl'])
        P.op('pool', lambda e: e.memset(zr[:], 0.0), writes=['zr'])
        uT = sb(st, "uT_m", [64, 4, NCH], F32)
        u2T = sb(st, "u2T_m", [64, 4, NCH], F32)
        flT = sb(st, "flT_m", [64, 4, NCH], F32)
        decB = sb(st, "decB", [128, 4, NCH], F32)
        with ExitStack() as s2:
            li = sb(s2, "li", [NCH, 4, 64], F32)
            lf = sb(s2, "lf", [NCH, 4, 64], F32)
            ones = sb(s2, "ones", [NCH, 64], F32)
            Fin = sb(s2, "Fin", [NCH, 4, 64], F32)
            Ft = sb(s2, "Ft", [NCH, 4, 64], F32)
            a_ = sb(s2, "a_", [NCH, 4, 64], F32)
            Ain = sb(s2, "Ain", [NCH, 4, 64], F32)
            tot = sb(s2, "tot", [NCH, 4], F32)
            cmax = sb(s2, "cmax", [NCH, 4], F32)
            cmT = sb(s2, "cmT", [4, NCH], F32)
            ET = sb(s2, "ET", [4, NCH], F32)
            ETn = sb(s2, "ETn", [4, NCH], F32)
            Ec = sb(s2, "Ec", [NCH, 4], F32)
            Enc = sb(s2, "Enc", [NCH, 4], F32)
            tmp = sb(s2, "tmpm", [NCH, 4, 64], F32)
            uu = sb(s2, "uu", [NCH, 4, 64], F32)
            uu2 = sb(s2, "uu2", [NCH, 4, 64], F32)
            fl = sb(s2, "fl", [NCH, 4, 64], F32)
            dec = sb(s2, "dec", [NCH, 4], F32)
            decrep = sb(s2, "decrep", [NCH, 4, 128], F32)
            pa = ps(s2, "pa", [128, 512], F32)
            pb = ps(s2, "pb", [128, 512], F32)
            P.dma('sp', li[:], self.gi_d.rearrange("h (c j) -> c h j", j=64),
                  reads=[('gi_d', b) for b in range(S // 512)], writes=['li'])
            P.dma('sp', lf[:], self.gf_d.rearrange("h (c j) -> c h j", j=64),
                  reads=[('gf_d', b) for b in range(S // 512)], writes=['lf'])
            P.op('pool', lambda e: e.memset(ones[:], 1.0), writes=['ones'])
            for hh in range(4):
                P.op('dve', lambda e: e.tensor_tensor_scan(out=Fin[:, hh, :], data0=ones[:], data1=lf[:, hh, :],
                                                           initial=0.0, op0=ALU.mult, op1=ALU.add),
                     reads=['ones', 'lf'], writes=['Fin'])
            P.op('dve', lambda e: e.tensor_copy(out=tot[:], in_=Fin[:, :, 63]), reads=['Fin'], writes=['tot'])
            P.op('pe', lambda e: e.matmul(pa[0:NCH, 0:4], lhsT=ut[0:NCH, 0:NCH], rhs=tot[:], start=True, stop=True),
                 reads=['ut', 'tot'], writes=['pa'])
            P.op('dve', lambda e: e.tensor_tensor(out=Ft[:], in0=Fin[:],
                                                  in1=pa[0:NCH, 0:4].unsqueeze(2).broadcast_to([NCH, 4, 64]), op=ALU.add),
                 reads=['Fin', 'pa'], writes=['Ft'])
            P.op('dve', lambda e: e.tensor_tensor(out=a_[:], in0=li[:], in1=Ft[:], op=ALU.subtract),
                 reads=['li', 'Ft'], writes=['a_'])
            for hh in range(4):
                P.op('dve', lambda e: e.tensor_tensor_scan(out=Ain[:, hh, :], data0=a_[:, hh, :], data1=a_[:, hh, :],
                                                           initial=-1e30, op0=ALU.max, op1=ALU.max),
                     reads=['a_'], writes=['Ain'])
            P.op('dve', lambda e: e.tensor_copy(out=cmax[:], in_=Ain[:, :, 63]), reads=['Ain'], writes=['cmax'])
            P.op('pe', lambda e: e.transpose(out=pb[0:4, 0:NCH], in_=cmax[:], identity=identf[0:NCH, 0:NCH]),
                 reads=['cmax', 'identf'], writes=['pb'])
            P.op('dve', lambda e: e.tensor_copy(out=cmT[:], in_=pb[0:4, 0:NCH]), reads=['pb'], writes=['cmT'])
            P.op('dve', lambda e: e.tensor_tensor_scan(out=ET[:], data0=cmT[:], data1=cmT[:], initial=0.0,
                                                       op0=ALU.max, op1=ALU.max), reads=['cmT'], writes=['ET'])
            if NCH > 1:
                P.op('dve', lambda e: e.tensor_copy(out=ETn[:, 0:NCH - 1], in_=ET[:, 1:NCH]), reads=['ET'], writes=['ETn'])
            P.op('dve', lambda e: e.tensor_copy(out=ETn[:, NCH - 1:NCH], in_=ET[:, NCH - 1:NCH]), reads=['ET'], writes=['ETn'])
            P.op('pe', lambda e: e.transpose(out=pa[0:NCH, 0:4], in_=ET[:], identity=identf[0:4, 0:4]),
                 reads=['ET', 'identf'], writes=['pa'])
            P.op('dve', lambda e: e.tensor_copy(out=Ec[:], in_=pa[0:NCH, 0:4]), reads=['pa'], writes=['Ec'])
            P.op('pe', lambda e: e.transpose(out=pb[0:NCH, 0:4], in_=ETn[:], identity=identf[0:4, 0:4]),
                 reads=['ETn', 'identf'], writes=['pb'])
            P.op('dve', lambda e: e.tensor_copy(out=Enc[:], in_=pb[0:NCH, 0:4]), reads=['pb'], writes=['Enc'])
            Eb = Ec[:].unsqueeze(2).broadcast_to([NCH, 4, 64])
            Enb = Enc[:].unsqueeze(2).broadcast_to([NCH, 4, 64])
            P.op('dve', lambda e: e.tensor_tensor(out=tmp[:], in0=a_[:], in1=Eb, op=ALU.subtract),
                 reads=['a_', 'Ec'], writes=['tmp'])
            P.op('act', lambda e: e.activation(out=uu[:], in_=tmp[:], func=AF.Exp, bias=LNSC), reads=['tmp'], writes=['uu'])
            P.op('dve', lambda e: e.tensor_tensor(out=tmp[:], in0=a_[:], in1=Enb, op=ALU.subtract),
                 reads=['a_', 'Enc'], writes=['tmp'])
            P.op('act', lambda e: e.activation(out=uu2[:], in_=tmp[:], func=AF.Exp, bias=LNSC), reads=['tmp'], writes=['uu2'])
            P.op('dve', lambda e: e.tensor_tensor(out=tmp[:], in0=Ft[:], in1=Eb, op=ALU.add),
                 reads=['Ft', 'Ec'], writes=['tmp'])
            P.op('act', lambda e: e.activation(out=fl[:], in_=tmp[:], func=AF.Exp, scale=-1.0), reads=['tmp'], writes=['fl'])
            P.op('dve', lambda e: e.tensor_tensor(out=dec[:], in0=Ec[:], in1=Enc[:], op=ALU.subtract),
                 reads=['Ec', 'Enc'], writes=['dec'])
            P.op('act', lambda e: e.activation(out=dec[:], in_=dec[:], func=AF.Exp), reads=['dec'], writes=['dec'])
            P.op('dve', lambda e: e.tensor_copy(out=decrep[:], in_=dec[:].unsqueeze(2).broadcast_to([NCH, 4, 128])),
                 reads=['dec'], writes=['decrep'])
            for src, dst, nm in ((uu, uT, 'uT'), (uu2, u2T, 'u2T'), (fl, flT, 'flT')):
                for hh in range(4):
                    P.op('pe', lambda e: e.transpose(out=pa[0:64, hh * 128:hh * 128 + NCH], in_=src[:, hh, :],
                                                     identity=identf[0:NCH, 0:NCH]),
                         reads=['uu', 'uu2', 'fl', 'identf'], writes=['pa'])
                P.op('dve', lambda e: e.tensor_copy(out=dst[:], in_=pa[0:64, :].rearrange("p (h c) -> p h c", h=4)[:, :, 0:NCH]),
                     reads=['pa'], writes=[nm])
            for hh in range(4):
                P.op('pe', lambda e: e.matmul(pb[:, hh * 128:hh * 128 + NCH], lhsT=decrep[:, hh, :],
                                              rhs=identf[0:NCH, 0:NCH], start=True, stop=True),
                     reads=['decrep', 'identf'], writes=['pb'])
            P.op('dve', lambda e: e.tensor_copy(out=decB[:], in_=pb[:].rearrange("p (h c) -> p h c", h=4)[:, :, 0:NCH]),
                 reads=['pb'], writes=['decB'])
            self.barrier()

        pSr = [ps(st, "pSr", [128, 512], F32) for _ in range(3)]
        pMi = ps(st, "pMi", [128, 512], F32)
        pMb = pMi[:, 128:192].bitcast(BF16)
        pMb2 = pMi[:, 192:256].bitcast(BF16)
        pM = pMi[:, 256:288]
        bg = BG()

        GC = 4
        NG = S // (64 * GC)
        TG = 64 * GC
        qg = [sb(st, "qg", [128, 4, TG], BF16) for _ in range(2)]
        kg = [sb(st, "kg", [128, 4, TG], BF16) for _ in range(2)]
        vg = [sb(st, "vg", [64, GC, 4, 129], BF16) for _ in range(2)]
        ogg = [sb(st, "ogg", [64, GC, 512], BF16) for _ in range(2)]
        ym = [sb(st, "ym", [64, GC, 512], BF16) for _ in range(2)]
        G = [sb(st, "G", [128, 129], F32) for _ in range(4)]
        Gb = [sb(st, "Gb", [128, 129], BF16) for _ in range(4)]
        ku2 = [sb(st, "ku2", [64, 128], BF16) for _ in range(2)]
        Sm = [sb(st, "Sm", [64, 64], BF16) for _ in range(2)]
        junkm = sb(st, "junkm", [64, 128], BF16)
        scm = [sb(st, "scm", [64, 8], F32) for _ in range(2)]
        mlA = [ps(st, "mlA", [128, 512], F32) for _ in range(1)]

        def gen_ml():
            for i in range(2):
                P.op('pool', lambda e: e.memset(vg[i][:], 1.0), writes=[('vg', i)])
            for hh in range(4):
                P.op('pool', lambda e: e.memset(G[hh][:], 0.0), writes=[('G', hh)])
                P.op('pool', lambda e: e.memset(Gb[hh][:], 0.0), writes=[('Gb', hh)])
            yield

            def load_group(g):
                i = g % 2
                tk = slice(g * TG, (g + 1) * TG)
                bl = sorted(set([(g * TG) // 512, ((g + 1) * TG - 1) // 512]))
                jl = list(range((g * TG) // 128, ((g + 1) * TG + 127) // 128))
                P.dma('sp', qg[i][:], self.qkT[0:512, tk].rearrange("(h p) t -> p h t", p=128),
                      reads=[('qkT', ft, b) for ft in range(4) for b in bl], writes=[('qg', i)])
                P.dma('sp', kg[i][:], self.qkT[512:1024, tk].rearrange("(h p) t -> p h t", p=128),
                      reads=[('qkT', ft, b) for ft in range(4, 8) for b in bl], writes=[('kg', i)])
                for hh in range(4):
                    P.dma('sp', vg[i][:, :, hh, 0:128],
                          self.mv_d[tk, hh * 128:(hh + 1) * 128].rearrange("(c s) e -> s c e", s=64),
                          reads=[('mv_d', j) for j in jl], writes=[('vg', i)])
                P.dma('sp', ogg[i][:], self.og_d[tk, :].rearrange("(c s) e -> s c e", s=64),
                      reads=[('og_d', j) for j in jl], writes=[('ogg', i)])
            load_group(0)
            it = 0
            for g in range(NG):
                if g + 1 < NG:
                    load_group(g + 1)
                gi_ = g % 2
                for cl in range(GC):
                    c = g * GC + cl
                    for hh in range(4):
                        k_ = kg[gi_][:, hh, cl * 64:(cl + 1) * 64]
                        q_ = qg[gi_][:, hh, cl * 64:(cl + 1) * 64]
                        v_ = vg[gi_][:, cl, hh, :]
                        i2 = it % 2
                        it += 1
                        A = mlA[i2 % len(mlA)]
                        if os.environ.get("UNPACK"):
                            pS_ = pSr[0][0:64, 0:64]
                            pO_ = pSr[1][0:64, 0:129]
                            pG_ = pSr[2][:, 0:129]
                            pkT_ = A[0:64, 0:64].bitcast(BF16)
                        else:
                            pS_ = A[0:64, 0:64]
                            pO_ = A[0:64, 64:193]
                            pG_ = A[:, 256:385]
                            pkT_ = A[0:64, 448:512].bitcast(BF16)
                        P.op('pe', lambda e: e.transpose(out=pkT_, in_=k_, identity=identb[:]),
                             reads=[('kg', gi_), 'identb'], writes=[('mlA', i2 % len(mlA))])
                        P.op('pe', lambda e: e.matmul(pS_, lhsT=k_, rhs=q_, start=True, stop=True),
                             reads=[('kg', gi_), ('qg', gi_)], writes=[('mlA', i2 % len(mlA))])
                        yield
                        P.op('act', lambda e: e.activation(out=ku2[i2][:], in_=pkT_, func=AF.Copy,
                                                           scale=u2T[:, hh, c:c + 1]),
                             reads=[('mlA', i2 % len(mlA)), 'u2T'], writes=[('ku2', i2)])
                        P.op('dve', lambda e: e.scalar_tensor_tensor(out=Sm[i2][:], in0=pS_,
                                                                     scalar=uT[:, hh, c:c + 1], in1=tri[0:64, 0:64],
                                                                     op0=ALU.mult, op1=ALU.mult),
                             reads=[('mlA', i2 % len(mlA)), 'uT', 'tri'], writes=[('Sm', i2)])
                        yield
                        P.op('pe', lambda e: e.matmul(pO_, lhsT=Sm[i2][:], rhs=v_, start=True, stop=False),
                             reads=[('Sm', i2), ('vg', gi_)], writes=[('mlA', i2 % len(mlA))])
                        P.op('pe', lambda e: e.matmul(pO_, lhsT=q_, rhs=Gb[hh][:], start=False, stop=True),
                             reads=[('qg', gi_), ('Gb', hh)], writes=[('mlA', i2 % len(mlA))])
                        P.op('pe', lambda e: e.matmul(pG_, lhsT=ku2[i2][:], rhs=v_, start=True, stop=True),
                             reads=[('ku2', i2), ('vg', gi_)], writes=[('mlA', i2 % len(mlA))])
                        yield
                        P.op('dve', lambda e: e.scalar_tensor_tensor(out=G[hh][:], in0=G[hh][:], scalar=decB[:, hh, c:c + 1],
                                                                     in1=pG_, op0=ALU.mult, op1=ALU.add),
                             reads=[('G', hh), 'decB', ('mlA', i2 % len(mlA))], writes=[('G', hh)])
                        P.op('pool', lambda e: e.tensor_copy(out=Gb[hh][:], in_=G[hh][:]), reads=[('G', hh)], writes=[('Gb', hh)])
                        s_ = scm[i2]
                        sk = ('scm', i2)
                        P.op('act', lambda e: e.activation(out=junkm[:], in_=pO_[:, 0:128], func=AF.Square,
                                                           scale=128.0 ** -0.5, accum_out=s_[:, 0:1]),
                             reads=[('mlA', i2 % len(mlA))], writes=['junkm', sk])
                        yield
                        P.op('dve', lambda e: e.tensor_scalar(out=s_[:, 6:7], in0=pO_[:, 128:129], scalar1=-1.0,
                                                              scalar2=flT[:, hh, c:c + 1], op0=ALU.mult, op1=ALU.max),
                             reads=[('mlA', i2 % len(mlA)), 'flT'], writes=[sk])
                        P.op('dve', lambda e: e.tensor_tensor(out=s_[:, 1:2], in0=s_[:, 6:7], in1=pO_[:, 128:129], op=ALU.max),
                             reads=[('mlA', i2 % len(mlA)), sk], writes=[sk])
                        P.op('dve', lambda e: e.tensor_tensor(out=s_[:, 2:3], in0=s_[:, 1:2], in1=s_[:, 1:2], op=ALU.mult),
                             reads=[sk], writes=[sk])
                        P.op('dve', lambda e: e.scalar_tensor_tensor(out=s_[:, 3:4], in0=s_[:, 2:3], scalar=EPS, in1=s_[:, 0:1],
                                                                     op0=ALU.mult, op1=ALU.add), reads=[sk], writes=[sk])
                        yield
                        if os.environ.get("USE_SQRT"):
                            P.op('act', lambda e: e.activation(out=s_[:, 4:5], in_=s_[:, 3:4], func=AF.Sqrt), reads=[sk], writes=[sk])
                            P.op('dve', lambda e: e.reciprocal(out=s_[:, 5:6], in_=s_[:, 4:5]), reads=[sk], writes=[sk])
                        else:
                            P.op('act', lambda e: e.activation(out=s_[:, 4:5], in_=s_[:, 3:4], func=AF.Ln), reads=[sk], writes=[sk])
                            P.op('act', lambda e: e.activation(out=s_[:, 5:6], in_=s_[:, 4:5], func=AF.Exp, scale=-0.5), reads=[sk], writes=[sk])
                        P.op('dve', lambda e: e.scalar_tensor_tensor(out=ym[gi_][:, cl, hh * 128:(hh + 1) * 128],
                                                                     in0=pO_[:, 0:128], scalar=s_[:, 5:6],
                                                                     in1=ogg[gi_][:, cl, hh * 128:(hh + 1) * 128],
                                                                     op0=ALU.mult, op1=ALU.mult),
                             reads=[('mlA', i2 % len(mlA)), sk, ('ogg', gi_)], writes=[('ym', gi_)])
                        yield
                P.dma('sp', self.mix_d[g * TG:(g + 1) * TG, 0:512].rearrange("(c s) e -> s c e", s=64), ym[gi_][:],
                      reads=[('ym', gi_)], writes=[('mixm', g)])
        ML_YIELDS = NCH * 4 * 6 + 1
        g_ml = gen_ml()
        if os.environ.get("NO_NSA"):
            for _ in g_ml:
                pass
        else:
            self._phase2_nsa_pipe(l, pSr, pMi, identb, zl, zr, bgen=g_ml, bg_yields=ML_YIELDS)
        self.barrier()


Builder.phase2b = _phase2b
```

```python
import math
import os
from contextlib import ExitStack

import numpy as np
import ml_dtypes

import concourse.bass as bass
import concourse.mybir as mybir
from concourse.bass_utils import run_bass_kernel_spmd

F32 = mybir.dt.float32
BF16 = mybir.dt.bfloat16
AF = mybir.ActivationFunctionType
ALU = mybir.AluOpType
AX = mybir.AxisListType

D_MODEL = 1024
D_IN = 3476
D_FF = 4096
EPS = 1e-6
NEGM = -30000.0
SAME_ENGINE_SYNC = True


class Prog:
    def __init__(self, nc, es, n_dma=24):
        self.nc = nc
        self.es = es
        self.eng = dict(pe=nc.tensor, act=nc.scalar, dve=nc.vector, pool=nc.gpsimd, sp=nc.sync)
        self.esem = {e: es.enter_context(nc.semaphore("s_" + e)) for e in self.eng}
        self.ecnt = {e: 0 for e in self.eng}
        self.dsem = [es.enter_context(nc.semaphore("d_%d" % i)) for i in range(n_dma)]
        self.dval = [0] * n_dma
        self.dnext = 0
        self.seen = {e: {} for e in self.eng}
        self.lastw = {}
        self.readers = {}
        self.nops = 0

    def _wait(self, eng, ev):
        kind, name, val = ev
        if kind == 'e' and name == eng:
            if eng == 'pe' or eng == 'sp' or not SAME_ENGINE_SYNC:
                return
        key = (kind, name)
        if self.seen[eng].get(key, 0) >= val:
            return
        self.seen[eng][key] = val
        sem = self.esem[name] if kind == 'e' else self.dsem[name]
        self.eng[eng].wait_ge(sem, val)

    def _deps(self, reads, writes):
        evs = []
        for k in reads:
            w = self.lastw.get(k)
            if w is not None:
                evs.append(w)
        for k in writes:
            w = self.lastw.get(k)
            if w is not None:
                evs.append(w)
            evs.extend(self.readers.get(k, ()))
        return evs

    def _commit(self, me, reads, writes):
        for k in reads:
            lst = self.readers.setdefault(k, [])
            lst[:] = [r for r in lst if (r[0], r[1]) != (me[0], me[1])]
            lst.append(me)
        for k in writes:
            self.lastw[k] = me
            self.readers[k] = []

    def op(self, eng, fn, reads=(), writes=()):
        for ev in self._deps(reads, writes):
            self._wait(eng, ev)
        ins = fn(self.eng[eng])
        self.ecnt[eng] += 1
        ins.then_inc(self.esem[eng], 1)
        self._commit(('e', eng, self.ecnt[eng]), reads, writes)
        self.nops += 1
        return ins

    def dma(self, q, out, in_, reads=(), writes=(), **kw):
        if q == 'auto':
            qs = os.environ.get("DMAQ", "sp").split(",")
            self.rr = getattr(self, 'rr', 0) + 1
            q = qs[self.rr % len(qs)]
        slot = self.dnext
        self.dnext = (self.dnext + 1) % len(self.dsem)
        evs = self._deps(reads, writes)
        if self.dval[slot] > 0:
            evs.append(('d', slot, self.dval[slot]))
        for ev in evs:
            self._wait(q, ev)
        self.dval[slot] += 16
        self.eng[q].dma_start(out=out, in_=in_, **kw).then_inc(self.dsem[slot], 16)
        self._commit(('d', slot, self.dval[slot]), reads, writes)
        self.nops += 1

    def finish(self):
        for slot in range(len(self.dsem)):
            if self.dval[slot] > 0:
                self._wait('sp', ('d', slot, self.dval[slot]))
        for e in self.eng:
            if e != 'sp' and self.ecnt[e] > 0:
                self._wait('sp', ('e', e, self.ecnt[e]))


class Builder:
    def __init__(self, S, depth, dbg=None):
        self.S = S
        self.depth = depth
        self.dbg = dbg or []
        self.nc = bass.Bass("TRN2", target_bir_lowering=False)
        self.es = ExitStack()
        self.P = Prog(self.nc, self.es)
        self.uid = 0

    def dram_in(self, name, shape, dt=F32):
        return self.nc.dram_tensor(name, list(shape), dt, kind="ExternalInput").ap()

    def dram_out(self, name, shape, dt=F32):
        return self.nc.dram_tensor(name, list(shape), dt, kind="ExternalOutput").ap()

    def dram_tmp(self, name, shape, dt):
        return self.nc.dram_tensor(name, list(shape), dt, kind="Internal").ap()

    def sb(self, st, name, shape, dt):
        self.uid += 1
        return st.enter_context(self.nc.sbuf_tensor("%s_%d" % (name, self.uid), list(shape), dt))

    def ps(self, st, name, shape, dt=F32):
        self.uid += 1
        return st.enter_context(self.nc.psum_tensor("%s_%d" % (name, self.uid), list(shape), dt))

    def dma_mid(self, out, in_, n_mid, step, reads=(), writes=()):
        for a in range(0, n_mid, step):
            b_ = min(n_mid, a + step)
            self.P.dma('sp', out[:, a:b_, :], in_[:, a:b_, :], reads=reads, writes=writes)

    def barrier(self):
        P = self.P
        for e in P.eng:
            for o in P.eng:
                if P.ecnt[o] > 0 and not (o == e and e in ('pe', 'sp')):
                    key = ('e', o)
                    if P.seen[e].get(key, 0) < P.ecnt[o]:
                        P.seen[e][key] = P.ecnt[o]
                        P.eng[e].wait_ge(P.esem[o], P.ecnt[o])
            for slot in range(len(P.dsem)):
                if P.dval[slot] > 0:
                    P._wait(e, ('d', slot, P.dval[slot]))

    def declare(self):
        S, L = self.S, self.depth
        self.x_in = self.dram_in("x", [S, D_MODEL])
        self.w_in = self.dram_in("w_in", [L, D_MODEL, D_IN])
        self.b_if = self.dram_in("b_if", [L, 8])
        self.conv_qk = self.dram_in("conv_qk", [L, 4, 1024])
        self.m_norm = self.dram_in("m_norm", [L, 512])
        self.moba_qk_norm = self.dram_in("moba_qk_norm", [L, 2, 64])
        self.nsa_q_norm = self.dram_in("nsa_q_norm", [L, 64])
        self.nsa_k_norm = self.dram_in("nsa_k_norm", [L, 3, 64])
        self.cmp_pe = self.dram_in("cmp_pe", [L, 2, 32, 64])
        self.cmp_w1 = self.dram_in("cmp_w1", [L, 2, 2048, 128])
        self.cmp_w2 = self.dram_in("cmp_w2", [L, 2, 128, 64])
        self.w_out = self.dram_in("w_out", [L, 1024, 1024])
        self.norm_mix = self.dram_in("norm_mix", [L, 1024])
        self.norm_ffn = self.dram_in("norm_ffn", [L, 1024])
        self.w_ff1 = self.dram_in("w_ff1", [L, 1024, D_FF])
        self.w_ff2 = self.dram_in("w_ff2", [L, D_FF, 1024])
        self.c_identb = self.dram_in("c_identb", [128, 128], BF16)
        self.c_identf = self.dram_in("c_identf", [128, 128], F32)
        self.c_cos = self.dram_in("c_cos", [S, 8], F32)
        self.c_sin = self.dram_in("c_sin", [S, 8], F32)
        self.c_ut = self.dram_in("c_ut", [128, 128], F32)
        self.c_tri = self.dram_in("c_tri", [128, 128], F32)
        self.c_ind32 = self.dram_in("c_ind32", [32, S], BF16)
        self.c_ind64 = self.dram_in("c_ind64", [64, S], BF16)
        self.c_tm4 = self.dram_in("c_tm4", [128, 4, 512], BF16)
        self.c_trib = self.dram_in("c_trib", [128, 128], BF16)
        self.c_triw = self.dram_in("c_triw", [128, 128], BF16)
        self.c_cmask = self.dram_in("c_cmask", [128, 17, 128], BF16)
        self.c_ovl = self.dram_in("c_ovl", [512, 128], BF16)
        self.c_cols = self.dram_in("c_cols", [128, 4], F32)
        self.y_out = self.dram_out("y", [S, D_MODEL])
        dbg = self.dbg

        def tmp(name, shape, dt):
            if name in dbg:
                return self.dram_out(name, shape, dt)
            return self.dram_tmp(name, shape, dt)
        self.xbuf = tmp("xbuf", [S, D_MODEL], F32)
        self.qkT = tmp("qkT", [1024, S], BF16)
        self.kcvcT = tmp("kcvcT", [128, S], BF16)
        self.gi_d = tmp("gi_d", [4, S], F32)
        self.gf_d = tmp("gf_d", [4, S], F32)
        self.mv_d = tmp("mv_d", [S, 512], BF16)
        self.og_d = tmp("og_d", [S, 512], BF16)
        self.bqT = tmp("bqT", [256, S], BF16)
        self.bkT = tmp("bkT", [256, S], BF16)
        self.bv_d = tmp("bv_d", [S, 256], BF16)
        self.nqrT = tmp("nqrT", [256, S], BF16)
        self.nquT = tmp("nquT", [256, S], BF16)
        self.ksT = tmp("ksT", [64, S], BF16)
        self.kwT = tmp("kwT", [64, S], BF16)
        self.vs_d = tmp("vs_d", [S, 64], BF16)
        self.vw_d = tmp("vw_d", [S, 64], BF16)
        self.ng_d = tmp("ng_d", [S, 12], F32)
        self.mix_d = tmp("mix_d", [S, 1024], BF16)

    def phase1(self, l, xsrc):
        S, P, nc = self.S, self.P, self.nc
        NB = S // 512
        with ExitStack() as st:
            sb, ps = self.sb, self.ps
            w_bf = sb(st, "w_in", [128, 8, D_IN], BF16)
            with ExitStack() as st2:
                stage = [sb(st2, "wst", [128, D_IN], F32) for _ in range(2)]
                for kt in range(8):
                    s_ = stage[kt % 2]
                    P.dma('sp', s_[:], self.w_in[l, kt * 128:(kt + 1) * 128, :], writes=[('wst', kt % 2)])
                    P.op(['pool', 'dve'][kt % 2], lambda e: e.tensor_copy(out=w_bf[:, kt, :], in_=s_[:]),
                         reads=[('wst', kt % 2)], writes=[('w_in', kt)])
                self.barrier()
            identb = sb(st, "identb", [128, 128], BF16)
            gmix = sb(st, "gmix", [128, 1024], F32)
            mnorm = sb(st, "mnorm", [128, 512], F32)
            gain20 = sb(st, "gain20", [128, 20, 64], F32)
            convw = sb(st, "convw", [128, 4, 8], F32)
            bi = sb(st, "bi", [4, 1], F32)
            nbf = sb(st, "nbf", [4, 1], F32)
            cosT = sb(st, "cosT", [128, S // 128, 8], F32)
            sinT = sb(st, "sinT", [128, S // 128, 8], F32)
            P.dma('sp', identb[:], self.c_identb, writes=['identb'])
            P.dma('sp', gmix[:], self.norm_mix[l].partition_broadcast(128), writes=['gmix'])
            P.dma('sp', mnorm[:], self.m_norm[l].partition_broadcast(128), writes=['mnorm'])
            P.op('pool', lambda e: e.memset(gain20[:], 1.0), writes=['gain20'])
            for i in range(20):
                src = None
                if i < 4:
                    src = self.moba_qk_norm[l, 0]
                elif i < 8:
                    src = self.moba_qk_norm[l, 1]
                elif 12 <= i < 16:
                    src = self.nsa_q_norm[l]
                elif i == 16:
                    src = self.nsa_k_norm[l, 1]
                elif i == 18:
                    src = self.nsa_k_norm[l, 2]
                if src is not None:
                    P.dma('sp', gain20[:, i, :], src.partition_broadcast(128), writes=['gain20'])
            with nc.allow_non_contiguous_dma(reason="tiny conv weight transpose"):
                for jj in range(4):
                    P.dma('sp', convw[:, jj, :], self.conv_qk[l, jj].rearrange("(ft p) -> p ft", p=128), writes=['convw'])
                P.dma('sp', bi[:], self.b_if[l, 0:4].rearrange("(p o) -> p o", o=1), writes=['bi'])
                P.dma('sp', nbf[:], self.b_if[l, 4:8].rearrange("(p o) -> p o", o=1), writes=['nbf'])
            P.op('dve', lambda e: e.tensor_scalar(out=nbf[:], in0=nbf[:], scalar1=-1.0, scalar2=None, op0=ALU.mult),
                 reads=['nbf'], writes=['nbf'])
            self.dma_mid(cosT[:], self.c_cos.rearrange("(j p) r -> p j r", p=128), S // 128, 8, writes=['cos'])
            self.dma_mid(sinT[:], self.c_sin.rearrange("(j p) r -> p j r", p=128), S // 128, 8, writes=['sin'])

            mh20 = sb(st, "mh20", [128, 20], F32)
            P.op('pool', lambda e: e.memset(mh20[:], -0.5), writes=['mh20'])
            xt = [sb(st, "xt", [128, 4, 1024], F32) for _ in range(2)]
            junk2_2 = [sb(st, "junk2", [128, 1280], BF16) for _ in range(2)]
            ss_2 = [sb(st, "ss", [128, 4], F32) for _ in range(2)]
            rstd_2 = [sb(st, "rstd", [128, 4], F32) for _ in range(2)]
            h_2 = [sb(st, "h", [128, 4, 1024], BF16)] * 2
            hT_2 = [sb(st, "hT", [128, 8, 512], BF16) for _ in range(2)]
            cbuf = [sb(st, "cbuf", [128, 515], F32) for _ in range(8)]
            for ft in range(8):
                P.op('pool', lambda e: e.memset(cbuf[ft][:, 0:3], 0.0), writes=[('cbuf', ft)])
            acc_2 = [sb(st, "acc", [128, 512], F32) for _ in range(2)]
            fo = [sb(st, "fo", [128, 512], BF16) for _ in range(2)]
            gsb = sb(st, "gsb", [4, 512], F32)
            gsb2 = sb(st, "gsb2", [4, 512], F32)
            mvb_2 = [sb(st, "mvb", [128, 512], BF16) for _ in range(2)]
            sg_2 = [sb(st, "sg", [128, 512], F32) for _ in range(2)]
            ogb_2 = [sb(st, "ogb", [128, 512], BF16) for _ in range(2)]
            tm_2 = [sb(st, "tm", [128, 1292], F32) for _ in range(2)]
            ssh_2 = [sb(st, "ssh", [128, 20], F32) for _ in range(2)]
            rs20_2 = [sb(st, "rs20", [128, 20], F32) for _ in range(2)]
            nrm_2 = [sb(st, "nrm", [128, 20, 64], F32) for _ in range(2)]
            nqu_2 = [sb(st, "nqu", [128, 256], BF16) for _ in range(2)]
            rt_2 = [[sb(st, "rt", [128, 20, 8], F32) for _ in range(4)] for _ in range(2)]
            nb_2 = [sb(st, "nb", [128, 20, 64], BF16) for _ in range(2)]
            tmb_2 = [sb(st, "tmb", [128, 1280], BF16) for _ in range(2)]
            tTa_2 = [sb(st, "tTa", [128, 8, 128], BF16) for _ in range(2)]
            tTb_2 = [sb(st, "tTb", [128, 2, 128], BF16) for _ in range(2)]
            ngs_2 = [sb(st, "ngs", [128, 12], F32) for _ in range(2)]
            pT_2 = [ps(st, "pT", [128, 8, 128], BF16) for _ in range(2)]
            pT = pT_2[0]
            pTb = ps(st, "pTb", [128, 2, 128], BF16)
            pf = [ps(st, "pf", [128, 512], F32) for _ in range(2)]
            pg = [ps(st, "pg", [4, 512], F32) for _ in range(1)]
            pt = [ps(st, "pt", [128, 512], F32) for _ in range(2)]
            pfi = 0
            pti = 0

            def load_x(b):
                P.dma('sp', xt[b % 2][:], xsrc[b * 512:(b + 1) * 512, :].rearrange("(j p) d -> p j d", p=128),
                      reads=[('xres', 2 * b), ('xres', 2 * b + 1)], writes=[('xt', b % 2)])
            load_x(0)
            pend_chain = []
            for b in range(NB):
                t0 = b * 512
                if b + 1 < NB:
                    load_x(b + 1)
                def pro_norm(bb):
                    z = bb % 2
                    x_ = xt[z]
                    kx = ('xt', z)
                    ss, rstd, h = ss_2[z], rstd_2[z], h_2[z]
                    for j in range(4):
                        P.op('act', lambda e: e.activation(out=junk2_2[0][:, 0:1024], in_=x_[:, j, :], func=AF.Square,
                                                           scale=1.0 / 32, accum_out=ss[:, j:j + 1]),
                             reads=[kx], writes=[('junk2', 0), ('ss', z)])
                    P.op('dve', lambda e: e.tensor_scalar(out=ss[:], in0=ss[:], scalar1=EPS, scalar2=None, op0=ALU.add),
                         reads=[('ss', z)], writes=[('ss', z)])
                    P.op('pool', lambda e: e.tensor_tensor(out=rstd[:], in0=ss[:], in1=mh20[:, 0:4], op=ALU.pow),
                         reads=[('ss', z), 'mh20'], writes=[('rstd', z)])
                    for j in range(4):
                        P.op('dve', lambda e: e.scalar_tensor_tensor(out=h[:, j, :], in0=x_[:, j, :], scalar=rstd[:, j:j + 1],
                                                                     in1=gmix[:], op0=ALU.mult, op1=ALU.mult),
                             reads=[kx, ('rstd', z), 'gmix'], writes=[('h', j)])

                def pro_tr(bb):
                    z = bb % 2
                    h, hT_ = h_2[z], hT_2[z]
                    for j in range(4):
                        pT = pT_2[j % 2]
                        for kt in range(8):
                            P.op('pe', lambda e: e.transpose(out=pT[:, kt, :], in_=h[:, j, kt * 128:(kt + 1) * 128],
                                                             identity=identb[:]),
                                 reads=[('h', j), 'identb'], writes=[('pT', j % 2)])
                        P.op(['act', 'dve'][j % 2], lambda e: (e.copy if j % 2 == 0 else e.tensor_copy)(out=hT_[:, :, j * 128:(j + 1) * 128], in_=pT[:]),
                             reads=[('pT', j % 2)], writes=[('hT', z, j)])
                if b == 0:
                    pro_norm(0)
                    pro_tr(0)
                hT = hT_2[b % 2]
                hk = [('hT', b % 2, j) for j in range(4)]
                wk = [('w_in', kt) for kt in range(8)]
                def ft_s1(ft):
                    nonlocal pfi
                    c0 = ft * 128 if ft < 8 else 3080
                    p_ = pf[pfi % 2]
                    pk = ('pf', pfi % 2)
                    pfi += 1
                    for kt in range(8):
                        P.op('pe', lambda e: e.matmul(p_[:], lhsT=w_bf[:, kt, c0:c0 + 128], rhs=hT[:, kt, :],
                                                      start=(kt == 0), stop=(kt == 7)),
                             reads=hk + [wk[kt]], writes=[pk])
                    f_ = fo[ft % 2]
                    fk = ('fo', ft % 2)
                    if ft < 8:
                        cb = cbuf[ft]
                        ck = ('cbuf', ft)
                        P.op('act', lambda e: e.copy(out=cb[:, 3:515], in_=p_[:]), reads=[pk], writes=[ck])
                    else:
                        P.op('act', lambda e: e.copy(out=f_[:], in_=p_[:]), reads=[pk], writes=[fk])
                        P.dma('auto', self.kcvcT[:, t0:t0 + 512], f_[:], reads=[fk], writes=[('kcvcT', b)])

                def ft_s2(ft):
                    if ft >= 8:
                        return
                    f_ = fo[ft % 2]
                    fk = ('fo', ft % 2)
                    cb = cbuf[ft]
                    ck = ('cbuf', ft)
                    acc = acc_2[ft % 2]
                    ak = ('acc', ft % 2)
                    P.op('dve', lambda e: e.tensor_scalar(out=acc[:], in0=cb[:, 0:512], scalar1=convw[:, 0, ft:ft + 1],
                                                          scalar2=None, op0=ALU.mult),
                         reads=[ck, 'convw'], writes=[ak])
                    for jj in range(1, 4):
                        P.op('dve', lambda e: e.scalar_tensor_tensor(out=acc[:], in0=cb[:, jj:jj + 512],
                                                                     scalar=convw[:, jj, ft:ft + 1], in1=acc[:],
                                                                     op0=ALU.mult, op1=ALU.add),
                             reads=[ck, 'convw', ak], writes=[ak])
                    P.op('act', lambda e: e.activation(out=f_[:], in_=acc[:], func=AF.Silu), reads=[ak], writes=[fk])
                    P.dma('auto', self.qkT[ft * 128:(ft + 1) * 128, t0:t0 + 512], f_[:], reads=[fk],
                          writes=[('qkT', ft, b)])
                    P.op('pool', lambda e: e.tensor_copy(out=cb[:, 0:3], in_=cb[:, 512:515]), reads=[ck], writes=[ck])
                ft_s1(0)
                for ft in range(9):
                    if ft + 1 < 9:
                        ft_s1(ft + 1)
                    ft_s2(ft)
                if b + 1 < NB:
                    pro_norm(b + 1)
                def gate_mm(gi_):
                    c0 = 2048 + 4 * gi_
                    for kt in range(8):
                        P.op('pe', lambda e: e.matmul(pg[0][:], lhsT=w_bf[:, kt, c0:c0 + 4], rhs=hT[:, kt, :],
                                                      start=(kt == 0), stop=(kt == 7)),
                             reads=hk + [wk[kt]], writes=[('pg', 0)])
                gate_mm(0)
                P.op('act', lambda e: e.activation(out=gsb[:], in_=pg[0][:], func=AF.Identity, bias=bi[:, 0:1]),
                     reads=[('pg', 0), 'bi'], writes=['gsb'])
                P.dma('auto', self.gi_d[:, t0:t0 + 512], gsb[:], reads=['gsb'], writes=[('gi_d', b)])
                gate_mm(1)
                P.op('act', lambda e: e.activation(out=gsb2[:], in_=pg[0][:], func=AF.Exp, bias=nbf[:, 0:1], scale=-1.0),
                     reads=[('pg', 0), 'nbf'], writes=['gsb2'])
                P.op('act', lambda e: e.activation(out=gsb2[:], in_=gsb2[:], func=AF.Ln, bias=1.0),
                     reads=['gsb2'], writes=['gsb2'])
                P.op('dve', lambda e: e.tensor_scalar(out=gsb2[:], in0=gsb2[:], scalar1=-1.0, scalar2=None, op0=ALU.mult),
                     reads=['gsb2'], writes=['gsb2'])
                P.dma('auto', self.gf_d[:, t0:t0 + 512], gsb2[:], reads=['gsb2'], writes=[('gf_d', b)])
                for j in range(4):
                    tok = slice(t0 + j * 128, t0 + (j + 1) * 128)
                    jg = b * 4 + j
                    z_ = jg % 2
                    mvb, sg, ogb, tm, ssh, rs20, nrm, nqu, nb, tmb, tTa, tTb, ngs = (
                        mvb_2[z_], sg_2[z_], ogb_2[z_], tm_2[z_], ssh_2[z_], rs20_2[z_], nrm_2[z_], nqu_2[z_], nb_2[z_],
                        tmb_2[z_], tTa_2[z_], tTb_2[z_], ngs_2[z_])
                    rt = rt_2[z_]
                    pT = pT_2[z_]
                    junk2 = junk2_2[z_]

                    def tm_mm(c0, n):
                        nonlocal pti
                        p_ = pt[pti % 2]
                        pk = ('pt', pti % 2)
                        pti += 1
                        for kt in range(8):
                            P.op('pe', lambda e: e.matmul(p_[:, 0:n], lhsT=hT[:, kt, j * 128:(j + 1) * 128],
                                                          rhs=w_bf[:, kt, c0:c0 + n], start=(kt == 0), stop=(kt == 7)),
                                 reads=[('hT', b % 2, j), wk[kt]], writes=[pk])
                        if pend_chain:
                            for _ in range(5):
                                try:
                                    next(pend_chain[0])
                                except StopIteration:
                                    pend_chain.pop(0)
                                    break
                        return p_, pk
                    p_, pk = tm_mm(1024, 512)
                    P.op('act', lambda e: e.copy(out=mvb[:], in_=p_[:]), reads=[pk], writes=[('mvb', z_)])
                    P.dma('auto', self.mv_d[tok, :], mvb[:], reads=[('mvb', z_)], writes=[('mv_d', jg)])
                    p_, pk = tm_mm(1536, 512)
                    P.op('act', lambda e: e.activation(out=sg[:], in_=p_[:], func=AF.Sigmoid), reads=[pk], writes=[('sg', z_)])
                    P.op('pool', lambda e: e.tensor_tensor(out=ogb[:], in0=sg[:], in1=mnorm[:], op=ALU.mult),
                         reads=[('sg', z_), 'mnorm'], writes=[('ogb', z_)])
                    P.dma('auto', self.og_d[tok, :], ogb[:], reads=[('ogb', z_)], writes=[('og_d', jg)])
                    p_, pk = tm_mm(2056, 512)
                    P.op('dve', lambda e: e.tensor_copy(out=tm[:, 0:512], in_=p_[:]), reads=[pk], writes=[('tm', z_)])
                    p_, pk = tm_mm(2568, 512)
                    P.op('act', lambda e: e.copy(out=tm[:, 512:1024], in_=p_[:]), reads=[pk], writes=[('tm', z_)])
                    p_, pk = tm_mm(3208, 268)
                    P.op('dve', lambda e: e.tensor_copy(out=tm[:, 1024:1292], in_=p_[:, 0:268]), reads=[pk], writes=[('tm', z_)])
                    def chain(tok=tok, jg=jg, z_=z_, tm=tm, ssh=ssh, rs20=rs20, nrm=nrm, nqu=nqu, nb=nb, tmb=tmb, tTa=tTa,
                              tTb=tTb, ngs=ngs, rt=rt, junk2=junk2, pT=pT):
                        yield
                        P.op('act', lambda e: e.activation(out=junk2[:, 0:1280], in_=tm[:, 0:1280], func=AF.Square),
                             reads=[('tm', z_)], writes=[('junk2', z_)])
                        yield
                        P.op('dve', lambda e: e.tensor_reduce(out=ssh[:], in_=junk2[:, 0:1280].rearrange("p (h d) -> p h d", d=64),
                                                              axis=AX.X, op=ALU.add),
                             reads=[('junk2', z_)], writes=[('ssh', z_)])
                        yield
                        P.op('dve', lambda e: e.tensor_scalar(out=ssh[:], in0=ssh[:], scalar1=1.0 / 64, scalar2=EPS,
                                                              op0=ALU.mult, op1=ALU.add), reads=[('ssh', z_)], writes=[('ssh', z_)])
                        yield
                        P.op('pool', lambda e: e.tensor_tensor(out=rs20[:], in0=ssh[:], in1=mh20[:], op=ALU.pow),
                             reads=[('ssh', z_), 'mh20'], writes=[('rs20', z_)])
                        yield
                        P.op('dve', lambda e: e.tensor_tensor(out=nrm[:], in0=tm[:, 0:1280].rearrange("p (h d) -> p h d", d=64),
                                                              in1=rs20[:].unsqueeze(2).broadcast_to([128, 20, 64]), op=ALU.mult),
                             reads=[('tm', z_), ('rs20', z_)], writes=[('nrm', z_)])
                        yield
                        P.op('dve', lambda e: e.tensor_tensor(out=nrm[:], in0=nrm[:], in1=gain20[:], op=ALU.mult),
                             reads=[('nrm', z_), 'gain20'], writes=[('nrm', z_)])
                        yield
                        P.op('act', lambda e: e.copy(out=nqu[:].rearrange("p (h d) -> p h d", d=64), in_=nrm[:, 12:16, :]),
                             reads=[('nrm', z_)], writes=[('nqu', z_)])
                        cb_ = cosT[:, jg, :].unsqueeze(1).broadcast_to([128, 20, 8])
                        sb_ = sinT[:, jg, :].unsqueeze(1).broadcast_to([128, 20, 8])
                        x1 = nrm[:, :, 0:8]
                        x2 = nrm[:, :, 8:16]
                        yield
                        P.op('dve', lambda e: e.tensor_tensor(out=rt[0][:], in0=x1, in1=cb_, op=ALU.mult),
                             reads=[('nrm', z_), 'cos'], writes=[('rt', 0, z_)])
                        yield
                        P.op('pool', lambda e: e.tensor_tensor(out=rt[1][:], in0=x2, in1=sb_, op=ALU.mult),
                             reads=[('nrm', z_), 'sin'], writes=[('rt', 1, z_)])
                        yield
                        P.op('dve', lambda e: e.tensor_tensor(out=rt[2][:], in0=x2, in1=cb_, op=ALU.mult),
                             reads=[('nrm', z_), 'cos'], writes=[('rt', 2, z_)])
                        yield
                        P.op('pool', lambda e: e.tensor_tensor(out=rt[3][:], in0=x1, in1=sb_, op=ALU.mult),
                             reads=[('nrm', z_), 'sin'], writes=[('rt', 3, z_)])
                        yield
                        P.op('dve', lambda e: e.tensor_tensor(out=x1, in0=rt[0][:], in1=rt[1][:], op=ALU.subtract),
                             reads=[('rt', 0, z_), ('rt', 1, z_)], writes=[('nrm', z_)])
                        yield
                        P.op('dve', lambda e: e.tensor_tensor(out=x2, in0=rt[2][:], in1=rt[3][:], op=ALU.add),
                             reads=[('rt', 2, z_), ('rt', 3, z_)], writes=[('nrm', z_)])
                        yield
                        P.op('act', lambda e: e.copy(out=nb[:], in_=nrm[:]), reads=[('nrm', z_)], writes=[('nb', z_)])
                        yield
                        P.op('act', lambda e: e.copy(out=tmb[:], in_=tm[:, 0:1280]), reads=[('tm', z_)], writes=[('tmb', z_)])
                        yield
                        P.op('act', lambda e: e.activation(out=ngs[:], in_=tm[:, 1280:1292], func=AF.Sigmoid),
                             reads=[('tm', z_)], writes=[('ngs', z_)])
                        srcs = [nb[:, 0:2, :], nb[:, 2:4, :], nb[:, 4:6, :], nb[:, 6:8, :], nb[:, 12:14, :], nb[:, 14:16, :]]
                        yield
                        for i, s_ in enumerate(srcs):
                            P.op('pe', lambda e: e.transpose(out=pT[:, i, :], in_=s_.rearrange("p h d -> p (h d)"),
                                                             identity=identb[:]), reads=[('nb', z_), 'identb'], writes=[('pT', z_)])
                        yield
                        for i in range(2):
                            P.op('pe', lambda e: e.transpose(out=pT[:, 6 + i, :], in_=nqu[:, i * 128:(i + 1) * 128],
                                                             identity=identb[:]), reads=[('nqu', z_), 'identb'], writes=[('pT', z_)])
                        yield
                        for i, hh in enumerate((16, 18)):
                            P.op('pe', lambda e: e.transpose(out=pTb[:, i, :], in_=nb[:, hh:hh + 2, :].rearrange("p h d -> p (h d)"),
                                                             identity=identb[:]), reads=[('nb', z_), 'identb'], writes=['pTb'])
                        yield
                        P.op('dve', lambda e: e.tensor_copy(out=tTa[:], in_=pT[:]), reads=[('pT', z_)], writes=[('tTa', z_)])
                        yield
                        P.op('act', lambda e: e.copy(out=tTb[:], in_=pTb[:]), reads=['pTb'], writes=[('tTb', z_)])
                        yield
                        for i, dst in enumerate((self.bqT, self.bkT, self.nqrT, self.nquT)):
                            P.dma('auto', dst[:, tok].rearrange("(a p) t -> p a t", p=128), tTa[:, 2 * i:2 * i + 2, :],
                                  reads=[('tTa', z_)], writes=[(("bqT","bkT","nqrT","nquT")[i], jg)])
                        yield
                        P.dma('auto', self.ksT[:, tok], tTb[0:64, 0, :], reads=[('tTb', z_)], writes=[('ksT', jg)])
                        yield
                        P.dma('auto', self.kwT[:, tok], tTb[0:64, 1, :], reads=[('tTb', z_)], writes=[('kwT', jg)])
                        yield
                        P.dma('auto', self.bv_d[tok, :], tmb[:, 512:768], reads=[('tmb', z_)], writes=[('bv_d', jg)])
                        yield
                        P.dma('auto', self.vs_d[tok, :], tmb[:, 1088:1152], reads=[('tmb', z_)], writes=[('vs_d', jg)])
                        yield
                        P.dma('auto', self.vw_d[tok, :], tmb[:, 1216:1280], reads=[('tmb', z_)], writes=[('vw_d', jg)])
                        yield
                        P.dma('auto', self.ng_d[tok, :], ngs[:], reads=[('ngs', z_)], writes=[('ng_d', jg)])
                    for _ in (pend_chain.pop(0) if pend_chain else ()):
                        pass
                    pend_chain.append(chain())
                    if j == 1 and b + 1 < NB:
                        pro_tr(b + 1)
            while pend_chain:
                for _ in pend_chain.pop(0):
                    pass
            self.barrier()

    def phase3(self, l, xsrc, xdst):
        S, P, nc = self.S, self.P, self.nc
        NB = S // 256
        with ExitStack() as st:
            sb, ps = self.sb, self.ps
            wo = sb(st, "wo", [128, 8, 1024], BF16)
            w1 = sb(st, "w1", [128, 8, 4096], BF16)
            w2 = sb(st, "w2", [128, 32, 1024], BF16)
            with ExitStack() as st2:
                stage = [sb(st2, "wst3", [128, 4096], F32) for _ in range(2)]
                slabs = []
                for i in range(2):
                    slabs.append((self.w_out[l, i * 512:(i + 1) * 512, :].rearrange("(a p) d -> p a d", p=128),
                                  wo[:, i * 4:(i + 1) * 4, :], ('wo', i), True))
                for kt in range(8):
                    slabs.append((self.w_ff1[l, kt * 128:(kt + 1) * 128, :], w1[:, kt, :], ('w1', kt), False))
                for i in range(8):
                    slabs.append((self.w_ff2[l, i * 512:(i + 1) * 512, :].rearrange("(a p) d -> p a d", p=128),
                                  w2[:, i * 4:(i + 1) * 4, :], ('w2', i), True))
                for n, (src, dst, key, three) in enumerate(slabs):
                    s_ = stage[n % 2]
                    sv = s_[:].rearrange("p (a d) -> p a d", a=4) if three else s_[:]
                    P.dma('sp', sv, src, writes=[('wst3', n % 2)])
                    eng = ['pool', 'dve', 'act'][n % 3]
                    if eng == 'act':
                        P.op(eng, lambda e: e.copy(out=dst, in_=sv), reads=[('wst3', n % 2)], writes=[key])
                    else:
                        P.op(eng, lambda e: e.tensor_copy(out=dst, in_=sv), reads=[('wst3', n % 2)], writes=[key])
                self.barrier()
            identb = sb(st, "identb3", [128, 128], BF16)
            gffn = sb(st, "gffn", [128, 1024], F32)
            P.dma('sp', identb[:], self.c_identb, writes=['identb'])
            P.dma('sp', gffn[:], self.norm_ffn[l].partition_broadcast(128), writes=['gffn'])
            xt_2 = [sb(st, "xt3", [128, 2, 1024], F32) for _ in range(2)]
            mixb_2 = [sb(st, "mixb", [128, 2, 1024], BF16) for _ in range(2)]
            mh3 = sb(st, "mh3", [128, 2], F32)
            P.op('pool', lambda e: e.memset(mh3[:], -0.5), writes=['mh3'])
            mT = sb(st, "mT", [128, 8, 256], BF16)
            hT = sb(st, "hT3", [128, 8, 256], BF16)
            uT = sb(st, "uT", [128, 32, 256], BF16)
            rr = [sb(st, "rr", [128, 256], F32) for _ in range(2)]
            junk = sb(st, "junk3", [128, 1024], BF16)
            ss = sb(st, "ss3", [128, 2], F32)
            rstd = sb(st, "rstd3", [128, 2], F32)
            pT = ps(st, "pT3", [128, 8, 128], BF16)
            py = [ps(st, "py", [128, 512], F32) for _ in range(2)]
            pz = [ps(st, "pz", [128, 256], F32) for _ in range(2)]
            pyi = 0
            pzi = 0
            wok = [('wo', 0), ('wo', 1)]
            def load3(b):
                t0_ = b * 256
                P.dma('sp', xt_2[b % 2][:], xsrc[t0_:t0_ + 256, :].rearrange("(j p) d -> p j d", p=128), reads=[('xres', b)],
                      writes=[('xt', b % 2)])
                P.dma('sp', mixb_2[b % 2][:], self.mix_d[t0_:t0_ + 256, :].rearrange("(j p) d -> p j d", p=128),
                      writes=[('mixb', b % 2)])
            load3(0)
            def frontA(b):
                nonlocal pyi, pzi
                t0 = b * 256
                xt = xt_2[b % 2]
                mixb = mixb_2[b % 2]
                xk = ('xt', b % 2)
                mk = ('mixb', b % 2)
                for j in range(2):
                    for kt in range(8):
                        P.op('pe', lambda e: e.transpose(out=pT[:, kt, :], in_=mixb[:, j, kt * 128:(kt + 1) * 128],
                                                         identity=identb[:]), reads=[mk, 'identb'], writes=['pT'])
                    P.op('act', lambda e: e.copy(out=mT[:, :, j * 128:(j + 1) * 128], in_=pT[:]),
                         reads=['pT'], writes=[('mT', j)])
                for j in range(2):
                    for hf in range(2):
                        p_ = py[pyi % 2]
                        pk = ('py', pyi % 2)
                        pyi += 1
                        for kt in range(8):
                            P.op('pe', lambda e: e.matmul(p_[:], lhsT=mT[:, kt, j * 128:(j + 1) * 128],
                                                          rhs=wo[:, kt, hf * 512:(hf + 1) * 512],
                                                          start=(kt == 0), stop=(kt == 7)),
                                 reads=[('mT', j), wok[kt // 4]], writes=[pk])
                        P.op('dve', lambda e: e.tensor_tensor(out=xt[:, j, hf * 512:(hf + 1) * 512],
                                                              in0=xt[:, j, hf * 512:(hf + 1) * 512], in1=p_[:], op=ALU.add),
                             reads=[xk, pk], writes=[xk])
                for j in range(2):
                    P.op('act', lambda e: e.activation(out=junk[:], in_=xt[:, j, :], func=AF.Square, scale=1.0 / 32,
                                                       accum_out=ss[:, j:j + 1]), reads=[xk], writes=['junk', 'ss'])
                P.op('dve', lambda e: e.tensor_scalar(out=ss[:], in0=ss[:], scalar1=EPS, scalar2=None, op0=ALU.add),
                     reads=['ss'], writes=['ss'])
                P.op('pool', lambda e: e.tensor_tensor(out=rstd[:], in0=ss[:], in1=mh3[:], op=ALU.pow), reads=['ss', 'mh3'], writes=['rstd'])
                for j in range(2):
                    P.op('dve', lambda e: e.scalar_tensor_tensor(out=mixb[:, j, :], in0=xt[:, j, :], scalar=rstd[:, j:j + 1],
                                                                 in1=gffn[:], op0=ALU.mult, op1=ALU.mult),
                         reads=[xk, 'rstd', 'gffn'], writes=[mk])

            def frontB(b):
                nonlocal pyi, pzi
                t0 = b * 256
                xt = xt_2[b % 2]
                mixb = mixb_2[b % 2]
                xk = ('xt', b % 2)
                mk = ('mixb', b % 2)
                for j in range(2):
                    for kt in range(8):
                        P.op('pe', lambda e: e.transpose(out=pT[:, kt, :], in_=mixb[:, j, kt * 128:(kt + 1) * 128],
                                                         identity=identb[:]), reads=[mk, 'identb'], writes=['pT'])
                    P.op('act', lambda e: e.copy(out=hT[:, :, j * 128:(j + 1) * 128], in_=pT[:]),
                         reads=['pT'], writes=[('hT', j)])

            def ffn1(b):
                nonlocal pyi, pzi
                t0 = b * 256
                xt = xt_2[b % 2]
                mixb = mixb_2[b % 2]
                xk = ('xt', b % 2)
                mk = ('mixb', b % 2)
                hk = [('hT', 0), ('hT', 1)]
                for ft in range(32):
                    p_ = pz[pzi % 2]
                    pk = ('pz', pzi % 2)
                    r_ = rr[pzi % 2]
                    rk = ('rr', pzi % 2)
                    pzi += 1
                    for kt in range(8):
                        P.op('pe', lambda e: e.matmul(p_[:], lhsT=w1[:, kt, ft * 128:(ft + 1) * 128], rhs=hT[:, kt, :],
                                                      start=(kt == 0), stop=(kt == 7)),
                             reads=hk + [('w1', kt)], writes=[pk])
                    P.op('act', lambda e: e.activation(out=r_[:], in_=p_[:], func=AF.Relu), reads=[pk], writes=[rk])
                    P.op('dve', lambda e: e.tensor_tensor(out=uT[:, ft, :], in0=r_[:], in1=r_[:], op=ALU.mult),
                         reads=[rk], writes=[('uT', ft)])

            def ffn2(b):
                nonlocal pyi, pzi
                t0 = b * 256
                xt = xt_2[b % 2]
                mixb = mixb_2[b % 2]
                xk = ('xt', b % 2)
                mk = ('mixb', b % 2)
                uk = [('uT', ft) for ft in range(32)]
                for j in range(2):
                    for hf in range(2):
                        p_ = py[pyi % 2]
                        pk = ('py', pyi % 2)
                        pyi += 1
                        for ft in range(32):
                            P.op('pe', lambda e: e.matmul(p_[:], lhsT=uT[:, ft, j * 128:(j + 1) * 128],
                                                          rhs=w2[:, ft, hf * 512:(hf + 1) * 512],
                                                          start=(ft == 0), stop=(ft == 31)),
                                 reads=[uk[ft], ('w2', ft // 4)], writes=[pk])
                        P.op('dve', lambda e: e.tensor_tensor(out=xt[:, j, hf * 512:(hf + 1) * 512],
                                                              in0=xt[:, j, hf * 512:(hf + 1) * 512], in1=p_[:], op=ALU.add),
                             reads=[xk, pk], writes=[xk])
                P.dma('sp', xdst[t0:t0 + 256, :].rearrange("(j p) d -> p j d", p=128), xt[:], reads=[xk],
                      writes=[('xres', b)])
            frontA(0)
            frontB(0)
            for b in range(NB):
                if b + 1 < NB:
                    load3(b + 1)
                ffn1(b)
                if b + 1 < NB:
                    frontA(b + 1)
                ffn2(b)
                if b + 1 < NB:
                    frontB(b + 1)
            self.barrier()

    def _ml_norm(self, P, s_, sk, pO_t, pok, junk, flT, mhalf, ym_t, ymk, og_t, ogk, hh, c, cl):
        P.op('act', lambda e: e.activation(out=junk[:], in_=pO_t[0:64, 0:128], func=AF.Square,
                                           scale=128.0 ** -0.5, accum_out=s_[:, 0:1]),
             reads=[pok], writes=['junk', sk])
        P.op('dve', lambda e: e.tensor_scalar(out=s_[:, 6:7], in0=pO_t[0:64, 128:129], scalar1=-1.0,
                                              scalar2=flT[:, hh, c:c + 1], op0=ALU.mult, op1=ALU.max),
             reads=[pok, 'flT'], writes=[sk])
        P.op('dve', lambda e: e.tensor_tensor(out=s_[:, 1:2], in0=s_[:, 6:7], in1=pO_t[0:64, 128:129], op=ALU.max),
             reads=[pok, sk], writes=[sk])
        P.op('dve', lambda e: e.tensor_tensor(out=s_[:, 2:3], in0=s_[:, 1:2], in1=s_[:, 1:2], op=ALU.mult),
             reads=[sk], writes=[sk])
        P.op('dve', lambda e: e.scalar_tensor_tensor(out=s_[:, 3:4], in0=s_[:, 2:3], scalar=EPS, in1=s_[:, 0:1],
                                                     op0=ALU.mult, op1=ALU.add), reads=[sk], writes=[sk])
        P.op('pool', lambda e: e.tensor_tensor(out=s_[:, 5:6], in0=s_[:, 3:4], in1=mhalf[:, 0:1], op=ALU.pow),
             reads=[sk, 'mhalf'], writes=[sk])
        P.op('dve', lambda e: e.scalar_tensor_tensor(out=ym_t[:, cl, hh * 128:(hh + 1) * 128],
                                                     in0=pO_t[0:64, 0:128], scalar=s_[:, 5:6],
                                                     in1=og_t[:, cl, hh * 128:(hh + 1) * 128],
                                                     op0=ALU.mult, op1=ALU.mult),
             reads=[pok, sk, ogk], writes=[ymk])

    def phase2_mlstm(self, l):
        S, P, nc = self.S, self.P, self.nc
        NCH = S // 64
        LNSC = math.log(128.0 ** -0.5)
        with ExitStack() as st:
            sb, ps = self.sb, self.ps
            identb = sb(st, "identbm", [128, 128], BF16)
            identf = sb(st, "identfm", [128, 128], F32)
            ut = sb(st, "ut", [128, 128], F32)
            tri = sb(st, "tri", [128, 128], F32)
            P.dma('sp', identb[:], self.c_identb, writes=['identb'])
            P.dma('sp', identf[:], self.c_identf, writes=['identf'])
            P.dma('sp', ut[:], self.c_ut, writes=['ut'])
            P.dma('sp', tri[:], self.c_tri, writes=['tri'])
            uT = sb(st, "uT_m", [64, 4, NCH], F32)
            u2T = sb(st, "u2T_m", [64, 4, NCH], F32)
            flT = sb(st, "flT_m", [64, 4, NCH], F32)
            decB = sb(st, "decB", [128, 4, NCH], F32)
            with ExitStack() as s2:
                li = sb(s2, "li", [NCH, 4, 64], F32)
                lf = sb(s2, "lf", [NCH, 4, 64], F32)
                ones = sb(s2, "ones", [NCH, 64], F32)
                Fin = sb(s2, "Fin", [NCH, 4, 64], F32)
                Ft = sb(s2, "Ft", [NCH, 4, 64], F32)
                a_ = sb(s2, "a_", [NCH, 4, 64], F32)
                Ain = sb(s2, "Ain", [NCH, 4, 64], F32)
                tot = sb(s2, "tot", [NCH, 4], F32)
                cmax = sb(s2, "cmax", [NCH, 4], F32)
                cmT = sb(s2, "cmT", [4, NCH], F32)
                ET = sb(s2, "ET", [4, NCH], F32)
                ETn = sb(s2, "ETn", [4, NCH], F32)
                Ec = sb(s2, "Ec", [NCH, 4], F32)
                Enc = sb(s2, "Enc", [NCH, 4], F32)
                tmp = sb(s2, "tmpm", [NCH, 4, 64], F32)
                uu = sb(s2, "uu", [NCH, 4, 64], F32)
                uu2 = sb(s2, "uu2", [NCH, 4, 64], F32)
                fl = sb(s2, "fl", [NCH, 4, 64], F32)
                dec = sb(s2, "dec", [NCH, 4], F32)
                decrep = sb(s2, "decrep", [NCH, 4, 128], F32)
                pa = ps(s2, "pa", [128, 512], F32)
                pb = ps(s2, "pb", [128, 512], F32)
                P.dma('sp', li[:], self.gi_d.rearrange("h (c j) -> c h j", j=64),
                      reads=[('gi_d', b) for b in range(S // 512)], writes=['li'])
                P.dma('sp', lf[:], self.gf_d.rearrange("h (c j) -> c h j", j=64),
                      reads=[('gf_d', b) for b in range(S // 512)], writes=['lf'])
                P.op('pool', lambda e: e.memset(ones[:], 1.0), writes=['ones'])
                for hh in range(4):
                    P.op('dve', lambda e: e.tensor_tensor_scan(out=Fin[:, hh, :], data0=ones[:], data1=lf[:, hh, :],
                                                               initial=0.0, op0=ALU.mult, op1=ALU.add),
                         reads=['ones', 'lf'], writes=['Fin'])
                P.op('dve', lambda e: e.tensor_copy(out=tot[:], in_=Fin[:, :, 63]), reads=['Fin'], writes=['tot'])
                P.op('pe', lambda e: e.matmul(pa[0:NCH, 0:4], lhsT=ut[0:NCH, 0:NCH], rhs=tot[:], start=True, stop=True),
                     reads=['ut', 'tot'], writes=['pa'])
                P.op('dve', lambda e: e.tensor_tensor(out=Ft[:], in0=Fin[:],
                                                      in1=pa[0:NCH, 0:4].unsqueeze(2).broadcast_to([NCH, 4, 64]), op=ALU.add),
                     reads=['Fin', 'pa'], writes=['Ft'])
                P.op('dve', lambda e: e.tensor_tensor(out=a_[:], in0=li[:], in1=Ft[:], op=ALU.subtract),
                     reads=['li', 'Ft'], writes=['a_'])
                for hh in range(4):
                    P.op('dve', lambda e: e.tensor_tensor_scan(out=Ain[:, hh, :], data0=a_[:, hh, :], data1=a_[:, hh, :],
                                                               initial=-1e30, op0=ALU.max, op1=ALU.max),
                         reads=['a_'], writes=['Ain'])
                P.op('dve', lambda e: e.tensor_copy(out=cmax[:], in_=Ain[:, :, 63]), reads=['Ain'], writes=['cmax'])
                P.op('pe', lambda e: e.transpose(out=pb[0:4, 0:NCH], in_=cmax[:], identity=identf[0:NCH, 0:NCH]),
                     reads=['cmax', 'identf'], writes=['pb'])
                P.op('dve', lambda e: e.tensor_copy(out=cmT[:], in_=pb[0:4, 0:NCH]), reads=['pb'], writes=['cmT'])
                P.op('dve', lambda e: e.tensor_tensor_scan(out=ET[:], data0=cmT[:], data1=cmT[:], initial=0.0,
                                                           op0=ALU.max, op1=ALU.max), reads=['cmT'], writes=['ET'])
                if NCH > 1:
                    P.op('dve', lambda e: e.tensor_copy(out=ETn[:, 0:NCH - 1], in_=ET[:, 1:NCH]), reads=['ET'], writes=['ETn'])
                P.op('dve', lambda e: e.tensor_copy(out=ETn[:, NCH - 1:NCH], in_=ET[:, NCH - 1:NCH]), reads=['ET'], writes=['ETn'])
                P.op('pe', lambda e: e.transpose(out=pa[0:NCH, 0:4], in_=ET[:], identity=identf[0:4, 0:4]),
                     reads=['ET', 'identf'], writes=['pa'])
                P.op('dve', lambda e: e.tensor_copy(out=Ec[:], in_=pa[0:NCH, 0:4]), reads=['pa'], writes=['Ec'])
                P.op('pe', lambda e: e.transpose(out=pb[0:NCH, 0:4], in_=ETn[:], identity=identf[0:4, 0:4]),
                     reads=['ETn', 'identf'], writes=['pb'])
                P.op('dve', lambda e: e.tensor_copy(out=Enc[:], in_=pb[0:NCH, 0:4]), reads=['pb'], writes=['Enc'])
                Eb = Ec[:].unsqueeze(2).broadcast_to([NCH, 4, 64])
                Enb = Enc[:].unsqueeze(2).broadcast_to([NCH, 4, 64])
                P.op('dve', lambda e: e.tensor_tensor(out=tmp[:], in0=a_[:], in1=Eb, op=ALU.subtract),
                     reads=['a_', 'Ec'], writes=['tmp'])
                P.op('act', lambda e: e.activation(out=uu[:], in_=tmp[:], func=AF.Exp, bias=LNSC), reads=['tmp'], writes=['uu'])
                P.op('dve', lambda e: e.tensor_tensor(out=tmp[:], in0=a_[:], in1=Enb, op=ALU.subtract),
                     reads=['a_', 'Enc'], writes=['tmp'])
                P.op('act', lambda e: e.activation(out=uu2[:], in_=tmp[:], func=AF.Exp, bias=LNSC), reads=['tmp'], writes=['uu2'])
                P.op('dve', lambda e: e.tensor_tensor(out=tmp[:], in0=Ft[:], in1=Eb, op=ALU.add),
                     reads=['Ft', 'Ec'], writes=['tmp'])
                P.op('act', lambda e: e.activation(out=fl[:], in_=tmp[:], func=AF.Exp, scale=-1.0), reads=['tmp'], writes=['fl'])
                P.op('dve', lambda e: e.tensor_tensor(out=dec[:], in0=Ec[:], in1=Enc[:], op=ALU.subtract),
                     reads=['Ec', 'Enc'], writes=['dec'])
                P.op('act', lambda e: e.activation(out=dec[:], in_=dec[:], func=AF.Exp), reads=['dec'], writes=['dec'])
                P.op('dve', lambda e: e.tensor_copy(out=decrep[:], in_=dec[:].unsqueeze(2).broadcast_to([NCH, 4, 128])),
                     reads=['dec'], writes=['decrep'])
                for src, dst, nm in ((uu, uT, 'uT'), (uu2, u2T, 'u2T'), (fl, flT, 'flT')):
                    for hh in range(4):
                        P.op('pe', lambda e: e.transpose(out=pa[0:64, hh * 128:hh * 128 + NCH], in_=src[:, hh, :],
                                                         identity=identf[0:NCH, 0:NCH]),
                             reads=['uu', 'uu2', 'fl', 'identf'], writes=['pa'])
                    P.op('dve', lambda e: e.tensor_copy(out=dst[:], in_=pa[0:64, :].rearrange("p (h c) -> p h c", h=4)[:, :, 0:NCH]),
                         reads=['pa'], writes=[nm])
                for hh in range(4):
                    P.op('pe', lambda e: e.matmul(pb[:, hh * 128:hh * 128 + NCH], lhsT=decrep[:, hh, :],
                                                  rhs=identf[0:NCH, 0:NCH], start=True, stop=True),
                         reads=['decrep', 'identf'], writes=['pb'])
                P.op('dve', lambda e: e.tensor_copy(out=decB[:], in_=pb[:].rearrange("p (h c) -> p h c", h=4)[:, :, 0:NCH]),
                     reads=['pb'], writes=['decB'])
                self.barrier()
            NG = S // 512
            qg = [sb(st, "qg", [128, 4, 512], BF16) for _ in range(2)]
            kg = [sb(st, "kg", [128, 4, 512], BF16) for _ in range(2)]
            vg = [sb(st, "vg", [64, 8, 4, 129], BF16) for _ in range(2)]
            ogg = [sb(st, "ogg", [64, 8, 512], BF16) for _ in range(2)]
            ym = [sb(st, "ym", [64, 8, 512], BF16) for _ in range(2)]
            G = [sb(st, "G", [128, 129], F32) for _ in range(4)]
            Gb = [sb(st, "Gb", [128, 129], BF16) for _ in range(4)]
            ku2 = [sb(st, "ku2", [64, 128], BF16) for _ in range(2)]
            Sm = [sb(st, "Sm", [64, 64], BF16) for _ in range(2)]
            Smu = [sb(st, "Smu", [64, 64], F32) for _ in range(2)]
            junk = sb(st, "junkm", [64, 128], BF16)
            mhalf = sb(st, "mhalf", [64, 4], F32)
            P.op('pool', lambda e: e.memset(mhalf[:], -0.5), writes=['mhalf'])
            osb = [sb(st, "osb", [64, 4, 129], F32) for _ in range(2)]
            ssq = [sb(st, "ssq", [64, 4], F32) for _ in range(2)]
            nt = [sb(st, "nt", [64, 4, 4], F32) for _ in range(2)]
            ytmp = sb(st, "ytmp", [64, 4, 128], F32)
            sc = [sb(st, "scm", [64, 8], F32) for _ in range(2)]
            pkT = [ps(st, "pkT", [128, 1024], BF16) for _ in range(2)]
            pS = [ps(st, "pS", [128, 512], F32) for _ in range(2)]
            pO = [ps(st, "pO", [128, 512], F32) for _ in range(2)]
            pG = [ps(st, "pG", [128, 512], F32) for _ in range(2)]
            for i in range(2):
                P.op('pool', lambda e: e.memset(vg[i][:], 1.0), writes=[('vg', i)])
            for hh in range(4):
                P.op('pool', lambda e: e.memset(G[hh][:], 0.0), writes=[('G', hh)])
                P.op('pool', lambda e: e.memset(Gb[hh][:], 0.0), writes=[('Gb', hh)])

            def load_group(g):
                i = g % 2
                tk = slice(g * 512, (g + 1) * 512)
                P.dma('sp', qg[i][:], self.qkT[0:512, tk].rearrange("(h p) t -> p h t", p=128),
                      reads=[('qkT', ft, g) for ft in range(4)], writes=[('qg', i)])
                P.dma('sp', kg[i][:], self.qkT[512:1024, tk].rearrange("(h p) t -> p h t", p=128),
                      reads=[('qkT', ft, g) for ft in range(4, 8)], writes=[('kg', i)])
                for hh in range(4):
                    P.dma('sp', vg[i][:, :, hh, 0:128],
                          self.mv_d[tk, hh * 128:(hh + 1) * 128].rearrange("(c s) e -> s c e", s=64),
                          reads=[('mv_d', 4 * g + j) for j in range(4)], writes=[('vg', i)])
                P.dma('sp', ogg[i][:], self.og_d[tk, :].rearrange("(c s) e -> s c e", s=64),
                      reads=[('og_d', 4 * g + j) for j in range(4)], writes=[('ogg', i)])
            load_group(0)
            steps = [(g, cl, hh) for g in range(NG) for cl in range(8) for hh in range(4)]

            def stageA(n):
                g, cl, hh = steps[n]
                gi_ = g % 2
                c = g * 8 + cl
                i2 = n % 2
                k_ = kg[gi_][:, hh, cl * 64:(cl + 1) * 64]
                q_ = qg[gi_][:, hh, cl * 64:(cl + 1) * 64]
                P.op('pe', lambda e: e.transpose(out=pkT[i2][0:64, 0:128], in_=k_, identity=identb[:]),
                     reads=[('kg', gi_), 'identb'], writes=[('pkT', i2)])
                P.op('pe', lambda e: e.matmul(pS[i2][0:64, 0:64], lhsT=k_, rhs=q_, start=True, stop=True),
                     reads=[('kg', gi_), ('qg', gi_)], writes=[('pS', i2)])
                P.op('act', lambda e: e.activation(out=ku2[i2][:], in_=pkT[i2][0:64, 0:128], func=AF.Copy,
                                                   scale=u2T[:, hh, c:c + 1]),
                     reads=[('pkT', i2), 'u2T'], writes=[('ku2', i2)])
                P.op('act', lambda e: e.activation(out=Smu[i2][:], in_=pS[i2][0:64, 0:64], func=AF.Copy,
                                                   scale=uT[:, hh, c:c + 1]),
                     reads=[('pS', i2), 'uT'], writes=[('Smu', i2)])
                P.op('pool', lambda e: e.tensor_tensor(out=Sm[i2][:], in0=Smu[i2][:], in1=tri[0:64, 0:64], op=ALU.mult),
                     reads=[('Smu', i2), 'tri'], writes=[('Sm', i2)])

            def stageB(n):
                g, cl, hh = steps[n]
                gi_ = g % 2
                c = g * 8 + cl
                i2 = n % 2
                q_ = qg[gi_][:, hh, cl * 64:(cl + 1) * 64]
                v_ = vg[gi_][:, cl, hh, :]
                P.op('pe', lambda e: e.matmul(pO[i2][0:64, 0:129], lhsT=Sm[i2][:], rhs=v_, start=True, stop=False),
                     reads=[('Sm', i2), ('vg', gi_)], writes=[('pO', i2)])
                P.op('pe', lambda e: e.matmul(pO[i2][0:64, 0:129], lhsT=q_, rhs=Gb[hh][:], start=False, stop=True),
                     reads=[('qg', gi_), ('Gb', hh)], writes=[('pO', i2)])
                P.op('pe', lambda e: e.matmul(pG[i2][:, 0:129], lhsT=ku2[i2][:], rhs=v_, start=True, stop=True),
                     reads=[('ku2', i2), ('vg', gi_)], writes=[('pG', i2)])
                P.op('dve', lambda e: e.scalar_tensor_tensor(out=G[hh][:], in0=G[hh][:], scalar=decB[:, hh, c:c + 1],
                                                             in1=pG[i2][:, 0:129], op0=ALU.mult, op1=ALU.add),
                     reads=[('G', hh), 'decB', ('pG', i2)], writes=[('G', hh)])
                P.op('dve', lambda e: e.tensor_copy(out=Gb[hh][:], in_=G[hh][:]), reads=[('G', hh)], writes=[('Gb', hh)])
                cb = c % 2
                P.op('act', lambda e: e.activation(out=junk[:], in_=pO[i2][0:64, 0:128], func=AF.Square,
                                                   scale=128.0 ** -0.5, accum_out=ssq[cb][:, hh:hh + 1]),
                     reads=[('pO', i2)], writes=['junk', ('ssq', cb, hh)])
                P.op('act', lambda e: e.copy(out=osb[cb][:, hh, :], in_=pO[i2][0:64, 0:129]),
                     reads=[('pO', i2)], writes=[('osb', cb, hh)])

            def stageN(n):
                g, cl, hh = steps[n]
                if hh != 3:
                    return
                gi_ = g % 2
                c = g * 8 + cl
                cb = c % 2
                t_ = nt[cb]
                tk = ('nt', cb)
                den = osb[cb][:, :, 128]
                ok = [('osb', cb, h_) for h_ in range(4)]
                P.op('dve', lambda e: e.tensor_scalar(out=t_[:, 0, :], in0=den, scalar1=-1.0, scalar2=None, op0=ALU.mult),
                     reads=ok, writes=[tk])
                P.op('dve', lambda e: e.tensor_tensor(out=t_[:, 0, :], in0=t_[:, 0, :], in1=flT[:, :, c], op=ALU.max),
                     reads=[tk, 'flT'], writes=[tk])
                P.op('dve', lambda e: e.tensor_tensor(out=t_[:, 0, :], in0=t_[:, 0, :], in1=den, op=ALU.max),
                     reads=[tk] + ok, writes=[tk])
                P.op('dve', lambda e: e.tensor_tensor(out=t_[:, 1, :], in0=t_[:, 0, :], in1=t_[:, 0, :], op=ALU.mult),
                     reads=[tk], writes=[tk])
                P.op('dve', lambda e: e.scalar_tensor_tensor(out=t_[:, 2, :], in0=t_[:, 1, :], scalar=EPS, in1=ssq[cb][:],
                                                             op0=ALU.mult, op1=ALU.add),
                     reads=[tk] + [('ssq', cb, h_) for h_ in range(4)], writes=[tk])
                P.op('pool', lambda e: e.tensor_tensor(out=t_[:, 3, :], in0=t_[:, 2, :], in1=mhalf[:], op=ALU.pow),
                     reads=[tk, 'mhalf'], writes=[tk])
                P.op('dve', lambda e: e.tensor_tensor(out=ytmp[:], in0=osb[cb][:, :, 0:128],
                                                      in1=t_[:, 3, :].unsqueeze(2).broadcast_to([64, 4, 128]), op=ALU.mult),
                     reads=ok + [tk], writes=['ytmp'])
                P.op('dve', lambda e: e.tensor_tensor(out=ym[gi_][:, cl, :].rearrange("p (h d) -> p h d", h=4), in0=ytmp[:],
                                                      in1=ogg[gi_][:, cl, :].rearrange("p (h d) -> p h d", h=4), op=ALU.mult),
                     reads=['ytmp', ('ogg', gi_)], writes=[('ym', gi_)])
                if cl == 7:
                    P.dma('sp', self.mix_d[g * 512:(g + 1) * 512, 0:512].rearrange("(c s) e -> s c e", s=64), ym[gi_][:],
                          reads=[('ym', gi_)], writes=[('mixm', g)])
            NS = len(steps)
            stageA(0)
            for n in range(NS):
                g, cl, hh = steps[n]
                if n + 1 < NS:
                    stageA(n + 1)
                stageB(n)
                if n >= 1:
                    stageN(n - 1)
                if cl == 0 and hh == 1 and g + 1 < NG:
                    load_group(g + 1)
            stageN(NS - 1)
            self.barrier()

    def _negB(self, st, name, g1, g2):
        P = self.P
        ga = self.sb(st, name + "_ga", [128, 64], F32)
        gb = self.sb(st, name + "_gb", [128, 64], F32)
        m1 = self.sb(st, name + "_m1", [128, 1], F32)
        m2 = self.sb(st, name + "_m2", [128, 1], F32)
        nb = self.sb(st, name, [128, 1], F32)
        P.dma('sp', ga[:], g1.partition_broadcast(128), writes=[name + 'ga'])
        P.dma('sp', gb[:], g2.partition_broadcast(128), writes=[name + 'gb'])
        P.op('dve', lambda e: e.tensor_reduce(out=m1[:], in_=ga[:], axis=AX.X, op=ALU.max, apply_absolute_value=True),
             reads=[name + 'ga'], writes=[name + 'm1'])
        P.op('dve', lambda e: e.tensor_reduce(out=m2[:], in_=gb[:], axis=AX.X, op=ALU.max, apply_absolute_value=True),
             reads=[name + 'gb'], writes=[name + 'm2'])
        P.op('dve', lambda e: e.scalar_tensor_tensor(out=nb[:], in0=m1[:], scalar=-8.0, in1=m2[:], op0=ALU.mult, op1=ALU.mult),
             reads=[name + 'm1', name + 'm2'], writes=[name])
        return nb

    def phase2_moba(self, l):
        S, P, nc = self.S, self.P, self.nc
        NQC = S // 512
        NKT = S // 128
        NB = S // 256
        with ExitStack() as st:
            sb, ps = self.sb, self.ps
            identb = sb(st, "identb_b", [128, 128], BF16)
            tm4 = sb(st, "tm4", [128, 4, 512], BF16)
            zl = sb(st, "zl", [1, 128], BF16)
            zr = sb(st, "zr", [1, 512], BF16)
            P.dma('sp', identb[:], self.c_identb, writes=['identb'])
            P.dma('sp', tm4[:], self.c_tm4, writes=['tm4'])
            P.op('pool', lambda e: e.memset(zl[:], 0.0), writes=['zl'])
            P.op('pool', lambda e: e.memset(zr[:], 0.0), writes=['zr'])
            negB = self._negB(st, "negBm", self.moba_qk_norm[l, 0], self.moba_qk_norm[l, 1])
            KX = sb(st, "KX", [96, S], BF16)
            VX = sb(st, "VX", [128, NKT, 65], BF16)
            kmean = sb(st, "kmean", [64, 32], F32)
            kmb = sb(st, "kmb", [64, 32], BF16)
            QX = [sb(st, "QX", [96, 512], BF16) for _ in range(2)]
            gsb = sb(st, "gsb_b", [128, 32], F32)
            m8 = sb(st, "m8", [128, 8], F32)
            sel = sb(st, "sel_b", [128, 32], F32)
            MBw = sb(st, "MBw", [128, 128], BF16)
            MLA = int(os.environ.get("LA", "3"))
            PT = [sb(st, "PT", [128, 512], BF16) for _ in range(MLA + 2)]
            rz = sb(st, "rz_b", [128, 4], F32)
            yb = [sb(st, "yb", [128, 4, 64], BF16) for _ in range(2)]
            pS = [ps(st, "pS_b", [128, 512], F32) for _ in range(MLA + 1)]
            pO = [ps(st, "pO_b", [128, 512], F32) for _ in range(2)]
            pM = ps(st, "pM_b", [128, 512], F32)
            pMb = ps(st, "pMb_b", [128, 1024], BF16)
            P.dma('sp', KX[64:96, :], self.c_ind32, writes=['KXi'])
            P.op('pool', lambda e: e.memset(VX[:], 1.0), writes=['VX'])
            P.op('pool', lambda e: e.memset(MBw[:], 0.0), writes=['MBw'])
            P.op('pool', lambda e: e.memset(kmean[:], 0.0), writes=['kmean'])
            all_tiles = list(range(S // 128))
            si = 0
            qi = 0
            for h in range(4):
                P.dma('sp', KX[0:64, :], self.bkT[h * 64:(h + 1) * 64, :], reads=[('bkT', j) for j in all_tiles], writes=['KX'])
                self.dma_mid(VX[:, :, 0:64], self.bv_d[:, h * 64:(h + 1) * 64].rearrange("(kt p) d -> p kt d", p=128), NKT, 8,
                             reads=[('bv_d', j) for j in all_tiles], writes=['VX'])
                P.op('dve', lambda e: e.tensor_reduce(out=kmean[:, 0:NB], in_=KX[0:64, :].rearrange("p (n k) -> p n k", k=256),
                                                      axis=AX.X, op=ALU.add), reads=['KX'], writes=['kmean'])
                P.op('dve', lambda e: e.tensor_scalar(out=kmb[:], in0=kmean[:], scalar1=1.0 / 256, scalar2=None, op0=ALU.mult),
                     reads=['kmean'], writes=['kmb'])
                def prep(qc, Q_, qk_):
                    q0 = qc * 512
                    P.dma('sp', Q_[0:64, :], self.bqT[h * 64:(h + 1) * 64, q0:q0 + 512],
                          reads=[('bqT', 4 * qc + j) for j in range(4)], writes=[qk_])
                    yield
                    for j in range(4):
                        own = 2 * qc + j // 2
                        P.op('pe', lambda e: e.matmul(pM[:, 0:32], lhsT=Q_[0:64, j * 128:(j + 1) * 128], rhs=kmb[:],
                                                      start=True, stop=True), reads=[qk_, 'kmb'], writes=['pM'])
                        yield
                        P.op('pool', lambda e: e.memset(gsb[:], -1e30), writes=['gsb'])
                        yield
                        if own > 0:
                            P.op('dve', lambda e: e.tensor_copy(out=gsb[:, 0:own], in_=pM[:, 0:own]), reads=['pM'], writes=['gsb'])
                            yield
                        yield
                        P.op('dve', lambda e: e.max(out=m8[:], in_=gsb[:]), reads=['gsb'], writes=['m8'])
                        yield
                        P.op('dve', lambda e: e.tensor_scalar(out=sel[:], in0=gsb[:], scalar1=m8[:, 2:3], scalar2=None,
                                                              op0=ALU.is_ge), reads=['gsb', 'm8'], writes=['sel'])
                        yield
                        yield
                        P.op('dve', lambda e: e.tensor_scalar(out=MBw[:, 64:96], in0=sel[:], scalar1=-NEGM, scalar2=NEGM,
                                                              op0=ALU.mult, op1=ALU.add), reads=['sel'], writes=['MBw'])
                        yield
                        P.op('dve', lambda e: e.memset(MBw[:, 64 + own:65 + own], 0.0), writes=['MBw'])
                        yield
                        if own + 1 < 32:
                            P.op('dve', lambda e: e.memset(MBw[:, 65 + own:96], NEGM), writes=['MBw'])
                            yield
                        yield
                        P.op('pe', lambda e: e.transpose(out=pMb[:, 0:128], in_=MBw[:], identity=identb[:]),
                             reads=['MBw', 'identb'], writes=['pMb'])
                        yield
                        P.op('act', lambda e: e.copy(out=Q_[64:96, j * 128:(j + 1) * 128], in_=pMb[64:96, 0:128]),
                             reads=['pMb'], writes=[qk_])
                        yield
                        yield
                for _ in prep(0, QX[qi % 2], ('QX', qi % 2)):
                    pass
                for qc in range(NQC):
                    Q_ = QX[qi % 2]
                    qk_ = ('QX', qi % 2)
                    qi += 1
                    q0 = qc * 512
                    gp = prep(qc + 1, QX[qi % 2], ('QX', qi % 2)) if qc + 1 < NQC else None
                    pacc = 0.0
                    prate = 52.0 / (4 * qc + 4)
                    po = pO[qc % 2]
                    pok = ('pO', qc % 2)
                    P.op('pe', lambda e: e.matmul(po[:, 0:260], lhsT=zl[:], rhs=zr[:, 0:260], start=True, stop=True,
                                                  skip_group_check=True), reads=['zl', 'zr'], writes=[pok])
                    nkt = 4 * qc + 4
                    LA = MLA
                    base = si

                    def qkmm(kt):
                        pp = pS[(base + kt) % (MLA + 1)]
                        P.op('pe', lambda e: e.matmul(pp[:], lhsT=KX[:, kt * 128:(kt + 1) * 128], rhs=Q_[:], start=True, stop=True),
                             reads=['KX', 'KXi', qk_], writes=[('pS', (base + kt) % (MLA + 1))])
                    for kt in range(min(LA, nkt)):
                        qkmm(kt)
                    for kt in range(nkt):
                        if kt + LA < nkt:
                            qkmm(kt + LA)
                        p_ = pS[si % (MLA + 1)]
                        pk = ('pS', si % (MLA + 1))
                        t_ = PT[si % (MLA + 2)]
                        tk = ('PT', si % (MLA + 2))
                        si += 1
                        if LA == 0:
                            qkmm(kt)
                        P.op('act', lambda e: e.activation(out=t_[:], in_=p_[:], func=AF.Exp, bias=negB[:, 0:1], scale=0.125),
                             reads=[pk, 'negBm'], writes=[tk])
                        off = kt - 4 * qc
                        if off >= 0:
                            P.op('dve', lambda e: e.tensor_tensor(out=t_[:], in0=t_[:], in1=tm4[:, off, :], op=ALU.mult),
                                 reads=[tk, 'tm4'], writes=[tk])
                        for j in range(4):
                            if kt > 4 * qc + j:
                                continue
                            P.op('pe', lambda e: e.matmul(po[:, j * 65:(j + 1) * 65], lhsT=t_[:, j * 128:(j + 1) * 128],
                                                          rhs=VX[:, kt, :], start=False, stop=(kt == 4 * qc + j),
                                                          skip_group_check=True), reads=[tk, 'VX'], writes=[pok])
                        if gp is not None:
                            pacc += prate
                            while pacc >= 1.0 and gp is not None:
                                pacc -= 1.0
                                try:
                                    next(gp)
                                except StopIteration:
                                    gp = None
                    if gp is not None:
                        for _ in gp:
                            pass
                    y_ = yb[qc % 2]
                    yk = ('yb', qc % 2)
                    pov = po[:, 0:260].rearrange("p (j d) -> p j d", d=65)
                    P.op('dve', lambda e: e.reciprocal(out=rz[:], in_=pov[:, :, 64]), reads=[pok], writes=['rz'])
                    P.op('dve', lambda e: e.tensor_tensor(out=y_[:], in0=pov[:, :, 0:64],
                                                          in1=rz[:].unsqueeze(2).broadcast_to([128, 4, 64]), op=ALU.mult),
                         reads=[pok, 'rz'], writes=[yk])
                    P.dma('sp', self.mix_d[q0:q0 + 512, 512 + h * 64:512 + (h + 1) * 64].rearrange("(j p) d -> p j d", p=128),
                          y_[:], reads=[yk], writes=[('mixb', h, qc)])
            self.barrier()

    def phase2_nsa(self, l):
        S, P, nc = self.S, self.P, self.nc
        NT = S // 128
        Nc = S // 16 - 1
        NCT = max(1, S // 2048)
        with ExitStack() as st:
            sb, ps = self.sb, self.ps
            identb = sb(st, "identb_n", [128, 128], BF16)
            trib = sb(st, "trib", [128, 128], BF16)
            triw = sb(st, "triw", [128, 128], BF16)
            cmask = sb(st, "cmask", [128, 17, 128], BF16)
            OVL = sb(st, "OVL", [128, NCT, 128], BF16)
            cols = sb(st, "cols", [128, 4], F32)
            zl = sb(st, "zl_n", [1, 128], BF16)
            zr = sb(st, "zr_n", [1, 512], BF16)
            NG = sb(st, "NG", [128, NT, 12], F32)
            P.dma('sp', identb[:], self.c_identb, writes=['identb'])
            P.dma('sp', trib[:], self.c_trib, writes=['trib'])
            P.dma('sp', triw[:], self.c_triw, writes=['triw'])
            P.dma('sp', cmask[:], self.c_cmask, writes=['cmask'])
            P.dma('sp', OVL[:], self.c_ovl.rearrange("(ct p) n -> p ct n", p=128)[:, 0:NCT, :], writes=['OVL'])
            P.dma('sp', cols[:], self.c_cols, writes=['cols'])
            P.op('pool', lambda e: e.memset(zl[:], 0.0), writes=['zl'])
            P.op('pool', lambda e: e.memset(zr[:], 0.0), writes=['zr'])
            allt = list(range(NT))
            self.dma_mid(NG[:], self.ng_d.rearrange("(j p) g -> p j g", p=128), NT, 8, reads=[('ng_d', j) for j in allt], writes=['NG'])
            negBc = self._negB(st, "negBc", self.nsa_q_norm[l], self.nsa_k_norm[l, 0])
            negBs = self._negB(st, "negBs", self.nsa_q_norm[l], self.nsa_k_norm[l, 1])
            negBw = self._negB(st, "negBw", self.nsa_q_norm[l], self.nsa_k_norm[l, 2])
            KSX = sb(st, "KSX", [128, S], BF16)
            KWX = sb(st, "KWX", [64, S], BF16)
            VSX = sb(st, "VSX", [128, NT, 65], BF16)
            VWX = sb(st, "VWX", [128, NT, 65], BF16)
            KcT = sb(st, "KcT", [64, NCT * 128], BF16)
            VCX = sb(st, "VCX", [128, NCT, 65], BF16)
            P.dma('sp', KSX[0:64, :], self.ksT, reads=[('ksT', j) for j in allt], writes=['KSX'])
            P.dma('sp', KSX[64:128, :], self.c_ind64, writes=['KSXi'])
            P.dma('sp', KWX[:], self.kwT, reads=[('kwT', j) for j in allt], writes=['KWX'])
            for V_, src, nm in ((VSX, self.vs_d, 'vs_d'), (VWX, self.vw_d, 'vw_d')):
                P.op('pool', lambda e: e.memset(V_[:], 1.0), writes=[nm + 'X'])
                self.dma_mid(V_[:, :, 0:64], src.rearrange("(kt p) d -> p kt d", p=128), NT, 8,
                             reads=[(nm, j) for j in allt], writes=[nm + 'X'])
            P.op('pool', lambda e: e.memset(VCX[:], 1.0), writes=['VCX'])
            pS = [ps(st, "pS_n", [128, 512], F32) for _ in range(2)]
            pOc = ps(st, "pOc", [128, 512], F32)
            pU = ps(st, "pU", [128, 512], F32)
            pOs = ps(st, "pOs", [128, 512], F32)
            pOw = ps(st, "pOw", [128, 512], F32)
            pMb = ps(st, "pMb_n", [128, 1024], BF16)
            pM = ps(st, "pM_n", [128, 512], F32)
            with ExitStack() as s2:
                KCV = sb(s2, "KCV", [128, S], BF16)
                W1s = sb(s2, "W1s", [128, 32, 128], F32)
                W1 = sb(s2, "W1", [128, 32, 128], BF16)
                pes = sb(s2, "pes", [32, 128], F32)
                peb = sb(s2, "peb", [32, 128], BF16)
                peT = sb(s2, "peT", [128, 32], BF16)
                w2s = sb(s2, "w2s", [128, 2, 64], F32)
                w2 = sb(s2, "w2", [128, 2, 64], BF16)
                gk0 = sb(s2, "gk0", [128, 64], F32)
                bias = sb(s2, "bias_c", [128, 2], F32)
                hidb = sb(s2, "hidb", [128, NCT * 128], BF16)
                kc32 = sb(s2, "kc32", [128, 64], F32)
                kcn = sb(s2, "kcn", [128, 64], BF16)
                junk = sb(s2, "junk_c", [128, 64], F32)
                ssc = sb(s2, "ssc", [128, 2], F32)
                P.dma('sp', KCV[:], self.kcvcT, reads=[('kcvcT', b) for b in range(S // 512)], writes=['KCV'])
                for br in range(2):
                    P.dma('sp', W1s[64 * br:64 * br + 64], self.cmp_w1[l, br].rearrange("(r d) j -> d r j", d=64), writes=['W1s'])
                    P.dma('sp', w2s[:, br, :], self.cmp_w2[l, br], writes=['w2s'])
                    P.dma('sp', pes[:, 64 * br:64 * br + 64], self.cmp_pe[l, br], writes=['pes'])
                P.dma('sp', gk0[:], self.nsa_k_norm[l, 0].partition_broadcast(128), writes=['gk0'])
                P.op('dve', lambda e: e.tensor_copy(out=W1[:], in_=W1s[:]), reads=['W1s'], writes=['W1'])
                P.op('dve', lambda e: e.tensor_copy(out=peb[:], in_=pes[:]), reads=['pes'], writes=['peb'])
                P.op('pe', lambda e: e.transpose(out=pMb[:, 0:32], in_=peb[:], identity=identb[0:32, 0:32]),
                     reads=['peb', 'identb'], writes=['pMb'])
                P.op('dve', lambda e: e.tensor_copy(out=peT[:], in_=pMb[:, 0:32]), reads=['pMb'], writes=['peT'])
                P.op('dve', lambda e: e.tensor_copy(out=w2[:], in_=w2s[:]), reads=['w2s'], writes=['w2'])
                P.op('pool', lambda e: e.memset(hidb[:], 0.0), writes=['hidb'])
                for br in range(2):
                    rows = slice(64 * br, 64 * br + 64)
                    kview = KCV[rows, :].rearrange("p (c s) -> p c s", s=16)
                    for r in range(32):
                        P.op('pe', lambda e: e.matmul(pM[:, 0:1], lhsT=W1[rows, r, :], rhs=peT[rows, r:r + 1],
                                                      start=(r == 0), stop=(r == 31)), reads=['W1', 'peT'], writes=['pM'])
                    P.op('dve', lambda e: e.tensor_copy(out=bias[:, br:br + 1], in_=pM[:, 0:1]), reads=['pM'], writes=['bias'])
                    for r in range(32):
                        rhs = kview[:, 0:Nc, r] if r < 16 else kview[:, 1:Nc + 1, r - 16]
                        P.op('pe', lambda e: e.matmul(pS[0][:, 0:Nc], lhsT=W1[rows, r, :], rhs=rhs,
                                                      start=(r == 0), stop=(r == 31)), reads=['W1', 'KCV'], writes=[('pS', 0)])
                    P.op('act', lambda e: e.activation(out=hidb[:, 0:Nc], in_=pS[0][:, 0:Nc], func=AF.Silu, bias=bias[:, br:br + 1]),
                         reads=[('pS', 0), 'bias'], writes=['hidb'])
                    for ct in range(NCT):
                        P.op('pe', lambda e: e.matmul(pM[:, 0:64], lhsT=hidb[:, ct * 128:(ct + 1) * 128], rhs=w2[:, br, :],
                                                      start=True, stop=True), reads=['hidb', 'w2'], writes=['pM'])
                        if br == 0:
                            P.op('act', lambda e: e.activation(out=junk[:], in_=pM[:, 0:64], func=AF.Square, scale=0.125,
                                                               accum_out=ssc[:, 0:1]), reads=['pM'], writes=['junk_c', 'ssc'])
                            P.op('dve', lambda e: e.tensor_scalar(out=ssc[:, 0:1], in0=ssc[:, 0:1], scalar1=EPS, scalar2=None,
                                                                  op0=ALU.add), reads=['ssc'], writes=['ssc'])
                            P.op('act', lambda e: e.activation(out=ssc[:, 0:1], in_=ssc[:, 0:1], func=AF.Sqrt), reads=['ssc'], writes=['ssc'])
                            P.op('dve', lambda e: e.reciprocal(out=ssc[:, 1:2], in_=ssc[:, 0:1]), reads=['ssc'], writes=['ssc'])
                            P.op('dve', lambda e: e.scalar_tensor_tensor(out=kcn[:], in0=pM[:, 0:64], scalar=ssc[:, 1:2], in1=gk0[:],
                                                                         op0=ALU.mult, op1=ALU.mult),
                                 reads=['pM', 'ssc', 'gk0'], writes=['kcn'])
                            P.op('pe', lambda e: e.transpose(out=pMb[0:64, 0:128], in_=kcn[:], identity=identb[:]),
                                 reads=['kcn', 'identb'], writes=['pMb'])
                            P.op('act', lambda e: e.copy(out=KcT[:, ct * 128:(ct + 1) * 128], in_=pMb[0:64, 0:128]),
                                 reads=['pMb'], writes=['KcT'])
                        else:
                            P.op('act', lambda e: e.copy(out=VCX[:, ct, 0:64], in_=pM[:, 0:64]), reads=['pM'], writes=['VCX'])
                self.barrier()
            QU = [sb(st, "QU", [64, 512], BF16) for _ in range(2)]
            QR0 = [sb(st, "QR0", [128, 512], BF16) for _ in range(2)]
            QR1 = [sb(st, "QR1", [128, 512], BF16) for _ in range(2)]
            PTc = [sb(st, "PTc", [128, 512], BF16) for _ in range(NCT)]
            PT = [sb(st, "PTn", [128, 512], BF16) for _ in range(3)]
            zz = sb(st, "zz", [128, 3, 4], F32)
            rzz = sb(st, "rzz", [128, 3, 4], F32)
            coef = sb(st, "coef", [128, 3, 4], F32)
            imp = sb(st, "imp", [128, 128], F32)
            work = sb(st, "work", [128, 128], F32)
            m8a = sb(st, "m8a", [128, 8], F32)
            m8b = sb(st, "m8b", [128, 8], F32)
            selm = sb(st, "selm", [128, 128], F32)
            MB = sb(st, "MB", [128, 128], BF16)
            MBs = sb(st, "MBs", [128, 128], BF16)
            yacc = sb(st, "yacc", [128, 4, 64], F32)
            yn = [sb(st, "yn", [128, 256], BF16) for _ in range(2)]
            si = 0

            def seed(t, n, key):
                P.op('pe', lambda e: e.matmul(t[:, 0:n], lhsT=zl[:], rhs=zr[:, 0:n], start=True, stop=True, skip_group_check=True),
                     reads=['zl', 'zr'], writes=[key])

            def hb(ap):
                return ap.unsqueeze(1).broadcast_to([ap.shape[0], 4, 128])

            for m in range(NT):
                t0 = m * 128
                b2 = m % 2
                qu, qr0, qr1 = QU[b2], QR0[b2], QR1[b2]
                P.dma('sp', qu[:].rearrange("p (h t) -> p h t", h=4), self.nquT[:, t0:t0 + 128].rearrange("(h d) t -> d h t", d=64),
                      reads=[('nquT', m)], writes=[('QU', b2)])
                P.dma('sp', qr0[0:64, :].rearrange("p (h t) -> p h t", h=4), self.nqrT[:, t0:t0 + 128].rearrange("(h d) t -> d h t", d=64),
                      reads=[('nqrT', m)], writes=[('QR0', b2)])
                use_g1 = (2 * m + 1) >= 64
                if use_g1:
                    P.dma('sp', qr1[0:64, :].rearrange("p (h t) -> p h t", h=4),
                          self.nqrT[:, t0:t0 + 128].rearrange("(h d) t -> d h t", d=64), reads=[('nqrT', m)], writes=[('QR1', b2)])
                ctn = min(NCT, (8 * m + 6) // 128 + 1)
                for ct in range(ctn):
                    p_ = pS[si % 2]
                    pk = ('pS', si % 2)
                    si += 1
                    P.op('pe', lambda e: e.matmul(p_[:], lhsT=KcT[:, ct * 128:(ct + 1) * 128], rhs=qu[:], start=True, stop=True),
                         reads=['KcT', ('QU', b2)], writes=[pk])
                    P.op('act', lambda e: e.activation(out=PTc[ct][:], in_=p_[:], func=AF.Exp, bias=negBc[:, 0:1], scale=0.125),
                         reads=[pk, 'negBc'], writes=[('PTc', ct)])
                    r = m - 16 * ct
                    if r <= 16:
                        P.op('pool', lambda e: e.tensor_tensor(out=PTc[ct][:].rearrange("p (h t) -> p h t", h=4),
                                                               in0=PTc[ct][:].rearrange("p (h t) -> p h t", h=4),
                                                               in1=hb(cmask[:, r, :]), op=ALU.mult),
                             reads=[('PTc', ct), 'cmask'], writes=[('PTc', ct)])
                seed(pOc, 260, 'pOc')
                seed(pU, 512, 'pU')
                for h in range(4):
                    for ct in range(ctn):
                        P.op('pe', lambda e: e.matmul(pOc[:, h * 65:(h + 1) * 65], lhsT=PTc[ct][:, h * 128:(h + 1) * 128],
                                                      rhs=VCX[:, ct, :], start=False, stop=(ct == ctn - 1), skip_group_check=True),
                             reads=[('PTc', ct), 'VCX'], writes=['pOc'])
                        P.op('pe', lambda e: e.matmul(pU[:, h * 128:(h + 1) * 128], lhsT=PTc[ct][:, h * 128:(h + 1) * 128],
                                                      rhs=OVL[:, ct, :], start=False, stop=(ct == ctn - 1), skip_group_check=True),
                             reads=[('PTc', ct), 'OVL'], writes=['pU'])
                pocv = pOc[:, 0:260].rearrange("p (h d) -> p h d", d=65)
                P.op('dve', lambda e: e.tensor_scalar(out=zz[:, 0, :], in0=pocv[:, :, 64], scalar1=1e-30, scalar2=None, op0=ALU.max),
                     reads=['pOc'], writes=['zz0'])
                P.op('dve', lambda e: e.reciprocal(out=rzz[:, 0, :], in_=zz[:, 0, :]), reads=['zz0'], writes=['rzz0'])
                P.op('dve', lambda e: e.tensor_scalar(out=imp[:], in0=pU[:, 0:128], scalar1=rzz[:, 0, 0:1], scalar2=None, op0=ALU.mult),
                     reads=['pU', 'rzz0'], writes=['imp'])
                for h in range(1, 4):
                    P.op('dve', lambda e: e.scalar_tensor_tensor(out=imp[:], in0=pU[:, h * 128:(h + 1) * 128], scalar=rzz[:, 0, h:h + 1],
                                                                 in1=imp[:], op0=ALU.mult, op1=ALU.add),
                         reads=['pU', 'rzz0', 'imp'], writes=['imp'])
                n1 = 2 * m + 1
                if n1 + 1 < 128:
                    P.op('pool', lambda e: e.memset(imp[:, n1 + 1:128], -1e30), reads=['imp'], writes=['imp'])
                P.op('pool', lambda e: e.tensor_copy(out=imp[:, n1:n1 + 1], in_=cols[:, 0:1]), reads=['cols', 'imp'], writes=['imp'])
                P.op('pool', lambda e: e.memset(imp[:, n1 - 1:n1], 1e9), reads=['imp'], writes=['imp'])
                if n1 - 2 >= 0:
                    P.op('dve', lambda e: e.tensor_tensor(out=imp[:, n1 - 2:n1 - 1], in0=imp[:, n1 - 2:n1 - 1], in1=cols[:, 1:2], op=ALU.max),
                         reads=['cols', 'imp'], writes=['imp'])
                P.op('pool', lambda e: e.memset(imp[:, 0:1], 1e9), reads=['imp'], writes=['imp'])
                P.op('dve', lambda e: e.max(out=m8a[:], in_=imp[:]), reads=['imp'], writes=['m8a'])
                P.op('dve', lambda e: e.match_replace(out=work[:], in_to_replace=m8a[:], in_values=imp[:], imm_value=-1e30),
                     reads=['imp', 'm8a'], writes=['work'])
                P.op('dve', lambda e: e.max(out=m8b[:], in_=work[:]), reads=['work'], writes=['m8b'])
                P.op('dve', lambda e: e.tensor_scalar(out=selm[:], in0=imp[:], scalar1=m8b[:, 7:8], scalar2=None, op0=ALU.is_ge),
                     reads=['imp', 'm8b'], writes=['selm'])
                P.op('dve', lambda e: e.tensor_scalar(out=MB[:], in0=selm[:], scalar1=-NEGM, scalar2=NEGM, op0=ALU.mult, op1=ALU.add),
                     reads=['selm'], writes=['MB'])
                if n1 + 1 < 128:
                    P.op('pool', lambda e: e.memset(MB[:, n1 + 1:128], NEGM), reads=['MB'], writes=['MB'])
                P.op('pool', lambda e: e.tensor_copy(out=MB[:, n1:n1 + 1], in_=cols[:, 2:3]), reads=['cols', 'MB'], writes=['MB'])
                P.op('pool', lambda e: e.tensor_copy(out=MBs[:, 0:64], in_=MB[:, 64:128]), reads=['MB'], writes=['MBs'])
                P.op('pool', lambda e: e.tensor_copy(out=MBs[:, 64:128], in_=MB[:, 0:64]), reads=['MB'], writes=['MBs'])
                P.op('pe', lambda e: e.transpose(out=pMb[:, 0:128], in_=MBs[:], identity=identb[:]), reads=['MBs', 'identb'], writes=['pMb'])
                P.op('act', lambda e: e.copy(out=qr0[64:128, :].rearrange("p (h t) -> p h t", h=4), in_=hb(pMb[64:128, 0:128])),
                     reads=['pMb'], writes=[('QR0', b2)])
                if use_g1:
                    P.op('pe', lambda e: e.transpose(out=pMb[:, 128:256], in_=MB[:], identity=identb[:]), reads=['MB', 'identb'], writes=['pMb'])
                    P.op('act', lambda e: e.copy(out=qr1[64:128, :].rearrange("p (h t) -> p h t", h=4), in_=hb(pMb[64:128, 128:256])),
                         reads=['pMb'], writes=[('QR1', b2)])
                seed(pOs, 260, 'pOs')
                for kt in range(m + 1):
                    g = kt // 32
                    qr_, qrk = (qr0, ('QR0', b2)) if g == 0 else (qr1, ('QR1', b2))
                    p_ = pS[si % 2]
                    pk = ('pS', si % 2)
                    t_ = PT[si % 3]
                    tk = ('PTn', si % 3)
                    si += 1
                    P.op('pe', lambda e: e.matmul(p_[:], lhsT=KSX[:, kt * 128:(kt + 1) * 128], rhs=qr_[:], start=True, stop=True),
                         reads=['KSX', 'KSXi', qrk], writes=[pk])
                    P.op('act', lambda e: e.activation(out=t_[:], in_=p_[:], func=AF.Exp, bias=negBs[:, 0:1], scale=0.125),
                         reads=[pk, 'negBs'], writes=[tk])
                    if kt == m:
                        P.op('pool', lambda e: e.tensor_tensor(out=t_[:].rearrange("p (h t) -> p h t", h=4),
                                                               in0=t_[:].rearrange("p (h t) -> p h t", h=4), in1=hb(trib[:]), op=ALU.mult),
                             reads=[tk, 'trib'], writes=[tk])
                    for h in range(4):
                        P.op('pe', lambda e: e.matmul(pOs[:, h * 65:(h + 1) * 65], lhsT=t_[:, h * 128:(h + 1) * 128], rhs=VSX[:, kt, :],
                                                      start=False, stop=(kt == m), skip_group_check=True),
                             reads=[tk, 'vs_dX'], writes=['pOs'])
                seed(pOw, 260, 'pOw')
                for kt in range(max(0, m - 4), m + 1):
                    p_ = pS[si % 2]
                    pk = ('pS', si % 2)
                    t_ = PT[si % 3]
                    tk = ('PTn', si % 3)
                    si += 1
                    P.op('pe', lambda e: e.matmul(p_[:], lhsT=KWX[:, kt * 128:(kt + 1) * 128], rhs=qr0[0:64, :], start=True, stop=True),
                         reads=['KWX', ('QR0', b2)], writes=[pk])
                    P.op('act', lambda e: e.activation(out=t_[:], in_=p_[:], func=AF.Exp, bias=negBw[:, 0:1], scale=0.125),
                         reads=[pk, 'negBw'], writes=[tk])
                    if kt == m or kt == m - 4:
                        mk = trib if kt == m else triw
                        P.op('pool', lambda e: e.tensor_tensor(out=t_[:].rearrange("p (h t) -> p h t", h=4),
                                                               in0=t_[:].rearrange("p (h t) -> p h t", h=4), in1=hb(mk[:]), op=ALU.mult),
                             reads=[tk, 'trib', 'triw'], writes=[tk])
                    for h in range(4):
                        P.op('pe', lambda e: e.matmul(pOw[:, h * 65:(h + 1) * 65], lhsT=t_[:, h * 128:(h + 1) * 128], rhs=VWX[:, kt, :],
                                                      start=False, stop=(kt == m), skip_group_check=True),
                             reads=[tk, 'vw_dX'], writes=['pOw'])
                posv = pOs[:, 0:260].rearrange("p (h d) -> p h d", d=65)
                powv = pOw[:, 0:260].rearrange("p (h d) -> p h d", d=65)
                P.op('dve', lambda e: e.tensor_copy(out=zz[:, 1, :], in_=posv[:, :, 64]), reads=['pOs'], writes=['zz1'])
                P.op('dve', lambda e: e.tensor_copy(out=zz[:, 2, :], in_=powv[:, :, 64]), reads=['pOw'], writes=['zz1'])
                P.op('dve', lambda e: e.reciprocal(out=rzz[:, 1:3, :], in_=zz[:, 1:3, :]), reads=['zz1'], writes=['rzz1'])
                P.op('dve', lambda e: e.tensor_tensor(out=coef[:], in0=rzz[:], in1=NG[:, m, :].rearrange("p (b h) -> p b h", b=3), op=ALU.mult),
                     reads=['rzz0', 'rzz1', 'NG'], writes=['coef'])
                for h in range(4):
                    P.op('dve', lambda e: e.tensor_scalar(out=yacc[:, h, :], in0=pocv[:, h, 0:64], scalar1=coef[:, 0, h:h + 1], scalar2=None,
                                                          op0=ALU.mult), reads=['pOc', 'coef'], writes=[('yacc', h)])
                    P.op('dve', lambda e: e.scalar_tensor_tensor(out=yacc[:, h, :], in0=posv[:, h, 0:64], scalar=coef[:, 1, h:h + 1],
                                                                 in1=yacc[:, h, :], op0=ALU.mult, op1=ALU.add),
                         reads=['pOs', 'coef', ('yacc', h)], writes=[('yacc', h)])
                    P.op('dve', lambda e: e.scalar_tensor_tensor(out=yn[b2][:, h * 64:(h + 1) * 64], in0=powv[:, h, 0:64],
                                                                 scalar=coef[:, 2, h:h + 1], in1=yacc[:, h, :], op0=ALU.mult, op1=ALU.add),
                         reads=['pOw', 'coef', ('yacc', h)], writes=[('yn', b2)])
                P.dma('sp', self.mix_d[t0:t0 + 128, 768:1024], yn[b2][:], reads=[('yn', b2)], writes=[('mixn', m)])
            self.barrier()

    def build_all(self):
        self.declare()
        for l in range(self.depth):
            xsrc = self.x_in if l == 0 else self.xbuf
            xdst = self.y_out if l == self.depth - 1 else self.xbuf
            self.phase1(l, xsrc)
            if os.environ.get("PH2", "old") == "new":
                self.phase2(l)
            elif os.environ.get("PH2", "old") == "b":
                self.phase2_moba(l)
                self.phase2b(l)
            else:
                self.phase2_mlstm(l)
                self.phase2_moba(l)
                self.phase2_nsa2(l)
            self.phase3(l, xsrc, xdst)
        self.P.finish()
        return self.nc


def make_consts(S):
    bf = ml_dtypes.bfloat16
    half = 8
    inv = np.exp(-math.log(500000.0) * np.arange(half, dtype=np.float32) * (2.0 / 16)).astype(np.float32)
    ang = np.arange(S, dtype=np.float32)[:, None] * inv[None, :]
    d = dict(c_identb=np.eye(128, dtype=np.float32).astype(bf), c_identf=np.eye(128, dtype=np.float32),
             c_cos=np.cos(ang).astype(np.float32), c_sin=np.sin(ang).astype(np.float32))
    i = np.arange(128)
    d['c_ut'] = (i[:, None] < i[None, :]).astype(np.float32)
    d['c_tri'] = (i[:, None] <= i[None, :]).astype(np.float32)
    key = np.arange(S)
    d['c_ind32'] = (key[None, :] // 256 == np.arange(32)[:, None]).astype(np.float32).astype(bf)
    d['c_ind64'] = (((key[None, :] // 64) % 64) == np.arange(64)[:, None]).astype(np.float32).astype(bf)
    k = np.arange(128)[:, None, None]
    o = np.arange(4)[None, :, None]
    q = np.arange(512)[None, None, :]
    d['c_tm4'] = (q >= k + 128 * o).astype(np.float32).astype(bf)
    d['c_trib'] = (i[:, None] <= i[None, :]).astype(np.float32).astype(bf)
    d['c_triw'] = (i[:, None] > i[None, :]).astype(np.float32).astype(bf)
    ii = np.arange(128)[:, None, None]
    r = np.arange(17)[None, :, None]
    j = np.arange(128)[None, None, :]
    d['c_cmask'] = (16 * ii + 31 <= 128 * r + j).astype(np.float32).astype(bf)
    c = np.arange(512)[:, None]
    n = np.arange(128)[None, :]
    Nc = S // 16 - 1
    d['c_ovl'] = ((c >= 4 * n - 1) & (c <= 4 * n + 3) & (c < Nc)).astype(np.float32).astype(bf)
    cols = np.zeros((128, 4), np.float32)
    cols[:64, 0] = -1e30
    cols[64:, 0] = 1e9
    cols[:64, 1] = 1e9
    cols[64:, 1] = -1e30
    cols[:64, 2] = NEGM
    cols[64:, 2] = 0.0
    d['c_cols'] = cols
    return d


_CACHE = {}


def kernel(x, w_in, b_if, conv_qk, m_norm, moba_qk_norm, nsa_q_norm, nsa_k_norm,
           cmp_pe, cmp_w1, cmp_w2, w_out, norm_mix, norm_ffn, w_ff1, w_ff2):
    x = np.asarray(x, dtype=np.float32)
    Bsz, S, _ = x.shape
    depth = int(np.asarray(w_in).shape[0])
    n_cores = 8
    shared = dict(w_in=w_in, b_if=b_if, conv_qk=conv_qk, m_norm=m_norm, moba_qk_norm=moba_qk_norm,
                  nsa_q_norm=nsa_q_norm, nsa_k_norm=nsa_k_norm, cmp_pe=cmp_pe, cmp_w1=cmp_w1, cmp_w2=cmp_w2,
                  w_out=w_out, norm_mix=norm_mix, norm_ffn=norm_ffn, w_ff1=w_ff1, w_ff2=w_ff2)
    shared = {k: np.ascontiguousarray(np.asarray(v, dtype=np.float32)) for k, v in shared.items()}
    shared.update(make_consts(S))
    B = Builder(S, depth)
    nc = B.build_all()
    in_maps = []
    for c in range(n_cores):
        m = dict(shared)
        m['x'] = np.ascontiguousarray(x[c % Bsz])
        in_maps.append(m)
    res = run_bass_kernel_spmd(nc, in_maps, core_ids=list(range(n_cores)))
    out = np.stack([np.asarray(res.results[b]["y"], dtype=np.float32) for b in range(Bsz)], axis=0)
    return out


class BG:
    def __init__(self):
        self.gens = []
        self.acc = 0.0

    def add(self, g):
        self.gens.append(g)

    def step(self, rate=1.0):
        self.acc += rate
        while self.acc >= 1.0:
            self.acc -= 1.0
            for g in list(self.gens):
                try:
                    next(g)
                except StopIteration:
                    self.gens.remove(g)

    def drain(self, g=None):
        if g is None:
            while self.gens:
                self.step(1.0)
        else:
            while g in self.gens:
                try:
                    next(g)
                except StopIteration:
                    self.gens.remove(g)


def _phase2(self, l):
    S, P, nc = self.S, self.P, self.nc
    NCH = S // 64
    NT = S // 128
    NQC = S // 512
    NB = S // 256
    Nc = S // 16 - 1
    NCT = max(1, S // 2048)
    LNSC = math.log(128.0 ** -0.5)
    sb, ps = self.sb, self.ps
    allt = list(range(NT))
    with ExitStack() as st:
        identb = sb(st, "identb2", [128, 128], BF16)
        identf = sb(st, "identf2", [128, 128], F32)
        ut = sb(st, "ut2", [128, 128], F32)
        tri = sb(st, "tri2", [128, 128], F32)
        zl = sb(st, "zl2", [1, 128], BF16)
        zr = sb(st, "zr2", [1, 512], BF16)
        P.dma('sp', identb[:], self.c_identb, writes=['identb'])
        P.dma('sp', identf[:], self.c_identf, writes=['identf'])
        P.dma('sp', ut[:], self.c_ut, writes=['ut'])
        P.dma('sp', tri[:], self.c_tri, writes=['tri'])
        P.op('pool', lambda e: e.memset(zl[:], 0.0), writes=['zl'])
        P.op('pool', lambda e: e.memset(zr[:], 0.0), writes=['zr'])
        uT = sb(st, "uT_m", [64, 4, NCH], F32)
        u2T = sb(st, "u2T_m", [64, 4, NCH], F32)
        flT = sb(st, "flT_m", [64, 4, NCH], F32)
        decB = sb(st, "decB", [128, 4, NCH], F32)
        with ExitStack() as s2:
            li = sb(s2, "li", [NCH, 4, 64], F32)
            lf = sb(s2, "lf", [NCH, 4, 64], F32)
            ones = sb(s2, "ones", [NCH, 64], F32)
            Fin = sb(s2, "Fin", [NCH, 4, 64], F32)
            Ft = sb(s2, "Ft", [NCH, 4, 64], F32)
            a_ = sb(s2, "a_", [NCH, 4, 64], F32)
            Ain = sb(s2, "Ain", [NCH, 4, 64], F32)
            tot = sb(s2, "tot", [NCH, 4], F32)
            cmax = sb(s2, "cmax", [NCH, 4], F32)
            cmT = sb(s2, "cmT", [4, NCH], F32)
            ET = sb(s2, "ET", [4, NCH], F32)
            ETn = sb(s2, "ETn", [4, NCH], F32)
            Ec = sb(s2, "Ec", [NCH, 4], F32)
            Enc = sb(s2, "Enc", [NCH, 4], F32)
            tmp = sb(s2, "tmpm", [NCH, 4, 64], F32)
            uu = sb(s2, "uu", [NCH, 4, 64], F32)
            uu2 = sb(s2, "uu2", [NCH, 4, 64], F32)
            fl = sb(s2, "fl", [NCH, 4, 64], F32)
            dec = sb(s2, "dec", [NCH, 4], F32)
            decrep = sb(s2, "decrep", [NCH, 4, 128], F32)
            pa = ps(s2, "pa", [128, 512], F32)
            pb = ps(s2, "pb", [128, 512], F32)
            P.dma('sp', li[:], self.gi_d.rearrange("h (c j) -> c h j", j=64),
                  reads=[('gi_d', b) for b in range(S // 512)], writes=['li'])
            P.dma('sp', lf[:], self.gf_d.rearrange("h (c j) -> c h j", j=64),
                  reads=[('gf_d', b) for b in range(S // 512)], writes=['lf'])
            P.op('pool', lambda e: e.memset(ones[:], 1.0), writes=['ones'])
            for hh in range(4):
                P.op('dve', lambda e: e.tensor_tensor_scan(out=Fin[:, hh, :], data0=ones[:], data1=lf[:, hh, :],
                                                           initial=0.0, op0=ALU.mult, op1=ALU.add),
                     reads=['ones', 'lf'], writes=['Fin'])
            P.op('dve', lambda e: e.tensor_copy(out=tot[:], in_=Fin[:, :, 63]), reads=['Fin'], writes=['tot'])
            P.op('pe', lambda e: e.matmul(pa[0:NCH, 0:4], lhsT=ut[0:NCH, 0:NCH], rhs=tot[:], start=True, stop=True),
                 reads=['ut', 'tot'], writes=['pa'])
            P.op('dve', lambda e: e.tensor_tensor(out=Ft[:], in0=Fin[:],
                                                  in1=pa[0:NCH, 0:4].unsqueeze(2).broadcast_to([NCH, 4, 64]), op=ALU.add),
                 reads=['Fin', 'pa'], writes=['Ft'])
            P.op('dve', lambda e: e.tensor_tensor(out=a_[:], in0=li[:], in1=Ft[:], op=ALU.subtract),
                 reads=['li', 'Ft'], writes=['a_'])
            for hh in range(4):
                P.op('dve', lambda e: e.tensor_tensor_scan(out=Ain[:, hh, :], data0=a_[:, hh, :], data1=a_[:, hh, :],
                                                           initial=-1e30, op0=ALU.max, op1=ALU.max),
                     reads=['a_'], writes=['Ain'])
            P.op('dve', lambda e: e.tensor_copy(out=cmax[:], in_=Ain[:, :, 63]), reads=['Ain'], writes=['cmax'])
            P.op('pe', lambda e: e.transpose(out=pb[0:4, 0:NCH], in_=cmax[:], identity=identf[0:NCH, 0:NCH]),
                 reads=['cmax', 'identf'], writes=['pb'])
            P.op('dve', lambda e: e.tensor_copy(out=cmT[:], in_=pb[0:4, 0:NCH]), reads=['pb'], writes=['cmT'])
            P.op('dve', lambda e: e.tensor_tensor_scan(out=ET[:], data0=cmT[:], data1=cmT[:], initial=0.0,
                                                       op0=ALU.max, op1=ALU.max), reads=['cmT'], writes=['ET'])
            if NCH > 1:
                P.op('dve', lambda e: e.tensor_copy(out=ETn[:, 0:NCH - 1], in_=ET[:, 1:NCH]), reads=['ET'], writes=['ETn'])
            P.op('dve', lambda e: e.tensor_copy(out=ETn[:, NCH - 1:NCH], in_=ET[:, NCH - 1:NCH]), reads=['ET'], writes=['ETn'])
            P.op('pe', lambda e: e.transpose(out=pa[0:NCH, 0:4], in_=ET[:], identity=identf[0:4, 0:4]),
                 reads=['ET', 'identf'], writes=['pa'])
            P.op('dve', lambda e: e.tensor_copy(out=Ec[:], in_=pa[0:NCH, 0:4]), reads=['pa'], writes=['Ec'])
            P.op('pe', lambda e: e.transpose(out=pb[0:NCH, 0:4], in_=ETn[:], identity=identf[0:4, 0:4]),
                 reads=['ETn', 'identf'], writes=['pb'])
            P.op('dve', lambda e: e.tensor_copy(out=Enc[:], in_=pb[0:NCH, 0:4]), reads=['pb'], writes=['Enc'])
            Eb = Ec[:].unsqueeze(2).broadcast_to([NCH, 4, 64])
            Enb = Enc[:].unsqueeze(2).broadcast_to([NCH, 4, 64])
            P.op('dve', lambda e: e.tensor_tensor(out=tmp[:], in0=a_[:], in1=Eb, op=ALU.subtract),
                 reads=['a_', 'Ec'], writes=['tmp'])
            P.op('act', lambda e: e.activation(out=uu[:], in_=tmp[:], func=AF.Exp, bias=LNSC), reads=['tmp'], writes=['uu'])
            P.op('dve', lambda e: e.tensor_tensor(out=tmp[:], in0=a_[:], in1=Enb, op=ALU.subtract),
                 reads=['a_', 'Enc'], writes=['tmp'])
            P.op('act', lambda e: e.activation(out=uu2[:], in_=tmp[:], func=AF.Exp, bias=LNSC), reads=['tmp'], writes=['uu2'])
            P.op('dve', lambda e: e.tensor_tensor(out=tmp[:], in0=Ft[:], in1=Eb, op=ALU.add),
                 reads=['Ft', 'Ec'], writes=['tmp'])
            P.op('act', lambda e: e.activation(out=fl[:], in_=tmp[:], func=AF.Exp, scale=-1.0), reads=['tmp'], writes=['fl'])
            P.op('dve', lambda e: e.tensor_tensor(out=dec[:], in0=Ec[:], in1=Enc[:], op=ALU.subtract),
                 reads=['Ec', 'Enc'], writes=['dec'])
            P.op('act', lambda e: e.activation(out=dec[:], in_=dec[:], func=AF.Exp), reads=['dec'], writes=['dec'])
            P.op('dve', lambda e: e.tensor_copy(out=decrep[:], in_=dec[:].unsqueeze(2).broadcast_to([NCH, 4, 128])),
                 reads=['dec'], writes=['decrep'])
            for src, dst, nm in ((uu, uT, 'uT'), (uu2, u2T, 'u2T'), (fl, flT, 'flT')):
                for hh in range(4):
                    P.op('pe', lambda e: e.transpose(out=pa[0:64, hh * 128:hh * 128 + NCH], in_=src[:, hh, :],
                                                     identity=identf[0:NCH, 0:NCH]),
                         reads=['uu', 'uu2', 'fl', 'identf'], writes=['pa'])
                P.op('dve', lambda e: e.tensor_copy(out=dst[:], in_=pa[0:64, :].rearrange("p (h c) -> p h c", h=4)[:, :, 0:NCH]),
                     reads=['pa'], writes=[nm])
            for hh in range(4):
                P.op('pe', lambda e: e.matmul(pb[:, hh * 128:hh * 128 + NCH], lhsT=decrep[:, hh, :],
                                              rhs=identf[0:NCH, 0:NCH], start=True, stop=True),
                     reads=['decrep', 'identf'], writes=['pb'])
            P.op('dve', lambda e: e.tensor_copy(out=decB[:], in_=pb[:].rearrange("p (h c) -> p h c", h=4)[:, :, 0:NCH]),
                 reads=['pb'], writes=['decB'])
            self.barrier()

        pSr = [ps(st, "pSr", [128, 512], F32) for _ in range(3)]
        pMi = ps(st, "pMi", [128, 512], F32)
        pMb = pMi[:, 128:192].bitcast(BF16)
        pMb2 = pMi[:, 192:256].bitcast(BF16)
        pM = pMi[:, 256:288]
        bg = BG()

        sm = ExitStack()
        st = sm
        GC = 4
        NG = S // (64 * GC)
        TG = 64 * GC
        qg = [sb(st, "qg", [128, 4, TG], BF16) for _ in range(2)]
        kg = [sb(st, "kg", [128, 4, TG], BF16) for _ in range(2)]
        vg = [sb(st, "vg", [64, GC, 4, 129], BF16) for _ in range(2)]
        ogg = [sb(st, "ogg", [64, GC, 512], BF16) for _ in range(2)]
        ym = [sb(st, "ym", [64, GC, 512], BF16) for _ in range(2)]
        G = [sb(st, "G", [128, 129], F32) for _ in range(4)]
        Gb = [sb(st, "Gb", [128, 129], BF16) for _ in range(4)]
        ku2 = [sb(st, "ku2", [64, 128], BF16) for _ in range(2)]
        Sm = [sb(st, "Sm", [64, 64], BF16) for _ in range(2)]
        junkm = sb(st, "junkm", [64, 128], BF16)
        scm = [sb(st, "scm", [64, 8], F32) for _ in range(2)]
        mlA = [ps(st, "mlA", [128, 512], F32) for _ in range(2)]

        def gen_ml():
            for i in range(2):
                P.op('pool', lambda e: e.memset(vg[i][:], 1.0), writes=[('vg', i)])
            for hh in range(4):
                P.op('pool', lambda e: e.memset(G[hh][:], 0.0), writes=[('G', hh)])
                P.op('pool', lambda e: e.memset(Gb[hh][:], 0.0), writes=[('Gb', hh)])
            yield

            def load_group(g):
                i = g % 2
                tk = slice(g * TG, (g + 1) * TG)
                bl = sorted(set([(g * TG) // 512, ((g + 1) * TG - 1) // 512]))
                jl = list(range((g * TG) // 128, ((g + 1) * TG + 127) // 128))
                P.dma('sp', qg[i][:], self.qkT[0:512, tk].rearrange("(h p) t -> p h t", p=128),
                      reads=[('qkT', ft, b) for ft in range(4) for b in bl], writes=[('qg', i)])
                P.dma('sp', kg[i][:], self.qkT[512:1024, tk].rearrange("(h p) t -> p h t", p=128),
                      reads=[('qkT', ft, b) for ft in range(4, 8) for b in bl], writes=[('kg', i)])
                for hh in range(4):
                    P.dma('sp', vg[i][:, :, hh, 0:128],
                          self.mv_d[tk, hh * 128:(hh + 1) * 128].rearrange("(c s) e -> s c e", s=64),
                          reads=[('mv_d', j) for j in jl], writes=[('vg', i)])
                P.dma('sp', ogg[i][:], self.og_d[tk, :].rearrange("(c s) e -> s c e", s=64),
                      reads=[('og_d', j) for j in jl], writes=[('ogg', i)])
            load_group(0)
            it = 0
            for g in range(NG):
                if g + 1 < NG:
                    load_group(g + 1)
                gi_ = g % 2
                for cl in range(GC):
                    c = g * GC + cl
                    for hh in range(4):
                        k_ = kg[gi_][:, hh, cl * 64:(cl + 1) * 64]
                        q_ = qg[gi_][:, hh, cl * 64:(cl + 1) * 64]
                        v_ = vg[gi_][:, cl, hh, :]
                        i2 = it % 2
                        it += 1
                        A = mlA[i2]
                        pS_ = A[0:64, 0:64]
                        pO_ = A[0:64, 64:193]
                        pG_ = A[:, 256:385]
                        pkT_ = A[0:64, 448:512].bitcast(BF16)
                        P.op('pe', lambda e: e.transpose(out=pkT_, in_=k_, identity=identb[:]),
                             reads=[('kg', gi_), 'identb'], writes=[('mlA', i2)])
                        P.op('pe', lambda e: e.matmul(pS_, lhsT=k_, rhs=q_, start=True, stop=True),
                             reads=[('kg', gi_), ('qg', gi_)], writes=[('mlA', i2)])
                        yield
                        P.op('act', lambda e: e.activation(out=ku2[i2][:], in_=pkT_, func=AF.Copy,
                                                           scale=u2T[:, hh, c:c + 1]),
                             reads=[('mlA', i2), 'u2T'], writes=[('ku2', i2)])
                        P.op('dve', lambda e: e.scalar_tensor_tensor(out=Sm[i2][:], in0=pS_,
                                                                     scalar=uT[:, hh, c:c + 1], in1=tri[0:64, 0:64],
                                                                     op0=ALU.mult, op1=ALU.mult),
                             reads=[('mlA', i2), 'uT', 'tri'], writes=[('Sm', i2)])
                        yield
                        P.op('pe', lambda e: e.matmul(pO_, lhsT=Sm[i2][:], rhs=v_, start=True, stop=False),
                             reads=[('Sm', i2), ('vg', gi_)], writes=[('mlA', i2)])
                        P.op('pe', lambda e: e.matmul(pO_, lhsT=q_, rhs=Gb[hh][:], start=False, stop=True),
                             reads=[('qg', gi_), ('Gb', hh)], writes=[('mlA', i2)])
                        P.op('pe', lambda e: e.matmul(pG_, lhsT=ku2[i2][:], rhs=v_, start=True, stop=True),
                             reads=[('ku2', i2), ('vg', gi_)], writes=[('mlA', i2)])
                        yield
                        P.op('dve', lambda e: e.scalar_tensor_tensor(out=G[hh][:], in0=G[hh][:], scalar=decB[:, hh, c:c + 1],
                                                                     in1=pG_, op0=ALU.mult, op1=ALU.add),
                             reads=[('G', hh), 'decB', ('mlA', i2)], writes=[('G', hh)])
                        P.op('pool', lambda e: e.tensor_copy(out=Gb[hh][:], in_=G[hh][:]), reads=[('G', hh)], writes=[('Gb', hh)])
                        s_ = scm[i2]
                        sk = ('scm', i2)
                        P.op('act', lambda e: e.activation(out=junkm[:], in_=pO_[:, 0:128], func=AF.Square,
                                                           scale=128.0 ** -0.5, accum_out=s_[:, 0:1]),
                             reads=[('mlA', i2)], writes=['junkm', sk])
                        yield
                        P.op('dve', lambda e: e.tensor_scalar(out=s_[:, 6:7], in0=pO_[:, 128:129], scalar1=-1.0,
                                                              scalar2=flT[:, hh, c:c + 1], op0=ALU.mult, op1=ALU.max),
                             reads=[('mlA', i2), 'flT'], writes=[sk])
                        P.op('dve', lambda e: e.tensor_tensor(out=s_[:, 1:2], in0=s_[:, 6:7], in1=pO_[:, 128:129], op=ALU.max),
                             reads=[('mlA', i2), sk], writes=[sk])
                        P.op('dve', lambda e: e.tensor_tensor(out=s_[:, 2:3], in0=s_[:, 1:2], in1=s_[:, 1:2], op=ALU.mult),
                             reads=[sk], writes=[sk])
                        P.op('dve', lambda e: e.scalar_tensor_tensor(out=s_[:, 3:4], in0=s_[:, 2:3], scalar=EPS, in1=s_[:, 0:1],
                                                                     op0=ALU.mult, op1=ALU.add), reads=[sk], writes=[sk])
                        yield
                        P.op('act', lambda e: e.activation(out=s_[:, 4:5], in_=s_[:, 3:4], func=AF.Ln), reads=[sk], writes=[sk])
                        P.op('act', lambda e: e.activation(out=s_[:, 5:6], in_=s_[:, 4:5], func=AF.Exp, scale=-0.5), reads=[sk], writes=[sk])
                        P.op('dve', lambda e: e.scalar_tensor_tensor(out=ym[gi_][:, cl, hh * 128:(hh + 1) * 128],
                                                                     in0=pO_[:, 0:128], scalar=s_[:, 5:6],
                                                                     in1=ogg[gi_][:, cl, hh * 128:(hh + 1) * 128],
                                                                     op0=ALU.mult, op1=ALU.mult),
                             reads=[('mlA', i2), sk, ('ogg', gi_)], writes=[('ym', gi_)])
                        yield
                P.dma('sp', self.mix_d[g * TG:(g + 1) * TG, 0:512].rearrange("(c s) e -> s c e", s=64), ym[gi_][:],
                      reads=[('ym', gi_)], writes=[('mixm', g)])
        ML_YIELDS = NCH * 4 * 6 + 1
        g_ml = gen_ml()
        if not os.environ.get("SKIP_ML"):
            bg.add(g_ml)

        with sm:
            tm4 = sb(sm, "tm4", [128, 4, 512], BF16)
            P.dma('sp', tm4[:], self.c_tm4, writes=['tm4'])
            negB = self._negB(sm, "negBm", self.moba_qk_norm[l, 0], self.moba_qk_norm[l, 1])
            KX = [sb(sm, "KX", [96, S], BF16) for _ in range(2)]
            QXA = [sb(sm, "QXA", [96, S], BF16) for _ in range(2)]
            VX = [sb(sm, "VX", [128, NT, 65], BF16) for _ in range(2)]
            kmean = sb(sm, "kmean", [64, 32], F32)
            kmb = [sb(sm, "kmb", [64, 32], BF16) for _ in range(2)]
            gsb = [sb(sm, "gsb_b", [128, 32], F32) for _ in range(2)]
            m8 = [sb(sm, "m8", [128, 8], F32) for _ in range(2)]
            sel = [sb(sm, "sel_b", [128, 32], F32) for _ in range(2)]
            MBw = [sb(sm, "MBw", [128, 128], BF16) for _ in range(2)]
            PT = [sb(sm, "PT", [128, 512], BF16) for _ in range(4)]
            rz = sb(sm, "rz_b", [128, 4], F32)
            yb = [sb(sm, "yb", [128, 4, 64], BF16) for _ in range(2)]
            pO = [ps(sm, "pO_b", [128, 512], F32) for _ in range(1)] * 2
            if os.environ.get("NO_BITCAST"):
                pMb = ps(sm, "pMbx", [128, 128], BF16)[:]
            for i in range(2):
                P.dma('sp', KX[i][64:96, :], self.c_ind32, writes=[('KXi', i)])
                P.op('pool', lambda e: e.memset(VX[i][:], 1.0), writes=[('VX', i)])
                P.op('pool', lambda e: e.memset(MBw[i][:], 0.0), writes=[('MBw', i)])
            P.op('pool', lambda e: e.memset(kmean[:], 0.0), writes=['kmean'])

            def gen_prep(h):
                i = h % 2
                P.dma('sp', KX[i][0:64, :], self.bkT[h * 64:(h + 1) * 64, :], reads=[('bkT', j) for j in allt], writes=[('KX', i)])
                self.dma_mid(VX[i][:, :, 0:64], self.bv_d[:, h * 64:(h + 1) * 64].rearrange("(kt p) d -> p kt d", p=128), NT, 8,
                             reads=[('bv_d', j) for j in allt], writes=[('VX', i)])
                P.dma('sp', QXA[i][0:64, :], self.bqT[h * 64:(h + 1) * 64, :], reads=[('bqT', j) for j in allt], writes=[('QXAq', i)])
                yield
                P.op('dve', lambda e: e.tensor_reduce(out=kmean[:, 0:NB], in_=KX[i][0:64, :].rearrange("p (n k) -> p n k", k=256),
                                                      axis=AX.X, op=ALU.add), reads=[('KX', i)], writes=['kmean'])
                P.op('dve', lambda e: e.tensor_scalar(out=kmb[i][:], in0=kmean[:], scalar1=1.0 / 256, scalar2=None, op0=ALU.mult),
                     reads=['kmean'], writes=[('kmb', i)])
                yield
                for jt in range(NT if os.environ.get("BIS", "0") != "2" else 0):
                    own = jt // 2
                    b_ = jt % 2
                    P.op('pe', lambda e: e.matmul(pM, lhsT=QXA[i][0:64, jt * 128:(jt + 1) * 128], rhs=kmb[i][:],
                                                  start=True, stop=True), reads=[('QXAq', i), ('kmb', i)], writes=['pMi'])
                    P.op('pool', lambda e: e.memset(gsb[b_][:], -1e30), writes=[('gsb', b_)])
                    if own > 0:
                        P.op('dve', lambda e: e.tensor_copy(out=gsb[b_][:, 0:own], in_=pM[:, 0:own]), reads=['pMi'], writes=[('gsb', b_)])
                    yield
                    P.op('dve', lambda e: e.max(out=m8[b_][:], in_=gsb[b_][:]), reads=[('gsb', b_)], writes=[('m8', b_)])
                    P.op('dve', lambda e: e.tensor_scalar(out=sel[b_][:], in0=gsb[b_][:], scalar1=m8[b_][:, 2:3], scalar2=None,
                                                          op0=ALU.is_ge), reads=[('gsb', b_), ('m8', b_)], writes=[('sel', b_)])
                    P.op('dve', lambda e: e.tensor_scalar(out=MBw[b_][:, 64:96], in0=sel[b_][:], scalar1=-NEGM, scalar2=NEGM,
                                                          op0=ALU.mult, op1=ALU.add), reads=[('sel', b_)], writes=[('MBw', b_)])
                    P.op('dve', lambda e: e.memset(MBw[b_][:, 64 + own:65 + own], 0.0), writes=[('MBw', b_)])
                    if own + 1 < 32:
                        P.op('dve', lambda e: e.memset(MBw[b_][:, 65 + own:96], NEGM), writes=[('MBw', b_)])
                    yield
                    P.op('pe', lambda e: e.transpose(out=pMb, in_=MBw[b_][:], identity=identb[:]),
                         reads=[('MBw', b_), 'identb'], writes=['pMi'])
                    P.op('act', lambda e: e.copy(out=QXA[i][64:96, jt * 128:(jt + 1) * 128], in_=pMb[64:96, :]),
                         reads=['pMi'], writes=[('QXAm', i, jt)])
                    yield
            PREP_YIELDS = 3 * NT + 2
            main_iters_h = sum(4 * qc + 4 for qc in range(NQC))
            ml_rate = ML_YIELDS / float(4 * main_iters_h) * 1.15
            prep_rate = PREP_YIELDS / float(main_iters_h) * 1.3
            g0 = gen_prep(0)
            for _ in g0:
                bg.step(1.0)
            si = 0
            pi = 0
            for h in range(4):
                i = h % 2
                gp = None
                pacc = 0.0
                if h + 1 < 4:
                    gp = gen_prep(h + 1)
                    if os.environ.get("NOINT"):
                        for _ in gp:
                            pass
                        gp = None
                for qc in range(min(NQC, int(os.environ.get("QCMAX", "99"))) if (os.environ.get("BIS", "0") == "0" and h < int(os.environ.get("HMAX", "9"))) else 0):
                    q0 = qc * 512
                    Q_ = QXA[i][:, q0:q0 + 512]
                    qkeys = [('QXAq', i)] + [('QXAm', i, 4 * qc + j) for j in range(4)]
                    po = pO[pi % 2]
                    pok = ('pO', pi % 2)
                    pi += 1
                    P.op('pe', lambda e: e.matmul(po[:, 0:260], lhsT=zl[:], rhs=zr[:, 0:260], start=True, stop=True,
                                                  skip_group_check=True), reads=['zl', 'zr'], writes=[pok])
                    nkt = 4 * qc + 4
                    base = si

                    def qk(kt):
                        p_ = pSr[(base + kt) % 3]
                        P.op('pe', lambda e: e.matmul(p_[:], lhsT=KX[i][:, kt * 128:(kt + 1) * 128], rhs=Q_, start=True, stop=True),
                             reads=[('KX', i), ('KXi', i)] + qkeys, writes=[('pSr', (base + kt) % 3)])
                    LA = int(os.environ.get("LA", "2"))
                    for kt in range(min(LA, nkt)):
                        qk(kt)
                    for kt in range(nkt):
                        if kt + LA < nkt:
                            qk(kt + LA)
                        p_ = pSr[(base + kt) % 3]
                        pk = ('pSr', (base + kt) % 3)
                        t_ = PT[(base + kt) % 4]
                        tk = ('PT', (base + kt) % 4)
                        P.op('act', lambda e: e.activation(out=t_[:], in_=p_[:], func=AF.Exp, bias=negB[:, 0:1], scale=0.125),
                             reads=[pk, 'negBm'], writes=[tk])
                        off = kt - 4 * qc
                        if off >= 0:
                            P.op('pool', lambda e: e.tensor_tensor(out=t_[:], in0=t_[:], in1=tm4[:, off, :], op=ALU.mult),
                                 reads=[tk, 'tm4'], writes=[tk])
                        for j in range(4):
                            if kt > 4 * qc + j:
                                continue
                            P.op('pe', lambda e: e.matmul(po[:, j * 65:(j + 1) * 65], lhsT=t_[:, j * 128:(j + 1) * 128],
                                                          rhs=VX[i][:, kt, :], start=False, stop=(kt == 4 * qc + j),
                                                          skip_group_check=True), reads=[tk, ('VX', i)], writes=[pok])
                        bg.step(ml_rate)
                        if gp is not None:
                            pacc += prep_rate
                            while pacc >= 1.0:
                                pacc -= 1.0
                                try:
                                    next(gp)
                                except StopIteration:
                                    gp = None
                                    break
                    si += nkt
                    y_ = yb[qc % 2]
                    yk = ('yb', qc % 2)
                    pov = po[:, 0:260].rearrange("p (j d) -> p j d", d=65)
                    P.op('dve', lambda e: e.reciprocal(out=rz[:], in_=pov[:, :, 64]), reads=[pok], writes=['rz'])
                    P.op('dve', lambda e: e.tensor_tensor(out=y_[:], in0=pov[:, :, 0:64],
                                                          in1=rz[:].unsqueeze(2).broadcast_to([128, 4, 64]), op=ALU.mult),
                         reads=[pok, 'rz'], writes=[yk])
                    P.dma('sp', self.mix_d[q0:q0 + 512, 512 + h * 64:512 + (h + 1) * 64].rearrange("(j p) d -> p j d", p=128),
                          y_[:], reads=[yk], writes=[('mixb', h, qc)])
                if gp is not None:
                    for _ in gp:
                        bg.step(1.0)
            bg.drain()
            self.barrier()
        if not os.environ.get("SKIP_NSA"):
            self._phase2_nsa_pipe(l, pSr, pMi, identb, zl, zr)
        self.barrier()


Builder.phase2 = _phase2


def _phase2_nsa_pipe(self, l, pSr, pMi, identb, zl, zr, bgen=None, bg_yields=0):
    S, P, nc = self.S, self.P, self.nc
    NT = S // 128
    Nc = S // 16 - 1
    NCT = max(1, S // 2048)
    sb, ps = self.sb, self.ps
    allt = list(range(NT))
    pMb = pMi[:, 128:192].bitcast(BF16)
    pMb2 = pMi[:, 192:256].bitcast(BF16)
    pM = pMi[:, 256:320]
    with ExitStack() as st:
        trib = sb(st, "trib", [128, 128], BF16)
        triw = sb(st, "triw", [128, 128], BF16)
        cmask = sb(st, "cmask", [128, 17, 128], BF16)
        OVL = sb(st, "OVL", [128, NCT, 128], BF16)
        cols = sb(st, "cols", [128, 4], F32)
        NG = sb(st, "NG", [128, NT, 12], F32)
        P.dma('sp', trib[:], self.c_trib, writes=['trib'])
        P.dma('sp', triw[:], self.c_triw, writes=['triw'])
        P.dma('sp', cmask[:], self.c_cmask, writes=['cmask'])
        P.dma('sp', OVL[:], self.c_ovl.rearrange("(ct p) n -> p ct n", p=128)[:, 0:NCT, :], writes=['OVL'])
        P.dma('sp', cols[:], self.c_cols, writes=['cols'])
        self.dma_mid(NG[:], self.ng_d.rearrange("(j p) g -> p j g", p=128), NT, 8, reads=[('ng_d', j) for j in allt], writes=['NG'])
        negBc = self._negB(st, "negBc", self.nsa_q_norm[l], self.nsa_k_norm[l, 0])
        negBs = self._negB(st, "negBs", self.nsa_q_norm[l], self.nsa_k_norm[l, 1])
        negBw = self._negB(st, "negBw", self.nsa_q_norm[l], self.nsa_k_norm[l, 2])
        KSX = sb(st, "KSX", [128, S], BF16)
        KWX = sb(st, "KWX", [64, S], BF16)
        VSX = sb(st, "VSX", [128, NT, 65], BF16)
        VWX = sb(st, "VWX", [128, NT, 65], BF16)
        KcT = sb(st, "KcT", [64, NCT * 128], BF16)
        VCX = sb(st, "VCX", [128, NCT, 65], BF16)
        P.dma('sp', KSX[0:64, :], self.ksT, reads=[('ksT', j) for j in allt], writes=['KSX'])
        P.dma('sp', KSX[64:128, :], self.c_ind64, writes=['KSXi'])
        P.dma('sp', KWX[:], self.kwT, reads=[('kwT', j) for j in allt], writes=['KWX'])
        for V_, src, nm in ((VSX, self.vs_d, 'vs_d'), (VWX, self.vw_d, 'vw_d')):
            P.op('pool', lambda e: e.memset(V_[:], 1.0), writes=[nm + 'X'])
            self.dma_mid(V_[:, :, 0:64], src.rearrange("(kt p) d -> p kt d", p=128), NT, 8,
                         reads=[(nm, j) for j in allt], writes=[nm + 'X'])
        P.op('pool', lambda e: e.memset(VCX[:], 1.0), writes=['VCX'])
        pSc = ps(st, "pSc", [128, 512], F32) if bgen is None else pMi
        pSc_key = 'pSc' if bgen is None else 'pMi'
        pOcU = ps(st, "pOcU", [128, 512], F32)
        pOs = ps(st, "pOs", [128, 512], F32)
        pOw = ps(st, "pOw", [128, 512], F32)
        with ExitStack() as s2:
            KCV = sb(s2, "KCV", [128, S], BF16)
            W1s = sb(s2, "W1s", [128, 32, 128], F32)
            W1 = sb(s2, "W1", [128, 32, 128], BF16)
            pes = sb(s2, "pes", [32, 128], F32)
            peb = sb(s2, "peb", [32, 128], BF16)
            peT = sb(s2, "peT", [128, 32], BF16)
            w2s = sb(s2, "w2s", [128, 2, 64], F32)
            w2 = sb(s2, "w2", [128, 2, 64], BF16)
            gk0 = sb(s2, "gk0", [128, 64], F32)
            bias = sb(s2, "bias_c", [128, 2], F32)
            hidb = sb(s2, "hidb", [128, NCT * 128], BF16)
            kcn = sb(s2, "kcn", [128, 64], BF16)
            junk = sb(s2, "junk_c", [128, 64], F32)
            ssc = sb(s2, "ssc", [128, 2], F32)
            P.dma('sp', KCV[:], self.kcvcT, reads=[('kcvcT', b) for b in range(S // 512)], writes=['KCV'])
            for br in range(2):
                P.dma('sp', W1s[64 * br:64 * br + 64], self.cmp_w1[l, br].rearrange("(r d) j -> d r j", d=64), writes=['W1s'])
                P.dma('sp', w2s[:, br, :], self.cmp_w2[l, br], writes=['w2s'])
                P.dma('sp', pes[:, 64 * br:64 * br + 64], self.cmp_pe[l, br], writes=['pes'])
            P.dma('sp', gk0[:], self.nsa_k_norm[l, 0].partition_broadcast(128), writes=['gk0'])
            P.op('dve', lambda e: e.tensor_copy(out=W1[:], in_=W1s[:]), reads=['W1s'], writes=['W1'])
            P.op('dve', lambda e: e.tensor_copy(out=peb[:], in_=pes[:]), reads=['pes'], writes=['peb'])
            P.op('pe', lambda e: e.transpose(out=pMb[:, 0:32], in_=peb[:], identity=identb[0:32, 0:32]),
                 reads=['peb', 'identb'], writes=['pMi'])
            P.op('dve', lambda e: e.tensor_copy(out=peT[:], in_=pMb[:, 0:32]), reads=['pMi'], writes=['peT'])
            P.op('dve', lambda e: e.tensor_copy(out=w2[:], in_=w2s[:]), reads=['w2s'], writes=['w2'])
            P.op('pool', lambda e: e.memset(hidb[:], 0.0), writes=['hidb'])
            for br in range(2):
                rows = slice(64 * br, 64 * br + 64)
                kview = KCV[rows, :].rearrange("p (c s) -> p c s", s=16)
                for r in range(32):
                    P.op('pe', lambda e: e.matmul(pM[:, 0:1], lhsT=W1[rows, r, :], rhs=peT[rows, r:r + 1],
                                                  start=(r == 0), stop=(r == 31)), reads=['W1', 'peT'], writes=['pMi'])
                P.op('dve', lambda e: e.tensor_copy(out=bias[:, br:br + 1], in_=pM[:, 0:1]), reads=['pMi'], writes=['bias'])
                for r in range(32):
                    rhs = kview[:, 0:Nc, r] if r < 16 else kview[:, 1:Nc + 1, r - 16]
                    P.op('pe', lambda e: e.matmul(pSc[:, 0:Nc], lhsT=W1[rows, r, :], rhs=rhs,
                                                  start=(r == 0), stop=(r == 31)), reads=['W1', 'KCV'], writes=[pSc_key])
                P.op('act', lambda e: e.activation(out=hidb[:, 0:Nc], in_=pSc[:, 0:Nc], func=AF.Silu, bias=bias[:, br:br + 1]),
                     reads=[pSc_key, 'bias'], writes=['hidb'])
                for ct in range(NCT):
                    P.op('pe', lambda e: e.matmul(pM[:, 0:64], lhsT=hidb[:, ct * 128:(ct + 1) * 128], rhs=w2[:, br, :],
                                                  start=True, stop=True), reads=['hidb', 'w2'], writes=['pMi'])
                    if br == 0:
                        P.op('act', lambda e: e.activation(out=junk[:], in_=pM[:, 0:64], func=AF.Square, scale=0.125,
                                                           accum_out=ssc[:, 0:1]), reads=['pMi'], writes=['junk_c', 'ssc'])
                        P.op('dve', lambda e: e.tensor_scalar(out=ssc[:, 0:1], in0=ssc[:, 0:1], scalar1=EPS, scalar2=None,
                                                              op0=ALU.add), reads=['ssc'], writes=['ssc'])
                        P.op('act', lambda e: e.activation(out=ssc[:, 0:1], in_=ssc[:, 0:1], func=AF.Sqrt), reads=['ssc'], writes=['ssc'])
                        P.op('dve', lambda e: e.reciprocal(out=ssc[:, 1:2], in_=ssc[:, 0:1]), reads=['ssc'], writes=['ssc'])
                        P.op('dve', lambda e: e.scalar_tensor_tensor(out=kcn[:], in0=pM[:, 0:64], scalar=ssc[:, 1:2], in1=gk0[:],
                                                                     op0=ALU.mult, op1=ALU.mult),
                             reads=['pMi', 'ssc', 'gk0'], writes=['kcn'])
                        P.op('pe', lambda e: e.transpose(out=pMb[0:64, :], in_=kcn[:], identity=identb[:]),
                             reads=['kcn', 'identb'], writes=['pMi'])
                        P.op('act', lambda e: e.copy(out=KcT[:, ct * 128:(ct + 1) * 128], in_=pMb[0:64, :]),
                             reads=['pMi'], writes=['KcT'])
                    else:
                        P.op('act', lambda e: e.copy(out=VCX[:, ct, 0:64], in_=pM[:, 0:64]), reads=['pMi'], writes=['VCX'])
            self.barrier()
        QU = [sb(st, "QU", [64, 512], BF16) for _ in range(2)]
        QR0 = [sb(st, "QR0", [128, 512], BF16) for _ in range(2)]
        QR1 = [sb(st, "QR1", [128, 512], BF16) for _ in range(2)]
        PTc = [sb(st, "PTc", [128, 512], BF16) for _ in range(NCT)]
        PT = [sb(st, "PTn", [128, 512], BF16) for _ in range(4)]
        zc = sb(st, "zc", [128, 4], F32)
        rzc = sb(st, "rzc", [128, 4], F32)
        zz = sb(st, "zz", [128, 2, 4], F32)
        rzz = sb(st, "rzz", [128, 2, 4], F32)
        coef = sb(st, "coef", [128, 2, 4], F32)
        imp = sb(st, "imp", [128, 128], F32)
        work = sb(st, "work", [128, 128], F32)
        m8a = sb(st, "m8a", [128, 8], F32)
        m8b = sb(st, "m8b", [128, 8], F32)
        selm = sb(st, "selm", [128, 128], F32)
        MB = sb(st, "MB", [128, 128], BF16)
        MBs = sb(st, "MBs", [128, 128], BF16)
        yc = [sb(st, "yc", [128, 4, 64], F32) for _ in range(2)]
        yacc = sb(st, "yacc", [128, 4, 64], F32)
        yn = [sb(st, "yn", [128, 256], BF16) for _ in range(2)]

        def hb(ap):
            return ap.unsqueeze(1).broadcast_to([ap.shape[0], 4, 128])

        def h4(ap):
            return ap.rearrange("p (h t) -> p h t", h=4)

        def gen_prep(m):
            t0 = m * 128
            b2 = m % 2
            qu, qr0, qr1 = QU[b2], QR0[b2], QR1[b2]
            use_g1 = (2 * m + 1) >= 64
            P.dma('sp', h4(qu[:]), self.nquT[:, t0:t0 + 128].rearrange("(h d) t -> d h t", d=64),
                  reads=[('nquT', m)], writes=[('QU', b2)])
            P.dma('sp', h4(qr0[0:64, :]), self.nqrT[:, t0:t0 + 128].rearrange("(h d) t -> d h t", d=64),
                  reads=[('nqrT', m)], writes=[('QR0q', b2)])
            if use_g1:
                P.dma('sp', h4(qr1[0:64, :]), self.nqrT[:, t0:t0 + 128].rearrange("(h d) t -> d h t", d=64),
                      reads=[('nqrT', m)], writes=[('QR1q', b2)])
            yield
            ctn = min(NCT, (8 * m + 6) // 128 + 1)
            for ct in range(ctn):
                P.op('pe', lambda e: e.matmul(pSc[:], lhsT=KcT[:, ct * 128:(ct + 1) * 128], rhs=qu[:], start=True, stop=True),
                     reads=['KcT', ('QU', b2)], writes=[pSc_key])
                yield
                P.op('act', lambda e: e.activation(out=PTc[ct][:], in_=pSc[:], func=AF.Exp, bias=negBc[:, 0:1], scale=0.125),
                     reads=[pSc_key, 'negBc'], writes=[('PTc', ct)])
                yield
                r = m - 16 * ct
                if r <= 16:
                    P.op('dve', lambda e: e.tensor_tensor(out=h4(PTc[ct][:]), in0=h4(PTc[ct][:]), in1=hb(cmask[:, r, :]), op=ALU.mult),
                         reads=[('PTc', ct), 'cmask'], writes=[('PTc', ct)])
                    yield
                yield
            for h in range(4):
                for ct in range(ctn):
                    P.op('pe', lambda e: e.matmul(pOcU[:, h * 65:(h + 1) * 65], lhsT=PTc[ct][:, h * 128:(h + 1) * 128],
                                                  rhs=VCX[:, ct, :], start=(ct == 0), stop=(ct == ctn - 1)),
                         reads=[('PTc', ct), 'VCX'], writes=['pOcU'])
                    yield
                for ct in range(ctn):
                    P.op('pe', lambda e: e.matmul(pOcU[:, 260:388], lhsT=PTc[ct][:, h * 128:(h + 1) * 128],
                                                  rhs=OVL[:, ct, :], start=(ct == 0), stop=(ct == ctn - 1)),
                         reads=[('PTc', ct), 'OVL'], writes=['pOcU'])
                    yield
                P.op('dve', lambda e: e.tensor_scalar(out=zc[:, h:h + 1], in0=pOcU[:, h * 65 + 64:h * 65 + 65], scalar1=1e-30,
                                                      scalar2=None, op0=ALU.max), reads=['pOcU'], writes=['zc'])
                yield
                P.op('dve', lambda e: e.reciprocal(out=rzc[:, h:h + 1], in_=zc[:, h:h + 1]), reads=['zc'], writes=['rzc'])
                yield
                if h == 0:
                    P.op('dve', lambda e: e.tensor_scalar(out=imp[:], in0=pOcU[:, 260:388], scalar1=rzc[:, 0:1], scalar2=None,
                                                          op0=ALU.mult), reads=['pOcU', 'rzc'], writes=['imp'])
                    yield
                else:
                    P.op('dve', lambda e: e.scalar_tensor_tensor(out=imp[:], in0=pOcU[:, 260:388], scalar=rzc[:, h:h + 1],
                                                                 in1=imp[:], op0=ALU.mult, op1=ALU.add),
                         reads=['pOcU', 'rzc', 'imp'], writes=['imp'])
                    yield
                P.op('dve', lambda e: e.tensor_scalar(out=yc[b2][:, h, :], in0=pOcU[:, h * 65:h * 65 + 64], scalar1=rzc[:, h:h + 1],
                                                      scalar2=None, op0=ALU.mult), reads=['pOcU', 'rzc'], writes=[('yc', b2)])
                yield
                yield
            n1 = 2 * m + 1
            if n1 + 1 < 128:
                P.op('pool', lambda e: e.memset(imp[:, n1 + 1:128], -1e30), reads=['imp'], writes=['imp'])
                yield
            P.op('pool', lambda e: e.tensor_copy(out=imp[:, n1:n1 + 1], in_=cols[:, 0:1]), reads=['cols', 'imp'], writes=['imp'])
            yield
            P.op('pool', lambda e: e.memset(imp[:, n1 - 1:n1], 1e9), reads=['imp'], writes=['imp'])
            yield
            if n1 - 2 >= 0:
                P.op('dve', lambda e: e.tensor_tensor(out=imp[:, n1 - 2:n1 - 1], in0=imp[:, n1 - 2:n1 - 1], in1=cols[:, 1:2], op=ALU.max),
                     reads=['cols', 'imp'], writes=['imp'])
                yield
            P.op('pool', lambda e: e.memset(imp[:, 0:1], 1e9), reads=['imp'], writes=['imp'])
            yield
            yield
            P.op('dve', lambda e: e.max(out=m8a[:], in_=imp[:]), reads=['imp'], writes=['m8a'])
            yield
            P.op('dve', lambda e: e.match_replace(out=work[:], in_to_replace=m8a[:], in_values=imp[:], imm_value=-1e30),
                 reads=['imp', 'm8a'], writes=['work'])
            yield
            P.op('dve', lambda e: e.max(out=m8b[:], in_=work[:]), reads=['work'], writes=['m8b'])
            yield
            yield
            P.op('dve', lambda e: e.tensor_scalar(out=selm[:], in0=imp[:], scalar1=m8b[:, 7:8], scalar2=None, op0=ALU.is_ge),
                 reads=['imp', 'm8b'], writes=['selm'])
            yield
            P.op('dve', lambda e: e.tensor_scalar(out=MB[:], in0=selm[:], scalar1=-NEGM, scalar2=NEGM, op0=ALU.mult, op1=ALU.add),
                 reads=['selm'], writes=['MB'])
            yield
            if n1 + 1 < 128:
                P.op('pool', lambda e: e.memset(MB[:, n1 + 1:128], NEGM), reads=['MB'], writes=['MB'])
                yield
            P.op('pool', lambda e: e.tensor_copy(out=MB[:, n1:n1 + 1], in_=cols[:, 2:3]), reads=['cols', 'MB'], writes=['MB'])
            yield
            yield
            P.op('pool', lambda e: e.tensor_copy(out=MBs[:, 0:64], in_=MB[:, 64:128]), reads=['MB'], writes=['MBs'])
            yield
            P.op('pool', lambda e: e.tensor_copy(out=MBs[:, 64:128], in_=MB[:, 0:64]), reads=['MB'], writes=['MBs'])
            yield
            P.op('pe', lambda e: e.transpose(out=pMb, in_=MBs[:], identity=identb[:]), reads=['MBs', 'identb'], writes=['pMi'])
            yield
            P.op('act', lambda e: e.copy(out=h4(qr0[64:128, :]), in_=hb(pMb[64:128, :])), reads=['pMi'], writes=[('QR0m', b2)])
            yield
            if use_g1:
                P.op('pe', lambda e: e.transpose(out=pMb2, in_=MB[:], identity=identb[:]), reads=['MB', 'identb'], writes=['pMi'])
                yield
                P.op('act', lambda e: e.copy(out=h4(qr1[64:128, :]), in_=hb(pMb2[64:128, :])), reads=['pMi'], writes=[('QR1m', b2)])
                yield
            yield

        si = 0
        total_main = sum((m_ + 1) + (m_ + 1 - max(0, m_ - 4)) for m_ in range(NT))
        bg_rate = (bg_yields / float(total_main)) * 1.1 if bgen is not None else 0.0
        bg_state = {'g': bgen, 'acc': 0.0}

        def bg_step():
            if bg_state['g'] is None:
                return
            bg_state['acc'] += bg_rate
            while bg_state['acc'] >= 1.0 and bg_state['g'] is not None:
                bg_state['acc'] -= 1.0
                try:
                    next(bg_state['g'])
                except StopIteration:
                    bg_state['g'] = None
        g = gen_prep(0)
        for _ in g:
            pass
        for m in range(NT):
            t0 = m * 128
            b2 = m % 2
            qr0, qr1 = QR0[b2], QR1[b2]
            gp = gen_prep(m + 1) if m + 1 < NT else None
            P.op('pe', lambda e: e.matmul(pOs[:, 0:260], lhsT=zl[:], rhs=zr[:, 0:260], start=True, stop=True, skip_group_check=True),
                 reads=['zl', 'zr'], writes=['pOs'])
            P.op('pe', lambda e: e.matmul(pOw[:, 0:260], lhsT=zl[:], rhs=zr[:, 0:260], start=True, stop=True, skip_group_check=True),
                 reads=['zl', 'zr'], writes=['pOw'])
            items = [('s', kt) for kt in range(m + 1)] + [('w', kt) for kt in range(max(0, m - 4), m + 1)]
            n_it = len(items)
            rate = 80.0 / n_it
            base = si

            def qk(ix):
                typ, kt = items[ix]
                p_ = pSr[(base + ix) % 3]
                pk = ('pSr', (base + ix) % 3)
                if typ == 's':
                    if kt // 32 == 0:
                        P.op('pe', lambda e: e.matmul(p_[:], lhsT=KSX[:, kt * 128:(kt + 1) * 128], rhs=qr0[:], start=True, stop=True),
                             reads=['KSX', 'KSXi', ('QR0q', b2), ('QR0m', b2)], writes=[pk])
                    else:
                        P.op('pe', lambda e: e.matmul(p_[:], lhsT=KSX[:, kt * 128:(kt + 1) * 128], rhs=qr1[:], start=True, stop=True),
                             reads=['KSX', 'KSXi', ('QR1q', b2), ('QR1m', b2)], writes=[pk])
                else:
                    P.op('pe', lambda e: e.matmul(p_[:], lhsT=KWX[:, kt * 128:(kt + 1) * 128], rhs=qr0[0:64, :], start=True, stop=True),
                         reads=['KWX', ('QR0q', b2)], writes=[pk])
            for ix in range(min(2, n_it)):
                qk(ix)
            pacc = 0.0
            for ix in range(n_it):
                if ix + 2 < n_it:
                    qk(ix + 2)
                typ, kt = items[ix]
                p_ = pSr[(base + ix) % 3]
                pk = ('pSr', (base + ix) % 3)
                t_ = PT[(base + ix) % 4]
                tk = ('PTn', (base + ix) % 4)
                nb_ = negBs if typ == 's' else negBw
                P.op('act', lambda e: e.activation(out=t_[:], in_=p_[:], func=AF.Exp, bias=nb_[:, 0:1], scale=0.125),
                     reads=[pk, 'negBs', 'negBw'], writes=[tk])
                mk = None
                if kt == m:
                    mk = trib
                elif typ == 'w' and kt == m - 4:
                    mk = triw
                if mk is not None:
                    P.op('dve', lambda e: e.tensor_tensor(out=h4(t_[:]), in0=h4(t_[:]), in1=hb(mk[:]), op=ALU.mult),
                         reads=[tk, 'trib', 'triw'], writes=[tk])
                po_, pok, V_, vk = (pOs, 'pOs', VSX, 'vs_dX') if typ == 's' else (pOw, 'pOw', VWX, 'vw_dX')
                for h in range(4):
                    P.op('pe', lambda e: e.matmul(po_[:, h * 65:(h + 1) * 65], lhsT=t_[:, h * 128:(h + 1) * 128], rhs=V_[:, kt, :],
                                                  start=False, stop=(kt == m), skip_group_check=True),
                         reads=[tk, vk], writes=[pok])
                bg_step()
                if gp is not None:
                    pacc += rate
                    while pacc >= 1.0 and gp is not None:
                        pacc -= 1.0
                        try:
                            next(gp)
                        except StopIteration:
                            gp = None
            si += n_it
            if gp is not None:
                for _ in gp:
                    pass
            posv = pOs[:, 0:260].rearrange("p (h d) -> p h d", d=65)
            powv = pOw[:, 0:260].rearrange("p (h d) -> p h d", d=65)
            P.op('dve', lambda e: e.tensor_copy(out=zz[:, 0, :], in_=posv[:, :, 64]), reads=['pOs'], writes=['zz'])
            P.op('dve', lambda e: e.tensor_copy(out=zz[:, 1, :], in_=powv[:, :, 64]), reads=['pOw'], writes=['zz'])
            P.op('dve', lambda e: e.reciprocal(out=rzz[:], in_=zz[:]), reads=['zz'], writes=['rzz'])
            P.op('dve', lambda e: e.tensor_tensor(out=coef[:], in0=rzz[:], in1=NG[:, m, 4:12].rearrange("p (b h) -> p b h", b=2), op=ALU.mult),
                 reads=['rzz', 'NG'], writes=['coef'])
            for h in range(4):
                P.op('dve', lambda e: e.tensor_scalar(out=yacc[:, h, :], in0=yc[b2][:, h, :], scalar1=NG[:, m, h:h + 1], scalar2=None,
                                                      op0=ALU.mult), reads=[('yc', b2), 'NG'], writes=[('yacc', h)])
                P.op('dve', lambda e: e.scalar_tensor_tensor(out=yacc[:, h, :], in0=posv[:, h, 0:64], scalar=coef[:, 0, h:h + 1],
                                                             in1=yacc[:, h, :], op0=ALU.mult, op1=ALU.add),
                     reads=['pOs', 'coef', ('yacc', h)], writes=[('yacc', h)])
                P.op('dve', lambda e: e.scalar_tensor_tensor(out=yn[b2][:, h * 64:(h + 1) * 64], in0=powv[:, h, 0:64],
                                                             scalar=coef[:, 1, h:h + 1], in1=yacc[:, h, :], op0=ALU.mult, op1=ALU.add),
                     reads=['pOw', 'coef', ('yacc', h)], writes=[('yn', b2)])
            P.dma('sp', self.mix_d[t0:t0 + 128, 768:1024], yn[b2][:], reads=[('yn', b2)], writes=[('mixn', m)])
        while bg_state['g'] is not None:
            try:
                next(bg_state['g'])
            except StopIteration:
                bg_state['g'] = None


Builder._phase2_nsa_pipe = _phase2_nsa_pipe


def _phase2_nsa2(self, l):
    P = self.P
    with ExitStack() as st:
        identb = self.sb(st, "identb_n2", [128, 128], BF16)
        zl = self.sb(st, "zl_n2", [1, 128], BF16)
        zr = self.sb(st, "zr_n2", [1, 512], BF16)
        P.dma('sp', identb[:], self.c_identb, writes=['identb'])
        P.op('pool', lambda e: e.memset(zl[:], 0.0), writes=['zl'])
        P.op('pool', lambda e: e.memset(zr[:], 0.0), writes=['zr'])
        pSr = [self.ps(st, "pSr2", [128, 512], F32) for _ in range(3)]
        pMi = self.ps(st, "pMi2", [128, 512], F32)
        self._phase2_nsa_pipe(l, pSr, pMi, identb, zl, zr)
        self.barrier()


Builder.phase2_nsa2 = _phase2_nsa2


def _phase2b(self, l):
    S, P, nc = self.S, self.P, self.nc
    NCH = S // 64
    NT = S // 128
    NQC = S // 512
    NB = S // 256
    Nc = S // 16 - 1
    NCT = max(1, S // 2048)
    LNSC = math.log(128.0 ** -0.5)
    sb, ps = self.sb, self.ps
    allt = list(range(NT))
    with ExitStack() as st:
        identb = sb(st, "identb2", [128, 128], BF16)
        identf = sb(st, "identf2", [128, 128], F32)
        ut = sb(st, "ut2", [128, 128], F32)
        tri = sb(st, "tri2", [128, 128], F32)
        zl = sb(st, "zl2", [1, 128], BF16)
        zr = sb(st, "zr2", [1, 512], BF16)
        P.dma('sp', identb[:], self.c_identb, writes=['identb'])
        P.dma('sp', identf[:], self.c_identf, writes=['identf'])
        P.dma('sp', ut[:], self.c_ut, writes=['ut'])
        P.dma('sp', tri[:], self.c_tri, writes=['tri'])
        P.op('pool', lambda e: e.memset(zl[:], 0.0), writes=['zl'])
        P.op('pool', lambda e: e.memset(zr[:], 0.0), writes=['zr'])
        uT = sb(st, "uT_m", [64, 4, NCH], F32)
        u2T = sb(st, "u2T_m", [64, 4, NCH], F32)
        flT = sb(st, "flT_m", [64, 4, NCH], F32)
        decB = sb(st, "decB", [128, 4, NCH], F32)
        with ExitStack() as s2:
            li = sb(s2, "li", [NCH, 4, 64], F32)
            lf = sb(s2, "lf", [NCH, 4, 64], F32)
            ones = sb(s2, "ones", [NCH, 64], F32)
            Fin = sb(s2, "Fin", [NCH, 4, 64], F32)
            Ft = sb(s2, "Ft", [NCH, 4, 64], F32)
            a_ = sb(s2, "a_", [NCH, 4, 64], F32)
            Ain = sb(s2, "Ain", [NCH, 4, 64], F32)
            tot = sb(s2, "tot", [NCH, 4], F32)
            cmax = sb(s2, "cmax", [NCH, 4], F32)
            cmT = sb(s2, "cmT", [4, NCH], F32)
            ET = sb(s2, "ET", [4, NCH], F32)
            ETn = sb(s2, "ETn", [4, NCH], F32)
            Ec = sb(s2, "Ec", [NCH, 4], F32)
            Enc = sb(s2, "Enc", [NCH, 4], F32)
            tmp = sb(s2, "tmpm", [NCH, 4, 64], F32)
            uu = sb(s2, "uu", [NCH, 4, 64], F32)
            uu2 = sb(s2, "uu2", [NCH, 4, 64], F32)
            fl = sb(s2, "fl", [NCH, 4, 64], F32)
            dec = sb(s2, "dec", [NCH, 4], F32)
            decrep = sb(s2, "decrep", [NCH, 4, 128], F32)
            pa = ps(s2, "pa", [128, 512], F32)
            pb = ps(s2, "pb", [128, 512], F32)
            P.dma('sp', li[:], self.gi_d.rearrange("h (c j) -> c h j", j=64),
                  reads=[('gi_d', b) for b in range(S // 512)], writes=['li'])
            P.dma('sp', lf[:], self.gf_d.rearrange("h (c j) -> c h j", j=64),
                  reads=[('gf_d', b) for b in range(S // 512)], writes=['lf'])
            P.op('pool', lambda e: e.memset(ones[:], 1.0), writes=['ones'])
            for hh in range(4):
                P.op('dve', lambda e: e.tensor_tensor_scan(out=Fin[:, hh, :], data0=ones[:], data1=lf[:, hh, :],
                                                           initial=0.0, op0=ALU.mult, op1=ALU.add),
                     reads=['ones', 'lf'], writes=['Fin'])
            P.op('dve', lambda e: e.tensor_copy(out=tot[:], in_=Fin[:, :, 63]), reads=['Fin'], writes=['tot'])
            P.op('pe', lambda e: e.matmul(pa[0:NCH, 0:4], lhsT=ut[0:NCH, 0:NCH], rhs=tot[:], start=True, stop=True),
                 reads=['ut', 'tot'], writes=['pa'])
            P.op('dve', lambda e: e.tensor_tensor(out=Ft[:], in0=Fin[:],
                                                  in1=pa[0:NCH, 0:4].unsqueeze(2).broadcast_to([NCH, 4, 64]), op=ALU.add),
                 reads=['Fin', 'pa'], writes=['Ft'])
            P.op('dve', lambda e: e.tensor_tensor(out=a_[:], in0=li[:], in1=Ft[:], op=ALU.subtract),
                 reads=['li', 'Ft'], writes=['a_'])
            for hh in range(4):
                P.op('dve', lambda e: e.tensor_tensor_scan(out=Ain[:, hh, :], data0=a_[:, hh, :], data1=a_[:, hh, :],
                                                           initial=-1e30, op0=ALU.max, op1=ALU.max),
                     reads=['a_'], writes=['Ain'])
            P.op('dve', lambda e: e.tensor_copy(out=cmax[:], in_=Ain[:, :, 63]), reads=['Ain'], writes=['cmax'])
            P.op('pe', lambda e: e.transpose(out=pb[0:4, 0:NCH], in_=cmax[:], identity=identf[0:NCH, 0:NCH]),
                 reads=['cmax', 'identf'], writes=['pb'])
            P.op('dve', lambda e: e.tensor_copy(out=cmT[:], in_=pb[0:4, 0:NCH]), reads=['pb'], writes=['cmT'])
            P.op('dve', lambda e: e.tensor_tensor_scan(out=ET[:], data0=cmT[:], data1=cmT[:], initial=0.0,
                                                       op0=ALU.max, op1=ALU.max), reads=['cmT'], writes=['ET'])
            if NCH > 1:
                P.op('dve', lambda e: e.tensor_copy(out=ETn[:, 0:NCH - 1], in_=ET[:, 1:NCH]), reads=['ET'], writes=['ETn'])
            P.op('dve', lambda e: e.tensor_copy(out=ETn[:, NCH - 1:NCH], in_=ET[:, NCH - 1:NCH]), reads=['ET'], writes=['ETn'])
            P.op('pe', lambda e: e.transpose(out=pa[0:NCH, 0:4], in_=ET[:], identity=identf[0:4, 0:4]),
                 reads=['ET', 'identf'], writes=['pa'])
            P.op('dve', lambda e: e.tensor_copy(out=Ec[:], in_=pa[0:NCH, 0:4]), reads=['pa'], writes=['Ec'])
            P.op('pe', lambda e: e.transpose(out=pb[0:NCH, 0:4], in_=ETn[:], identity=identf[0:4, 0:4]),
                 reads=['ETn', 'identf'], writes=['pb'])
            P.op('dve', lambda e: e.tensor_copy(out=Enc[:], in_=pb[0:NCH, 0:4]), reads=['pb'], writes=['Enc'])
            Eb = Ec[:].unsqueeze(2).broadcast_to([NCH, 4, 64])
            Enb = Enc[:].unsqueeze(2).broadcast_to([NCH, 4, 64])
            P.op('dve', lambda e: e.tensor_tensor(out=tmp[:], in0=a_[:], in1=Eb, op=ALU.subtract),
                 reads=['a_', 'Ec'], writes=['tmp'])
            P.op('act', lambda e: e.activation(out=uu[:], in_=tmp[:], func=AF.Exp, bias=LNSC), reads=['tmp'], writes=['uu'])
            P.op('dve', lambda e: e.tensor_tensor(out=tmp[:], in0=a_[:], in1=Enb, op=ALU.subtract),
                 reads=['a_', 'Enc'], writes=['tmp'])
            P.op('act', lambda e: e.activation(out=uu2[:], in_=tmp[:], func=AF.Exp, bias=LNSC), reads=['tmp'], writes=['uu2'])
            P.op('dve', lambda e: e.tensor_tensor(out=tmp[:], in0=Ft[:], in1=Eb, op=ALU.add),
                 reads=['Ft', 'Ec'], writes=['tmp'])
            P.op('act', lambda e: e.activation(out=fl[:], in_=tmp[:], func=AF.Exp, scale=-1.0), reads=['tmp'], writes=['fl'])
            P.op('dve', lambda e: e.tensor_tensor(out=dec[:], in0=Ec[:], in1=Enc[:], op=ALU.subtract),
                 reads=['Ec', 'Enc'], writes=['dec'])
            P.op('act', lambda e: e.activation(out=dec[:], in_=dec[:], func=AF.Exp), reads=['dec'], writes=['dec'])
            P.op('dve', lambda e: e.tensor_copy(out=decrep[:], in_=dec[:].unsqueeze(2).broadcast_to([NCH, 4, 128])),
                 reads=['dec'], writes=['decrep'])
            for src, dst, nm in ((uu, uT, 'uT'), (uu2, u2T, 'u2T'), (fl, flT, 'flT')):
                for hh in range(4):
                    P.op('pe', lambda e: e.transpose(out=pa[0:64, hh * 128:hh * 128 + NCH], in_=src[:, hh, :],
                                                     identity=identf[0:NCH, 0:NCH]),
                         reads=['uu', 'uu2', 'fl', 'identf'], writes=['pa'])
                P.op('dve', lambda e: e.tensor_copy(out=dst[:], in_=pa[0:64, :].rearrange("p (h c) -> p h c", h=4)[:, :, 0:NCH]),
                     reads=['pa'], writes=[nm])
            for hh in range(4):
                P.op('pe', lambda e: e.matmul(pb[:, hh * 128:hh * 128 + NCH], lhsT=decrep[:, hh, :],
                                              rhs=identf[0:NCH, 0:NCH], start=True, stop=True),
                     reads=['decrep', 'identf'], writes=['pb'])
            P.op('dve', lambda e: e.tensor_copy(out=decB[:], in_=pb[:].rearrange("p (h c) -> p h c", h=4)[:, :, 0:NCH]),
                 reads=['pb'], writes=['decB'])
            self.barrier()

        pSr = [ps(st, "pSr", [128, 512], F32) for _ in range(3)]
        pMi = ps(st, "pMi", [128, 512], F32)
        pMb = pMi[:, 128:192].bitcast(BF16)
        pMb2 = pMi[:, 192:256].bitcast(BF16)
        pM = pMi[:, 256:288]
        bg = BG()

        GC = 4
        NG = S // (64 * GC)
        TG = 64 * GC
        qg = [sb(st, "qg", [128, 4, TG], BF16) for _ in range(2)]
        kg = [sb(st, "kg", [128, 4, TG], BF16) for _ in range(2)]
        vg = [sb(st, "vg", [64, GC, 4, 129], BF16) for _ in range(2)]
        ogg = [sb(st, "ogg", [64, GC, 512], BF16) for _ in range(2)]
        ym = [sb(st, "ym", [64, GC, 512], BF16) for _ in range(2)]
        G = [sb(st, "G", [128, 129], F32) for _ in range(4)]
        Gb = [sb(st, "Gb", [128, 129], BF16) for _ in range(4)]
        ku2 = [sb(st, "ku2", [64, 128], BF16) for _ in range(2)]
        Sm = [sb(st, "Sm", [64, 64], BF16) for _ in range(2)]
        junkm = sb(st, "junkm", [64, 128], BF16)
        scm = [sb(st, "scm", [64, 8], F32) for _ in range(2)]
        mlA = [ps(st, "mlA", [128, 512], F32) for _ in range(1)]

        def gen_ml():
            for i in range(2):
                P.op('pool', lambda e: e.memset(vg[i][:], 1.0), writes=[('vg', i)])
            for hh in range(4):
                P.op('pool', lambda e: e.memset(G[hh][:], 0.0), writes=[('G', hh)])
                P.op('pool', lambda e: e.memset(Gb[hh][:], 0.0), writes=[('Gb', hh)])
            yield

            def load_group(g):
                i = g % 2
                tk = slice(g * TG, (g + 1) * TG)
                bl = sorted(set([(g * TG) // 512, ((g + 1) * TG - 1) // 512]))
                jl = list(range((g * TG) // 128, ((g + 1) * TG + 127) // 128))
                P.dma('sp', qg[i][:], self.qkT[0:512, tk].rearrange("(h p) t -> p h t", p=128),
                      reads=[('qkT', ft, b) for ft in range(4) for b in bl], writes=[('qg', i)])
                P.dma('sp', kg[i][:], self.qkT[512:1024, tk].rearrange("(h p) t -> p h t", p=128),
                      reads=[('qkT', ft, b) for ft in range(4, 8) for b in bl], writes=[('kg', i)])
                for hh in range(4):
                    P.dma('sp', vg[i][:, :, hh, 0:128],
                          self.mv_d[tk, hh * 128:(hh + 1) * 128].rearrange("(c s) e -> s c e", s=64),
                          reads=[('mv_d', j) for j in jl], writes=[('vg', i)])
                P.dma('sp', ogg[i][:], self.og_d[tk, :].rearrange("(c s) e -> s c e", s=64),
                      reads=[('og_d', j) for j in jl], writes=[('ogg', i)])
            load_group(0)
            it = 0
            for g in range(NG):
                if g + 1 < NG:
                    load_group(g + 1)
                gi_ = g % 2
                for cl in range(GC):
                    c = g * GC + cl
                    for hh in range(4):
                        k_ = kg[gi_][:, hh, cl * 64:(cl + 1) * 64]
                        q_ = qg[gi_][:, hh, cl * 64:(cl + 1) * 64]
                        v_ = vg[gi_][:, cl, hh, :]
                        i2 = it % 2
                        it += 1
                        A = mlA[i2 % len(mlA)]
                        if os.environ.get("UNPACK"):
                            pS_ = pSr[0][0:64, 0:64]
                            pO_ = pSr[1][0:64, 0:129]
                            pG_ = pSr[2][:, 0:129]
                            pkT_ = A[0:64, 0:64].bitcast(BF16)
                        else:
                            pS_ = A[0:64, 0:64]
                            pO_ = A[0:64, 64:193]
                            pG_ = A[:, 256:385]
                            pkT_ = A[0:64, 448:512].bitcast(BF16)
                        P.op('pe', lambda e: e.transpose(out=pkT_, in_=k_, identity=identb[:]),
                             reads=[('kg', gi_), 'identb'], writes=[('mlA', i2 % len(mlA))])
                        P.op('pe', lambda e: e.matmul(pS_, lhsT=k_, rhs=q_, start=True, stop=True),
                             reads=[('kg', gi_), ('qg', gi_)], writes=[('mlA', i2 % len(mlA))])
                        yield
                        P.op('act', lambda e: e.activation(out=ku2[i2][:], in_=pkT_, func=AF.Copy,
                                                           scale=u2T[:, hh, c:c + 1]),
                             reads=[('mlA', i2 % len(mlA)), 'u2T'], writes=[('ku2', i2)])
                        P.op('dve', lambda e: e.scalar_tensor_tensor(out=Sm[i2][:], in0=pS_,
                                                                     scalar=uT[:, hh, c:c + 1], in1=tri[0:64, 0:64],
                                                                     op0=ALU.mult, op1=ALU.mult),
                             reads=[('mlA', i2 % len(mlA)), 'uT', 'tri'], writes=[('Sm', i2)])
                        yield
                        P.op('pe', lambda e: e.matmul(pO_, lhsT=Sm[i2][:], rhs=v_, start=True, stop=False),
                             reads=[('Sm', i2), ('vg', gi_)], writes=[('mlA', i2 % len(mlA))])
                        P.op('pe', lambda e: e.matmul(pO_, lhsT=q_, rhs=Gb[hh][:], start=False, stop=True),
                             reads=[('qg', gi_), ('Gb', hh)], writes=[('mlA', i2 % len(mlA))])
                        P.op('pe', lambda e: e.matmul(pG_, lhsT=ku2[i2][:], rhs=v_, start=True, stop=True),
                             reads=[('ku2', i2), ('vg', gi_)], writes=[('mlA', i2 % len(mlA))])
                        yield
                        P.op('dve', lambda e: e.scalar_tensor_tensor(out=G[hh][:], in0=G[hh][:], scalar=decB[:, hh, c:c + 1],
                                                                     in1=pG_, op0=ALU.mult, op1=ALU.add),
                             reads=[('G', hh), 'decB', ('mlA', i2 % len(mlA))], writes=[('G', hh)])
                        P.op('pool', lambda e: e.tensor_copy(out=Gb[hh][:], in_=G[hh][:]), reads=[('G', hh)], writes=[('Gb', hh)])
                        s_ = scm[i2]
                        sk = ('scm', i2)
                        P.op('act', lambda e: e.activation(out=junkm[:], in_=pO_[:, 0:128], func=AF.Square,
                                                           scale=128.0 ** -0.5, accum_out=s_[:, 0:1]),
                             reads=[('mlA', i2 % len(mlA))], writes=['junkm', sk])
                        yield
                        P.op('dve', lambda e: e.tensor_scalar(out=s_[:, 6:7], in0=pO_[:, 128:129], scalar1=-1.0,
                                                              scalar2=flT[:, hh, c:c + 1], op0=ALU.mult, op1=ALU.max),
                             reads=[('mlA', i2 % len(mlA)), 'flT'], writes=[sk])
                        P.op('dve', lambda e: e.tensor_tensor(out=s_[:, 1:2], in0=s_[:, 6:7], in1=pO_[:, 128:129], op=ALU.max),
                             reads=[('mlA', i2 % len(mlA)), sk], writes=[sk])
                        P.op('dve', lambda e: e.tensor_tensor(out=s_[:, 2:3], in0=s_[:, 1:2], in1=s_[:, 1:2], op=ALU.mult),
                             reads=[sk], writes=[sk])
                        P.op('dve', lambda e: e.scalar_tensor_tensor(out=s_[:, 3:4], in0=s_[:, 2:3], scalar=EPS, in1=s_[:, 0:1],
                                                                     op0=ALU.mult, op1=ALU.add), reads=[sk], writes=[sk])
                        yield
                        if os.environ.get("USE_SQRT"):
                            P.op('act', lambda e: e.activation(out=s_[:, 4:5], in_=s_[:, 3:4], func=AF.Sqrt), reads=[sk], writes=[sk])
                            P.op('dve', lambda e: e.reciprocal(out=s_[:, 5:6], in_=s_[:, 4:5]), reads=[sk], writes=[sk])
                        else:
                            P.op('act', lambda e: e.activation(out=s_[:, 4:5], in_=s_[:, 3:4], func=AF.Ln), reads=[sk], writes=[sk])
                            P.op('act', lambda e: e.activation(out=s_[:, 5:6], in_=s_[:, 4:5], func=AF.Exp, scale=-0.5), reads=[sk], writes=[sk])
                        P.op('dve', lambda e: e.scalar_tensor_tensor(out=ym[gi_][:, cl, hh * 128:(hh + 1) * 128],
                                                                     in0=pO_[:, 0:128], scalar=s_[:, 5:6],
                                                                     in1=ogg[gi_][:, cl, hh * 128:(hh + 1) * 128],
                                                                     op0=ALU.mult, op1=ALU.mult),
                             reads=[('mlA', i2 % len(mlA)), sk, ('ogg', gi_)], writes=[('ym', gi_)])
                        yield
                P.dma('sp', self.mix_d[g * TG:(g + 1) * TG, 0:512].rearrange("(c s) e -> s c e", s=64), ym[gi_][:],
                      reads=[('ym', gi_)], writes=[('mixm', g)])
        ML_YIELDS = NCH * 4 * 6 + 1
        g_ml = gen_ml()
        if os.environ.get("NO_NSA"):
            for _ in g_ml:
                pass
        else:
            self._phase2_nsa_pipe(l, pSr, pMi, identb, zl, zr, bgen=g_ml, bg_yields=ML_YIELDS)
        self.barrier()


Builder.phase2b = _phase2b
```

```python
import math
import os
from contextlib import ExitStack

import numpy as np
import ml_dtypes

import concourse.bass as bass
import concourse.mybir as mybir
from concourse.bass_utils import run_bass_kernel_spmd

F32 = mybir.dt.float32
BF16 = mybir.dt.bfloat16
AF = mybir.ActivationFunctionType
ALU = mybir.AluOpType
AX = mybir.AxisListType

D_MODEL = 1024
D_IN = 3476
D_FF = 4096
EPS = 1e-6
NEGM = -30000.0
SAME_ENGINE_SYNC = True


class Prog:
    def __init__(self, nc, es, n_dma=24):
        self.nc = nc
        self.es = es
        self.eng = dict(pe=nc.tensor, act=nc.scalar, dve=nc.vector, pool=nc.gpsimd, sp=nc.sync)
        self.esem = {e: es.enter_context(nc.semaphore("s_" + e)) for e in self.eng}
        self.ecnt = {e: 0 for e in self.eng}
        self.dsem = [es.enter_context(nc.semaphore("d_%d" % i)) for i in range(n_dma)]
        self.dval = [0] * n_dma
        self.dnext = 0
        self.seen = {e: {} for e in self.eng}
        self.lastw = {}
        self.readers = {}
        self.nops = 0

    def _wait(self, eng, ev):
        kind, name, val = ev
        if kind == 'e' and name == eng:
            if eng == 'pe' or eng == 'sp' or not SAME_ENGINE_SYNC:
                return
        key = (kind, name)
        if self.seen[eng].get(key, 0) >= val:
            return
        self.seen[eng][key] = val
        sem = self.esem[name] if kind == 'e' else self.dsem[name]
        self.eng[eng].wait_ge(sem, val)

    def _deps(self, reads, writes):
        evs = []
        for k in reads:
            w = self.lastw.get(k)
            if w is not None:
                evs.append(w)
        for k in writes:
            w = self.lastw.get(k)
            if w is not None:
                evs.append(w)
            evs.extend(self.readers.get(k, ()))
        return evs

    def _commit(self, me, reads, writes):
        for k in reads:
            lst = self.readers.setdefault(k, [])
            lst[:] = [r for r in lst if (r[0], r[1]) != (me[0], me[1])]
            lst.append(me)
        for k in writes:
            self.lastw[k] = me
            self.readers[k] = []

    def op(self, eng, fn, reads=(), writes=()):
        for ev in self._deps(reads, writes):
            self._wait(eng, ev)
        ins = fn(self.eng[eng])
        self.ecnt[eng] += 1
        ins.then_inc(self.esem[eng], 1)
        self._commit(('e', eng, self.ecnt[eng]), reads, writes)
        self.nops += 1
        return ins

    def dma(self, q, out, in_, reads=(), writes=(), **kw):
        if q == 'auto':
            qs = os.environ.get("DMAQ", "sp").split(",")
            self.rr = getattr(self, 'rr', 0) + 1
            q = qs[self.rr % len(qs)]
        slot = self.dnext
        self.dnext = (self.dnext + 1) % len(self.dsem)
        evs = self._deps(reads, writes)
        if self.dval[slot] > 0:
            evs.append(('d', slot, self.dval[slot]))
        for ev in evs:
            self._wait(q, ev)
        self.dval[slot] += 16
        self.eng[q].dma_start(out=out, in_=in_, **kw).then_inc(self.dsem[slot], 16)
        self._commit(('d', slot, self.dval[slot]), reads, writes)
        self.nops += 1

    def finish(self):
        for slot in range(len(self.dsem)):
            if self.dval[slot] > 0:
                self._wait('sp', ('d', slot, self.dval[slot]))
        for e in self.eng:
            if e != 'sp' and self.ecnt[e] > 0:
                self._wait('sp', ('e', e, self.ecnt[e]))


class Builder:
    def __init__(self, S, depth, dbg=None):
        self.S = S
        self.depth = depth
        self.dbg = dbg or []
        self.nc = bass.Bass("TRN2", target_bir_lowering=False)
        self.es = ExitStack()
        self.P = Prog(self.nc, self.es)
        self.uid = 0

    def dram_in(self, name, shape, dt=F32):
        return self.nc.dram_tensor(name, list(shape), dt, kind="ExternalInput").ap()

    def dram_out(self, name, shape, dt=F32):
        return self.nc.dram_tensor(name, list(shape), dt, kind="ExternalOutput").ap()

    def dram_tmp(self, name, shape, dt):
        return self.nc.dram_tensor(name, list(shape), dt, kind="Internal").ap()

    def sb(self, st, name, shape, dt):
        self.uid += 1
        return st.enter_context(self.nc.sbuf_tensor("%s_%d" % (name, self.uid), list(shape), dt))

    def ps(self, st, name, shape, dt=F32):
        self.uid += 1
        return st.enter_context(self.nc.psum_tensor("%s_%d" % (name, self.uid), list(shape), dt))

    def dma_mid(self, out, in_, n_mid, step, reads=(), writes=()):
        for a in range(0, n_mid, step):
            b_ = min(n_mid, a + step)
            self.P.dma('sp', out[:, a:b_, :], in_[:, a:b_, :], reads=reads, writes=writes)

    def barrier(self):
        P = self.P
        for e in P.eng:
            for o in P.eng:
                if P.ecnt[o] > 0 and not (o == e and e in ('pe', 'sp')):
                    key = ('e', o)
                    if P.seen[e].get(key, 0) < P.ecnt[o]:
                        P.seen[e][key] = P.ecnt[o]
                        P.eng[e].wait_ge(P.esem[o], P.ecnt[o])
            for slot in range(len(P.dsem)):
                if P.dval[slot] > 0:
                    P._wait(e, ('d', slot, P.dval[slot]))

    def declare(self):
        S, L = self.S, self.depth
        self.x_in = self.dram_in("x", [S, D_MODEL])
        self.w_in = self.dram_in("w_in", [L, D_MODEL, D_IN])
        self.b_if = self.dram_in("b_if", [L, 8])
        self.conv_qk = self.dram_in("conv_qk", [L, 4, 1024])
        self.m_norm = self.dram_in("m_norm", [L, 512])
        self.moba_qk_norm = self.dram_in("moba_qk_norm", [L, 2, 64])
        self.nsa_q_norm = self.dram_in("nsa_q_norm", [L, 64])
        self.nsa_k_norm = self.dram_in("nsa_k_norm", [L, 3, 64])
        self.cmp_pe = self.dram_in("cmp_pe", [L, 2, 32, 64])
        self.cmp_w1 = self.dram_in("cmp_w1", [L, 2, 2048, 128])
        self.cmp_w2 = self.dram_in("cmp_w2", [L, 2, 128, 64])
        self.w_out = self.dram_in("w_out", [L, 1024, 1024])
        self.norm_mix = self.dram_in("norm_mix", [L, 1024])
        self.norm_ffn = self.dram_in("norm_ffn", [L, 1024])
        self.w_ff1 = self.dram_in("w_ff1", [L, 1024, D_FF])
        self.w_ff2 = self.dram_in("w_ff2", [L, D_FF, 1024])
        self.c_identb = self.dram_in("c_identb", [128, 128], BF16)
        self.c_identf = self.dram_in("c_identf", [128, 128], F32)
        self.c_cos = self.dram_in("c_cos", [S, 8], F32)
        self.c_sin = self.dram_in("c_sin", [S, 8], F32)
        self.c_ut = self.dram_in("c_ut", [128, 128], F32)
        self.c_tri = self.dram_in("c_tri", [128, 128], F32)
        self.c_ind32 = self.dram_in("c_ind32", [32, S], BF16)
        self.c_ind64 = self.dram_in("c_ind64", [64, S], BF16)
        self.c_tm4 = self.dram_in("c_tm4", [128, 4, 512], BF16)
        self.c_trib = self.dram_in("c_trib", [128, 128], BF16)
        self.c_triw = self.dram_in("c_triw", [128, 128], BF16)
        self.c_cmask = self.dram_in("c_cmask", [128, 17, 128], BF16)
        self.c_ovl = self.dram_in("c_ovl", [512, 128], BF16)
        self.c_cols = self.dram_in("c_cols", [128, 4], F32)
        self.y_out = self.dram_out("y", [S, D_MODEL])
        dbg = self.dbg

        def tmp(name, shape, dt):
            if name in dbg:
                return self.dram_out(name, shape, dt)
            return self.dram_tmp(name, shape, dt)
        self.xbuf = tmp("xbuf", [S, D_MODEL], F32)
        self.qkT = tmp("qkT", [1024, S], BF16)
        self.kcvcT = tmp("kcvcT", [128, S], BF16)
        self.gi_d = tmp("gi_d", [4, S], F32)
        self.gf_d = tmp("gf_d", [4, S], F32)
        self.mv_d = tmp("mv_d", [S, 512], BF16)
        self.og_d = tmp("og_d", [S, 512], BF16)
        self.bqT = tmp("bqT", [256, S], BF16)
        self.bkT = tmp("bkT", [256, S], BF16)
        self.bv_d = tmp("bv_d", [S, 256], BF16)
        self.nqrT = tmp("nqrT", [256, S], BF16)
        self.nquT = tmp("nquT", [256, S], BF16)
        self.ksT = tmp("ksT", [64, S], BF16)
        self.kwT = tmp("kwT", [64, S], BF16)
        self.vs_d = tmp("vs_d", [S, 64], BF16)
        self.vw_d = tmp("vw_d", [S, 64], BF16)
        self.ng_d = tmp("ng_d", [S, 12], F32)
        self.mix_d = tmp("mix_d", [S, 1024], BF16)

    def phase1(self, l, xsrc):
        S, P, nc = self.S, self.P, self.nc
        NB = S // 512
        with ExitStack() as st:
            sb, ps = self.sb, self.ps
            w_bf = sb(st, "w_in", [128, 8, D_IN], BF16)
            with ExitStack() as st2:
                stage = [sb(st2, "wst", [128, D_IN], F32) for _ in range(2)]
                for kt in range(8):
                    s_ = stage[kt % 2]
                    P.dma('sp', s_[:], self.w_in[l, kt * 128:(kt + 1) * 128, :], writes=[('wst', kt % 2)])
                    P.op(['pool', 'dve'][kt % 2], lambda e: e.tensor_copy(out=w_bf[:, kt, :], in_=s_[:]),
                         reads=[('wst', kt % 2)], writes=[('w_in', kt)])
                self.barrier()
            identb = sb(st, "identb", [128, 128], BF16)
            gmix = sb(st, "gmix", [128, 1024], F32)
            mnorm = sb(st, "mnorm", [128, 512], F32)
            gain20 = sb(st, "gain20", [128, 20, 64], F32)
            convw = sb(st, "convw", [128, 4, 8], F32)
            bi = sb(st, "bi", [4, 1], F32)
            nbf = sb(st, "nbf", [4, 1], F32)
            cosT = sb(st, "cosT", [128, S // 128, 8], F32)
            sinT = sb(st, "sinT", [128, S // 128, 8], F32)
            P.dma('sp', identb[:], self.c_identb, writes=['identb'])
            P.dma('sp', gmix[:], self.norm_mix[l].partition_broadcast(128), writes=['gmix'])
            P.dma('sp', mnorm[:], self.m_norm[l].partition_broadcast(128), writes=['mnorm'])
            P.op('pool', lambda e: e.memset(gain20[:], 1.0), writes=['gain20'])
            for i in range(20):
                src = None
                if i < 4:
                    src = self.moba_qk_norm[l, 0]
                elif i < 8:
                    src = self.moba_qk_norm[l, 1]
                elif 12 <= i < 16:
                    src = self.nsa_q_norm[l]
                elif i == 16:
                    src = self.nsa_k_norm[l, 1]
                elif i == 18:
                    src = self.nsa_k_norm[l, 2]
                if src is not None:
                    P.dma('sp', gain20[:, i, :], src.partition_broadcast(128), writes=['gain20'])
            with nc.allow_non_contiguous_dma(reason="tiny conv weight transpose"):
                for jj in range(4):
                    P.dma('sp', convw[:, jj, :], self.conv_qk[l, jj].rearrange("(ft p) -> p ft", p=128), writes=['convw'])
                P.dma('sp', bi[:], self.b_if[l, 0:4].rearrange("(p o) -> p o", o=1), writes=['bi'])
                P.dma('sp', nbf[:], self.b_if[l, 4:8].rearrange("(p o) -> p o", o=1), writes=['nbf'])
            P.op('dve', lambda e: e.tensor_scalar(out=nbf[:], in0=nbf[:], scalar1=-1.0, scalar2=None, op0=ALU.mult),
                 reads=['nbf'], writes=['nbf'])
            self.dma_mid(cosT[:], self.c_cos.rearrange("(j p) r -> p j r", p=128), S // 128, 8, writes=['cos'])
            self.dma_mid(sinT[:], self.c_sin.rearrange("(j p) r -> p j r", p=128), S // 128, 8, writes=['sin'])

            mh20 = sb(st, "mh20", [128, 20], F32)
            P.op('pool', lambda e: e.memset(mh20[:], -0.5), writes=['mh20'])
            xt = [sb(st, "xt", [128, 4, 1024], F32) for _ in range(2)]
            junk2_2 = [sb(st, "junk2", [128, 1280], BF16) for _ in range(2)]
            ss_2 = [sb(st, "ss", [128, 4], F32) for _ in range(2)]
            rstd_2 = [sb(st, "rstd", [128, 4], F32) for _ in range(2)]
            h_2 = [sb(st, "h", [128, 4, 1024], BF16)] * 2
            hT_2 = [sb(st, "hT", [128, 8, 512], BF16) for _ in range(2)]
            cbuf = [sb(st, "cbuf", [128, 515], F32) for _ in range(8)]
            for ft in range(8):
                P.op('pool', lambda e: e.memset(cbuf[ft][:, 0:3], 0.0), writes=[('cbuf', ft)])
            acc_2 = [sb(st, "acc", [128, 512], F32) for _ in range(2)]
            fo = [sb(st, "fo", [128, 512], BF16) for _ in range(2)]
            gsb = sb(st, "gsb", [4, 512], F32)
            gsb2 = sb(st, "gsb2", [4, 512], F32)
            mvb_2 = [sb(st, "mvb", [128, 512], BF16) for _ in range(2)]
            sg_2 = [sb(st, "sg", [128, 512], F32) for _ in range(2)]
            ogb_2 = [sb(st, "ogb", [128, 512], BF16) for _ in range(2)]
            tm_2 = [sb(st, "tm", [128, 1292], F32) for _ in range(2)]
            ssh_2 = [sb(st, "ssh", [128, 20], F32) for _ in range(2)]
            rs20_2 = [sb(st, "rs20", [128, 20], F32) for _ in range(2)]
            nrm_2 = [sb(st, "nrm", [128, 20, 64], F32) for _ in range(2)]
            nqu_2 = [sb(st, "nqu", [128, 256], BF16) for _ in range(2)]
            rt_2 = [[sb(st, "rt", [128, 20, 8], F32) for _ in range(4)] for _ in range(2)]
            nb_2 = [sb(st, "nb", [128, 20, 64], BF16) for _ in range(2)]
            tmb_2 = [sb(st, "tmb", [128, 1280], BF16) for _ in range(2)]
            tTa_2 = [sb(st, "tTa", [128, 8, 128], BF16) for _ in range(2)]
            tTb_2 = [sb(st, "tTb", [128, 2, 128], BF16) for _ in range(2)]
            ngs_2 = [sb(st, "ngs", [128, 12], F32) for _ in range(2)]
            pT_2 = [ps(st, "pT", [128, 8, 128], BF16) for _ in range(2)]
            pT = pT_2[0]
            pTb = ps(st, "pTb", [128, 2, 128], BF16)
            pf = [ps(st, "pf", [128, 512], F32) for _ in range(2)]
            pg = [ps(st, "pg", [4, 512], F32) for _ in range(1)]
            pt = [ps(st, "pt", [128, 512], F32) for _ in range(2)]
            pfi = 0
            pti = 0

            def load_x(b):
                P.dma('sp', xt[b % 2][:], xsrc[b * 512:(b + 1) * 512, :].rearrange("(j p) d -> p j d", p=128),
                      reads=[('xres', 2 * b), ('xres', 2 * b + 1)], writes=[('xt', b % 2)])
            load_x(0)
            pend_chain = []
            for b in range(NB):
                t0 = b * 512
                if b + 1 < NB:
                    load_x(b + 1)
                def pro_norm(bb):
                    z = bb % 2
                    x_ = xt[z]
                    kx = ('xt', z)
                    ss, rstd, h = ss_2[z], rstd_2[z], h_2[z]
                    for j in range(4):
                        P.op('act', lambda e: e.activation(out=junk2_2[0][:, 0:1024], in_=x_[:, j, :], func=AF.Square,
                                                           scale=1.0 / 32, accum_out=ss[:, j:j + 1]),
                             reads=[kx], writes=[('junk2', 0), ('ss', z)])
                    P.op('dve', lambda e: e.tensor_scalar(out=ss[:], in0=ss[:], scalar1=EPS, scalar2=None, op0=ALU.add),
                         reads=[('ss', z)], writes=[('ss', z)])
                    P.op('pool', lambda e: e.tensor_tensor(out=rstd[:], in0=ss[:], in1=mh20[:, 0:4], op=ALU.pow),
                         reads=[('ss', z), 'mh20'], writes=[('rstd', z)])
                    for j in range(4):
                        P.op('dve', lambda e: e.scalar_tensor_tensor(out=h[:, j, :], in0=x_[:, j, :], scalar=rstd[:, j:j + 1],
                                                                     in1=gmix[:], op0=ALU.mult, op1=ALU.mult),
                             reads=[kx, ('rstd', z), 'gmix'], writes=[('h', j)])

                def pro_tr(bb):
                    z = bb % 2
                    h, hT_ = h_2[z], hT_2[z]
                    for j in range(4):
                        pT = pT_2[j % 2]
                        for kt in range(8):
                            P.op('pe', lambda e: e.transpose(out=pT[:, kt, :], in_=h[:, j, kt * 128:(kt + 1) * 128],
                                                             identity=identb[:]),
                                 reads=[('h', j), 'identb'], writes=[('pT', j % 2)])
                        P.op(['act', 'dve'][j % 2], lambda e: (e.copy if j % 2 == 0 else e.tensor_copy)(out=hT_[:, :, j * 128:(j + 1) * 128], in_=pT[:]),
                             reads=[('pT', j % 2)], writes=[('hT', z, j)])
                if b == 0:
                    pro_norm(0)
                    pro_tr(0)
                hT = hT_2[b % 2]
                hk = [('hT', b % 2, j) for j in range(4)]
                wk = [('w_in', kt) for kt in range(8)]
                def ft_s1(ft):
                    nonlocal pfi
                    c0 = ft * 128 if ft < 8 else 3080
                    p_ = pf[pfi % 2]
                    pk = ('pf', pfi % 2)
                    pfi += 1
                    for kt in range(8):
                        P.op('pe', lambda e: e.matmul(p_[:], lhsT=w_bf[:, kt, c0:c0 + 128], rhs=hT[:, kt, :],
                                                      start=(kt == 0), stop=(kt == 7)),
                             reads=hk + [wk[kt]], writes=[pk])
                    f_ = fo[ft % 2]
                    fk = ('fo', ft % 2)
                    if ft < 8:
                        cb = cbuf[ft]
                        ck = ('cbuf', ft)
                        P.op('act', lambda e: e.copy(out=cb[:, 3:515], in_=p_[:]), reads=[pk], writes=[ck])
                    else:
                        P.op('act', lambda e: e.copy(out=f_[:], in_=p_[:]), reads=[pk], writes=[fk])
                        P.dma('auto', self.kcvcT[:, t0:t0 + 512], f_[:], reads=[fk], writes=[('kcvcT', b)])

                def ft_s2(ft):
                    if ft >= 8:
                        return
                    f_ = fo[ft % 2]
                    fk = ('fo', ft % 2)
                    cb = cbuf[ft]
                    ck = ('cbuf', ft)
                    acc = acc_2[ft % 2]
                    ak = ('acc', ft % 2)
                    P.op('dve', lambda e: e.tensor_scalar(out=acc[:], in0=cb[:, 0:512], scalar1=convw[:, 0, ft:ft + 1],
                                                          scalar2=None, op0=ALU.mult),
                         reads=[ck, 'convw'], writes=[ak])
                    for jj in range(1, 4):
                        P.op('dve', lambda e: e.scalar_tensor_tensor(out=acc[:], in0=cb[:, jj:jj + 512],
                                                                     scalar=convw[:, jj, ft:ft + 1], in1=acc[:],
                                                                     op0=ALU.mult, op1=ALU.add),
                             reads=[ck, 'convw', ak], writes=[ak])
                    P.op('act', lambda e: e.activation(out=f_[:], in_=acc[:], func=AF.Silu), reads=[ak], writes=[fk])
                    P.dma('auto', self.qkT[ft * 128:(ft + 1) * 128, t0:t0 + 512], f_[:], reads=[fk],
                          writes=[('qkT', ft, b)])
                    P.op('pool', lambda e: e.tensor_copy(out=cb[:, 0:3], in_=cb[:, 512:515]), reads=[ck], writes=[ck])
                ft_s1(0)
                for ft in range(9):
                    if ft + 1 < 9:
                        ft_s1(ft + 1)
                    ft_s2(ft)
                if b + 1 < NB:
                    pro_norm(b + 1)
                def gate_mm(gi_):
                    c0 = 2048 + 4 * gi_
                    for kt in range(8):
                        P.op('pe', lambda e: e.matmul(pg[0][:], lhsT=w_bf[:, kt, c0:c0 + 4], rhs=hT[:, kt, :],
                                                      start=(kt == 0), stop=(kt == 7)),
                             reads=hk + [wk[kt]], writes=[('pg', 0)])
                gate_mm(0)
                P.op('act', lambda e: e.activation(out=gsb[:], in_=pg[0][:], func=AF.Identity, bias=bi[:, 0:1]),
                     reads=[('pg', 0), 'bi'], writes=['gsb'])
                P.dma('auto', self.gi_d[:, t0:t0 + 512], gsb[:], reads=['gsb'], writes=[('gi_d', b)])
                gate_mm(1)
                P.op('act', lambda e: e.activation(out=gsb2[:], in_=pg[0][:], func=AF.Exp, bias=nbf[:, 0:1], scale=-1.0),
                     reads=[('pg', 0), 'nbf'], writes=['gsb2'])
                P.op('act', lambda e: e.activation(out=gsb2[:], in_=gsb2[:], func=AF.Ln, bias=1.0),
                     reads=['gsb2'], writes=['gsb2'])
                P.op('dve', lambda e: e.tensor_scalar(out=gsb2[:], in0=gsb2[:], scalar1=-1.0, scalar2=None, op0=ALU.mult),
                     reads=['gsb2'], writes=['gsb2'])
                P.dma('auto', self.gf_d[:, t0:t0 + 512], gsb2[:], reads=['gsb2'], writes=[('gf_d', b)])
                for j in range(4):
                    tok = slice(t0 + j * 128, t0 + (j + 1) * 128)
                    jg = b * 4 + j
                    z_ = jg % 2
                    mvb, sg, ogb, tm, ssh, rs20, nrm, nqu, nb, tmb, tTa, tTb, ngs = (
                        mvb_2[z_], sg_2[z_], ogb_2[z_], tm_2[z_], ssh_2[z_], rs20_2[z_], nrm_2[z_], nqu_2[z_], nb_2[z_],
                        tmb_2[z_], tTa_2[z_], tTb_2[z_], ngs_2[z_])
                    rt = rt_2[z_]
                    pT = pT_2[z_]
                    junk2 = junk2_2[z_]

                    def tm_mm(c0, n):
                        nonlocal pti
                        p_ = pt[pti % 2]
                        pk = ('pt', pti % 2)
                        pti += 1
                        for kt in range(8):
                            P.op('pe', lambda e: e.matmul(p_[:, 0:n], lhsT=hT[:, kt, j * 128:(j + 1) * 128],
                                                          rhs=w_bf[:, kt, c0:c0 + n], start=(kt == 0), stop=(kt == 7)),
                                 reads=[('hT', b % 2, j), wk[kt]], writes=[pk])
                        if pend_chain:
                            for _ in range(5):
                                try:
                                    next(pend_chain[0])
                                except StopIteration:
                                    pend_chain.pop(0)
                                    break
                        return p_, pk
                    p_, pk = tm_mm(1024, 512)
                    P.op('act', lambda e: e.copy(out=mvb[:], in_=p_[:]), reads=[pk], writes=[('mvb', z_)])
                    P.dma('auto', self.mv_d[tok, :], mvb[:], reads=[('mvb', z_)], writes=[('mv_d', jg)])
                    p_, pk = tm_mm(1536, 512)
                    P.op('act', lambda e: e.activation(out=sg[:], in_=p_[:], func=AF.Sigmoid), reads=[pk], writes=[('sg', z_)])
                    P.op('dve', lambda e: e.tensor_tensor(out=ogb[:], in0=sg[:], in1=mnorm[:], op=ALU.mult),
                         reads=[('sg', z_), 'mnorm'], writes=[('ogb', z_)])
                    P.dma('auto', self.og_d[tok, :], ogb[:], reads=[('ogb', z_)], writes=[('og_d', jg)])
                    p_, pk = tm_mm(2056, 512)
                    P.op('dve', lambda e: e.tensor_copy(out=tm[:, 0:512], in_=p_[:]), reads=[pk], writes=[('tm', z_)])
                    p_, pk = tm_mm(2568, 512)
                    P.op('act', lambda e: e.copy(out=tm[:, 512:1024], in_=p_[:]), reads=[pk], writes=[('tm', z_)])
                    p_, pk = tm_mm(3208, 268)
                    P.op('dve', lambda e: e.tensor_copy(out=tm[:, 1024:1292], in_=p_[:, 0:268]), reads=[pk], writes=[('tm', z_)])
                    def chain(tok=tok, jg=jg, z_=z_, tm=tm, ssh=ssh, rs20=rs20, nrm=nrm, nqu=nqu, nb=nb, tmb=tmb, tTa=tTa,
                              tTb=tTb, ngs=ngs, rt=rt, junk2=junk2, pT=pT):
                        yield
                        P.op('act', lambda e: e.activation(out=junk2[:, 0:1280], in_=tm[:, 0:1280], func=AF.Square),
                             reads=[('tm', z_)], writes=[('junk2', z_)])
                        yield
                        P.op('dve', lambda e: e.tensor_reduce(out=ssh[:], in_=junk2[:, 0:1280].rearrange("p (h d) -> p h d", d=64),
                                                              axis=AX.X, op=ALU.add),
                             reads=[('junk2', z_)], writes=[('ssh', z_)])
                        yield
                        P.op('dve', lambda e: e.tensor_scalar(out=ssh[:], in0=ssh[:], scalar1=1.0 / 64, scalar2=EPS,
                                                              op0=ALU.mult, op1=ALU.add), reads=[('ssh', z_)], writes=[('ssh', z_)])
                        yield
                        P.op('pool', lambda e: e.tensor_tensor(out=rs20[:], in0=ssh[:], in1=mh20[:], op=ALU.pow),
                             reads=[('ssh', z_), 'mh20'], writes=[('rs20', z_)])
                        yield
                        P.op('dve', lambda e: e.tensor_tensor(out=nrm[:], in0=tm[:, 0:1280].rearrange("p (h d) -> p h d", d=64),
                                                              in1=rs20[:].unsqueeze(2).broadcast_to([128, 20, 64]), op=ALU.mult),
                             reads=[('tm', z_), ('rs20', z_)], writes=[('nrm', z_)])
                        yield
                        P.op('dve', lambda e: e.tensor_tensor(out=nrm[:], in0=nrm[:], in1=gain20[:], op=ALU.mult),
                             reads=[('nrm', z_), 'gain20'], writes=[('nrm', z_)])
                        yield
                        P.op('act', lambda e: e.copy(out=nqu[:].rearrange("p (h d) -> p h d", d=64), in_=nrm[:, 12:16, :]),
                             reads=[('nrm', z_)], writes=[('nqu', z_)])
                        cb_ = cosT[:, jg, :].unsqueeze(1).broadcast_to([128, 20, 8])
                        sb_ = sinT[:, jg, :].unsqueeze(1).broadcast_to([128, 20, 8])
                        x1 = nrm[:, :, 0:8]
                        x2 = nrm[:, :, 8:16]
                        yield
                        P.op('dve', lambda e: e.tensor_tensor(out=rt[0][:], in0=x1, in1=cb_, op=ALU.mult),
                             reads=[('nrm', z_), 'cos'], writes=[('rt', 0, z_)])
                        yield
                        P.op('dve', lambda e: e.tensor_tensor(out=rt[1][:], in0=x2, in1=sb_, op=ALU.mult),
                             reads=[('nrm', z_), 'sin'], writes=[('rt', 1, z_)])
                        yield
                        P.op('dve', lambda e: e.tensor_tensor(out=rt[2][:], in0=x2, in1=cb_, op=ALU.mult),
                             reads=[('nrm', z_), 'cos'], writes=[('rt', 2, z_)])
                        yield
                        P.op('dve', lambda e: e.tensor_tensor(out=rt[3][:], in0=x1, in1=sb_, op=ALU.mult),
                             reads=[('nrm', z_), 'sin'], writes=[('rt', 3, z_)])
                        yield
                        P.op('dve', lambda e: e.tensor_tensor(out=x1, in0=rt[0][:], in1=rt[1][:], op=ALU.subtract),
                             reads=[('rt', 0, z_), ('rt', 1, z_)], writes=[('nrm', z_)])
                        yield
                        P.op('dve', lambda e: e.tensor_tensor(out=x2, in0=rt[2][:], in1=rt[3][:], op=ALU.add),
                             reads=[('rt', 2, z_), ('rt', 3, z_)], writes=[('nrm', z_)])
                        yield
                        P.op('act', lambda e: e.copy(out=nb[:], in_=nrm[:]), reads=[('nrm', z_)], writes=[('nb', z_)])
                        yield
                        P.op('act', lambda e: e.copy(out=tmb[:], in_=tm[:, 0:1280]), reads=[('tm', z_)], writes=[('tmb', z_)])
                        yield
                        P.op('act', lambda e: e.activation(out=ngs[:], in_=tm[:, 1280:1292], func=AF.Sigmoid),
                             reads=[('tm', z_)], writes=[('ngs', z_)])
                        srcs = [nb[:, 0:2, :], nb[:, 2:4, :], nb[:, 4:6, :], nb[:, 6:8, :], nb[:, 12:14, :], nb[:, 14:16, :]]
                        yield
                        for i, s_ in enumerate(srcs):
                            P.op('pe', lambda e: e.transpose(out=pT[:, i, :], in_=s_.rearrange("p h d -> p (h d)"),
                                                             identity=identb[:]), reads=[('nb', z_), 'identb'], writes=[('pT', z_)])
                        yield
                        for i in range(2):
                            P.op('pe', lambda e: e.transpose(out=pT[:, 6 + i, :], in_=nqu[:, i * 128:(i + 1) * 128],
                                                             identity=identb[:]), reads=[('nqu', z_), 'identb'], writes=[('pT', z_)])
                        yield
                        for i, hh in enumerate((16, 18)):
                            P.op('pe', lambda e: e.transpose(out=pTb[:, i, :], in_=nb[:, hh:hh + 2, :].rearrange("p h d -> p (h d)"),
                                                             identity=identb[:]), reads=[('nb', z_), 'identb'], writes=['pTb'])
                        yield
                        P.op('dve', lambda e: e.tensor_copy(out=tTa[:], in_=pT[:]), reads=[('pT', z_)], writes=[('tTa', z_)])
                        yield
                        P.op('act', lambda e: e.copy(out=tTb[:], in_=pTb[:]), reads=['pTb'], writes=[('tTb', z_)])
                        yield
                        for i, dst in enumerate((self.bqT, self.bkT, self.nqrT, self.nquT)):
                            P.dma('auto', dst[:, tok].rearrange("(a p) t -> p a t", p=128), tTa[:, 2 * i:2 * i + 2, :],
                                  reads=[('tTa', z_)], writes=[(("bqT","bkT","nqrT","nquT")[i], jg)])
                        yield
                        P.dma('auto', self.ksT[:, tok], tTb[0:64, 0, :], reads=[('tTb', z_)], writes=[('ksT', jg)])
                        yield
                        P.dma('auto', self.kwT[:, tok], tTb[0:64, 1, :], reads=[('tTb', z_)], writes=[('kwT', jg)])
                        yield
                        P.dma('auto', self.bv_d[tok, :], tmb[:, 512:768], reads=[('tmb', z_)], writes=[('bv_d', jg)])
                        yield
                        P.dma('auto', self.vs_d[tok, :], tmb[:, 1088:1152], reads=[('tmb', z_)], writes=[('vs_d', jg)])
                        yield
                        P.dma('auto', self.vw_d[tok, :], tmb[:, 1216:1280], reads=[('tmb', z_)], writes=[('vw_d', jg)])
                        yield
                        P.dma('auto', self.ng_d[tok, :], ngs[:], reads=[('ngs', z_)], writes=[('ng_d', jg)])
                    for _ in (pend_chain.pop(0) if pend_chain else ()):
                        pass
                    pend_chain.append(chain())
                    if j == 1 and b + 1 < NB:
                        pro_tr(b + 1)
            while pend_chain:
                for _ in pend_chain.pop(0):
                    pass
            self.barrier()

    def phase3(self, l, xsrc, xdst):
        S, P, nc = self.S, self.P, self.nc
        NB = S // 256
        with ExitStack() as st:
            sb, ps = self.sb, self.ps
            wo = sb(st, "wo", [128, 8, 1024], BF16)
            w1 = sb(st, "w1", [128, 8, 4096], BF16)
            w2 = sb(st, "w2", [128, 32, 1024], BF16)
            with ExitStack() as st2:
                stage = [sb(st2, "wst3", [128, 4096], F32) for _ in range(2)]
                slabs = []
                for i in range(2):
                    slabs.append((self.w_out[l, i * 512:(i + 1) * 512, :].rearrange("(a p) d -> p a d", p=128),
                                  wo[:, i * 4:(i + 1) * 4, :], ('wo', i), True))
                for kt in range(8):
                    slabs.append((self.w_ff1[l, kt * 128:(kt + 1) * 128, :], w1[:, kt, :], ('w1', kt), False))
                for i in range(8):
                    slabs.append((self.w_ff2[l, i * 512:(i + 1) * 512, :].rearrange("(a p) d -> p a d", p=128),
                                  w2[:, i * 4:(i + 1) * 4, :], ('w2', i), True))
                for n, (src, dst, key, three) in enumerate(slabs):
                    s_ = stage[n % 2]
                    sv = s_[:].rearrange("p (a d) -> p a d", a=4) if three else s_[:]
                    P.dma('sp', sv, src, writes=[('wst3', n % 2)])
                    eng = ['pool', 'dve', 'act'][n % 3]
                    if eng == 'act':
                        P.op(eng, lambda e: e.copy(out=dst, in_=sv), reads=[('wst3', n % 2)], writes=[key])
                    else:
                        P.op(eng, lambda e: e.tensor_copy(out=dst, in_=sv), reads=[('wst3', n % 2)], writes=[key])
                self.barrier()
            identb = sb(st, "identb3", [128, 128], BF16)
            gffn = sb(st, "gffn", [128, 1024], F32)
            P.dma('sp', identb[:], self.c_identb, writes=['identb'])
            P.dma('sp', gffn[:], self.norm_ffn[l].partition_broadcast(128), writes=['gffn'])
            xt_2 = [sb(st, "xt3", [128, 2, 1024], F32) for _ in range(2)]
            mixb_2 = [sb(st, "mixb", [128, 2, 1024], BF16) for _ in range(2)]
            mh3 = sb(st, "mh3", [128, 2], F32)
            P.op('pool', lambda e: e.memset(mh3[:], -0.5), writes=['mh3'])
            mT = sb(st, "mT", [128, 8, 256], BF16)
            hT = sb(st, "hT3", [128, 8, 256], BF16)
            uT = sb(st, "uT", [128, 32, 256], BF16)
            rr = [sb(st, "rr", [128, 256], F32) for _ in range(2)]
            junk = sb(st, "junk3", [128, 1024], BF16)
            ss = sb(st, "ss3", [128, 2], F32)
            rstd = sb(st, "rstd3", [128, 2], F32)
            pT = ps(st, "pT3", [128, 8, 128], BF16)
            py = [ps(st, "py", [128, 512], F32) for _ in range(2)]
            pz = [ps(st, "pz", [128, 256], F32) for _ in range(2)]
            pyi = 0
            pzi = 0
            wok = [('wo', 0), ('wo', 1)]
            def load3(b):
                t0_ = b * 256
                P.dma('sp', xt_2[b % 2][:], xsrc[t0_:t0_ + 256, :].rearrange("(j p) d -> p j d", p=128), reads=[('xres', b)],
                      writes=[('xt', b % 2)])
                P.dma('sp', mixb_2[b % 2][:], self.mix_d[t0_:t0_ + 256, :].rearrange("(j p) d -> p j d", p=128),
                      writes=[('mixb', b % 2)])
            load3(0)
            def frontA(b):
                nonlocal pyi, pzi
                t0 = b * 256
                xt = xt_2[b % 2]
                mixb = mixb_2[b % 2]
                xk = ('xt', b % 2)
                mk = ('mixb', b % 2)
                for j in range(2):
                    for kt in range(8):
                        P.op('pe', lambda e: e.transpose(out=pT[:, kt, :], in_=mixb[:, j, kt * 128:(kt + 1) * 128],
                                                         identity=identb[:]), reads=[mk, 'identb'], writes=['pT'])
                    P.op('act', lambda e: e.copy(out=mT[:, :, j * 128:(j + 1) * 128], in_=pT[:]),
                         reads=['pT'], writes=[('mT', j)])
                for j in range(2):
                    for hf in range(2):
                        p_ = py[pyi % 2]
                        pk = ('py', pyi % 2)
                        pyi += 1
                        for kt in range(8):
                            P.op('pe', lambda e: e.matmul(p_[:], lhsT=mT[:, kt, j * 128:(j + 1) * 128],
                                                          rhs=wo[:, kt, hf * 512:(hf + 1) * 512],
                                                          start=(kt == 0), stop=(kt == 7)),
                                 reads=[('mT', j), wok[kt // 4]], writes=[pk])
                        P.op('dve', lambda e: e.tensor_tensor(out=xt[:, j, hf * 512:(hf + 1) * 512],
                                                              in0=xt[:, j, hf * 512:(hf + 1) * 512], in1=p_[:], op=ALU.add),
                             reads=[xk, pk], writes=[xk])
                for j in range(2):
                    P.op('act', lambda e: e.activation(out=junk[:], in_=xt[:, j, :], func=AF.Square, scale=1.0 / 32,
                                                       accum_out=ss[:, j:j + 1]), reads=[xk], writes=['junk', 'ss'])
                P.op('dve', lambda e: e.tensor_scalar(out=ss[:], in0=ss[:], scalar1=EPS, scalar2=None, op0=ALU.add),
                     reads=['ss'], writes=['ss'])
                P.op('pool', lambda e: e.tensor_tensor(out=rstd[:], in0=ss[:], in1=mh3[:], op=ALU.pow), reads=['ss', 'mh3'], writes=['rstd'])
                for j in range(2):
                    P.op('dve', lambda e: e.scalar_tensor_tensor(out=mixb[:, j, :], in0=xt[:, j, :], scalar=rstd[:, j:j + 1],
                                                                 in1=gffn[:], op0=ALU.mult, op1=ALU.mult),
                         reads=[xk, 'rstd', 'gffn'], writes=[mk])

            def frontB(b):
                nonlocal pyi, pzi
                t0 = b * 256
                xt = xt_2[b % 2]
                mixb = mixb_2[b % 2]
                xk = ('xt', b % 2)
                mk = ('mixb', b % 2)
                for j in range(2):
                    for kt in range(8):
                        P.op('pe', lambda e: e.transpose(out=pT[:, kt, :], in_=mixb[:, j, kt * 128:(kt + 1) * 128],
                                                         identity=identb[:]), reads=[mk, 'identb'], writes=['pT'])
                    P.op('act', lambda e: e.copy(out=hT[:, :, j * 128:(j + 1) * 128], in_=pT[:]),
                         reads=['pT'], writes=[('hT', j)])

            def ffn1(b):
                nonlocal pyi, pzi
                t0 = b * 256
                xt = xt_2[b % 2]
                mixb = mixb_2[b % 2]
                xk = ('xt', b % 2)
                mk = ('mixb', b % 2)
                hk = [('hT', 0), ('hT', 1)]
                for ft in range(32):
                    p_ = pz[pzi % 2]
                    pk = ('pz', pzi % 2)
                    r_ = rr[pzi % 2]
                    rk = ('rr', pzi % 2)
                    pzi += 1
                    for kt in range(8):
                        P.op('pe', lambda e: e.matmul(p_[:], lhsT=w1[:, kt, ft * 128:(ft + 1) * 128], rhs=hT[:, kt, :],
                                                      start=(kt == 0), stop=(kt == 7)),
                             reads=hk + [('w1', kt)], writes=[pk])
                    P.op('act', lambda e: e.activation(out=r_[:], in_=p_[:], func=AF.Relu), reads=[pk], writes=[rk])
                    P.op('dve', lambda e: e.tensor_tensor(out=uT[:, ft, :], in0=r_[:], in1=r_[:], op=ALU.mult),
                         reads=[rk], writes=[('uT', ft)])

            def ffn2(b):
                nonlocal pyi, pzi
                t0 = b * 256
                xt = xt_2[b % 2]
                mixb = mixb_2[b % 2]
                xk = ('xt', b % 2)
                mk = ('mixb', b % 2)
                uk = [('uT', ft) for ft in range(32)]
                for j in range(2):
                    for hf in range(2):
                        p_ = py[pyi % 2]
                        pk = ('py', pyi % 2)
                        pyi += 1
                        for ft in range(32):
                            P.op('pe', lambda e: e.matmul(p_[:], lhsT=uT[:, ft, j * 128:(j + 1) * 128],
                                                          rhs=w2[:, ft, hf * 512:(hf + 1) * 512],
                                                          start=(ft == 0), stop=(ft == 31)),
                                 reads=[uk[ft], ('w2', ft // 4)], writes=[pk])
                        P.op('dve', lambda e: e.tensor_tensor(out=xt[:, j, hf * 512:(hf + 1) * 512],
                                                              in0=xt[:, j, hf * 512:(hf + 1) * 512], in1=p_[:], op=ALU.add),
                             reads=[xk, pk], writes=[xk])
                P.dma('sp', xdst[t0:t0 + 256, :].rearrange("(j p) d -> p j d", p=128), xt[:], reads=[xk],
                      writes=[('xres', b)])
            frontA(0)
            frontB(0)
            for b in range(NB):
                if b + 1 < NB:
                    load3(b + 1)
                ffn1(b)
                if b + 1 < NB:
                    frontA(b + 1)
                ffn2(b)
                if b + 1 < NB:
                    frontB(b + 1)
            self.barrier()

    def _ml_norm(self, P, s_, sk, pO_t, pok, junk, flT, mhalf, ym_t, ymk, og_t, ogk, hh, c, cl):
        P.op('act', lambda e: e.activation(out=junk[:], in_=pO_t[0:64, 0:128], func=AF.Square,
                                           scale=128.0 ** -0.5, accum_out=s_[:, 0:1]),
             reads=[pok], writes=['junk', sk])
        P.op('dve', lambda e: e.tensor_scalar(out=s_[:, 6:7], in0=pO_t[0:64, 128:129], scalar1=-1.0,
                                              scalar2=flT[:, hh, c:c + 1], op0=ALU.mult, op1=ALU.max),
             reads=[pok, 'flT'], writes=[sk])
        P.op('dve', lambda e: e.tensor_tensor(out=s_[:, 1:2], in0=s_[:, 6:7], in1=pO_t[0:64, 128:129], op=ALU.max),
             reads=[pok, sk], writes=[sk])
        P.op('dve', lambda e: e.tensor_tensor(out=s_[:, 2:3], in0=s_[:, 1:2], in1=s_[:, 1:2], op=ALU.mult),
             reads=[sk], writes=[sk])
        P.op('dve', lambda e: e.scalar_tensor_tensor(out=s_[:, 3:4], in0=s_[:, 2:3], scalar=EPS, in1=s_[:, 0:1],
                                                     op0=ALU.mult, op1=ALU.add), reads=[sk], writes=[sk])
        P.op('pool', lambda e: e.tensor_tensor(out=s_[:, 5:6], in0=s_[:, 3:4], in1=mhalf[:, 0:1], op=ALU.pow),
             reads=[sk, 'mhalf'], writes=[sk])
        P.op('dve', lambda e: e.scalar_tensor_tensor(out=ym_t[:, cl, hh * 128:(hh + 1) * 128],
                                                     in0=pO_t[0:64, 0:128], scalar=s_[:, 5:6],
                                                     in1=og_t[:, cl, hh * 128:(hh + 1) * 128],
                                                     op0=ALU.mult, op1=ALU.mult),
             reads=[pok, sk, ogk], writes=[ymk])

    def phase2_mlstm(self, l):
        S, P, nc = self.S, self.P, self.nc
        NCH = S // 64
        LNSC = math.log(128.0 ** -0.5)
        with ExitStack() as st:
            sb, ps = self.sb, self.ps
            identb = sb(st, "identbm", [128, 128], BF16)
            identf = sb(st, "identfm", [128, 128], F32)
            ut = sb(st, "ut", [128, 128], F32)
            tri = sb(st, "tri", [128, 128], F32)
            P.dma('sp', identb[:], self.c_identb, writes=['identb'])
            P.dma('sp', identf[:], self.c_identf, writes=['identf'])
            P.dma('sp', ut[:], self.c_ut, writes=['ut'])
            P.dma('sp', tri[:], self.c_tri, writes=['tri'])
            uT = sb(st, "uT_m", [64, 4, NCH], F32)
            u2T = sb(st, "u2T_m", [64, 4, NCH], F32)
            flT = sb(st, "flT_m", [64, 4, NCH], F32)
            decB = sb(st, "decB", [128, 4, NCH], F32)
            with ExitStack() as s2:
                li = sb(s2, "li", [NCH, 4, 64], F32)
                lf = sb(s2, "lf", [NCH, 4, 64], F32)
                ones = sb(s2, "ones", [NCH, 64], F32)
                Fin = sb(s2, "Fin", [NCH, 4, 64], F32)
                Ft = sb(s2, "Ft", [NCH, 4, 64], F32)
                a_ = sb(s2, "a_", [NCH, 4, 64], F32)
                Ain = sb(s2, "Ain", [NCH, 4, 64], F32)
                tot = sb(s2, "tot", [NCH, 4], F32)
                cmax = sb(s2, "cmax", [NCH, 4], F32)
                cmT = sb(s2, "cmT", [4, NCH], F32)
                ET = sb(s2, "ET", [4, NCH], F32)
                ETn = sb(s2, "ETn", [4, NCH], F32)
                Ec = sb(s2, "Ec", [NCH, 4], F32)
                Enc = sb(s2, "Enc", [NCH, 4], F32)
                tmp = sb(s2, "tmpm", [NCH, 4, 64], F32)
                uu = sb(s2, "uu", [NCH, 4, 64], F32)
                uu2 = sb(s2, "uu2", [NCH, 4, 64], F32)
                fl = sb(s2, "fl", [NCH, 4, 64], F32)
                dec = sb(s2, "dec", [NCH, 4], F32)
                decrep = sb(s2, "decrep", [NCH, 4, 128], F32)
                pa = ps(s2, "pa", [128, 512], F32)
                pb = ps(s2, "pb", [128, 512], F32)
                P.dma('sp', li[:], self.gi_d.rearrange("h (c j) -> c h j", j=64),
                      reads=[('gi_d', b) for b in range(S // 512)], writes=['li'])
                P.dma('sp', lf[:], self.gf_d.rearrange("h (c j) -> c h j", j=64),
                      reads=[('gf_d', b) for b in range(S // 512)], writes=['lf'])
                P.op('pool', lambda e: e.memset(ones[:], 1.0), writes=['ones'])
                for hh in range(4):
                    P.op('dve', lambda e: e.tensor_tensor_scan(out=Fin[:, hh, :], data0=ones[:], data1=lf[:, hh, :],
                                                               initial=0.0, op0=ALU.mult, op1=ALU.add),
                         reads=['ones', 'lf'], writes=['Fin'])
                P.op('dve', lambda e: e.tensor_copy(out=tot[:], in_=Fin[:, :, 63]), reads=['Fin'], writes=['tot'])
                P.op('pe', lambda e: e.matmul(pa[0:NCH, 0:4], lhsT=ut[0:NCH, 0:NCH], rhs=tot[:], start=True, stop=True),
                     reads=['ut', 'tot'], writes=['pa'])
                P.op('dve', lambda e: e.tensor_tensor(out=Ft[:], in0=Fin[:],
                                                      in1=pa[0:NCH, 0:4].unsqueeze(2).broadcast_to([NCH, 4, 64]), op=ALU.add),
                     reads=['Fin', 'pa'], writes=['Ft'])
                P.op('dve', lambda e: e.tensor_tensor(out=a_[:], in0=li[:], in1=Ft[:], op=ALU.subtract),
                     reads=['li', 'Ft'], writes=['a_'])
                for hh in range(4):
                    P.op('dve', lambda e: e.tensor_tensor_scan(out=Ain[:, hh, :], data0=a_[:, hh, :], data1=a_[:, hh, :],
                                                               initial=-1e30, op0=ALU.max, op1=ALU.max),
                         reads=['a_'], writes=['Ain'])
                P.op('dve', lambda e: e.tensor_copy(out=cmax[:], in_=Ain[:, :, 63]), reads=['Ain'], writes=['cmax'])
                P.op('pe', lambda e: e.transpose(out=pb[0:4, 0:NCH], in_=cmax[:], identity=identf[0:NCH, 0:NCH]),
                     reads=['cmax', 'identf'], writes=['pb'])
                P.op('dve', lambda e: e.tensor_copy(out=cmT[:], in_=pb[0:4, 0:NCH]), reads=['pb'], writes=['cmT'])
                P.op('dve', lambda e: e.tensor_tensor_scan(out=ET[:], data0=cmT[:], data1=cmT[:], initial=0.0,
                                                           op0=ALU.max, op1=ALU.max), reads=['cmT'], writes=['ET'])
                if NCH > 1:
                    P.op('dve', lambda e: e.tensor_copy(out=ETn[:, 0:NCH - 1], in_=ET[:, 1:NCH]), reads=['ET'], writes=['ETn'])
                P.op('dve', lambda e: e.tensor_copy(out=ETn[:, NCH - 1:NCH], in_=ET[:, NCH - 1:NCH]), reads=['ET'], writes=['ETn'])
                P.op('pe', lambda e: e.transpose(out=pa[0:NCH, 0:4], in_=ET[:], identity=identf[0:4, 0:4]),
                     reads=['ET', 'identf'], writes=['pa'])
                P.op('dve', lambda e: e.tensor_copy(out=Ec[:], in_=pa[0:NCH, 0:4]), reads=['pa'], writes=['Ec'])
                P.op('pe', lambda e: e.transpose(out=pb[0:NCH, 0:4], in_=ETn[:], identity=identf[0:4, 0:4]),
                     reads=['ETn', 'identf'], writes=['pb'])
                P.op('dve', lambda e: e.tensor_copy(out=Enc[:], in_=pb[0:NCH, 0:4]), reads=['pb'], writes=['Enc'])
                Eb = Ec[:].unsqueeze(2).broadcast_to([NCH, 4, 64])
                Enb = Enc[:].unsqueeze(2).broadcast_to([NCH, 4, 64])
                P.op('dve', lambda e: e.tensor_tensor(out=tmp[:], in0=a_[:], in1=Eb, op=ALU.subtract),
                     reads=['a_', 'Ec'], writes=['tmp'])
                P.op('act', lambda e: e.activation(out=uu[:], in_=tmp[:], func=AF.Exp, bias=LNSC), reads=['tmp'], writes=['uu'])
                P.op('dve', lambda e: e.tensor_tensor(out=tmp[:], in0=a_[:], in1=Enb, op=ALU.subtract),
                     reads=['a_', 'Enc'], writes=['tmp'])
                P.op('act', lambda e: e.activation(out=uu2[:], in_=tmp[:], func=AF.Exp, bias=LNSC), reads=['tmp'], writes=['uu2'])
                P.op('dve', lambda e: e.tensor_tensor(out=tmp[:], in0=Ft[:], in1=Eb, op=ALU.add),
                     reads=['Ft', 'Ec'], writes=['tmp'])
                P.op('act', lambda e: e.activation(out=fl[:], in_=tmp[:], func=AF.Exp, scale=-1.0), reads=['tmp'], writes=['fl'])
                P.op('dve', lambda e: e.tensor_tensor(out=dec[:], in0=Ec[:], in1=Enc[:], op=ALU.subtract),
                     reads=['Ec', 'Enc'], writes=['dec'])
                P.op('act', lambda e: e.activation(out=dec[:], in_=dec[:], func=AF.Exp), reads=['dec'], writes=['dec'])
                P.op('dve', lambda e: e.tensor_copy(out=decrep[:], in_=dec[:].unsqueeze(2).broadcast_to([NCH, 4, 128])),
                     reads=['dec'], writes=['decrep'])
                for src, dst, nm in ((uu, uT, 'uT'), (uu2, u2T, 'u2T'), (fl, flT, 'flT')):
                    for hh in range(4):
                        P.op('pe', lambda e: e.transpose(out=pa[0:64, hh * 128:hh * 128 + NCH], in_=src[:, hh, :],
                                                         identity=identf[0:NCH, 0:NCH]),
                             reads=['uu', 'uu2', 'fl', 'identf'], writes=['pa'])
                    P.op('dve', lambda e: e.tensor_copy(out=dst[:], in_=pa[0:64, :].rearrange("p (h c) -> p h c", h=4)[:, :, 0:NCH]),
                         reads=['pa'], writes=[nm])
                for hh in range(4):
                    P.op('pe', lambda e: e.matmul(pb[:, hh * 128:hh * 128 + NCH], lhsT=decrep[:, hh, :],
                                                  rhs=identf[0:NCH, 0:NCH], start=True, stop=True),
                         reads=['decrep', 'identf'], writes=['pb'])
                P.op('dve', lambda e: e.tensor_copy(out=decB[:], in_=pb[:].rearrange("p (h c) -> p h c", h=4)[:, :, 0:NCH]),
                     reads=['pb'], writes=['decB'])
                self.barrier()
            NG = S // 512
            qg = [sb(st, "qg", [128, 4, 512], BF16) for _ in range(2)]
            kg = [sb(st, "kg", [128, 4, 512], BF16) for _ in range(2)]
            vg = [sb(st, "vg", [64, 8, 4, 129], BF16) for _ in range(2)]
            ogg = [sb(st, "ogg", [64, 8, 512], BF16) for _ in range(2)]
            ym = [sb(st, "ym", [64, 8, 512], BF16) for _ in range(2)]
            G = [sb(st, "G", [128, 129], F32) for _ in range(4)]
            Gb = [sb(st, "Gb", [128, 129], BF16) for _ in range(4)]
            ku2 = [sb(st, "ku2", [64, 128], BF16) for _ in range(2)]
            Sm = [sb(st, "Sm", [64, 64], BF16) for _ in range(2)]
            Smu = [sb(st, "Smu", [64, 64], F32) for _ in range(2)]
            junk = sb(st, "junkm", [64, 128], BF16)
            mhalf = sb(st, "mhalf", [64, 4], F32)
            P.op('pool', lambda e: e.memset(mhalf[:], -0.5), writes=['mhalf'])
            osb = [sb(st, "osb", [64, 4, 129], F32) for _ in range(2)]
            ssq = [sb(st, "ssq", [64, 4], F32) for _ in range(2)]
            nt = [sb(st, "nt", [64, 4, 4], F32) for _ in range(2)]
            ytmp = sb(st, "ytmp", [64, 4, 128], F32)
            sc = [sb(st, "scm", [64, 8], F32) for _ in range(2)]
            pkT = [ps(st, "pkT", [128, 1024], BF16) for _ in range(2)]
            pS = [ps(st, "pS", [128, 512], F32) for _ in range(2)]
            pO = [ps(st, "pO", [128, 512], F32) for _ in range(2)]
            pG = [ps(st, "pG", [128, 512], F32) for _ in range(2)]
            for i in range(2):
                P.op('pool', lambda e: e.memset(vg[i][:], 1.0), writes=[('vg', i)])
            for hh in range(4):
                P.op('pool', lambda e: e.memset(G[hh][:], 0.0), writes=[('G', hh)])
                P.op('pool', lambda e: e.memset(Gb[hh][:], 0.0), writes=[('Gb', hh)])

            def load_group(g):
                i = g % 2
                tk = slice(g * 512, (g + 1) * 512)
                P.dma('sp', qg[i][:], self.qkT[0:512, tk].rearrange("(h p) t -> p h t", p=128),
                      reads=[('qkT', ft, g) for ft in range(4)], writes=[('qg', i)])
                P.dma('sp', kg[i][:], self.qkT[512:1024, tk].rearrange("(h p) t -> p h t", p=128),
                      reads=[('qkT', ft, g) for ft in range(4, 8)], writes=[('kg', i)])
                for hh in range(4):
                    P.dma('sp', vg[i][:, :, hh, 0:128],
                          self.mv_d[tk, hh * 128:(hh + 1) * 128].rearrange("(c s) e -> s c e", s=64),
                          reads=[('mv_d', 4 * g + j) for j in range(4)], writes=[('vg', i)])
                P.dma('sp', ogg[i][:], self.og_d[tk, :].rearrange("(c s) e -> s c e", s=64),
                      reads=[('og_d', 4 * g + j) for j in range(4)], writes=[('ogg', i)])
            load_group(0)
            steps = [(g, cl, hh) for g in range(NG) for cl in range(8) for hh in range(4)]

            def stageA(n):
                g, cl, hh = steps[n]
                gi_ = g % 2
                c = g * 8 + cl
                i2 = n % 2
                k_ = kg[gi_][:, hh, cl * 64:(cl + 1) * 64]
                q_ = qg[gi_][:, hh, cl * 64:(cl + 1) * 64]
                P.op('pe', lambda e: e.transpose(out=pkT[i2][0:64, 0:128], in_=k_, identity=identb[:]),
                     reads=[('kg', gi_), 'identb'], writes=[('pkT', i2)])
                P.op('pe', lambda e: e.matmul(pS[i2][0:64, 0:64], lhsT=k_, rhs=q_, start=True, stop=True),
                     reads=[('kg', gi_), ('qg', gi_)], writes=[('pS', i2)])
                P.op('act', lambda e: e.activation(out=ku2[i2][:], in_=pkT[i2][0:64, 0:128], func=AF.Copy,
                                                   scale=u2T[:, hh, c:c + 1]),
                     reads=[('pkT', i2), 'u2T'], writes=[('ku2', i2)])
                P.op('act', lambda e: e.activation(out=Smu[i2][:], in_=pS[i2][0:64, 0:64], func=AF.Copy,
                                                   scale=uT[:, hh, c:c + 1]),
                     reads=[('pS', i2), 'uT'], writes=[('Smu', i2)])
                P.op('pool', lambda e: e.tensor_tensor(out=Sm[i2][:], in0=Smu[i2][:], in1=tri[0:64, 0:64], op=ALU.mult),
                     reads=[('Smu', i2), 'tri'], writes=[('Sm', i2)])

            def stageB(n):
                g, cl, hh = steps[n]
                gi_ = g % 2
                c = g * 8 + cl
                i2 = n % 2
                q_ = qg[gi_][:, hh, cl * 64:(cl + 1) * 64]
                v_ = vg[gi_][:, cl, hh, :]
                P.op('pe', lambda e: e.matmul(pO[i2][0:64, 0:129], lhsT=Sm[i2][:], rhs=v_, start=True, stop=False),
                     reads=[('Sm', i2), ('vg', gi_)], writes=[('pO', i2)])
                P.op('pe', lambda e: e.matmul(pO[i2][0:64, 0:129], lhsT=q_, rhs=Gb[hh][:], start=False, stop=True),
                     reads=[('qg', gi_), ('Gb', hh)], writes=[('pO', i2)])
                P.op('pe', lambda e: e.matmul(pG[i2][:, 0:129], lhsT=ku2[i2][:], rhs=v_, start=True, stop=True),
                     reads=[('ku2', i2), ('vg', gi_)], writes=[('pG', i2)])
                P.op('dve', lambda e: e.scalar_tensor_tensor(out=G[hh][:], in0=G[hh][:], scalar=decB[:, hh, c:c + 1],
                                                             in1=pG[i2][:, 0:129], op0=ALU.mult, op1=ALU.add),
                     reads=[('G', hh), 'decB', ('pG', i2)], writes=[('G', hh)])
                P.op('dve', lambda e: e.tensor_copy(out=Gb[hh][:], in_=G[hh][:]), reads=[('G', hh)], writes=[('Gb', hh)])
                cb = c % 2
                P.op('act', lambda e: e.activation(out=junk[:], in_=pO[i2][0:64, 0:128], func=AF.Square,
                                                   scale=128.0 ** -0.5, accum_out=ssq[cb][:, hh:hh + 1]),
                     reads=[('pO', i2)], writes=['junk', ('ssq', cb, hh)])
                P.op('act', lambda e: e.copy(out=osb[cb][:, hh, :], in_=pO[i2][0:64, 0:129]),
                     reads=[('pO', i2)], writes=[('osb', cb, hh)])

            def stageN(n):
                g, cl, hh = steps[n]
                if hh != 3:
                    return
                gi_ = g % 2
                c = g * 8 + cl
                cb = c % 2
                t_ = nt[cb]
                tk = ('nt', cb)
                den = osb[cb][:, :, 128]
                ok = [('osb', cb, h_) for h_ in range(4)]
                P.op('dve', lambda e: e.tensor_scalar(out=t_[:, 0, :], in0=den, scalar1=-1.0, scalar2=None, op0=ALU.mult),
                     reads=ok, writes=[tk])
                P.op('dve', lambda e: e.tensor_tensor(out=t_[:, 0, :], in0=t_[:, 0, :], in1=flT[:, :, c], op=ALU.max),
                     reads=[tk, 'flT'], writes=[tk])
                P.op('dve', lambda e: e.tensor_tensor(out=t_[:, 0, :], in0=t_[:, 0, :], in1=den, op=ALU.max),
                     reads=[tk] + ok, writes=[tk])
                P.op('dve', lambda e: e.tensor_tensor(out=t_[:, 1, :], in0=t_[:, 0, :], in1=t_[:, 0, :], op=ALU.mult),
                     reads=[tk], writes=[tk])
                P.op('dve', lambda e: e.scalar_tensor_tensor(out=t_[:, 2, :], in0=t_[:, 1, :], scalar=EPS, in1=ssq[cb][:],
                                                             op0=ALU.mult, op1=ALU.add),
                     reads=[tk] + [('ssq', cb, h_) for h_ in range(4)], writes=[tk])
                P.op('pool', lambda e: e.tensor_tensor(out=t_[:, 3, :], in0=t_[:, 2, :], in1=mhalf[:], op=ALU.pow),
                     reads=[tk, 'mhalf'], writes=[tk])
                P.op('dve', lambda e: e.tensor_tensor(out=ytmp[:], in0=osb[cb][:, :, 0:128],
                                                      in1=t_[:, 3, :].unsqueeze(2).broadcast_to([64, 4, 128]), op=ALU.mult),
                     reads=ok + [tk], writes=['ytmp'])
                P.op('dve', lambda e: e.tensor_tensor(out=ym[gi_][:, cl, :].rearrange("p (h d) -> p h d", h=4), in0=ytmp[:],
                                                      in1=ogg[gi_][:, cl, :].rearrange("p (h d) -> p h d", h=4), op=ALU.mult),
                     reads=['ytmp', ('ogg', gi_)], writes=[('ym', gi_)])
                if cl == 7:
                    P.dma('sp', self.mix_d[g * 512:(g + 1) * 512, 0:512].rearrange("(c s) e -> s c e", s=64), ym[gi_][:],
                          reads=[('ym', gi_)], writes=[('mixm', g)])
            NS = len(steps)
            stageA(0)
            for n in range(NS):
                g, cl, hh = steps[n]
                if n + 1 < NS:
                    stageA(n + 1)
                stageB(n)
                if n >= 1:
                    stageN(n - 1)
                if cl == 0 and hh == 1 and g + 1 < NG:
                    load_group(g + 1)
            stageN(NS - 1)
            self.barrier()

    def _negB(self, st, name, g1, g2):
        P = self.P
        ga = self.sb(st, name + "_ga", [128, 64], F32)
        gb = self.sb(st, name + "_gb", [128, 64], F32)
        m1 = self.sb(st, name + "_m1", [128, 1], F32)
        m2 = self.sb(st, name + "_m2", [128, 1], F32)
        nb = self.sb(st, name, [128, 1], F32)
        P.dma('sp', ga[:], g1.partition_broadcast(128), writes=[name + 'ga'])
        P.dma('sp', gb[:], g2.partition_broadcast(128), writes=[name + 'gb'])
        P.op('dve', lambda e: e.tensor_reduce(out=m1[:], in_=ga[:], axis=AX.X, op=ALU.max, apply_absolute_value=True),
             reads=[name + 'ga'], writes=[name + 'm1'])
        P.op('dve', lambda e: e.tensor_reduce(out=m2[:], in_=gb[:], axis=AX.X, op=ALU.max, apply_absolute_value=True),
             reads=[name + 'gb'], writes=[name + 'm2'])
        P.op('dve', lambda e: e.scalar_tensor_tensor(out=nb[:], in0=m1[:], scalar=-8.0, in1=m2[:], op0=ALU.mult, op1=ALU.mult),
             reads=[name + 'm1', name + 'm2'], writes=[name])
        return nb

    def phase2_moba(self, l):
        S, P, nc = self.S, self.P, self.nc
        NQC = S // 512
        NKT = S // 128
        NB = S // 256
        with ExitStack() as st:
            sb, ps = self.sb, self.ps
            identb = sb(st, "identb_b", [128, 128], BF16)
            tm4 = sb(st, "tm4", [128, 4, 512], BF16)
            zl = sb(st, "zl", [1, 128], BF16)
            zr = sb(st, "zr", [1, 512], BF16)
            P.dma('sp', identb[:], self.c_identb, writes=['identb'])
            P.dma('sp', tm4[:], self.c_tm4, writes=['tm4'])
            P.op('pool', lambda e: e.memset(zl[:], 0.0), writes=['zl'])
            P.op('pool', lambda e: e.memset(zr[:], 0.0), writes=['zr'])
            negB = self._negB(st, "negBm", self.moba_qk_norm[l, 0], self.moba_qk_norm[l, 1])
            KX = sb(st, "KX", [96, S], BF16)
            VX = sb(st, "VX", [128, NKT, 65], BF16)
            kmean = sb(st, "kmean", [64, 32], F32)
            kmb = sb(st, "kmb", [64, 32], BF16)
            QX = [sb(st, "QX", [96, 512], BF16) for _ in range(2)]
            gsb = sb(st, "gsb_b", [128, 32], F32)
            m8 = sb(st, "m8", [128, 8], F32)
            sel = sb(st, "sel_b", [128, 32], F32)
            MBw = sb(st, "MBw", [128, 128], BF16)
            MLA = int(os.environ.get("LA", "3"))
            PT = [sb(st, "PT", [128, 512], BF16) for _ in range(MLA + 2)]
            rz = sb(st, "rz_b", [128, 4], F32)
            yb = [sb(st, "yb", [128, 4, 64], BF16) for _ in range(2)]
            pS = [ps(st, "pS_b", [128, 512], F32) for _ in range(MLA + 1)]
            pO = [ps(st, "pO_b", [128, 512], F32) for _ in range(2)]
            pM = ps(st, "pM_b", [128, 512], F32)
            pMb = ps(st, "pMb_b", [128, 1024], BF16)
            P.dma('sp', KX[64:96, :], self.c_ind32, writes=['KXi'])
            P.op('pool', lambda e: e.memset(VX[:], 1.0), writes=['VX'])
            P.op('pool', lambda e: e.memset(MBw[:], 0.0), writes=['MBw'])
            P.op('pool', lambda e: e.memset(kmean[:], 0.0), writes=['kmean'])
            all_tiles = list(range(S // 128))
            si = 0
            qi = 0
            for h in range(4):
                P.dma('sp', KX[0:64, :], self.bkT[h * 64:(h + 1) * 64, :], reads=[('bkT', j) for j in all_tiles], writes=['KX'])
                self.dma_mid(VX[:, :, 0:64], self.bv_d[:, h * 64:(h + 1) * 64].rearrange("(kt p) d -> p kt d", p=128), NKT, 8,
                             reads=[('bv_d', j) for j in all_tiles], writes=['VX'])
                P.op('dve', lambda e: e.tensor_reduce(out=kmean[:, 0:NB], in_=KX[0:64, :].rearrange("p (n k) -> p n k", k=256),
                                                      axis=AX.X, op=ALU.add), reads=['KX'], writes=['kmean'])
                P.op('dve', lambda e: e.tensor_scalar(out=kmb[:], in0=kmean[:], scalar1=1.0 / 256, scalar2=None, op0=ALU.mult),
                     reads=['kmean'], writes=['kmb'])
                def prep(qc, Q_, qk_):
                    q0 = qc * 512
                    P.dma('sp', Q_[0:64, :], self.bqT[h * 64:(h + 1) * 64, q0:q0 + 512],
                          reads=[('bqT', 4 * qc + j) for j in range(4)], writes=[qk_])
                    yield
                    for j in range(4):
                        own = 2 * qc + j // 2
                        P.op('pe', lambda e: e.matmul(pM[:, 0:32], lhsT=Q_[0:64, j * 128:(j + 1) * 128], rhs=kmb[:],
                                                      start=True, stop=True), reads=[qk_, 'kmb'], writes=['pM'])
                        yield
                        P.op('pool', lambda e: e.memset(gsb[:], -1e30), writes=['gsb'])
                        yield
                        if own > 0:
                            P.op('dve', lambda e: e.tensor_copy(out=gsb[:, 0:own], in_=pM[:, 0:own]), reads=['pM'], writes=['gsb'])
                            yield
                        yield
                        P.op('dve', lambda e: e.max(out=m8[:], in_=gsb[:]), reads=['gsb'], writes=['m8'])
                        yield
                        P.op('dve', lambda e: e.tensor_scalar(out=sel[:], in0=gsb[:], scalar1=m8[:, 2:3], scalar2=None,
                                                              op0=ALU.is_ge), reads=['gsb', 'm8'], writes=['sel'])
                        yield
                        yield
                        P.op('dve', lambda e: e.tensor_scalar(out=MBw[:, 64:96], in0=sel[:], scalar1=-NEGM, scalar2=NEGM,
                                                              op0=ALU.mult, op1=ALU.add), reads=['sel'], writes=['MBw'])
                        yield
                        P.op('dve', lambda e: e.memset(MBw[:, 64 + own:65 + own], 0.0), writes=['MBw'])
                        yield
                        if own + 1 < 32:
                            P.op('dve', lambda e: e.memset(MBw[:, 65 + own:96], NEGM), writes=['MBw'])
                            yield
                        yield
                        P.op('pe', lambda e: e.transpose(out=pMb[:, 0:128], in_=MBw[:], identity=identb[:]),
                             reads=['MBw', 'identb'], writes=['pMb'])
                        yield
                        P.op('act', lambda e: e.copy(out=Q_[64:96, j * 128:(j + 1) * 128], in_=pMb[64:96, 0:128]),
                             reads=['pMb'], writes=[qk_])
                        yield
                        yield
                for _ in prep(0, QX[qi % 2], ('QX', qi % 2)):
                    pass
                for qc in range(NQC):
                    Q_ = QX[qi % 2]
                    qk_ = ('QX', qi % 2)
                    qi += 1
                    q0 = qc * 512
                    gp = prep(qc + 1, QX[qi % 2], ('QX', qi % 2)) if qc + 1 < NQC else None
                    pacc = 0.0
                    prate = 52.0 / (4 * qc + 4)
                    po = pO[qc % 2]
                    pok = ('pO', qc % 2)
                    P.op('pe', lambda e: e.matmul(po[:, 0:260], lhsT=zl[:], rhs=zr[:, 0:260], start=True, stop=True,
                                                  skip_group_check=True), reads=['zl', 'zr'], writes=[pok])
                    nkt = 4 * qc + 4
                    LA = MLA
                    base = si

                    def qkmm(kt):
                        pp = pS[(base + kt) % (MLA + 1)]
                        P.op('pe', lambda e: e.matmul(pp[:], lhsT=KX[:, kt * 128:(kt + 1) * 128], rhs=Q_[:], start=True, stop=True),
                             reads=['KX', 'KXi', qk_], writes=[('pS', (base + kt) % (MLA + 1))])
                    for kt in range(min(LA, nkt)):
                        qkmm(kt)
                    for kt in range(nkt):
                        if kt + LA < nkt:
                            qkmm(kt + LA)
                        p_ = pS[si % (MLA + 1)]
                        pk = ('pS', si % (MLA + 1))
                        t_ = PT[si % (MLA + 2)]
                        tk = ('PT', si % (MLA + 2))
                        si += 1
                        if LA == 0:
                            qkmm(kt)
                        P.op('act', lambda e: e.activation(out=t_[:], in_=p_[:], func=AF.Exp, bias=negB[:, 0:1], scale=0.125),
                             reads=[pk, 'negBm'], writes=[tk])
                        off = kt - 4 * qc
                        if off >= 0:
                            P.op('dve', lambda e: e.tensor_tensor(out=t_[:], in0=t_[:], in1=tm4[:, off, :], op=ALU.mult),
                                 reads=[tk, 'tm4'], writes=[tk])
                        for j in range(4):
                            if kt > 4 * qc + j:
                                continue
                            P.op('pe', lambda e: e.matmul(po[:, j * 65:(j + 1) * 65], lhsT=t_[:, j * 128:(j + 1) * 128],
                                                          rhs=VX[:, kt, :], start=False, stop=(kt == 4 * qc + j),
                                                          skip_group_check=True), reads=[tk, 'VX'], writes=[pok])
                        if gp is not None:
                            pacc += prate
                            while pacc >= 1.0 and gp is not None:
                                pacc -= 1.0
                                try:
                                    next(gp)
                                except StopIteration:
                                    gp = None
                    if gp is not None:
                        for _ in gp:
                            pass
                    y_ = yb[qc % 2]
                    yk = ('yb', qc % 2)
                    pov = po[:, 0:260].rearrange("p (j d) -> p j d", d=65)
                    P.op('dve', lambda e: e.reciprocal(out=rz[:], in_=pov[:, :, 64]), reads=[pok], writes=['rz'])
                    P.op('dve', lambda e: e.tensor_tensor(out=y_[:], in0=pov[:, :, 0:64],
                                                          in1=rz[:].unsqueeze(2).broadcast_to([128, 4, 64]), op=ALU.mult),
                         reads=[pok, 'rz'], writes=[yk])
                    P.dma('sp', self.mix_d[q0:q0 + 512, 512 + h * 64:512 + (h + 1) * 64].rearrange("(j p) d -> p j d", p=128),
                          y_[:], reads=[yk], writes=[('mixb', h, qc)])
            self.barrier()

    def phase2_nsa(self, l):
        S, P, nc = self.S, self.P, self.nc
        NT = S // 128
        Nc = S // 16 - 1
        NCT = max(1, S // 2048)
        with ExitStack() as st:
            sb, ps = self.sb, self.ps
            identb = sb(st, "identb_n", [128, 128], BF16)
            trib = sb(st, "trib", [128, 128], BF16)
            triw = sb(st, "triw", [128, 128], BF16)
            cmask = sb(st, "cmask", [128, 17, 128], BF16)
            OVL = sb(st, "OVL", [128, NCT, 128], BF16)
            cols = sb(st, "cols", [128, 4], F32)
            zl = sb(st, "zl_n", [1, 128], BF16)
            zr = sb(st, "zr_n", [1, 512], BF16)
            NG = sb(st, "NG", [128, NT, 12], F32)
            P.dma('sp', identb[:], self.c_identb, writes=['identb'])
            P.dma('sp', trib[:], self.c_trib, writes=['trib'])
            P.dma('sp', triw[:], self.c_triw, writes=['triw'])
            P.dma('sp', cmask[:], self.c_cmask, writes=['cmask'])
            P.dma('sp', OVL[:], self.c_ovl.rearrange("(ct p) n -> p ct n", p=128)[:, 0:NCT, :], writes=['OVL'])
            P.dma('sp', cols[:], self.c_cols, writes=['cols'])
            P.op('pool', lambda e: e.memset(zl[:], 0.0), writes=['zl'])
            P.op('pool', lambda e: e.memset(zr[:], 0.0), writes=['zr'])
            allt = list(range(NT))
            self.dma_mid(NG[:], self.ng_d.rearrange("(j p) g -> p j g", p=128), NT, 8, reads=[('ng_d', j) for j in allt], writes=['NG'])
            negBc = self._negB(st, "negBc", self.nsa_q_norm[l], self.nsa_k_norm[l, 0])
            negBs = self._negB(st, "negBs", self.nsa_q_norm[l], self.nsa_k_norm[l, 1])
            negBw = self._negB(st, "negBw", self.nsa_q_norm[l], self.nsa_k_norm[l, 2])
            KSX = sb(st, "KSX", [128, S], BF16)
            KWX = sb(st, "KWX", [64, S], BF16)
            VSX = sb(st, "VSX", [128, NT, 65], BF16)
            VWX = sb(st, "VWX", [128, NT, 65], BF16)
            KcT = sb(st, "KcT", [64, NCT * 128], BF16)
            VCX = sb(st, "VCX", [128, NCT, 65], BF16)
            P.dma('sp', KSX[0:64, :], self.ksT, reads=[('ksT', j) for j in allt], writes=['KSX'])
            P.dma('sp', KSX[64:128, :], self.c_ind64, writes=['KSXi'])
            P.dma('sp', KWX[:], self.kwT, reads=[('kwT', j) for j in allt], writes=['KWX'])
            for V_, src, nm in ((VSX, self.vs_d, 'vs_d'), (VWX, self.vw_d, 'vw_d')):
                P.op('pool', lambda e: e.memset(V_[:], 1.0), writes=[nm + 'X'])
                self.dma_mid(V_[:, :, 0:64], src.rearrange("(kt p) d -> p kt d", p=128), NT, 8,
                             reads=[(nm, j) for j in allt], writes=[nm + 'X'])
            P.op('pool', lambda e: e.memset(VCX[:], 1.0), writes=['VCX'])
            pS = [ps(st, "pS_n", [128, 512], F32) for _ in range(2)]
            pOc = ps(st, "pOc", [128, 512], F32)
            pU = ps(st, "pU", [128, 512], F32)
            pOs = ps(st, "pOs", [128, 512], F32)
            pOw = ps(st, "pOw", [128, 512], F32)
            pMb = ps(st, "pMb_n", [128, 1024], BF16)
            pM = ps(st, "pM_n", [128, 512], F32)
            with ExitStack() as s2:
                KCV = sb(s2, "KCV", [128, S], BF16)
                W1s = sb(s2, "W1s", [128, 32, 128], F32)
                W1 = sb(s2, "W1", [128, 32, 128], BF16)
                pes = sb(s2, "pes", [32, 128], F32)
                peb = sb(s2, "peb", [32, 128], BF16)
                peT = sb(s2, "peT", [128, 32], BF16)
                w2s = sb(s2, "w2s", [128, 2, 64], F32)
                w2 = sb(s2, "w2", [128, 2, 64], BF16)
                gk0 = sb(s2, "gk0", [128, 64], F32)
                bias = sb(s2, "bias_c", [128, 2], F32)
                hidb = sb(s2, "hidb", [128, NCT * 128], BF16)
                kc32 = sb(s2, "kc32", [128, 64], F32)
                kcn = sb(s2, "kcn", [128, 64], BF16)
                junk = sb(s2, "junk_c", [128, 64], F32)
                ssc = sb(s2, "ssc", [128, 2], F32)
                P.dma('sp', KCV[:], self.kcvcT, reads=[('kcvcT', b) for b in range(S // 512)], writes=['KCV'])
                for br in range(2):
                    P.dma('sp', W1s[64 * br:64 * br + 64], self.cmp_w1[l, br].rearrange("(r d) j -> d r j", d=64), writes=['W1s'])
                    P.dma('sp', w2s[:, br, :], self.cmp_w2[l, br], writes=['w2s'])
                    P.dma('sp', pes[:, 64 * br:64 * br + 64], self.cmp_pe[l, br], writes=['pes'])
                P.dma('sp', gk0[:], self.nsa_k_norm[l, 0].partition_broadcast(128), writes=['gk0'])
                P.op('dve', lambda e: e.tensor_copy(out=W1[:], in_=W1s[:]), reads=['W1s'], writes=['W1'])
                P.op('dve', lambda e: e.tensor_copy(out=peb[:], in_=pes[:]), reads=['pes'], writes=['peb'])
                P.op('pe', lambda e: e.transpose(out=pMb[:, 0:32], in_=peb[:], identity=identb[0:32, 0:32]),
                     reads=['peb', 'identb'], writes=['pMb'])
                P.op('dve', lambda e: e.tensor_copy(out=peT[:], in_=pMb[:, 0:32]), reads=['pMb'], writes=['peT'])
                P.op('dve', lambda e: e.tensor_copy(out=w2[:], in_=w2s[:]), reads=['w2s'], writes=['w2'])
                P.op('pool', lambda e: e.memset(hidb[:], 0.0), writes=['hidb'])
                for br in range(2):
                    rows = slice(64 * br, 64 * br + 64)
                    kview = KCV[rows, :].rearrange("p (c s) -> p c s", s=16)
                    for r in range(32):
                        P.op('pe', lambda e: e.matmul(pM[:, 0:1], lhsT=W1[rows, r, :], rhs=peT[rows, r:r + 1],
                                                      start=(r == 0), stop=(r == 31)), reads=['W1', 'peT'], writes=['pM'])
                    P.op('dve', lambda e: e.tensor_copy(out=bias[:, br:br + 1], in_=pM[:, 0:1]), reads=['pM'], writes=['bias'])
                    for r in range(32):
                        rhs = kview[:, 0:Nc, r] if r < 16 else kview[:, 1:Nc + 1, r - 16]
                        P.op('pe', lambda e: e.matmul(pS[0][:, 0:Nc], lhsT=W1[rows, r, :], rhs=rhs,
                                                      start=(r == 0), stop=(r == 31)), reads=['W1', 'KCV'], writes=[('pS', 0)])
                    P.op('act', lambda e: e.activation(out=hidb[:, 0:Nc], in_=pS[0][:, 0:Nc], func=AF.Silu, bias=bias[:, br:br + 1]),
                         reads=[('pS', 0), 'bias'], writes=['hidb'])
                    for ct in range(NCT):
                        P.op('pe', lambda e: e.matmul(pM[:, 0:64], lhsT=hidb[:, ct * 128:(ct + 1) * 128], rhs=w2[:, br, :],
                                                      start=True, stop=True), reads=['hidb', 'w2'], writes=['pM'])
                        if br == 0:
                            P.op('act', lambda e: e.activation(out=junk[:], in_=pM[:, 0:64], func=AF.Square, scale=0.125,
                                                               accum_out=ssc[:, 0:1]), reads=['pM'], writes=['junk_c', 'ssc'])
                            P.op('dve', lambda e: e.tensor_scalar(out=ssc[:, 0:1], in0=ssc[:, 0:1], scalar1=EPS, scalar2=None,
                                                                  op0=ALU.add), reads=['ssc'], writes=['ssc'])
                            P.op('act', lambda e: e.activation(out=ssc[:, 0:1], in_=ssc[:, 0:1], func=AF.Sqrt), reads=['ssc'], writes=['ssc'])
                            P.op('dve', lambda e: e.reciprocal(out=ssc[:, 1:2], in_=ssc[:, 0:1]), reads=['ssc'], writes=['ssc'])
                            P.op('dve', lambda e: e.scalar_tensor_tensor(out=kcn[:], in0=pM[:, 0:64], scalar=ssc[:, 1:2], in1=gk0[:],
                                                                         op0=ALU.mult, op1=ALU.mult),
                                 reads=['pM', 'ssc', 'gk0'], writes=['kcn'])
                            P.op('pe', lambda e: e.transpose(out=pMb[0:64, 0:128], in_=kcn[:], identity=identb[:]),
                                 reads=['kcn', 'identb'], writes=['pMb'])
                            P.op('act', lambda e: e.copy(out=KcT[:, ct * 128:(ct + 1) * 128], in_=pMb[0:64, 0:128]),
                                 reads=['pMb'], writes=['KcT'])
                        else:
                            P.op('act', lambda e: e.copy(out=VCX[:, ct, 0:64], in_=pM[:, 0:64]), reads=['pM'], writes=['VCX'])
                self.barrier()
            QU = [sb(st, "QU", [64, 512], BF16) for _ in range(2)]
            QR0 = [sb(st, "QR0", [128, 512], BF16) for _ in range(2)]
            QR1 = [sb(st, "QR1", [128, 512], BF16) for _ in range(2)]
            PTc = [sb(st, "PTc", [128, 512], BF16) for _ in range(NCT)]
            PT = [sb(st, "PTn", [128, 512], BF16) for _ in range(3)]
            zz = sb(st, "zz", [128, 3, 4], F32)
            rzz = sb(st, "rzz", [128, 3, 4], F32)
            coef = sb(st, "coef", [128, 3, 4], F32)
            imp = sb(st, "imp", [128, 128], F32)
            work = sb(st, "work", [128, 128], F32)
            m8a = sb(st, "m8a", [128, 8], F32)
            m8b = sb(st, "m8b", [128, 8], F32)
            selm = sb(st, "selm", [128, 128], F32)
            MB = sb(st, "MB", [128, 128], BF16)
            MBs = sb(st, "MBs", [128, 128], BF16)
            yacc = sb(st, "yacc", [128, 4, 64], F32)
            yn = [sb(st, "yn", [128, 256], BF16) for _ in range(2)]
            si = 0

            def seed(t, n, key):
                P.op('pe', lambda e: e.matmul(t[:, 0:n], lhsT=zl[:], rhs=zr[:, 0:n], start=True, stop=True, skip_group_check=True),
                     reads=['zl', 'zr'], writes=[key])

            def hb(ap):
                return ap.unsqueeze(1).broadcast_to([ap.shape[0], 4, 128])

            for m in range(NT):
                t0 = m * 128
                b2 = m % 2
                qu, qr0, qr1 = QU[b2], QR0[b2], QR1[b2]
                P.dma('sp', qu[:].rearrange("p (h t) -> p h t", h=4), self.nquT[:, t0:t0 + 128].rearrange("(h d) t -> d h t", d=64),
                      reads=[('nquT', m)], writes=[('QU', b2)])
                P.dma('sp', qr0[0:64, :].rearrange("p (h t) -> p h t", h=4), self.nqrT[:, t0:t0 + 128].rearrange("(h d) t -> d h t", d=64),
                      reads=[('nqrT', m)], writes=[('QR0', b2)])
                use_g1 = (2 * m + 1) >= 64
                if use_g1:
                    P.dma('sp', qr1[0:64, :].rearrange("p (h t) -> p h t", h=4),
                          self.nqrT[:, t0:t0 + 128].rearrange("(h d) t -> d h t", d=64), reads=[('nqrT', m)], writes=[('QR1', b2)])
                ctn = min(NCT, (8 * m + 6) // 128 + 1)
                for ct in range(ctn):
                    p_ = pS[si % 2]
                    pk = ('pS', si % 2)
                    si += 1
                    P.op('pe', lambda e: e.matmul(p_[:], lhsT=KcT[:, ct * 128:(ct + 1) * 128], rhs=qu[:], start=True, stop=True),
                         reads=['KcT', ('QU', b2)], writes=[pk])
                    P.op('act', lambda e: e.activation(out=PTc[ct][:], in_=p_[:], func=AF.Exp, bias=negBc[:, 0:1], scale=0.125),
                         reads=[pk, 'negBc'], writes=[('PTc', ct)])
                    r = m - 16 * ct
                    if r <= 16:
                        P.op('pool', lambda e: e.tensor_tensor(out=PTc[ct][:].rearrange("p (h t) -> p h t", h=4),
                                                               in0=PTc[ct][:].rearrange("p (h t) -> p h t", h=4),
                                                               in1=hb(cmask[:, r, :]), op=ALU.mult),
                             reads=[('PTc', ct), 'cmask'], writes=[('PTc', ct)])
                seed(pOc, 260, 'pOc')
                seed(pU, 512, 'pU')
                for h in range(4):
                    for ct in range(ctn):
                        P.op('pe', lambda e: e.matmul(pOc[:, h * 65:(h + 1) * 65], lhsT=PTc[ct][:, h * 128:(h + 1) * 128],
                                                      rhs=VCX[:, ct, :], start=False, stop=(ct == ctn - 1), skip_group_check=True),
                             reads=[('PTc', ct), 'VCX'], writes=['pOc'])
                        P.op('pe', lambda e: e.matmul(pU[:, h * 128:(h + 1) * 128], lhsT=PTc[ct][:, h * 128:(h + 1) * 128],
                                                      rhs=OVL[:, ct, :], start=False, stop=(ct == ctn - 1), skip_group_check=True),
                             reads=[('PTc', ct), 'OVL'], writes=['pU'])
                pocv = pOc[:, 0:260].rearrange("p (h d) -> p h d", d=65)
                P.op('dve', lambda e: e.tensor_scalar(out=zz[:, 0, :], in0=pocv[:, :, 64], scalar1=1e-30, scalar2=None, op0=ALU.max),
                     reads=['pOc'], writes=['zz0'])
                P.op('dve', lambda e: e.reciprocal(out=rzz[:, 0, :], in_=zz[:, 0, :]), reads=['zz0'], writes=['rzz0'])
                P.op('dve', lambda e: e.tensor_scalar(out=imp[:], in0=pU[:, 0:128], scalar1=rzz[:, 0, 0:1], scalar2=None, op0=ALU.mult),
                     reads=['pU', 'rzz0'], writes=['imp'])
                for h in range(1, 4):
                    P.op('dve', lambda e: e.scalar_tensor_tensor(out=imp[:], in0=pU[:, h * 128:(h + 1) * 128], scalar=rzz[:, 0, h:h + 1],
                                                                 in1=imp[:], op0=ALU.mult, op1=ALU.add),
                         reads=['pU', 'rzz0', 'imp'], writes=['imp'])
                n1 = 2 * m + 1
                if n1 + 1 < 128:
                    P.op('pool', lambda e: e.memset(imp[:, n1 + 1:128], -1e30), reads=['imp'], writes=['imp'])
                P.op('pool', lambda e: e.tensor_copy(out=imp[:, n1:n1 + 1], in_=cols[:, 0:1]), reads=['cols', 'imp'], writes=['imp'])
                P.op('pool', lambda e: e.memset(imp[:, n1 - 1:n1], 1e9), reads=['imp'], writes=['imp'])
                if n1 - 2 >= 0:
                    P.op('dve', lambda e: e.tensor_tensor(out=imp[:, n1 - 2:n1 - 1], in0=imp[:, n1 - 2:n1 - 1], in1=cols[:, 1:2], op=ALU.max),
                         reads=['cols', 'imp'], writes=['imp'])
                P.op('pool', lambda e: e.memset(imp[:, 0:1], 1e9), reads=['imp'], writes=['imp'])
                P.op('dve', lambda e: e.max(out=m8a[:], in_=imp[:]), reads=['imp'], writes=['m8a'])
                P.op('dve', lambda e: e.match_replace(out=work[:], in_to_replace=m8a[:], in_values=imp[:], imm_value=-1e30),
                     reads=['imp', 'm8a'], writes=['work'])
                P.op('dve', lambda e: e.max(out=m8b[:], in_=work[:]), reads=['work'], writes=['m8b'])
                P.op('dve', lambda e: e.tensor_scalar(out=selm[:], in0=imp[:], scalar1=m8b[:, 7:8], scalar2=None, op0=ALU.is_ge),
                     reads=['imp', 'm8b'], writes=['selm'])
                P.op('dve', lambda e: e.tensor_scalar(out=MB[:], in0=selm[:], scalar1=-NEGM, scalar2=NEGM, op0=ALU.mult, op1=ALU.add),
                     reads=['selm'], writes=['MB'])
                if n1 + 1 < 128:
                    P.op('pool', lambda e: e.memset(MB[:, n1 + 1:128], NEGM), reads=['MB'], writes=['MB'])
                P.op('pool', lambda e: e.tensor_copy(out=MB[:, n1:n1 + 1], in_=cols[:, 2:3]), reads=['cols', 'MB'], writes=['MB'])
                P.op('pool', lambda e: e.tensor_copy(out=MBs[:, 0:64], in_=MB[:, 64:128]), reads=['MB'], writes=['MBs'])
                P.op('pool', lambda e: e.tensor_copy(out=MBs[:, 64:128], in_=MB[:, 0:64]), reads=['MB'], writes=['MBs'])
                P.op('pe', lambda e: e.transpose(out=pMb[:, 0:128], in_=MBs[:], identity=identb[:]), reads=['MBs', 'identb'], writes=['pMb'])
                P.op('act', lambda e: e.copy(out=qr0[64:128, :].rearrange("p (h t) -> p h t", h=4), in_=hb(pMb[64:128, 0:128])),
                     reads=['pMb'], writes=[('QR0', b2)])
                if use_g1:
                    P.op('pe', lambda e: e.transpose(out=pMb[:, 128:256], in_=MB[:], identity=identb[:]), reads=['MB', 'identb'], writes=['pMb'])
                    P.op('act', lambda e: e.copy(out=qr1[64:128, :].rearrange("p (h t) -> p h t", h=4), in_=hb(pMb[64:128, 128:256])),
                         reads=['pMb'], writes=[('QR1', b2)])
                seed(pOs, 260, 'pOs')
                for kt in range(m + 1):
                    g = kt // 32
                    qr_, qrk = (qr0, ('QR0', b2)) if g == 0 else (qr1, ('QR1', b2))
                    p_ = pS[si % 2]
                    pk = ('pS', si % 2)
                    t_ = PT[si % 3]
                    tk = ('PTn', si % 3)
                    si += 1
                    P.op('pe', lambda e: e.matmul(p_[:], lhsT=KSX[:, kt * 128:(kt + 1) * 128], rhs=qr_[:], start=True, stop=True),
                         reads=['KSX', 'KSXi', qrk], writes=[pk])
                    P.op('act', lambda e: e.activation(out=t_[:], in_=p_[:], func=AF.Exp, bias=negBs[:, 0:1], scale=0.125),
                         reads=[pk, 'negBs'], writes=[tk])
                    if kt == m:
                        P.op('pool', lambda e: e.tensor_tensor(out=t_[:].rearrange("p (h t) -> p h t", h=4),
                                                               in0=t_[:].rearrange("p (h t) -> p h t", h=4), in1=hb(trib[:]), op=ALU.mult),
                             reads=[tk, 'trib'], writes=[tk])
                    for h in range(4):
                        P.op('pe', lambda e: e.matmul(pOs[:, h * 65:(h + 1) * 65], lhsT=t_[:, h * 128:(h + 1) * 128], rhs=VSX[:, kt, :],
                                                      start=False, stop=(kt == m), skip_group_check=True),
                             reads=[tk, 'vs_dX'], writes=['pOs'])
                seed(pOw, 260, 'pOw')
                for kt in range(max(0, m - 4), m + 1):
                    p_ = pS[si % 2]
                    pk = ('pS', si % 2)
                    t_ = PT[si % 3]
                    tk = ('PTn', si % 3)
                    si += 1
                    P.op('pe', lambda e: e.matmul(p_[:], lhsT=KWX[:, kt * 128:(kt + 1) * 128], rhs=qr0[0:64, :], start=True, stop=True),
                         reads=['KWX', ('QR0', b2)], writes=[pk])
                    P.op('act', lambda e: e.activation(out=t_[:], in_=p_[:], func=AF.Exp, bias=negBw[:, 0:1], scale=0.125),
                         reads=[pk, 'negBw'], writes=[tk])
                    if kt == m or kt == m - 4:
                        mk = trib if kt == m else triw
                        P.op('pool', lambda e: e.tensor_tensor(out=t_[:].rearrange("p (h t) -> p h t", h=4),
                                                               in0=t_[:].rearrange("p (h t) -> p h t", h=4), in1=hb(mk[:]), op=ALU.mult),
                             reads=[tk, 'trib', 'triw'], writes=[tk])
                    for h in range(4):
                        P.op('pe', lambda e: e.matmul(pOw[:, h * 65:(h + 1) * 65], lhsT=t_[:, h * 128:(h + 1) * 128], rhs=VWX[:, kt, :],
                                                      start=False, stop=(kt == m), skip_group_check=True),
                             reads=[tk, 'vw_dX'], writes=['pOw'])
                posv = pOs[:, 0:260].rearrange("p (h d) -> p h d", d=65)
                powv = pOw[:, 0:260].rearrange("p (h d) -> p h d", d=65)
                P.op('dve', lambda e: e.tensor_copy(out=zz[:, 1, :], in_=posv[:, :, 64]), reads=['pOs'], writes=['zz1'])
                P.op('dve', lambda e: e.tensor_copy(out=zz[:, 2, :], in_=powv[:, :, 64]), reads=['pOw'], writes=['zz1'])
                P.op('dve', lambda e: e.reciprocal(out=rzz[:, 1:3, :], in_=zz[:, 1:3, :]), reads=['zz1'], writes=['rzz1'])
                P.op('dve', lambda e: e.tensor_tensor(out=coef[:], in0=rzz[:], in1=NG[:, m, :].rearrange("p (b h) -> p b h", b=3), op=ALU.mult),
                     reads=['rzz0', 'rzz1', 'NG'], writes=['coef'])
                for h in range(4):
                    P.op('dve', lambda e: e.tensor_scalar(out=yacc[:, h, :], in0=pocv[:, h, 0:64], scalar1=coef[:, 0, h:h + 1], scalar2=None,
                                                          op0=ALU.mult), reads=['pOc', 'coef'], writes=[('yacc', h)])
                    P.op('dve', lambda e: e.scalar_tensor_tensor(out=yacc[:, h, :], in0=posv[:, h, 0:64], scalar=coef[:, 1, h:h + 1],
                                                                 in1=yacc[:, h, :], op0=ALU.mult, op1=ALU.add),
                         reads=['pOs', 'coef', ('yacc', h)], writes=[('yacc', h)])
                    P.op('dve', lambda e: e.scalar_tensor_tensor(out=yn[b2][:, h * 64:(h + 1) * 64], in0=powv[:, h, 0:64],
                                                                 scalar=coef[:, 2, h:h + 1], in1=yacc[:, h, :], op0=ALU.mult, op1=ALU.add),
                         reads=['pOw', 'coef', ('yacc', h)], writes=[('yn', b2)])
                P.dma('sp', self.mix_d[t0:t0 + 128, 768:1024], yn[b2][:], reads=[('yn', b2)], writes=[('mixn', m)])
            self.barrier()

    def build_all(self):
        self.declare()
        for l in range(self.depth):
            xsrc = self.x_in if l == 0 else self.xbuf
            xdst = self.y_out if l == self.depth - 1 else self.xbuf
            self.phase1(l, xsrc)
            if os.environ.get("PH2", "old") == "new":
                self.phase2(l)
            elif os.environ.get("PH2", "old") == "b":
                self.phase2_moba(l)
                self.phase2b(l)
            else:
                self.phase2_mlstm(l)
                self.phase2_moba(l)
                self.phase2_nsa2(l)
            self.phase3(l, xsrc, xdst)
        self.P.finish()
        return self.nc


def make_consts(S):
    bf = ml_dtypes.bfloat16
    half = 8
    inv = np.exp(-math.log(500000.0) * np.arange(half, dtype=np.float32) * (2.0 / 16)).astype(np.float32)
    ang = np.arange(S, dtype=np.float32)[:, None] * inv[None, :]
    d = dict(c_identb=np.eye(128, dtype=np.float32).astype(bf), c_identf=np.eye(128, dtype=np.float32),
             c_cos=np.cos(ang).astype(np.float32), c_sin=np.sin(ang).astype(np.float32))
    i = np.arange(128)
    d['c_ut'] = (i[:, None] < i[None, :]).astype(np.float32)
    d['c_tri'] = (i[:, None] <= i[None, :]).astype(np.float32)
    key = np.arange(S)
    d['c_ind32'] = (key[None, :] // 256 == np.arange(32)[:, None]).astype(np.float32).astype(bf)
    d['c_ind64'] = (((key[None, :] // 64) % 64) == np.arange(64)[:, None]).astype(np.float32).astype(bf)
    k = np.arange(128)[:, None, None]
    o = np.arange(4)[None, :, None]
    q = np.arange(512)[None, None, :]
    d['c_tm4'] = (q >= k + 128 * o).astype(np.float32).astype(bf)
    d['c_trib'] = (i[:, None] <= i[None, :]).astype(np.float32).astype(bf)
    d['c_triw'] = (i[:, None] > i[None, :]).astype(np.float32).astype(bf)
    ii = np.arange(128)[:, None, None]
    r = np.arange(17)[None, :, None]
    j = np.arange(128)[None, None, :]
    d['c_cmask'] = (16 * ii + 31 <= 128 * r + j).astype(np.float32).astype(bf)
    c = np.arange(512)[:, None]
    n = np.arange(128)[None, :]
    Nc = S // 16 - 1
    d['c_ovl'] = ((c >= 4 * n - 1) & (c <= 4 * n + 3) & (c < Nc)).astype(np.float32).astype(bf)
    cols = np.zeros((128, 4), np.float32)
    cols[:64, 0] = -1e30
    cols[64:, 0] = 1e9
    cols[:64, 1] = 1e9
    cols[64:, 1] = -1e30
    cols[:64, 2] = NEGM
    cols[64:, 2] = 0.0
    d['c_cols'] = cols
    return d


_CACHE = {}


def kernel(x, w_in, b_if, conv_qk, m_norm, moba_qk_norm, nsa_q_norm, nsa_k_norm,
           cmp_pe, cmp_w1, cmp_w2, w_out, norm_mix, norm_ffn, w_ff1, w_ff2):
    x = np.asarray(x, dtype=np.float32)
    Bsz, S, _ = x.shape
    depth = int(np.asarray(w_in).shape[0])
    n_cores = 8
    shared = dict(w_in=w_in, b_if=b_if, conv_qk=conv_qk, m_norm=m_norm, moba_qk_norm=moba_qk_norm,
                  nsa_q_norm=nsa_q_norm, nsa_k_norm=nsa_k_norm, cmp_pe=cmp_pe, cmp_w1=cmp_w1, cmp_w2=cmp_w2,
                  w_out=w_out, norm_mix=norm_mix, norm_ffn=norm_ffn, w_ff1=w_ff1, w_ff2=w_ff2)
    shared = {k: np.ascontiguousarray(np.asarray(v, dtype=np.float32)) for k, v in shared.items()}
    shared.update(make_consts(S))
    B = Builder(S, depth)
    nc = B.build_all()
    in_maps = []
    for c in range(n_cores):
        m = dict(shared)
        m['x'] = np.ascontiguousarray(x[c % Bsz])
        in_maps.append(m)
    res = run_bass_kernel_spmd(nc, in_maps, core_ids=list(range(n_cores)))
    out = np.stack([np.asarray(res.results[b]["y"], dtype=np.float32) for b in range(Bsz)], axis=0)
    return out


class BG:
    def __init__(self):
        self.gens = []
        self.acc = 0.0

    def add(self, g):
        self.gens.append(g)

    def step(self, rate=1.0):
        self.acc += rate
        while self.acc >= 1.0:
            self.acc -= 1.0
            for g in list(self.gens):
                try:
                    next(g)
                except StopIteration:
                    self.gens.remove(g)

    def drain(self, g=None):
        if g is None:
            while self.gens:
                self.step(1.0)
        else:
            while g in self.gens:
                try:
                    next(g)
                except StopIteration:
                    self.gens.remove(g)


def _phase2(self, l):
    S, P, nc = self.S, self.P, self.nc
    NCH = S // 64
    NT = S // 128
    NQC = S // 512
    NB = S // 256
    Nc = S // 16 - 1
    NCT = max(1, S // 2048)
    LNSC = math.log(128.0 ** -0.5)
    sb, ps = self.sb, self.ps
    allt = list(range(NT))
    with ExitStack() as st:
        identb = sb(st, "identb2", [128, 128], BF16)
        identf = sb(st, "identf2", [128, 128], F32)
        ut = sb(st, "ut2", [128, 128], F32)
        tri = sb(st, "tri2", [128, 128], F32)
        zl = sb(st, "zl2", [1, 128], BF16)
        zr = sb(st, "zr2", [1, 512], BF16)
        P.dma('sp', identb[:], self.c_identb, writes=['identb'])
        P.dma('sp', identf[:], self.c_identf, writes=['identf'])
        P.dma('sp', ut[:], self.c_ut, writes=['ut'])
        P.dma('sp', tri[:], self.c_tri, writes=['tri'])
        P.op('pool', lambda e: e.memset(zl[:], 0.0), writes=['zl'])
        P.op('pool', lambda e: e.memset(zr[:], 0.0), writes=['zr'])
        uT = sb(st, "uT_m", [64, 4, NCH], F32)
        u2T = sb(st, "u2T_m", [64, 4, NCH], F32)
        flT = sb(st, "flT_m", [64, 4, NCH], F32)
        decB = sb(st, "decB", [128, 4, NCH], F32)
        with ExitStack() as s2:
            li = sb(s2, "li", [NCH, 4, 64], F32)
            lf = sb(s2, "lf", [NCH, 4, 64], F32)
            ones = sb(s2, "ones", [NCH, 64], F32)
            Fin = sb(s2, "Fin", [NCH, 4, 64], F32)
            Ft = sb(s2, "Ft", [NCH, 4, 64], F32)
            a_ = sb(s2, "a_", [NCH, 4, 64], F32)
            Ain = sb(s2, "Ain", [NCH, 4, 64], F32)
            tot = sb(s2, "tot", [NCH, 4], F32)
            cmax = sb(s2, "cmax", [NCH, 4], F32)
            cmT = sb(s2, "cmT", [4, NCH], F32)
            ET = sb(s2, "ET", [4, NCH], F32)
            ETn = sb(s2, "ETn", [4, NCH], F32)
            Ec = sb(s2, "Ec", [NCH, 4], F32)
            Enc = sb(s2, "Enc", [NCH, 4], F32)
            tmp = sb(s2, "tmpm", [NCH, 4, 64], F32)
            uu = sb(s2, "uu", [NCH, 4, 64], F32)
            uu2 = sb(s2, "uu2", [NCH, 4, 64], F32)
            fl = sb(s2, "fl", [NCH, 4, 64], F32)
            dec = sb(s2, "dec", [NCH, 4], F32)
            decrep = sb(s2, "decrep", [NCH, 4, 128], F32)
            pa = ps(s2, "pa", [128, 512], F32)
            pb = ps(s2, "pb", [128, 512], F32)
            P.dma('sp', li[:], self.gi_d.rearrange("h (c j) -> c h j", j=64),
                  reads=[('gi_d', b) for b in range(S // 512)], writes=['li'])
            P.dma('sp', lf[:], self.gf_d.rearrange("h (c j) -> c h j", j=64),
                  reads=[('gf_d', b) for b in range(S // 512)], writes=['lf'])
            P.op('pool', lambda e: e.memset(ones[:], 1.0), writes=['ones'])
            for hh in range(4):
                P.op('dve', lambda e: e.tensor_tensor_scan(out=Fin[:, hh, :], data0=ones[:], data1=lf[:, hh, :],
                                                           initial=0.0, op0=ALU.mult, op1=ALU.add),
                     reads=['ones', 'lf'], writes=['Fin'])
            P.op('dve', lambda e: e.tensor_copy(out=tot[:], in_=Fin[:, :, 63]), reads=['Fin'], writes=['tot'])
            P.op('pe', lambda e: e.matmul(pa[0:NCH, 0:4], lhsT=ut[0:NCH, 0:NCH], rhs=tot[:], start=True, stop=True),
                 reads=['ut', 'tot'], writes=['pa'])
            P.op('dve', lambda e: e.tensor_tensor(out=Ft[:], in0=Fin[:],
                                                  in1=pa[0:NCH, 0:4].unsqueeze(2).broadcast_to([NCH, 4, 64]), op=ALU.add),
                 reads=['Fin', 'pa'], writes=['Ft'])
            P.op('dve', lambda e: e.tensor_tensor(out=a_[:], in0=li[:], in1=Ft[:], op=ALU.subtract),
                 reads=['li', 'Ft'], writes=['a_'])
            for hh in range(4):
                P.op('dve', lambda e: e.tensor_tensor_scan(out=Ain[:, hh, :], data0=a_[:, hh, :], data1=a_[:, hh, :],
                                                           initial=-1e30, op0=ALU.max, op1=ALU.max),
                     reads=['a_'], writes=['Ain'])
            P.op('dve', lambda e: e.tensor_copy(out=cmax[:], in_=Ain[:, :, 63]), reads=['Ain'], writes=['cmax'])
            P.op('pe', lambda e: e.transpose(out=pb[0:4, 0:NCH], in_=cmax[:], identity=identf[0:NCH, 0:NCH]),
                 reads=['cmax', 'identf'], writes=['pb'])
            P.op('dve', lambda e: e.tensor_copy(out=cmT[:], in_=pb[0:4, 0:NCH]), reads=['pb'], writes=['cmT'])
            P.op('dve', lambda e: e.tensor_tensor_scan(out=ET[:], data0=cmT[:], data1=cmT[:], initial=0.0,
                                                       op0=ALU.max, op1=ALU.max), reads=['cmT'], writes=['ET'])
            if NCH > 1:
                P.op('dve', lambda e: e.tensor_copy(out=ETn[:, 0:NCH - 1], in_=ET[:, 1:NCH]), reads=['ET'], writes=['ETn'])
            P.op('dve', lambda e: e.tensor_copy(out=ETn[:, NCH - 1:NCH], in_=ET[:, NCH - 1:NCH]), reads=['ET'], writes=['ETn'])
            P.op('pe', lambda e: e.transpose(out=pa[0:NCH, 0:4], in_=ET[:], identity=identf[0:4, 0:4]),
                 reads=['ET', 'identf'], writes=['pa'])
            P.op('dve', lambda e: e.tensor_copy(out=Ec[:], in_=pa[0:NCH, 0:4]), reads=['pa'], writes=['Ec'])
            P.op('pe', lambda e: e.transpose(out=pb[0:NCH, 0:4], in_=ETn[:], identity=identf[0:4, 0:4]),
                 reads=['ETn', 'identf'], writes=['pb'])
            P.op('dve', lambda e: e.tensor_copy(out=Enc[:], in_=pb[0:NCH, 0:4]), reads=['pb'], writes=['Enc'])
            Eb = Ec[:].unsqueeze(2).broadcast_to([NCH, 4, 64])
            Enb = Enc[:].unsqueeze(2).broadcast_to([NCH, 4, 64])
            P.op('dve', lambda e: e.tensor_tensor(out=tmp[:], in0=a_[:], in1=Eb, op=ALU.subtract),
                 reads=['a_', 'Ec'], writes=['tmp'])
            P.op('act', lambda e: e.activation(out=uu[:], in_=tmp[:], func=AF.Exp, bias=LNSC), reads=['tmp'], writes=['uu'])
            P.op('dve', lambda e: e.tensor_tensor(out=tmp[:], in0=a_[:], in1=Enb, op=ALU.subtract),
                 reads=['a_', 'Enc'], writes=['tmp'])
            P.op('act', lambda e: e.activation(out=uu2[:], in_=tmp[:], func=AF.Exp, bias=LNSC), reads=['tmp'], writes=['uu2'])
            P.op('dve', lambda e: e.tensor_tensor(out=tmp[:], in0=Ft[:], in1=Eb, op=ALU.add),
                 reads=['Ft', 'Ec'], writes=['tmp'])
            P.op('act', lambda e: e.activation(out=fl[:], in_=tmp[:], func=AF.Exp, scale=-1.0), reads=['tmp'], writes=['fl'])
            P.op('dve', lambda e: e.tensor_tensor(out=dec[:], in0=Ec[:], in1=Enc[:], op=ALU.subtract),
                 reads=['Ec', 'Enc'], writes=['dec'])
            P.op('act', lambda e: e.activation(out=dec[:], in_=dec[:], func=AF.Exp), reads=['dec'], writes=['dec'])
            P.op('dve', lambda e: e.tensor_copy(out=decrep[:], in_=dec[:].unsqueeze(2).broadcast_to([NCH, 4, 128])),
                 reads=['dec'], writes=['decrep'])
            for src, dst, nm in ((uu, uT, 'uT'), (uu2, u2T, 'u2T'), (fl, flT, 'flT')):
                for hh in range(4):
                    P.op('pe', lambda e: e.transpose(out=pa[0:64, hh * 128:hh * 128 + NCH], in_=src[:, hh, :],
                                                     identity=identf[0:NCH, 0:NCH]),
                         reads=['uu', 'uu2', 'fl', 'identf'], writes=['pa'])
                P.op('dve', lambda e: e.tensor_copy(out=dst[:], in_=pa[0:64, :].rearrange("p (h c) -> p h c", h=4)[:, :, 0:NCH]),
                     reads=['pa'], writes=[nm])
            for hh in range(4):
                P.op('pe', lambda e: e.matmul(pb[:, hh * 128:hh * 128 + NCH], lhsT=decrep[:, hh, :],
                                              rhs=identf[0:NCH, 0:NCH], start=True, stop=True),
                     reads=['decrep', 'identf'], writes=['pb'])
            P.op('dve', lambda e: e.tensor_copy(out=decB[:], in_=pb[:].rearrange("p (h c) -> p h c", h=4)[:, :, 0:NCH]),
                 reads=['pb'], writes=['decB'])
            self.barrier()

        pSr = [ps(st, "pSr", [128, 512], F32) for _ in range(3)]
        pMi = ps(st, "pMi", [128, 512], F32)
        pMb = pMi[:, 128:192].bitcast(BF16)
        pMb2 = pMi[:, 192:256].bitcast(BF16)
        pM = pMi[:, 256:288]
        bg = BG()

        sm = ExitStack()
        st = sm
        GC = 4
        NG = S // (64 * GC)
        TG = 64 * GC
        qg = [sb(st, "qg", [128, 4, TG], BF16) for _ in range(2)]
        kg = [sb(st, "kg", [128, 4, TG], BF16) for _ in range(2)]
        vg = [sb(st, "vg", [64, GC, 4, 129], BF16) for _ in range(2)]
        ogg = [sb(st, "ogg", [64, GC, 512], BF16) for _ in range(2)]
        ym = [sb(st, "ym", [64, GC, 512], BF16) for _ in range(2)]
        G = [sb(st, "G", [128, 129], F32) for _ in range(4)]
        Gb = [sb(st, "Gb", [128, 129], BF16) for _ in range(4)]
        ku2 = [sb(st, "ku2", [64, 128], BF16) for _ in range(2)]
        Sm = [sb(st, "Sm", [64, 64], BF16) for _ in range(2)]
        junkm = sb(st, "junkm", [64, 128], BF16)
        scm = [sb(st, "scm", [64, 8], F32) for _ in range(2)]
        mlA = [ps(st, "mlA", [128, 512], F32) for _ in range(2)]

        def gen_ml():
            for i in range(2):
                P.op('pool', lambda e: e.memset(vg[i][:], 1.0), writes=[('vg', i)])
            for hh in range(4):
                P.op('pool', lambda e: e.memset(G[hh][:], 0.0), writes=[('G', hh)])
                P.op('pool', lambda e: e.memset(Gb[hh][:], 0.0), writes=[('Gb', hh)])
            yield

            def load_group(g):
                i = g % 2
                tk = slice(g * TG, (g + 1) * TG)
                bl = sorted(set([(g * TG) // 512, ((g + 1) * TG - 1) // 512]))
                jl = list(range((g * TG) // 128, ((g + 1) * TG + 127) // 128))
                P.dma('sp', qg[i][:], self.qkT[0:512, tk].rearrange("(h p) t -> p h t", p=128),
                      reads=[('qkT', ft, b) for ft in range(4) for b in bl], writes=[('qg', i)])
                P.dma('sp', kg[i][:], self.qkT[512:1024, tk].rearrange("(h p) t -> p h t", p=128),
                      reads=[('qkT', ft, b) for ft in range(4, 8) for b in bl], writes=[('kg', i)])
                for hh in range(4):
                    P.dma('sp', vg[i][:, :, hh, 0:128],
                          self.mv_d[tk, hh * 128:(hh + 1) * 128].rearrange("(c s) e -> s c e", s=64),
                          reads=[('mv_d', j) for j in jl], writes=[('vg', i)])
                P.dma('sp', ogg[i][:], self.og_d[tk, :].rearrange("(c s) e -> s c e", s=64),
                      reads=[('og_d', j) for j in jl], writes=[('ogg', i)])
            load_group(0)
            it = 0
            for g in range(NG):
                if g + 1 < NG:
                    load_group(g + 1)
                gi_ = g % 2
                for cl in range(GC):
                    c = g * GC + cl
                    for hh in range(4):
                        k_ = kg[gi_][:, hh, cl * 64:(cl + 1) * 64]
                        q_ = qg[gi_][:, hh, cl * 64:(cl + 1) * 64]
                        v_ = vg[gi_][:, cl, hh, :]
                        i2 = it % 2
                        it += 1
                        A = mlA[i2]
                        pS_ = A[0:64, 0:64]
                        pO_ = A[0:64, 64:193]
                        pG_ = A[:, 256:385]
                        pkT_ = A[0:64, 448:512].bitcast(BF16)
                        P.op('pe', lambda e: e.transpose(out=pkT_, in_=k_, identity=identb[:]),
                             reads=[('kg', gi_), 'identb'], writes=[('mlA', i2)])
                        P.op('pe', lambda e: e.matmul(pS_, lhsT=k_, rhs=q_, start=True, stop=True),
                             reads=[('kg', gi_), ('qg', gi_)], writes=[('mlA', i2)])
                        yield
                        P.op('act', lambda e: e.activation(out=ku2[i2][:], in_=pkT_, func=AF.Copy,
                                                           scale=u2T[:, hh, c:c + 1]),
                             reads=[('mlA', i2), 'u2T'], writes=[('ku2', i2)])
                        P.op('dve', lambda e: e.scalar_tensor_tensor(out=Sm[i2][:], in0=pS_,
                                                                     scalar=uT[:, hh, c:c + 1], in1=tri[0:64, 0:64],
                                                                     op0=ALU.mult, op1=ALU.mult),
                             reads=[('mlA', i2), 'uT', 'tri'], writes=[('Sm', i2)])
                        yield
                        P.op('pe', lambda e: e.matmul(pO_, lhsT=Sm[i2][:], rhs=v_, start=True, stop=False),
                             reads=[('Sm', i2), ('vg', gi_)], writes=[('mlA', i2)])
                        P.op('pe', lambda e: e.matmul(pO_, lhsT=q_, rhs=Gb[hh][:], start=False, stop=True),
                             reads=[('qg', gi_), ('Gb', hh)], writes=[('mlA', i2)])
                        P.op('pe', lambda e: e.matmul(pG_, lhsT=ku2[i2][:], rhs=v_, start=True, stop=True),
                             reads=[('ku2', i2), ('vg', gi_)], writes=[('mlA', i2)])
                        yield
                        P.op('dve', lambda e: e.scalar_tensor_tensor(out=G[hh][:], in0=G[hh][:], scalar=decB[:, hh, c:c + 1],
                                                                     in1=pG_, op0=ALU.mult, op1=ALU.add),
                             reads=[('G', hh), 'decB', ('mlA', i2)], writes=[('G', hh)])
                        P.op('pool', lambda e: e.tensor_copy(out=Gb[hh][:], in_=G[hh][:]), reads=[('G', hh)], writes=[('Gb', hh)])
                        s_ = scm[i2]
                        sk = ('scm', i2)
                        P.op('act', lambda e: e.activation(out=junkm[:], in_=pO_[:, 0:128], func=AF.Square,
                                                           scale=128.0 ** -0.5, accum_out=s_[:, 0:1]),
                             reads=[('mlA', i2)], writes=['junkm', sk])
                        yield
                        P.op('dve', lambda e: e.tensor_scalar(out=s_[:, 6:7], in0=pO_[:, 128:129], scalar1=-1.0,
                                                              scalar2=flT[:, hh, c:c + 1], op0=ALU.mult, op1=ALU.max),
                             reads=[('mlA', i2), 'flT'], writes=[sk])
                        P.op('dve', lambda e: e.tensor_tensor(out=s_[:, 1:2], in0=s_[:, 6:7], in1=pO_[:, 128:129], op=ALU.max),
                             reads=[('mlA', i2), sk], writes=[sk])
                        P.op('dve', lambda e: e.tensor_tensor(out=s_[:, 2:3], in0=s_[:, 1:2], in1=s_[:, 1:2], op=ALU.mult),
                             reads=[sk], writes=[sk])
                        P.op('dve', lambda e: e.scalar_tensor_tensor(out=s_[:, 3:4], in0=s_[:, 2:3], scalar=EPS, in1=s_[:, 0:1],
                                                                     op0=ALU.mult, op1=ALU.add), reads=[sk], writes=[sk])
                        yield
                        P.op('act', lambda e: e.activation(out=s_[:, 4:5], in_=s_[:, 3:4], func=AF.Ln), reads=[sk], writes=[sk])
                        P.op('act', lambda e: e.activation(out=s_[:, 5:6], in_=s_[:, 4:5], func=AF.Exp, scale=-0.5), reads=[sk], writes=[sk])
                        P.op('dve', lambda e: e.scalar_tensor_tensor(out=ym[gi_][:, cl, hh * 128:(hh + 1) * 128],
                                                                     in0=pO_[:, 0:128], scalar=s_[:, 5:6],
                                                                     in1=ogg[gi_][:, cl, hh * 128:(hh + 1) * 128],
                                                                     op0=ALU.mult, op1=ALU.mult),
                             reads=[('mlA', i2), sk, ('ogg', gi_)], writes=[('ym', gi_)])
                        yield
                P.dma('sp', self.mix_d[g * TG:(g + 1) * TG, 0:512].rearrange("(c s) e -> s c e", s=64), ym[gi_][:],
                      reads=[('ym', gi_)], writes=[('mixm', g)])
        ML_YIELDS = NCH * 4 * 6 + 1
        g_ml = gen_ml()
        if not os.environ.get("SKIP_ML"):
            bg.add(g_ml)

        with sm:
            tm4 = sb(sm, "tm4", [128, 4, 512], BF16)
            P.dma('sp', tm4[:], self.c_tm4, writes=['tm4'])
            negB = self._negB(sm, "negBm", self.moba_qk_norm[l, 0], self.moba_qk_norm[l, 1])
            KX = [sb(sm, "KX", [96, S], BF16) for _ in range(2)]
            QXA = [sb(sm, "QXA", [96, S], BF16) for _ in range(2)]
            VX = [sb(sm, "VX", [128, NT, 65], BF16) for _ in range(2)]
            kmean = sb(sm, "kmean", [64, 32], F32)
            kmb = [sb(sm, "kmb", [64, 32], BF16) for _ in range(2)]
            gsb = [sb(sm, "gsb_b", [128, 32], F32) for _ in range(2)]
            m8 = [sb(sm, "m8", [128, 8], F32) for _ in range(2)]
            sel = [sb(sm, "sel_b", [128, 32], F32) for _ in range(2)]
            MBw = [sb(sm, "MBw", [128, 128], BF16) for _ in range(2)]
            PT = [sb(sm, "PT", [128, 512], BF16) for _ in range(4)]
            rz = sb(sm, "rz_b", [128, 4], F32)
            yb = [sb(sm, "yb", [128, 4, 64], BF16) for _ in range(2)]
            pO = [ps(sm, "pO_b", [128, 512], F32) for _ in range(1)] * 2
            if os.environ.get("NO_BITCAST"):
                pMb = ps(sm, "pMbx", [128, 128], BF16)[:]
            for i in range(2):
                P.dma('sp', KX[i][64:96, :], self.c_ind32, writes=[('KXi', i)])
                P.op('pool', lambda e: e.memset(VX[i][:], 1.0), writes=[('VX', i)])
                P.op('pool', lambda e: e.memset(MBw[i][:], 0.0), writes=[('MBw', i)])
            P.op('pool', lambda e: e.memset(kmean[:], 0.0), writes=['kmean'])

            def gen_prep(h):
                i = h % 2
                P.dma('sp', KX[i][0:64, :], self.bkT[h * 64:(h + 1) * 64, :], reads=[('bkT', j) for j in allt], writes=[('KX', i)])
                self.dma_mid(VX[i][:, :, 0:64], self.bv_d[:, h * 64:(h + 1) * 64].rearrange("(kt p) d -> p kt d", p=128), NT, 8,
                             reads=[('bv_d', j) for j in allt], writes=[('VX', i)])
                P.dma('sp', QXA[i][0:64, :], self.bqT[h * 64:(h + 1) * 64, :], reads=[('bqT', j) for j in allt], writes=[('QXAq', i)])
                yield
                P.op('dve', lambda e: e.tensor_reduce(out=kmean[:, 0:NB], in_=KX[i][0:64, :].rearrange("p (n k) -> p n k", k=256),
                                                      axis=AX.X, op=ALU.add), reads=[('KX', i)], writes=['kmean'])
                P.op('dve', lambda e: e.tensor_scalar(out=kmb[i][:], in0=kmean[:], scalar1=1.0 / 256, scalar2=None, op0=ALU.mult),
                     reads=['kmean'], writes=[('kmb', i)])
                yield
                for jt in range(NT if os.environ.get("BIS", "0") != "2" else 0):
                    own = jt // 2
                    b_ = jt % 2
                    P.op('pe', lambda e: e.matmul(pM, lhsT=QXA[i][0:64, jt * 128:(jt + 1) * 128], rhs=kmb[i][:],
                                                  start=True, stop=True), reads=[('QXAq', i), ('kmb', i)], writes=['pMi'])
                    P.op('pool', lambda e: e.memset(gsb[b_][:], -1e30), writes=[('gsb', b_)])
                    if own > 0:
                        P.op('dve', lambda e: e.tensor_copy(out=gsb[b_][:, 0:own], in_=pM[:, 0:own]), reads=['pMi'], writes=[('gsb', b_)])
                    yield
                    P.op('dve', lambda e: e.max(out=m8[b_][:], in_=gsb[b_][:]), reads=[('gsb', b_)], writes=[('m8', b_)])
                    P.op('dve', lambda e: e.tensor_scalar(out=sel[b_][:], in0=gsb[b_][:], scalar1=m8[b_][:, 2:3], scalar2=None,
                                                          op0=ALU.is_ge), reads=[('gsb', b_), ('m8', b_)], writes=[('sel', b_)])
                    P.op('dve', lambda e: e.tensor_scalar(out=MBw[b_][:, 64:96], in0=sel[b_][:], scalar1=-NEGM, scalar2=NEGM,
                                                          op0=ALU.mult, op1=ALU.add), reads=[('sel', b_)], writes=[('MBw', b_)])
                    P.op('dve', lambda e: e.memset(MBw[b_][:, 64 + own:65 + own], 0.0), writes=[('MBw', b_)])
                    if own + 1 < 32:
                        P.op('dve', lambda e: e.memset(MBw[b_][:, 65 + own:96], NEGM), writes=[('MBw', b_)])
                    yield
                    P.op('pe', lambda e: e.transpose(out=pMb, in_=MBw[b_][:], identity=identb[:]),
                         reads=[('MBw', b_), 'identb'], writes=['pMi'])
                    P.op('act', lambda e: e.copy(out=QXA[i][64:96, jt * 128:(jt + 1) * 128], in_=pMb[64:96, :]),
                         reads=['pMi'], writes=[('QXAm', i, jt)])
                    yield
            PREP_YIELDS = 3 * NT + 2
            main_iters_h = sum(4 * qc + 4 for qc in range(NQC))
            ml_rate = ML_YIELDS / float(4 * main_iters_h) * 1.15
            prep_rate = PREP_YIELDS / float(main_iters_h) * 1.3
            g0 = gen_prep(0)
            for _ in g0:
                bg.step(1.0)
            si = 0
            pi = 0
            for h in range(4):
                i = h % 2
                gp = None
                pacc = 0.0
                if h + 1 < 4:
                    gp = gen_prep(h + 1)
                    if os.environ.get("NOINT"):
                        for _ in gp:
                            pass
                        gp = None
                for qc in range(min(NQC, int(os.environ.get("QCMAX", "99"))) if (os.environ.get("BIS", "0") == "0" and h < int(os.environ.get("HMAX", "9"))) else 0):
                    q0 = qc * 512
                    Q_ = QXA[i][:, q0:q0 + 512]
                    qkeys = [('QXAq', i)] + [('QXAm', i, 4 * qc + j) for j in range(4)]
                    po = pO[pi % 2]
                    pok = ('pO', pi % 2)
                    pi += 1
                    P.op('pe', lambda e: e.matmul(po[:, 0:260], lhsT=zl[:], rhs=zr[:, 0:260], start=True, stop=True,
                                                  skip_group_check=True), reads=['zl', 'zr'], writes=[pok])
                    nkt = 4 * qc + 4
                    base = si

                    def qk(kt):
                        p_ = pSr[(base + kt) % 3]
                        P.op('pe', lambda e: e.matmul(p_[:], lhsT=KX[i][:, kt * 128:(kt + 1) * 128], rhs=Q_, start=True, stop=True),
                             reads=[('KX', i), ('KXi', i)] + qkeys, writes=[('pSr', (base + kt) % 3)])
                    LA = int(os.environ.get("LA", "2"))
                    for kt in range(min(LA, nkt)):
                        qk(kt)
                    for kt in range(nkt):
                        if kt + LA < nkt:
                            qk(kt + LA)
                        p_ = pSr[(base + kt) % 3]
                        pk = ('pSr', (base + kt) % 3)
                        t_ = PT[(base + kt) % 4]
                        tk = ('PT', (base + kt) % 4)
                        P.op('act', lambda e: e.activation(out=t_[:], in_=p_[:], func=AF.Exp, bias=negB[:, 0:1], scale=0.125),
                             reads=[pk, 'negBm'], writes=[tk])
                        off = kt - 4 * qc
                        if off >= 0:
                            P.op('pool', lambda e: e.tensor_tensor(out=t_[:], in0=t_[:], in1=tm4[:, off, :], op=ALU.mult),
                                 reads=[tk, 'tm4'], writes=[tk])
                        for j in range(4):
                            if kt > 4 * qc + j:
                                continue
                            P.op('pe', lambda e: e.matmul(po[:, j * 65:(j + 1) * 65], lhsT=t_[:, j * 128:(j + 1) * 128],
                                                          rhs=VX[i][:, kt, :], start=False, stop=(kt == 4 * qc + j),
                                                          skip_group_check=True), reads=[tk, ('VX', i)], writes=[pok])
                        bg.step(ml_rate)
                        if gp is not None:
                            pacc += prep_rate
                            while pacc >= 1.0:
                                pacc -= 1.0
                                try:
                                    next(gp)
                                except StopIteration:
                                    gp = None
                                    break
                    si += nkt
                    y_ = yb[qc % 2]
                    yk = ('yb', qc % 2)
                    pov = po[:, 0:260].rearrange("p (j d) -> p j d", d=65)
                    P.op('dve', lambda e: e.reciprocal(out=rz[:], in_=pov[:, :, 64]), reads=[pok], writes=['rz'])
                    P.op('dve', lambda e: e.tensor_tensor(out=y_[:], in0=pov[:, :, 0:64],
                                                          in1=rz[:].unsqueeze(2).broadcast_to([128, 4, 64]), op=ALU.mult),
                         reads=[pok, 'rz'], writes=[yk])
                    P.dma('sp', self.mix_d[q0:q0 + 512, 512 + h * 64:512 + (h + 1) * 64].rearrange("(j p) d -> p j d", p=128),
                          y_[:], reads=[yk], writes=[('mixb', h, qc)])
                if gp is not None:
                    for _ in gp:
                        bg.step(1.0)
            bg.drain()
            self.barrier()
        if not os.environ.get("SKIP_NSA"):
            self._phase2_nsa_pipe(l, pSr, pMi, identb, zl, zr)
        self.barrier()


Builder.phase2 = _phase2


def _phase2_nsa_pipe(self, l, pSr, pMi, identb, zl, zr, bgen=None, bg_yields=0):
    S, P, nc = self.S, self.P, self.nc
    NT = S // 128
    Nc = S // 16 - 1
    NCT = max(1, S // 2048)
    sb, ps = self.sb, self.ps
    allt = list(range(NT))
    pMb = pMi[:, 128:192].bitcast(BF16)
    pMb2 = pMi[:, 192:256].bitcast(BF16)
    pM = pMi[:, 256:320]
    with ExitStack() as st:
        trib = sb(st, "trib", [128, 128], BF16)
        triw = sb(st, "triw", [128, 128], BF16)
        cmask = sb(st, "cmask", [128, 17, 128], BF16)
        OVL = sb(st, "OVL", [128, NCT, 128], BF16)
        cols = sb(st, "cols", [128, 4], F32)
        NG = sb(st, "NG", [128, NT, 12], F32)
        P.dma('sp', trib[:], self.c_trib, writes=['trib'])
        P.dma('sp', triw[:], self.c_triw, writes=['triw'])
        P.dma('sp', cmask[:], self.c_cmask, writes=['cmask'])
        P.dma('sp', OVL[:], self.c_ovl.rearrange("(ct p) n -> p ct n", p=128)[:, 0:NCT, :], writes=['OVL'])
        P.dma('sp', cols[:], self.c_cols, writes=['cols'])
        self.dma_mid(NG[:], self.ng_d.rearrange("(j p) g -> p j g", p=128), NT, 8, reads=[('ng_d', j) for j in allt], writes=['NG'])
        negBc = self._negB(st, "negBc", self.nsa_q_norm[l], self.nsa_k_norm[l, 0])
        negBs = self._negB(st, "negBs", self.nsa_q_norm[l], self.nsa_k_norm[l, 1])
        negBw = self._negB(st, "negBw", self.nsa_q_norm[l], self.nsa_k_norm[l, 2])
        KSX = sb(st, "KSX", [128, S], BF16)
        KWX = sb(st, "KWX", [64, S], BF16)
        VSX = sb(st, "VSX", [128, NT, 65], BF16)
        VWX = sb(st, "VWX", [128, NT, 65], BF16)
        KcT = sb(st, "KcT", [64, NCT * 128], BF16)
        VCX = sb(st, "VCX", [128, NCT, 65], BF16)
        P.dma('sp', KSX[0:64, :], self.ksT, reads=[('ksT', j) for j in allt], writes=['KSX'])
        P.dma('sp', KSX[64:128, :], self.c_ind64, writes=['KSXi'])
        P.dma('sp', KWX[:], self.kwT, reads=[('kwT', j) for j in allt], writes=['KWX'])
        for V_, src, nm in ((VSX, self.vs_d, 'vs_d'), (VWX, self.vw_d, 'vw_d')):
            P.op('pool', lambda e: e.memset(V_[:], 1.0), writes=[nm + 'X'])
            self.dma_mid(V_[:, :, 0:64], src.rearrange("(kt p) d -> p kt d", p=128), NT, 8,
                         reads=[(nm, j) for j in allt], writes=[nm + 'X'])
        P.op('pool', lambda e: e.memset(VCX[:], 1.0), writes=['VCX'])
        pSc = ps(st, "pSc", [128, 512], F32) if bgen is None else pMi
        pSc_key = 'pSc' if bgen is None else 'pMi'
        pOcU = ps(st, "pOcU", [128, 512], F32)
        pOs = ps(st, "pOs", [128, 512], F32)
        pOw = ps(st, "pOw", [128, 512], F32)
        with ExitStack() as s2:
            KCV = sb(s2, "KCV", [128, S], BF16)
            W1s = sb(s2, "W1s", [128, 32, 128], F32)
            W1 = sb(s2, "W1", [128, 32, 128], BF16)
            pes = sb(s2, "pes", [32, 128], F32)
            peb = sb(s2, "peb", [32, 128], BF16)
            peT = sb(s2, "peT", [128, 32], BF16)
            w2s = sb(s2, "w2s", [128, 2, 64], F32)
            w2 = sb(s2, "w2", [128, 2, 64], BF16)
            gk0 = sb(s2, "gk0", [128, 64], F32)
            bias = sb(s2, "bias_c", [128, 2], F32)
            hidb = sb(s2, "hidb", [128, NCT * 128], BF16)
            kcn = sb(s2, "kcn", [128, 64], BF16)
            junk = sb(s2, "junk_c", [128, 64], F32)
            ssc = sb(s2, "ssc", [128, 2], F32)
            P.dma('sp', KCV[:], self.kcvcT, reads=[('kcvcT', b) for b in range(S // 512)], writes=['KCV'])
            for br in range(2):
                P.dma('sp', W1s[64 * br:64 * br + 64], self.cmp_w1[l, br].rearrange("(r d) j -> d r j", d=64), writes=['W1s'])
                P.dma('sp', w2s[:, br, :], self.cmp_w2[l, br], writes=['w2s'])
                P.dma('sp', pes[:, 64 * br:64 * br + 64], self.cmp_pe[l, br], writes=['pes'])
            P.dma('sp', gk0[:], self.nsa_k_norm[l, 0].partition_broadcast(128), writes=['gk0'])
            P.op('dve', lambda e: e.tensor_copy(out=W1[:], in_=W1s[:]), reads=['W1s'], writes=['W1'])
            P.op('dve', lambda e: e.tensor_copy(out=peb[:], in_=pes[:]), reads=['pes'], writes=['peb'])
            P.op('pe', lambda e: e.transpose(out=pMb[:, 0:32], in_=peb[:], identity=identb[0:32, 0:32]),
                 reads=['peb', 'identb'], writes=['pMi'])
            P.op('dve', lambda e: e.tensor_copy(out=peT[:], in_=pMb[:, 0:32]), reads=['pMi'], writes=['peT'])
            P.op('dve', lambda e: e.tensor_copy(out=w2[:], in_=w2s[:]), reads=['w2s'], writes=['w2'])
            P.op('pool', lambda e: e.memset(hidb[:], 0.0), writes=['hidb'])
            for br in range(2):
                rows = slice(64 * br, 64 * br + 64)
                kview = KCV[rows, :].rearrange("p (c s) -> p c s", s=16)
                for r in range(32):
                    P.op('pe', lambda e: e.matmul(pM[:, 0:1], lhsT=W1[rows, r, :], rhs=peT[rows, r:r + 1],
                                                  start=(r == 0), stop=(r == 31)), reads=['W1', 'peT'], writes=['pMi'])
                P.op('dve', lambda e: e.tensor_copy(out=bias[:, br:br + 1], in_=pM[:, 0:1]), reads=['pMi'], writes=['bias'])
                for r in range(32):
                    rhs = kview[:, 0:Nc, r] if r < 16 else kview[:, 1:Nc + 1, r - 16]
                    P.op('pe', lambda e: e.matmul(pSc[:, 0:Nc], lhsT=W1[rows, r, :], rhs=rhs,
                                                  start=(r == 0), stop=(r == 31)), reads=['W1', 'KCV'], writes=[pSc_key])
                P.op('act', lambda e: e.activation(out=hidb[:, 0:Nc], in_=pSc[:, 0:Nc], func=AF.Silu, bias=bias[:, br:br + 1]),
                     reads=[pSc_key, 'bias'], writes=['hidb'])
                for ct in range(NCT):
                    P.op('pe', lambda e: e.matmul(pM[:, 0:64], lhsT=hidb[:, ct * 128:(ct + 1) * 128], rhs=w2[:, br, :],
                                                  start=True, stop=True), reads=['hidb', 'w2'], writes=['pMi'])
                    if br == 0:
                        P.op('act', lambda e: e.activation(out=junk[:], in_=pM[:, 0:64], func=AF.Square, scale=0.125,
                                                           accum_out=ssc[:, 0:1]), reads=['pMi'], writes=['junk_c', 'ssc'])
                        P.op('dve', lambda e: e.tensor_scalar(out=ssc[:, 0:1], in0=ssc[:, 0:1], scalar1=EPS, scalar2=None,
                                                              op0=ALU.add), reads=['ssc'], writes=['ssc'])
                        P.op('act', lambda e: e.activation(out=ssc[:, 0:1], in_=ssc[:, 0:1], func=AF.Sqrt), reads=['ssc'], writes=['ssc'])
                        P.op('dve', lambda e: e.reciprocal(out=ssc[:, 1:2], in_=ssc[:, 0:1]), reads=['ssc'], writes=['ssc'])
                        P.op('dve', lambda e: e.scalar_tensor_tensor(out=kcn[:], in0=pM[:, 0:64], scalar=ssc[:, 1:2], in1=gk0[:],
                                                                     op0=ALU.mult, op1=ALU.mult),
                             reads=['pMi', 'ssc', 'gk0'], writes=['kcn'])
                        P.op('pe', lambda e: e.transpose(out=pMb[0:64, :], in_=kcn[:], identity=identb[:]),
                             reads=['kcn', 'identb'], writes=['pMi'])
                        P.op('act', lambda e: e.copy(out=KcT[:, ct * 128:(ct + 1) * 128], in_=pMb[0:64, :]),
                             reads=['pMi'], writes=['KcT'])
                    else:
                        P.op('act', lambda e: e.copy(out=VCX[:, ct, 0:64], in_=pM[:, 0:64]), reads=['pMi'], writes=['VCX'])
            self.barrier()
        QU = [sb(st, "QU", [64, 512], BF16) for _ in range(2)]
        QR0 = [sb(st, "QR0", [128, 512], BF16) for _ in range(2)]
        QR1 = [sb(st, "QR1", [128, 512], BF16) for _ in range(2)]
        PTc = [sb(st, "PTc", [128, 512], BF16) for _ in range(NCT)]
        PT = [sb(st, "PTn", [128, 512], BF16) for _ in range(4)]
        zc = sb(st, "zc", [128, 4], F32)
        rzc = sb(st, "rzc", [128, 4], F32)
        zz = sb(st, "zz", [128, 2, 4], F32)
        rzz = sb(st, "rzz", [128, 2, 4], F32)
        coef = sb(st, "coef", [128, 2, 4], F32)
        imp = sb(st, "imp", [128, 128], F32)
        work = sb(st, "work", [128, 128], F32)
        m8a = sb(st, "m8a", [128, 8], F32)
        m8b = sb(st, "m8b", [128, 8], F32)
        selm = sb(st, "selm", [128, 128], F32)
        MB = sb(st, "MB", [128, 128], BF16)
        MBs = sb(st, "MBs", [128, 128], BF16)
        yc = [sb(st, "yc", [128, 4, 64], F32) for _ in range(2)]
        yacc = sb(st, "yacc", [128, 4, 64], F32)
        yn = [sb(st, "yn", [128, 256], BF16) for _ in range(2)]

        def hb(ap):
            return ap.unsqueeze(1).broadcast_to([ap.shape[0], 4, 128])

        def h4(ap):
            return ap.rearrange("p (h t) -> p h t", h=4)

        def gen_prep(m):
            t0 = m * 128
            b2 = m % 2
            qu, qr0, qr1 = QU[b2], QR0[b2], QR1[b2]
            use_g1 = (2 * m + 1) >= 64
            P.dma('sp', h4(qu[:]), self.nquT[:, t0:t0 + 128].rearrange("(h d) t -> d h t", d=64),
                  reads=[('nquT', m)], writes=[('QU', b2)])
            P.dma('sp', h4(qr0[0:64, :]), self.nqrT[:, t0:t0 + 128].rearrange("(h d) t -> d h t", d=64),
                  reads=[('nqrT', m)], writes=[('QR0q', b2)])
            if use_g1:
                P.dma('sp', h4(qr1[0:64, :]), self.nqrT[:, t0:t0 + 128].rearrange("(h d) t -> d h t", d=64),
                      reads=[('nqrT', m)], writes=[('QR1q', b2)])
            yield
            ctn = min(NCT, (8 * m + 6) // 128 + 1)
            for ct in range(ctn):
                P.op('pe', lambda e: e.matmul(pSc[:], lhsT=KcT[:, ct * 128:(ct + 1) * 128], rhs=qu[:], start=True, stop=True),
                     reads=['KcT', ('QU', b2)], writes=[pSc_key])
                yield
                P.op('act', lambda e: e.activation(out=PTc[ct][:], in_=pSc[:], func=AF.Exp, bias=negBc[:, 0:1], scale=0.125),
                     reads=[pSc_key, 'negBc'], writes=[('PTc', ct)])
                yield
                r = m - 16 * ct
                if r <= 16:
                    P.op('dve', lambda e: e.tensor_tensor(out=h4(PTc[ct][:]), in0=h4(PTc[ct][:]), in1=hb(cmask[:, r, :]), op=ALU.mult),
                         reads=[('PTc', ct), 'cmask'], writes=[('PTc', ct)])
                    yield
                yield
            for h in range(4):
                for ct in range(ctn):
                    P.op('pe', lambda e: e.matmul(pOcU[:, h * 65:(h + 1) * 65], lhsT=PTc[ct][:, h * 128:(h + 1) * 128],
                                                  rhs=VCX[:, ct, :], start=(ct == 0), stop=(ct == ctn - 1)),
                         reads=[('PTc', ct), 'VCX'], writes=['pOcU'])
                    yield
                for ct in range(ctn):
                    P.op('pe', lambda e: e.matmul(pOcU[:, 260:388], lhsT=PTc[ct][:, h * 128:(h + 1) * 128],
                                                  rhs=OVL[:, ct, :], start=(ct == 0), stop=(ct == ctn - 1)),
                         reads=[('PTc', ct), 'OVL'], writes=['pOcU'])
                    yield
                P.op('dve', lambda e: e.tensor_scalar(out=zc[:, h:h + 1], in0=pOcU[:, h * 65 + 64:h * 65 + 65], scalar1=1e-30,
                                                      scalar2=None, op0=ALU.max), reads=['pOcU'], writes=['zc'])
                yield
                P.op('dve', lambda e: e.reciprocal(out=rzc[:, h:h + 1], in_=zc[:, h:h + 1]), reads=['zc'], writes=['rzc'])
                yield
                if h == 0:
                    P.op('dve', lambda e: e.tensor_scalar(out=imp[:], in0=pOcU[:, 260:388], scalar1=rzc[:, 0:1], scalar2=None,
                                                          op0=ALU.mult), reads=['pOcU', 'rzc'], writes=['imp'])
                    yield
                else:
                    P.op('dve', lambda e: e.scalar_tensor_tensor(out=imp[:], in0=pOcU[:, 260:388], scalar=rzc[:, h:h + 1],
                                                                 in1=imp[:], op0=ALU.mult, op1=ALU.add),
                         reads=['pOcU', 'rzc', 'imp'], writes=['imp'])
                    yield
                P.op('dve', lambda e: e.tensor_scalar(out=yc[b2][:, h, :], in0=pOcU[:, h * 65:h * 65 + 64], scalar1=rzc[:, h:h + 1],
                                                      scalar2=None, op0=ALU.mult), reads=['pOcU', 'rzc'], writes=[('yc', b2)])
                yield
                yield
            n1 = 2 * m + 1
            if n1 + 1 < 128:
                P.op('pool', lambda e: e.memset(imp[:, n1 + 1:128], -1e30), reads=['imp'], writes=['imp'])
                yield
            P.op('pool', lambda e: e.tensor_copy(out=imp[:, n1:n1 + 1], in_=cols[:, 0:1]), reads=['cols', 'imp'], writes=['imp'])
            yield
            P.op('pool', lambda e: e.memset(imp[:, n1 - 1:n1], 1e9), reads=['imp'], writes=['imp'])
            yield
            if n1 - 2 >= 0:
                P.op('dve', lambda e: e.tensor_tensor(out=imp[:, n1 - 2:n1 - 1], in0=imp[:, n1 - 2:n1 - 1], in1=cols[:, 1:2], op=ALU.max),
                     reads=['cols', 'imp'], writes=['imp'])
                yield
            P.op('pool', lambda e: e.memset(imp[:, 0:1], 1e9), reads=['imp'], writes=['imp'])
            yield
            yield
            P.op('dve', lambda e: e.max(out=m8a[:], in_=imp[:]), reads=['imp'], writes=['m8a'])
            yield
            P.op('dve', lambda e: e.match_replace(out=work[:], in_to_replace=m8a[:], in_values=imp[:], imm_value=-1e30),
                 reads=['imp', 'm8a'], writes=['work'])
            yield
            P.op('dve', lambda e: e.max(out=m8b[:], in_=work[:]), reads=['work'], writes=['m8b'])
            yield
            yield
            P.op('dve', lambda e: e.tensor_scalar(out=selm[:], in0=imp[:], scalar1=m8b[:, 7:8], scalar2=None, op0=ALU.is_ge),
                 reads=['imp', 'm8b'], writes=['selm'])
            yield
            P.op('dve', lambda e: e.tensor_scalar(out=MB[:], in0=selm[:], scalar1=-NEGM, scalar2=NEGM, op0=ALU.mult, op1=ALU.add),
                 reads=['selm'], writes=['MB'])
            yield
            if n1 + 1 < 128:
                P.op('pool', lambda e: e.memset(MB[:, n1 + 1:128], NEGM), reads=['MB'], writes=['MB'])
                yield
            P.op('pool', lambda e: e.tensor_copy(out=MB[:, n1:n1 + 1], in_=cols[:, 2:3]), reads=['cols', 'MB'], writes=['MB'])
            yield
            yield
            P.op('pool', lambda e: e.tensor_copy(out=MBs[:, 0:64], in_=MB[:, 64:128]), reads=['MB'], writes=['MBs'])
            yield
            P.op('pool', lambda e: e.tensor_copy(out=MBs[:, 64:128], in_=MB[:, 0:64]), reads=['MB'], writes=['MBs'])
            yield
            P.op('pe', lambda e: e.transpose(out=pMb, in_=MBs[:], identity=identb[:]), reads=['MBs', 'identb'], writes=['pMi'])
            yield
            P.op('act', lambda e: e.copy(out=h4(qr0[64:128, :]), in_=hb(pMb[64:128, :])), reads=['pMi'], writes=[('QR0m', b2)])
            yield
            if use_g1:
                P.op('pe', lambda e: e.transpose(out=pMb2, in_=MB[:], identity=identb[:]), reads=['MB', 'identb'], writes=['pMi'])
                yield
                P.op('act', lambda e: e.copy(out=h4(qr1[64:128, :]), in_=hb(pMb2[64:128, :])), reads=['pMi'], writes=[('QR1m', b2)])
                yield
            yield

        si = 0
        total_main = sum((m_ + 1) + (m_ + 1 - max(0, m_ - 4)) for m_ in range(NT))
        bg_rate = (bg_yields / float(total_main)) * 1.1 if bgen is not None else 0.0
        bg_state = {'g': bgen, 'acc': 0.0}

        def bg_step():
            if bg_state['g'] is None:
                return
            bg_state['acc'] += bg_rate
            while bg_state['acc'] >= 1.0 and bg_state['g'] is not None:
                bg_state['acc'] -= 1.0
                try:
                    next(bg_state['g'])
                except StopIteration:
                    bg_state['g'] = None
        g = gen_prep(0)
        for _ in g:
            pass
        for m in range(NT):
            t0 = m * 128
            b2 = m % 2
            qr0, qr1 = QR0[b2], QR1[b2]
            gp = gen_prep(m + 1) if m + 1 < NT else None
            P.op('pe', lambda e: e.matmul(pOs[:, 0:260], lhsT=zl[:], rhs=zr[:, 0:260], start=True, stop=True, skip_group_check=True),
                 reads=['zl', 'zr'], writes=['pOs'])
            P.op('pe', lambda e: e.matmul(pOw[:, 0:260], lhsT=zl[:], rhs=zr[:, 0:260], start=True, stop=True, skip_group_check=True),
                 reads=['zl', 'zr'], writes=['pOw'])
            items = [('s', kt) for kt in range(m + 1)] + [('w', kt) for kt in range(max(0, m - 4), m + 1)]
            n_it = len(items)
            rate = 80.0 / n_it
            base = si

            def qk(ix):
                typ, kt = items[ix]
                p_ = pSr[(base + ix) % 3]
                pk = ('pSr', (base + ix) % 3)
                if typ == 's':
                    if kt // 32 == 0:
                        P.op('pe', lambda e: e.matmul(p_[:], lhsT=KSX[:, kt * 128:(kt + 1) * 128], rhs=qr0[:], start=True, stop=True),
                             reads=['KSX', 'KSXi', ('QR0q', b2), ('QR0m', b2)], writes=[pk])
                    else:
                        P.op('pe', lambda e: e.matmul(p_[:], lhsT=KSX[:, kt * 128:(kt + 1) * 128], rhs=qr1[:], start=True, stop=True),
                             reads=['KSX', 'KSXi', ('QR1q', b2), ('QR1m', b2)], writes=[pk])
                else:
                    P.op('pe', lambda e: e.matmul(p_[:], lhsT=KWX[:, kt * 128:(kt + 1) * 128], rhs=qr0[0:64, :], start=True, stop=True),
                         reads=['KWX', ('QR0q', b2)], writes=[pk])
            for ix in range(min(2, n_it)):
                qk(ix)
            pacc = 0.0
            for ix in range(n_it):
                if ix + 2 < n_it:
                    qk(ix + 2)
                typ, kt = items[ix]
                p_ = pSr[(base + ix) % 3]
                pk = ('pSr', (base + ix) % 3)
                t_ = PT[(base + ix) % 4]
                tk = ('PTn', (base + ix) % 4)
                nb_ = negBs if typ == 's' else negBw
                P.op('act', lambda e: e.activation(out=t_[:], in_=p_[:], func=AF.Exp, bias=nb_[:, 0:1], scale=0.125),
                     reads=[pk, 'negBs', 'negBw'], writes=[tk])
                mk = None
                if kt == m:
                    mk = trib
                elif typ == 'w' and kt == m - 4:
                    mk = triw
                if mk is not None:
                    P.op('dve', lambda e: e.tensor_tensor(out=h4(t_[:]), in0=h4(t_[:]), in1=hb(mk[:]), op=ALU.mult),
                         reads=[tk, 'trib', 'triw'], writes=[tk])
                po_, pok, V_, vk = (pOs, 'pOs', VSX, 'vs_dX') if typ == 's' else (pOw, 'pOw', VWX, 'vw_dX')
                for h in range(4):
                    P.op('pe', lambda e: e.matmul(po_[:, h * 65:(h + 1) * 65], lhsT=t_[:, h * 128:(h + 1) * 128], rhs=V_[:, kt, :],
                                                  start=False, stop=(kt == m), skip_group_check=True),
                         reads=[tk, vk], writes=[pok])
                bg_step()
                if gp is not None:
                    pacc += rate
                    while pacc >= 1.0 and gp is not None:
                        pacc -= 1.0
                        try:
                            next(gp)
                        except StopIteration:
                            gp = None
            si += n_it
            if gp is not None:
                for _ in gp:
                    pass
            posv = pOs[:, 0:260].rearrange("p (h d) -> p h d", d=65)
            powv = pOw[:, 0:260].rearrange("p (h d) -> p h d", d=65)
            P.op('dve', lambda e: e.tensor_copy(out=zz[:, 0, :], in_=posv[:, :, 64]), reads=['pOs'], writes=['zz'])
            P.op('dve', lambda e: e.tensor_copy(out=zz[:, 1, :], in_=powv[:, :, 64]), reads=['pOw'], writes=['zz'])
            P.op('dve', lambda e: e.reciprocal(out=rzz[:], in_=zz[:]), reads=['zz'], writes=['rzz'])
            P.op('dve', lambda e: e.tensor_tensor(out=coef[:], in0=rzz[:], in1=NG[:, m, 4:12].rearrange("p (b h) -> p b h", b=2), op=ALU.mult),
                 reads=['rzz', 'NG'], writes=['coef'])
            for h in range(4):
                P.op('dve', lambda e: e.tensor_scalar(out=yacc[:, h, :], in0=yc[b2][:, h, :], scalar1=NG[:, m, h:h + 1], scalar2=None,
                                                      op0=ALU.mult), reads=[('yc', b2), 'NG'], writes=[('yacc', h)])
                P.op('dve', lambda e: e.scalar_tensor_tensor(out=yacc[:, h, :], in0=posv[:, h, 0:64], scalar=coef[:, 0, h:h + 1],
                                                             in1=yacc[:, h, :], op0=ALU.mult, op1=ALU.add),
                     reads=['pOs', 'coef', ('yacc', h)], writes=[('yacc', h)])
                P.op('dve', lambda e: e.scalar_tensor_tensor(out=yn[b2][:, h * 64:(h + 1) * 64], in0=powv[:, h, 0:64],
                                                             scalar=coef[:, 1, h:h + 1], in1=yacc[:, h, :], op0=ALU.mult, op1=ALU.add),
                     reads=['pOw', 'coef', ('yacc', h)], writes=[('yn', b2)])
            P.dma('sp', self.mix_d[t0:t0 + 128, 768:1024], yn[b2][:], reads=[('yn', b2)], writes=[('mixn', m)])
        while bg_state['g'] is not None:
            try:
                next(bg_state['g'])
            except StopIteration:
                bg_state['g'] = None


Builder._phase2_nsa_pipe = _phase2_nsa_pipe


def _phase2_nsa2(self, l):
    P = self.P
    with ExitStack() as st:
        identb = self.sb(st, "identb_n2", [128, 128], BF16)
        zl = self.sb(st, "zl_n2", [1, 128], BF16)
        zr = self.sb(st, "zr_n2", [1, 512], BF16)
        P.dma('sp', identb[:], self.c_identb, writes=['identb'])
        P.op('pool', lambda e: e.memset(zl[:], 0.0), writes=['zl'])
        P.op('pool', lambda e: e.memset(zr[:], 0.0), writes=['zr'])
        pSr = [self.ps(st, "pSr2", [128, 512], F32) for _ in range(3)]
        pMi = self.ps(st, "pMi2", [128, 512], F32)
        self._phase2_nsa_pipe(l, pSr, pMi, identb, zl, zr)
        self.barrier()


Builder.phase2_nsa2 = _phase2_nsa2


def _phase2b(self, l):
    S, P, nc = self.S, self.P, self.nc
    NCH = S // 64
    NT = S // 128
    NQC = S // 512
    NB = S // 256
    Nc = S // 16 - 1
    NCT = max(1, S // 2048)
    LNSC = math.log(128.0 ** -0.5)
    sb, ps = self.sb, self.ps
    allt = list(range(NT))
    with ExitStack() as st:
        identb = sb(st, "identb2", [128, 128], BF16)
        identf = sb(st, "identf2", [128, 128], F32)
        ut = sb(st, "ut2", [128, 128], F32)
        tri = sb(st, "tri2", [128, 128], F32)
        zl = sb(st, "zl2", [1, 128], BF16)
        zr = sb(st, "zr2", [1, 512], BF16)
        P.dma('sp', identb[:], self.c_identb, writes=['identb'])
        P.dma('sp', identf[:], self.c_identf, writes=['identf'])
        P.dma('sp', ut[:], self.c_ut, writes=['ut'])
        P.dma('sp', tri[:], self.c_tri, writes=['tri'])
        P.op('pool', lambda e: e.memset(zl[:], 0.0), writes=['zl'])
        P.op('pool', lambda e: e.memset(zr[:], 0.0), writes=['zr'])
        uT = sb(st, "uT_m", [64, 4, NCH], F32)
        u2T = sb(st, "u2T_m", [64, 4, NCH], F32)
        flT = sb(st, "flT_m", [64, 4, NCH], F32)
        decB = sb(st, "decB", [128, 4, NCH], F32)
        with ExitStack() as s2:
            li = sb(s2, "li", [NCH, 4, 64], F32)
            lf = sb(s2, "lf", [NCH, 4, 64], F32)
            ones = sb(s2, "ones", [NCH, 64], F32)
            Fin = sb(s2, "Fin", [NCH, 4, 64], F32)
            Ft = sb(s2, "Ft", [NCH, 4, 64], F32)
            a_ = sb(s2, "a_", [NCH, 4, 64], F32)
            Ain = sb(s2, "Ain", [NCH, 4, 64], F32)
            tot = sb(s2, "tot", [NCH, 4], F32)
            cmax = sb(s2, "cmax", [NCH, 4], F32)
            cmT = sb(s2, "cmT", [4, NCH], F32)
            ET = sb(s2, "ET", [4, NCH], F32)
            ETn = sb(s2, "ETn", [4, NCH], F32)
            Ec = sb(s2, "Ec", [NCH, 4], F32)
            Enc = sb(s2, "Enc", [NCH, 4], F32)
            tmp = sb(s2, "tmpm", [NCH, 4, 64], F32)
            uu = sb(s2, "uu", [NCH, 4, 64], F32)
            uu2 = sb(s2, "uu2", [NCH, 4, 64], F32)
            fl = sb(s2, "fl", [NCH, 4, 64], F32)
            dec = sb(s2, "dec", [NCH, 4], F32)
            decrep = sb(s2, "decrep", [NCH, 4, 128], F32)
            pa = ps(s2, "pa", [128, 512], F32)
            pb = ps(s2, "pb", [128, 512], F32)
            P.dma('sp', li[:], self.gi_d.rearrange("h (c j) -> c h j", j=64),
                  reads=[('gi_d', b) for b in range(S // 512)], writes=['li'])
            P.dma('sp', lf[:], self.gf_d.rearrange("h (c j) -> c h j", j=64),
                  reads=[('gf_d', b) for b in range(S // 512)], writes=['lf'])
            P.op('pool', lambda e: e.memset(ones[:], 1.0), writes=['ones'])
            for hh in range(4):
                P.op('dve', lambda e: e.tensor_tensor_scan(out=Fin[:, hh, :], data0=ones[:], data1=lf[:, hh, :],
                                                           initial=0.0, op0=ALU.mult, op1=ALU.add),
                     reads=['ones', 'lf'], writes=['Fin'])
            P.op('dve', lambda e: e.tensor_copy(out=tot[:], in_=Fin[:, :, 63]), reads=['Fin'], writes=['tot'])
            P.op('pe', lambda e: e.matmul(pa[0:NCH, 0:4], lhsT=ut[0:NCH, 0:NCH], rhs=tot[:], start=True, stop=True),
                 reads=['ut', 'tot'], writes=['pa'])
            P.op('dve', lambda e: e.tensor_tensor(out=Ft[:], in0=Fin[:],
                                                  in1=pa[0:NCH, 0:4].unsqueeze(2).broadcast_to([NCH, 4, 64]), op=ALU.add),
                 reads=['Fin', 'pa'], writes=['Ft'])
            P.op('dve', lambda e: e.tensor_tensor(out=a_[:], in0=li[:], in1=Ft[:], op=ALU.subtract),
                 reads=['li', 'Ft'], writes=['a_'])
            for hh in range(4):
                P.op('dve', lambda e: e.tensor_tensor_scan(out=Ain[:, hh, :], data0=a_[:, hh, :], data1=a_[:, hh, :],
                                                           initial=-1e30, op0=ALU.max, op1=ALU.max),
                     reads=['a_'], writes=['Ain'])
            P.op('dve', lambda e: e.tensor_copy(out=cmax[:], in_=Ain[:, :, 63]), reads=['Ain'], writes=['cmax'])
            P.op('pe', lambda e: e.transpose(out=pb[0:4, 0:NCH], in_=cmax[:], identity=identf[0:NCH, 0:NCH]),
                 reads=['cmax', 'identf'], writes=['pb'])
            P.op('dve', lambda e: e.tensor_copy(out=cmT[:], in_=pb[0:4, 0:NCH]), reads=['pb'], writes=['cmT'])
            P.op('dve', lambda e: e.tensor_tensor_scan(out=ET[:], data0=cmT[:], data1=cmT[:], initial=0.0,
                                                       op0=ALU.max, op1=ALU.max), reads=['cmT'], writes=['ET'])
            if NCH > 1:
                P.op('dve', lambda e: e.tensor_copy(out=ETn[:, 0:NCH - 1], in_=ET[:, 1:NCH]), reads=['ET'], writes=['ETn'])
            P.op('dve', lambda e: e.tensor_copy(out=ETn[:, NCH - 1:NCH], in_=ET[:, NCH - 1:NCH]), reads=['ET'], writes=['ETn'])
            P.op('pe', lambda e: e.transpose(out=pa[0:NCH, 0:4], in_=ET[:], identity=identf[0:4, 0:4]),
                 reads=['ET', 'identf'], writes=['pa'])
            P.op('dve', lambda e: e.tensor_copy(out=Ec[:], in_=pa[0:NCH, 0:4]), reads=['pa'], writes=['Ec'])
            P.op('pe', lambda e: e.transpose(out=pb[0:NCH, 0:4], in_=ETn[:], identity=identf[0:4, 0:4]),
                 reads=['ETn', 'identf'], writes=['pb'])
            P.op('dve', lambda e: e.tensor_copy(out=Enc[:], in_=pb[0:NCH, 0:4]), reads=['pb'], writes=['Enc'])
            Eb = Ec[:].unsqueeze(2).broadcast_to([NCH, 4, 64])
            Enb = Enc[:].unsqueeze(2).broadcast_to([NCH, 4, 64])
            P.op('dve', lambda e: e.tensor_tensor(out=tmp[:], in0=a_[:], in1=Eb, op=ALU.subtract),
                 reads=['a_', 'Ec'], writes=['tmp'])
            P.op('act', lambda e: e.activation(out=uu[:], in_=tmp[:], func=AF.Exp, bias=LNSC), reads=['tmp'], writes=['uu'])
            P.op('dve', lambda e: e.tensor_tensor(out=tmp[:], in0=a_[:], in1=Enb, op=ALU.subtract),
                 reads=['a_', 'Enc'], writes=['tmp'])
            P.op('act', lambda e: e.activation(out=uu2[:], in_=tmp[:], func=AF.Exp, bias=LNSC), reads=['tmp'], writes=['uu2'])
            P.op('dve', lambda e: e.tensor_tensor(out=tmp[:], in0=Ft[:], in1=Eb, op=ALU.add),
                 reads=['Ft', 'Ec'], writes=['tmp'])
            P.op('act', lambda e: e.activation(out=fl[:], in_=tmp[:], func=AF.Exp, scale=-1.0), reads=['tmp'], writes=['fl'])
            P.op('dve', lambda e: e.tensor_tensor(out=dec[:], in0=Ec[:], in1=Enc[:], op=ALU.subtract),
                 reads=['Ec', 'Enc'], writes=['dec'])
            P.op('act', lambda e: e.activation(out=dec[:], in_=dec[:], func=AF.Exp), reads=['dec'], writes=['dec'])
            P.op('dve', lambda e: e.tensor_copy(out=decrep[:], in_=dec[:].unsqueeze(2).broadcast_to([NCH, 4, 128])),
                 reads=['dec'], writes=['decrep'])
            for src, dst, nm in ((uu, uT, 'uT'), (uu2, u2T, 'u2T'), (fl, flT, 'flT')):
                for hh in range(4):
                    P.op('pe', lambda e: e.transpose(out=pa[0:64, hh * 128:hh * 128 + NCH], in_=src[:, hh, :],
                                                     identity=identf[0:NCH, 0:NCH]),
                         reads=['uu', 'uu2', 'fl', 'identf'], writes=['pa'])
                P.op('dve', lambda e: e.tensor_copy(out=dst[:], in_=pa[0:64, :].rearrange("p (h c) -> p h c", h=4)[:, :, 0:NCH]),
                     reads=['pa'], writes=[nm])
            for hh in range(4):
                P.op('pe', lambda e: e.matmul(pb[:, hh * 128:hh * 128 + NCH], lhsT=decrep[:, hh, :],
                                              rhs=identf[0:NCH, 0:NCH], start=True, stop=True),
                     reads=['decrep', 'identf'], writes=['pb'])
            P.op('dve', lambda e: e.tensor_copy(out=decB[:], in_=pb[:].rearrange("p (h c) -> p h c", h=4)[:, :, 0:NCH]),
                 reads=['pb'], writes=['decB'])
            self.barrier()

        pSr = [ps(st, "pSr", [128, 512], F32) for _ in range(3)]
        pMi = ps(st, "pMi", [128, 512], F32)
        pMb = pMi[:, 128:192].bitcast(BF16)
        pMb2 = pMi[:, 192:256].bitcast(BF16)
        pM = pMi[:, 256:288]
        bg = BG()

        GC = 4
        NG = S // (64 * GC)
        TG = 64 * GC
        qg = [sb(st, "qg", [128, 4, TG], BF16) for _ in range(2)]
        kg = [sb(st, "kg", [128, 4, TG], BF16) for _ in range(2)]
        vg = [sb(st, "vg", [64, GC, 4, 129], BF16) for _ in range(2)]
        ogg = [sb(st, "ogg", [64, GC, 512], BF16) for _ in range(2)]
        ym = [sb(st, "ym", [64, GC, 512], BF16) for _ in range(2)]
        G = [sb(st, "G", [128, 129], F32) for _ in range(4)]
        Gb = [sb(st, "Gb", [128, 129], BF16) for _ in range(4)]
        ku2 = [sb(st, "ku2", [64, 128], BF16) for _ in range(2)]
        Sm = [sb(st, "Sm", [64, 64], BF16) for _ in range(2)]
        junkm = sb(st, "junkm", [64, 128], BF16)
        scm = [sb(st, "scm", [64, 8], F32) for _ in range(2)]
        mlA = [ps(st, "mlA", [128, 512], F32) for _ in range(1)]

        def gen_ml():
            for i in range(2):
                P.op('pool', lambda e: e.memset(vg[i][:], 1.0), writes=[('vg', i)])
            for hh in range(4):
                P.op('pool', lambda e: e.memset(G[hh][:], 0.0), writes=[('G', hh)])
                P.op('pool', lambda e: e.memset(Gb[hh][:], 0.0), writes=[('Gb', hh)])
            yield

            def load_group(g):
                i = g % 2
                tk = slice(g * TG, (g + 1) * TG)
                bl = sorted(set([(g * TG) // 512, ((g + 1) * TG - 1) // 512]))
                jl = list(range((g * TG) // 128, ((g + 1) * TG + 127) // 128))
                P.dma('sp', qg[i][:], self.qkT[0:512, tk].rearrange("(h p) t -> p h t", p=128),
                      reads=[('qkT', ft, b) for ft in range(4) for b in bl], writes=[('qg', i)])
                P.dma('sp', kg[i][:], self.qkT[512:1024, tk].rearrange("(h p) t -> p h t", p=128),
                      reads=[('qkT', ft, b) for ft in range(4, 8) for b in bl], writes=[('kg', i)])
                for hh in range(4):
                    P.dma('sp', vg[i][:, :, hh, 0:128],
                          self.mv_d[tk, hh * 128:(hh + 1) * 128].rearrange("(c s) e -> s c e", s=64),
                          reads=[('mv_d', j) for j in jl], writes=[('vg', i)])
                P.dma('sp', ogg[i][:], self.og_d[tk, :].rearrange("(c s) e -> s c e", s=64),
                      reads=[('og_d', j) for j in jl], writes=[('ogg', i)])
            load_group(0)
            it = 0
            for g in range(NG):
                if g + 1 < NG:
                    load_group(g + 1)
                gi_ = g % 2
                for cl in range(GC):
                    c = g * GC + cl
                    for hh in range(4):
                        k_ = kg[gi_][:, hh, cl * 64:(cl + 1) * 64]
                        q_ = qg[gi_][:, hh, cl * 64:(cl + 1) * 64]
                        v_ = vg[gi_][:, cl, hh, :]
                        i2 = it % 2
                        it += 1
                        A = mlA[i2 % len(mlA)]
                        if os.environ.get("UNPACK"):
                            pS_ = pSr[0][0:64, 0:64]
                            pO_ = pSr[1][0:64, 0:129]
                            pG_ = pSr[2][:, 0:129]
                            pkT_ = A[0:64, 0:64].bitcast(BF16)
                        else:
                            pS_ = A[0:64, 0:64]
                            pO_ = A[0:64, 64:193]
                            pG_ = A[:, 256:385]
                            pkT_ = A[0:64, 448:512].bitcast(BF16)
                        P.op('pe', lambda e: e.transpose(out=pkT_, in_=k_, identity=identb[:]),
                             reads=[('kg', gi_), 'identb'], writes=[('mlA', i2 % len(mlA))])
                        P.op('pe', lambda e: e.matmul(pS_, lhsT=k_, rhs=q_, start=True, stop=True),
                             reads=[('kg', gi_), ('qg', gi_)], writes=[('mlA', i2 % len(mlA))])
                        yield
                        P.op('act', lambda e: e.activation(out=ku2[i2][:], in_=pkT_, func=AF.Copy,
                                                           scale=u2T[:, hh, c:c + 1]),
                             reads=[('mlA', i2 % len(mlA)), 'u2T'], writes=[('ku2', i2)])
                        P.op('dve', lambda e: e.scalar_tensor_tensor(out=Sm[i2][:], in0=pS_,
                                                                     scalar=uT[:, hh, c:c + 1], in1=tri[0:64, 0:64],
                                                                     op0=ALU.mult, op1=ALU.mult),
                             reads=[('mlA', i2 % len(mlA)), 'uT', 'tri'], writes=[('Sm', i2)])
                        yield
                        P.op('pe', lambda e: e.matmul(pO_, lhsT=Sm[i2][:], rhs=v_, start=True, stop=False),
                             reads=[('Sm', i2), ('vg', gi_)], writes=[('mlA', i2 % len(mlA))])
                        P.op('pe', lambda e: e.matmul(pO_, lhsT=q_, rhs=Gb[hh][:], start=False, stop=True),
                             reads=[('qg', gi_), ('Gb', hh)], writes=[('mlA', i2 % len(mlA))])
                        P.op('pe', lambda e: e.matmul(pG_, lhsT=ku2[i2][:], rhs=v_, start=True, stop=True),
                             reads=[('ku2', i2), ('vg', gi_)], writes=[('mlA', i2 % len(mlA))])
                        yield
                        P.op('dve', lambda e: e.scalar_tensor_tensor(out=G[hh][:], in0=G[hh][:], scalar=decB[:, hh, c:c + 1],
                                                                     in1=pG_, op0=ALU.mult, op1=ALU.add),
                             reads=[('G', hh), 'decB', ('mlA', i2 % len(mlA))], writes=[('G', hh)])
                        P.op('pool', lambda e: e.tensor_copy(out=Gb[hh][:], in_=G[hh][:]), reads=[('G', hh)], writes=[('Gb', hh)])
                        s_ = scm[i2]
                        sk = ('scm', i2)
                        P.op('act', lambda e: e.activation(out=junkm[:], in_=pO_[:, 0:128], func=AF.Square,
                                                           scale=128.0 ** -0.5, accum_out=s_[:, 0:1]),
                             reads=[('mlA', i2 % len(mlA))], writes=['junkm', sk])
                        yield
                        P.op('dve', lambda e: e.tensor_scalar(out=s_[:, 6:7], in0=pO_[:, 128:129], scalar1=-1.0,
                                                              scalar2=flT[:, hh, c:c + 1], op0=ALU.mult, op1=ALU.max),
                             reads=[('mlA', i2 % len(mlA)), 'flT'], writes=[sk])
                        P.op('dve', lambda e: e.tensor_tensor(out=s_[:, 1:2], in0=s_[:, 6:7], in1=pO_[:, 128:129], op=ALU.max),
                             reads=[('mlA', i2 % len(mlA)), sk], writes=[sk])
                        P.op('dve', lambda e: e.tensor_tensor(out=s_[:, 2:3], in0=s_[:, 1:2], in1=s_[:, 1:2], op=ALU.mult),
                             reads=[sk], writes=[sk])
                        P.op('dve', lambda e: e.scalar_tensor_tensor(out=s_[:, 3:4], in0=s_[:, 2:3], scalar=EPS, in1=s_[:, 0:1],
                                                                     op0=ALU.mult, op1=ALU.add), reads=[sk], writes=[sk])
                        yield
                        if os.environ.get("USE_SQRT"):
                            P.op('act', lambda e: e.activation(out=s_[:, 4:5], in_=s_[:, 3:4], func=AF.Sqrt), reads=[sk], writes=[sk])
                            P.op('dve', lambda e: e.reciprocal(out=s_[:, 5:6], in_=s_[:, 4:5]), reads=[sk], writes=[sk])
                        else:
                            P.op('act', lambda e: e.activation(out=s_[:, 4:5], in_=s_[:, 3:4], func=AF.Ln), reads=[sk], writes=[sk])
                            P.op('act', lambda e: e.activation(out=s_[:, 5:6], in_=s_[:, 4:5], func=AF.Exp, scale=-0.5), reads=[sk], writes=[sk])
                        P.op('dve', lambda e: e.scalar_tensor_tensor(out=ym[gi_][:, cl, hh * 128:(hh + 1) * 128],
                                                                     in0=pO_[:, 0:128], scalar=s_[:, 5:6],
                                                                     in1=ogg[gi_][:, cl, hh * 128:(hh + 1) * 128],
                                                                     op0=ALU.mult, op1=ALU.mult),
                             reads=[('mlA', i2 % len(mlA)), sk, ('ogg', gi_)], writes=[('ym', gi_)])
                        yield
                P.dma('sp', self.mix_d[g * TG:(g + 1) * TG, 0:512].rearrange("(c s) e -> s c e", s=64), ym[gi_][:],
                      reads=[('ym', gi_)], writes=[('mixm', g)])
        ML_YIELDS = NCH * 4 * 6 + 1
        g_ml = gen_ml()
        if os.environ.get("NO_NSA"):
            for _ in g_ml:
                pass
        else:
            self._phase2_nsa_pipe(l, pSr, pMi, identb, zl, zr, bgen=g_ml, bg_yields=ML_YIELDS)
        self.barrier()


Builder.phase2b = _phase2b
```

```python
import math
import os
from contextlib import ExitStack

import numpy as np
import ml_dtypes

import concourse.bass as bass
import concourse.mybir as mybir
from concourse.bass_utils import run_bass_kernel_spmd

F32 = mybir.dt.float32
BF16 = mybir.dt.bfloat16
AF = mybir.ActivationFunctionType
ALU = mybir.AluOpType
AX = mybir.AxisListType

D_MODEL = 1024
D_IN = 3476
D_FF = 4096
EPS = 1e-6
NEGM = -30000.0
SAME_ENGINE_SYNC = True


class Prog:
    def __init__(self, nc, es, n_dma=24):
        self.nc = nc
        self.es = es
        self.eng = dict(pe=nc.tensor, act=nc.scalar, dve=nc.vector, pool=nc.gpsimd, sp=nc.sync)
        self.esem = {e: es.enter_context(nc.semaphore("s_" + e)) for e in self.eng}
        self.ecnt = {e: 0 for e in self.eng}
        self.dsem = [es.enter_context(nc.semaphore("d_%d" % i)) for i in range(n_dma)]
        self.dval = [0] * n_dma
        self.dnext = 0
        self.seen = {e: {} for e in self.eng}
        self.lastw = {}
        self.readers = {}
        self.nops = 0

    def _wait(self, eng, ev):
        kind, name, val = ev
        if kind == 'e' and name == eng:
            if eng == 'pe' or eng == 'sp' or not SAME_ENGINE_SYNC:
                return
        key = (kind, name)
        if self.seen[eng].get(key, 0) >= val:
            return
        self.seen[eng][key] = val
        sem = self.esem[name] if kind == 'e' else self.dsem[name]
        self.eng[eng].wait_ge(sem, val)

    def _deps(self, reads, writes):
        evs = []
        for k in reads:
            w = self.lastw.get(k)
            if w is not None:
                evs.append(w)
        for k in writes:
            w = self.lastw.get(k)
            if w is not None:
                evs.append(w)
            evs.extend(self.readers.get(k, ()))
        return evs

    def _commit(self, me, reads, writes):
        for k in reads:
            lst = self.readers.setdefault(k, [])
            lst[:] = [r for r in lst if (r[0], r[1]) != (me[0], me[1])]
            lst.append(me)
        for k in writes:
            self.lastw[k] = me
            self.readers[k] = []

    def op(self, eng, fn, reads=(), writes=()):
        for ev in self._deps(reads, writes):
            self._wait(eng, ev)
        ins = fn(self.eng[eng])
        self.ecnt[eng] += 1
        ins.then_inc(self.esem[eng], 1)
        self._commit(('e', eng, self.ecnt[eng]), reads, writes)
        self.nops += 1
        return ins

    def dma(self, q, out, in_, reads=(), writes=(), **kw):
        if q == 'auto':
            qs = os.environ.get("DMAQ", "sp").split(",")
            self.rr = getattr(self, 'rr', 0) + 1
            q = qs[self.rr % len(qs)]
        slot = self.dnext
        self.dnext = (self.dnext + 1) % len(self.dsem)
        evs = self._deps(reads, writes)
        if self.dval[slot] > 0:
            evs.append(('d', slot, self.dval[slot]))
        for ev in evs:
            self._wait(q, ev)
        self.dval[slot] += 16
        self.eng[q].dma_start(out=out, in_=in_, **kw).then_inc(self.dsem[slot], 16)
        self._commit(('d', slot, self.dval[slot]), reads, writes)
        self.nops += 1

    def finish(self):
        for slot in range(len(self.dsem)):
            if self.dval[slot] > 0:
                self._wait('sp', ('d', slot, self.dval[slot]))
        for e in self.eng:
            if e != 'sp' and self.ecnt[e] > 0:
                self._wait('sp', ('e', e, self.ecnt[e]))


class Builder:
    def __init__(self, S, depth, dbg=None):
        self.S = S
        self.depth = depth
        self.dbg = dbg or []
        self.nc = bass.Bass("TRN2", target_bir_lowering=False)
        self.es = ExitStack()
        self.P = Prog(self.nc, self.es)
        self.uid = 0

    def dram_in(self, name, shape, dt=F32):
        return self.nc.dram_tensor(name, list(shape), dt, kind="ExternalInput").ap()

    def dram_out(self, name, shape, dt=F32):
        return self.nc.dram_tensor(name, list(shape), dt, kind="ExternalOutput").ap()

    def dram_tmp(self, name, shape, dt):
        return self.nc.dram_tensor(name, list(shape), dt, kind="Internal").ap()

    def sb(self, st, name, shape, dt):
        self.uid += 1
        return st.enter_context(self.nc.sbuf_tensor("%s_%d" % (name, self.uid), list(shape), dt))

    def ps(self, st, name, shape, dt=F32):
        self.uid += 1
        return st.enter_context(self.nc.psum_tensor("%s_%d" % (name, self.uid), list(shape), dt))

    def dma_mid(self, out, in_, n_mid, step, reads=(), writes=()):
        for a in range(0, n_mid, step):
            b_ = min(n_mid, a + step)
            self.P.dma('sp', out[:, a:b_, :], in_[:, a:b_, :], reads=reads, writes=writes)

    def barrier(self):
        P = self.P
        for e in P.eng:
            for o in P.eng:
                if P.ecnt[o] > 0 and not (o == e and e in ('pe', 'sp')):
                    key = ('e', o)
                    if P.seen[e].get(key, 0) < P.ecnt[o]:
                        P.seen[e][key] = P.ecnt[o]
                        P.eng[e].wait_ge(P.esem[o], P.ecnt[o])
            for slot in range(len(P.dsem)):
                if P.dval[slot] > 0:
                    P._wait(e, ('d', slot, P.dval[slot]))

    def declare(self):
        S, L = self.S, self.depth
        self.x_in = self.dram_in("x", [S, D_MODEL])
        self.w_in = self.dram_in("w_in", [L, D_MODEL, D_IN])
        self.b_if = self.dram_in("b_if", [L, 8])
        self.conv_qk = self.dram_in("conv_qk", [L, 4, 1024])
        self.m_norm = self.dram_in("m_norm", [L, 512])
        self.moba_qk_norm = self.dram_in("moba_qk_norm", [L, 2, 64])
        self.nsa_q_norm = self.dram_in("nsa_q_norm", [L, 64])
        self.nsa_k_norm = self.dram_in("nsa_k_norm", [L, 3, 64])
        self.cmp_pe = self.dram_in("cmp_pe", [L, 2, 32, 64])
        self.cmp_w1 = self.dram_in("cmp_w1", [L, 2, 2048, 128])
        self.cmp_w2 = self.dram_in("cmp_w2", [L, 2, 128, 64])
        self.w_out = self.dram_in("w_out", [L, 1024, 1024])
        self.norm_mix = self.dram_in("norm_mix", [L, 1024])
        self.norm_ffn = self.dram_in("norm_ffn", [L, 1024])
        self.w_ff1 = self.dram_in("w_ff1", [L, 1024, D_FF])
        self.w_ff2 = self.dram_in("w_ff2", [L, D_FF, 1024])
        self.c_identb = self.dram_in("c_identb", [128, 128], BF16)
        self.c_identf = self.dram_in("c_identf", [128, 128], F32)
        self.c_cos = self.dram_in("c_cos", [S, 8], F32)
        self.c_sin = self.dram_in("c_sin", [S, 8], F32)
        self.c_ut = self.dram_in("c_ut", [128, 128], F32)
        self.c_tri = self.dram_in("c_tri", [128, 128], F32)
        self.c_ind32 = self.dram_in("c_ind32", [32, S], BF16)
        self.c_ind64 = self.dram_in("c_ind64", [64, S], BF16)
        self.c_tm4 = self.dram_in("c_tm4", [128, 4, 512], BF16)
        self.c_trib = self.dram_in("c_trib", [128, 128], BF16)
        self.c_triw = self.dram_in("c_triw", [128, 128], BF16)
        self.c_cmask = self.dram_in("c_cmask", [128, 17, 128], BF16)
        self.c_ovl = self.dram_in("c_ovl", [512, 128], BF16)
        self.c_cols = self.dram_in("c_cols", [128, 4], F32)
        self.y_out = self.dram_out("y", [S, D_MODEL])
        dbg = self.dbg

        def tmp(name, shape, dt):
            if name in dbg:
                return self.dram_out(name, shape, dt)
            return self.dram_tmp(name, shape, dt)
        self.xbuf = tmp("xbuf", [S, D_MODEL], F32)
        self.qkT = tmp("qkT", [1024, S], BF16)
        self.kcvcT = tmp("kcvcT", [128, S], BF16)
        self.gi_d = tmp("gi_d", [4, S], F32)
        self.gf_d = tmp("gf_d", [4, S], F32)
        self.mv_d = tmp("mv_d", [S, 512], BF16)
        self.og_d = tmp("og_d", [S, 512], BF16)
        self.bqT = tmp("bqT", [256, S], BF16)
        self.bkT = tmp("bkT", [256, S], BF16)
        self.bv_d = tmp("bv_d", [S, 256], BF16)
        self.nqrT = tmp("nqrT", [256, S], BF16)
        self.nquT = tmp("nquT", [256, S], BF16)
        self.ksT = tmp("ksT", [64, S], BF16)
        self.kwT = tmp("kwT", [64, S], BF16)
        self.vs_d = tmp("vs_d", [S, 64], BF16)
        self.vw_d = tmp("vw_d", [S, 64], BF16)
        self.ng_d = tmp("ng_d", [S, 12], F32)
        self.mix_d = tmp("mix_d", [S, 1024], BF16)

    def phase1(self, l, xsrc):
        S, P, nc = self.S, self.P, self.nc
        NB = S // 512
        with ExitStack() as st:
            sb, ps = self.sb, self.ps
            w_bf = sb(st, "w_in", [128, 8, D_IN], BF16)
            with ExitStack() as st2:
                stage = [sb(st2, "wst", [128, D_IN], F32) for _ in range(2)]
                for kt in range(8):
                    s_ = stage[kt % 2]
                    P.dma('sp', s_[:], self.w_in[l, kt * 128:(kt + 1) * 128, :], writes=[('wst', kt % 2)])
                    P.op(['pool', 'dve'][kt % 2], lambda e: e.tensor_copy(out=w_bf[:, kt, :], in_=s_[:]),
                         reads=[('wst', kt % 2)], writes=[('w_in', kt)])
                self.barrier()
            identb = sb(st, "identb", [128, 128], BF16)
            gmix = sb(st, "gmix", [128, 1024], F32)
            mnorm = sb(st, "mnorm", [128, 512], F32)
            gain20 = sb(st, "gain20", [128, 20, 64], F32)
            convw = sb(st, "convw", [128, 4, 8], F32)
            bi = sb(st, "bi", [4, 1], F32)
            nbf = sb(st, "nbf", [4, 1], F32)
            cosT = sb(st, "cosT", [128, S // 128, 8], F32)
            sinT = sb(st, "sinT", [128, S // 128, 8], F32)
            P.dma('sp', identb[:], self.c_identb, writes=['identb'])
            P.dma('sp', gmix[:], self.norm_mix[l].partition_broadcast(128), writes=['gmix'])
            P.dma('sp', mnorm[:], self.m_norm[l].partition_broadcast(128), writes=['mnorm'])
            P.op('pool', lambda e: e.memset(gain20[:], 1.0), writes=['gain20'])
            for i in range(20):
                src = None
                if i < 4:
                    src = self.moba_qk_norm[l, 0]
                elif i < 8:
                    src = self.moba_qk_norm[l, 1]
                elif 12 <= i < 16:
                    src = self.nsa_q_norm[l]
                elif i == 16:
                    src = self.nsa_k_norm[l, 1]
                elif i == 18:
                    src = self.nsa_k_norm[l, 2]
                if src is not None:
                    P.dma('sp', gain20[:, i, :], src.partition_broadcast(128), writes=['gain20'])
            with nc.allow_non_contiguous_dma(reason="tiny conv weight transpose"):
                for jj in range(4):
                    P.dma('sp', convw[:, jj, :], self.conv_qk[l, jj].rearrange("(ft p) -> p ft", p=128), writes=['convw'])
                P.dma('sp', bi[:], self.b_if[l, 0:4].rearrange("(p o) -> p o", o=1), writes=['bi'])
                P.dma('sp', nbf[:], self.b_if[l, 4:8].rearrange("(p o) -> p o", o=1), writes=['nbf'])
            P.op('dve', lambda e: e.tensor_scalar(out=nbf[:], in0=nbf[:], scalar1=-1.0, scalar2=None, op0=ALU.mult),
                 reads=['nbf'], writes=['nbf'])
            self.dma_mid(cosT[:], self.c_cos.rearrange("(j p) r -> p j r", p=128), S // 128, 8, writes=['cos'])
            self.dma_mid(sinT[:], self.c_sin.rearrange("(j p) r -> p j r", p=128), S // 128, 8, writes=['sin'])

            mh20 = sb(st, "mh20", [128, 20], F32)
            P.op('pool', lambda e: e.memset(mh20[:], -0.5), writes=['mh20'])
            xt = [sb(st, "xt", [128, 4, 1024], F32) for _ in range(2)]
            junk2_2 = [sb(st, "junk2", [128, 1280], BF16) for _ in range(2)]
            ss_2 = [sb(st, "ss", [128, 4], F32) for _ in range(2)]
            rstd_2 = [sb(st, "rstd", [128, 4], F32) for _ in range(2)]
            h_2 = [sb(st, "h", [128, 4, 1024], BF16)] * 2
            hT_2 = [sb(st, "hT", [128, 8, 512], BF16) for _ in range(2)]
            cbuf = [sb(st, "cbuf", [128, 515], F32) for _ in range(8)]
            for ft in range(8):
                P.op('pool', lambda e: e.memset(cbuf[ft][:, 0:3], 0.0), writes=[('cbuf', ft)])
            acc_2 = [sb(st, "acc", [128, 512], F32) for _ in range(2)]
            fo = [sb(st, "fo", [128, 512], BF16) for _ in range(2)]
            gsb = sb(st, "gsb", [4, 512], F32)
            gsb2 = sb(st, "gsb2", [4, 512], F32)
            mvb_2 = [sb(st, "mvb", [128, 512], BF16) for _ in range(2)]
            sg_2 = [sb(st, "sg", [128, 512], F32) for _ in range(2)]
            ogb_2 = [sb(st, "ogb", [128, 512], BF16) for _ in range(2)]
            tm_2 = [sb(st, "tm", [128, 1292], F32) for _ in range(2)]
            ssh_2 = [sb(st, "ssh", [128, 20], F32) for _ in range(2)]
            rs20_2 = [sb(st, "rs20", [128, 20], F32) for _ in range(2)]
            nrm_2 = [sb(st, "nrm", [128, 20, 64], F32) for _ in range(2)]
            nqu_2 = [sb(st, "nqu", [128, 256], BF16) for _ in range(2)]
            rt_2 = [[sb(st, "rt", [128, 20, 8], F32) for _ in range(4)] for _ in range(2)]
            nb_2 = [sb(st, "nb", [128, 20, 64], BF16) for _ in range(2)]
            tmb_2 = [sb(st, "tmb", [128, 1280], BF16) for _ in range(2)]
            tTa_2 = [sb(st, "tTa", [128, 8, 128], BF16) for _ in range(2)]
            tTb_2 = [sb(st, "tTb", [128, 2, 128], BF16) for _ in range(2)]
            ngs_2 = [sb(st, "ngs", [128, 12], F32) for _ in range(2)]
            pT_2 = [ps(st, "pT", [128, 8, 128], BF16) for _ in range(2)]
            pT = pT_2[0]
            pTb = ps(st, "pTb", [128, 2, 128], BF16)
            pf = [ps(st, "pf", [128, 512], F32) for _ in range(2)]
            pg = [ps(st, "pg", [4, 512], F32) for _ in range(1)]
            pt = [ps(st, "pt", [128, 512], F32) for _ in range(2)]
            pfi = 0
            pti = 0

            def load_x(b):
                P.dma('sp', xt[b % 2][:], xsrc[b * 512:(b + 1) * 512, :].rearrange("(j p) d -> p j d", p=128),
                      reads=[('xres', 2 * b), ('xres', 2 * b + 1)], writes=[('xt', b % 2)])
            load_x(0)
            pend_chain = []
            for b in range(NB):
                t0 = b * 512
                if b + 1 < NB:
                    load_x(b + 1)
                def pro_norm(bb):
                    z = bb % 2
                    x_ = xt[z]
                    kx = ('xt', z)
                    ss, rstd, h = ss_2[z], rstd_2[z], h_2[z]
                    for j in range(4):
                        P.op('act', lambda e: e.activation(out=junk2_2[0][:, 0:1024], in_=x_[:, j, :], func=AF.Square,
                                                           scale=1.0 / 32, accum_out=ss[:, j:j + 1]),
                             reads=[kx], writes=[('junk2', 0), ('ss', z)])
                    P.op('dve', lambda e: e.tensor_scalar(out=ss[:], in0=ss[:], scalar1=EPS, scalar2=None, op0=ALU.add),
                         reads=[('ss', z)], writes=[('ss', z)])
                    P.op('pool', lambda e: e.tensor_tensor(out=rstd[:], in0=ss[:], in1=mh20[:, 0:4], op=ALU.pow),
                         reads=[('ss', z), 'mh20'], writes=[('rstd', z)])
                    for j in range(4):
                        P.op('dve', lambda e: e.scalar_tensor_tensor(out=h[:, j, :], in0=x_[:, j, :], scalar=rstd[:, j:j + 1],
                                                                     in1=gmix[:], op0=ALU.mult, op1=ALU.mult),
                             reads=[kx, ('rstd', z), 'gmix'], writes=[('h', j)])

                def pro_tr(bb):
                    z = bb % 2
                    h, hT_ = h_2[z], hT_2[z]
                    for j in range(4):
                        pT = pT_2[j % 2]
                        for kt in range(8):
                            P.op('pe', lambda e: e.transpose(out=pT[:, kt, :], in_=h[:, j, kt * 128:(kt + 1) * 128],
                                                             identity=identb[:]),
                                 reads=[('h', j), 'identb'], writes=[('pT', j % 2)])
                        P.op(['act', 'dve'][j % 2], lambda e: (e.copy if j % 2 == 0 else e.tensor_copy)(out=hT_[:, :, j * 128:(j + 1) * 128], in_=pT[:]),
                             reads=[('pT', j % 2)], writes=[('hT', z, j)])
                if b == 0:
                    pro_norm(0)
                    pro_tr(0)
                hT = hT_2[b % 2]
                hk = [('hT', b % 2, j) for j in range(4)]
                wk = [('w_in', kt) for kt in range(8)]
                def ft_s1(ft):
                    nonlocal pfi
                    c0 = ft * 128 if ft < 8 else 3080
                    p_ = pf[pfi % 2]
                    pk = ('pf', pfi % 2)
                    pfi += 1
                    for kt in range(8):
                        P.op('pe', lambda e: e.matmul(p_[:], lhsT=w_bf[:, kt, c0:c0 + 128], rhs=hT[:, kt, :],
                                                      start=(kt == 0), stop=(kt == 7)),
                             reads=hk + [wk[kt]], writes=[pk])
                    f_ = fo[ft % 2]
                    fk = ('fo', ft % 2)
                    if ft < 8:
                        cb = cbuf[ft]
                        ck = ('cbuf', ft)
                        P.op('act', lambda e: e.copy(out=cb[:, 3:515], in_=p_[:]), reads=[pk], writes=[ck])
                    else:
                        P.op('act', lambda e: e.copy(out=f_[:], in_=p_[:]), reads=[pk], writes=[fk])
                        P.dma('auto', self.kcvcT[:, t0:t0 + 512], f_[:], reads=[fk], writes=[('kcvcT', b)])

                def ft_s2(ft):
                    if ft >= 8:
                        return
                    f_ = fo[ft % 2]
                    fk = ('fo', ft % 2)
                    cb = cbuf[ft]
                    ck = ('cbuf', ft)
                    acc = acc_2[ft % 2]
                    ak = ('acc', ft % 2)
                    P.op('dve', lambda e: e.tensor_scalar(out=acc[:], in0=cb[:, 0:512], scalar1=convw[:, 0, ft:ft + 1],
                                                          scalar2=None, op0=ALU.mult),
                         reads=[ck, 'convw'], writes=[ak])
                    for jj in range(1, 4):
                        P.op('dve', lambda e: e.scalar_tensor_tensor(out=acc[:], in0=cb[:, jj:jj + 512],
                                                                     scalar=convw[:, jj, ft:ft + 1], in1=acc[:],
                                                                     op0=ALU.mult, op1=ALU.add),
                             reads=[ck, 'convw', ak], writes=[ak])
                    P.op('act', lambda e: e.activation(out=f_[:], in_=acc[:], func=AF.Silu), reads=[ak], writes=[fk])
                    P.dma('auto', self.qkT[ft * 128:(ft + 1) * 128, t0:t0 + 512], f_[:], reads=[fk],
                          writes=[('qkT', ft, b)])
                    P.op('pool', lambda e: e.tensor_copy(out=cb[:, 0:3], in_=cb[:, 512:515]), reads=[ck], writes=[ck])
                ft_s1(0)
                for ft in range(9):
                    if ft + 1 < 9:
                        ft_s1(ft + 1)
                    ft_s2(ft)
                if b + 1 < NB:
                    pro_norm(b + 1)
                def gate_mm(gi_):
                    c0 = 2048 + 4 * gi_
                    for kt in range(8):
                        P.op('pe', lambda e: e.matmul(pg[0][:], lhsT=w_bf[:, kt, c0:c0 + 4], rhs=hT[:, kt, :],
                                                      start=(kt == 0), stop=(kt == 7)),
                             reads=hk + [wk[kt]], writes=[('pg', 0)])
                gate_mm(0)
                P.op('act', lambda e: e.activation(out=gsb[:], in_=pg[0][:], func=AF.Identity, bias=bi[:, 0:1]),
                     reads=[('pg', 0), 'bi'], writes=['gsb'])
                P.dma('auto', self.gi_d[:, t0:t0 + 512], gsb[:], reads=['gsb'], writes=[('gi_d', b)])
                gate_mm(1)
                P.op('act', lambda e: e.activation(out=gsb2[:], in_=pg[0][:], func=AF.Exp, bias=nbf[:, 0:1], scale=-1.0),
                     reads=[('pg', 0), 'nbf'], writes=['gsb2'])
                P.op('act', lambda e: e.activation(out=gsb2[:], in_=gsb2[:], func=AF.Ln, bias=1.0),
                     reads=['gsb2'], writes=['gsb2'])
                P.op('dve', lambda e: e.tensor_scalar(out=gsb2[:], in0=gsb2[:], scalar1=-1.0, scalar2=None, op0=ALU.mult),
                     reads=['gsb2'], writes=['gsb2'])
                P.dma('auto', self.gf_d[:, t0:t0 + 512], gsb2[:], reads=['gsb2'], writes=[('gf_d', b)])
                for j in range(4):
                    tok = slice(t0 + j * 128, t0 + (j + 1) * 128)
                    jg = b * 4 + j
                    z_ = jg % 2
                    mvb, sg, ogb, tm, ssh, rs20, nrm, nqu, nb, tmb, tTa, tTb, ngs = (
                        mvb_2[z_], sg_2[z_], ogb_2[z_], tm_2[z_], ssh_2[z_], rs20_2[z_], nrm_2[z_], nqu_2[z_], nb_2[z_],
                        tmb_2[z_], tTa_2[z_], tTb_2[z_], ngs_2[z_])
                    rt = rt_2[z_]
                    pT = pT_2[z_]
                    junk2 = junk2_2[z_]

                    def tm_mm(c0, n):
                        nonlocal pti
                        p_ = pt[pti % 2]
                        pk = ('pt', pti % 2)
                        pti += 1
                        for kt in range(8):
                            P.op('pe', lambda e: e.matmul(p_[:, 0:n], lhsT=hT[:, kt, j * 128:(j + 1) * 128],
                                                          rhs=w_bf[:, kt, c0:c0 + n], start=(kt == 0), stop=(kt == 7)),
                                 reads=[('hT', b % 2, j), wk[kt]], writes=[pk])
                        if pend_chain:
                            for _ in range(5):
                                try:
                                    next(pend_chain[0])
                                except StopIteration:
                                    pend_chain.pop(0)
                                    break
                        return p_, pk
                    p_, pk = tm_mm(1024, 512)
                    P.op('act', lambda e: e.copy(out=mvb[:], in_=p_[:]), reads=[pk], writes=[('mvb', z_)])
                    P.dma('auto', self.mv_d[tok, :], mvb[:], reads=[('mvb', z_)], writes=[('mv_d', jg)])
                    p_, pk = tm_mm(1536, 512)
                    P.op('act', lambda e: e.activation(out=sg[:], in_=p_[:], func=AF.Sigmoid), reads=[pk], writes=[('sg', z_)])
                    P.op('dve', lambda e: e.tensor_tensor(out=ogb[:], in0=sg[:], in1=mnorm[:], op=ALU.mult),
                         reads=[('sg', z_), 'mnorm'], writes=[('ogb', z_)])
                    P.dma('auto', self.og_d[tok, :], ogb[:], reads=[('ogb', z_)], writes=[('og_d', jg)])
                    p_, pk = tm_mm(2056, 512)
                    P.op('dve', lambda e: e.tensor_copy(out=tm[:, 0:512], in_=p_[:]), reads=[pk], writes=[('tm', z_)])
                    p_, pk = tm_mm(2568, 512)
                    P.op('act', lambda e: e.copy(out=tm[:, 512:1024], in_=p_[:]), reads=[pk], writes=[('tm', z_)])
                    p_, pk = tm_mm(3208, 268)
                    P.op('dve', lambda e: e.tensor_copy(out=tm[:, 1024:1292], in_=p_[:, 0:268]), reads=[pk], writes=[('tm', z_)])
                    def chain(tok=tok, jg=jg, z_=z_, tm=tm, ssh=ssh, rs20=rs20, nrm=nrm, nqu=nqu, nb=nb, tmb=tmb, tTa=tTa,
                              tTb=tTb, ngs=ngs, rt=rt, junk2=junk2, pT=pT):
                        yield
                        P.op('act', lambda e: e.activation(out=junk2[:, 0:1280], in_=tm[:, 0:1280], func=AF.Square),
                             reads=[('tm', z_)], writes=[('junk2', z_)])
                        yield
                        P.op('dve', lambda e: e.tensor_reduce(out=ssh[:], in_=junk2[:, 0:1280].rearrange("p (h d) -> p h d", d=64),
                                                              axis=AX.X, op=ALU.add),
                             reads=[('junk2', z_)], writes=[('ssh', z_)])
                        yield
                        P.op('dve', lambda e: e.tensor_scalar(out=ssh[:], in0=ssh[:], scalar1=1.0 / 64, scalar2=EPS,
                                                              op0=ALU.mult, op1=ALU.add), reads=[('ssh', z_)], writes=[('ssh', z_)])
                        yield
                        P.op('pool', lambda e: e.tensor_tensor(out=rs20[:], in0=ssh[:], in1=mh20[:], op=ALU.pow),
                             reads=[('ssh', z_), 'mh20'], writes=[('rs20', z_)])
                        yield
                        P.op('dve', lambda e: e.tensor_tensor(out=nrm[:], in0=tm[:, 0:1280].rearrange("p (h d) -> p h d", d=64),
                                                              in1=rs20[:].unsqueeze(2).broadcast_to([128, 20, 64]), op=ALU.mult),
                             reads=[('tm', z_), ('rs20', z_)], writes=[('nrm', z_)])
                        yield
                        P.op('dve', lambda e: e.tensor_tensor(out=nrm[:], in0=nrm[:], in1=gain20[:], op=ALU.mult),
                             reads=[('nrm', z_), 'gain20'], writes=[('nrm', z_)])
                        yield
                        P.op('act', lambda e: e.copy(out=nqu[:].rearrange("p (h d) -> p h d", d=64), in_=nrm[:, 12:16, :]),
                             reads=[('nrm', z_)], writes=[('nqu', z_)])
                        cb_ = cosT[:, jg, :].unsqueeze(1).broadcast_to([128, 20, 8])
                        sb_ = sinT[:, jg, :].unsqueeze(1).broadcast_to([128, 20, 8])
                        x1 = nrm[:, :, 0:8]
                        x2 = nrm[:, :, 8:16]
                        yield
                        P.op('dve', lambda e: e.tensor_tensor(out=rt[0][:], in0=x1, in1=cb_, op=ALU.mult),
                             reads=[('nrm', z_), 'cos'], writes=[('rt', 0, z_)])
                        yield
                        P.op('dve', lambda e: e.tensor_tensor(out=rt[1][:], in0=x2, in1=sb_, op=ALU.mult),
                             reads=[('nrm', z_), 'sin'], writes=[('rt', 1, z_)])
                        yield
                        P.op('dve', lambda e: e.tensor_tensor(out=rt[2][:], in0=x2, in1=cb_, op=ALU.mult),
                             reads=[('nrm', z_), 'cos'], writes=[('rt', 2, z_)])
                        yield
                        P.op('dve', lambda e: e.tensor_tensor(out=rt[3][:], in0=x1, in1=sb_, op=ALU.mult),
                             reads=[('nrm', z_), 'sin'], writes=[('rt', 3, z_)])
                        yield
                        P.op('dve', lambda e: e.tensor_tensor(out=x1, in0=rt[0][:], in1=rt[1][:], op=ALU.subtract),
                             reads=[('rt', 0, z_), ('rt', 1, z_)], writes=[('nrm', z_)])
                        yield
                        P.op('dve', lambda e: e.tensor_tensor(out=x2, in0=rt[2][:], in1=rt[3][:], op=ALU.add),
                             reads=[('rt', 2, z_), ('rt', 3, z_)], writes=[('nrm', z_)])
                        yield
                        P.op('act', lambda e: e.copy(out=nb[:], in_=nrm[:]), reads=[('nrm', z_)], writes=[('nb', z_)])
                        yield
                        P.op('act', lambda e: e.copy(out=tmb[:], in_=tm[:, 0:1280]), reads=[('tm', z_)], writes=[('tmb', z_)])
                        yield
                        P.op('act', lambda e: e.activation(out=ngs[:], in_=tm[:, 1280:1292], func=AF.Sigmoid),
                             reads=[('tm', z_)], writes=[('ngs', z_)])
                        srcs = [nb[:, 0:2, :], nb[:, 2:4, :], nb[:, 4:6, :], nb[:, 6:8, :], nb[:, 12:14, :], nb[:, 14:16, :]]
                        yield
                        for i, s_ in enumerate(srcs):
                            P.op('pe', lambda e: e.transpose(out=pT[:, i, :], in_=s_.rearrange("p h d -> p (h d)"),
                                                             identity=identb[:]), reads=[('nb', z_), 'identb'], writes=[('pT', z_)])
                        yield
                        for i in range(2):
                            P.op('pe', lambda e: e.transpose(out=pT[:, 6 + i, :], in_=nqu[:, i * 128:(i + 1) * 128],
                                                             identity=identb[:]), reads=[('nqu', z_), 'identb'], writes=[('pT', z_)])
                        yield
                        for i, hh in enumerate((16, 18)):
                            P.op('pe', lambda e: e.transpose(out=pTb[:, i, :], in_=nb[:, hh:hh + 2, :].rearrange("p h d -> p (h d)"),
                                                             identity=identb[:]), reads=[('nb', z_), 'identb'], writes=['pTb'])
                        yield
                        P.op('dve', lambda e: e.tensor_copy(out=tTa[:], in_=pT[:]), reads=[('pT', z_)], writes=[('tTa', z_)])
                        yield
                        P.op('act', lambda e: e.copy(out=tTb[:], in_=pTb[:]), reads=['pTb'], writes=[('tTb', z_)])
                        yield
                        for i, dst in enumerate((self.bqT, self.bkT, self.nqrT, self.nquT)):
                            P.dma('auto', dst[:, tok].rearrange("(a p) t -> p a t", p=128), tTa[:, 2 * i:2 * i + 2, :],
                                  reads=[('tTa', z_)], writes=[(("bqT","bkT","nqrT","nquT")[i], jg)])
                        yield
                        P.dma('auto', self.ksT[:, tok], tTb[0:64, 0, :], reads=[('tTb', z_)], writes=[('ksT', jg)])
                        yield
                        P.dma('auto', self.kwT[:, tok], tTb[0:64, 1, :], reads=[('tTb', z_)], writes=[('kwT', jg)])
                        yield
                        P.dma('auto', self.bv_d[tok, :], tmb[:, 512:768], reads=[('tmb', z_)], writes=[('bv_d', jg)])
                        yield
                        P.dma('auto', self.vs_d[tok, :], tmb[:, 1088:1152], reads=[('tmb', z_)], writes=[('vs_d', jg)])
                        yield
                        P.dma('auto', self.vw_d[tok, :], tmb[:, 1216:1280], reads=[('tmb', z_)], writes=[('vw_d', jg)])
                        yield
                        P.dma('auto', self.ng_d[tok, :], ngs[:], reads=[('ngs', z_)], writes=[('ng_d', jg)])
                    for _ in (pend_chain.pop(0) if pend_chain else ()):
                        pass
                    pend_chain.append(chain())
                    if j == 1 and b + 1 < NB:
                        pro_tr(b + 1)
            while pend_chain:
                for _ in pend_chain.pop(0):
                    pass
            self.barrier()

    def phase3(self, l, xsrc, xdst):
        S, P, nc = self.S, self.P, self.nc
        NB = S // 256
        with ExitStack() as st:
            sb, ps = self.sb, self.ps
            wo = sb(st, "wo", [128, 8, 1024], BF16)
            w1 = sb(st, "w1", [128, 8, 4096], BF16)
            w2 = sb(st, "w2", [128, 32, 1024], BF16)
            with ExitStack() as st2:
                stage = [sb(st2, "wst3", [128, 4096], F32) for _ in range(2)]
                slabs = []
                for i in range(2):
                    slabs.append((self.w_out[l, i * 512:(i + 1) * 512, :].rearrange("(a p) d -> p a d", p=128),
                                  wo[:, i * 4:(i + 1) * 4, :], ('wo', i), True))
                for kt in range(8):
                    slabs.append((self.w_ff1[l, kt * 128:(kt + 1) * 128, :], w1[:, kt, :], ('w1', kt), False))
                for i in range(8):
                    slabs.append((self.w_ff2[l, i * 512:(i + 1) * 512, :].rearrange("(a p) d -> p a d", p=128),
                                  w2[:, i * 4:(i + 1) * 4, :], ('w2', i), True))
                for n, (src, dst, key, three) in enumerate(slabs):
                    s_ = stage[n % 2]
                    sv = s_[:].rearrange("p (a d) -> p a d", a=4) if three else s_[:]
                    P.dma('sp', sv, src, writes=[('wst3', n % 2)])
                    eng = ['pool', 'dve', 'act'][n % 3]
                    if eng == 'act':
                        P.op(eng, lambda e: e.copy(out=dst, in_=sv), reads=[('wst3', n % 2)], writes=[key])
                    else:
                        P.op(eng, lambda e: e.tensor_copy(out=dst, in_=sv), reads=[('wst3', n % 2)], writes=[key])
                self.barrier()
            identb = sb(st, "identb3", [128, 128], BF16)
            gffn = sb(st, "gffn", [128, 1024], F32)
            P.dma('sp', identb[:], self.c_identb, writes=['identb'])
            P.dma('sp', gffn[:], self.norm_ffn[l].partition_broadcast(128), writes=['gffn'])
            xt_2 = [sb(st, "xt3", [128, 2, 1024], F32) for _ in range(2)]
            mixb_2 = [sb(st, "mixb", [128, 2, 1024], BF16) for _ in range(2)]
            mh3 = sb(st, "mh3", [128, 2], F32)
            P.op('pool', lambda e: e.memset(mh3[:], -0.5), writes=['mh3'])
            mT = sb(st, "mT", [128, 8, 256], BF16)
            hT = sb(st, "hT3", [128, 8, 256], BF16)
            uT = sb(st, "uT", [128, 32, 256], BF16)
            rr = [sb(st, "rr", [128, 256], F32) for _ in range(2)]
            junk = sb(st, "junk3", [128, 1024], BF16)
            ss = sb(st, "ss3", [128, 2], F32)
            rstd = sb(st, "rstd3", [128, 2], F32)
            pT = ps(st, "pT3", [128, 8, 128], BF16)
            py = [ps(st, "py", [128, 512], F32) for _ in range(2)]
            pz = [ps(st, "pz", [128, 256], F32) for _ in range(2)]
            pyi = 0
            pzi = 0
            wok = [('wo', 0), ('wo', 1)]
            def load3(b):
                t0_ = b * 256
                P.dma('sp', xt_2[b % 2][:], xsrc[t0_:t0_ + 256, :].rearrange("(j p) d -> p j d", p=128), reads=[('xres', b)],
                      writes=[('xt', b % 2)])
                P.dma('sp', mixb_2[b % 2][:], self.mix_d[t0_:t0_ + 256, :].rearrange("(j p) d -> p j d", p=128),
                      writes=[('mixb', b % 2)])
            load3(0)
            def frontA(b):
                nonlocal pyi, pzi
                t0 = b * 256
                xt = xt_2[b % 2]
                mixb = mixb_2[b % 2]
                xk = ('xt', b % 2)
                mk = ('mixb', b % 2)
                for j in range(2):
                    for kt in range(8):
                        P.op('pe', lambda e: e.transpose(out=pT[:, kt, :], in_=mixb[:, j, kt * 128:(kt + 1) * 128],
                                                         identity=identb[:]), reads=[mk, 'identb'], writes=['pT'])
                    P.op('act', lambda e: e.copy(out=mT[:, :, j * 128:(j + 1) * 128], in_=pT[:]),
                         reads=['pT'], writes=[('mT', j)])
                for j in range(2):
                    for hf in range(2):
                        p_ = py[pyi % 2]
                        pk = ('py', pyi % 2)
                        pyi += 1
                        for kt in range(8):
                            P.op('pe', lambda e: e.matmul(p_[:], lhsT=mT[:, kt, j * 128:(j + 1) * 128],
                                                          rhs=wo[:, kt, hf * 512:(hf + 1) * 512],
                                                          start=(kt == 0), stop=(kt == 7)),
                                 reads=[('mT', j), wok[kt // 4]], writes=[pk])
                        P.op('dve', lambda e: e.tensor_tensor(out=xt[:, j, hf * 512:(hf + 1) * 512],
                                                              in0=xt[:, j, hf * 512:(hf + 1) * 512], in1=p_[:], op=ALU.add),
                             reads=[xk, pk], writes=[xk])
                for j in range(2):
                    P.op('act', lambda e: e.activation(out=junk[:], in_=xt[:, j, :], func=AF.Square, scale=1.0 / 32,
                                                       accum_out=ss[:, j:j + 1]), reads=[xk], writes=['junk', 'ss'])
                P.op('dve', lambda e: e.tensor_scalar(out=ss[:], in0=ss[:], scalar1=EPS, scalar2=None, op0=ALU.add),
                     reads=['ss'], writes=['ss'])
                P.op('pool', lambda e: e.tensor_tensor(out=rstd[:], in0=ss[:], in1=mh3[:], op=ALU.pow), reads=['ss', 'mh3'], writes=['rstd'])
                for j in range(2):
                    P.op('dve', lambda e: e.scalar_tensor_tensor(out=mixb[:, j, :], in0=xt[:, j, :], scalar=rstd[:, j:j + 1],
                                                                 in1=gffn[:], op0=ALU.mult, op1=ALU.mult),
                         reads=[xk, 'rstd', 'gffn'], writes=[mk])

            def frontB(b):
                nonlocal pyi, pzi
                t0 = b * 256
                xt = xt_2[b % 2]
                mixb = mixb_2[b % 2]
                xk = ('xt', b % 2)
                mk = ('mixb', b % 2)
                for j in range(2):
                    for kt in range(8):
                        P.op('pe', lambda e: e.transpose(out=pT[:, kt, :], in_=mixb[:, j, kt * 128:(kt + 1) * 128],
                                                         identity=identb[:]), reads=[mk, 'identb'], writes=['pT'])
                    P.op('act', lambda e: e.copy(out=hT[:, :, j * 128:(j + 1) * 128], in_=pT[:]),
                         reads=['pT'], writes=[('hT', j)])

            def ffn1(b):
                nonlocal pyi, pzi
                t0 = b * 256
                xt = xt_2[b % 2]
                mixb = mixb_2[b % 2]
                xk = ('xt', b % 2)
                mk = ('mixb', b % 2)
                hk = [('hT', 0), ('hT', 1)]
                for ft in range(32):
                    p_ = pz[pzi % 2]
                    pk = ('pz', pzi % 2)
                    r_ = rr[pzi % 2]
                    rk = ('rr', pzi % 2)
                    pzi += 1
                    for kt in range(8):
                        P.op('pe', lambda e: e.matmul(p_[:], lhsT=w1[:, kt, ft * 128:(ft + 1) * 128], rhs=hT[:, kt, :],
                                                      start=(kt == 0), stop=(kt == 7)),
                             reads=hk + [('w1', kt)], writes=[pk])
                    P.op('act', lambda e: e.activation(out=r_[:], in_=p_[:], func=AF.Relu), reads=[pk], writes=[rk])
                    P.op('dve', lambda e: e.tensor_tensor(out=uT[:, ft, :], in0=r_[:], in1=r_[:], op=ALU.mult),
                         reads=[rk], writes=[('uT', ft)])

            def ffn2(b):
                nonlocal pyi, pzi
                t0 = b * 256
                xt = xt_2[b % 2]
                mixb = mixb_2[b % 2]
                xk = ('xt', b % 2)
                mk = ('mixb', b % 2)
                uk = [('uT', ft) for ft in range(32)]
                for j in range(2):
                    for hf in range(2):
                        p_ = py[pyi % 2]
                        pk = ('py', pyi % 2)
                        pyi += 1
                        for ft in range(32):
                            P.op('pe', lambda e: e.matmul(p_[:], lhsT=uT[:, ft, j * 128:(j + 1) * 128],
                                                          rhs=w2[:, ft, hf * 512:(hf + 1) * 512],
                                                          start=(ft == 0), stop=(ft == 31)),
                                 reads=[uk[ft], ('w2', ft // 4)], writes=[pk])
                        P.op('dve', lambda e: e.tensor_tensor(out=xt[:, j, hf * 512:(hf + 1) * 512],
                                                              in0=xt[:, j, hf * 512:(hf + 1) * 512], in1=p_[:], op=ALU.add),
                             reads=[xk, pk], writes=[xk])
                P.dma('sp', xdst[t0:t0 + 256, :].rearrange("(j p) d -> p j d", p=128), xt[:], reads=[xk],
                      writes=[('xres', b)])
            frontA(0)
            frontB(0)
            for b in range(NB):
                if b + 1 < NB:
                    load3(b + 1)
                ffn1(b)
                if b + 1 < NB:
                    frontA(b + 1)
                ffn2(b)
                if b + 1 < NB:
                    frontB(b + 1)
            self.barrier()

    def _ml_norm(self, P, s_, sk, pO_t, pok, junk, flT, mhalf, ym_t, ymk, og_t, ogk, hh, c, cl):
        P.op('act', lambda e: e.activation(out=junk[:], in_=pO_t[0:64, 0:128], func=AF.Square,
                                           scale=128.0 ** -0.5, accum_out=s_[:, 0:1]),
             reads=[pok], writes=['junk', sk])
        P.op('dve', lambda e: e.tensor_scalar(out=s_[:, 6:7], in0=pO_t[0:64, 128:129], scalar1=-1.0,
                                              scalar2=flT[:, hh, c:c + 1], op0=ALU.mult, op1=ALU.max),
             reads=[pok, 'flT'], writes=[sk])
        P.op('dve', lambda e: e.tensor_tensor(out=s_[:, 1:2], in0=s_[:, 6:7], in1=pO_t[0:64, 128:129], op=ALU.max),
             reads=[pok, sk], writes=[sk])
        P.op('dve', lambda e: e.tensor_tensor(out=s_[:, 2:3], in0=s_[:, 1:2], in1=s_[:, 1:2], op=ALU.mult),
             reads=[sk], writes=[sk])
        P.op('dve', lambda e: e.scalar_tensor_tensor(out=s_[:, 3:4], in0=s_[:, 2:3], scalar=EPS, in1=s_[:, 0:1],
                                                     op0=ALU.mult, op1=ALU.add), reads=[sk], writes=[sk])
        P.op('pool', lambda e: e.tensor_tensor(out=s_[:, 5:6], in0=s_[:, 3:4], in1=mhalf[:, 0:1], op=ALU.pow),
             reads=[sk, 'mhalf'], writes=[sk])
        P.op('dve', lambda e: e.scalar_tensor_tensor(out=ym_t[:, cl, hh * 128:(hh + 1) * 128],
                                                     in0=pO_t[0:64, 0:128], scalar=s_[:, 5:6],
                                                     in1=og_t[:, cl, hh * 128:(hh + 1) * 128],
                                                     op0=ALU.mult, op1=ALU.mult),
             reads=[pok, sk, ogk], writes=[ymk])

    def phase2_mlstm(self, l):
        S, P, nc = self.S, self.P, self.nc
        NCH = S // 64
        LNSC = math.log(128.0 ** -0.5)
        with ExitStack() as st:
            sb, ps = self.sb, self.ps
            identb = sb(st, "identbm", [128, 128], BF16)
            identf = sb(st, "identfm", [128, 128], F32)
            ut = sb(st, "ut", [128, 128], F32)
            tri = sb(st, "tri", [128, 128], F32)
            P.dma('sp', identb[:], self.c_identb, writes=['identb'])
            P.dma('sp', identf[:], self.c_identf, writes=['identf'])
            P.dma('sp', ut[:], self.c_ut, writes=['ut'])
            P.dma('sp', tri[:], self.c_tri, writes=['tri'])
            uT = sb(st, "uT_m", [64, 4, NCH], F32)
            u2T = sb(st, "u2T_m", [64, 4, NCH], F32)
            flT = sb(st, "flT_m", [64, 4, NCH], F32)
            decB = sb(st, "decB", [128, 4, NCH], F32)
            with ExitStack() as s2:
                li = sb(s2, "li", [NCH, 4, 64], F32)
                lf = sb(s2, "lf", [NCH, 4, 64], F32)
                ones = sb(s2, "ones", [NCH, 64], F32)
                Fin = sb(s2, "Fin", [NCH, 4, 64], F32)
                Ft = sb(s2, "Ft", [NCH, 4, 64], F32)
                a_ = sb(s2, "a_", [NCH, 4, 64], F32)
                Ain = sb(s2, "Ain", [NCH, 4, 64], F32)
                tot = sb(s2, "tot", [NCH, 4], F32)
                cmax = sb(s2, "cmax", [NCH, 4], F32)
                cmT = sb(s2, "cmT", [4, NCH], F32)
                ET = sb(s2, "ET", [4, NCH], F32)
                ETn = sb(s2, "ETn", [4, NCH], F32)
                Ec = sb(s2, "Ec", [NCH, 4], F32)
                Enc = sb(s2, "Enc", [NCH, 4], F32)
                tmp = sb(s2, "tmpm", [NCH, 4, 64], F32)
                uu = sb(s2, "uu", [NCH, 4, 64], F32)
                uu2 = sb(s2, "uu2", [NCH, 4, 64], F32)
                fl = sb(s2, "fl", [NCH, 4, 64], F32)
                dec = sb(s2, "dec", [NCH, 4], F32)
                decrep = sb(s2, "decrep", [NCH, 4, 128], F32)
                pa = ps(s2, "pa", [128, 512], F32)
                pb = ps(s2, "pb", [128, 512], F32)
                P.dma('sp', li[:], self.gi_d.rearrange("h (c j) -> c h j", j=64),
                      reads=[('gi_d', b) for b in range(S // 512)], writes=['li'])
                P.dma('sp', lf[:], self.gf_d.rearrange("h (c j) -> c h j", j=64),
                      reads=[('gf_d', b) for b in range(S // 512)], writes=['lf'])
                P.op('pool', lambda e: e.memset(ones[:], 1.0), writes=['ones'])
                for hh in range(4):
                    P.op('dve', lambda e: e.tensor_tensor_scan(out=Fin[:, hh, :], data0=ones[:], data1=lf[:, hh, :],
                                                               initial=0.0, op0=ALU.mult, op1=ALU.add),
                         reads=['ones', 'lf'], writes=['Fin'])
                P.op('dve', lambda e: e.tensor_copy(out=tot[:], in_=Fin[:, :, 63]), reads=['Fin'], writes=['tot'])
                P.op('pe', lambda e: e.matmul(pa[0:NCH, 0:4], lhsT=ut[0:NCH, 0:NCH], rhs=tot[:], start=True, stop=True),
                     reads=['ut', 'tot'], writes=['pa'])
                P.op('dve', lambda e: e.tensor_tensor(out=Ft[:], in0=Fin[:],
                                                      in1=pa[0:NCH, 0:4].unsqueeze(2).broadcast_to([NCH, 4, 64]), op=ALU.add),
                     reads=['Fin', 'pa'], writes=['Ft'])
                P.op('dve', lambda e: e.tensor_tensor(out=a_[:], in0=li[:], in1=Ft[:], op=ALU.subtract),
                     reads=['li', 'Ft'], writes=['a_'])
                for hh in range(4):
                    P.op('dve', lambda e: e.tensor_tensor_scan(out=Ain[:, hh, :], data0=a_[:, hh, :], data1=a_[:, hh, :],
                                                               initial=-1e30, op0=ALU.max, op1=ALU.max),
                         reads=['a_'], writes=['Ain'])
                P.op('dve', lambda e: e.tensor_copy(out=cmax[:], in_=Ain[:, :, 63]), reads=['Ain'], writes=['cmax'])
                P.op('pe', lambda e: e.transpose(out=pb[0:4, 0:NCH], in_=cmax[:], identity=identf[0:NCH, 0:NCH]),
                     reads=['cmax', 'identf'], writes=['pb'])
                P.op('dve', lambda e: e.tensor_copy(out=cmT[:], in_=pb[0:4, 0:NCH]), reads=['pb'], writes=['cmT'])
                P.op('dve', lambda e: e.tensor_tensor_scan(out=ET[:], data0=cmT[:], data1=cmT[:], initial=0.0,
                                                           op0=ALU.max, op1=ALU.max), reads=['cmT'], writes=['ET'])
                if NCH > 1:
                    P.op('dve', lambda e: e.tensor_copy(out=ETn[:, 0:NCH - 1], in_=ET[:, 1:NCH]), reads=['ET'], writes=['ETn'])
                P.op('dve', lambda e: e.tensor_copy(out=ETn[:, NCH - 1:NCH], in_=ET[:, NCH - 1:NCH]), reads=['ET'], writes=['ETn'])
                P.op('pe', lambda e: e.transpose(out=pa[0:NCH, 0:4], in_=ET[:], identity=identf[0:4, 0:4]),
                     reads=['ET', 'identf'], writes=['pa'])
                P.op('dve', lambda e: e.tensor_copy(out=Ec[:], in_=pa[0:NCH, 0:4]), reads=['pa'], writes=['Ec'])
                P.op('pe', lambda e: e.transpose(out=pb[0:NCH, 0:4], in_=ETn[:], identity=identf[0:4, 0:4]),
                     reads=['ETn', 'identf'], writes=['pb'])
                P.op('dve', lambda e: e.tensor_copy(out=Enc[:], in_=pb[0:NCH, 0:4]), reads=['pb'], writes=['Enc'])
                Eb = Ec[:].unsqueeze(2).broadcast_to([NCH, 4, 64])
                Enb = Enc[:].unsqueeze(2).broadcast_to([NCH, 4, 64])
                P.op('dve', lambda e: e.tensor_tensor(out=tmp[:], in0=a_[:], in1=Eb, op=ALU.subtract),
                     reads=['a_', 'Ec'], writes=['tmp'])
                P.op('act', lambda e: e.activation(out=uu[:], in_=tmp[:], func=AF.Exp, bias=LNSC), reads=['tmp'], writes=['uu'])
                P.op('dve', lambda e: e.tensor_tensor(out=tmp[:], in0=a_[:], in1=Enb, op=ALU.subtract),
                     reads=['a_', 'Enc'], writes=['tmp'])
                P.op('act', lambda e: e.activation(out=uu2[:], in_=tmp[:], func=AF.Exp, bias=LNSC), reads=['tmp'], writes=['uu2'])
                P.op('dve', lambda e: e.tensor_tensor(out=tmp[:], in0=Ft[:], in1=Eb, op=ALU.add),
                     reads=['Ft', 'Ec'], writes=['tmp'])
                P.op('act', lambda e: e.activation(out=fl[:], in_=tmp[:], func=AF.Exp, scale=-1.0), reads=['tmp'], writes=['fl'])
                P.op('dve', lambda e: e.tensor_tensor(out=dec[:], in0=Ec[:], in1=Enc[:], op=ALU.subtract),
                     reads=['Ec', 'Enc'], writes=['dec'])
                P.op('act', lambda e: e.activation(out=dec[:], in_=dec[:], func=AF.Exp), reads=['dec'], writes=['dec'])
                P.op('dve', lambda e: e.tensor_copy(out=decrep[:], in_=dec[:].unsqueeze(2).broadcast_to([NCH, 4, 128])),
                     reads=['dec'], writes=['decrep'])
                for src, dst, nm in ((uu, uT, 'uT'), (uu2, u2T, 'u2T'), (fl, flT, 'flT')):
                    for hh in range(4):
                        P.op('pe', lambda e: e.transpose(out=pa[0:64, hh * 128:hh * 128 + NCH], in_=src[:, hh, :],
                                                         identity=identf[0:NCH, 0:NCH]),
                             reads=['uu', 'uu2', 'fl', 'identf'], writes=['pa'])
                    P.op('dve', lambda e: e.tensor_copy(out=dst[:], in_=pa[0:64, :].rearrange("p (h c) -> p h c", h=4)[:, :, 0:NCH]),
                         reads=['pa'], writes=[nm])
                for hh in range(4):
                    P.op('pe', lambda e: e.matmul(pb[:, hh * 128:hh * 128 + NCH], lhsT=decrep[:, hh, :],
                                                  rhs=identf[0:NCH, 0:NCH], start=True, stop=True),
                         reads=['decrep', 'identf'], writes=['pb'])
                P.op('dve', lambda e: e.tensor_copy(out=decB[:], in_=pb[:].rearrange("p (h c) -> p h c", h=4)[:, :, 0:NCH]),
                     reads=['pb'], writes=['decB'])
                self.barrier()
            NG = S // 512
            qg = [sb(st, "qg", [128, 4, 512], BF16) for _ in range(2)]
            kg = [sb(st, "kg", [128, 4, 512], BF16) for _ in range(2)]
            vg = [sb(st, "vg", [64, 8, 4, 129], BF16) for _ in range(2)]
            ogg = [sb(st, "ogg", [64, 8, 512], BF16) for _ in range(2)]
            ym = [sb(st, "ym", [64, 8, 512], BF16) for _ in range(2)]
            G = [sb(st, "G", [128, 129], F32) for _ in range(4)]
            Gb = [sb(st, "Gb", [128, 129], BF16) for _ in range(4)]
            ku2 = [sb(st, "ku2", [64, 128], BF16) for _ in range(2)]
            Sm = [sb(st, "Sm", [64, 64], BF16) for _ in range(2)]
            Smu = [sb(st, "Smu", [64, 64], F32) for _ in range(2)]
            junk = sb(st, "junkm", [64, 128], BF16)
            mhalf = sb(st, "mhalf", [64, 4], F32)
            P.op('pool', lambda e: e.memset(mhalf[:], -0.5), writes=['mhalf'])
            osb = [sb(st, "osb", [64, 4, 129], F32) for _ in range(2)]
            ssq = [sb(st, "ssq", [64, 4], F32) for _ in range(2)]
            nt = [sb(st, "nt", [64, 4, 4], F32) for _ in range(2)]
            ytmp = sb(st, "ytmp", [64, 4, 128], F32)
            sc = [sb(st, "scm", [64, 8], F32) for _ in range(2)]
            pkT = [ps(st, "pkT", [128, 1024], BF16) for _ in range(2)]
            pS = [ps(st, "pS", [128, 512], F32) for _ in range(2)]
            pO = [ps(st, "pO", [128, 512], F32) for _ in range(2)]
            pG = [ps(st, "pG", [128, 512], F32) for _ in range(2)]
            for i in range(2):
                P.op('pool', lambda e: e.memset(vg[i][:], 1.0), writes=[('vg', i)])
            for hh in range(4):
                P.op('pool', lambda e: e.memset(G[hh][:], 0.0), writes=[('G', hh)])
                P.op('pool', lambda e: e.memset(Gb[hh][:], 0.0), writes=[('Gb', hh)])

            def load_group(g):
                i = g % 2
                tk = slice(g * 512, (g + 1) * 512)
                P.dma('sp', qg[i][:], self.qkT[0:512, tk].rearrange("(h p) t -> p h t", p=128),
                      reads=[('qkT', ft, g) for ft in range(4)], writes=[('qg', i)])
                P.dma('sp', kg[i][:], self.qkT[512:1024, tk].rearrange("(h p) t -> p h t", p=128),
                      reads=[('qkT', ft, g) for ft in range(4, 8)], writes=[('kg', i)])
                for hh in range(4):
                    P.dma('sp', vg[i][:, :, hh, 0:128],
                          self.mv_d[tk, hh * 128:(hh + 1) * 128].rearrange("(c s) e -> s c e", s=64),
                          reads=[('mv_d', 4 * g + j) for j in range(4)], writes=[('vg', i)])
                P.dma('sp', ogg[i][:], self.og_d[tk, :].rearrange("(c s) e -> s c e", s=64),
                      reads=[('og_d', 4 * g + j) for j in range(4)], writes=[('ogg', i)])
            load_group(0)
            steps = [(g, cl, hh) for g in range(NG) for cl in range(8) for hh in range(4)]

            def stageA(n):
                g, cl, hh = steps[n]
                gi_ = g % 2
                c = g * 8 + cl
                i2 = n % 2
                k_ = kg[gi_][:, hh, cl * 64:(cl + 1) * 64]
                q_ = qg[gi_][:, hh, cl * 64:(cl + 1) * 64]
                P.op('pe', lambda e: e.transpose(out=pkT[i2][0:64, 0:128], in_=k_, identity=identb[:]),
                     reads=[('kg', gi_), 'identb'], writes=[('pkT', i2)])
                P.op('pe', lambda e: e.matmul(pS[i2][0:64, 0:64], lhsT=k_, rhs=q_, start=True, stop=True),
                     reads=[('kg', gi_), ('qg', gi_)], writes=[('pS', i2)])
                P.op('act', lambda e: e.activation(out=ku2[i2][:], in_=pkT[i2][0:64, 0:128], func=AF.Copy,
                                                   scale=u2T[:, hh, c:c + 1]),
                     reads=[('pkT', i2), 'u2T'], writes=[('ku2', i2)])
                P.op('act', lambda e: e.activation(out=Smu[i2][:], in_=pS[i2][0:64, 0:64], func=AF.Copy,
                                                   scale=uT[:, hh, c:c + 1]),
                     reads=[('pS', i2), 'uT'], writes=[('Smu', i2)])
                P.op('pool', lambda e: e.tensor_tensor(out=Sm[i2][:], in0=Smu[i2][:], in1=tri[0:64, 0:64], op=ALU.mult),
                     reads=[('Smu', i2), 'tri'], writes=[('Sm', i2)])

            def stageB(n):
                g, cl, hh = steps[n]
                gi_ = g % 2
                c = g * 8 + cl
                i2 = n % 2
                q_ = qg[gi_][:, hh, cl * 64:(cl + 1) * 64]
                v_ = vg[gi_][:, cl, hh, :]
                P.op('pe', lambda e: e.matmul(pO[i2][0:64, 0:129], lhsT=Sm[i2][:], rhs=v_, start=True, stop=False),
                     reads=[('Sm', i2), ('vg', gi_)], writes=[('pO', i2)])
                P.op('pe', lambda e: e.matmul(pO[i2][0:64, 0:129], lhsT=q_, rhs=Gb[hh][:], start=False, stop=True),
                     reads=[('qg', gi_), ('Gb', hh)], writes=[('pO', i2)])
                P.op('pe', lambda e: e.matmul(pG[i2][:, 0:129], lhsT=ku2[i2][:], rhs=v_, start=True, stop=True),
                     reads=[('ku2', i2), ('vg', gi_)], writes=[('pG', i2)])
                P.op('dve', lambda e: e.scalar_tensor_tensor(out=G[hh][:], in0=G[hh][:], scalar=decB[:, hh, c:c + 1],
                                                             in1=pG[i2][:, 0:129], op0=ALU.mult, op1=ALU.add),
                     reads=[('G', hh), 'decB', ('pG', i2)], writes=[('G', hh)])
                P.op('dve', lambda e: e.tensor_copy(out=Gb[hh][:], in_=G[hh][:]), reads=[('G', hh)], writes=[('Gb', hh)])
                cb = c % 2
                P.op('act', lambda e: e.activation(out=junk[:], in_=pO[i2][0:64, 0:128], func=AF.Square,
                                                   scale=128.0 ** -0.5, accum_out=ssq[cb][:, hh:hh + 1]),
                     reads=[('pO', i2)], writes=['junk', ('ssq', cb, hh)])
                P.op('act', lambda e: e.copy(out=osb[cb][:, hh, :], in_=pO[i2][0:64, 0:129]),
                     reads=[('pO', i2)], writes=[('osb', cb, hh)])

            def stageN(n):
                g, cl, hh = steps[n]
                if hh != 3:
                    return
                gi_ = g % 2
                c = g * 8 + cl
                cb = c % 2
                t_ = nt[cb]
                tk = ('nt', cb)
                den = osb[cb][:, :, 128]
                ok = [('osb', cb, h_) for h_ in range(4)]
                P.op('dve', lambda e: e.tensor_scalar(out=t_[:, 0, :], in0=den, scalar1=-1.0, scalar2=None, op0=ALU.mult),
                     reads=ok, writes=[tk])
                P.op('dve', lambda e: e.tensor_tensor(out=t_[:, 0, :], in0=t_[:, 0, :], in1=flT[:, :, c], op=ALU.max),
                     reads=[tk, 'flT'], writes=[tk])
                P.op('dve', lambda e: e.tensor_tensor(out=t_[:, 0, :], in0=t_[:, 0, :], in1=den, op=ALU.max),
                     reads=[tk] + ok, writes=[tk])
                P.op('dve', lambda e: e.tensor_tensor(out=t_[:, 1, :], in0=t_[:, 0, :], in1=t_[:, 0, :], op=ALU.mult),
                     reads=[tk], writes=[tk])
                P.op('dve', lambda e: e.scalar_tensor_tensor(out=t_[:, 2, :], in0=t_[:, 1, :], scalar=EPS, in1=ssq[cb][:],
                                                             op0=ALU.mult, op1=ALU.add),
                     reads=[tk] + [('ssq', cb, h_) for h_ in range(4)], writes=[tk])
                P.op('pool', lambda e: e.tensor_tensor(out=t_[:, 3, :], in0=t_[:, 2, :], in1=mhalf[:], op=ALU.pow),
                     reads=[tk, 'mhalf'], writes=[tk])
                P.op('dve', lambda e: e.tensor_tensor(out=ytmp[:], in0=osb[cb][:, :, 0:128],
                                                      in1=t_[:, 3, :].unsqueeze(2).broadcast_to([64, 4, 128]), op=ALU.mult),
                     reads=ok + [tk], writes=['ytmp'])
                P.op('dve', lambda e: e.tensor_tensor(out=ym[gi_][:, cl, :].rearrange("p (h d) -> p h d", h=4), in0=ytmp[:],
                                                      in1=ogg[gi_][:, cl, :].rearrange("p (h d) -> p h d", h=4), op=ALU.mult),
                     reads=['ytmp', ('ogg', gi_)], writes=[('ym', gi_)])
                if cl == 7:
                    P.dma('sp', self.mix_d[g * 512:(g + 1) * 512, 0:512].rearrange("(c s) e -> s c e", s=64), ym[gi_][:],
                          reads=[('ym', gi_)], writes=[('mixm', g)])
            NS = len(steps)
            stageA(0)
            for n in range(NS):
                g, cl, hh = steps[n]
                if n + 1 < NS:
                    stageA(n + 1)
                stageB(n)
                if n >= 1:
                    stageN(n - 1)
                if cl == 0 and hh == 1 and g + 1 < NG:
                    load_group(g + 1)
            stageN(NS - 1)
            self.barrier()

    def _negB(self, st, name, g1, g2):
        P = self.P
        ga = self.sb(st, name + "_ga", [128, 64], F32)
        gb = self.sb(st, name + "_gb", [128, 64], F32)
        m1 = self.sb(st, name + "_m1", [128, 1], F32)
        m2 = self.sb(st, name + "_m2", [128, 1], F32)
        nb = self.sb(st, name, [128, 1], F32)
        P.dma('sp', ga[:], g1.partition_broadcast(128), writes=[name + 'ga'])
        P.dma('sp', gb[:], g2.partition_broadcast(128), writes=[name + 'gb'])
        P.op('dve', lambda e: e.tensor_reduce(out=m1[:], in_=ga[:], axis=AX.X, op=ALU.max, apply_absolute_value=True),
             reads=[name + 'ga'], writes=[name + 'm1'])
        P.op('dve', lambda e: e.tensor_reduce(out=m2[:], in_=gb[:], axis=AX.X, op=ALU.max, apply_absolute_value=True),
             reads=[name + 'gb'], writes=[name + 'm2'])
        P.op('dve', lambda e: e.scalar_tensor_tensor(out=nb[:], in0=m1[:], scalar=-8.0, in1=m2[:], op0=ALU.mult, op1=ALU.mult),
             reads=[name + 'm1', name + 'm2'], writes=[name])
        return nb

    def phase2_moba(self, l):
        S, P, nc = self.S, self.P, self.nc
        NQC = S // 512
        NKT = S // 128
        NB = S // 256
        with ExitStack() as st:
            sb, ps = self.sb, self.ps
            identb = sb(st, "identb_b", [128, 128], BF16)
            tm4 = sb(st, "tm4", [128, 4, 512], BF16)
            zl = sb(st, "zl", [1, 128], BF16)
            zr = sb(st, "zr", [1, 512], BF16)
            P.dma('sp', identb[:], self.c_identb, writes=['identb'])
            P.dma('sp', tm4[:], self.c_tm4, writes=['tm4'])
            P.op('pool', lambda e: e.memset(zl[:], 0.0), writes=['zl'])
            P.op('pool', lambda e: e.memset(zr[:], 0.0), writes=['zr'])
            negB = self._negB(st, "negBm", self.moba_qk_norm[l, 0], self.moba_qk_norm[l, 1])
            KX = sb(st, "KX", [96, S], BF16)
            VX = sb(st, "VX", [128, NKT, 65], BF16)
            kmean = sb(st, "kmean", [64, 32], F32)
            kmb = sb(st, "kmb", [64, 32], BF16)
            QX = [sb(st, "QX", [96, 512], BF16) for _ in range(2)]
            gsb = sb(st, "gsb_b", [128, 32], F32)
            m8 = sb(st, "m8", [128, 8], F32)
            sel = sb(st, "sel_b", [128, 32], F32)
            MBw = sb(st, "MBw", [128, 128], BF16)
            MLA = int(os.environ.get("LA", "3"))
            PT = [sb(st, "PT", [128, 512], BF16) for _ in range(MLA + 2)]
            rz = sb(st, "rz_b", [128, 4], F32)
            yb = [sb(st, "yb", [128, 4, 64], BF16) for _ in range(2)]
            pS = [ps(st, "pS_b", [128, 512], F32) for _ in range(MLA + 1)]
            pO = [ps(st, "pO_b", [128, 512], F32) for _ in range(2)]
            pM = ps(st, "pM_b", [128, 512], F32)
            pMb = ps(st, "pMb_b", [128, 1024], BF16)
            P.dma('sp', KX[64:96, :], self.c_ind32, writes=['KXi'])
            P.op('pool', lambda e: e.memset(VX[:], 1.0), writes=['VX'])
            P.op('pool', lambda e: e.memset(MBw[:], 0.0), writes=['MBw'])
            P.op('pool', lambda e: e.memset(kmean[:], 0.0), writes=['kmean'])
            all_tiles = list(range(S // 128))
            si = 0
            qi = 0
            for h in range(4):
                P.dma('sp', KX[0:64, :], self.bkT[h * 64:(h + 1) * 64, :], reads=[('bkT', j) for j in all_tiles], writes=['KX'])
                self.dma_mid(VX[:, :, 0:64], self.bv_d[:, h * 64:(h + 1) * 64].rearrange("(kt p) d -> p kt d", p=128), NKT, 8,
                             reads=[('bv_d', j) for j in all_tiles], writes=['VX'])
                P.op('dve', lambda e: e.tensor_reduce(out=kmean[:, 0:NB], in_=KX[0:64, :].rearrange("p (n k) -> p n k", k=256),
                                                      axis=AX.X, op=ALU.add), reads=['KX'], writes=['kmean'])
                P.op('dve', lambda e: e.tensor_scalar(out=kmb[:], in0=kmean[:], scalar1=1.0 / 256, scalar2=None, op0=ALU.mult),
                     reads=['kmean'], writes=['kmb'])
                def prep(qc, Q_, qk_):
                    q0 = qc * 512
                    P.dma('sp', Q_[0:64, :], self.bqT[h * 64:(h + 1) * 64, q0:q0 + 512],
                          reads=[('bqT', 4 * qc + j) for j in range(4)], writes=[qk_])
                    yield
                    for j in range(4):
                        own = 2 * qc + j // 2
                        P.op('pe', lambda e: e.matmul(pM[:, 0:32], lhsT=Q_[0:64, j * 128:(j + 1) * 128], rhs=kmb[:],
                                                      start=True, stop=True), reads=[qk_, 'kmb'], writes=['pM'])
                        yield
                        P.op('pool', lambda e: e.memset(gsb[:], -1e30), writes=['gsb'])
                        yield
                        if own > 0:
                            P.op('dve', lambda e: e.tensor_copy(out=gsb[:, 0:own], in_=pM[:, 0:own]), reads=['pM'], writes=['gsb'])
                            yield
                        yield
                        P.op('dve', lambda e: e.max(out=m8[:], in_=gsb[:]), reads=['gsb'], writes=['m8'])
                        yield
                        P.op('dve', lambda e: e.tensor_scalar(out=sel[:], in0=gsb[:], scalar1=m8[:, 2:3], scalar2=None,
                                                              op0=ALU.is_ge), reads=['gsb', 'm8'], writes=['sel'])
                        yield
                        yield
                        P.op('dve', lambda e: e.tensor_scalar(out=MBw[:, 64:96], in0=sel[:], scalar1=-NEGM, scalar2=NEGM,
                                                              op0=ALU.mult, op1=ALU.add), reads=['sel'], writes=['MBw'])
                        yield
                        P.op('dve', lambda e: e.memset(MBw[:, 64 + own:65 + own], 0.0), writes=['MBw'])
                        yield
                        if own + 1 < 32:
                            P.op('dve', lambda e: e.memset(MBw[:, 65 + own:96], NEGM), writes=['MBw'])
                            yield
                        yield
                        P.op('pe', lambda e: e.transpose(out=pMb[:, 0:128], in_=MBw[:], identity=identb[:]),
                             reads=['MBw', 'identb'], writes=['pMb'])
                        yield
                        P.op('act', lambda e: e.copy(out=Q_[64:96, j * 128:(j + 1) * 128], in_=pMb[64:96, 0:128]),
                             reads=['pMb'], writes=[qk_])
                        yield
                        yield
                for _ in prep(0, QX[qi % 2], ('QX', qi % 2)):
                    pass
                for qc in range(NQC):
                    Q_ = QX[qi % 2]
                    qk_ = ('QX', qi % 2)
                    qi += 1
                    q0 = qc * 512
                    gp = prep(qc + 1, QX[qi % 2], ('QX', qi % 2)) if qc + 1 < NQC else None
                    pacc = 0.0
                    prate = 52.0 / (4 * qc + 4)
                    po = pO[qc % 2]
                    pok = ('pO', qc % 2)
                    P.op('pe', lambda e: e.matmul(po[:, 0:260], lhsT=zl[:], rhs=zr[:, 0:260], start=True, stop=True,
                                                  skip_group_check=True), reads=['zl', 'zr'], writes=[pok])
                    nkt = 4 * qc + 4
                    LA = MLA
                    base = si

                    def qkmm(kt):
                        pp = pS[(base + kt) % (MLA + 1)]
                        P.op('pe', lambda e: e.matmul(pp[:], lhsT=KX[:, kt * 128:(kt + 1) * 128], rhs=Q_[:], start=True, stop=True),
                             reads=['KX', 'KXi', qk_], writes=[('pS', (base + kt) % (MLA + 1))])
                    for kt in range(min(LA, nkt)):
                        qkmm(kt)
                    for kt in range(nkt):
                        if kt + LA < nkt:
                            qkmm(kt + LA)
                        p_ = pS[si % (MLA + 1)]
                        pk = ('pS', si % (MLA + 1))
                        t_ = PT[si % (MLA + 2)]
                        tk = ('PT', si % (MLA + 2))
                        si += 1
                        if LA == 0:
                            qkmm(kt)
                        P.op('act', lambda e: e.activation(out=t_[:], in_=p_[:], func=AF.Exp, bias=negB[:, 0:1], scale=0.125),
                             reads=[pk, 'negBm'], writes=[tk])
                        off = kt - 4 * qc
                        if off >= 0:
                            P.op('dve', lambda e: e.tensor_tensor(out=t_[:], in0=t_[:], in1=tm4[:, off, :], op=ALU.mult),
                                 reads=[tk, 'tm4'], writes=[tk])
                        for j in range(4):
                            if kt > 4 * qc + j:
                                continue
                            P.op('pe', lambda e: e.matmul(po[:, j * 65:(j + 1) * 65], lhsT=t_[:, j * 128:(j + 1) * 128],
                                                          rhs=VX[:, kt, :], start=False, stop=(kt == 4 * qc + j),
                                                          skip_group_check=True), reads=[tk, 'VX'], writes=[pok])
                        if gp is not None:
                            pacc += prate
                            while pacc >= 1.0 and gp is not None:
                                pacc -= 1.0
                                try:
                                    next(gp)
                                except StopIteration:
                                    gp = None
                    if gp is not None:
                        for _ in gp:
                            pass
                    y_ = yb[qc % 2]
                    yk = ('yb', qc % 2)
                    pov = po[:, 0:260].rearrange("p (j d) -> p j d", d=65)
                    P.op('dve', lambda e: e.reciprocal(out=rz[:], in_=pov[:, :, 64]), reads=[pok], writes=['rz'])
                    P.op('dve', lambda e: e.tensor_tensor(out=y_[:], in0=pov[:, :, 0:64],
                                                          in1=rz[:].unsqueeze(2).broadcast_to([128, 4, 64]), op=ALU.mult),
                         reads=[pok, 'rz'], writes=[yk])
                    P.dma('sp', self.mix_d[q0:q0 + 512, 512 + h * 64:512 + (h + 1) * 64].rearrange("(j p) d -> p j d", p=128),
                          y_[:], reads=[yk], writes=[('mixb', h, qc)])
            self.barrier()

    def phase2_nsa(self, l):
        S, P, nc = self.S, self.P, self.nc
        NT = S // 128
        Nc = S // 16 - 1
        NCT = max(1, S // 2048)
        with ExitStack() as st:
            sb, ps = self.sb, self.ps
            identb = sb(st, "identb_n", [128, 128], BF16)
            trib = sb(st, "trib", [128, 128], BF16)
            triw = sb(st, "triw", [128, 128], BF16)
            cmask = sb(st, "cmask", [128, 17, 128], BF16)
            OVL = sb(st, "OVL", [128, NCT, 128], BF16)
            cols = sb(st, "cols", [128, 4], F32)
            zl = sb(st, "zl_n", [1, 128], BF16)
            zr = sb(st, "zr_n", [1, 512], BF16)
            NG = sb(st, "NG", [128, NT, 12], F32)
            P.dma('sp', identb[:], self.c_identb, writes=['identb'])
            P.dma('sp', trib[:], self.c_trib, writes=['trib'])
            P.dma('sp', triw[:], self.c_triw, writes=['triw'])
            P.dma('sp', cmask[:], self.c_cmask, writes=['cmask'])
            P.dma('sp', OVL[:], self.c_ovl.rearrange("(ct p) n -> p ct n", p=128)[:, 0:NCT, :], writes=['OVL'])
            P.dma('sp', cols[:], self.c_cols, writes=['cols'])
            P.op('pool', lambda e: e.memset(zl[:], 0.0), writes=['zl'])
            P.op('pool', lambda e: e.memset(zr[:], 0.0), writes=['zr'])
            allt = list(range(NT))
            self.dma_mid(NG[:], self.ng_d.rearrange("(j p) g -> p j g", p=128), NT, 8, reads=[('ng_d', j) for j in allt], writes=['NG'])
            negBc = self._negB(st, "negBc", self.nsa_q_norm[l], self.nsa_k_norm[l, 0])
            negBs = self._negB(st, "negBs", self.nsa_q_norm[l], self.nsa_k_norm[l, 1])
            negBw = self._negB(st, "negBw", self.nsa_q_norm[l], self.nsa_k_norm[l, 2])
            KSX = sb(st, "KSX", [128, S], BF16)
            KWX = sb(st, "KWX", [64, S], BF16)
            VSX = sb(st, "VSX", [128, NT, 65], BF16)
            VWX = sb(st, "VWX", [128, NT, 65], BF16)
            KcT = sb(st, "KcT", [64, NCT * 128], BF16)
            VCX = sb(st, "VCX", [128, NCT, 65], BF16)
            P.dma('sp', KSX[0:64, :], self.ksT, reads=[('ksT', j) for j in allt], writes=['KSX'])
            P.dma('sp', KSX[64:128, :], self.c_ind64, writes=['KSXi'])
            P.dma('sp', KWX[:], self.kwT, reads=[('kwT', j) for j in allt], writes=['KWX'])
            for V_, src, nm in ((VSX, self.vs_d, 'vs_d'), (VWX, self.vw_d, 'vw_d')):
                P.op('pool', lambda e: e.memset(V_[:], 1.0), writes=[nm + 'X'])
                self.dma_mid(V_[:, :, 0:64], src.rearrange("(kt p) d -> p kt d", p=128), NT, 8,
                             reads=[(nm, j) for j in allt], writes=[nm + 'X'])
            P.op('pool', lambda e: e.memset(VCX[:], 1.0), writes=['VCX'])
            pS = [ps(st, "pS_n", [128, 512], F32) for _ in range(2)]
            pOc = ps(st, "pOc", [128, 512], F32)
            pU = ps(st, "pU", [128, 512], F32)
            pOs = ps(st, "pOs", [128, 512], F32)
            pOw = ps(st, "pOw", [128, 512], F32)
            pMb = ps(st, "pMb_n", [128, 1024], BF16)
            pM = ps(st, "pM_n", [128, 512], F32)
            with ExitStack() as s2:
                KCV = sb(s2, "KCV", [128, S], BF16)
                W1s = sb(s2, "W1s", [128, 32, 128], F32)
                W1 = sb(s2, "W1", [128, 32, 128], BF16)
                pes = sb(s2, "pes", [32, 128], F32)
                peb = sb(s2, "peb", [32, 128], BF16)
                peT = sb(s2, "peT", [128, 32], BF16)
                w2s = sb(s2, "w2s", [128, 2, 64], F32)
                w2 = sb(s2, "w2", [128, 2, 64], BF16)
                gk0 = sb(s2, "gk0", [128, 64], F32)
                bias = sb(s2, "bias_c", [128, 2], F32)
                hidb = sb(s2, "hidb", [128, NCT * 128], BF16)
                kc32 = sb(s2, "kc32", [128, 64], F32)
                kcn = sb(s2, "kcn", [128, 64], BF16)
                junk = sb(s2, "junk_c", [128, 64], F32)
                ssc = sb(s2, "ssc", [128, 2], F32)
                P.dma('sp', KCV[:], self.kcvcT, reads=[('kcvcT', b) for b in range(S // 512)], writes=['KCV'])
                for br in range(2):
                    P.dma('sp', W1s[64 * br:64 * br + 64], self.cmp_w1[l, br].rearrange("(r d) j -> d r j", d=64), writes=['W1s'])
                    P.dma('sp', w2s[:, br, :], self.cmp_w2[l, br], writes=['w2s'])
                    P.dma('sp', pes[:, 64 * br:64 * br + 64], self.cmp_pe[l, br], writes=['pes'])
                P.dma('sp', gk0[:], self.nsa_k_norm[l, 0].partition_broadcast(128), writes=['gk0'])
                P.op('dve', lambda e: e.tensor_copy(out=W1[:], in_=W1s[:]), reads=['W1s'], writes=['W1'])
                P.op('dve', lambda e: e.tensor_copy(out=peb[:], in_=pes[:]), reads=['pes'], writes=['peb'])
                P.op('pe', lambda e: e.transpose(out=pMb[:, 0:32], in_=peb[:], identity=identb[0:32, 0:32]),
                     reads=['peb', 'identb'], writes=['pMb'])
                P.op('dve', lambda e: e.tensor_copy(out=peT[:], in_=pMb[:, 0:32]), reads=['pMb'], writes=['peT'])
                P.op('dve', lambda e: e.tensor_copy(out=w2[:], in_=w2s[:]), reads=['w2s'], writes=['w2'])
                P.op('pool', lambda e: e.memset(hidb[:], 0.0), writes=['hidb'])
                for br in range(2):
                    rows = slice(64 * br, 64 * br + 64)
                    kview = KCV[rows, :].rearrange("p (c s) -> p c s", s=16)
                    for r in range(32):
                        P.op('pe', lambda e: e.matmul(pM[:, 0:1], lhsT=W1[rows, r, :], rhs=peT[rows, r:r + 1],
                                                      start=(r == 0), stop=(r == 31)), reads=['W1', 'peT'], writes=['pM'])
                    P.op('dve', lambda e: e.tensor_copy(out=bias[:, br:br + 1], in_=pM[:, 0:1]), reads=['pM'], writes=['bias'])
                    for r in range(32):
                        rhs = kview[:, 0:Nc, r] if r < 16 else kview[:, 1:Nc + 1, r - 16]
                        P.op('pe', lambda e: e.matmul(pS[0][:, 0:Nc], lhsT=W1[rows, r, :], rhs=rhs,
                                                      start=(r == 0), stop=(r == 31)), reads=['W1', 'KCV'], writes=[('pS', 0)])
                    P.op('act', lambda e: e.activation(out=hidb[:, 0:Nc], in_=pS[0][:, 0:Nc], func=AF.Silu, bias=bias[:, br:br + 1]),
                         reads=[('pS', 0), 'bias'], writes=['hidb'])
                    for ct in range(NCT):
                        P.op('pe', lambda e: e.matmul(pM[:, 0:64], lhsT=hidb[:, ct * 128:(ct + 1) * 128], rhs=w2[:, br, :],
                                                      start=True, stop=True), reads=['hidb', 'w2'], writes=['pM'])
                        if br == 0:
                            P.op('act', lambda e: e.activation(out=junk[:], in_=pM[:, 0:64], func=AF.Square, scale=0.125,
                                                               accum_out=ssc[:, 0:1]), reads=['pM'], writes=['junk_c', 'ssc'])
                            P.op('dve', lambda e: e.tensor_scalar(out=ssc[:, 0:1], in0=ssc[:, 0:1], scalar1=EPS, scalar2=None,
                                                                  op0=ALU.add), reads=['ssc'], writes=['ssc'])
                            P.op('act', lambda e: e.activation(out=ssc[:, 0:1], in_=ssc[:, 0:1], func=AF.Sqrt), reads=['ssc'], writes=['ssc'])
                            P.op('dve', lambda e: e.reciprocal(out=ssc[:, 1:2], in_=ssc[:, 0:1]), reads=['ssc'], writes=['ssc'])
                            P.op('dve', lambda e: e.scalar_tensor_tensor(out=kcn[:], in0=pM[:, 0:64], scalar=ssc[:, 1:2], in1=gk0[:],
                                                                         op0=ALU.mult, op1=ALU.mult),
                                 reads=['pM', 'ssc', 'gk0'], writes=['kcn'])
                            P.op('pe', lambda e: e.transpose(out=pMb[0:64, 0:128], in_=kcn[:], identity=identb[:]),
                                 reads=['kcn', 'identb'], writes=['pMb'])
                            P.op('act', lambda e: e.copy(out=KcT[:, ct * 128:(ct + 1) * 128], in_=pMb[0:64, 0:128]),
                                 reads=['pMb'], writes=['KcT'])
                        else:
                            P.op('act', lambda e: e.copy(out=VCX[:, ct, 0:64], in_=pM[:, 0:64]), reads=['pM'], writes=['VCX'])
                self.barrier()
            QU = [sb(st, "QU", [64, 512], BF16) for _ in range(2)]
            QR0 = [sb(st, "QR0", [128, 512], BF16) for _ in range(2)]
            QR1 = [sb(st, "QR1", [128, 512], BF16) for _ in range(2)]
            PTc = [sb(st, "PTc", [128, 512], BF16) for _ in range(NCT)]
            PT = [sb(st, "PTn", [128, 512], BF16) for _ in range(3)]
            zz = sb(st, "zz", [128, 3, 4], F32)
            rzz = sb(st, "rzz", [128, 3, 4], F32)
            coef = sb(st, "coef", [128, 3, 4], F32)
            imp = sb(st, "imp", [128, 128], F32)
            work = sb(st, "work", [128, 128], F32)
            m8a = sb(st, "m8a", [128, 8], F32)
            m8b = sb(st, "m8b", [128, 8], F32)
            selm = sb(st, "selm", [128, 128], F32)
            MB = sb(st, "MB", [128, 128], BF16)
            MBs = sb(st, "MBs", [128, 128], BF16)
            yacc = sb(st, "yacc", [128, 4, 64], F32)
            yn = [sb(st, "yn", [128, 256], BF16) for _ in range(2)]
            si = 0

            def seed(t, n, key):
                P.op('pe', lambda e: e.matmul(t[:, 0:n], lhsT=zl[:], rhs=zr[:, 0:n], start=True, stop=True, skip_group_check=True),
                     reads=['zl', 'zr'], writes=[key])

            def hb(ap):
                return ap.unsqueeze(1).broadcast_to([ap.shape[0], 4, 128])

            for m in range(NT):
                t0 = m * 128
                b2 = m % 2
                qu, qr0, qr1 = QU[b2], QR0[b2], QR1[b2]
                P.dma('sp', qu[:].rearrange("p (h t) -> p h t", h=4), self.nquT[:, t0:t0 + 128].rearrange("(h d) t -> d h t", d=64),
                      reads=[('nquT', m)], writes=[('QU', b2)])
                P.dma('sp', qr0[0:64, :].rearrange("p (h t) -> p h t", h=4), self.nqrT[:, t0:t0 + 128].rearrange("(h d) t -> d h t", d=64),
                      reads=[('nqrT', m)], writes=[('QR0', b2)])
                use_g1 = (2 * m + 1) >= 64
                if use_g1:
                    P.dma('sp', qr1[0:64, :].rearrange("p (h t) -> p h t", h=4),
                          self.nqrT[:, t0:t0 + 128].rearrange("(h d) t -> d h t", d=64), reads=[('nqrT', m)], writes=[('QR1', b2)])
                ctn = min(NCT, (8 * m + 6) // 128 + 1)
                for ct in range(ctn):
                    p_ = pS[si % 2]
                    pk = ('pS', si % 2)
                    si += 1
                    P.op('pe', lambda e: e.matmul(p_[:], lhsT=KcT[:, ct * 128:(ct + 1) * 128], rhs=qu[:], start=True, stop=True),
                         reads=['KcT', ('QU', b2)], writes=[pk])
                    P.op('act', lambda e: e.activation(out=PTc[ct][:], in_=p_[:], func=AF.Exp, bias=negBc[:, 0:1], scale=0.125),
                         reads=[pk, 'negBc'], writes=[('PTc', ct)])
                    r = m - 16 * ct
                    if r <= 16:
                        P.op('pool', lambda e: e.tensor_tensor(out=PTc[ct][:].rearrange("p (h t) -> p h t", h=4),
                                                               in0=PTc[ct][:].rearrange("p (h t) -> p h t", h=4),
                                                               in1=hb(cmask[:, r, :]), op=ALU.mult),
                             reads=[('PTc', ct), 'cmask'], writes=[('PTc', ct)])
                seed(pOc, 260, 'pOc')
                seed(pU, 512, 'pU')
                for h in range(4):
                    for ct in range(ctn):
                        P.op('pe', lambda e: e.matmul(pOc[:, h * 65:(h + 1) * 65], lhsT=PTc[ct][:, h * 128:(h + 1) * 128],
                                                      rhs=VCX[:, ct, :], start=False, stop=(ct == ctn - 1), skip_group_check=True),
                             reads=[('PTc', ct), 'VCX'], writes=['pOc'])
                        P.op('pe', lambda e: e.matmul(pU[:, h * 128:(h + 1) * 128], lhsT=PTc[ct][:, h * 128:(h + 1) * 128],
                                                      rhs=OVL[:, ct, :], start=False, stop=(ct == ctn - 1), skip_group_check=True),
                             reads=[('PTc', ct), 'OVL'], writes=['pU'])
                pocv = pOc[:, 0:260].rearrange("p (h d) -> p h d", d=65)
                P.op('dve', lambda e: e.tensor_scalar(out=zz[:, 0, :], in0=pocv[:, :, 64], scalar1=1e-30, scalar2=None, op0=ALU.max),
                     reads=['pOc'], writes=['zz0'])
                P.op('dve', lambda e: e.reciprocal(out=rzz[:, 0, :], in_=zz[:, 0, :]), reads=['zz0'], writes=['rzz0'])
                P.op('dve', lambda e: e.tensor_scalar(out=imp[:], in0=pU[:, 0:128], scalar1=rzz[:, 0, 0:1], scalar2=None, op0=ALU.mult),
                     reads=['pU', 'rzz0'], writes=['imp'])
                for h in range(1, 4):
                    P.op('dve', lambda e: e.scalar_tensor_tensor(out=imp[:], in0=pU[:, h * 128:(h + 1) * 128], scalar=rzz[:, 0, h:h + 1],
                                                                 in1=imp[:], op0=ALU.mult, op1=ALU.add),
                         reads=['pU', 'rzz0', 'imp'], writes=['imp'])
                n1 = 2 * m + 1
                if n1 + 1 < 128:
                    P.op('pool', lambda e: e.memset(imp[:, n1 + 1:128], -1e30), reads=['imp'], writes=['imp'])
                P.op('pool', lambda e: e.tensor_copy(out=imp[:, n1:n1 + 1], in_=cols[:, 0:1]), reads=['cols', 'imp'], writes=['imp'])
                P.op('pool', lambda e: e.memset(imp[:, n1 - 1:n1], 1e9), reads=['imp'], writes=['imp'])
                if n1 - 2 >= 0:
                    P.op('dve', lambda e: e.tensor_tensor(out=imp[:, n1 - 2:n1 - 1], in0=imp[:, n1 - 2:n1 - 1], in1=cols[:, 1:2], op=ALU.max),
                         reads=['cols', 'imp'], writes=['imp'])
                P.op('pool', lambda e: e.memset(imp[:, 0:1], 1e9), reads=['imp'], writes=['imp'])
                P.op('dve', lambda e: e.max(out=m8a[:], in_=imp[:]), reads=['imp'], writes=['m8a'])
                P.op('dve', lambda e: e.match_replace(out=work[:], in_to_replace=m8a[:], in_values=imp[:], imm_value=-1e30),
                     reads=['imp', 'm8a'], writes=['work'])
                P.op('dve', lambda e: e.max(out=m8b[:], in_=work[:]), reads=['work'], writes=['m8b'])
                P.op('dve', lambda e: e.tensor_scalar(out=selm[:], in0=imp[:], scalar1=m8b[:, 7:8], scalar2=None, op0=ALU.is_ge),
                     reads=['imp', 'm8b'], writes=['selm'])
                P.op('dve', lambda e: e.tensor_scalar(out=MB[:], in0=selm[:], scalar1=-NEGM, scalar2=NEGM, op0=ALU.mult, op1=ALU.add),
                     reads=['selm'], writes=['MB'])
                if n1 + 1 < 128:
                    P.op('pool', lambda e: e.memset(MB[:, n1 + 1:128], NEGM), reads=['MB'], writes=['MB'])
                P.op('pool', lambda e: e.tensor_copy(out=MB[:, n1:n1 + 1], in_=cols[:, 2:3]), reads=['cols', 'MB'], writes=['MB'])
                P.op('pool', lambda e: e.tensor_copy(out=MBs[:, 0:64], in_=MB[:, 64:128]), reads=['MB'], writes=['MBs'])
                P.op('pool', lambda e: e.tensor_copy(out=MBs[:, 64:128], in_=MB[:, 0:64]), reads=['MB'], writes=['MBs'])
                P.op('pe', lambda e: e.transpose(out=pMb[:, 0:128], in_=MBs[:], identity=identb[:]), reads=['MBs', 'identb'], writes=['pMb'])
                P.op('act', lambda e: e.copy(out=qr0[64:128, :].rearrange("p (h t) -> p h t", h=4), in_=hb(pMb[64:128, 0:128])),
                     reads=['pMb'], writes=[('QR0', b2)])
                if use_g1:
                    P.op('pe', lambda e: e.transpose(out=pMb[:, 128:256], in_=MB[:], identity=identb[:]), reads=['MB', 'identb'], writes=['pMb'])
                    P.op('act', lambda e: e.copy(out=qr1[64:128, :].rearrange("p (h t) -> p h t", h=4), in_=hb(pMb[64:128, 128:256])),
                         reads=['pMb'], writes=[('QR1', b2)])
                seed(pOs, 260, 'pOs')
                for kt in range(m + 1):
                    g = kt // 32
                    qr_, qrk = (qr0, ('QR0', b2)) if g == 0 else (qr1, ('QR1', b2))
                    p_ = pS[si % 2]
                    pk = ('pS', si % 2)
                    t_ = PT[si % 3]
                    tk = ('PTn', si % 3)
                    si += 1
                    P.op('pe', lambda e: e.matmul(p_[:], lhsT=KSX[:, kt * 128:(kt + 1) * 128], rhs=qr_[:], start=True, stop=True),
                         reads=['KSX', 'KSXi', qrk], writes=[pk])
                    P.op('act', lambda e: e.activation(out=t_[:], in_=p_[:], func=AF.Exp, bias=negBs[:, 0:1], scale=0.125),
                         reads=[pk, 'negBs'], writes=[tk])
                    if kt == m:
                        P.op('pool', lambda e: e.tensor_tensor(out=t_[:].rearrange("p (h t) -> p h t", h=4),
                                                               in0=t_[:].rearrange("p (h t) -> p h t", h=4), in1=hb(trib[:]), op=ALU.mult),
                             reads=[tk, 'trib'], writes=[tk])
                    for h in range(4):
                        P.op('pe', lambda e: e.matmul(pOs[:, h * 65:(h + 1) * 65], lhsT=t_[:, h * 128:(h + 1) * 128], rhs=VSX[:, kt, :],
                                                      start=False, stop=(kt == m), skip_group_check=True),
                             reads=[tk, 'vs_dX'], writes=['pOs'])
                seed(pOw, 260, 'pOw')
                for kt in range(max(0, m - 4), m + 1):
                    p_ = pS[si % 2]
                    pk = ('pS', si % 2)
                    t_ = PT[si % 3]
                    tk = ('PTn', si % 3)
                    si += 1
                    P.op('pe', lambda e: e.matmul(p_[:], lhsT=KWX[:, kt * 128:(kt + 1) * 128], rhs=qr0[0:64, :], start=True, stop=True),
                         reads=['KWX', ('QR0', b2)], writes=[pk])
                    P.op('act', lambda e: e.activation(out=t_[:], in_=p_[:], func=AF.Exp, bias=negBw[:, 0:1], scale=0.125),
                         reads=[pk, 'negBw'], writes=[tk])
                    if kt == m or kt == m - 4:
                        mk = trib if kt == m else triw
                        P.op('pool', lambda e: e.tensor_tensor(out=t_[:].rearrange("p (h t) -> p h t", h=4),
                                                               in0=t_[:].rearrange("p (h t) -> p h t", h=4), in1=hb(mk[:]), op=ALU.mult),
                             reads=[tk, 'trib', 'triw'], writes=[tk])
                    for h in range(4):
                        P.op('pe', lambda e: e.matmul(pOw[:, h * 65:(h + 1) * 65], lhsT=t_[:, h * 128:(h + 1) * 128], rhs=VWX[:, kt, :],
                                                      start=False, stop=(kt == m), skip_group_check=True),
                             reads=[tk, 'vw_dX'], writes=['pOw'])
                posv = pOs[:, 0:260].rearrange("p (h d) -> p h d", d=65)
                powv = pOw[:, 0:260].rearrange("p (h d) -> p h d", d=65)
                P.op('dve', lambda e: e.tensor_copy(out=zz[:, 1, :], in_=posv[:, :, 64]), reads=['pOs'], writes=['zz1'])
                P.op('dve', lambda e: e.tensor_copy(out=zz[:, 2, :], in_=powv[:, :, 64]), reads=['pOw'], writes=['zz1'])
                P.op('dve', lambda e: e.reciprocal(out=rzz[:, 1:3, :], in_=zz[:, 1:3, :]), reads=['zz1'], writes=['rzz1'])
                P.op('dve', lambda e: e.tensor_tensor(out=coef[:], in0=rzz[:], in1=NG[:, m, :].rearrange("p (b h) -> p b h", b=3), op=ALU.mult),
                     reads=['rzz0', 'rzz1', 'NG'], writes=['coef'])
                for h in range(4):
                    P.op('dve', lambda e: e.tensor_scalar(out=yacc[:, h, :], in0=pocv[:, h, 0:64], scalar1=coef[:, 0, h:h + 1], scalar2=None,
                                                          op0=ALU.mult), reads=['pOc', 'coef'], writes=[('yacc', h)])
                    P.op('dve', lambda e: e.scalar_tensor_tensor(out=yacc[:, h, :], in0=posv[:, h, 0:64], scalar=coef[:, 1, h:h + 1],
                                                                 in1=yacc[:, h, :], op0=ALU.mult, op1=ALU.add),
                         reads=['pOs', 'coef', ('yacc', h)], writes=[('yacc', h)])
                    P.op('dve', lambda e: e.scalar_tensor_tensor(out=yn[b2][:, h * 64:(h + 1) * 64], in0=powv[:, h, 0:64],
                                                                 scalar=coef[:, 2, h:h + 1], in1=yacc[:, h, :], op0=ALU.mult, op1=ALU.add),
                         reads=['pOw', 'coef', ('yacc', h)], writes=[('yn', b2)])
                P.dma('sp', self.mix_d[t0:t0 + 128, 768:1024], yn[b2][:], reads=[('yn', b2)], writes=[('mixn', m)])
            self.barrier()

    def build_all(self):
        self.declare()
        for l in range(self.depth):
            xsrc = self.x_in if l == 0 else self.xbuf
            xdst = self.y_out if l == self.depth - 1 else self.xbuf
            self.phase1(l, xsrc)
            if os.environ.get("PH2", "old") == "new":
                self.phase2(l)
            elif os.environ.get("PH2", "old") == "b":
                self.phase2_moba(l)
                self.phase2b(l)
            else:
                self.phase2_mlstm(l)
                self.phase2_moba(l)
                self.phase2_nsa2(l)
            self.phase3(l, xsrc, xdst)
        self.P.finish()
        return self.nc


def make_consts(S):
    bf = ml_dtypes.bfloat16
    half = 8
    inv = np.exp(-math.log(500000.0) * np.arange(half, dtype=np.float32) * (2.0 / 16)).astype(np.float32)
    ang = np.arange(S, dtype=np.float32)[:, None] * inv[None, :]
    d = dict(c_identb=np.eye(128, dtype=np.float32).astype(bf), c_identf=np.eye(128, dtype=np.float32),
             c_cos=np.cos(ang).astype(np.float32), c_sin=np.sin(ang).astype(np.float32))
    i = np.arange(128)
    d['c_ut'] = (i[:, None] < i[None, :]).astype(np.float32)
    d['c_tri'] = (i[:, None] <= i[None, :]).astype(np.float32)
    key = np.arange(S)
    d['c_ind32'] = (key[None, :] // 256 == np.arange(32)[:, None]).astype(np.float32).astype(bf)
    d['c_ind64'] = (((key[None, :] // 64) % 64) == np.arange(64)[:, None]).astype(np.float32).astype(bf)
    k = np.arange(128)[:, None, None]
    o = np.arange(4)[None, :, None]
    q = np.arange(512)[None, None, :]
    d['c_tm4'] = (q >= k + 128 * o).astype(np.float32).astype(bf)
    d['c_trib'] = (i[:, None] <= i[None, :]).astype(np.float32).astype(bf)
    d['c_triw'] = (i[:, None] > i[None, :]).astype(np.float32).astype(bf)
    ii = np.arange(128)[:, None, None]
    r = np.arange(17)[None, :, None]
    j = np.arange(128)[None, None, :]
    d['c_cmask'] = (16 * ii + 31 <= 128 * r + j).astype(np.float32).astype(bf)
    c = np.arange(512)[:, None]
    n = np.arange(128)[None, :]
    Nc = S // 16 - 1
    d['c_ovl'] = ((c >= 4 * n - 1) & (c <= 4 * n + 3) & (c < Nc)).astype(np.float32).astype(bf)
    cols = np.zeros((128, 4), np.float32)
    cols[:64, 0] = -1e30
    cols[64:, 0] = 1e9
    cols[:64, 1] = 1e9
    cols[64:, 1] = -1e30
    cols[:64, 2] = NEGM
    cols[64:, 2] = 0.0
    d['c_cols'] = cols
    return d


_CACHE = {}


def kernel(x, w_in, b_if, conv_qk, m_norm, moba_qk_norm, nsa_q_norm, nsa_k_norm,
           cmp_pe, cmp_w1, cmp_w2, w_out, norm_mix, norm_ffn, w_ff1, w_ff2):
    x = np.asarray(x, dtype=np.float32)
    Bsz, S, _ = x.shape
    depth = int(np.asarray(w_in).shape[0])
    n_cores = 8
    shared = dict(w_in=w_in, b_if=b_if, conv_qk=conv_qk, m_norm=m_norm, moba_qk_norm=moba_qk_norm,
                  nsa_q_norm=nsa_q_norm, nsa_k_norm=nsa_k_norm, cmp_pe=cmp_pe, cmp_w1=cmp_w1, cmp_w2=cmp_w2,
                  w_out=w_out, norm_mix=norm_mix, norm_ffn=norm_ffn, w_ff1=w_ff1, w_ff2=w_ff2)
    shared = {k: np.ascontiguousarray(np.asarray(v, dtype=np.float32)) for k, v in shared.items()}
    shared.update(make_consts(S))
    B = Builder(S, depth)
    nc = B.build_all()
    in_maps = []
    for c in range(n_cores):
        m = dict(shared)
        m['x'] = np.ascontiguousarray(x[c % Bsz])
        in_maps.append(m)
    res = run_bass_kernel_spmd(nc, in_maps, core_ids=list(range(n_cores)))
    out = np.stack([np.asarray(res.results[b]["y"], dtype=np.float32) for b in range(Bsz)], axis=0)
    return out


class BG:
    def __init__(self):
        self.gens = []
        self.acc = 0.0

    def add(self, g):
        self.gens.append(g)

    def step(self, rate=1.0):
        self.acc += rate
        while self.acc >= 1.0:
            self.acc -= 1.0
            for g in list(self.gens):
                try:
                    next(g)
                except StopIteration:
                    self.gens.remove(g)

    def drain(self, g=None):
        if g is None:
            while self.gens:
                self.step(1.0)
        else:
            while g in self.gens:
                try:
                    next(g)
                except StopIteration:
                    self.gens.remove(g)


def _phase2(self, l):
    S, P, nc = self.S, self.P, self.nc
    NCH = S // 64
    NT = S // 128
    NQC = S // 512
    NB = S // 256
    Nc = S // 16 - 1
    NCT = max(1, S // 2048)
    LNSC = math.log(128.0 ** -0.5)
    sb, ps = self.sb, self.ps
    allt = list(range(NT))
    with ExitStack() as st:
        identb = sb(st, "identb2", [128, 128], BF16)
        identf = sb(st, "identf2", [128, 128], F32)
        ut = sb(st, "ut2", [128, 128], F32)
        tri = sb(st, "tri2", [128, 128], F32)
        zl = sb(st, "zl2", [1, 128], BF16)
        zr = sb(st, "zr2", [1, 512], BF16)
        P.dma('sp', identb[:], self.c_identb, writes=['identb'])
        P.dma('sp', identf[:], self.c_identf, writes=['identf'])
        P.dma('sp', ut[:], self.c_ut, writes=['ut'])
        P.dma('sp', tri[:], self.c_tri, writes=['tri'])
        P.op('pool', lambda e: e.memset(zl[:], 0.0), writes=['zl'])
        P.op('pool', lambda e: e.memset(zr[:], 0.0), writes=['zr'])
        uT = sb(st, "uT_m", [64, 4, NCH], F32)
        u2T = sb(st, "u2T_m", [64, 4, NCH], F32)
        flT = sb(st, "flT_m", [64, 4, NCH], F32)
        decB = sb(st, "decB", [128, 4, NCH], F32)
        with ExitStack() as s2:
            li = sb(s2, "li", [NCH, 4, 64], F32)
            lf = sb(s2, "lf", [NCH, 4, 64], F32)
            ones = sb(s2, "ones", [NCH, 64], F32)
            Fin = sb(s2, "Fin", [NCH, 4, 64], F32)
            Ft = sb(s2, "Ft", [NCH, 4, 64], F32)
            a_ = sb(s2, "a_", [NCH, 4, 64], F32)
            Ain = sb(s2, "Ain", [NCH, 4, 64], F32)
            tot = sb(s2, "tot", [NCH, 4], F32)
            cmax = sb(s2, "cmax", [NCH, 4], F32)
            cmT = sb(s2, "cmT", [4, NCH], F32)
            ET = sb(s2, "ET", [4, NCH], F32)
            ETn = sb(s2, "ETn", [4, NCH], F32)
            Ec = sb(s2, "Ec", [NCH, 4], F32)
            Enc = sb(s2, "Enc", [NCH, 4], F32)
            tmp = sb(s2, "tmpm", [NCH, 4, 64], F32)
            uu = sb(s2, "uu", [NCH, 4, 64], F32)
            uu2 = sb(s2, "uu2", [NCH, 4, 64], F32)
            fl = sb(s2, "fl", [NCH, 4, 64], F32)
            dec = sb(s2, "dec", [NCH, 4], F32)
            decrep = sb(s2, "decrep", [NCH, 4, 128], F32)
            pa = ps(s2, "pa", [128, 512], F32)
            pb = ps(s2, "pb", [128, 512], F32)
            P.dma('sp', li[:], self.gi_d.rearrange("h (c j) -> c h j", j=64),
                  reads=[('gi_d', b) for b in range(S // 512)], writes=['li'])
            P.dma('sp', lf[:], self.gf_d.rearrange("h (c j) -> c h j", j=64),
                  reads=[('gf_d', b) for b in range(S // 512)], writes=['lf'])
            P.op('pool', lambda e: e.memset(ones[:], 1.0), writes=['ones'])
            for hh in range(4):
                P.op('dve', lambda e: e.tensor_tensor_scan(out=Fin[:, hh, :], data0=ones[:], data1=lf[:, hh, :],
                                                           initial=0.0, op0=ALU.mult, op1=ALU.add),
                     reads=['ones', 'lf'], writes=['Fin'])
            P.op('dve', lambda e: e.tensor_copy(out=tot[:], in_=Fin[:, :, 63]), reads=['Fin'], writes=['tot'])
            P.op('pe', lambda e: e.matmul(pa[0:NCH, 0:4], lhsT=ut[0:NCH, 0:NCH], rhs=tot[:], start=True, stop=True),
                 reads=['ut', 'tot'], writes=['pa'])
            P.op('dve', lambda e: e.tensor_tensor(out=Ft[:], in0=Fin[:],
                                                  in1=pa[0:NCH, 0:4].unsqueeze(2).broadcast_to([NCH, 4, 64]), op=ALU.add),
                 reads=['Fin', 'pa'], writes=['Ft'])
            P.op('dve', lambda e: e.tensor_tensor(out=a_[:], in0=li[:], in1=Ft[:], op=ALU.subtract),
                 reads=['li', 'Ft'], writes=['a_'])
            for hh in range(4):
                P.op('dve', lambda e: e.tensor_tensor_scan(out=Ain[:, hh, :], data0=a_[:, hh, :], data1=a_[:, hh, :],
                                                           initial=-1e30, op0=ALU.max, op1=ALU.max),
                     reads=['a_'], writes=['Ain'])
            P.op('dve', lambda e: e.tensor_copy(out=cmax[:], in_=Ain[:, :, 63]), reads=['Ain'], writes=['cmax'])
            P.op('pe', lambda e: e.transpose(out=pb[0:4, 0:NCH], in_=cmax[:], identity=identf[0:NCH, 0:NCH]),
                 reads=['cmax', 'identf'], writes=['pb'])
            P.op('dve', lambda e: e.tensor_copy(out=cmT[:], in_=pb[0:4, 0:NCH]), reads=['pb'], writes=['cmT'])
            P.op('dve', lambda e: e.tensor_tensor_scan(out=ET[:], data0=cmT[:], data1=cmT[:], initial=0.0,
                                                       op0=ALU.max, op1=ALU.max), reads=['cmT'], writes=['ET'])
            if NCH > 1:
                P.op('dve', lambda e: e.tensor_copy(out=ETn[:, 0:NCH - 1], in_=ET[:, 1:NCH]), reads=['ET'], writes=['ETn'])
            P.op('dve', lambda e: e.tensor_copy(out=ETn[:, NCH - 1:NCH], in_=ET[:, NCH - 1:NCH]), reads=['ET'], writes=['ETn'])
            P.op('pe', lambda e: e.transpose(out=pa[0:NCH, 0:4], in_=ET[:], identity=identf[0:4, 0:4]),
                 reads=['ET', 'identf'], writes=['pa'])
            P.op('dve', lambda e: e.tensor_copy(out=Ec[:], in_=pa[0:NCH, 0:4]), reads=['pa'], writes=['Ec'])
            P.op('pe', lambda e: e.transpose(out=pb[0:NCH, 0:4], in_=ETn[:], identity=identf[0:4, 0:4]),
                 reads=['ETn', 'identf'], writes=['pb'])
            P.op('dve', lambda e: e.tensor_copy(out=Enc[:], in_=pb[0:NCH, 0:4]), reads=['pb'], writes=['Enc'])
            Eb = Ec[:].unsqueeze(2).broadcast_to([NCH, 4, 64])
            Enb = Enc[:].unsqueeze(2).broadcast_to([NCH, 4, 64])
            P.op('dve', lambda e: e.tensor_tensor(out=tmp[:], in0=a_[:], in1=Eb, op=ALU.subtract),
                 reads=['a_', 'Ec'], writes=['tmp'])
            P.op('act', lambda e: e.activation(out=uu[:], in_=tmp[:], func=AF.Exp, bias=LNSC), reads=['tmp'], writes=['uu'])
            P.op('dve', lambda e: e.tensor_tensor(out=tmp[:], in0=a_[:], in1=Enb, op=ALU.subtract),
                 reads=['a_', 'Enc'], writes=['tmp'])
            P.op('act', lambda e: e.activation(out=uu2[:], in_=tmp[:], func=AF.Exp, bias=LNSC), reads=['tmp'], writes=['uu2'])
            P.op('dve', lambda e: e.tensor_tensor(out=tmp[:], in0=Ft[:], in1=Eb, op=ALU.add),
                 reads=['Ft', 'Ec'], writes=['tmp'])
            P.op('act', lambda e: e.activation(out=fl[:], in_=tmp[:], func=AF.Exp, scale=-1.0), reads=['tmp'], writes=['fl'])
            P.op('dve', lambda e: e.tensor_tensor(out=dec[:], in0=Ec[:], in1=Enc[:], op=ALU.subtract),
                 reads=['Ec', 'Enc'], writes=['dec'])
            P.op('act', lambda e: e.activation(out=dec[:], in_=dec[:], func=AF.Exp), reads=['dec'], writes=['dec'])
            P.op('dve', lambda e: e.tensor_copy(out=decrep[:], in_=dec[:].unsqueeze(2).broadcast_to([NCH, 4, 128])),
                 reads=['dec'], writes=['decrep'])
            for src, dst, nm in ((uu, uT, 'uT'), (uu2, u2T, 'u2T'), (fl, flT, 'flT')):
                for hh in range(4):
                    P.op('pe', lambda e: e.transpose(out=pa[0:64, hh * 128:hh * 128 + NCH], in_=src[:, hh, :],
                                                     identity=identf[0:NCH, 0:NCH]),
                         reads=['uu', 'uu2', 'fl', 'identf'], writes=['pa'])
                P.op('dve', lambda e: e.tensor_copy(out=dst[:], in_=pa[0:64, :].rearrange("p (h c) -> p h c", h=4)[:, :, 0:NCH]),
                     reads=['pa'], writes=[nm])
            for hh in range(4):
                P.op('pe', lambda e: e.matmul(pb[:, hh * 128:hh * 128 + NCH], lhsT=decrep[:, hh, :],
                                              rhs=identf[0:NCH, 0:NCH], start=True, stop=True),
                     reads=['decrep', 'identf'], writes=['pb'])
            P.op('dve', lambda e: e.tensor_copy(out=decB[:], in_=pb[:].rearrange("p (h c) -> p h c", h=4)[:, :, 0:NCH]),
                 reads=['pb'], writes=['decB'])
            self.barrier()

        pSr = [ps(st, "pSr", [128, 512], F32) for _ in range(3)]
        pMi = ps(st, "pMi", [128, 512], F32)
        pMb = pMi[:, 128:192].bitcast(BF16)
        pMb2 = pMi[:, 192:256].bitcast(BF16)
        pM = pMi[:, 256:288]
        bg = BG()

        sm = ExitStack()
        st = sm
        GC = 4
        NG = S // (64 * GC)
        TG = 64 * GC
        qg = [sb(st, "qg", [128, 4, TG], BF16) for _ in range(2)]
        kg = [sb(st, "kg", [128, 4, TG], BF16) for _ in range(2)]
        vg = [sb(st, "vg", [64, GC, 4, 129], BF16) for _ in range(2)]
        ogg = [sb(st, "ogg", [64, GC, 512], BF16) for _ in range(2)]
        ym = [sb(st, "ym", [64, GC, 512], BF16) for _ in range(2)]
        G = [sb(st, "G", [128, 129], F32) for _ in range(4)]
        Gb = [sb(st, "Gb", [128, 129], BF16) for _ in range(4)]
        ku2 = [sb(st, "ku2", [64, 128], BF16) for _ in range(2)]
        Sm = [sb(st, "Sm", [64, 64], BF16) for _ in range(2)]
        junkm = sb(st, "junkm", [64, 128], BF16)
        scm = [sb(st, "scm", [64, 8], F32) for _ in range(2)]
        mlA = [ps(st, "mlA", [128, 512], F32) for _ in range(2)]

        def gen_ml():
            for i in range(2):
                P.op('pool', lambda e: e.memset(vg[i][:], 1.0), writes=[('vg', i)])
            for hh in range(4):
                P.op('pool', lambda e: e.memset(G[hh][:], 0.0), writes=[('G', hh)])
                P.op('pool', lambda e: e.memset(Gb[hh][:], 0.0), writes=[('Gb', hh)])
            yield

            def load_group(g):
                i = g % 2
                tk = slice(g * TG, (g + 1) * TG)
                bl = sorted(set([(g * TG) // 512, ((g + 1) * TG - 1) // 512]))
                jl = list(range((g * TG) // 128, ((g + 1) * TG + 127) // 128))
                P.dma('sp', qg[i][:], self.qkT[0:512, tk].rearrange("(h p) t -> p h t", p=128),
                      reads=[('qkT', ft, b) for ft in range(4) for b in bl], writes=[('qg', i)])
                P.dma('sp', kg[i][:], self.qkT[512:1024, tk].rearrange("(h p) t -> p h t", p=128),
                      reads=[('qkT', ft, b) for ft in range(4, 8) for b in bl], writes=[('kg', i)])
                for hh in range(4):
                    P.dma('sp', vg[i][:, :, hh, 0:128],
                          self.mv_d[tk, hh * 128:(hh + 1) * 128].rearrange("(c s) e -> s c e", s=64),
                          reads=[('mv_d', j) for j in jl], writes=[('vg', i)])
                P.dma('sp', ogg[i][:], self.og_d[tk, :].rearrange("(c s) e -> s c e", s=64),
                      reads=[('og_d', j) for j in jl], writes=[('ogg', i)])
            load_group(0)
            it = 0
            for g in range(NG):
                if g + 1 < NG:
                    load_group(g + 1)
                gi_ = g % 2
                for cl in range(GC):
                    c = g * GC + cl
                    for hh in range(4):
                        k_ = kg[gi_][:, hh, cl * 64:(cl + 1) * 64]
                        q_ = qg[gi_][:, hh, cl * 64:(cl + 1) * 64]
                        v_ = vg[gi_][:, cl, hh, :]
                        i2 = it % 2
                        it += 1
                        A = mlA[i2]
                        pS_ = A[0:64, 0:64]
                        pO_ = A[0:64, 64:193]
                        pG_ = A[:, 256:385]
                        pkT_ = A[0:64, 448:512].bitcast(BF16)
                        P.op('pe', lambda e: e.transpose(out=pkT_, in_=k_, identity=identb[:]),
                             reads=[('kg', gi_), 'identb'], writes=[('mlA', i2)])
                        P.op('pe', lambda e: e.matmul(pS_, lhsT=k_, rhs=q_, start=True, stop=True),
                             reads=[('kg', gi_), ('qg', gi_)], writes=[('mlA', i2)])
                        yield
                        P.op('act', lambda e: e.activation(out=ku2[i2][:], in_=pkT_, func=AF.Copy,
                                                           scale=u2T[:, hh, c:c + 1]),
                             reads=[('mlA', i2), 'u2T'], writes=[('ku2', i2)])
                        P.op('dve', lambda e: e.scalar_tensor_tensor(out=Sm[i2][:], in0=pS_,
                                                                     scalar=uT[:, hh, c:c + 1], in1=tri[0:64, 0:64],
                                                                     op0=ALU.mult, op1=ALU.mult),
                             reads=[('mlA', i2), 'uT', 'tri'], writes=[('Sm', i2)])
                        yield
                        P.op('pe', lambda e: e.matmul(pO_, lhsT=Sm[i2][:], rhs=v_, start=True, stop=False),
                             reads=[('Sm', i2), ('vg', gi_)], writes=[('mlA', i2)])
                        P.op('pe', lambda e: e.matmul(pO_, lhsT=q_, rhs=Gb[hh][:], start=False, stop=True),
                             reads=[('qg', gi_), ('Gb', hh)], writes=[('mlA', i2)])
                        P.op('pe', lambda e: e.matmul(pG_, lhsT=ku2[i2][:], rhs=v_, start=True, stop=True),
                             reads=[('ku2', i2), ('vg', gi_)], writes=[('mlA', i2)])
                        yield
                        P.op('dve', lambda e: e.scalar_tensor_tensor(out=G[hh][:], in0=G[hh][:], scalar=decB[:, hh, c:c + 1],
                                                                     in1=pG_, op0=ALU.mult, op1=ALU.add),
                             reads=[('G', hh), 'decB', ('mlA', i2)], writes=[('G', hh)])
                        P.op('pool', lambda e: e.tensor_copy(out=Gb[hh][:], in_=G[hh][:]), reads=[('G', hh)], writes=[('Gb', hh)])
                        s_ = scm[i2]
                        sk = ('scm', i2)
                        P.op('act', lambda e: e.activation(out=junkm[:], in_=pO_[:, 0:128], func=AF.Square,
                                                           scale=128.0 ** -0.5, accum_out=s_[:, 0:1]),
                             reads=[('mlA', i2)], writes=['junkm', sk])
                        yield
                        P.op('dve', lambda e: e.tensor_scalar(out=s_[:, 6:7], in0=pO_[:, 128:129], scalar1=-1.0,
                                                              scalar2=flT[:, hh, c:c + 1], op0=ALU.mult, op1=ALU.max),
                             reads=[('mlA', i2), 'flT'], writes=[sk])
                        P.op('dve', lambda e: e.tensor_tensor(out=s_[:, 1:2], in0=s_[:, 6:7], in1=pO_[:, 128:129], op=ALU.max),
                             reads=[('mlA', i2), sk], writes=[sk])
                        P.op('dve', lambda e: e.tensor_tensor(out=s_[:, 2:3], in0=s_[:, 1:2], in1=s_[:, 1:2], op=ALU.mult),
                             reads=[sk], writes=[sk])
                        P.op('dve', lambda e: e.scalar_tensor_tensor(out=s_[:, 3:4], in0=s_[:, 2:3], scalar=EPS, in1=s_[:, 0:1],
                                                                     op0=ALU.mult, op1=ALU.add), reads=[sk], writes=[sk])
                        yield
                        P.op('act', lambda e: e.activation(out=s_[:, 4:5], in_=s_[:, 3:4], func=AF.Ln), reads=[sk], writes=[sk])
                        P.op('act', lambda e: e.activation(out=s_[:, 5:6], in_=s_[:, 4:5], func=AF.Exp, scale=-0.5), reads=[sk], writes=[sk])
                        P.op('dve', lambda e: e.scalar_tensor_tensor(out=ym[gi_][:, cl, hh * 128:(hh + 1) * 128],
                                                                     in0=pO_[:, 0:128], scalar=s_[:, 5:6],
                                                                     in1=ogg[gi_][:, cl, hh * 128:(hh + 1) * 128],
                                                                     op0=ALU.mult, op1=ALU.mult),
                             reads=[('mlA', i2), sk, ('ogg', gi_)], writes=[('ym', gi_)])
                        yield
                P.dma('sp', self.mix_d[g * TG:(g + 1) * TG, 0:512].rearrange("(c s) e -> s c e", s=64), ym[gi_][:],
                      reads=[('ym', gi_)], writes=[('mixm', g)])
        ML_YIELDS = NCH * 4 * 6 + 1
        g_ml = gen_ml()
        if not os.environ.get("SKIP_ML"):
            bg.add(g_ml)

        with sm:
            tm4 = sb(sm, "tm4", [128, 4, 512], BF16)
            P.dma('sp', tm4[:], self.c_tm4, writes=['tm4'])
            negB = self._negB(sm, "negBm", self.moba_qk_norm[l, 0], self.moba_qk_norm[l, 1])
            KX = [sb(sm, "KX", [96, S], BF16) for _ in range(2)]
            QXA = [sb(sm, "QXA", [96, S], BF16) for _ in range(2)]
            VX = [sb(sm, "VX", [128, NT, 65], BF16) for _ in range(2)]
            kmean = sb(sm, "kmean", [64, 32], F32)
            kmb = [sb(sm, "kmb", [64, 32], BF16) for _ in range(2)]
            gsb = [sb(sm, "gsb_b", [128, 32], F32) for _ in range(2)]
            m8 = [sb(sm, "m8", [128, 8], F32) for _ in range(2)]
            sel = [sb(sm, "sel_b", [128, 32], F32) for _ in range(2)]
            MBw = [sb(sm, "MBw", [128, 128], BF16) for _ in range(2)]
            PT = [sb(sm, "PT", [128, 512], BF16) for _ in range(4)]
            rz = sb(sm, "rz_b", [128, 4], F32)
            yb = [sb(sm, "yb", [128, 4, 64], BF16) for _ in range(2)]
            pO = [ps(sm, "pO_b", [128, 512], F32) for _ in range(1)] * 2
            if os.environ.get("NO_BITCAST"):
                pMb = ps(sm, "pMbx", [128, 128], BF16)[:]
            for i in range(2):
                P.dma('sp', KX[i][64:96, :], self.c_ind32, writes=[('KXi', i)])
                P.op('pool', lambda e: e.memset(VX[i][:], 1.0), writes=[('VX', i)])
                P.op('pool', lambda e: e.memset(MBw[i][:], 0.0), writes=[('MBw', i)])
            P.op('pool', lambda e: e.memset(kmean[:], 0.0), writes=['kmean'])

            def gen_prep(h):
                i = h % 2
                P.dma('sp', KX[i][0:64, :], self.bkT[h * 64:(h + 1) * 64, :], reads=[('bkT', j) for j in allt], writes=[('KX', i)])
                self.dma_mid(VX[i][:, :, 0:64], self.bv_d[:, h * 64:(h + 1) * 64].rearrange("(kt p) d -> p kt d", p=128), NT, 8,
                             reads=[('bv_d', j) for j in allt], writes=[('VX', i)])
                P.dma('sp', QXA[i][0:64, :], self.bqT[h * 64:(h + 1) * 64, :], reads=[('bqT', j) for j in allt], writes=[('QXAq', i)])
                yield
                P.op('dve', lambda e: e.tensor_reduce(out=kmean[:, 0:NB], in_=KX[i][0:64, :].rearrange("p (n k) -> p n k", k=256),
                                                      axis=AX.X, op=ALU.add), reads=[('KX', i)], writes=['kmean'])
                P.op('dve', lambda e: e.tensor_scalar(out=kmb[i][:], in0=kmean[:], scalar1=1.0 / 256, scalar2=None, op0=ALU.mult),
                     reads=['kmean'], writes=[('kmb', i)])
                yield
                for jt in range(NT if os.environ.get("BIS", "0") != "2" else 0):
                    own = jt // 2
                    b_ = jt % 2
                    P.op('pe', lambda e: e.matmul(pM, lhsT=QXA[i][0:64, jt * 128:(jt + 1) * 128], rhs=kmb[i][:],
                                                  start=True, stop=True), reads=[('QXAq', i), ('kmb', i)], writes=['pMi'])
                    P.op('pool', lambda e: e.memset(gsb[b_][:], -1e30), writes=[('gsb', b_)])
                    if own > 0:
                        P.op('dve', lambda e: e.tensor_copy(out=gsb[b_][:, 0:own], in_=pM[:, 0:own]), reads=['pMi'], writes=[('gsb', b_)])
                    yield
                    P.op('dve', lambda e: e.max(out=m8[b_][:], in_=gsb[b_][:]), reads=[('gsb', b_)], writes=[('m8', b_)])
                    P.op('dve', lambda e: e.tensor_scalar(out=sel[b_][:], in0=gsb[b_][:], scalar1=m8[b_][:, 2:3], scalar2=None,
                                                          op0=ALU.is_ge), reads=[('gsb', b_), ('m8', b_)], writes=[('sel', b_)])
                    P.op('dve', lambda e: e.tensor_scalar(out=MBw[b_][:, 64:96], in0=sel[b_][:], scalar1=-NEGM, scalar2=NEGM,
                                                          op0=ALU.mult, op1=ALU.add), reads=[('sel', b_)], writes=[('MBw', b_)])
                    P.op('dve', lambda e: e.memset(MBw[b_][:, 64 + own:65 + own], 0.0), writes=[('MBw', b_)])
                    if own + 1 < 32:
                        P.op('dve', lambda e: e.memset(MBw[b_][:, 65 + own:96], NEGM), writes=[('MBw', b_)])
                    yield
                    P.op('pe', lambda e: e.transpose(out=pMb, in_=MBw[b_][:], identity=identb[:]),
                         reads=[('MBw', b_), 'identb'], writes=['pMi'])
                    P.op('act', lambda e: e.copy(out=QXA[i][64:96, jt * 128:(jt + 1) * 128], in_=pMb[64:96, :]),
                         reads=['pMi'], writes=[('QXAm', i, jt)])
                    yield
            PREP_YIELDS = 3 * NT + 2
            main_iters_h = sum(4 * qc + 4 for qc in range(NQC))
            ml_rate = ML_YIELDS / float(4 * main_iters_h) * 1.15
            prep_rate = PREP_YIELDS / float(main_iters_h) * 1.3
            g0 = gen_prep(0)
            for _ in g0:
                bg.step(1.0)
            si = 0
            pi = 0
            for h in range(4):
                i = h % 2
                gp = None
                pacc = 0.0
                if h + 1 < 4:
                    gp = gen_prep(h + 1)
                    if os.environ.get("NOINT"):
                        for _ in gp:
                            pass
                        gp = None
                for qc in range(min(NQC, int(os.environ.get("QCMAX", "99"))) if (os.environ.get("BIS", "0") == "0" and h < int(os.environ.get("HMAX", "9"))) else 0):
                    q0 = qc * 512
                    Q_ = QXA[i][:, q0:q0 + 512]
                    qkeys = [('QXAq', i)] + [('QXAm', i, 4 * qc + j) for j in range(4)]
                    po = pO[pi % 2]
                    pok = ('pO', pi % 2)
                    pi += 1
                    P.op('pe', lambda e: e.matmul(po[:, 0:260], lhsT=zl[:], rhs=zr[:, 0:260], start=True, stop=True,
                                                  skip_group_check=True), reads=['zl', 'zr'], writes=[pok])
                    nkt = 4 * qc + 4
                    base = si

                    def qk(kt):
                        p_ = pSr[(base + kt) % 3]
                        P.op('pe', lambda e: e.matmul(p_[:], lhsT=KX[i][:, kt * 128:(kt + 1) * 128], rhs=Q_, start=True, stop=True),
                             reads=[('KX', i), ('KXi', i)] + qkeys, writes=[('pSr', (base + kt) % 3)])
                    LA = int(os.environ.get("LA", "2"))
                    for kt in range(min(LA, nkt)):
                        qk(kt)
                    for kt in range(nkt):
                        if kt + LA < nkt:
                            qk(kt + LA)
                        p_ = pSr[(base + kt) % 3]
                        pk = ('pSr', (base + kt) % 3)
                        t_ = PT[(base + kt) % 4]
                        tk = ('PT', (base + kt) % 4)
                        P.op('act', lambda e: e.activation(out=t_[:], in_=p_[:], func=AF.Exp, bias=negB[:, 0:1], scale=0.125),
                             reads=[pk, 'negBm'], writes=[tk])
                        off = kt - 4 * qc
                        if off >= 0:
                            P.op('pool', lambda e: e.tensor_tensor(out=t_[:], in0=t_[:], in1=tm4[:, off, :], op=ALU.mult),
                                 reads=[tk, 'tm4'], writes=[tk])
                        for j in range(4):
                            if kt > 4 * qc + j:
                                continue
                            P.op('pe', lambda e: e.matmul(po[:, j * 65:(j + 1) * 65], lhsT=t_[:, j * 128:(j + 1) * 128],
                                                          rhs=VX[i][:, kt, :], start=False, stop=(kt == 4 * qc + j),
                                                          skip_group_check=True), reads=[tk, ('VX', i)], writes=[pok])
                        bg.step(ml_rate)
                        if gp is not None:
                            pacc += prep_rate
                            while pacc >= 1.0:
                                pacc -= 1.0
                                try:
                                    next(gp)
                                except StopIteration:
                                    gp = None
                                    break
                    si += nkt
                    y_ = yb[qc % 2]
                    yk = ('yb', qc % 2)
                    pov = po[:, 0:260].rearrange("p (j d) -> p j d", d=65)
                    P.op('dve', lambda e: e.reciprocal(out=rz[:], in_=pov[:, :, 64]), reads=[pok], writes=['rz'])
                    P.op('dve', lambda e: e.tensor_tensor(out=y_[:], in0=pov[:, :, 0:64],
                                                          in1=rz[:].unsqueeze(2).broadcast_to([128, 4, 64]), op=ALU.mult),
                         reads=[pok, 'rz'], writes=[yk])
                    P.dma('sp', self.mix_d[q0:q0 + 512, 512 + h * 64:512 + (h + 1) * 64].rearrange("(j p) d -> p j d", p=128),
                          y_[:], reads=[yk], writes=[('mixb', h, qc)])
                if gp is not None:
                    for _ in gp:
                        bg.step(1.0)
            bg.drain()
            self.barrier()
        if not os.environ.get("SKIP_NSA"):
            self._phase2_nsa_pipe(l, pSr, pMi, identb, zl, zr)
        self.barrier()


Builder.phase2 = _phase2


def _phase2_nsa_pipe(self, l, pSr, pMi, identb, zl, zr, bgen=None, bg_yields=0):
    S, P, nc = self.S, self.P, self.nc
    NT = S // 128
    Nc = S // 16 - 1
    NCT = max(1, S // 2048)
    sb, ps = self.sb, self.ps
    allt = list(range(NT))
    pMb = pMi[:, 128:192].bitcast(BF16)
    pMb2 = pMi[:, 192:256].bitcast(BF16)
    pM = pMi[:, 256:320]
    with ExitStack() as st:
        trib = sb(st, "trib", [128, 128], BF16)
        triw = sb(st, "triw", [128, 128], BF16)
        cmask = sb(st, "cmask", [128, 17, 128], BF16)
        OVL = sb(st, "OVL", [128, NCT, 128], BF16)
        cols = sb(st, "cols", [128, 4], F32)
        NG = sb(st, "NG", [128, NT, 12], F32)
        P.dma('sp', trib[:], self.c_trib, writes=['trib'])
        P.dma('sp', triw[:], self.c_triw, writes=['triw'])
        P.dma('sp', cmask[:], self.c_cmask, writes=['cmask'])
        P.dma('sp', OVL[:], self.c_ovl.rearrange("(ct p) n -> p ct n", p=128)[:, 0:NCT, :], writes=['OVL'])
        P.dma('sp', cols[:], self.c_cols, writes=['cols'])
        self.dma_mid(NG[:], self.ng_d.rearrange("(j p) g -> p j g", p=128), NT, 8, reads=[('ng_d', j) for j in allt], writes=['NG'])
        negBc = self._negB(st, "negBc", self.nsa_q_norm[l], self.nsa_k_norm[l, 0])
        negBs = self._negB(st, "negBs", self.nsa_q_norm[l], self.nsa_k_norm[l, 1])
        negBw = self._negB(st, "negBw", self.nsa_q_norm[l], self.nsa_k_norm[l, 2])
        KSX = sb(st, "KSX", [128, S], BF16)
        KWX = sb(st, "KWX", [64, S], BF16)
        VSX = sb(st, "VSX", [128, NT, 65], BF16)
        VWX = sb(st, "VWX", [128, NT, 65], BF16)
        KcT = sb(st, "KcT", [64, NCT * 128], BF16)
        VCX = sb(st, "VCX", [128, NCT, 65], BF16)
        P.dma('sp', KSX[0:64, :], self.ksT, reads=[('ksT', j) for j in allt], writes=['KSX'])
        P.dma('sp', KSX[64:128, :], self.c_ind64, writes=['KSXi'])
        P.dma('sp', KWX[:], self.kwT, reads=[('kwT', j) for j in allt], writes=['KWX'])
        for V_, src, nm in ((VSX, self.vs_d, 'vs_d'), (VWX, self.vw_d, 'vw_d')):
            P.op('pool', lambda e: e.memset(V_[:], 1.0), writes=[nm + 'X'])
            self.dma_mid(V_[:, :, 0:64], src.rearrange("(kt p) d -> p kt d", p=128), NT, 8,
                         reads=[(nm, j) for j in allt], writes=[nm + 'X'])
        P.op('pool', lambda e: e.memset(VCX[:], 1.0), writes=['VCX'])
        pSc = ps(st, "pSc", [128, 512], F32) if bgen is None else pMi
        pSc_key = 'pSc' if bgen is None else 'pMi'
        pOcU = ps(st, "pOcU", [128, 512], F32)
        pOs = ps(st, "pOs", [128, 512], F32)
        pOw = ps(st, "pOw", [128, 512], F32)
        with ExitStack() as s2:
            KCV = sb(s2, "KCV", [128, S], BF16)
            W1s = sb(s2, "W1s", [128, 32, 128], F32)
            W1 = sb(s2, "W1", [128, 32, 128], BF16)
            pes = sb(s2, "pes", [32, 128], F32)
            peb = sb(s2, "peb", [32, 128], BF16)
            peT = sb(s2, "peT", [128, 32], BF16)
            w2s = sb(s2, "w2s", [128, 2, 64], F32)
            w2 = sb(s2, "w2", [128, 2, 64], BF16)
            gk0 = sb(s2, "gk0", [128, 64], F32)
            bias = sb(s2, "bias_c", [128, 2], F32)
            hidb = sb(s2, "hidb", [128, NCT * 128], BF16)
            kcn = sb(s2, "kcn", [128, 64], BF16)
            junk = sb(s2, "junk_c", [128, 64], F32)
            ssc = sb(s2, "ssc", [128, 2], F32)
            P.dma('sp', KCV[:], self.kcvcT, reads=[('kcvcT', b) for b in range(S // 512)], writes=['KCV'])
            for br in range(2):
                P.dma('sp', W1s[64 * br:64 * br + 64], self.cmp_w1[l, br].rearrange("(r d) j -> d r j", d=64), writes=['W1s'])
                P.dma('sp', w2s[:, br, :], self.cmp_w2[l, br], writes=['w2s'])
                P.dma('sp', pes[:, 64 * br:64 * br + 64], self.cmp_pe[l, br], writes=['pes'])
            P.dma('sp', gk0[:], self.nsa_k_norm[l, 0].partition_broadcast(128), writes=['gk0'])
            P.op('dve', lambda e: e.tensor_copy(out=W1[:], in_=W1s[:]), reads=['W1s'], writes=['W1'])
            P.op('dve', lambda e: e.tensor_copy(out=peb[:], in_=pes[:]), reads=['pes'], writes=['peb'])
            P.op('pe', lambda e: e.transpose(out=pMb[:, 0:32], in_=peb[:], identity=identb[0:32, 0:32]),
                 reads=['peb', 'identb'], writes=['pMi'])
            P.op('dve', lambda e: e.tensor_copy(out=peT[:], in_=pMb[:, 0:32]), reads=['pMi'], writes=['peT'])
            P.op('dve', lambda e: e.tensor_copy(out=w2[:], in_=w2s[:]), reads=['w2s'], writes=['w2'])
            P.op('pool', lambda e: e.memset(hidb[:], 0.0), writes=['hidb'])
            for br in range(2):
                rows = slice(64 * br, 64 * br + 64)
                kview = KCV[rows, :].rearrange("p (c s) -> p c s", s=16)
                for r in range(32):
                    P.op('pe', lambda e: e.matmul(pM[:, 0:1], lhsT=W1[rows, r, :], rhs=peT[rows, r:r + 1],
                                                  start=(r == 0), stop=(r == 31)), reads=['W1', 'peT'], writes=['pMi'])
                P.op('dve', lambda e: e.tensor_copy(out=bias[:, br:br + 1], in_=pM[:, 0:1]), reads=['pMi'], writes=['bias'])
                for r in range(32):
                    rhs = kview[:, 0:Nc, r] if r < 16 else kview[:, 1:Nc + 1, r - 16]
                    P.op('pe', lambda e: e.matmul(pSc[:, 0:Nc], lhsT=W1[rows, r, :], rhs=rhs,
                                                  start=(r == 0), stop=(r == 31)), reads=['W1', 'KCV'], writes=[pSc_key])
                P.op('act', lambda e: e.activation(out=hidb[:, 0:Nc], in_=pSc[:, 0:Nc], func=AF.Silu, bias=bias[:, br:br + 1]),
                     reads=[pSc_key, 'bias'], writes=['hidb'])
                for ct in range(NCT):
                    P.op('pe', lambda e: e.matmul(pM[:, 0:64], lhsT=hidb[:, ct * 128:(ct + 1) * 128], rhs=w2[:, br, :],
                                                  start=True, stop=True), reads=['hidb', 'w2'], writes=['pMi'])
                    if br == 0:
                        P.op('act', lambda e: e.activation(out=junk[:], in_=pM[:, 0:64], func=AF.Square, scale=0.125,
                                                           accum_out=ssc[:, 0:1]), reads=['pMi'], writes=['junk_c', 'ssc'])
                        P.op('dve', lambda e: e.tensor_scalar(out=ssc[:, 0:1], in0=ssc[:, 0:1], scalar1=EPS, scalar2=None,
                                                              op0=ALU.add), reads=['ssc'], writes=['ssc'])
                        P.op('act', lambda e: e.activation(out=ssc[:, 0:1], in_=ssc[:, 0:1], func=AF.Sqrt), reads=['ssc'], writes=['ssc'])
                        P.op('dve', lambda e: e.reciprocal(out=ssc[:, 1:2], in_=ssc[:, 0:1]), reads=['ssc'], writes=['ssc'])
                        P.op('dve', lambda e: e.scalar_tensor_tensor(out=kcn[:], in0=pM[:, 0:64], scalar=ssc[:, 1:2], in1=gk0[:],
                                                                     op0=ALU.mult, op1=ALU.mult),
                             reads=['pMi', 'ssc', 'gk0'], writes=['kcn'])
                        P.op('pe', lambda e: e.transpose(out=pMb[0:64, :], in_=kcn[:], identity=identb[:]),
                             reads=['kcn', 'identb'], writes=['pMi'])
                        P.op('act', lambda e: e.copy(out=KcT[:, ct * 128:(ct + 1) * 128], in_=pMb[0:64, :]),
                             reads=['pMi'], writes=['KcT'])
                    else:
                        P.op('act', lambda e: e.copy(out=VCX[:, ct, 0:64], in_=pM[:, 0:64]), reads=['pMi'], writes=['VCX'])
            self.barrier()
        QU = [sb(st, "QU", [64, 512], BF16) for _ in range(2)]
        QR0 = [sb(st, "QR0", [128, 512], BF16) for _ in range(2)]
        QR1 = [sb(st, "QR1", [128, 512], BF16) for _ in range(2)]
        PTc = [sb(st, "PTc", [128, 512], BF16) for _ in range(NCT)]
        PT = [sb(st, "PTn", [128, 512], BF16) for _ in range(4)]
        zc = sb(st, "zc", [128, 4], F32)
        rzc = sb(st, "rzc", [128, 4], F32)
        zz = sb(st, "zz", [128, 2, 4], F32)
        rzz = sb(st, "rzz", [128, 2, 4], F32)
        coef = sb(st, "coef", [128, 2, 4], F32)
        imp = sb(st, "imp", [128, 128], F32)
        work = sb(st, "work", [128, 128], F32)
        m8a = sb(st, "m8a", [128, 8], F32)
        m8b = sb(st, "m8b", [128, 8], F32)
        selm = sb(st, "selm", [128, 128], F32)
        MB = sb(st, "MB", [128, 128], BF16)
        MBs = sb(st, "MBs", [128, 128], BF16)
        yc = [sb(st, "yc", [128, 4, 64], F32) for _ in range(2)]
        yacc = sb(st, "yacc", [128, 4, 64], F32)
        yn = [sb(st, "yn", [128, 256], BF16) for _ in range(2)]

        def hb(ap):
            return ap.unsqueeze(1).broadcast_to([ap.shape[0], 4, 128])

        def h4(ap):
            return ap.rearrange("p (h t) -> p h t", h=4)

        def gen_prep(m):
            t0 = m * 128
            b2 = m % 2
            qu, qr0, qr1 = QU[b2], QR0[b2], QR1[b2]
            use_g1 = (2 * m + 1) >= 64
            P.dma('sp', h4(qu[:]), self.nquT[:, t0:t0 + 128].rearrange("(h d) t -> d h t", d=64),
                  reads=[('nquT', m)], writes=[('QU', b2)])
            P.dma('sp', h4(qr0[0:64, :]), self.nqrT[:, t0:t0 + 128].rearrange("(h d) t -> d h t", d=64),
                  reads=[('nqrT', m)], writes=[('QR0q', b2)])
            if use_g1:
                P.dma('sp', h4(qr1[0:64, :]), self.nqrT[:, t0:t0 + 128].rearrange("(h d) t -> d h t", d=64),
                      reads=[('nqrT', m)], writes=[('QR1q', b2)])
            yield
            ctn = min(NCT, (8 * m + 6) // 128 + 1)
            for ct in range(ctn):
                P.op('pe', lambda e: e.matmul(pSc[:], lhsT=KcT[:, ct * 128:(ct + 1) * 128], rhs=qu[:], start=True, stop=True),
                     reads=['KcT', ('QU', b2)], writes=[pSc_key])
                yield
                P.op('act', lambda e: e.activation(out=PTc[ct][:], in_=pSc[:], func=AF.Exp, bias=negBc[:, 0:1], scale=0.125),
                     reads=[pSc_key, 'negBc'], writes=[('PTc', ct)])
                yield
                r = m - 16 * ct
                if r <= 16:
                    P.op('dve', lambda e: e.tensor_tensor(out=h4(PTc[ct][:]), in0=h4(PTc[ct][:]), in1=hb(cmask[:, r, :]), op=ALU.mult),
                         reads=[('PTc', ct), 'cmask'], writes=[('PTc', ct)])
                    yield
                yield
            for h in range(4):
                for ct in range(ctn):
                    P.op('pe', lambda e: e.matmul(pOcU[:, h * 65:(h + 1) * 65], lhsT=PTc[ct][:, h * 128:(h + 1) * 128],
                                                  rhs=VCX[:, ct, :], start=(ct == 0), stop=(ct == ctn - 1)),
                         reads=[('PTc', ct), 'VCX'], writes=['pOcU'])
                    yield
                for ct in range(ctn):
                    P.op('pe', lambda e: e.matmul(pOcU[:, 260:388], lhsT=PTc[ct][:, h * 128:(h + 1) * 128],
                                                  rhs=OVL[:, ct, :], start=(ct == 0), stop=(ct == ctn - 1)),
                         reads=[('PTc', ct), 'OVL'], writes=['pOcU'])
                    yield
                P.op('dve', lambda e: e.tensor_scalar(out=zc[:, h:h + 1], in0=pOcU[:, h * 65 + 64:h * 65 + 65], scalar1=1e-30,
                                                      scalar2=None, op0=ALU.max), reads=['pOcU'], writes=['zc'])
                yield
                P.op('dve', lambda e: e.reciprocal(out=rzc[:, h:h + 1], in_=zc[:, h:h + 1]), reads=['zc'], writes=['rzc'])
                yield
                if h == 0:
                    P.op('dve', lambda e: e.tensor_scalar(out=imp[:], in0=pOcU[:, 260:388], scalar1=rzc[:, 0:1], scalar2=None,
                                                          op0=ALU.mult), reads=['pOcU', 'rzc'], writes=['imp'])
                    yield
                else:
                    P.op('dve', lambda e: e.scalar_tensor_tensor(out=imp[:], in0=pOcU[:, 260:388], scalar=rzc[:, h:h + 1],
                                                                 in1=imp[:], op0=ALU.mult, op1=ALU.add),
                         reads=['pOcU', 'rzc', 'imp'], writes=['imp'])
                    yield
                P.op('dve', lambda e: e.tensor_scalar(out=yc[b2][:, h, :], in0=pOcU[:, h * 65:h * 65 + 64], scalar1=rzc[:, h:h + 1],
                                                      scalar2=None, op0=ALU.mult), reads=['pOcU', 'rzc'], writes=[('yc', b2)])
                yield
                yield
            n1 = 2 * m + 1
            if n1 + 1 < 128:
                P.op('dve', lambda e: e.memset(imp[:, n1 + 1:128], -1e30), reads=['imp'], writes=['imp'])
                yield
            P.op('dve', lambda e: e.tensor_copy(out=imp[:, n1:n1 + 1], in_=cols[:, 0:1]), reads=['cols', 'imp'], writes=['imp'])
            yield
            P.op('dve', lambda e: e.memset(imp[:, n1 - 1:n1], 1e9), reads=['imp'], writes=['imp'])
            yield
            if n1 - 2 >= 0:
                P.op('dve', lambda e: e.tensor_tensor(out=imp[:, n1 - 2:n1 - 1], in0=imp[:, n1 - 2:n1 - 1], in1=cols[:, 1:2], op=ALU.max),
                     reads=['cols', 'imp'], writes=['imp'])
                yield
            P.op('dve', lambda e: e.memset(imp[:, 0:1], 1e9), reads=['imp'], writes=['imp'])
            yield
            yield
            P.op('dve', lambda e: e.max(out=m8a[:], in_=imp[:]), reads=['imp'], writes=['m8a'])
            yield
            P.op('dve', lambda e: e.match_replace(out=work[:], in_to_replace=m8a[:], in_values=imp[:], imm_value=-1e30),
                 reads=['imp', 'm8a'], writes=['work'])
            yield
            P.op('dve', lambda e: e.max(out=m8b[:], in_=work[:]), reads=['work'], writes=['m8b'])
            yield
            yield
            P.op('dve', lambda e: e.tensor_scalar(out=selm[:], in0=imp[:], scalar1=m8b[:, 7:8], scalar2=None, op0=ALU.is_ge),
                 reads=['imp', 'm8b'], writes=['selm'])
            yield
            P.op('dve', lambda e: e.tensor_scalar(out=MB[:], in0=selm[:], scalar1=-NEGM, scalar2=NEGM, op0=ALU.mult, op1=ALU.add),
                 reads=['selm'], writes=['MB'])
            yield
            if n1 + 1 < 128:
                P.op('dve', lambda e: e.memset(MB[:, n1 + 1:128], NEGM), reads=['MB'], writes=['MB'])
                yield
            P.op('dve', lambda e: e.tensor_copy(out=MB[:, n1:n1 + 1], in_=cols[:, 2:3]), reads=['cols', 'MB'], writes=['MB'])
            yield
            yield
            P.op('dve', lambda e: e.tensor_copy(out=MBs[:, 0:64], in_=MB[:, 64:128]), reads=['MB'], writes=['MBs'])
            yield
            P.op('dve', lambda e: e.tensor_copy(out=MBs[:, 64:128], in_=MB[:, 0:64]), reads=['MB'], writes=['MBs'])
            yield
            P.op('pe', lambda e: e.transpose(out=pMb, in_=MBs[:], identity=identb[:]), reads=['MBs', 'identb'], writes=['pMi'])
            yield
            P.op('act', lambda e: e.copy(out=h4(qr0[64:128, :]), in_=hb(pMb[64:128, :])), reads=['pMi'], writes=[('QR0m', b2)])
            yield
            if use_g1:
                P.op('pe', lambda e: e.transpose(out=pMb2, in_=MB[:], identity=identb[:]), reads=['MB', 'identb'], writes=['pMi'])
                yield
                P.op('act', lambda e: e.copy(out=h4(qr1[64:128, :]), in_=hb(pMb2[64:128, :])), reads=['pMi'], writes=[('QR1m', b2)])
                yield
            yield

        si = 0
        total_main = sum((m_ + 1) + (m_ + 1 - max(0, m_ - 4)) for m_ in range(NT))
        bg_rate = (bg_yields / float(total_main)) * 1.1 if bgen is not None else 0.0
        bg_state = {'g': bgen, 'acc': 0.0}

        def bg_step():
            if bg_state['g'] is None:
                return
            bg_state['acc'] += bg_rate
            while bg_state['acc'] >= 1.0 and bg_state['g'] is not None:
                bg_state['acc'] -= 1.0
                try:
                    next(bg_state['g'])
                except StopIteration:
                    bg_state['g'] = None
        g = gen_prep(0)
        for _ in g:
            pass
        for m in range(NT):
            t0 = m * 128
            b2 = m % 2
            qr0, qr1 = QR0[b2], QR1[b2]
            gp = gen_prep(m + 1) if m + 1 < NT else None
            P.op('pe', lambda e: e.matmul(pOs[:, 0:260], lhsT=zl[:], rhs=zr[:, 0:260], start=True, stop=True, skip_group_check=True),
                 reads=['zl', 'zr'], writes=['pOs'])
            P.op('pe', lambda e: e.matmul(pOw[:, 0:260], lhsT=zl[:], rhs=zr[:, 0:260], start=True, stop=True, skip_group_check=True),
                 reads=['zl', 'zr'], writes=['pOw'])
            items = [('s', kt) for kt in range(m + 1)] + [('w', kt) for kt in range(max(0, m - 4), m + 1)]
            n_it = len(items)
            rate = 80.0 / n_it
            base = si

            def qk(ix):
                typ, kt = items[ix]
                p_ = pSr[(base + ix) % 3]
                pk = ('pSr', (base + ix) % 3)
                if typ == 's':
                    if kt // 32 == 0:
                        P.op('pe', lambda e: e.matmul(p_[:], lhsT=KSX[:, kt * 128:(kt + 1) * 128], rhs=qr0[:], start=True, stop=True),
                             reads=['KSX', 'KSXi', ('QR0q', b2), ('QR0m', b2)], writes=[pk])
                    else:
                        P.op('pe', lambda e: e.matmul(p_[:], lhsT=KSX[:, kt * 128:(kt + 1) * 128], rhs=qr1[:], start=True, stop=True),
                             reads=['KSX', 'KSXi', ('QR1q', b2), ('QR1m', b2)], writes=[pk])
                else:
                    P.op('pe', lambda e: e.matmul(p_[:], lhsT=KWX[:, kt * 128:(kt + 1) * 128], rhs=qr0[0:64, :], start=True, stop=True),
                         reads=['KWX', ('QR0q', b2)], writes=[pk])
            for ix in range(min(2, n_it)):
                qk(ix)
            pacc = 0.0
            for ix in range(n_it):
                if ix + 2 < n_it:
                    qk(ix + 2)
                typ, kt = items[ix]
                p_ = pSr[(base + ix) % 3]
                pk = ('pSr', (base + ix) % 3)
                t_ = PT[(base + ix) % 4]
                tk = ('PTn', (base + ix) % 4)
                nb_ = negBs if typ == 's' else negBw
                P.op('act', lambda e: e.activation(out=t_[:], in_=p_[:], func=AF.Exp, bias=nb_[:, 0:1], scale=0.125),
                     reads=[pk, 'negBs', 'negBw'], writes=[tk])
                mk = None
                if kt == m:
                    mk = trib
                elif typ == 'w' and kt == m - 4:
                    mk = triw
                if mk is not None:
                    P.op('dve', lambda e: e.tensor_tensor(out=h4(t_[:]), in0=h4(t_[:]), in1=hb(mk[:]), op=ALU.mult),
                         reads=[tk, 'trib', 'triw'], writes=[tk])
                po_, pok, V_, vk = (pOs, 'pOs', VSX, 'vs_dX') if typ == 's' else (pOw, 'pOw', VWX, 'vw_dX')
                for h in range(4):
                    P.op('pe', lambda e: e.matmul(po_[:, h * 65:(h + 1) * 65], lhsT=t_[:, h * 128:(h + 1) * 128], rhs=V_[:, kt, :],
                                                  start=False, stop=(kt == m), skip_group_check=True),
                         reads=[tk, vk], writes=[pok])
                bg_step()
                if gp is not None:
                    pacc += rate
                    while pacc >= 1.0 and gp is not None:
                        pacc -= 1.0
                        try:
                            next(gp)
                        except StopIteration:
                            gp = None
            si += n_it
            if gp is not None:
                for _ in gp:
                    pass
            posv = pOs[:, 0:260].rearrange("p (h d) -> p h d", d=65)
            powv = pOw[:, 0:260].rearrange("p (h d) -> p h d", d=65)
            P.op('dve', lambda e: e.tensor_copy(out=zz[:, 0, :], in_=posv[:, :, 64]), reads=['pOs'], writes=['zz'])
            P.op('dve', lambda e: e.tensor_copy(out=zz[:, 1, :], in_=powv[:, :, 64]), reads=['pOw'], writes=['zz'])
            P.op('dve', lambda e: e.reciprocal(out=rzz[:], in_=zz[:]), reads=['zz'], writes=['rzz'])
            P.op('dve', lambda e: e.tensor_tensor(out=coef[:], in0=rzz[:], in1=NG[:, m, 4:12].rearrange("p (b h) -> p b h", b=2), op=ALU.mult),
                 reads=['rzz', 'NG'], writes=['coef'])
            for h in range(4):
                P.op('dve', lambda e: e.tensor_scalar(out=yacc[:, h, :], in0=yc[b2][:, h, :], scalar1=NG[:, m, h:h + 1], scalar2=None,
                                                      op0=ALU.mult), reads=[('yc', b2), 'NG'], writes=[('yacc', h)])
                P.op('dve', lambda e: e.scalar_tensor_tensor(out=yacc[:, h, :], in0=posv[:, h, 0:64], scalar=coef[:, 0, h:h + 1],
                                                             in1=yacc[:, h, :], op0=ALU.mult, op1=ALU.add),
                     reads=['pOs', 'coef', ('yacc', h)], writes=[('yacc', h)])
                P.op('dve', lambda e: e.scalar_tensor_tensor(out=yn[b2][:, h * 64:(h + 1) * 64], in0=powv[:, h, 0:64],
                                                             scalar=coef[:, 1, h:h + 1], in1=yacc[:, h, :], op0=ALU.mult, op1=ALU.add),
                     reads=['pOw', 'coef', ('yacc', h)], writes=[('yn', b2)])
            P.dma('sp', self.mix_d[t0:t0 + 128, 768:1024], yn[b2][:], reads=[('yn', b2)], writes=[('mixn', m)])
        while bg_state['g'] is not None:
            try:
                next(bg_state['g'])
            except StopIteration:
                bg_state['g'] = None


Builder._phase2_nsa_pipe = _phase2_nsa_pipe


def _phase2_nsa2(self, l):
    P = self.P
    with ExitStack() as st:
        identb = self.sb(st, "identb_n2", [128, 128], BF16)
        zl = self.sb(st, "zl_n2", [1, 128], BF16)
        zr = self.sb(st, "zr_n2", [1, 512], BF16)
        P.dma('sp', identb[:], self.c_identb, writes=['identb'])
        P.op('pool', lambda e: e.memset(zl[:], 0.0), writes=['zl'])
        P.op('pool', lambda e: e.memset(zr[:], 0.0), writes=['zr'])
        pSr = [self.ps(st, "pSr2", [128, 512], F32) for _ in range(3)]
        pMi = self.ps(st, "pMi2", [128, 512], F32)
        self._phase2_nsa_pipe(l, pSr, pMi, identb, zl, zr)
        self.barrier()


Builder.phase2_nsa2 = _phase2_nsa2


def _phase2b(self, l):
    S, P, nc = self.S, self.P, self.nc
    NCH = S // 64
    NT = S // 128
    NQC = S // 512
    NB = S // 256
    Nc = S // 16 - 1
    NCT = max(1, S // 2048)
    LNSC = math.log(128.0 ** -0.5)
    sb, ps = self.sb, self.ps
    allt = list(range(NT))
    with ExitStack() as st:
        identb = sb(st, "identb2", [128, 128], BF16)
        identf = sb(st, "identf2", [128, 128], F32)
        ut = sb(st, "ut2", [128, 128], F32)
        tri = sb(st, "tri2", [128, 128], F32)
        zl = sb(st, "zl2", [1, 128], BF16)
        zr = sb(st, "zr2", [1, 512], BF16)
        P.dma('sp', identb[:], self.c_identb, writes=['identb'])
        P.dma('sp', identf[:], self.c_identf, writes=['identf'])
        P.dma('sp', ut[:], self.c_ut, writes=['ut'])
        P.dma('sp', tri[:], self.c_tri, writes=['tri'])
        P.op('pool', lambda e: e.memset(zl[:], 0.0), writes=['zl'])
        P.op('pool', lambda e: e.memset(zr[:], 0.0), writes=['zr'])
        uT = sb(st, "uT_m", [64, 4, NCH], F32)
        u2T = sb(st, "u2T_m", [64, 4, NCH], F32)
        flT = sb(st, "flT_m", [64, 4, NCH], F32)
        decB = sb(st, "decB", [128, 4, NCH], F32)
        with ExitStack() as s2:
            li = sb(s2, "li", [NCH, 4, 64], F32)
            lf = sb(s2, "lf", [NCH, 4, 64], F32)
            ones = sb(s2, "ones", [NCH, 64], F32)
            Fin = sb(s2, "Fin", [NCH, 4, 64], F32)
            Ft = sb(s2, "Ft", [NCH, 4, 64], F32)
            a_ = sb(s2, "a_", [NCH, 4, 64], F32)
            Ain = sb(s2, "Ain", [NCH, 4, 64], F32)
            tot = sb(s2, "tot", [NCH, 4], F32)
            cmax = sb(s2, "cmax", [NCH, 4], F32)
            cmT = sb(s2, "cmT", [4, NCH], F32)
            ET = sb(s2, "ET", [4, NCH], F32)
            ETn = sb(s2, "ETn", [4, NCH], F32)
            Ec = sb(s2, "Ec", [NCH, 4], F32)
            Enc = sb(s2, "Enc", [NCH, 4], F32)
            tmp = sb(s2, "tmpm", [NCH, 4, 64], F32)
            uu = sb(s2, "uu", [NCH, 4, 64], F32)
            uu2 = sb(s2, "uu2", [NCH, 4, 64], F32)
            fl = sb(s2, "fl", [NCH, 4, 64], F32)
            dec = sb(s2, "dec", [NCH, 4], F32)
            decrep = sb(s2, "decrep", [NCH, 4, 128], F32)
            pa = ps(s2, "pa", [128, 512], F32)
            pb = ps(s2, "pb", [128, 512], F32)
            P.dma('sp', li[:], self.gi_d.rearrange("h (c j) -> c h j", j=64),
                  reads=[('gi_d', b) for b in range(S // 512)], writes=['li'])
            P.dma('sp', lf[:], self.gf_d.rearrange("h (c j) -> c h j", j=64),
                  reads=[('gf_d', b) for b in range(S // 512)], writes=['lf'])
            P.op('pool', lambda e: e.memset(ones[:], 1.0), writes=['ones'])
            for hh in range(4):
                P.op('dve', lambda e: e.tensor_tensor_scan(out=Fin[:, hh, :], data0=ones[:], data1=lf[:, hh, :],
                                                           initial=0.0, op0=ALU.mult, op1=ALU.add),
                     reads=['ones', 'lf'], writes=['Fin'])
            P.op('dve', lambda e: e.tensor_copy(out=tot[:], in_=Fin[:, :, 63]), reads=['Fin'], writes=['tot'])
            P.op('pe', lambda e: e.matmul(pa[0:NCH, 0:4], lhsT=ut[0:NCH, 0:NCH], rhs=tot[:], start=True, stop=True),
                 reads=['ut', 'tot'], writes=['pa'])
            P.op('dve', lambda e: e.tensor_tensor(out=Ft[:], in0=Fin[:],
                                                  in1=pa[0:NCH, 0:4].unsqueeze(2).broadcast_to([NCH, 4, 64]), op=ALU.add),
                 reads=['Fin', 'pa'], writes=['Ft'])
            P.op('dve', lambda e: e.tensor_tensor(out=a_[:], in0=li[:], in1=Ft[:], op=ALU.subtract),
                 reads=['li', 'Ft'], writes=['a_'])
            for hh in range(4):
                P.op('dve', lambda e: e.tensor_tensor_scan(out=Ain[:, hh, :], data0=a_[:, hh, :], data1=a_[:, hh, :],
                                                           initial=-1e30, op0=ALU.max, op1=ALU.max),
                     reads=['a_'], writes=['Ain'])
            P.op('dve', lambda e: e.tensor_copy(out=cmax[:], in_=Ain[:, :, 63]), reads=['Ain'], writes=['cmax'])
            P.op('pe', lambda e: e.transpose(out=pb[0:4, 0:NCH], in_=cmax[:], identity=identf[0:NCH, 0:NCH]),
                 reads=['cmax', 'identf'], writes=['pb'])
            P.op('dve', lambda e: e.tensor_copy(out=cmT[:], in_=pb[0:4, 0:NCH]), reads=['pb'], writes=['cmT'])
            P.op('dve', lambda e: e.tensor_tensor_scan(out=ET[:], data0=cmT[:], data1=cmT[:], initial=0.0,
                                                       op0=ALU.max, op1=ALU.max), reads=['cmT'], writes=['ET'])
            if NCH > 1:
                P.op('dve', lambda e: e.tensor_copy(out=ETn[:, 0:NCH - 1], in_=ET[:, 1:NCH]), reads=['ET'], writes=['ETn'])
            P.op('dve', lambda e: e.tensor_copy(out=ETn[:, NCH - 1:NCH], in_=ET[:, NCH - 1:NCH]), reads=['ET'], writes=['ETn'])
            P.op('pe', lambda e: e.transpose(out=pa[0:NCH, 0:4], in_=ET[:], identity=identf[0:4, 0:4]),
                 reads=['ET', 'identf'], writes=['pa'])
            P.op('dve', lambda e: e.tensor_copy(out=Ec[:], in_=pa[0:NCH, 0:4]), reads=['pa'], writes=['Ec'])
            P.op('pe', lambda e: e.transpose(out=pb[0:NCH, 0:4], in_=ETn[:], identity=identf[0:4, 0:4]),
                 reads=['ETn', 'identf'], writes=['pb'])
            P.op('dve', lambda e: e.tensor_copy(out=Enc[:], in_=pb[0:NCH, 0:4]), reads=['pb'], writes=['Enc'])
            Eb = Ec[:].unsqueeze(2).broadcast_to([NCH, 4, 64])
            Enb = Enc[:].unsqueeze(2).broadcast_to([NCH, 4, 64])
            P.op('dve', lambda e: e.tensor_tensor(out=tmp[:], in0=a_[:], in1=Eb, op=ALU.subtract),
                 reads=['a_', 'Ec'], writes=['tmp'])
            P.op('act', lambda e: e.activation(out=uu[:], in_=tmp[:], func=AF.Exp, bias=LNSC), reads=['tmp'], writes=['uu'])
            P.op('dve', lambda e: e.tensor_tensor(out=tmp[:], in0=a_[:], in1=Enb, op=ALU.subtract),
                 reads=['a_', 'Enc'], writes=['tmp'])
            P.op('act', lambda e: e.activation(out=uu2[:], in_=tmp[:], func=AF.Exp, bias=LNSC), reads=['tmp'], writes=['uu2'])
            P.op('dve', lambda e: e.tensor_tensor(out=tmp[:], in0=Ft[:], in1=Eb, op=ALU.add),
                 reads=['Ft', 'Ec'], writes=['tmp'])
            P.op('act', lambda e: e.activation(out=fl[:], in_=tmp[:], func=AF.Exp, scale=-1.0), reads=['tmp'], writes=['fl'])
            P.op('dve', lambda e: e.tensor_tensor(out=dec[:], in0=Ec[:], in1=Enc[:], op=ALU.subtract),
                 reads=['Ec', 'Enc'], writes=['dec'])
            P.op('act', lambda e: e.activation(out=dec[:], in_=dec[:], func=AF.Exp), reads=['dec'], writes=['dec'])
            P.op('dve', lambda e: e.tensor_copy(out=decrep[:], in_=dec[:].unsqueeze(2).broadcast_to([NCH, 4, 128])),
                 reads=['dec'], writes=['decrep'])
            for src, dst, nm in ((uu, uT, 'uT'), (uu2, u2T, 'u2T'), (fl, flT, 'flT')):
                for hh in range(4):
                    P.op('pe', lambda e: e.transpose(out=pa[0:64, hh * 128:hh * 128 + NCH], in_=src[:, hh, :],
                                                     identity=identf[0:NCH, 0:NCH]),
                         reads=['uu', 'uu2', 'fl', 'identf'], writes=['pa'])
                P.op('dve', lambda e: e.tensor_copy(out=dst[:], in_=pa[0:64, :].rearrange("p (h c) -> p h c", h=4)[:, :, 0:NCH]),
                     reads=['pa'], writes=[nm])
            for hh in range(4):
                P.op('pe', lambda e: e.matmul(pb[:, hh * 128:hh * 128 + NCH], lhsT=decrep[:, hh, :],
                                              rhs=identf[0:NCH, 0:NCH], start=True, stop=True),
                     reads=['decrep', 'identf'], writes=['pb'])
            P.op('dve', lambda e: e.tensor_copy(out=decB[:], in_=pb[:].rearrange("p (h c) -> p h c", h=4)[:, :, 0:NCH]),
                 reads=['pb'], writes=['decB'])
            self.barrier()

        pSr = [ps(st, "pSr", [128, 512], F32) for _ in range(3)]
        pMi = ps(st, "pMi", [128, 512], F32)
        pMb = pMi[:, 128:192].bitcast(BF16)
        pMb2 = pMi[:, 192:256].bitcast(BF16)
        pM = pMi[:, 256:288]
        bg = BG()

        GC = 4
        NG = S // (64 * GC)
        TG = 64 * GC
        qg = [sb(st, "qg", [128, 4, TG], BF16) for _ in range(2)]
        kg = [sb(st, "kg", [128, 4, TG], BF16) for _ in range(2)]
        vg = [sb(st, "vg", [64, GC, 4, 129], BF16) for _ in range(2)]
        ogg = [sb(st, "ogg", [64, GC, 512], BF16) for _ in range(2)]
        ym = [sb(st, "ym", [64, GC, 512], BF16) for _ in range(2)]
        G = [sb(st, "G", [128, 129], F32) for _ in range(4)]
        Gb = [sb(st, "Gb", [128, 129], BF16) for _ in range(4)]
        ku2 = [sb(st, "ku2", [64, 128], BF16) for _ in range(2)]
        Sm = [sb(st, "Sm", [64, 64], BF16) for _ in range(2)]
        junkm = sb(st, "junkm", [64, 128], BF16)
        scm = [sb(st, "scm", [64, 8], F32) for _ in range(2)]
        mlA = [ps(st, "mlA", [128, 512], F32) for _ in range(1)]

        def gen_ml():
            for i in range(2):
                P.op('pool', lambda e: e.memset(vg[i][:], 1.0), writes=[('vg', i)])
            for hh in range(4):
                P.op('pool', lambda e: e.memset(G[hh][:], 0.0), writes=[('G', hh)])
                P.op('pool', lambda e: e.memset(Gb[hh][:], 0.0), writes=[('Gb', hh)])
            yield

            def load_group(g):
                i = g % 2
                tk = slice(g * TG, (g + 1) * TG)
                bl = sorted(set([(g * TG) // 512, ((g + 1) * TG - 1) // 512]))
                jl = list(range((g * TG) // 128, ((g + 1) * TG + 127) // 128))
                P.dma('sp', qg[i][:], self.qkT[0:512, tk].rearrange("(h p) t -> p h t", p=128),
                      reads=[('qkT', ft, b) for ft in range(4) for b in bl], writes=[('qg', i)])
                P.dma('sp', kg[i][:], self.qkT[512:1024, tk].rearrange("(h p) t -> p h t", p=128),
                      reads=[('qkT', ft, b) for ft in range(4, 8) for b in bl], writes=[('kg', i)])
                for hh in range(4):
                    P.dma('sp', vg[i][:, :, hh, 0:128],
                          self.mv_d[tk, hh * 128:(hh + 1) * 128].rearrange("(c s) e -> s c e", s=64),
                          reads=[('mv_d', j) for j in jl], writes=[('vg', i)])
                P.dma('sp', ogg[i][:], self.og_d[tk, :].rearrange("(c s) e -> s c e", s=64),
                      reads=[('og_d', j) for j in jl], writes=[('ogg', i)])
            load_group(0)
            it = 0
            for g in range(NG):
                if g + 1 < NG:
                    load_group(g + 1)
                gi_ = g % 2
                for cl in range(GC):
                    c = g * GC + cl
                    for hh in range(4):
                        k_ = kg[gi_][:, hh, cl * 64:(cl + 1) * 64]
                        q_ = qg[gi_][:, hh, cl * 64:(cl + 1) * 64]
                        v_ = vg[gi_][:, cl, hh, :]
                        i2 = it % 2
                        it += 1
                        A = mlA[i2 % len(mlA)]
                        if os.environ.get("UNPACK"):
                            pS_ = pSr[0][0:64, 0:64]
                            pO_ = pSr[1][0:64, 0:129]
                            pG_ = pSr[2][:, 0:129]
                            pkT_ = A[0:64, 0:64].bitcast(BF16)
                        else:
                            pS_ = A[0:64, 0:64]
                            pO_ = A[0:64, 64:193]
                            pG_ = A[:, 256:385]
                            pkT_ = A[0:64, 448:512].bitcast(BF16)
                        P.op('pe', lambda e: e.transpose(out=pkT_, in_=k_, identity=identb[:]),
                             reads=[('kg', gi_), 'identb'], writes=[('mlA', i2 % len(mlA))])
                        P.op('pe', lambda e: e.matmul(pS_, lhsT=k_, rhs=q_, start=True, stop=True),
                             reads=[('kg', gi_), ('qg', gi_)], writes=[('mlA', i2 % len(mlA))])
                        yield
                        P.op('act', lambda e: e.activation(out=ku2[i2][:], in_=pkT_, func=AF.Copy,
                                                           scale=u2T[:, hh, c:c + 1]),
                             reads=[('mlA', i2 % len(mlA)), 'u2T'], writes=[('ku2', i2)])
                        P.op('dve', lambda e: e.scalar_tensor_tensor(out=Sm[i2][:], in0=pS_,
                                                                     scalar=uT[:, hh, c:c + 1], in1=tri[0:64, 0:64],
                                                                     op0=ALU.mult, op1=ALU.mult),
                             reads=[('mlA', i2 % len(mlA)), 'uT', 'tri'], writes=[('Sm', i2)])
                        yield
                        P.op('pe', lambda e: e.matmul(pO_, lhsT=Sm[i2][:], rhs=v_, start=True, stop=False),
                             reads=[('Sm', i2), ('vg', gi_)], writes=[('mlA', i2 % len(mlA))])
                        P.op('pe', lambda e: e.matmul(pO_, lhsT=q_, rhs=Gb[hh][:], start=False, stop=True),
                             reads=[('qg', gi_), ('Gb', hh)], writes=[('mlA', i2 % len(mlA))])
                        P.op('pe', lambda e: e.matmul(pG_, lhsT=ku2[i2][:], rhs=v_, start=True, stop=True),
                             reads=[('ku2', i2), ('vg', gi_)], writes=[('mlA', i2 % len(mlA))])
                        yield
                        P.op('dve', lambda e: e.scalar_tensor_tensor(out=G[hh][:], in0=G[hh][:], scalar=decB[:, hh, c:c + 1],
                                                                     in1=pG_, op0=ALU.mult, op1=ALU.add),
                             reads=[('G', hh), 'decB', ('mlA', i2 % len(mlA))], writes=[('G', hh)])
                        P.op('pool', lambda e: e.tensor_copy(out=Gb[hh][:], in_=G[hh][:]), reads=[('G', hh)], writes=[('Gb', hh)])
                        s_ = scm[i2]
                        sk = ('scm', i2)
                        P.op('act', lambda e: e.activation(out=junkm[:], in_=pO_[:, 0:128], func=AF.Square,
                                                           scale=128.0 ** -0.5, accum_out=s_[:, 0:1]),
                             reads=[('mlA', i2 % len(mlA))], writes=['junkm', sk])
                        yield
                        P.op('dve', lambda e: e.tensor_scalar(out=s_[:, 6:7], in0=pO_[:, 128:129], scalar1=-1.0,
                                                              scalar2=flT[:, hh, c:c + 1], op0=ALU.mult, op1=ALU.max),
                             reads=[('mlA', i2 % len(mlA)), 'flT'], writes=[sk])
                        P.op('dve', lambda e: e.tensor_tensor(out=s_[:, 1:2], in0=s_[:, 6:7], in1=pO_[:, 128:129], op=ALU.max),
                             reads=[('mlA', i2 % len(mlA)), sk], writes=[sk])
                        P.op('dve', lambda e: e.tensor_tensor(out=s_[:, 2:3], in0=s_[:, 1:2], in1=s_[:, 1:2], op=ALU.mult),
                             reads=[sk], writes=[sk])
                        P.op('dve', lambda e: e.scalar_tensor_tensor(out=s_[:, 3:4], in0=s_[:, 2:3], scalar=EPS, in1=s_[:, 0:1],
                                                                     op0=ALU.mult, op1=ALU.add), reads=[sk], writes=[sk])
                        yield
                        if os.environ.get("USE_SQRT"):
                            P.op('act', lambda e: e.activation(out=s_[:, 4:5], in_=s_[:, 3:4], func=AF.Sqrt), reads=[sk], writes=[sk])
                            P.op('dve', lambda e: e.reciprocal(out=s_[:, 5:6], in_=s_[:, 4:5]), reads=[sk], writes=[sk])
                        else:
                            P.op('act', lambda e: e.activation(out=s_[:, 4:5], in_=s_[:, 3:4], func=AF.Ln), reads=[sk], writes=[sk])
                            P.op('act', lambda e: e.activation(out=s_[:, 5:6], in_=s_[:, 4:5], func=AF.Exp, scale=-0.5), reads=[sk], writes=[sk])
                        P.op('dve', lambda e: e.scalar_tensor_tensor(out=ym[gi_][:, cl, hh * 128:(hh + 1) * 128],
                                                                     in0=pO_[:, 0:128], scalar=s_[:, 5:6],
                                                                     in1=ogg[gi_][:, cl, hh * 128:(hh + 1) * 128],
                                                                     op0=ALU.mult, op1=ALU.mult),
                             reads=[('mlA', i2 % len(mlA)), sk, ('ogg', gi_)], writes=[('ym', gi_)])
                        yield
                P.dma('sp', self.mix_d[g * TG:(g + 1) * TG, 0:512].rearrange("(c s) e -> s c e", s=64), ym[gi_][:],
                      reads=[('ym', gi_)], writes=[('mixm', g)])
        ML_YIELDS = NCH * 4 * 6 + 1
        g_ml = gen_ml()
        if os.environ.get("NO_NSA"):
            for _ in g_ml:
                pass
        else:
            self._phase2_nsa_pipe(l, pSr, pMi, identb, zl, zr, bgen=g_ml, bg_yields=ML_YIELDS)
        self.barrier()


Builder.phase2b = _phase2b
```
